# Optimizing a Trainium2 kernel written in Bass

```python
import jax, jax.numpy as jnp
from jax import lax
import numpy as np

D_MODEL = 1024
BATCH = 8
SEQ = 4096
DEPTH = 2

N_META = 16
CHUNK = 64
NORM_EPS = 1e-6
M_HEADS = 4
M_DH = D_MODEL // 2 // M_HEADS
M_W = M_HEADS * M_DH
GATE_CAP = 15.0
M_SLICE = 4 * M_W + 2 * M_HEADS
R_DH = 64
R_HEADS = D_MODEL // 2 // R_DH
R_W = R_HEADS * R_DH
R_RANK_W = 64
R_RANK_A = 64
R_RANK_G = 128
R_LN_EPS = 64e-5
R_SLICE = 3 * R_W + R_RANK_W + R_RANK_A + R_RANK_G
EVEN_IN = M_SLICE + R_SLICE
EVEN_MIX = M_W + R_W
T_HEADS = 4
T_DK = D_MODEL // T_HEADS
T_DV = 2 * T_DK
T_WV = T_HEADS * T_DV
ODD_IN = 2 * D_MODEL + 2 * T_WV
ROPE_BASE = 10000.0
D_FF = 2816
CONV_W = 3
N_EVEN = (DEPTH + 1) // 2
N_ODD = DEPTH // 2

kernel_name = 'hybrid_mlstm_rwkv7_retention_convffn'


def _rmsnorm(x, g):
    xf = x.astype(jnp.float32)
    y = xf * lax.rsqrt(jnp.mean(xf * xf, axis=-1, keepdims=True) + NORM_EPS)
    return (y * g.astype(jnp.float32)).astype(x.dtype)


def _head_layernorm(x, eps):
    mu = jnp.mean(x, axis=-1, keepdims=True)
    xc = x - mu
    return xc * lax.rsqrt(jnp.mean(xc * xc, axis=-1, keepdims=True) + eps)


def _softcap(x):
    return GATE_CAP * jnp.tanh(x / GATE_CAP)


def _shift(z):
    return jnp.pad(z, ((0, 0), (1, 0), (0, 0)))[:, :-1]


def _to_chunks(z, fill):
    b, l = z.shape[:2]
    pad = CHUNK - N_META
    z = jnp.pad(z, [(0, 0), (pad, 0)] + [(0, 0)] * (z.ndim - 2), constant_values=fill)
    nc = (l + pad) // CHUNK
    z = z.reshape((b, nc, CHUNK) + z.shape[2:])
    return z.transpose((1, 0, 3, 2) + tuple(range(4, z.ndim)))


def _from_chunks(y, l):
    nc, b, h, c, d = y.shape
    y = y.transpose(1, 0, 3, 2, 4).reshape(b, nc * c, h, d)
    return y[:, nc * c - l:]


def _mlstm(q, k, v, logi, logf):
    b, l, nh, dk = q.shape
    dv = v.shape[-1]
    xs = (_to_chunks(q, 0.0), _to_chunks(k, 0.0), _to_chunks(v, 0.0),
          _to_chunks(logi, -jnp.inf), _to_chunks(logf, 0.0))
    causal = jnp.tril(jnp.ones((CHUNK, CHUNK), dtype=bool))

    def step(carry, inp):
        c_st, n_st, m_st = carry
        qc, kc, vc, li, lf = inp
        bcum = jnp.cumsum(lf, axis=-1)
        g = bcum[..., -1]
        dmat = jnp.where(causal, bcum[..., :, None] - bcum[..., None, :] + li[..., None, :], -jnp.inf)
        inter = bcum + m_st[..., None]
        m_row = jnp.maximum(inter, jnp.max(dmat, axis=-1))
        s = jnp.einsum('bhtd,bhsd->bhts', qc, kc) * jnp.exp(dmat - m_row[..., None])
        w_inter = jnp.exp(inter - m_row)
        num = jnp.einsum('bhts,bhsv->bhtv', s, vc) + w_inter[..., None] * jnp.einsum('bhtd,bhdv->bhtv', qc, c_st)
        den = jnp.sum(s, axis=-1) + w_inter * jnp.einsum('bhtd,bhd->bht', qc, n_st)
        hc = num / jnp.maximum(jnp.abs(den), jnp.exp(-m_row))[..., None]
        a_log = g[..., None] - bcum + li
        m_new = jnp.maximum(g + m_st, jnp.max(a_log, axis=-1))
        carry_decay = jnp.exp(g + m_st - m_new)
        kw = kc * jnp.exp(a_log - m_new[..., None])[..., None]
        c_new = carry_decay[..., None, None] * c_st + jnp.einsum('bhsd,bhsv->bhdv', kw, vc)
        n_new = carry_decay[..., None] * n_st + jnp.sum(kw, axis=-2)
        return (c_new, n_new, m_new), hc

    init = (jnp.zeros((b, nh, dk, dv), jnp.float32), jnp.zeros((b, nh, dk), jnp.float32),
            jnp.zeros((b, nh), jnp.float32))
    _, hs = lax.scan(step, init, xs)
    return _from_chunks(hs, l)


def _rwkv7_scan(r, w, k, v, kk, a):
    b, l, nh, n = r.shape
    xs = tuple(t.transpose(1, 0, 2, 3) for t in (r, w, k, v, kk, a))

    def step(state, inp):
        rt, wt, kt, vt, kkt, at = inp
        sa = jnp.einsum('bhvk,bhk->bhv', state, -kkt)
        state = (state * wt[:, :, None, :] + sa[..., None] * (kkt * at)[:, :, None, :]
                 + vt[..., None] * kt[:, :, None, :])
        return state, jnp.einsum('bhvk,bhk->bhv', state, rt)

    _, out = lax.scan(step, jnp.zeros((b, nh, n, n), jnp.float32), xs)
    return out.transpose(1, 0, 2, 3)


def _rotary(x, pos):
    d = x.shape[-1]
    inv = 1.0 / (ROPE_BASE ** jnp.linspace(0.0, 1.0, d // 2, dtype=jnp.float32))
    ang = pos[:, None] * jnp.repeat(inv, 2)[None, :]
    sin, cos = jnp.sin(ang)[:, None, :], jnp.cos(ang)[:, None, :]
    rot = jnp.stack((-x[..., 1::2], x[..., ::2]), axis=-1).reshape(x.shape)
    return x * cos + rot * sin


def _retention(q, k, v):
    l = q.shape[1]
    log_gamma = jnp.log(1.0 - jnp.power(2.0, -5.0 - jnp.arange(T_HEADS, dtype=jnp.float32)))
    idx = jnp.arange(CHUNK, dtype=jnp.float32)
    diff = idx[:, None] - idx[None, :]
    intra = jnp.where(diff >= 0, jnp.exp(log_gamma[:, None, None] * jnp.maximum(diff, 0.0)), 0.0)
    q_dec = jnp.exp(log_gamma[:, None] * (idx + 1.0))[..., None]
    k_dec = jnp.exp(log_gamma[:, None] * (CHUNK - 1.0 - idx))[..., None]
    c_dec = jnp.exp(log_gamma * CHUNK)
    xs = (_to_chunks(q, 0.0), _to_chunks(k, 0.0), _to_chunks(v, 0.0))

    def step(state, inp):
        qc, kc, vc = inp
        s = jnp.einsum('bhtd,bhsd->bhts', qc, kc) * intra
        out = jnp.einsum('bhts,bhsv->bhtv', s, vc) + jnp.einsum('bhtd,bhdv->bhtv', qc * q_dec, state)
        state = c_dec[:, None, None] * state + jnp.einsum('bhsd,bhsv->bhdv', kc * k_dec, vc)
        return state, out

    b = q.shape[0]
    _, out = lax.scan(step, jnp.zeros((b, T_HEADS, T_DK, T_DV), jnp.float32), xs)
    return _from_chunks(out, l)


def _even_mixer(h, w_in, w_out, m_b_i, m_b_f, m_norm, r_mu, r_w0, r_w2, r_a0, r_a2, r_g2,
                r_k_k, r_k_a, r_r_k, r_ln_w, r_ln_b):
    b, l, _ = h.shape
    z = (h @ w_in).astype(jnp.float32)
    zm, zr = z[..., :M_SLICE], z[..., M_SLICE:]
    mq = zm[..., :M_W].reshape(b, l, M_HEADS, M_DH)
    mk = zm[..., M_W:2 * M_W].reshape(b, l, M_HEADS, M_DH) * (M_DH ** -0.5)
    mv = zm[..., 2 * M_W:3 * M_W].reshape(b, l, M_HEADS, M_DH)
    mo = jax.nn.sigmoid(zm[..., 3 * M_W:4 * M_W])
    logi = _softcap(zm[..., 4 * M_W:4 * M_W + M_HEADS] + m_b_i)
    logf = jax.nn.log_sigmoid(_softcap(zm[..., 4 * M_W + M_HEADS:] + m_b_f))
    hm = _mlstm(mq, mk, mv, logi, logf)
    hm = hm * lax.rsqrt(jnp.mean(hm * hm, axis=-1, keepdims=True) + NORM_EPS)
    hm = hm.reshape(b, l, M_W) * m_norm * mo
    zr = zr + (_shift(zr) - zr) * r_mu
    o1, o2, o3 = R_W, 2 * R_W, 3 * R_W
    o4, o5 = o3 + R_RANK_W, o3 + R_RANK_W + R_RANK_A
    rr, kr, vr = zr[..., :o1], zr[..., o1:o2], zr[..., o2:o3]
    xw, xa, xg = zr[..., o3:o4], zr[..., o4:o5], zr[..., o5:]
    wlog = -jax.nn.softplus(-(r_w0 + jnp.tanh(xw) @ r_w2)) - 0.5
    decay = jnp.exp(-jnp.exp(wlog))
    aa = jax.nn.sigmoid(r_a0 + xa @ r_a2)
    gg = jax.nn.sigmoid(xg) @ r_g2
    heads = lambda t: t.reshape(b, l, R_HEADS, R_DH)
    kk = heads(kr * r_k_k)
    kk = kk / jnp.maximum(jnp.sqrt(jnp.sum(kk * kk, axis=-1, keepdims=True)), 1e-12)
    kr = kr * (1.0 + (aa - 1.0) * r_k_a)
    rh, kh, vh = heads(rr), heads(kr), heads(vr)
    orr = _rwkv7_scan(rh, heads(decay), kh, vh, kk, heads(aa))
    orr = (_head_layernorm(orr, R_LN_EPS) * r_ln_w.reshape(R_HEADS, R_DH)
           + r_ln_b.reshape(R_HEADS, R_DH))
    orr = orr + jnp.sum(rh * kh * r_r_k, axis=-1, keepdims=True) * vh
    orr = orr.reshape(b, l, R_W) * gg
    return jnp.concatenate([hm, orr], axis=-1).astype(h.dtype) @ w_out


def _odd_mixer(h, w_in, w_out):
    b, l, _ = h.shape
    z = (h @ w_in).astype(jnp.float32)
    d = D_MODEL
    q = z[..., :d].reshape(b, l, T_HEADS, T_DK)
    k = z[..., d:2 * d].reshape(b, l, T_HEADS, T_DK) * (T_DK ** -0.5)
    v = z[..., 2 * d:2 * d + T_WV].reshape(b, l, T_HEADS, T_DV)
    gate = z[..., 2 * d + T_WV:]
    pos = jnp.arange(l, dtype=jnp.float32)
    o = _retention(_rotary(q, pos), _rotary(k, pos), v)
    o = _head_layernorm(o, NORM_EPS).reshape(b, l, T_WV) * jax.nn.silu(gate)
    return o.astype(h.dtype) @ w_out


def _conv_ffn(h, w_up, conv_w, conv_b, w_down):
    u = h @ w_up
    val, gate = u[..., :D_FF], u[..., D_FF:]
    gate = lax.conv_general_dilated(gate, conv_w[:, None, :], window_strides=(1,),
                                    padding=[(CONV_W - 1, 0)],
                                    dimension_numbers=('NWC', 'WIO', 'NWC'),
                                    feature_group_count=D_FF) + conv_b
    return (jax.nn.silu(gate) * val) @ w_down


def setup_inputs(seed: int = 0) -> dict:
    key = jax.random.key(seed)
    ks = list(jax.random.split(key, 32))
    def nrm(shape, scale):
        return jax.random.normal(ks.pop(), shape, jnp.float32) * scale
    d = D_MODEL
    x = nrm((BATCH, SEQ, d), 1.0)
    meta_tokens = nrm((N_META, d), 1.0)
    norm_mix = 1.0 + nrm((DEPTH, d), 0.02)
    norm_ffn = 1.0 + nrm((DEPTH, d), 0.02)
    norm_final = 1.0 + nrm((d,), 0.02)
    e_w_in = nrm((N_EVEN, d, EVEN_IN), d ** -0.5)
    e_w_out = nrm((N_EVEN, EVEN_MIX, d), EVEN_MIX ** -0.5)
    m_b_i = -1.0 + nrm((N_EVEN, M_HEADS), 0.1)
    m_b_f = jnp.linspace(3.0, 6.0, M_HEADS, dtype=jnp.float32)[None] + nrm((N_EVEN, M_HEADS), 0.1)
    m_norm = 1.0 + nrm((N_EVEN, M_W), 0.02)
    r_mu = jax.random.uniform(ks.pop(), (N_EVEN, R_SLICE), jnp.float32)
    r_w0 = (-6.5 + 5.0 * jnp.linspace(0.0, 1.0, R_W, dtype=jnp.float32) ** 0.85)[None] + nrm((N_EVEN, R_W), 0.1)
    r_w2 = nrm((N_EVEN, R_RANK_W, R_W), 0.1 * R_RANK_W ** -0.5)
    r_a0 = nrm((N_EVEN, R_W), 0.1)
    r_a2 = nrm((N_EVEN, R_RANK_A, R_W), 0.5 * R_RANK_A ** -0.5)
    r_g2 = nrm((N_EVEN, R_RANK_G, R_W), R_RANK_G ** -0.5)
    r_k_k = 0.85 + nrm((N_EVEN, R_W), 0.02)
    r_k_a = 1.0 + nrm((N_EVEN, R_W), 0.02)
    r_r_k = nrm((N_EVEN, R_HEADS, R_DH), 0.1)
    r_ln_w = 1.0 + nrm((N_EVEN, R_W), 0.02)
    r_ln_b = nrm((N_EVEN, R_W), 0.01)
    o_w_in = nrm((N_ODD, d, ODD_IN), d ** -0.5)
    o_w_out = nrm((N_ODD, T_WV, d), T_WV ** -0.5)
    f_w_up = nrm((DEPTH, d, 2 * D_FF), d ** -0.5)
    f_conv_w = nrm((DEPTH, CONV_W, D_FF), CONV_W ** -0.5)
    f_conv_b = nrm((DEPTH, D_FF), 0.01)
    f_w_down = nrm((DEPTH, D_FF, d), D_FF ** -0.5)
    return {'x': x, 'meta_tokens': meta_tokens, 'norm_mix': norm_mix, 'norm_ffn': norm_ffn,
            'norm_final': norm_final, 'e_w_in': e_w_in, 'e_w_out': e_w_out, 'm_b_i': m_b_i,
            'm_b_f': m_b_f, 'm_norm': m_norm, 'r_mu': r_mu, 'r_w0': r_w0, 'r_w2': r_w2,
            'r_a0': r_a0, 'r_a2': r_a2, 'r_g2': r_g2, 'r_k_k': r_k_k, 'r_k_a': r_k_a,
            'r_r_k': r_r_k, 'r_ln_w': r_ln_w, 'r_ln_b': r_ln_b, 'o_w_in': o_w_in,
            'o_w_out': o_w_out, 'f_w_up': f_w_up, 'f_conv_w': f_conv_w, 'f_conv_b': f_conv_b,
            'f_w_down': f_w_down}


def reference(x, meta_tokens, norm_mix, norm_ffn, norm_final, e_w_in, e_w_out, m_b_i, m_b_f,
              m_norm, r_mu, r_w0, r_w2, r_a0, r_a2, r_g2, r_k_k, r_k_a, r_r_k, r_ln_w, r_ln_b,
              o_w_in, o_w_out, f_w_up, f_conv_w, f_conv_b, f_w_down):
    b = x.shape[0]
    meta = jnp.broadcast_to(meta_tokens[None].astype(x.dtype), (b, N_META, D_MODEL))
    h = jnp.concatenate([meta, x], axis=1)
    for layer in range(DEPTH):
        j = layer // 2
        hn = _rmsnorm(h, norm_mix[layer])
        if layer % 2 == 0:
            h = h + _even_mixer(hn, e_w_in[j], e_w_out[j], m_b_i[j], m_b_f[j], m_norm[j],
                                r_mu[j], r_w0[j], r_w2[j], r_a0[j], r_a2[j], r_g2[j],
                                r_k_k[j], r_k_a[j], r_r_k[j], r_ln_w[j], r_ln_b[j])
        else:
            h = h + _odd_mixer(hn, o_w_in[j], o_w_out[j])
        h = h + _conv_ffn(_rmsnorm(h, norm_ffn[layer]), f_w_up[layer], f_conv_w[layer],
                          f_conv_b[layer], f_w_down[layer])
    return _rmsnorm(h, norm_final)[:, N_META:]
```

```python
import contextlib
import numpy as np
import concourse.bass as bass
import concourse.mybir as mybir

F32 = mybir.dt.float32
BF16 = mybir.dt.bfloat16
I32 = mybir.dt.int32
AF = mybir.ActivationFunctionType
ALU = mybir.AluOpType
AX = mybir.AxisListType


class Buf:
    __slots__ = ("t", "w", "r", "dsem", "dcount", "name", "excl")

    def __init__(self, t, name=""):
        self.t = t
        self.w = {}
        self.r = {}
        self.dsem = None
        self.dcount = 0
        self.name = name
        self.excl = False

    def __getitem__(self, idx):
        return self.t[idx]


class Eng:
    def __init__(self, name, obj, sem):
        self.name = name
        self.obj = obj
        self.sem = sem
        self.count = 0
        self.seen = {}


class KB:
    def __init__(self, nc, es):
        self.nc = nc
        self.es = es
        self.sems = {}
        self.E = {}
        for name, obj in (("pe", nc.tensor), ("act", nc.scalar), ("dve", nc.vector),
                          ("pool", nc.gpsimd), ("sp", nc.sync)):
            sem = es.enter_context(nc.semaphore("s_" + name))
            self.E[name] = Eng(name, obj, sem)
            self.sems[id(sem)] = sem
        self.dma_tokens = {}
        self.nbuf = 0

    def sb(self, shape, dt, name=None):
        self.nbuf += 1
        name = f"{name or 'b'}_{self.nbuf}"
        t = self.es.enter_context(self.nc.sbuf_tensor(name, list(shape), dt))
        return Buf(t, name)

    def ps(self, shape, dt, name=None):
        self.nbuf += 1
        name = f"{name or 'p'}_{self.nbuf}"
        t = self.es.enter_context(self.nc.psum_tensor(name, list(shape), dt))
        b = Buf(t, name)
        b.excl = True
        return b

    def newsem(self, name):
        sem = self.es.enter_context(self.nc.semaphore(name))
        self.sems[id(sem)] = sem
        return sem

    def _wait(self, e, deps):
        for sid, val in deps.items():
            if e.seen.get(sid, 0) < val:
                e.obj.wait_ge(self.sems[sid], val)
                e.seen[sid] = val

    def _deps(self, e, reads, writes):
        deps = {}
        own = id(e.sem)
        for b in reads:
            for sid, v in b.w.items():
                if deps.get(sid, 0) < v:
                    deps[sid] = v
            if b.excl:
                for sid, v in b.r.items():
                    if sid != own and deps.get(sid, 0) < v:
                        deps[sid] = v
        for b in writes:
            for d in (b.w, b.r):
                for sid, v in d.items():
                    if sid == own:
                        continue
                    if deps.get(sid, 0) < v:
                        deps[sid] = v
        return deps

    def op(self, eng, fn, reads=(), writes=(), inc=True):
        e = self.E[eng]
        self._wait(e, self._deps(e, reads, writes))
        ins = fn(e.obj)
        if inc:
            e.count += 1
            ins.then_inc(e.sem, 1)
            val = e.count
        else:
            val = e.count + 1
        sid = id(e.sem)
        for b in reads:
            if b.r.get(sid, 0) < val:
                b.r[sid] = val
        for b in writes:
            if b.w.get(sid, 0) < val:
                b.w[sid] = val
        return ins

    def dma(self, out_ap, in_ap, reads=(), writes=(), sem_buf=None, q="sp"):
        e = self.E[q]
        self._wait(e, self._deps(e, reads, writes))
        b = sem_buf
        if b.dsem is None:
            b.dsem = self.newsem("d_" + b.name)
        b.dcount += 16
        e.obj.dma_start(out=out_ap, in_=in_ap).then_inc(b.dsem, 16)
        sid = id(b.dsem)
        for x in reads:
            x.r[sid] = b.dcount
        for x in writes:
            x.w[sid] = b.dcount
        self.dma_tokens[sid] = b.dcount

    def barrier(self):
        targets = {id(e.sem): e.count for e in self.E.values() if e.count > 0}
        targets.update(self.dma_tokens)
        for e in self.E.values():
            self._wait(e, {k: v for k, v in targets.items() if k != id(e.sem)})

    def final_wait(self):
        e = self.E["sp"]
        self._wait(e, dict(self.dma_tokens))


from concourse.bass_utils import run_bass_kernel_spmd

D = 1024
NMETA = 16
DFF = 2816
NFC = DFF // 128

W_SPECS = {
    "meta_tokens": (16, 1024), "norm_mix": (2, 1024), "norm_ffn": (2, 1024), "norm_final": (1, 1024),
    "e_w_in": (1024, 3848), "e_w_out": (1024, 1024), "m_b_i": (1, 4), "m_b_f": (1, 4), "m_norm": (1, 512),
    "r_mu": (1, 1792), "r_w0": (1, 512), "r_w2": (64, 512), "r_a0": (1, 512), "r_a2": (64, 512),
    "r_g2": (128, 512), "r_k_k": (1, 512), "r_k_a": (1, 512), "r_r_k": (1, 512), "r_ln_w": (1, 512),
    "r_ln_b": (1, 512), "o_w_in_p": (1024, 6144), "o_w_out": (2048, 1024), "f_w_up": (2048, 5632),
    "f_conv_w": (6, 2816), "f_conv_b": (2, 2816), "f_w_down": (5632, 1024),
}


def host_consts():
    c = {}
    c["c_ident"] = np.eye(128, dtype=np.float32)
    i = np.arange(128)
    c["c_ue"] = (i[:, None] <= i[None, :]).astype(np.float32)
    c["c_su"] = (i[:, None] < i[None, :]).astype(np.float32)
    c["c_iota"] = np.broadcast_to(np.arange(128, dtype=np.float32)[None, :], (128, 128)).copy()
    c["c_pidx"] = np.arange(128, dtype=np.float32)[:, None].copy()
    bo = np.zeros((128, 128), np.float32)
    bo[:64, :64] = 1.0
    bo[64:, 64:] = 1.0
    c["c_blk"] = bo
    c["c_inv"] = (np.float32(1.0) / np.power(np.float32(10000.0), np.linspace(0.0, 1.0, 128, dtype=np.float32))
                  ).astype(np.float32)[:, None].copy()
    return c


class Ctx:
    pass


def tile_rows(NT):
    tiles = [(0, NMETA)]
    for i in range(NT):
        tiles.append((NMETA + 128 * i, 128))
    return tiles


def build(NT, phases=(1, 2, 3, 4), debug=False, final=True):
    nc = bass.Bass("TRN2", target_bir_lowering=False)
    SEQ = 128 * NT
    L = NMETA + SEQ
    dr = {}
    dr["x"] = nc.dram_tensor("x", [SEQ, D], F32, kind="ExternalInput")
    for k, shp in W_SPECS.items():
        dr[k] = nc.dram_tensor(k, list(shp), F32, kind="ExternalInput")
    for k, v in host_consts().items():
        dr[k] = nc.dram_tensor(k, list(v.shape), F32, kind="ExternalInput")
    out = nc.dram_tensor("out", [SEQ, D], F32, kind="ExternalOutput")
    H = {}
    for i in (1, 2, 3):
        H[i] = nc.dram_tensor(f"H{i}", [L, D], F32, kind=("ExternalOutput" if debug else "Internal"))

    tiles = tile_rows(NT)
    es = contextlib.ExitStack()
    with es:
        kb = KB(nc, es)
        PS = [kb.ps([128, 512], F32, f"psb{i}") for i in range(7)]
        PST = kb.ps([128, 1024], BF16, "pstr")
        g = Ctx()
        g.nc, g.kb, g.dr, g.H, g.out, g.tiles, g.PS, g.PST = nc, kb, dr, H, out, tiles, PS, PST
        g.psi = 0
        g.ident_f = kb.sb([128, 128], F32, "ident_f")
        g.ident_b = kb.sb([128, 128], BF16, "ident_b")
        kb.dma(g.ident_f[:, :], dr["c_ident"].ap()[:, :], writes=[g.ident_f], sem_buf=g.ident_f)
        kb.op("dve", lambda e: e.tensor_copy(out=g.ident_b[:, :], in_=g.ident_f[:, :]),
              reads=[g.ident_f], writes=[g.ident_b])

        plist = [p for p in (1, 2, 3, 4) if p in phases]
        src = 0
        for p in plist:
            dst = p if p != plist[-1] else 4
            with contextlib.ExitStack() as pes:
                kb.es = pes
                if p in (2, 4):
                    phase_ffn(g, layer=(0 if p == 2 else 1), src=src, dst=dst, final=final)
                elif p == 1:
                    phase_l0(g, src=src, dst=dst, final=final)
                elif p == 3:
                    phase_l1(g, src=src, dst=dst, final=final)
                kb.barrier()
            kb.es = es
            src = dst
        kb.final_wait()
    return nc


def next_ps(g):
    b = g.PS[g.psi % len(g.PS)]
    g.psi += 1
    return b


def bc_rows(handle, row, n, parts=128, col0=0, ncols_total=None):
    ncols_total = ncols_total if ncols_total is not None else handle.shape[1]
    return bass.AP(handle, row * ncols_total + col0, [[0, parts], [1, n]])


def load_h(g, src, ti, HT):
    kb = g.kb
    r0, T = g.tiles[ti]
    if src == 0:
        if ti == 0:
            ap = g.dr["meta_tokens"].ap()[0:NMETA, :]
        else:
            ap = g.dr["x"].ap()[r0 - NMETA:r0 - NMETA + T, :]
    else:
        ap = g.H[src].ap()[r0:r0 + T, :]
    kb.dma(HT[:T, :], ap, writes=[HT], sem_buf=HT)


def store_h(g, dst, ti, HO, final, Gfin=None, scratch=None):
    kb = g.kb
    r0, T = g.tiles[ti]
    if dst != 4:
        kb.dma(g.H[dst].ap()[r0:r0 + T, :], HO[:T, :], reads=[HO], sem_buf=HO)
        return
    if ti == 0:
        return
    if final:
        ss, rstd, junk = scratch
        kb.op("act", lambda e: e.activation(out=junk[:T, 0:D], in_=HO[:T, :], func=AF.Square, accum_out=ss[:T, :]),
              reads=[HO], writes=[junk, ss])
        rstd_from_ss(kb, ss, rstd, T, 1.0 / D, 1e-6)
        kb.op("dve", lambda e: e.scalar_tensor_tensor(out=HO[:T, :], in0=HO[:T, :], scalar=rstd[:T, :],
                                                      in1=Gfin[:T, :], op0=ALU.mult, op1=ALU.mult),
              reads=[HO, rstd, Gfin], writes=[HO])
    kb.dma(g.out.ap()[r0 - NMETA:r0 - NMETA + T, :], HO[:T, :], reads=[HO], sem_buf=HO)


def load_weight_bf16(g, dram_handle, row0, K, N, W, stg, col0=0, ncols_total=None):
    kb = g.kb
    SW = stg[0].t.shape[1]
    engs = ("dve", "pool", "act")
    cnt = getattr(g, "_lw_cnt", 0)
    for kc in range(K // 128):
        for j0 in range(0, N, SW):
            w = min(SW, N - j0)
            s = stg[cnt % len(stg)]
            kb.dma(s[:, :w], dram_handle.ap()[row0 + kc * 128: row0 + (kc + 1) * 128, col0 + j0: col0 + j0 + w],
                   writes=[s], sem_buf=s)
            en = engs[cnt % 3]
            if en == "act":
                kb.op("act", lambda e: e.copy(out=W[:, kc, j0:j0 + w], in_=s[:, :w]), reads=[s], writes=[W])
            else:
                kb.op(en, lambda e: e.tensor_copy(out=W[:, kc, j0:j0 + w], in_=s[:, :w]), reads=[s], writes=[W])
            cnt += 1
    g._lw_cnt = cnt


def rstd_from_ss(kb, ss, rstd, T, scale, eps, ap_fn=None):
    a = (lambda b: b[:T, :]) if ap_fn is None else ap_fn
    kb.op("act", lambda e: e.activation(out=a(rstd), in_=a(ss), func=AF.Sqrt, scale=scale, bias=eps),
          reads=[ss], writes=[rstd])
    kb.op("dve", lambda e: e.reciprocal(out=a(rstd), in_=a(rstd)), reads=[rstd], writes=[rstd])


def load_vec_fm(g, handle, row, nch, dstbuf, dst_ap, vtmp, col0=0):
    kb = g.kb
    ncols = handle.shape[1]
    src = bass.AP(handle, row * ncols + col0, [[128, nch], [1, 128]])
    kb.dma(vtmp[:nch, :], src, writes=[vtmp], sem_buf=vtmp)
    pt = next_ps(g)
    kb.op("pe", lambda e: e.transpose(out=pt[:, :nch], in_=vtmp[:nch, :], identity=g.ident_f[:nch, :nch]),
          reads=[vtmp, g.ident_f], writes=[pt])
    kb.op("dve", lambda e: e.tensor_copy(out=dst_ap, in_=pt[:, :nch]), reads=[pt], writes=[dstbuf])


def rmsnorm_T(g, HT, T, Gb, hn, hnT, ss, rstd, junk):
    kb = g.kb
    kb.op("act", lambda e: e.activation(out=junk[:T, 0:D], in_=HT[:T, :], func=AF.Square, accum_out=ss[:T, :]),
          reads=[HT], writes=[junk, ss])
    rstd_from_ss(kb, ss, rstd, T, 1.0 / D, 1e-6)
    kb.op("dve", lambda e: e.scalar_tensor_tensor(out=hn[:T, :], in0=HT[:T, :], scalar=rstd[:T, :],
                                                  in1=Gb[:T, :], op0=ALU.mult, op1=ALU.mult),
          reads=[HT, rstd, Gb], writes=[hn])
    PST = g.PST
    for kc in range(8):
        kb.op("pe", lambda e: e.transpose(out=PST[:, kc * T:(kc + 1) * T], in_=hn[:T, kc * 128:(kc + 1) * 128],
                                          identity=g.ident_b[:T, :T]),
              reads=[hn, g.ident_b], writes=[PST], inc=(kc == 7))
    kb.op("act", lambda e: e.copy(out=hnT[:, :, :T], in_=PST[:, 0:8 * T].rearrange("p (k t) -> p k t", k=8)),
          reads=[PST], writes=[hnT])


def phase_ffn(g, layer, src, dst, final):
    kb, nc, dr = g.kb, g.nc, g.dr
    Wup = kb.sb([128, 8, 2 * DFF], BF16, "Wup")
    Wdn = kb.sb([128, NFC, D], BF16, "Wdn")
    with contextlib.ExitStack() as ses:
        old = kb.es
        kb.es = ses
        stg = [kb.sb([128, 1408], F32, f"stg{i}") for i in range(3)]
        load_weight_bf16(g, dr["f_w_up"], layer * D, D, 2 * DFF, Wup, stg)
        load_weight_bf16(g, dr["f_w_down"], layer * DFF, DFF, D, Wdn, stg)
        kb.barrier()
        kb.es = old
    Gb = kb.sb([128, D], F32, "Gb")
    kb.dma(Gb[:, :], bc_rows(dr["norm_ffn"], layer, D), writes=[Gb], sem_buf=Gb)
    Gfin = None
    if dst == 4 and final:
        Gfin = kb.sb([128, D], F32, "Gfin")
        kb.dma(Gfin[:, :], bc_rows(dr["norm_final"], 0, D), writes=[Gfin], sem_buf=Gfin)
    CW = kb.sb([128, 3, NFC], F32, "CW")
    CB = kb.sb([128, NFC], F32, "CB")
    vtmp = kb.sb([32, 128], F32, "vtmp")
    for j in range(3):
        load_vec_fm(g, dr["f_conv_w"], layer * 3 + j, NFC, CW, CW[:, j, :], vtmp)
    load_vec_fm(g, dr["f_conv_b"], layer, NFC, CB, CB[:, :], vtmp)
    HTs = [kb.sb([128, D], F32, f"HT{i}") for i in range(2)]
    HOs = [kb.sb([128, D], F32, f"HO{i}") for i in range(2)]
    hn = kb.sb([128, D], BF16, "hn")
    hnT = kb.sb([128, 8, 128], BF16, "hnT")
    junk = kb.sb([128, D], BF16, "junk")
    ss = kb.sb([128, 1], F32, "ss")
    rstd = kb.sb([128, 1], F32, "rstd")
    G = kb.sb([128, NFC, 130], F32, "G")
    ACC = [kb.sb([128, 4, 128], F32, f"acc{i}") for i in range(2)]
    SIL = [kb.sb([128, 4, 128], F32, f"sil{i}") for i in range(2)]
    ACTT = kb.sb([128, NFC, 128], BF16, "ACTT")
    kb.op("dve", lambda e: e.memset(G[:, :, :], 0.0), writes=[G])

    for ti, (r0, T) in enumerate(g.tiles):
        HT = HTs[ti % 2]
        HO = HOs[ti % 2]
        load_h(g, src, ti, HT)
        rmsnorm_T(g, HT, T, Gb, hn, hnT, ss, rstd, junk)
        step = 0
        for c0 in range(0, NFC, 4):
            nch = min(4, NFC - c0)
            pg = next_ps(g)
            pv = next_ps(g)
            for j in range(nch):
                for kc in range(8):
                    kb.op("pe", lambda e: e.matmul(pg[:, j * T:(j + 1) * T],
                                                   Wup[:, kc, DFF + (c0 + j) * 128: DFF + (c0 + j + 1) * 128],
                                                   hnT[:, kc, :T], start=(kc == 0), stop=(kc == 7)),
                          reads=[Wup, hnT], writes=[pg], inc=(kc == 7))
            for j in range(nch):
                for kc in range(8):
                    kb.op("pe", lambda e: e.matmul(pv[:, j * T:(j + 1) * T],
                                                   Wup[:, kc, (c0 + j) * 128:(c0 + j + 1) * 128],
                                                   hnT[:, kc, :T], start=(kc == 0), stop=(kc == 7)),
                          reads=[Wup, hnT], writes=[pv], inc=(kc == 7))
            kb.op("act", lambda e: e.copy(out=G[:, c0:c0 + nch, 2:2 + T],
                                          in_=pg[:, 0:nch * T].rearrange("p (c t) -> p c t", c=nch)),
                  reads=[pg], writes=[G])
            acc = ACC[step % 2]
            sil = SIL[step % 2]
            for j in range(nch):
                c = c0 + j
                kb.op("dve", lambda e: e.tensor_scalar(out=acc[:, j, :T], in0=G[:, c, 2:2 + T],
                                                       scalar1=CW[:, 2, c:c + 1], scalar2=CB[:, c:c + 1],
                                                       op0=ALU.mult, op1=ALU.add),
                      reads=[G, CW, CB], writes=[acc])
                kb.op("dve", lambda e: e.scalar_tensor_tensor(out=acc[:, j, :T], in0=G[:, c, 1:1 + T],
                                                              scalar=CW[:, 1, c:c + 1], in1=acc[:, j, :T],
                                                              op0=ALU.mult, op1=ALU.add),
                      reads=[G, CW, acc], writes=[acc])
                kb.op("dve", lambda e: e.scalar_tensor_tensor(out=acc[:, j, :T], in0=G[:, c, 0:T],
                                                              scalar=CW[:, 0, c:c + 1], in1=acc[:, j, :T],
                                                              op0=ALU.mult, op1=ALU.add),
                      reads=[G, CW, acc], writes=[acc])
            kb.op("act", lambda e: e.activation(out=sil[:, 0:nch, :T], in_=acc[:, 0:nch, :T], func=AF.Silu),
                  reads=[acc], writes=[sil])
            kb.op("dve", lambda e: e.tensor_tensor(out=ACTT[:, c0:c0 + nch, :T], in0=sil[:, 0:nch, :T],
                                                   in1=pv[:, 0:nch * T].rearrange("p (c t) -> p c t", c=nch),
                                                   op=ALU.mult),
                  reads=[sil, pv], writes=[ACTT])
            step += 1
        kb.op("dve", lambda e: e.tensor_copy(out=G[:, :, 0:2], in_=G[:, :, T:T + 2]), reads=[G], writes=[G])
        for nb in range(2):
            po = next_ps(g)
            for c in range(NFC):
                kb.op("pe", lambda e: e.matmul(po[:T, :], ACTT[:, c, :T], Wdn[:, c, nb * 512:(nb + 1) * 512],
                                               start=(c == 0), stop=(c == NFC - 1)),
                      reads=[ACTT, Wdn], writes=[po], inc=(c == NFC - 1))
            kb.op("dve", lambda e: e.tensor_tensor(out=HO[:T, nb * 512:(nb + 1) * 512],
                                                   in0=HT[:T, nb * 512:(nb + 1) * 512], in1=po[:T, :], op=ALU.add),
                  reads=[HT, po], writes=[HO])
        store_h(g, dst, ti, HO, final, Gfin, (ss, rstd, junk))


EH = 0.6065306597126334
ISQ = 0.08838834764831845
NEGBIG = -30000.0


def phase_l0(g, src, dst, final):
    import os
    CUT = int(os.environ.get('CUT', '99'))
    SUB = int(os.environ.get('SUB', '99'))
    HFN = int(os.environ.get('HFN', '2'))
    kb, nc, dr = g.kb, g.nc, g.dr
    Win = kb.sb([128, 8, 3848], BF16, "Win")
    Wout = kb.sb([128, 8, D], BF16, "Wout")
    W2A = kb.sb([128, 512], BF16, "W2A")
    G2 = kb.sb([128, 512], BF16, "G2")
    with contextlib.ExitStack() as ses:
        old = kb.es
        kb.es = ses
        stg = [kb.sb([128, 1924], F32, f"stg{i}") for i in range(3)]
        load_weight_bf16(g, dr["e_w_in"], 0, D, 3848, Win, stg)
        load_weight_bf16(g, dr["e_w_out"], 0, D, D, Wout, stg)
        s0 = stg[0]
        kb.dma(s0[0:64, 0:512], dr["r_w2"].ap()[:, :], writes=[s0], sem_buf=s0)
        kb.dma(s0[64:128, 0:512], dr["r_a2"].ap()[:, :], writes=[s0], sem_buf=s0)
        kb.op("dve", lambda e: e.tensor_copy(out=W2A[:, :], in_=s0[:, 0:512]), reads=[s0], writes=[W2A])
        s1 = stg[1]
        kb.dma(s1[:, 0:512], dr["r_g2"].ap()[:, :], writes=[s1], sem_buf=s1)
        kb.op("dve", lambda e: e.tensor_copy(out=G2[:, :], in_=s1[:, 0:512]), reads=[s1], writes=[G2])
        kb.barrier()
        kb.es = old
    F = lambda shape, name: kb.sb(shape, F32, name)
    Bf = lambda shape, name: kb.sb(shape, BF16, name)
    Gb = F([128, D], "Gb")
    kb.dma(Gb[:, :], bc_rows(dr["norm_mix"], 0, D), writes=[Gb], sem_buf=Gb)
    ue = F([128, 128], "ue")
    su = F([128, 128], "su")
    blk = F([128, 128], "blk")
    kb.dma(ue[:, :], dr["c_ue"].ap()[:, :], writes=[ue], sem_buf=ue)
    kb.dma(su[:, :], dr["c_su"].ap()[:, :], writes=[su], sem_buf=su)
    kb.dma(blk[:, :], dr["c_blk"].ap()[:, :], writes=[blk], sem_buf=blk)
    sl = F([128, 128], "sl")
    kb.op("dve", lambda e: e.tensor_scalar(out=sl[:, :], in0=ue[:, :], scalar1=-1.0, scalar2=1.0, op0=ALU.mult, op1=ALU.add),
          reads=[ue], writes=[sl])
    neg = F([128, 128], "neg")
    kb.op("dve", lambda e: e.tensor_scalar(out=neg[:, :], in0=sl[:, :], scalar1=NEGBIG, scalar2=None, op0=ALU.mult),
          reads=[sl], writes=[neg])
    blk64 = F([128, 128], "blk64")
    kb.op("dve", lambda e: e.tensor_scalar(out=blk64[:, :], in0=blk[:, :], scalar1=1.0 / 64.0, scalar2=None, op0=ALU.mult),
          reads=[blk], writes=[blk64])
    onesf = F([128, 128], "onesf")
    kb.op("dve", lambda e: e.memset(onesf[:, :], 1.0 / 128.0), writes=[onesf])
    onesb = Bf([128, 128], "onesb")
    kb.op("dve", lambda e: e.memset(onesb[:, :], 1.0), writes=[onesb])
    ones1 = F([128, 128], "ones1")
    kb.op("dve", lambda e: e.memset(ones1[:, :], 1.0), writes=[ones1])
    vtmp = F([32, 128], "vtmp")
    MU = F([128, 14], "MU"); W0 = F([128, 4], "W0"); A0 = F([128, 4], "A0"); KK = F([128, 4], "KK")
    KA = F([128, 4], "KA"); RRK = F([128, 4], "RRK"); LNW = F([128, 4], "LNW"); LNB = F([128, 4], "LNB")
    MN = F([128, 4], "MN")
    load_vec_fm(g, dr["r_mu"], 0, 14, MU, MU[:, :], vtmp)
    for nm, buf in (("r_w0", W0), ("r_a0", A0), ("r_k_k", KK), ("r_k_a", KA), ("r_r_k", RRK), ("r_ln_w", LNW),
                    ("r_ln_b", LNB), ("m_norm", MN)):
        load_vec_fm(g, dr[nm], 0, 4, buf, buf[:, :], vtmp)
    BG = F([128, 8], "BG")
    kb.dma(BG[:, 0:4], bc_rows(dr["m_b_i"], 0, 4), writes=[BG], sem_buf=BG)
    kb.dma(BG[:, 4:8], bc_rows(dr["m_b_f"], 0, 4), writes=[BG], sem_buf=BG)
    C = F([128, 4, 129], "C")
    Cb = Bf([128, 4, 128], "Cb")
    nbc = Bf([128, 4, 128], "nbc")
    ST = F([128, 4, 64], "ST")
    STb = Bf([128, 4, 64], "STb")
    ZR = F([128, 14, 129], "ZR")
    for b_ in (C, ST, ZR):
        kb.op("dve", lambda e: e.memset(b_[:, :, :], 0.0), writes=[b_])
    for b_ in (Cb, nbc, STb):
        kb.op("dve", lambda e: e.memset(b_[:, :, :], 0.0), writes=[b_])
    vTM1 = Bf([128, 4, 129], "vTM1")
    kb.op("dve", lambda e: e.memset(vTM1[:, :, :], 1.0), writes=[vTM1])
    HTs = [F([128, D], f"HT{i}") for i in range(2)]
    hn = Bf([128, D], "hn"); hnT = Bf([128, 8, 128], "hnT"); junk = Bf([128, D], "junk")
    ss = F([128, 1], "ss"); rstd = F([128, 1], "rstd")
    qTb = Bf([128, 4, 128], "qTb"); kTb = Bf([128, 4, 128], "kTb"); moT = F([128, 4, 128], "moT")
    gx = F([128, 8], "gx"); th = F([128, 8], "th"); ex = F([128, 4], "ex"); LI = F([128, 4], "LI"); LF = F([128, 4], "LF")
    lmb = F([128, 4], "lmb"); LFb = F([128, 4, 128], "LFb"); arg = F([128, 4, 128], "arg"); ET = F([128, 4, 128], "ET")
    eB = F([128, 4, 128], "eB"); gcol = F([128, 4], "gcol"); ew = F([128, 4], "ew"); qs = Bf([128, 4, 128], "qs")
    sT = Bf([128, 4, 128], "sT"); kw = Bf([128, 4, 128], "kw")
    cden = F([128, 4, 128], "cden"); hT = F([128, 4, 128], "hT"); hsq = F([128, 4, 128], "hsq"); rs4 = F([128, 4, 128], "rs4")
    mixT = Bf([128, 8, 128], "mixT")
    kTMf = F([128, 512], "kTMf")
    Z2 = F([128, 14, 128], "Z2"); D1 = Z2
    LIN = Bf([128, 128], "LIN"); sxg = Bf([128, 128], "sxg")
    sw = arg; aa = ET; GG = LFb
    kkr = cden; tq = hsq; rn = rs4; kp = moT
    CS = hT; CSp = F([128, 4, 128], "CSp"); csl = F([128, 4], "csl")
    eW = F([128, 4, 128], "eW"); eWp = eB; eWi = F([128, 4, 128], "eWi"); eWT = F([128, 4, 128], "eWT")
    kka = F([128, 4, 128], "kka")
    AR = Bf([128, 4, 2, 128], "AR"); BT = Bf([128, 4, 128], "BT"); KT = Bf([128, 4, 128], "KT")
    BH = Bf([128, 4, 128], "BH"); KH = Bf([128, 4, 128], "KH"); vb = Bf([128, 4, 128], "vb")
    bonus = F([128, 4, 128], "bonus")
    BTm = [Bf([128, 4, 128], f"BTm{i}") for i in range(2)]
    KTm = [Bf([128, 4, 128], f"KTm{i}") for i in range(2)]
    ATm = [Bf([128, 4, 128], f"ATm{i}") for i in range(2)]
    STbd = Bf([128, 4, 128], "STbd")
    kb.op("dve", lambda e: e.memset(STbd[:, :, :], 0.0), writes=[STbd])
    VTM = Bf([128, 8, 64], "VTM"); BHT = Bf([128, 8, 64], "BHT"); KHT = Bf([128, 8, 64], "KHT"); UTM = Bf([128, 8, 64], "UTM")
    Xa = [F([128, 8, 128], "Xa0"), F([128, 8, 128], "Xa1")]
    XTa = [F([128, 8, 128], "XTa0"), F([128, 8, 128], "XTa1")]
    Pm = F([128, 8, 128], "Pm")
    ARB = Bf([128, 8, 128], "ARB"); AAK = Bf([128, 8, 128], "AAK"); ARK = Bf([128, 8, 128], "ARK")
    P1 = kTMf
    Of = CS; Osq = tq; mean_s = rn; var = kka

    def b3(buf, T, n=4):
        return buf[:, 0:n].unsqueeze(2).to_broadcast([128, n, T])

    for ti, (r0, T) in enumerate(g.tiles):
        if os.environ.get('ONLY_T0') and ti > 0:
            break
        HT = HTs[ti % 2]
        if ti == 0:
            load_h(g, src, 0, HT)
        rmsnorm_T(g, HT, T, Gb, hn, hnT, ss, rstd, junk)
        if ti + 1 < len(g.tiles) and not os.environ.get('ONLY_T0'):
            load_h(g, src, ti + 1, HTs[(ti + 1) % 2])

        def proj_fm(pbank, j, col):
            for kc in range(8):
                kb.op("pe", lambda e: e.matmul(pbank[:, j * T:(j + 1) * T], Win[:, kc, col:col + 128], hnT[:, kc, :T],
                                               start=(kc == 0), stop=(kc == 7)),
                      reads=[Win, hnT], writes=[pbank], inc=(kc == 7))

        def proj_tm(pbank, col, n, c0=0):
            for kc in range(8):
                kb.op("pe", lambda e: e.matmul(pbank[:T, c0:c0 + n], hnT[:, kc, :T], Win[:, kc, col:col + n],
                                               start=(kc == 0), stop=(kc == 7)),
                      reads=[hnT, Win], writes=[pbank], inc=(kc == 7))

        def v3(pbank, n=4):
            return pbank[:, 0:n * T].rearrange("p (c t) -> p c t", c=n)

        pq = next_ps(g); pk = next_ps(g); pmo = next_ps(g)
        for h in range(4):
            proj_fm(pq, h, h * 128)
        for h in range(4):
            proj_fm(pk, h, 512 + h * 128)
        for h in range(4):
            proj_fm(pmo, h, 1536 + h * 128)
        kb.op("act", lambda e: e.copy(out=qTb[:, :, :T], in_=v3(pq)), reads=[pq], writes=[qTb])
        kb.op("act", lambda e: e.copy(out=kTb[:, :, :T], in_=v3(pk)), reads=[pk], writes=[kTb])
        kb.op("act", lambda e: e.activation(out=moT[:, :, :T], in_=v3(pmo), func=AF.Sigmoid), reads=[pmo], writes=[moT])
        pkt = next_ps(g); pvt = next_ps(g); pgt = next_ps(g)
        proj_tm(pkt, 512, 512)
        proj_tm(pvt, 1024, 512)
        proj_tm(pgt, 2048, 8)
        kb.op("act", lambda e: e.copy(out=vTM1[:T, :, 0:128], in_=pvt[:T, :].rearrange("p (h v) -> p h v", h=4)),
              reads=[pvt], writes=[vTM1])
        kb.op("act", lambda e: e.copy(out=kTMf[:T, :], in_=pkt[:T, :]), reads=[pkt], writes=[kTMf])
        if CUT <= 1:
            store_h(g, dst, ti, HT, final, None, (ss, rstd, junk))
            continue
        kb.op("dve", lambda e: e.tensor_tensor(out=gx[:T, :], in0=pgt[:T, 0:8], in1=BG[:T, :], op=ALU.add),
              reads=[pgt, BG], writes=[gx])
        kb.op("act", lambda e: e.activation(out=th[:T, :], in_=gx[:T, :], func=AF.Tanh, scale=1.0 / 15.0), reads=[gx], writes=[th])
        kb.op("dve", lambda e: e.tensor_scalar(out=LI[:T, :], in0=th[:T, 0:4], scalar1=15.0, scalar2=None, op0=ALU.mult),
              reads=[th], writes=[LI])
        kb.op("act", lambda e: e.activation(out=ex[:T, :], in_=th[:T, 4:8], func=AF.Exp, scale=-15.0), reads=[th], writes=[ex])
        kb.op("act", lambda e: e.activation(out=ex[:T, :], in_=ex[:T, :], func=AF.Ln, bias=1.0), reads=[ex], writes=[ex])
        kb.op("dve", lambda e: e.tensor_scalar(out=LF[:T, :], in0=ex[:T, :], scalar1=-1.0, scalar2=None, op0=ALU.mult),
              reads=[ex], writes=[LF])
        pbc = next_ps(g)
        kb.op("pe", lambda e: e.matmul(pbc[:T, 0:4], ue[:T, :T], LF[:T, :], start=True, stop=True), reads=[ue, LF], writes=[pbc])
        kb.op("dve", lambda e: e.tensor_tensor(out=lmb[:T, :], in0=LI[:T, :], in1=pbc[:T, 0:4], op=ALU.subtract),
              reads=[LI, pbc], writes=[lmb])
        kb.op("dve", lambda e: e.tensor_copy(out=LFb[:T, :, :], in_=LF[:T, 0:4].unsqueeze(2).to_broadcast([T, 4, 128])),
              reads=[LF], writes=[LFb])
        if CUT <= 2:
            store_h(g, dst, ti, HT, final, None, (ss, rstd, junk))
            continue
        pB = next_ps(g)
        for h in range(4):
            kb.op("pe", lambda e: e.matmul(pB[:, h * T:(h + 1) * T], LFb[:T, h, :], ue[:T, :T], start=True, stop=True),
                  reads=[LFb, ue], writes=[pB], inc=(h == 3))
        kb.op("dve", lambda e: e.tensor_tensor(out=arg[:T, :, :T], in0=v3(pB)[:T], in1=neg[:T, :T].unsqueeze(1).to_broadcast([T, 4, T]),
                                               op=ALU.add), reads=[pB, neg], writes=[arg])
        for h in range(4):
            kb.op("act", lambda e: e.activation(out=ET[:T, h, :T], in_=arg[:T, h, :T], func=AF.Exp, bias=lmb[:T, h:h + 1]),
                  reads=[arg, lmb], writes=[ET])
        kb.op("act", lambda e: e.activation(out=eB[:, :, :T], in_=v3(pB), func=AF.Exp), reads=[pB], writes=[eB])
        kb.op("dve", lambda e: e.tensor_copy(out=gcol[:, :], in_=v3(pB)[:, :, T - 1]), reads=[pB], writes=[gcol])
        for h in range(4):
            kb.op("act", lambda e: e.activation(out=ew[:T, h:h + 1], in_=lmb[:T, h:h + 1], func=AF.Exp, bias=gcol[:T, h:h + 1]),
                  reads=[lmb, gcol], writes=[ew])
        kb.op("dve", lambda e: e.tensor_tensor(out=qs[:, :, :T], in0=qTb[:, :, :T], in1=eB[:, :, :T], op=ALU.mult),
              reads=[qTb, eB], writes=[qs])
        if CUT <= 3:
            store_h(g, dst, ti, HT, final, None, (ss, rstd, junk))
            continue
        psc = next_ps(g)
        for h in range(4):
            kb.op("pe", lambda e: e.matmul(psc[:T, h * T:(h + 1) * T], kTb[:, h, :T], qTb[:, h, :T], start=True, stop=True),
                  reads=[kTb, qTb], writes=[psc], inc=(h == 3))
        kb.op("dve", lambda e: e.scalar_tensor_tensor(out=sT[:T, :, :T], in0=v3(psc)[:T], scalar=ISQ, in1=ET[:T, :, :T],
                                                      op0=ALU.mult, op1=ALU.mult), reads=[psc, ET], writes=[sT])
        pnum = next_ps(g); pden = next_ps(g)
        for h in range(4):
            kb.op("pe", lambda e: e.matmul(pnum[:, h * T:(h + 1) * T], vTM1[:T, h, 0:128], sT[:T, h, :T], start=True, stop=False),
                  reads=[vTM1, sT], writes=[pnum], inc=False)
            kb.op("pe", lambda e: e.matmul(pnum[:, h * T:(h + 1) * T], Cb[:, h, :], qs[:, h, :T], start=False, stop=True),
                  reads=[Cb, qs], writes=[pnum])
        for h in range(4):
            kb.op("pe", lambda e: e.matmul(pden[:, h * T:(h + 1) * T], onesb[:T, :], sT[:T, h, :T], start=True, stop=False),
                  reads=[onesb, sT], writes=[pden], inc=False)
            kb.op("pe", lambda e: e.matmul(pden[:, h * T:(h + 1) * T], nbc[:, h, :], qs[:, h, :T], start=False, stop=True),
                  reads=[nbc, qs], writes=[pden])
        kb.op("act", lambda e: e.activation(out=cden[:, :, :T], in_=v3(pden), func=AF.Abs), reads=[pden], writes=[cden])
        kb.op("dve", lambda e: e.tensor_scalar(out=cden[:, :, :T], in0=cden[:, :, :T], scalar1=1.0, scalar2=None, op0=ALU.max),
              reads=[cden], writes=[cden])
        kb.op("dve", lambda e: e.reciprocal(out=cden[:, :, :T], in_=cden[:, :, :T]), reads=[cden], writes=[cden])
        kb.op("dve", lambda e: e.tensor_tensor(out=hT[:, :, :T], in0=v3(pnum), in1=cden[:, :, :T], op=ALU.mult),
              reads=[pnum, cden], writes=[hT])
        kb.op("act", lambda e: e.activation(out=hsq[:, :, :T], in_=hT[:, :, :T], func=AF.Square), reads=[hT], writes=[hsq])
        pss = next_ps(g)
        for h in range(4):
            kb.op("pe", lambda e: e.matmul(pss[:, h * T:(h + 1) * T], onesf[:, :], hsq[:, h, :T], start=True, stop=True),
                  reads=[onesf, hsq], writes=[pss], inc=(h == 3))
        kb.op("act", lambda e: e.activation(out=rs4[:, :, :T], in_=v3(pss), func=AF.Sqrt, bias=1e-6), reads=[pss], writes=[rs4])
        kb.op("dve", lambda e: e.reciprocal(out=rs4[:, :, :T], in_=rs4[:, :, :T]), reads=[rs4], writes=[rs4])
        kb.op("dve", lambda e: e.tensor_tensor(out=hT[:, :, :T], in0=hT[:, :, :T], in1=rs4[:, :, :T], op=ALU.mult),
              reads=[hT, rs4], writes=[hT])
        kb.op("dve", lambda e: e.tensor_tensor(out=hT[:, :, :T], in0=hT[:, :, :T], in1=moT[:, :, :T], op=ALU.mult),
              reads=[hT, moT], writes=[hT])
        kb.op("dve", lambda e: e.tensor_tensor(out=mixT[:, 0:4, :T], in0=hT[:, :, :T], in1=b3(MN, T), op=ALU.mult),
              reads=[hT, MN], writes=[mixT])
        if CUT <= 4:
            store_h(g, dst, ti, HT, final, None, (ss, rstd, junk))
            continue
        for h in range(4):
            kb.op("dve", lambda e: e.tensor_scalar(out=kw[:T, h, :], in0=kTMf[:T, h * 128:(h + 1) * 128], scalar1=ew[:T, h:h + 1],
                                                   scalar2=ISQ, op0=ALU.mult, op1=ALU.mult), reads=[kTMf, ew], writes=[kw])
        for half in range(2):
            pC = next_ps(g)
            for hh in range(2):
                h = half * 2 + hh
                kb.op("pe", lambda e: e.matmul(pC[:, hh * 129:(hh + 1) * 129], kw[:T, h, :], vTM1[:T, h, :], start=True, stop=True),
                      reads=[kw, vTM1], writes=[pC], inc=(hh == 1))
            for hh in range(2):
                h = half * 2 + hh
                kb.op("dve", lambda e: e.scalar_tensor_tensor(out=C[:, h, :], in0=C[:, h, :], scalar=eB[:, h, T - 1:T],
                                                              in1=pC[:, hh * 129:(hh + 1) * 129], op0=ALU.mult, op1=ALU.add),
                      reads=[C, eB, pC], writes=[C])
        kb.op("act", lambda e: e.copy(out=Cb[:, :, :], in_=C[:, :, 0:128]), reads=[C], writes=[Cb])
        kb.op("dve", lambda e: e.tensor_copy(out=nbc[:, :, :], in_=C[:, :, 128:129].to_broadcast([128, 4, 128])),
              reads=[C], writes=[nbc])

        if CUT <= 5:
            store_h(g, dst, ti, HT, final, None, (ss, rstd, junk))
            continue
        zc = 2056
        for b0, n in ((0, 4), (4, 4), (8, 4), (12, 2)):
            pz = next_ps(g)
            for j in range(n):
                proj_fm(pz, j, zc + (b0 + j) * 128)
            kb.op("act", lambda e: e.copy(out=ZR[:, b0:b0 + n, 1:T + 1], in_=v3(pz, n)), reads=[pz], writes=[ZR])
        kb.op("dve", lambda e: e.tensor_tensor(out=D1[:, :, :T], in0=ZR[:, :, 0:T], in1=ZR[:, :, 1:T + 1], op=ALU.subtract),
              reads=[ZR], writes=[D1])
        kb.op("dve", lambda e: e.tensor_tensor(out=D1[:, :, :T], in0=D1[:, :, :T], in1=b3(MU, T, 14), op=ALU.mult),
              reads=[D1, MU], writes=[D1])
        kb.op("dve", lambda e: e.tensor_tensor(out=Z2[:, :, :T], in0=D1[:, :, :T], in1=ZR[:, :, 1:T + 1], op=ALU.add),
              reads=[D1, ZR], writes=[Z2])
        kb.op("dve", lambda e: e.tensor_copy(out=ZR[:, :, 0:1], in_=ZR[:, :, T:T + 1]), reads=[ZR], writes=[ZR])
        r_ = Z2[:, 0:4, :T]; k_ = Z2[:, 4:8, :T]; v_ = Z2[:, 8:12, :T]
        if CUT <= 6:
            store_h(g, dst, ti, HT, final, None, (ss, rstd, junk))
            continue
        kb.op("act", lambda e: e.activation(out=LIN[0:64, :T], in_=Z2[0:64, 12, :T], func=AF.Tanh), reads=[Z2], writes=[LIN])
        kb.op("act", lambda e: e.copy(out=LIN[64:128, :T], in_=Z2[64:128, 12, :T]), reads=[Z2], writes=[LIN])
        kb.op("act", lambda e: e.activation(out=sxg[:, :T], in_=Z2[:, 13, :T], func=AF.Sigmoid), reads=[Z2], writes=[sxg])
        pw = next_ps(g); pa = next_ps(g); pgg = next_ps(g)
        for c in range(4):
            kb.op("pe", lambda e: e.matmul(pw[:, c * T:(c + 1) * T], W2A[0:64, c * 128:(c + 1) * 128], LIN[0:64, :T], start=True, stop=True),
                  reads=[W2A, LIN], writes=[pw], inc=(c == 3))
        for c in range(4):
            kb.op("pe", lambda e: e.matmul(pa[:, c * T:(c + 1) * T], W2A[64:128, c * 128:(c + 1) * 128], LIN[64:128, :T], start=True, stop=True),
                  reads=[W2A, LIN], writes=[pa], inc=(c == 3))
        for c in range(4):
            kb.op("pe", lambda e: e.matmul(pgg[:, c * T:(c + 1) * T], G2[:, c * 128:(c + 1) * 128], sxg[:, :T], start=True, stop=True),
                  reads=[G2, sxg], writes=[pgg], inc=(c == 3))
        for c in range(4):
            kb.op("act", lambda e: e.activation(out=sw[:, c, :T], in_=pw[:, c * T:(c + 1) * T], func=AF.Sigmoid, bias=W0[:, c:c + 1]),
                  reads=[pw, W0], writes=[sw])
            kb.op("act", lambda e: e.activation(out=aa[:, c, :T], in_=pa[:, c * T:(c + 1) * T], func=AF.Sigmoid, bias=A0[:, c:c + 1]),
                  reads=[pa, A0], writes=[aa])
        kb.op("act", lambda e: e.copy(out=GG[:, :, :T], in_=v3(pgg)), reads=[pgg], writes=[GG])
        if CUT <= 7:
            store_h(g, dst, ti, HT, final, None, (ss, rstd, junk))
            continue
        kb.op("dve", lambda e: e.tensor_tensor(out=kkr[:, :, :T], in0=k_, in1=b3(KK, T), op=ALU.mult), reads=[Z2, KK], writes=[kkr])
        kb.op("act", lambda e: e.activation(out=tq[:, :, :T], in_=kkr[:, :, :T], func=AF.Square), reads=[kkr], writes=[tq])
        pn = next_ps(g)
        for c in range(4):
            kb.op("pe", lambda e: e.matmul(pn[:, c * T:(c + 1) * T], blk[:, :], tq[:, c, :T], start=True, stop=True),
                  reads=[blk, tq], writes=[pn], inc=(c == 3))
        kb.op("act", lambda e: e.activation(out=rn[:, :, :T], in_=v3(pn), func=AF.Sqrt), reads=[pn], writes=[rn])
        kb.op("dve", lambda e: e.tensor_scalar(out=rn[:, :, :T], in0=rn[:, :, :T], scalar1=1e-12, scalar2=None, op0=ALU.max),
              reads=[rn], writes=[rn])
        kb.op("dve", lambda e: e.reciprocal(out=rn[:, :, :T], in_=rn[:, :, :T]), reads=[rn], writes=[rn])
        kb.op("dve", lambda e: e.tensor_tensor(out=kkr[:, :, :T], in0=kkr[:, :, :T], in1=rn[:, :, :T], op=ALU.mult),
              reads=[kkr, rn], writes=[kkr])
        kb.op("dve", lambda e: e.scalar_tensor_tensor(out=tq[:, :, :T], in0=aa[:, :, :T], scalar=-1.0, in1=b3(KA, T),
                                                      op0=ALU.add, op1=ALU.mult), reads=[aa, KA], writes=[tq])
        kb.op("dve", lambda e: e.scalar_tensor_tensor(out=kp[:, :, :T], in0=tq[:, :, :T], scalar=1.0, in1=k_,
                                                      op0=ALU.add, op1=ALU.mult), reads=[tq, Z2], writes=[kp])
        if CUT <= 8:
            store_h(g, dst, ti, HT, final, None, (ss, rstd, junk))
            continue
        for c in range(4):
            kb.op("dve", lambda e: e.tensor_tensor_scan(out=CS[:, c, :T], data0=ones1[:, :T], data1=sw[:, c, :T], initial=0.0,
                                                        op0=ALU.mult, op1=ALU.add), reads=[ones1, sw], writes=[CS])
        kb.op("dve", lambda e: e.tensor_tensor(out=CSp[:, :, :T], in0=CS[:, :, :T], in1=sw[:, :, :T], op=ALU.subtract),
              reads=[CS, sw], writes=[CSp])
        kb.op("dve", lambda e: e.tensor_scalar(out=csl[:, :], in0=CS[:, :, T - 1], scalar1=-EH, scalar2=None, op0=ALU.mult),
              reads=[CS], writes=[csl])
        kb.op("act", lambda e: e.activation(out=eW[:, :, :T], in_=CS[:, :, :T], func=AF.Exp, scale=-EH), reads=[CS], writes=[eW])
        kb.op("act", lambda e: e.activation(out=eWp[:, :, :T], in_=CSp[:, :, :T], func=AF.Exp, scale=-EH), reads=[CSp], writes=[eWp])
        kb.op("act", lambda e: e.activation(out=eWi[:, :, :T], in_=CS[:, :, :T], func=AF.Exp, scale=EH), reads=[CS], writes=[eWi])
        for c in range(4):
            kb.op("act", lambda e: e.activation(out=eWT[:, c, :T], in_=CS[:, c, :T], func=AF.Exp, scale=EH, bias=csl[:, c:c + 1]),
                  reads=[CS, csl], writes=[eWT])
        kb.op("dve", lambda e: e.scalar_tensor_tensor(out=AR[:, :, 0, :T], in0=kkr[:, :, :T], scalar=-1.0, in1=eWp[:, :, :T],
                                                      op0=ALU.mult, op1=ALU.mult), reads=[kkr, eWp], writes=[AR])
        kb.op("dve", lambda e: e.tensor_tensor(out=AR[:, :, 1, :T], in0=r_, in1=eW[:, :, :T], op=ALU.mult), reads=[Z2, eW], writes=[AR])
        kb.op("dve", lambda e: e.tensor_tensor(out=kka[:, :, :T], in0=kkr[:, :, :T], in1=aa[:, :, :T], op=ALU.mult),
              reads=[kkr, aa], writes=[kka])
        kb.op("dve", lambda e: e.tensor_tensor(out=BT[:, :, :T], in0=kka[:, :, :T], in1=eWi[:, :, :T], op=ALU.mult),
              reads=[kka, eWi], writes=[BT])
        kb.op("dve", lambda e: e.tensor_tensor(out=KT[:, :, :T], in0=kp[:, :, :T], in1=eWi[:, :, :T], op=ALU.mult),
              reads=[kp, eWi], writes=[KT])
        kb.op("dve", lambda e: e.tensor_tensor(out=BH[:, :, :T], in0=kka[:, :, :T], in1=eWT[:, :, :T], op=ALU.mult),
              reads=[kka, eWT], writes=[BH])
        kb.op("dve", lambda e: e.tensor_tensor(out=KH[:, :, :T], in0=kp[:, :, :T], in1=eWT[:, :, :T], op=ALU.mult),
              reads=[kp, eWT], writes=[KH])
        kb.op("act", lambda e: e.copy(out=vb[:, :, :T], in_=v_), reads=[Z2], writes=[vb])
        kb.op("dve", lambda e: e.tensor_tensor(out=tq[:, :, :T], in0=r_, in1=kp[:, :, :T], op=ALU.mult), reads=[Z2, kp], writes=[tq])
        kb.op("dve", lambda e: e.tensor_tensor(out=tq[:, :, :T], in0=tq[:, :, :T], in1=b3(RRK, T), op=ALU.mult),
              reads=[tq, RRK], writes=[tq])
        prk = next_ps(g)
        for c in range(4):
            kb.op("pe", lambda e: e.matmul(prk[:, c * T:(c + 1) * T], blk[:, :], tq[:, c, :T], start=True, stop=True),
                  reads=[blk, tq], writes=[prk], inc=(c == 3))
        kb.op("dve", lambda e: e.tensor_tensor(out=bonus[:, :, :T], in0=v3(prk), in1=v_, op=ALU.mult), reads=[prk, Z2], writes=[bonus])
        if CUT <= 9:
            store_h(g, dst, ti, HT, final, None, (ss, rstd, junk))
            continue
        PST = g.PST
        for c in range(4):
            kb.op("pe", lambda e: e.transpose(out=PST[:T, c * 128:(c + 1) * 128], in_=vb[:, c, :T], identity=g.ident_b[:, :]),
                  reads=[vb, g.ident_b], writes=[PST], inc=False)
        for c in range(4):
            kb.op("pe", lambda e: e.transpose(out=PST[:T, (4 + c) * 128:(5 + c) * 128], in_=BH[:, c, :T], identity=g.ident_b[:, :]),
                  reads=[BH, g.ident_b], writes=[PST], inc=(c == 3))
        kb.op("act", lambda e: e.copy(out=VTM[:T, :, :], in_=PST[:T, 0:512].rearrange("p (h v) -> p h v", h=8)), reads=[PST], writes=[VTM])
        kb.op("act", lambda e: e.copy(out=BHT[:T, :, :], in_=PST[:T, 512:1024].rearrange("p (h v) -> p h v", h=8)), reads=[PST], writes=[BHT])
        for c in range(4):
            kb.op("pe", lambda e: e.transpose(out=PST[:T, c * 128:(c + 1) * 128], in_=KH[:, c, :T], identity=g.ident_b[:, :]),
                  reads=[KH, g.ident_b], writes=[PST], inc=(c == 3))
        kb.op("act", lambda e: e.copy(out=KHT[:T, :, :], in_=PST[:T, 0:512].rearrange("p (h v) -> p h v", h=8)), reads=[PST], writes=[KHT])
        if CUT <= 10:
            store_h(g, dst, ti, HT, final, None, (ss, rstd, junk))
            continue
        for hf in range(2):
            mcol = blk[:, 127 * hf:127 * hf + 1]
            kb.op("dve", lambda e: e.tensor_scalar(out=BTm[hf][:, :, :T], in0=BT[:, :, :T], scalar1=mcol, scalar2=None, op0=ALU.mult),
                  reads=[BT, blk], writes=[BTm[hf]])
            kb.op("dve", lambda e: e.tensor_scalar(out=KTm[hf][:, :, :T], in0=KT[:, :, :T], scalar1=mcol, scalar2=None, op0=ALU.mult),
                  reads=[KT, blk], writes=[KTm[hf]])
            kb.op("dve", lambda e: e.tensor_scalar(out=ATm[hf][:, :, :T], in0=AR[:, :, 0, :T], scalar1=mcol, scalar2=None, op0=ALU.mult),
                  reads=[AR, blk], writes=[ATm[hf]])
        X, XT = Xa[0], XTa[0]
        sub = su[:T, :T].unsqueeze(1).to_broadcast([T, 2, T])
        ueb = ue[:T, :T].unsqueeze(1).to_broadcast([T, 2, T])
        slb = sl[:T, :T].unsqueeze(1).to_broadcast([T, 4, T])
        for c in range(4):
            pNA = next_ps(g); pKA = next_ps(g)
            for hf in range(HFN):
                for j in range(2):
                    kb.op("pe", lambda e: e.matmul(pNA[:T, (hf * 2 + j) * T:(hf * 2 + j + 1) * T], BTm[hf][:, c, :T], AR[:, c, j, :T], start=True, stop=True),
                          reads=[BTm[hf], AR], writes=[pNA], inc=(hf == HFN - 1 and j == 1))
            for hf in range(2):
                for j in range(2):
                    kb.op("pe", lambda e: e.matmul(pKA[:T, (hf * 2 + j) * T:(hf * 2 + j + 1) * T], KTm[hf][:, c, :T], AR[:, c, j, :T], start=True, stop=True),
                          reads=[KTm[hf], AR], writes=[pKA], inc=(hf == HFN - 1 and j == 1))
            if SUB == 1:
                continue
            na4 = pNA[:T, 0:4 * T].rearrange("p (h j t) -> p h j t", h=2, j=2)
            ka4 = pKA[:T, 0:4 * T].rearrange("p (h j t) -> p h j t", h=2, j=2)
            kb.op("dve", lambda e: e.tensor_tensor(out=X[:T, 2 * c:2 * c + 2, :T], in0=na4[:, :, 0, :], in1=sub, op=ALU.mult),
                  reads=[pNA, su], writes=[X])
            kb.op("dve", lambda e: e.tensor_tensor(out=ARB[:T, 2 * c:2 * c + 2, :T], in0=na4[:, :, 1, :], in1=ueb, op=ALU.mult),
                  reads=[pNA, ue], writes=[ARB])
            kb.op("dve", lambda e: e.tensor_tensor(out=AAK[:T, 2 * c:2 * c + 2, :T], in0=ka4[:, :, 0, :], in1=sub, op=ALU.mult),
                  reads=[pKA, su], writes=[AAK])
            kb.op("dve", lambda e: e.tensor_tensor(out=ARK[:T, 2 * c:2 * c + 2, :T], in0=ka4[:, :, 1, :], in1=ueb, op=ALU.mult),
                  reads=[pKA, ue], writes=[ARK])
        if SUB <= 2:
            store_h(g, dst, ti, HT, final, None, (ss, rstd, junk))
            continue
        for half in range(2):
            pNb = next_ps(g)
            for j in range(4):
                h = half * 4 + j
                c, hf = h // 2, h % 2
                pl = slice(hf * 64, hf * 64 + 64)
                kb.op("pe", lambda e: e.matmul(pNb[:T, j * T:(j + 1) * T], ATm[hf][:, c, :T], BT[:, c, :T], start=True, stop=True),
                      reads=[ATm[hf], BT], writes=[pNb], inc=(j == 3))
            kb.op("dve", lambda e: e.tensor_tensor(out=XT[:T, half * 4:half * 4 + 4, :T], in0=v3(pNb)[:T], in1=slb, op=ALU.mult),
                  reads=[pNb, sl], writes=[XT])
        if CUT <= 11:
            store_h(g, dst, ti, HT, final, None, (ss, rstd, junk))
            continue
        kb.op("dve", lambda e: e.tensor_tensor(out=Pm[:T, :, :T], in0=X[:T, :, :T],
                                               in1=g.ident_f[:T, :T].unsqueeze(1).to_broadcast([T, 8, T]), op=ALU.add),
              reads=[X, g.ident_f], writes=[Pm])
        lv = 1
        cur = 0
        while lv * 2 < T:
            X, XT = Xa[cur], XTa[cur]
            Xn, XTn = Xa[1 - cur], XTa[1 - cur]
            for half in range(2):
                p1 = next_ps(g); p2 = next_ps(g)
                for j in range(4):
                    h = half * 4 + j
                    kb.op("pe", lambda e: e.matmul(p1[:T, j * T:(j + 1) * T], XT[:T, h, :T], X[:T, h, :T], start=True, stop=True),
                          reads=[XT, X], writes=[p1], inc=(j == 3))
                for j in range(4):
                    h = half * 4 + j
                    kb.op("pe", lambda e: e.matmul(p2[:T, j * T:(j + 1) * T], X[:T, h, :T], XT[:T, h, :T], start=True, stop=True),
                          reads=[XT, X], writes=[p2], inc=(j == 3))
                kb.op("act", lambda e: e.copy(out=Xn[:T, half * 4:half * 4 + 4, :T], in_=v3(p1)[:T]), reads=[p1], writes=[Xn])
                kb.op("dve", lambda e: e.tensor_copy(out=XTn[:T, half * 4:half * 4 + 4, :T], in_=v3(p2)[:T]), reads=[p2], writes=[XTn])
            for half in range(2):
                p3 = next_ps(g)
                for j in range(4):
                    h = half * 4 + j
                    kb.op("pe", lambda e: e.matmul(p3[:T, j * T:(j + 1) * T], XTn[:T, h, :T], Pm[:T, h, :T], start=True, stop=True),
                          reads=[XTn, Pm], writes=[p3], inc=(j == 3))
                kb.op("dve", lambda e: e.tensor_tensor(out=Pm[:T, half * 4:half * 4 + 4, :T], in0=Pm[:T, half * 4:half * 4 + 4, :T],
                                                       in1=v3(p3)[:T], op=ALU.add), reads=[Pm, p3], writes=[Pm])
            cur = 1 - cur
            lv *= 2
        if CUT <= 12:
            store_h(g, dst, ti, HT, final, None, (ss, rstd, junk))
            continue
        pP1 = next_ps(g)
        for h in range(8):
            c, hf = h // 2, h % 2
            pl = slice(hf * 64, hf * 64 + 64)
            kb.op("pe", lambda e: e.matmul(pP1[:T, h * 64:(h + 1) * 64], ATm[hf][:, c, :T], STb[:, c, :], start=True, stop=False),
                  reads=[ATm[hf], STb], writes=[pP1], inc=False)
            kb.op("pe", lambda e: e.matmul(pP1[:T, h * 64:(h + 1) * 64], AAK[:T, h, :T], VTM[:T, h, :], start=False, stop=True),
                  reads=[AAK, VTM], writes=[pP1], inc=(h == 7))
        kb.op("act", lambda e: e.copy(out=P1[:T, :], in_=pP1[:T, :]), reads=[pP1], writes=[P1])
        pU = next_ps(g)
        for h in range(8):
            kb.op("pe", lambda e: e.matmul(pU[:T, h * 64:(h + 1) * 64], Pm[:T, h, :T], P1[:T, h * 64:(h + 1) * 64], start=True, stop=True),
                  reads=[Pm, P1], writes=[pU], inc=(h == 7))
        kb.op("act", lambda e: e.copy(out=UTM[:T, :, :], in_=pU[:T, :].rearrange("p (h v) -> p h v", h=8)), reads=[pU], writes=[UTM])
        if CUT <= 13:
            store_h(g, dst, ti, HT, final, None, (ss, rstd, junk))
            continue
        pO = next_ps(g)
        for c in range(4):
            kb.op("pe", lambda e: e.matmul(pO[:, c * T:(c + 1) * T], STbd[:, c, :], AR[:, c, 1, :T], start=True, stop=False),
                  reads=[STbd, AR], writes=[pO], inc=False)
            for hf in range(2):
                h = 2 * c + hf
                pl = slice(hf * 64, hf * 64 + 64)
                kb.op("pe", lambda e: e.matmul(pO[pl, c * T:(c + 1) * T], UTM[:T, h, :], ARB[:T, h, :T], start=False, stop=False),
                      reads=[UTM, ARB], writes=[pO], inc=False)
                kb.op("pe", lambda e: e.matmul(pO[pl, c * T:(c + 1) * T], VTM[:T, h, :], ARK[:T, h, :T], start=False, stop=True),
                      reads=[VTM, ARK], writes=[pO], inc=(hf == 1))
        kb.op("act", lambda e: e.copy(out=Of[:, :, :T], in_=v3(pO)), reads=[pO], writes=[Of])
        pS = next_ps(g)
        for h in range(8):
            c, hf = h // 2, h % 2
            pl = slice(hf * 64, hf * 64 + 64)
            kb.op("pe", lambda e: e.matmul(pS[pl, c * 64:(c + 1) * 64], BHT[:T, h, :], UTM[:T, h, :], start=True, stop=False),
                  reads=[BHT, UTM], writes=[pS], inc=False)
            kb.op("pe", lambda e: e.matmul(pS[pl, c * 64:(c + 1) * 64], KHT[:T, h, :], VTM[:T, h, :], start=False, stop=True),
                  reads=[KHT, VTM], writes=[pS], inc=(h == 7))
        for c in range(4):
            kb.op("dve", lambda e: e.scalar_tensor_tensor(out=ST[:, c, :], in0=ST[:, c, :], scalar=eW[:, c, T - 1:T],
                                                          in1=pS[:, c * 64:(c + 1) * 64], op0=ALU.mult, op1=ALU.add),
                  reads=[ST, eW, pS], writes=[ST])
        kb.op("act", lambda e: e.copy(out=STb[:, :, :], in_=ST[:, :, :]), reads=[ST], writes=[STb])
        kb.op("act", lambda e: e.copy(out=STbd[0:64, :, 0:64], in_=ST[0:64, :, :]), reads=[ST], writes=[STbd])
        kb.op("act", lambda e: e.copy(out=STbd[64:128, :, 64:128], in_=ST[64:128, :, :]), reads=[ST], writes=[STbd])
        if CUT <= 14:
            store_h(g, dst, ti, HT, final, None, (ss, rstd, junk))
            continue
        kb.op("act", lambda e: e.activation(out=Osq[:, :, :T], in_=Of[:, :, :T], func=AF.Square), reads=[Of], writes=[Osq])
        pm_ = next_ps(g); pq_ = next_ps(g)
        for c in range(4):
            kb.op("pe", lambda e: e.matmul(pm_[:, c * T:(c + 1) * T], blk64[:, :], Of[:, c, :T], start=True, stop=True),
                  reads=[blk64, Of], writes=[pm_], inc=(c == 3))
        for c in range(4):
            kb.op("pe", lambda e: e.matmul(pq_[:, c * T:(c + 1) * T], blk64[:, :], Osq[:, c, :T], start=True, stop=True),
                  reads=[blk64, Osq], writes=[pq_], inc=(c == 3))
        kb.op("act", lambda e: e.copy(out=mean_s[:, :, :T], in_=v3(pm_)), reads=[pm_], writes=[mean_s])
        kb.op("dve", lambda e: e.scalar_tensor_tensor(out=var[:, :, :T], in0=mean_s[:, :, :T], scalar=-1.0, in1=mean_s[:, :, :T],
                                                      op0=ALU.mult, op1=ALU.mult), reads=[mean_s], writes=[var])
        kb.op("dve", lambda e: e.tensor_tensor(out=var[:, :, :T], in0=var[:, :, :T], in1=v3(pq_), op=ALU.add),
              reads=[var, pq_], writes=[var])
        kb.op("dve", lambda e: e.tensor_scalar(out=var[:, :, :T], in0=var[:, :, :T], scalar1=0.0, scalar2=None, op0=ALU.max),
              reads=[var], writes=[var])
        kb.op("act", lambda e: e.activation(out=var[:, :, :T], in_=var[:, :, :T], func=AF.Sqrt, bias=64e-5), reads=[var], writes=[var])
        kb.op("dve", lambda e: e.reciprocal(out=var[:, :, :T], in_=var[:, :, :T]), reads=[var], writes=[var])
        kb.op("dve", lambda e: e.tensor_tensor(out=Of[:, :, :T], in0=Of[:, :, :T], in1=mean_s[:, :, :T], op=ALU.subtract),
              reads=[Of, mean_s], writes=[Of])
        kb.op("dve", lambda e: e.tensor_tensor(out=Of[:, :, :T], in0=Of[:, :, :T], in1=var[:, :, :T], op=ALU.mult),
              reads=[Of, var], writes=[Of])
        kb.op("dve", lambda e: e.tensor_tensor(out=Of[:, :, :T], in0=Of[:, :, :T], in1=b3(LNW, T), op=ALU.mult),
              reads=[Of, LNW], writes=[Of])
        kb.op("dve", lambda e: e.tensor_tensor(out=Of[:, :, :T], in0=Of[:, :, :T], in1=b3(LNB, T), op=ALU.add),
              reads=[Of, LNB], writes=[Of])
        kb.op("dve", lambda e: e.tensor_tensor(out=Of[:, :, :T], in0=Of[:, :, :T], in1=bonus[:, :, :T], op=ALU.add),
              reads=[Of, bonus], writes=[Of])
        kb.op("dve", lambda e: e.tensor_tensor(out=mixT[:, 4:8, :T], in0=Of[:, :, :T], in1=GG[:, :, :T], op=ALU.mult),
              reads=[Of, GG], writes=[mixT])
        if CUT <= 15:
            store_h(g, dst, ti, HT, final, None, (ss, rstd, junk))
            continue
        if os.environ.get('ZERO_M'):
            kb.op("dve", lambda e: e.memset(mixT[:, 0:4, :], 0.0), writes=[mixT])
        if os.environ.get('ZERO_R'):
            kb.op("dve", lambda e: e.tensor_scalar(out=mixT[:, 4:8, :T], in0=mixT[:, 4:8, :T], scalar1=0.0, scalar2=None, op0=ALU.mult), reads=[mixT], writes=[mixT])
        for nb in range(2):
            pp = next_ps(g)
            for c in range(8):
                kb.op("pe", lambda e: e.matmul(pp[:T, :], mixT[:, c, :T], Wout[:, c, nb * 512:(nb + 1) * 512],
                                               start=(c == 0), stop=(c == 7)), reads=[mixT, Wout], writes=[pp], inc=(c == 7))
            kb.op("dve", lambda e: e.tensor_tensor(out=HT[:T, nb * 512:(nb + 1) * 512], in0=HT[:T, nb * 512:(nb + 1) * 512],
                                                   in1=pp[:T, :], op=ALU.add), reads=[HT, pp], writes=[HT])
        store_h(g, dst, ti, HT, final, None, (ss, rstd, junk))


LG = [float(np.log(1.0 - 2.0 ** (-5.0 - h))) for h in range(4)]
TWO_PI = 6.283185307179586
CW1 = 6.28125
CW2 = TWO_PI - CW1


LG = [float(np.log(1.0 - 2.0 ** (-5.0 - h))) for h in range(4)]
TWO_PI = 6.283185307179586
CW1 = 6.28125
CW2 = TWO_PI - CW1


LG = [float(np.log(1.0 - 2.0 ** (-5.0 - h))) for h in range(4)]
TWO_PI = 6.283185307179586
CW1 = 6.28125
CW2 = TWO_PI - CW1


def phase_l1(g, src, dst, final):
    kb, nc, dr = g.kb, g.nc, g.dr
    Win = kb.sb([128, 8, 6144], BF16, "Win")
    Wout = kb.sb([128, 16, D], BF16, "Wout")
    with contextlib.ExitStack() as ses:
        old = kb.es
        kb.es = ses
        stg = [kb.sb([128, 1536], F32, f"stg{i}") for i in range(3)]
        load_weight_bf16(g, dr["o_w_in_p"], 0, D, 6144, Win, stg)
        load_weight_bf16(g, dr["o_w_out"], 0, 2048, D, Wout, stg)
        kb.barrier()
        kb.es = old
    Gb = kb.sb([128, D], BF16, "Gb")
    Gfin = None
    iota = kb.sb([128, 128], F32, "iota")
    pidx = kb.sb([128, 1], F32, "pidx")
    ue = kb.sb([128, 128], F32, "ue")
    inv = kb.sb([128, 1], F32, "inv")
    kb.dma(iota[:, :], dr["c_iota"].ap()[:, :], writes=[iota], sem_buf=iota)
    kb.dma(pidx[:, :], dr["c_pidx"].ap()[:, :], writes=[pidx], sem_buf=pidx)
    kb.dma(ue[:, :], dr["c_ue"].ap()[:, :], writes=[ue], sem_buf=ue)
    kb.dma(inv[:, :], dr["c_inv"].ap()[:, :], writes=[inv], sem_buf=inv)
    DM = kb.sb([128, 4, 128], F32, "DM")
    DEC = kb.sb([128, 4, 128], F32, "DEC")
    KDEC = {128: kb.sb([128, 4], F32, "KDEC128"), 16: kb.sb([128, 4], F32, "KDEC16")}
    tms = kb.sb([128, 128], F32, "tms")
    kb.op("dve", lambda e: e.tensor_scalar(out=tms[:, :], in0=iota[:, :], scalar1=pidx[:, 0:1], scalar2=0.0,
                                           op0=ALU.subtract, op1=ALU.max), reads=[iota, pidx], writes=[tms])
    for h in range(4):
        kb.op("act", lambda e: e.activation(out=DM[:, h, :], in_=tms[:, :], func=AF.Exp, scale=LG[h]),
              reads=[tms], writes=[DM])
        kb.op("dve", lambda e: e.scalar_tensor_tensor(out=DM[:, h, :], in0=DM[:, h, :], scalar=1.0 / 16.0,
                                                      in1=ue[:, :], op0=ALU.mult, op1=ALU.mult),
              reads=[DM, ue], writes=[DM])
        kb.op("act", lambda e: e.activation(out=DEC[:, h, :], in_=iota[:, :], func=AF.Exp, scale=LG[h], bias=LG[h]),
              reads=[iota], writes=[DEC])
        for TT in (128, 16):
            kd = KDEC[TT]
            kb.op("act", lambda e: e.activation(out=kd[:, h:h + 1], in_=pidx[:, 0:1], func=AF.Exp, scale=-LG[h],
                                                bias=LG[h] * (TT - 1)), reads=[pidx], writes=[kd])
            kb.op("dve", lambda e: e.tensor_scalar(out=kd[:, h:h + 1], in0=kd[:, h:h + 1], scalar1=1.0 / 16.0,
                                                   scalar2=None, op0=ALU.mult), reads=[kd], writes=[kd])
    Sr = kb.sb([128, 8, 512], F32, "Sr")
    Srb = kb.sb([128, 8, 512], BF16, "Srb")
    kb.op("dve", lambda e: e.memset(Sr[:, :, :], 0.0), writes=[Sr])
    kb.op("pool", lambda e: e.memset(Srb[:, :, :], 0.0), writes=[Srb])
    HTs = [kb.sb([128, D], F32, f"HT{i}") for i in range(2)]
    hn = kb.sb([128, D], BF16, "hn")
    hnT = kb.sb([128, 8, 128], BF16, "hnT")
    ss = kb.sb([128, 1], F32, "ss")
    rstd = kb.sb([128, 1], F32, "rstd")
    ang = kb.sb([128, 128], F32, "ang")
    ang2 = kb.sb([128, 128], F32, "ang2")
    kf = kb.sb([128, 128], F32, "kf")
    ki = kb.sb([128, 128], I32, "ki")
    nsin = kb.sb([128, 128], F32, "nsin")
    ncos = kb.sb([128, 128], F32, "ncos")
    t1 = kb.sb([128, 4, 128], F32, "t1")
    t2 = kb.sb([128, 4, 128], F32, "t2")
    qb = kb.sb([128, 2, 4, 128], BF16, "qb")
    qdb = kb.sb([128, 2, 4, 128], BF16, "qdb")
    kbf = kb.sb([128, 2, 4, 128], BF16, "kbf")
    kdT = kb.sb([128, 8, 128], BF16, "kdT")
    sTm = kb.sb([128, 4, 128], BF16, "sTm")
    VT = kb.sb([128, 2048], BF16, "VT")
    GS = kb.sb([128, 2048], BF16, "GS")
    og = kb.sb([128, 2048], BF16, "og")
    ogT = kb.sb([128, 16, 128], BF16, "ogT")
    st6 = kb.sb([128, 6], F32, "st6")
    mv = kb.sb([128, 2], F32, "mv")
    rs = kb.sb([128, 1], F32, "rs")
    junk = og
    kb.dma(t1[:, :, :].rearrange("p a b -> p (a b)"), bc_rows(dr["norm_mix"], 1, 512), writes=[t1], sem_buf=t1)
    kb.op("dve", lambda e: e.tensor_copy(out=Gb[:, 0:512], in_=t1[:, :, :].rearrange("p a b -> p (a b)")), reads=[t1], writes=[Gb])
    kb.dma(t2[:, :, :].rearrange("p a b -> p (a b)"), bc_rows(dr["norm_mix"], 1, 512, col0=512), writes=[t2], sem_buf=t2)
    kb.op("dve", lambda e: e.tensor_copy(out=Gb[:, 512:1024], in_=t2[:, :, :].rearrange("p a b -> p (a b)")), reads=[t2], writes=[Gb])

    def sincos(dst_tbl, shift, pos0, T):
        kb.op("dve", lambda e: e.tensor_scalar(out=ang[:, :T], in0=iota[:, :T], scalar1=float(pos0), scalar2=inv[:, 0:1],
                                               op0=ALU.add, op1=ALU.mult), reads=[iota, inv], writes=[ang])
        if shift != 0.0:
            kb.op("dve", lambda e: e.tensor_scalar(out=ang[:, :T], in0=ang[:, :T], scalar1=shift, scalar2=None,
                                                   op0=ALU.add), reads=[ang], writes=[ang])
        kb.op("dve", lambda e: e.tensor_scalar(out=ki[:, :T], in0=ang[:, :T], scalar1=1.0 / TWO_PI, scalar2=None,
                                               op0=ALU.mult), reads=[ang], writes=[ki])
        kb.op("dve", lambda e: e.tensor_copy(out=kf[:, :T], in_=ki[:, :T]), reads=[ki], writes=[kf])
        kb.op("dve", lambda e: e.scalar_tensor_tensor(out=ang2[:, :T], in0=kf[:, :T], scalar=-CW1, in1=ang[:, :T],
                                                      op0=ALU.mult, op1=ALU.add), reads=[kf, ang], writes=[ang2])
        kb.op("dve", lambda e: e.scalar_tensor_tensor(out=ang2[:, :T], in0=kf[:, :T], scalar=-CW2, in1=ang2[:, :T],
                                                      op0=ALU.mult, op1=ALU.add), reads=[kf, ang2], writes=[ang2])
        kb.op("dve", lambda e: e.tensor_scalar(out=ang2[:, :T], in0=ang2[:, :T], scalar1=3.1415925, scalar2=-3.1415925,
                                               op0=ALU.min, op1=ALU.max), reads=[ang2], writes=[ang2])
        kb.op("act", lambda e: e.activation(out=dst_tbl[:, :T], in_=ang2[:, :T], func=AF.Sin),
              reads=[ang2], writes=[dst_tbl])

    for ti, (r0, T) in enumerate(g.tiles):
        HT = HTs[ti % 2]
        HO = HT
        if ti == 0:
            load_h(g, src, 0, HT)
        rmsnorm_T(g, HT, T, Gb, hn, hnT, ss, rstd, junk)
        if ti + 1 < len(g.tiles):
            load_h(g, src, ti + 1, HTs[(ti + 1) % 2])
        sincos(nsin, 0.0, r0, T)
        sincos(ncos, np.pi / 2, r0, T)
        sb_ = nsin[:, :T].unsqueeze(1).to_broadcast([128, 4, T])
        cb_ = ncos[:, :T].unsqueeze(1).to_broadcast([128, 4, T])
        for which in range(2):
            pe_ = next_ps(g)
            po_ = next_ps(g)
            for eo, pb in ((0, pe_), (1, po_)):
                for h in range(4):
                    col = which * 1024 + h * 256 + eo * 128
                    for kc in range(8):
                        kb.op("pe", lambda e: e.matmul(pb[:, h * T:(h + 1) * T], Win[:, kc, col:col + 128],
                                                       hnT[:, kc, :T], start=(kc == 0), stop=(kc == 7)),
                              reads=[Win, hnT], writes=[pb], inc=(kc == 7))
            pe3 = pe_[:, 0:4 * T].rearrange("p (h t) -> p h t", h=4)
            po3 = po_[:, 0:4 * T].rearrange("p (h t) -> p h t", h=4)
            kb.op("dve", lambda e: e.tensor_tensor(out=t1[:, :, :T], in0=pe3, in1=cb_, op=ALU.mult),
                  reads=[pe_, ncos], writes=[t1])
            kb.op("dve", lambda e: e.tensor_tensor(out=t2[:, :, :T], in0=po3, in1=sb_, op=ALU.mult),
                  reads=[po_, nsin], writes=[t2])
            dstb = qb if which == 0 else kbf
            kb.op("dve", lambda e: e.tensor_tensor(out=dstb[:, 0, :, :T], in0=t1[:, :, :T], in1=t2[:, :, :T],
                                                   op=ALU.subtract), reads=[t1, t2], writes=[dstb])
            kb.op("dve", lambda e: e.tensor_tensor(out=t1[:, :, :T], in0=po3, in1=cb_, op=ALU.mult),
                  reads=[po_, ncos], writes=[t1])
            kb.op("dve", lambda e: e.tensor_tensor(out=t2[:, :, :T], in0=pe3, in1=sb_, op=ALU.mult),
                  reads=[pe_, nsin], writes=[t2])
            kb.op("dve", lambda e: e.tensor_tensor(out=dstb[:, 1, :, :T], in0=t1[:, :, :T], in1=t2[:, :, :T],
                                                   op=ALU.add), reads=[t1, t2], writes=[dstb])
            if which == 0:
                for eo in range(2):
                    kb.op("pool", lambda e: e.tensor_tensor(out=qdb[:, eo, :, :T], in0=qb[:, eo, :, :T],
                                                            in1=DEC[:, :, :T], op=ALU.mult),
                          reads=[qb, DEC], writes=[qdb])
        PST = g.PST
        for h in range(4):
            for eo in range(2):
                j = h * 2 + eo
                kb.op("pe", lambda e: e.transpose(out=PST[:T, j * 128:(j + 1) * 128], in_=kbf[:, eo, h, :T],
                                                  identity=g.ident_b[:, :]),
                      reads=[kbf, g.ident_b], writes=[PST], inc=(j == 7))
        for h in range(4):
            kb.op("act", lambda e: e.activation(out=kdT[:T, 2 * h:2 * h + 2, :],
                                                in_=PST[:T, 2 * h * 128:(2 * h + 2) * 128].rearrange("p (j d) -> p j d", j=2),
                                                func=AF.Copy, scale=KDEC[T][:T, h:h + 1]),
                  reads=[PST, KDEC[T]], writes=[kdT])
        psc = next_ps(g)
        for h in range(4):
            for eo in range(2):
                kb.op("pe", lambda e: e.matmul(psc[:T, h * T:(h + 1) * T], kbf[:, eo, h, :T], qb[:, eo, h, :T],
                                               start=(eo == 0), stop=(eo == 1)),
                      reads=[kbf, qb], writes=[psc], inc=(eo == 1))
        kb.op("dve", lambda e: e.tensor_tensor(out=sTm[:T, :, :T],
                                               in0=psc[:T, 0:4 * T].rearrange("p (h t) -> p h t", h=4),
                                               in1=DM[:T, :, :T], op=ALU.mult), reads=[psc, DM], writes=[sTm])
        for nb in range(4):
            pvv = next_ps(g)
            for kc in range(8):
                kb.op("pe", lambda e: e.matmul(pvv[:T, :], hnT[:, kc, :T], Win[:, kc, 2048 + nb * 512:2048 + (nb + 1) * 512],
                                               start=(kc == 0), stop=(kc == 7)), reads=[hnT, Win], writes=[pvv], inc=(kc == 7))
            kb.op("act", lambda e: e.copy(out=VT[:T, nb * 512:(nb + 1) * 512], in_=pvv[:T, :]), reads=[pvv], writes=[VT])
        for nb in range(4):
            pgg = next_ps(g)
            for kc in range(8):
                kb.op("pe", lambda e: e.matmul(pgg[:T, :], hnT[:, kc, :T], Win[:, kc, 4096 + nb * 512:4096 + (nb + 1) * 512],
                                               start=(kc == 0), stop=(kc == 7)), reads=[hnT, Win], writes=[pgg], inc=(kc == 7))
            kb.op("act", lambda e: e.activation(out=GS[:T, nb * 512:(nb + 1) * 512], in_=pgg[:T, :], func=AF.Silu),
                  reads=[pgg], writes=[GS])
        for h in range(4):
            po = next_ps(g)
            kb.op("pe", lambda e: e.matmul(po[:T, :], sTm[:T, h, :T], VT[:T, h * 512:(h + 1) * 512], start=True, stop=False),
                  reads=[sTm, VT], writes=[po], inc=False)
            for eo in range(2):
                kb.op("pe", lambda e: e.matmul(po[:T, :], qdb[:, eo, h, :T], Srb[:, 2 * h + eo, :], start=False, stop=(eo == 1)),
                      reads=[qdb, Srb], writes=[po], inc=(eo == 1))
            kb.op("dve", lambda e: e.bn_stats(out=st6[:T, :], in_=po[:T, :]), reads=[po], writes=[st6])
            kb.op("dve", lambda e: e.bn_aggr(out=mv[:T, :], in_=st6[:T, :]), reads=[st6], writes=[mv])
            kb.op("act", lambda e: e.activation(out=rs[:T, :], in_=mv[:T, 1:2], func=AF.Sqrt, scale=1.0, bias=1e-6),
                  reads=[mv], writes=[rs])
            kb.op("dve", lambda e: e.reciprocal(out=rs[:T, :], in_=rs[:T, :]), reads=[rs], writes=[rs])
            kb.op("dve", lambda e: e.tensor_scalar(out=og[:T, h * 512:(h + 1) * 512], in0=po[:T, :], scalar1=mv[:T, 0:1], scalar2=rs[:T, 0:1],
                                                   op0=ALU.subtract, op1=ALU.mult), reads=[po, mv, rs], writes=[og])
            kb.op("pool", lambda e: e.tensor_tensor(out=og[:T, h * 512:(h + 1) * 512], in0=og[:T, h * 512:(h + 1) * 512],
                                                    in1=GS[:T, h * 512:(h + 1) * 512], op=ALU.mult),
                  reads=[og, GS], writes=[og])
        gT = [float(np.exp(LG[h] * T)) for h in range(4)]
        for h in range(4):
            for eo in range(2):
                j = 2 * h + eo
                pst_ = next_ps(g)
                kb.op("pe", lambda e: e.matmul(pst_[:, :], kdT[:T, j, :], VT[:T, h * 512:(h + 1) * 512], start=True, stop=True),
                      reads=[kdT, VT], writes=[pst_])
                kb.op("dve", lambda e: e.scalar_tensor_tensor(out=Sr[:, j, :], in0=Sr[:, j, :], scalar=gT[h], in1=pst_[:, :],
                                                              op0=ALU.mult, op1=ALU.add), reads=[Sr, pst_], writes=[Sr])
                kb.op("act", lambda e: e.copy(out=Srb[:, j, :], in_=Sr[:, j, :]), reads=[Sr], writes=[Srb])
        for half in range(2):
            for j in range(8):
                c = half * 8 + j
                kb.op("pe", lambda e: e.transpose(out=PST[:, j * T:(j + 1) * T], in_=og[:T, c * 128:(c + 1) * 128],
                                                  identity=g.ident_b[:T, :T]),
                      reads=[og, g.ident_b], writes=[PST], inc=(j == 7))
            kb.op("act", lambda e: e.copy(out=ogT[:, half * 8:(half + 1) * 8, :T],
                                          in_=PST[:, 0:8 * T].rearrange("p (k t) -> p k t", k=8)),
                  reads=[PST], writes=[ogT])
        for nb in range(2):
            pp = next_ps(g)
            for c in range(16):
                kb.op("pe", lambda e: e.matmul(pp[:T, :], ogT[:, c, :T], Wout[:, c, nb * 512:(nb + 1) * 512],
                                               start=(c == 0), stop=(c == 15)), reads=[ogT, Wout], writes=[pp], inc=(c == 15))
            kb.op("dve", lambda e: e.tensor_tensor(out=HO[:T, nb * 512:(nb + 1) * 512],
                                                   in0=HT[:T, nb * 512:(nb + 1) * 512], in1=pp[:T, :], op=ALU.add),
                  reads=[HT, pp], writes=[HO])
        store_h(g, dst, ti, HO, final, Gfin, (ss, rstd, junk))


def make_in_map(inputs, b, NT):
    m = {"x": np.ascontiguousarray(inputs["x"][b, :128 * NT])}
    for k, shp in W_SPECS.items():
        src_k = "o_w_in" if k == "o_w_in_p" else k
        m[k] = np.ascontiguousarray(np.asarray(inputs[src_k], np.float32).reshape(shp))
    m.update(host_consts())
    perm = np.arange(6144)
    for sec in range(2):
        for h in range(4):
            base = sec * 1024 + h * 256
            perm[base:base + 256] = np.concatenate([base + np.arange(0, 256, 2), base + np.arange(1, 256, 2)])
    m["o_w_in_p"] = np.ascontiguousarray(m["o_w_in_p"][:, perm])
    return m


NT_FULL = 32


def kernel(**inputs):
    nc = build(NT_FULL, phases=(1, 2, 3, 4), debug=False, final=True)
    in_maps = [make_in_map(inputs, b, NT_FULL) for b in range(8)]
    res = run_bass_kernel_spmd(nc, in_maps, core_ids=list(range(8)))
    return np.stack([np.asarray(r["out"], np.float32) for r in res.results], axis=0)
```

```python
import contextlib
import numpy as np
import concourse.bass as bass
import concourse.mybir as mybir

F32 = mybir.dt.float32
BF16 = mybir.dt.bfloat16
I32 = mybir.dt.int32
AF = mybir.ActivationFunctionType
ALU = mybir.AluOpType
AX = mybir.AxisListType


class Buf:
    __slots__ = ("t", "w", "r", "dsem", "dcount", "name", "excl")

    def __init__(self, t, name=""):
        self.t = t
        self.w = {}
        self.r = {}
        self.dsem = None
        self.dcount = 0
        self.name = name
        self.excl = False

    def __getitem__(self, idx):
        return self.t[idx]


class Eng:
    def __init__(self, name, obj, sem):
        self.name = name
        self.obj = obj
        self.sem = sem
        self.count = 0
        self.seen = {}


class KB:
    def __init__(self, nc, es):
        self.nc = nc
        self.es = es
        self.sems = {}
        self.E = {}
        for name, obj in (("pe", nc.tensor), ("act", nc.scalar), ("dve", nc.vector),
                          ("pool", nc.gpsimd), ("sp", nc.sync)):
            sem = es.enter_context(nc.semaphore("s_" + name))
            self.E[name] = Eng(name, obj, sem)
            self.sems[id(sem)] = sem
        self.dma_tokens = {}
        self.nbuf = 0

    def sb(self, shape, dt, name=None):
        self.nbuf += 1
        name = f"{name or 'b'}_{self.nbuf}"
        t = self.es.enter_context(self.nc.sbuf_tensor(name, list(shape), dt))
        return Buf(t, name)

    def ps(self, shape, dt, name=None):
        self.nbuf += 1
        name = f"{name or 'p'}_{self.nbuf}"
        t = self.es.enter_context(self.nc.psum_tensor(name, list(shape), dt))
        b = Buf(t, name)
        b.excl = True
        return b

    def newsem(self, name):
        sem = self.es.enter_context(self.nc.semaphore(name))
        self.sems[id(sem)] = sem
        return sem

    def _wait(self, e, deps):
        for sid, val in deps.items():
            if e.seen.get(sid, 0) < val:
                e.obj.wait_ge(self.sems[sid], val)
                e.seen[sid] = val

    def _deps(self, e, reads, writes):
        deps = {}
        own = id(e.sem)
        for b in reads:
            for sid, v in b.w.items():
                if deps.get(sid, 0) < v:
                    deps[sid] = v
            if b.excl:
                for sid, v in b.r.items():
                    if sid != own and deps.get(sid, 0) < v:
                        deps[sid] = v
        for b in writes:
            for d in (b.w, b.r):
                for sid, v in d.items():
                    if sid == own:
                        continue
                    if deps.get(sid, 0) < v:
                        deps[sid] = v
        return deps

    def op(self, eng, fn, reads=(), writes=(), inc=True):
        e = self.E[eng]
        self._wait(e, self._deps(e, reads, writes))
        ins = fn(e.obj)
        if inc:
            e.count += 1
            ins.then_inc(e.sem, 1)
            val = e.count
        else:
            val = e.count + 1
        sid = id(e.sem)
        for b in reads:
            if b.r.get(sid, 0) < val:
                b.r[sid] = val
        for b in writes:
            if b.w.get(sid, 0) < val:
                b.w[sid] = val
        return ins

    def dma(self, out_ap, in_ap, reads=(), writes=(), sem_buf=None, q="sp"):
        e = self.E[q]
        self._wait(e, self._deps(e, reads, writes))
        b = sem_buf
        if b.dsem is None:
            b.dsem = self.newsem("d_" + b.name)
        b.dcount += 16
        e.obj.dma_start(out=out_ap, in_=in_ap).then_inc(b.dsem, 16)
        sid = id(b.dsem)
        for x in reads:
            x.r[sid] = b.dcount
        for x in writes:
            x.w[sid] = b.dcount
        self.dma_tokens[sid] = b.dcount

    def barrier(self):
        targets = {id(e.sem): e.count for e in self.E.values() if e.count > 0}
        targets.update(self.dma_tokens)
        for e in self.E.values():
            self._wait(e, {k: v for k, v in targets.items() if k != id(e.sem)})

    def final_wait(self):
        e = self.E["sp"]
        self._wait(e, dict(self.dma_tokens))


from concourse.bass_utils import run_bass_kernel_spmd

D = 1024
NMETA = 16
DFF = 2816
NFC = DFF // 128

W_SPECS = {
    "meta_tokens": (16, 1024), "norm_mix": (2, 1024), "norm_ffn": (2, 1024), "norm_final": (1, 1024),
    "e_w_in": (1024, 3848), "e_w_out": (1024, 1024), "m_b_i": (1, 4), "m_b_f": (1, 4), "m_norm": (1, 512),
    "r_mu": (1, 1792), "r_w0": (1, 512), "r_w2": (64, 512), "r_a0": (1, 512), "r_a2": (64, 512),
    "r_g2": (128, 512), "r_k_k": (1, 512), "r_k_a": (1, 512), "r_r_k": (1, 512), "r_ln_w": (1, 512),
    "r_ln_b": (1, 512), "o_w_in_p": (1024, 6144), "o_w_out": (2048, 1024), "f_w_up": (2048, 5632),
    "f_conv_w": (6, 2816), "f_conv_b": (2, 2816), "f_w_down": (5632, 1024),
}


def host_consts():
    c = {}
    c["c_ident"] = np.eye(128, dtype=np.float32)
    i = np.arange(128)
    c["c_ue"] = (i[:, None] <= i[None, :]).astype(np.float32)
    c["c_su"] = (i[:, None] < i[None, :]).astype(np.float32)
    c["c_iota"] = np.broadcast_to(np.arange(128, dtype=np.float32)[None, :], (128, 128)).copy()
    c["c_pidx"] = np.arange(128, dtype=np.float32)[:, None].copy()
    bo = np.zeros((128, 128), np.float32)
    bo[:64, :64] = 1.0
    bo[64:, 64:] = 1.0
    c["c_blk"] = bo
    c["c_inv"] = (np.float32(1.0) / np.power(np.float32(10000.0), np.linspace(0.0, 1.0, 128, dtype=np.float32))
                  ).astype(np.float32)[:, None].copy()
    return c


class Ctx:
    pass


def tile_rows(NT):
    tiles = [(0, NMETA)]
    for i in range(NT):
        tiles.append((NMETA + 128 * i, 128))
    return tiles


def build(NT, phases=(1, 2, 3, 4), debug=False, final=True):
    nc = bass.Bass("TRN2", target_bir_lowering=False)
    SEQ = 128 * NT
    L = NMETA + SEQ
    dr = {}
    dr["x"] = nc.dram_tensor("x", [SEQ, D], F32, kind="ExternalInput")
    for k, shp in W_SPECS.items():
        dr[k] = nc.dram_tensor(k, list(shp), F32, kind="ExternalInput")
    for k, v in host_consts().items():
        dr[k] = nc.dram_tensor(k, list(v.shape), F32, kind="ExternalInput")
    out = nc.dram_tensor("out", [SEQ, D], F32, kind="ExternalOutput")
    H = {}
    for i in (1, 2, 3):
        H[i] = nc.dram_tensor(f"H{i}", [L, D], F32, kind=("ExternalOutput" if debug else "Internal"))

    tiles = tile_rows(NT)
    es = contextlib.ExitStack()
    with es:
        kb = KB(nc, es)
        PS = [kb.ps([128, 512], F32, f"psb{i}") for i in range(7)]
        PST = kb.ps([128, 1024], BF16, "pstr")
        g = Ctx()
        g.nc, g.kb, g.dr, g.H, g.out, g.tiles, g.PS, g.PST = nc, kb, dr, H, out, tiles, PS, PST
        g.psi = 0
        g.ident_f = kb.sb([128, 128], F32, "ident_f")
        g.ident_b = kb.sb([128, 128], BF16, "ident_b")
        kb.dma(g.ident_f[:, :], dr["c_ident"].ap()[:, :], writes=[g.ident_f], sem_buf=g.ident_f)
        kb.op("dve", lambda e: e.tensor_copy(out=g.ident_b[:, :], in_=g.ident_f[:, :]),
              reads=[g.ident_f], writes=[g.ident_b])

        plist = [p for p in (1, 2, 3, 4) if p in phases]
        src = 0
        for p in plist:
            dst = p if p != plist[-1] else 4
            with contextlib.ExitStack() as pes:
                kb.es = pes
                if p in (2, 4):
                    phase_ffn(g, layer=(0 if p == 2 else 1), src=src, dst=dst, final=final)
                elif p == 1:
                    phase_l0(g, src=src, dst=dst, final=final)
                elif p == 3:
                    phase_l1(g, src=src, dst=dst, final=final)
                kb.barrier()
            kb.es = es
            src = dst
        kb.final_wait()
    return nc


def next_ps(g):
    b = g.PS[g.psi % len(g.PS)]
    g.psi += 1
    return b


def bc_rows(handle, row, n, parts=128, col0=0, ncols_total=None):
    ncols_total = ncols_total if ncols_total is not None else handle.shape[1]
    return bass.AP(handle, row * ncols_total + col0, [[0, parts], [1, n]])


def load_h(g, src, ti, HT):
    kb = g.kb
    r0, T = g.tiles[ti]
    if src == 0:
        if ti == 0:
            ap = g.dr["meta_tokens"].ap()[0:NMETA, :]
        else:
            ap = g.dr["x"].ap()[r0 - NMETA:r0 - NMETA + T, :]
    else:
        ap = g.H[src].ap()[r0:r0 + T, :]
    kb.dma(HT[:T, :], ap, writes=[HT], sem_buf=HT)


def store_h(g, dst, ti, HO, final, Gfin=None, scratch=None):
    kb = g.kb
    r0, T = g.tiles[ti]
    if dst != 4:
        kb.dma(g.H[dst].ap()[r0:r0 + T, :], HO[:T, :], reads=[HO], sem_buf=HO)
        return
    if ti == 0:
        return
    if final:
        ss, rstd, junk = scratch
        kb.op("act", lambda e: e.activation(out=junk[:T, 0:D], in_=HO[:T, :], func=AF.Square, accum_out=ss[:T, :]),
              reads=[HO], writes=[junk, ss])
        rstd_from_ss(kb, ss, rstd, T, 1.0 / D, 1e-6)
        kb.op("dve", lambda e: e.scalar_tensor_tensor(out=HO[:T, :], in0=HO[:T, :], scalar=rstd[:T, :],
                                                      in1=Gfin[:T, :], op0=ALU.mult, op1=ALU.mult),
              reads=[HO, rstd, Gfin], writes=[HO])
    kb.dma(g.out.ap()[r0 - NMETA:r0 - NMETA + T, :], HO[:T, :], reads=[HO], sem_buf=HO)


def load_weight_bf16(g, dram_handle, row0, K, N, W, stg, col0=0, ncols_total=None):
    kb = g.kb
    SW = stg[0].t.shape[1]
    engs = ("dve", "act", "dve", "act", "dve", "pool", "dve", "act")
    cnt = getattr(g, "_lw_cnt", 0)
    for kc in range(K // 128):
        for j0 in range(0, N, SW):
            w = min(SW, N - j0)
            s = stg[cnt % len(stg)]
            kb.dma(s[:, :w], dram_handle.ap()[row0 + kc * 128: row0 + (kc + 1) * 128, col0 + j0: col0 + j0 + w],
                   writes=[s], sem_buf=s)
            en = engs[cnt % len(engs)]
            if en == "act":
                kb.op("act", lambda e: e.copy(out=W[:, kc, j0:j0 + w], in_=s[:, :w]), reads=[s], writes=[W])
            else:
                kb.op(en, lambda e: e.tensor_copy(out=W[:, kc, j0:j0 + w], in_=s[:, :w]), reads=[s], writes=[W])
            cnt += 1
    g._lw_cnt = cnt


def rstd_from_ss(kb, ss, rstd, T, scale, eps, ap_fn=None):
    a = (lambda b: b[:T, :]) if ap_fn is None else ap_fn
    kb.op("act", lambda e: e.activation(out=a(rstd), in_=a(ss), func=AF.Sqrt, scale=scale, bias=eps),
          reads=[ss], writes=[rstd])
    kb.op("dve", lambda e: e.reciprocal(out=a(rstd), in_=a(rstd)), reads=[rstd], writes=[rstd])


def load_vec_fm(g, handle, row, nch, dstbuf, dst_ap, vtmp, col0=0):
    kb = g.kb
    ncols = handle.shape[1]
    src = bass.AP(handle, row * ncols + col0, [[128, nch], [1, 128]])
    kb.dma(vtmp[:nch, :], src, writes=[vtmp], sem_buf=vtmp)
    pt = next_ps(g)
    kb.op("pe", lambda e: e.transpose(out=pt[:, :nch], in_=vtmp[:nch, :], identity=g.ident_f[:nch, :nch]),
          reads=[vtmp, g.ident_f], writes=[pt])
    kb.op("dve", lambda e: e.tensor_copy(out=dst_ap, in_=pt[:, :nch]), reads=[pt], writes=[dstbuf])


def rmsnorm_T(g, HT, T, Gb, hn, hnT, ss, rstd, junk):
    kb = g.kb
    kb.op("act", lambda e: e.activation(out=junk[:T, 0:D], in_=HT[:T, :], func=AF.Square, accum_out=ss[:T, :]),
          reads=[HT], writes=[junk, ss])
    rstd_from_ss(kb, ss, rstd, T, 1.0 / D, 1e-6)
    kb.op("dve", lambda e: e.scalar_tensor_tensor(out=hn[:T, :], in0=HT[:T, :], scalar=rstd[:T, :],
                                                  in1=Gb[:T, :], op0=ALU.mult, op1=ALU.mult),
          reads=[HT, rstd, Gb], writes=[hn])
    PST = g.PST
    for kc in range(8):
        kb.op("pe", lambda e: e.transpose(out=PST[:, kc * T:(kc + 1) * T], in_=hn[:T, kc * 128:(kc + 1) * 128],
                                          identity=g.ident_b[:T, :T]),
              reads=[hn, g.ident_b], writes=[PST], inc=(kc == 7))
    kb.op("act", lambda e: e.copy(out=hnT[:, :, :T], in_=PST[:, 0:8 * T].rearrange("p (k t) -> p k t", k=8)),
          reads=[PST], writes=[hnT])


def norm_stats(g, HT, T, Gb, hn, ss, rstd, junk):
    kb = g.kb
    kb.op("act", lambda e: e.activation(out=junk[:T, 0:D], in_=HT[:T, :], func=AF.Square, accum_out=ss[:T, :]),
          reads=[HT], writes=[junk, ss])
    rstd_from_ss(kb, ss, rstd, T, 1.0 / D, 1e-6)
    kb.op("dve", lambda e: e.scalar_tensor_tensor(out=hn[:T, :], in0=HT[:T, :], scalar=rstd[:T, :],
                                                  in1=Gb[:T, :], op0=ALU.mult, op1=ALU.mult),
          reads=[HT, rstd, Gb], writes=[hn])


def norm_transpose(g, hn, hnT, T):
    kb = g.kb
    PST = g.PST
    for kc in range(8):
        kb.op("pe", lambda e: e.transpose(out=PST[:, kc * T:(kc + 1) * T], in_=hn[:T, kc * 128:(kc + 1) * 128],
                                          identity=g.ident_b[:T, :T]),
              reads=[hn, g.ident_b], writes=[PST], inc=(kc == 7))
    kb.op("act", lambda e: e.copy(out=hnT[:, :, :T], in_=PST[:, 0:8 * T].rearrange("p (k t) -> p k t", k=8)),
          reads=[PST], writes=[hnT])


def phase_ffn(g, layer, src, dst, final):
    kb, nc, dr = g.kb, g.nc, g.dr
    Wup = kb.sb([128, 8, 2 * DFF], BF16, "Wup")
    Wdn = kb.sb([128, NFC, D], BF16, "Wdn")
    with contextlib.ExitStack() as ses:
        old = kb.es
        kb.es = ses
        stg = [kb.sb([128, 1408], F32, f"stg{i}") for i in range(3)]
        load_weight_bf16(g, dr["f_w_up"], layer * D, D, 2 * DFF, Wup, stg)
        load_weight_bf16(g, dr["f_w_down"], layer * DFF, DFF, D, Wdn, stg)
        kb.barrier()
        kb.es = old
    Gb = kb.sb([128, D], F32, "Gb")
    kb.dma(Gb[:, :], bc_rows(dr["norm_ffn"], layer, D), writes=[Gb], sem_buf=Gb)
    Gfin = None
    if dst == 4 and final:
        Gfin = kb.sb([128, D], F32, "Gfin")
        kb.dma(Gfin[:, :], bc_rows(dr["norm_final"], 0, D), writes=[Gfin], sem_buf=Gfin)
    CW = kb.sb([128, 3, NFC], F32, "CW")
    CB = kb.sb([128, NFC], F32, "CB")
    vtmp = kb.sb([32, 128], F32, "vtmp")
    for j in range(3):
        load_vec_fm(g, dr["f_conv_w"], layer * 3 + j, NFC, CW, CW[:, j, :], vtmp)
    load_vec_fm(g, dr["f_conv_b"], layer, NFC, CB, CB[:, :], vtmp)
    HTs = [kb.sb([128, D], F32, f"HT{i}") for i in range(3)]
    hns = [kb.sb([128, D], BF16, f"hn{i}") for i in range(2)]
    hnTs = [kb.sb([128, 8, 128], BF16, f"hnT{i}") for i in range(2)]
    junk = kb.sb([128, D], BF16, "junk")
    sss = [kb.sb([128, 1], F32, f"ss{i}") for i in range(3)]
    rstds = [kb.sb([128, 1], F32, f"rstd{i}") for i in range(3)]
    G = kb.sb([128, NFC, 130], F32, "G")
    ACC = [kb.sb([128, 4, 128], F32, f"acc{i}") for i in range(2)]
    SIL = [kb.sb([128, 4, 128], F32, f"sil{i}") for i in range(2)]
    ACTT = kb.sb([128, NFC, 128], BF16, "ACTT")
    kb.op("dve", lambda e: e.memset(G[:, :, :], 0.0), writes=[G])
    po_banks = [g.PS[5], g.PS[6]]
    rot = g.PS[0:5]
    rot_i = [0]

    def next_rot():
        b = rot[rot_i[0] % len(rot)]
        rot_i[0] += 1
        return b

    ntl = len(g.tiles)
    load_h(g, src, 0, HTs[0])
    norm_stats(g, HTs[0], g.tiles[0][1], Gb, hns[0], sss[0], rstds[0], junk)
    if ntl > 1:
        load_h(g, src, 1, HTs[1])
    norm_transpose(g, hns[0], hnTs[0], g.tiles[0][1])

    def down_part(c_lo, c_hi, T):
        for c in range(c_lo, c_hi):
            for nb in range(2):
                po = po_banks[nb]
                kb.op("pe", lambda e: e.matmul(po[:T, :], ACTT[:, c, :T], Wdn[:, c, nb * 512:(nb + 1) * 512],
                                               start=(c == 0), stop=(c == NFC - 1)),
                      reads=[ACTT, Wdn], writes=[po], inc=(c == c_hi - 1))

    for ti, (r0, T) in enumerate(g.tiles):
        HT = HTs[ti % 3]
        hnT = hnTs[ti % 2]
        steps = list(range(0, NFC, 4))
        for si, c0 in enumerate(steps):
            nch = min(4, NFC - c0)
            pg = next_rot()
            pv = next_rot()
            for j in range(nch):
                for kc in range(8):
                    kb.op("pe", lambda e: e.matmul(pg[:, j * T:(j + 1) * T],
                                                   Wup[:, kc, DFF + (c0 + j) * 128: DFF + (c0 + j + 1) * 128],
                                                   hnT[:, kc, :T], start=(kc == 0), stop=(kc == 7)),
                          reads=[Wup, hnT], writes=[pg], inc=(kc == 7))
            for j in range(nch):
                for kc in range(8):
                    kb.op("pe", lambda e: e.matmul(pv[:, j * T:(j + 1) * T],
                                                   Wup[:, kc, (c0 + j) * 128:(c0 + j + 1) * 128],
                                                   hnT[:, kc, :T], start=(kc == 0), stop=(kc == 7)),
                          reads=[Wup, hnT], writes=[pv], inc=(kc == 7))
            if si >= 1:
                down_part(steps[si - 1], c0, T)
            kb.op("act", lambda e: e.copy(out=G[:, c0:c0 + nch, 2:2 + T],
                                          in_=pg[:, 0:nch * T].rearrange("p (c t) -> p c t", c=nch)),
                  reads=[pg], writes=[G])
            acc = ACC[si % 2]
            sil = SIL[si % 2]
            for j in range(nch):
                c = c0 + j
                kb.op("dve", lambda e: e.tensor_scalar(out=acc[:, j, :T], in0=G[:, c, 2:2 + T],
                                                       scalar1=CW[:, 2, c:c + 1], scalar2=CB[:, c:c + 1],
                                                       op0=ALU.mult, op1=ALU.add),
                      reads=[G, CW, CB], writes=[acc])
                kb.op("dve", lambda e: e.scalar_tensor_tensor(out=acc[:, j, :T], in0=G[:, c, 1:1 + T],
                                                              scalar=CW[:, 1, c:c + 1], in1=acc[:, j, :T],
                                                              op0=ALU.mult, op1=ALU.add),
                      reads=[G, CW, acc], writes=[acc])
                kb.op("dve", lambda e: e.scalar_tensor_tensor(out=acc[:, j, :T], in0=G[:, c, 0:T],
                                                              scalar=CW[:, 0, c:c + 1], in1=acc[:, j, :T],
                                                              op0=ALU.mult, op1=ALU.add),
                      reads=[G, CW, acc], writes=[acc])
            kb.op("act", lambda e: e.activation(out=sil[:, 0:nch, :T], in_=acc[:, 0:nch, :T], func=AF.Silu),
                  reads=[acc], writes=[sil])
            kb.op("dve", lambda e: e.tensor_tensor(out=ACTT[:, c0:c0 + nch, :T], in0=sil[:, 0:nch, :T],
                                                   in1=pv[:, 0:nch * T].rearrange("p (c t) -> p c t", c=nch),
                                                   op=ALU.mult),
                  reads=[sil, pv], writes=[ACTT])
        if ti + 1 < ntl:
            Tn = g.tiles[ti + 1][1]
            norm_stats(g, HTs[(ti + 1) % 3], Tn, Gb, hns[(ti + 1) % 2], sss[(ti + 1) % 3], rstds[(ti + 1) % 3], junk)
        down_part(steps[-1], NFC, T)
        kb.op("dve", lambda e: e.tensor_copy(out=G[:, :, 0:2], in_=G[:, :, T:T + 2]), reads=[G], writes=[G])
        if ti + 1 < ntl:
            norm_transpose(g, hns[(ti + 1) % 2], hnTs[(ti + 1) % 2], g.tiles[ti + 1][1])
        for nb in range(2):
            kb.op("dve", lambda e: e.tensor_tensor(out=HT[:T, nb * 512:(nb + 1) * 512],
                                                   in0=HT[:T, nb * 512:(nb + 1) * 512], in1=po_banks[nb][:T, :], op=ALU.add),
                  reads=[HT, po_banks[nb]], writes=[HT])
        store_h(g, dst, ti, HT, final, Gfin, (sss[2 - ti % 2 if False else (ti + 2) % 3], rstds[(ti + 2) % 3], junk))
        if ti + 2 < ntl:
            load_h(g, src, ti + 2, HTs[(ti + 2) % 3])


EH = 0.6065306597126334
ISQ = 0.08838834764831845
NEGBIG = -30000.0


def phase_l0(g, src, dst, final):
    import os
    CUT = int(os.environ.get('CUT', '99'))
    SUB = int(os.environ.get('SUB', '99'))
    HFN = int(os.environ.get('HFN', '2'))
    kb, nc, dr = g.kb, g.nc, g.dr
    Win = kb.sb([128, 8, 3848], BF16, "Win")
    Wout = kb.sb([128, 8, D], BF16, "Wout")
    W2A = kb.sb([128, 512], BF16, "W2A")
    G2 = kb.sb([128, 512], BF16, "G2")
    with contextlib.ExitStack() as ses:
        old = kb.es
        kb.es = ses
        stg = [kb.sb([128, 1924], F32, f"stg{i}") for i in range(3)]
        load_weight_bf16(g, dr["e_w_in"], 0, D, 3848, Win, stg)
        load_weight_bf16(g, dr["e_w_out"], 0, D, D, Wout, stg)
        s0 = stg[0]
        kb.dma(s0[0:64, 0:512], dr["r_w2"].ap()[:, :], writes=[s0], sem_buf=s0)
        kb.dma(s0[64:128, 0:512], dr["r_a2"].ap()[:, :], writes=[s0], sem_buf=s0)
        kb.op("dve", lambda e: e.tensor_copy(out=W2A[:, :], in_=s0[:, 0:512]), reads=[s0], writes=[W2A])
        s1 = stg[1]
        kb.dma(s1[:, 0:512], dr["r_g2"].ap()[:, :], writes=[s1], sem_buf=s1)
        kb.op("dve", lambda e: e.tensor_copy(out=G2[:, :], in_=s1[:, 0:512]), reads=[s1], writes=[G2])
        kb.barrier()
        kb.es = old
    F = lambda shape, name: kb.sb(shape, F32, name)
    Bf = lambda shape, name: kb.sb(shape, BF16, name)
    Gb = F([128, D], "Gb")
    kb.dma(Gb[:, :], bc_rows(dr["norm_mix"], 0, D), writes=[Gb], sem_buf=Gb)
    ue = F([128, 128], "ue")
    su = F([128, 128], "su")
    blk = F([128, 128], "blk")
    kb.dma(ue[:, :], dr["c_ue"].ap()[:, :], writes=[ue], sem_buf=ue)
    kb.dma(su[:, :], dr["c_su"].ap()[:, :], writes=[su], sem_buf=su)
    kb.dma(blk[:, :], dr["c_blk"].ap()[:, :], writes=[blk], sem_buf=blk)
    sl = F([128, 128], "sl")
    kb.op("dve", lambda e: e.tensor_scalar(out=sl[:, :], in0=ue[:, :], scalar1=-1.0, scalar2=1.0, op0=ALU.mult, op1=ALU.add),
          reads=[ue], writes=[sl])
    neg = F([128, 128], "neg")
    kb.op("dve", lambda e: e.tensor_scalar(out=neg[:, :], in0=sl[:, :], scalar1=NEGBIG, scalar2=None, op0=ALU.mult),
          reads=[sl], writes=[neg])
    blk64 = F([128, 128], "blk64")
    kb.op("dve", lambda e: e.tensor_scalar(out=blk64[:, :], in0=blk[:, :], scalar1=1.0 / 64.0, scalar2=None, op0=ALU.mult),
          reads=[blk], writes=[blk64])
    onesf = F([128, 128], "onesf")
    kb.op("dve", lambda e: e.memset(onesf[:, :], 1.0 / 128.0), writes=[onesf])
    onesb = Bf([128, 128], "onesb")
    kb.op("dve", lambda e: e.memset(onesb[:, :], 1.0), writes=[onesb])
    ones1 = F([128, 128], "ones1")
    kb.op("dve", lambda e: e.memset(ones1[:, :], 1.0), writes=[ones1])
    vtmp = F([32, 128], "vtmp")
    MU = F([128, 14], "MU"); W0 = F([128, 4], "W0"); A0 = F([128, 4], "A0"); KK = F([128, 4], "KK")
    KA = F([128, 4], "KA"); RRK = F([128, 4], "RRK"); LNW = F([128, 4], "LNW"); LNB = F([128, 4], "LNB")
    MN = F([128, 4], "MN")
    load_vec_fm(g, dr["r_mu"], 0, 14, MU, MU[:, :], vtmp)
    for nm, buf in (("r_w0", W0), ("r_a0", A0), ("r_k_k", KK), ("r_k_a", KA), ("r_r_k", RRK), ("r_ln_w", LNW),
                    ("r_ln_b", LNB), ("m_norm", MN)):
        load_vec_fm(g, dr[nm], 0, 4, buf, buf[:, :], vtmp)
    BG = F([128, 8], "BG")
    kb.dma(BG[:, 0:4], bc_rows(dr["m_b_i"], 0, 4), writes=[BG], sem_buf=BG)
    kb.dma(BG[:, 4:8], bc_rows(dr["m_b_f"], 0, 4), writes=[BG], sem_buf=BG)
    C = F([128, 4, 129], "C")
    Cb = Bf([128, 4, 128], "Cb")
    nbc = Bf([128, 4, 128], "nbc")
    ST = F([128, 4, 64], "ST")
    STb = Bf([128, 4, 64], "STb")
    ZR = F([128, 14, 129], "ZR")
    for b_ in (C, ST, ZR):
        kb.op("dve", lambda e: e.memset(b_[:, :, :], 0.0), writes=[b_])
    for b_ in (Cb, nbc, STb):
        kb.op("dve", lambda e: e.memset(b_[:, :, :], 0.0), writes=[b_])
    vTM1 = Bf([128, 4, 129], "vTM1")
    kb.op("dve", lambda e: e.memset(vTM1[:, :, :], 1.0), writes=[vTM1])
    HTs = [F([128, D], f"HT{i}") for i in range(2)]
    hn = Bf([128, D], "hn"); hnT = Bf([128, 8, 128], "hnT"); junk = Bf([128, D], "junk")
    ss = F([128, 1], "ss"); rstd = F([128, 1], "rstd")
    qTb = Bf([128, 4, 128], "qTb"); kTb = Bf([128, 4, 128], "kTb"); moT = F([128, 4, 128], "moT")
    gx = F([128, 8], "gx"); th = F([128, 8], "th"); ex = F([128, 4], "ex"); LI = F([128, 4], "LI"); LF = F([128, 4], "LF")
    lmb = F([128, 4], "lmb"); LFb = F([128, 4, 128], "LFb"); arg = F([128, 4, 128], "arg"); ET = F([128, 4, 128], "ET")
    eB = F([128, 4, 128], "eB"); gcol = F([128, 4], "gcol"); ew = F([128, 4], "ew"); qs = Bf([128, 4, 128], "qs")
    sT = Bf([128, 4, 128], "sT"); kw = Bf([128, 4, 128], "kw")
    cden = F([128, 4, 128], "cden"); hT = F([128, 4, 128], "hT"); hsq = F([128, 4, 128], "hsq"); rs4 = F([128, 4, 128], "rs4")
    mixT = Bf([128, 8, 128], "mixT")
    kTMf = F([128, 512], "kTMf")
    Z2 = F([128, 14, 128], "Z2"); D1 = Z2
    LIN = Bf([128, 128], "LIN"); sxg = Bf([128, 128], "sxg")
    sw = arg; aa = ET; GG = LFb
    kkr = cden; tq = hsq; rn = rs4; kp = moT
    CS = hT; CSp = F([128, 4, 128], "CSp"); csl = F([128, 4], "csl")
    eW = F([128, 4, 128], "eW"); eWp = eB; eWi = F([128, 4, 128], "eWi"); eWT = F([128, 4, 128], "eWT")
    kka = F([128, 4, 128], "kka")
    AR = Bf([128, 4, 2, 128], "AR"); BT = Bf([128, 4, 128], "BT"); KT = Bf([128, 4, 128], "KT")
    BH = Bf([128, 4, 128], "BH"); KH = Bf([128, 4, 128], "KH"); vb = Bf([128, 4, 128], "vb")
    bonus = F([128, 4, 128], "bonus")
    BTm = [Bf([128, 4, 128], f"BTm{i}") for i in range(2)]
    KTm = [Bf([128, 4, 128], f"KTm{i}") for i in range(2)]
    ATm = [Bf([128, 4, 128], f"ATm{i}") for i in range(2)]
    STbd = Bf([128, 4, 128], "STbd")
    kb.op("dve", lambda e: e.memset(STbd[:, :, :], 0.0), writes=[STbd])
    VTM = Bf([128, 8, 64], "VTM"); BHT = Bf([128, 8, 64], "BHT"); KHT = Bf([128, 8, 64], "KHT"); UTM = Bf([128, 8, 64], "UTM")
    Xa = [F([128, 8, 128], "Xa0"), F([128, 8, 128], "Xa1")]
    XTa = [F([128, 8, 128], "XTa0"), F([128, 8, 128], "XTa1")]
    Pm = F([128, 8, 128], "Pm")
    ARB = Bf([128, 8, 128], "ARB"); AAK = Bf([128, 8, 128], "AAK"); ARK = Bf([128, 8, 128], "ARK")
    P1 = kTMf
    Of = CS; Osq = tq; mean_s = rn; var = kka

    def b3(buf, T, n=4):
        return buf[:, 0:n].unsqueeze(2).to_broadcast([128, n, T])

    for ti, (r0, T) in enumerate(g.tiles):
        if os.environ.get('ONLY_T0') and ti > 0:
            break
        HT = HTs[ti % 2]
        if ti == 0:
            load_h(g, src, 0, HT)
        rmsnorm_T(g, HT, T, Gb, hn, hnT, ss, rstd, junk)
        if ti + 1 < len(g.tiles) and not os.environ.get('ONLY_T0'):
            load_h(g, src, ti + 1, HTs[(ti + 1) % 2])

        def proj_fm(pbank, j, col):
            for kc in range(8):
                kb.op("pe", lambda e: e.matmul(pbank[:, j * T:(j + 1) * T], Win[:, kc, col:col + 128], hnT[:, kc, :T],
                                               start=(kc == 0), stop=(kc == 7)),
                      reads=[Win, hnT], writes=[pbank], inc=(kc == 7))

        def proj_tm(pbank, col, n, c0=0):
            for kc in range(8):
                kb.op("pe", lambda e: e.matmul(pbank[:T, c0:c0 + n], hnT[:, kc, :T], Win[:, kc, col:col + n],
                                               start=(kc == 0), stop=(kc == 7)),
                      reads=[hnT, Win], writes=[pbank], inc=(kc == 7))

        def v3(pbank, n=4):
            return pbank[:, 0:n * T].rearrange("p (c t) -> p c t", c=n)

        pq = next_ps(g); pk = next_ps(g); pmo = next_ps(g)
        for h in range(4):
            proj_fm(pq, h, h * 128)
        for h in range(4):
            proj_fm(pk, h, 512 + h * 128)
        for h in range(4):
            proj_fm(pmo, h, 1536 + h * 128)
        kb.op("act", lambda e: e.copy(out=qTb[:, :, :T], in_=v3(pq)), reads=[pq], writes=[qTb])
        kb.op("act", lambda e: e.copy(out=kTb[:, :, :T], in_=v3(pk)), reads=[pk], writes=[kTb])
        kb.op("act", lambda e: e.activation(out=moT[:, :, :T], in_=v3(pmo), func=AF.Sigmoid), reads=[pmo], writes=[moT])
        pkt = next_ps(g); pvt = next_ps(g); pgt = next_ps(g)
        proj_tm(pkt, 512, 512)
        proj_tm(pvt, 1024, 512)
        proj_tm(pgt, 2048, 8)
        kb.op("act", lambda e: e.copy(out=vTM1[:T, :, 0:128], in_=pvt[:T, :].rearrange("p (h v) -> p h v", h=4)),
              reads=[pvt], writes=[vTM1])
        kb.op("act", lambda e: e.copy(out=kTMf[:T, :], in_=pkt[:T, :]), reads=[pkt], writes=[kTMf])
        if CUT <= 1:
            store_h(g, dst, ti, HT, final, None, (ss, rstd, junk))
            continue
        kb.op("dve", lambda e: e.tensor_tensor(out=gx[:T, :], in0=pgt[:T, 0:8], in1=BG[:T, :], op=ALU.add),
              reads=[pgt, BG], writes=[gx])
        kb.op("act", lambda e: e.activation(out=th[:T, :], in_=gx[:T, :], func=AF.Tanh, scale=1.0 / 15.0), reads=[gx], writes=[th])
        kb.op("dve", lambda e: e.tensor_scalar(out=LI[:T, :], in0=th[:T, 0:4], scalar1=15.0, scalar2=None, op0=ALU.mult),
              reads=[th], writes=[LI])
        kb.op("act", lambda e: e.activation(out=ex[:T, :], in_=th[:T, 4:8], func=AF.Exp, scale=-15.0), reads=[th], writes=[ex])
        kb.op("act", lambda e: e.activation(out=ex[:T, :], in_=ex[:T, :], func=AF.Ln, bias=1.0), reads=[ex], writes=[ex])
        kb.op("dve", lambda e: e.tensor_scalar(out=LF[:T, :], in0=ex[:T, :], scalar1=-1.0, scalar2=None, op0=ALU.mult),
              reads=[ex], writes=[LF])
        pbc = next_ps(g)
        kb.op("pe", lambda e: e.matmul(pbc[:T, 0:4], ue[:T, :T], LF[:T, :], start=True, stop=True), reads=[ue, LF], writes=[pbc])
        kb.op("dve", lambda e: e.tensor_tensor(out=lmb[:T, :], in0=LI[:T, :], in1=pbc[:T, 0:4], op=ALU.subtract),
              reads=[LI, pbc], writes=[lmb])
        kb.op("dve", lambda e: e.tensor_copy(out=LFb[:T, :, :], in_=LF[:T, 0:4].unsqueeze(2).to_broadcast([T, 4, 128])),
              reads=[LF], writes=[LFb])
        if CUT <= 2:
            store_h(g, dst, ti, HT, final, None, (ss, rstd, junk))
            continue
        pB = next_ps(g)
        for h in range(4):
            kb.op("pe", lambda e: e.matmul(pB[:, h * T:(h + 1) * T], LFb[:T, h, :], ue[:T, :T], start=True, stop=True),
                  reads=[LFb, ue], writes=[pB], inc=(h == 3))
        kb.op("dve", lambda e: e.tensor_tensor(out=arg[:T, :, :T], in0=v3(pB)[:T], in1=neg[:T, :T].unsqueeze(1).to_broadcast([T, 4, T]),
                                               op=ALU.add), reads=[pB, neg], writes=[arg])
        for h in range(4):
            kb.op("act", lambda e: e.activation(out=ET[:T, h, :T], in_=arg[:T, h, :T], func=AF.Exp, bias=lmb[:T, h:h + 1]),
                  reads=[arg, lmb], writes=[ET])
        kb.op("act", lambda e: e.activation(out=eB[:, :, :T], in_=v3(pB), func=AF.Exp), reads=[pB], writes=[eB])
        kb.op("dve", lambda e: e.tensor_copy(out=gcol[:, :], in_=v3(pB)[:, :, T - 1]), reads=[pB], writes=[gcol])
        for h in range(4):
            kb.op("act", lambda e: e.activation(out=ew[:T, h:h + 1], in_=lmb[:T, h:h + 1], func=AF.Exp, bias=gcol[:T, h:h + 1]),
                  reads=[lmb, gcol], writes=[ew])
        kb.op("dve", lambda e: e.tensor_tensor(out=qs[:, :, :T], in0=qTb[:, :, :T], in1=eB[:, :, :T], op=ALU.mult),
              reads=[qTb, eB], writes=[qs])
        if CUT <= 3:
            store_h(g, dst, ti, HT, final, None, (ss, rstd, junk))
            continue
        psc = next_ps(g)
        for h in range(4):
            kb.op("pe", lambda e: e.matmul(psc[:T, h * T:(h + 1) * T], kTb[:, h, :T], qTb[:, h, :T], start=True, stop=True),
                  reads=[kTb, qTb], writes=[psc], inc=(h == 3))
        kb.op("dve", lambda e: e.scalar_tensor_tensor(out=sT[:T, :, :T], in0=v3(psc)[:T], scalar=ISQ, in1=ET[:T, :, :T],
                                                      op0=ALU.mult, op1=ALU.mult), reads=[psc, ET], writes=[sT])
        pnum = next_ps(g); pden = next_ps(g)
        for h in range(4):
            kb.op("pe", lambda e: e.matmul(pnum[:, h * T:(h + 1) * T], vTM1[:T, h, 0:128], sT[:T, h, :T], start=True, stop=False),
                  reads=[vTM1, sT], writes=[pnum], inc=False)
            kb.op("pe", lambda e: e.matmul(pnum[:, h * T:(h + 1) * T], Cb[:, h, :], qs[:, h, :T], start=False, stop=True),
                  reads=[Cb, qs], writes=[pnum])
        for h in range(4):
            kb.op("pe", lambda e: e.matmul(pden[:, h * T:(h + 1) * T], onesb[:T, :], sT[:T, h, :T], start=True, stop=False),
                  reads=[onesb, sT], writes=[pden], inc=False)
            kb.op("pe", lambda e: e.matmul(pden[:, h * T:(h + 1) * T], nbc[:, h, :], qs[:, h, :T], start=False, stop=True),
                  reads=[nbc, qs], writes=[pden])
        kb.op("act", lambda e: e.activation(out=cden[:, :, :T], in_=v3(pden), func=AF.Abs), reads=[pden], writes=[cden])
        kb.op("dve", lambda e: e.tensor_scalar(out=cden[:, :, :T], in0=cden[:, :, :T], scalar1=1.0, scalar2=None, op0=ALU.max),
              reads=[cden], writes=[cden])
        kb.op("dve", lambda e: e.reciprocal(out=cden[:, :, :T], in_=cden[:, :, :T]), reads=[cden], writes=[cden])
        kb.op("dve", lambda e: e.tensor_tensor(out=hT[:, :, :T], in0=v3(pnum), in1=cden[:, :, :T], op=ALU.mult),
              reads=[pnum, cden], writes=[hT])
        kb.op("act", lambda e: e.activation(out=hsq[:, :, :T], in_=hT[:, :, :T], func=AF.Square), reads=[hT], writes=[hsq])
        pss = next_ps(g)
        for h in range(4):
            kb.op("pe", lambda e: e.matmul(pss[:, h * T:(h + 1) * T], onesf[:, :], hsq[:, h, :T], start=True, stop=True),
                  reads=[onesf, hsq], writes=[pss], inc=(h == 3))
        kb.op("act", lambda e: e.activation(out=rs4[:, :, :T], in_=v3(pss), func=AF.Sqrt, bias=1e-6), reads=[pss], writes=[rs4])
        kb.op("dve", lambda e: e.reciprocal(out=rs4[:, :, :T], in_=rs4[:, :, :T]), reads=[rs4], writes=[rs4])
        kb.op("dve", lambda e: e.tensor_tensor(out=hT[:, :, :T], in0=hT[:, :, :T], in1=rs4[:, :, :T], op=ALU.mult),
              reads=[hT, rs4], writes=[hT])
        kb.op("dve", lambda e: e.tensor_tensor(out=hT[:, :, :T], in0=hT[:, :, :T], in1=moT[:, :, :T], op=ALU.mult),
              reads=[hT, moT], writes=[hT])
        kb.op("dve", lambda e: e.tensor_tensor(out=mixT[:, 0:4, :T], in0=hT[:, :, :T], in1=b3(MN, T), op=ALU.mult),
              reads=[hT, MN], writes=[mixT])
        if CUT <= 4:
            store_h(g, dst, ti, HT, final, None, (ss, rstd, junk))
            continue
        for h in range(4):
            kb.op("dve", lambda e: e.tensor_scalar(out=kw[:T, h, :], in0=kTMf[:T, h * 128:(h + 1) * 128], scalar1=ew[:T, h:h + 1],
                                                   scalar2=ISQ, op0=ALU.mult, op1=ALU.mult), reads=[kTMf, ew], writes=[kw])
        for half in range(2):
            pC = next_ps(g)
            for hh in range(2):
                h = half * 2 + hh
                kb.op("pe", lambda e: e.matmul(pC[:, hh * 129:(hh + 1) * 129], kw[:T, h, :], vTM1[:T, h, :], start=True, stop=True),
                      reads=[kw, vTM1], writes=[pC], inc=(hh == 1))
            for hh in range(2):
                h = half * 2 + hh
                kb.op("dve", lambda e: e.scalar_tensor_tensor(out=C[:, h, :], in0=C[:, h, :], scalar=eB[:, h, T - 1:T],
                                                              in1=pC[:, hh * 129:(hh + 1) * 129], op0=ALU.mult, op1=ALU.add),
                      reads=[C, eB, pC], writes=[C])
        kb.op("act", lambda e: e.copy(out=Cb[:, :, :], in_=C[:, :, 0:128]), reads=[C], writes=[Cb])
        kb.op("dve", lambda e: e.tensor_copy(out=nbc[:, :, :], in_=C[:, :, 128:129].to_broadcast([128, 4, 128])),
              reads=[C], writes=[nbc])

        if CUT <= 5:
            store_h(g, dst, ti, HT, final, None, (ss, rstd, junk))
            continue
        zc = 2056
        for b0, n in ((0, 4), (4, 4), (8, 4), (12, 2)):
            pz = next_ps(g)
            for j in range(n):
                proj_fm(pz, j, zc + (b0 + j) * 128)
            kb.op("act", lambda e: e.copy(out=ZR[:, b0:b0 + n, 1:T + 1], in_=v3(pz, n)), reads=[pz], writes=[ZR])
        kb.op("dve", lambda e: e.tensor_tensor(out=D1[:, :, :T], in0=ZR[:, :, 0:T], in1=ZR[:, :, 1:T + 1], op=ALU.subtract),
              reads=[ZR], writes=[D1])
        kb.op("dve", lambda e: e.tensor_tensor(out=D1[:, :, :T], in0=D1[:, :, :T], in1=b3(MU, T, 14), op=ALU.mult),
              reads=[D1, MU], writes=[D1])
        kb.op("dve", lambda e: e.tensor_tensor(out=Z2[:, :, :T], in0=D1[:, :, :T], in1=ZR[:, :, 1:T + 1], op=ALU.add),
              reads=[D1, ZR], writes=[Z2])
        kb.op("dve", lambda e: e.tensor_copy(out=ZR[:, :, 0:1], in_=ZR[:, :, T:T + 1]), reads=[ZR], writes=[ZR])
        r_ = Z2[:, 0:4, :T]; k_ = Z2[:, 4:8, :T]; v_ = Z2[:, 8:12, :T]
        if CUT <= 6:
            store_h(g, dst, ti, HT, final, None, (ss, rstd, junk))
            continue
        kb.op("act", lambda e: e.activation(out=LIN[0:64, :T], in_=Z2[0:64, 12, :T], func=AF.Tanh), reads=[Z2], writes=[LIN])
        kb.op("act", lambda e: e.copy(out=LIN[64:128, :T], in_=Z2[64:128, 12, :T]), reads=[Z2], writes=[LIN])
        kb.op("act", lambda e: e.activation(out=sxg[:, :T], in_=Z2[:, 13, :T], func=AF.Sigmoid), reads=[Z2], writes=[sxg])
        pw = next_ps(g); pa = next_ps(g); pgg = next_ps(g)
        for c in range(4):
            kb.op("pe", lambda e: e.matmul(pw[:, c * T:(c + 1) * T], W2A[0:64, c * 128:(c + 1) * 128], LIN[0:64, :T], start=True, stop=True),
                  reads=[W2A, LIN], writes=[pw], inc=(c == 3))
        for c in range(4):
            kb.op("pe", lambda e: e.matmul(pa[:, c * T:(c + 1) * T], W2A[64:128, c * 128:(c + 1) * 128], LIN[64:128, :T], start=True, stop=True),
                  reads=[W2A, LIN], writes=[pa], inc=(c == 3))
        for c in range(4):
            kb.op("pe", lambda e: e.matmul(pgg[:, c * T:(c + 1) * T], G2[:, c * 128:(c + 1) * 128], sxg[:, :T], start=True, stop=True),
                  reads=[G2, sxg], writes=[pgg], inc=(c == 3))
        for c in range(4):
            kb.op("act", lambda e: e.activation(out=sw[:, c, :T], in_=pw[:, c * T:(c + 1) * T], func=AF.Sigmoid, bias=W0[:, c:c + 1]),
                  reads=[pw, W0], writes=[sw])
            kb.op("act", lambda e: e.activation(out=aa[:, c, :T], in_=pa[:, c * T:(c + 1) * T], func=AF.Sigmoid, bias=A0[:, c:c + 1]),
                  reads=[pa, A0], writes=[aa])
        kb.op("act", lambda e: e.copy(out=GG[:, :, :T], in_=v3(pgg)), reads=[pgg], writes=[GG])
        if CUT <= 7:
            store_h(g, dst, ti, HT, final, None, (ss, rstd, junk))
            continue
        kb.op("dve", lambda e: e.tensor_tensor(out=kkr[:, :, :T], in0=k_, in1=b3(KK, T), op=ALU.mult), reads=[Z2, KK], writes=[kkr])
        kb.op("act", lambda e: e.activation(out=tq[:, :, :T], in_=kkr[:, :, :T], func=AF.Square), reads=[kkr], writes=[tq])
        pn = next_ps(g)
        for c in range(4):
            kb.op("pe", lambda e: e.matmul(pn[:, c * T:(c + 1) * T], blk[:, :], tq[:, c, :T], start=True, stop=True),
                  reads=[blk, tq], writes=[pn], inc=(c == 3))
        kb.op("act", lambda e: e.activation(out=rn[:, :, :T], in_=v3(pn), func=AF.Sqrt), reads=[pn], writes=[rn])
        kb.op("dve", lambda e: e.tensor_scalar(out=rn[:, :, :T], in0=rn[:, :, :T], scalar1=1e-12, scalar2=None, op0=ALU.max),
              reads=[rn], writes=[rn])
        kb.op("dve", lambda e: e.reciprocal(out=rn[:, :, :T], in_=rn[:, :, :T]), reads=[rn], writes=[rn])
        kb.op("dve", lambda e: e.tensor_tensor(out=kkr[:, :, :T], in0=kkr[:, :, :T], in1=rn[:, :, :T], op=ALU.mult),
              reads=[kkr, rn], writes=[kkr])
        kb.op("dve", lambda e: e.scalar_tensor_tensor(out=tq[:, :, :T], in0=aa[:, :, :T], scalar=-1.0, in1=b3(KA, T),
                                                      op0=ALU.add, op1=ALU.mult), reads=[aa, KA], writes=[tq])
        kb.op("dve", lambda e: e.scalar_tensor_tensor(out=kp[:, :, :T], in0=tq[:, :, :T], scalar=1.0, in1=k_,
                                                      op0=ALU.add, op1=ALU.mult), reads=[tq, Z2], writes=[kp])
        if CUT <= 8:
            store_h(g, dst, ti, HT, final, None, (ss, rstd, junk))
            continue
        for c in range(4):
            kb.op("dve", lambda e: e.tensor_tensor_scan(out=CS[:, c, :T], data0=ones1[:, :T], data1=sw[:, c, :T], initial=0.0,
                                                        op0=ALU.mult, op1=ALU.add), reads=[ones1, sw], writes=[CS])
        kb.op("dve", lambda e: e.tensor_tensor(out=CSp[:, :, :T], in0=CS[:, :, :T], in1=sw[:, :, :T], op=ALU.subtract),
              reads=[CS, sw], writes=[CSp])
        kb.op("dve", lambda e: e.tensor_scalar(out=csl[:, :], in0=CS[:, :, T - 1], scalar1=-EH, scalar2=None, op0=ALU.mult),
              reads=[CS], writes=[csl])
        kb.op("act", lambda e: e.activation(out=eW[:, :, :T], in_=CS[:, :, :T], func=AF.Exp, scale=-EH), reads=[CS], writes=[eW])
        kb.op("act", lambda e: e.activation(out=eWp[:, :, :T], in_=CSp[:, :, :T], func=AF.Exp, scale=-EH), reads=[CSp], writes=[eWp])
        kb.op("act", lambda e: e.activation(out=eWi[:, :, :T], in_=CS[:, :, :T], func=AF.Exp, scale=EH), reads=[CS], writes=[eWi])
        for c in range(4):
            kb.op("act", lambda e: e.activation(out=eWT[:, c, :T], in_=CS[:, c, :T], func=AF.Exp, scale=EH, bias=csl[:, c:c + 1]),
                  reads=[CS, csl], writes=[eWT])
        kb.op("dve", lambda e: e.scalar_tensor_tensor(out=AR[:, :, 0, :T], in0=kkr[:, :, :T], scalar=-1.0, in1=eWp[:, :, :T],
                                                      op0=ALU.mult, op1=ALU.mult), reads=[kkr, eWp], writes=[AR])
        kb.op("dve", lambda e: e.tensor_tensor(out=AR[:, :, 1, :T], in0=r_, in1=eW[:, :, :T], op=ALU.mult), reads=[Z2, eW], writes=[AR])
        kb.op("dve", lambda e: e.tensor_tensor(out=kka[:, :, :T], in0=kkr[:, :, :T], in1=aa[:, :, :T], op=ALU.mult),
              reads=[kkr, aa], writes=[kka])
        kb.op("dve", lambda e: e.tensor_tensor(out=BT[:, :, :T], in0=kka[:, :, :T], in1=eWi[:, :, :T], op=ALU.mult),
              reads=[kka, eWi], writes=[BT])
        kb.op("dve", lambda e: e.tensor_tensor(out=KT[:, :, :T], in0=kp[:, :, :T], in1=eWi[:, :, :T], op=ALU.mult),
              reads=[kp, eWi], writes=[KT])
        kb.op("dve", lambda e: e.tensor_tensor(out=BH[:, :, :T], in0=kka[:, :, :T], in1=eWT[:, :, :T], op=ALU.mult),
              reads=[kka, eWT], writes=[BH])
        kb.op("dve", lambda e: e.tensor_tensor(out=KH[:, :, :T], in0=kp[:, :, :T], in1=eWT[:, :, :T], op=ALU.mult),
              reads=[kp, eWT], writes=[KH])
        kb.op("act", lambda e: e.copy(out=vb[:, :, :T], in_=v_), reads=[Z2], writes=[vb])
        kb.op("dve", lambda e: e.tensor_tensor(out=tq[:, :, :T], in0=r_, in1=kp[:, :, :T], op=ALU.mult), reads=[Z2, kp], writes=[tq])
        kb.op("dve", lambda e: e.tensor_tensor(out=tq[:, :, :T], in0=tq[:, :, :T], in1=b3(RRK, T), op=ALU.mult),
              reads=[tq, RRK], writes=[tq])
        prk = next_ps(g)
        for c in range(4):
            kb.op("pe", lambda e: e.matmul(prk[:, c * T:(c + 1) * T], blk[:, :], tq[:, c, :T], start=True, stop=True),
                  reads=[blk, tq], writes=[prk], inc=(c == 3))
        kb.op("dve", lambda e: e.tensor_tensor(out=bonus[:, :, :T], in0=v3(prk), in1=v_, op=ALU.mult), reads=[prk, Z2], writes=[bonus])
        if CUT <= 9:
            store_h(g, dst, ti, HT, final, None, (ss, rstd, junk))
            continue
        PST = g.PST
        for c in range(4):
            kb.op("pe", lambda e: e.transpose(out=PST[:T, c * 128:(c + 1) * 128], in_=vb[:, c, :T], identity=g.ident_b[:, :]),
                  reads=[vb, g.ident_b], writes=[PST], inc=False)
        for c in range(4):
            kb.op("pe", lambda e: e.transpose(out=PST[:T, (4 + c) * 128:(5 + c) * 128], in_=BH[:, c, :T], identity=g.ident_b[:, :]),
                  reads=[BH, g.ident_b], writes=[PST], inc=(c == 3))
        kb.op("act", lambda e: e.copy(out=VTM[:T, :, :], in_=PST[:T, 0:512].rearrange("p (h v) -> p h v", h=8)), reads=[PST], writes=[VTM])
        kb.op("act", lambda e: e.copy(out=BHT[:T, :, :], in_=PST[:T, 512:1024].rearrange("p (h v) -> p h v", h=8)), reads=[PST], writes=[BHT])
        for c in range(4):
            kb.op("pe", lambda e: e.transpose(out=PST[:T, c * 128:(c + 1) * 128], in_=KH[:, c, :T], identity=g.ident_b[:, :]),
                  reads=[KH, g.ident_b], writes=[PST], inc=(c == 3))
        kb.op("act", lambda e: e.copy(out=KHT[:T, :, :], in_=PST[:T, 0:512].rearrange("p (h v) -> p h v", h=8)), reads=[PST], writes=[KHT])
        if CUT <= 10:
            store_h(g, dst, ti, HT, final, None, (ss, rstd, junk))
            continue
        for hf in range(2):
            mcol = blk[:, 127 * hf:127 * hf + 1]
            kb.op("dve", lambda e: e.tensor_scalar(out=BTm[hf][:, :, :T], in0=BT[:, :, :T], scalar1=mcol, scalar2=None, op0=ALU.mult),
                  reads=[BT, blk], writes=[BTm[hf]])
            kb.op("dve", lambda e: e.tensor_scalar(out=KTm[hf][:, :, :T], in0=KT[:, :, :T], scalar1=mcol, scalar2=None, op0=ALU.mult),
                  reads=[KT, blk], writes=[KTm[hf]])
            kb.op("dve", lambda e: e.tensor_scalar(out=ATm[hf][:, :, :T], in0=AR[:, :, 0, :T], scalar1=mcol, scalar2=None, op0=ALU.mult),
                  reads=[AR, blk], writes=[ATm[hf]])
        X, XT = Xa[0], XTa[0]
        sub = su[:T, :T].unsqueeze(1).to_broadcast([T, 2, T])
        ueb = ue[:T, :T].unsqueeze(1).to_broadcast([T, 2, T])
        slb = sl[:T, :T].unsqueeze(1).to_broadcast([T, 4, T])
        for c in range(4):
            pNA = next_ps(g); pKA = next_ps(g)
            for hf in range(HFN):
                for j in range(2):
                    kb.op("pe", lambda e: e.matmul(pNA[:T, (hf * 2 + j) * T:(hf * 2 + j + 1) * T], BTm[hf][:, c, :T], AR[:, c, j, :T], start=True, stop=True),
                          reads=[BTm[hf], AR], writes=[pNA], inc=(hf == HFN - 1 and j == 1))
            for hf in range(2):
                for j in range(2):
                    kb.op("pe", lambda e: e.matmul(pKA[:T, (hf * 2 + j) * T:(hf * 2 + j + 1) * T], KTm[hf][:, c, :T], AR[:, c, j, :T], start=True, stop=True),
                          reads=[KTm[hf], AR], writes=[pKA], inc=(hf == HFN - 1 and j == 1))
            if SUB == 1:
                continue
            na4 = pNA[:T, 0:4 * T].rearrange("p (h j t) -> p h j t", h=2, j=2)
            ka4 = pKA[:T, 0:4 * T].rearrange("p (h j t) -> p h j t", h=2, j=2)
            kb.op("dve", lambda e: e.tensor_tensor(out=X[:T, 2 * c:2 * c + 2, :T], in0=na4[:, :, 0, :], in1=sub, op=ALU.mult),
                  reads=[pNA, su], writes=[X])
            kb.op("dve", lambda e: e.tensor_tensor(out=ARB[:T, 2 * c:2 * c + 2, :T], in0=na4[:, :, 1, :], in1=ueb, op=ALU.mult),
                  reads=[pNA, ue], writes=[ARB])
            kb.op("dve", lambda e: e.tensor_tensor(out=AAK[:T, 2 * c:2 * c + 2, :T], in0=ka4[:, :, 0, :], in1=sub, op=ALU.mult),
                  reads=[pKA, su], writes=[AAK])
            kb.op("dve", lambda e: e.tensor_tensor(out=ARK[:T, 2 * c:2 * c + 2, :T], in0=ka4[:, :, 1, :], in1=ueb, op=ALU.mult),
                  reads=[pKA, ue], writes=[ARK])
        if SUB <= 2:
            store_h(g, dst, ti, HT, final, None, (ss, rstd, junk))
            continue
        for half in range(2):
            pNb = next_ps(g)
            for j in range(4):
                h = half * 4 + j
                c, hf = h // 2, h % 2
                pl = slice(hf * 64, hf * 64 + 64)
                kb.op("pe", lambda e: e.matmul(pNb[:T, j * T:(j + 1) * T], ATm[hf][:, c, :T], BT[:, c, :T], start=True, stop=True),
                      reads=[ATm[hf], BT], writes=[pNb], inc=(j == 3))
            kb.op("dve", lambda e: e.tensor_tensor(out=XT[:T, half * 4:half * 4 + 4, :T], in0=v3(pNb)[:T], in1=slb, op=ALU.mult),
                  reads=[pNb, sl], writes=[XT])
        if CUT <= 11:
            store_h(g, dst, ti, HT, final, None, (ss, rstd, junk))
            continue
        kb.op("dve", lambda e: e.tensor_tensor(out=Pm[:T, :, :T], in0=X[:T, :, :T],
                                               in1=g.ident_f[:T, :T].unsqueeze(1).to_broadcast([T, 8, T]), op=ALU.add),
              reads=[X, g.ident_f], writes=[Pm])
        lv = 1
        cur = 0
        while lv * 2 < T:
            X, XT = Xa[cur], XTa[cur]
            Xn, XTn = Xa[1 - cur], XTa[1 - cur]
            for half in range(2):
                p1 = next_ps(g); p2 = next_ps(g)
                for j in range(4):
                    h = half * 4 + j
                    kb.op("pe", lambda e: e.matmul(p1[:T, j * T:(j + 1) * T], XT[:T, h, :T], X[:T, h, :T], start=True, stop=True),
                          reads=[XT, X], writes=[p1], inc=(j == 3))
                for j in range(4):
                    h = half * 4 + j
                    kb.op("pe", lambda e: e.matmul(p2[:T, j * T:(j + 1) * T], X[:T, h, :T], XT[:T, h, :T], start=True, stop=True),
                          reads=[XT, X], writes=[p2], inc=(j == 3))
                kb.op("act", lambda e: e.copy(out=Xn[:T, half * 4:half * 4 + 4, :T], in_=v3(p1)[:T]), reads=[p1], writes=[Xn])
                kb.op("dve", lambda e: e.tensor_copy(out=XTn[:T, half * 4:half * 4 + 4, :T], in_=v3(p2)[:T]), reads=[p2], writes=[XTn])
            for half in range(2):
                p3 = next_ps(g)
                for j in range(4):
                    h = half * 4 + j
                    kb.op("pe", lambda e: e.matmul(p3[:T, j * T:(j + 1) * T], XTn[:T, h, :T], Pm[:T, h, :T], start=True, stop=True),
                          reads=[XTn, Pm], writes=[p3], inc=(j == 3))
                kb.op("dve", lambda e: e.tensor_tensor(out=Pm[:T, half * 4:half * 4 + 4, :T], in0=Pm[:T, half * 4:half * 4 + 4, :T],
                                                       in1=v3(p3)[:T], op=ALU.add), reads=[Pm, p3], writes=[Pm])
            cur = 1 - cur
            lv *= 2
        if CUT <= 12:
            store_h(g, dst, ti, HT, final, None, (ss, rstd, junk))
            continue
        pP1 = next_ps(g)
        for h in range(8):
            c, hf = h // 2, h % 2
            pl = slice(hf * 64, hf * 64 + 64)
            kb.op("pe", lambda e: e.matmul(pP1[:T, h * 64:(h + 1) * 64], ATm[hf][:, c, :T], STb[:, c, :], start=True, stop=False),
                  reads=[ATm[hf], STb], writes=[pP1], inc=False)
            kb.op("pe", lambda e: e.matmul(pP1[:T, h * 64:(h + 1) * 64], AAK[:T, h, :T], VTM[:T, h, :], start=False, stop=True),
                  reads=[AAK, VTM], writes=[pP1], inc=(h == 7))
        kb.op("act", lambda e: e.copy(out=P1[:T, :], in_=pP1[:T, :]), reads=[pP1], writes=[P1])
        pU = next_ps(g)
        for h in range(8):
            kb.op("pe", lambda e: e.matmul(pU[:T, h * 64:(h + 1) * 64], Pm[:T, h, :T], P1[:T, h * 64:(h + 1) * 64], start=True, stop=True),
                  reads=[Pm, P1], writes=[pU], inc=(h == 7))
        kb.op("act", lambda e: e.copy(out=UTM[:T, :, :], in_=pU[:T, :].rearrange("p (h v) -> p h v", h=8)), reads=[pU], writes=[UTM])
        if CUT <= 13:
            store_h(g, dst, ti, HT, final, None, (ss, rstd, junk))
            continue
        pO = next_ps(g)
        for c in range(4):
            kb.op("pe", lambda e: e.matmul(pO[:, c * T:(c + 1) * T], STbd[:, c, :], AR[:, c, 1, :T], start=True, stop=False),
                  reads=[STbd, AR], writes=[pO], inc=False)
            for hf in range(2):
                h = 2 * c + hf
                pl = slice(hf * 64, hf * 64 + 64)
                kb.op("pe", lambda e: e.matmul(pO[pl, c * T:(c + 1) * T], UTM[:T, h, :], ARB[:T, h, :T], start=False, stop=False),
                      reads=[UTM, ARB], writes=[pO], inc=False)
                kb.op("pe", lambda e: e.matmul(pO[pl, c * T:(c + 1) * T], VTM[:T, h, :], ARK[:T, h, :T], start=False, stop=True),
                      reads=[VTM, ARK], writes=[pO], inc=(hf == 1))
        kb.op("act", lambda e: e.copy(out=Of[:, :, :T], in_=v3(pO)), reads=[pO], writes=[Of])
        pS = next_ps(g)
        for h in range(8):
            c, hf = h // 2, h % 2
            pl = slice(hf * 64, hf * 64 + 64)
            kb.op("pe", lambda e: e.matmul(pS[pl, c * 64:(c + 1) * 64], BHT[:T, h, :], UTM[:T, h, :], start=True, stop=False),
                  reads=[BHT, UTM], writes=[pS], inc=False)
            kb.op("pe", lambda e: e.matmul(pS[pl, c * 64:(c + 1) * 64], KHT[:T, h, :], VTM[:T, h, :], start=False, stop=True),
                  reads=[KHT, VTM], writes=[pS], inc=(h == 7))
        for c in range(4):
            kb.op("dve", lambda e: e.scalar_tensor_tensor(out=ST[:, c, :], in0=ST[:, c, :], scalar=eW[:, c, T - 1:T],
                                                          in1=pS[:, c * 64:(c + 1) * 64], op0=ALU.mult, op1=ALU.add),
                  reads=[ST, eW, pS], writes=[ST])
        kb.op("act", lambda e: e.copy(out=STb[:, :, :], in_=ST[:, :, :]), reads=[ST], writes=[STb])
        kb.op("act", lambda e: e.copy(out=STbd[0:64, :, 0:64], in_=ST[0:64, :, :]), reads=[ST], writes=[STbd])
        kb.op("act", lambda e: e.copy(out=STbd[64:128, :, 64:128], in_=ST[64:128, :, :]), reads=[ST], writes=[STbd])
        if CUT <= 14:
            store_h(g, dst, ti, HT, final, None, (ss, rstd, junk))
            continue
        kb.op("act", lambda e: e.activation(out=Osq[:, :, :T], in_=Of[:, :, :T], func=AF.Square), reads=[Of], writes=[Osq])
        pm_ = next_ps(g); pq_ = next_ps(g)
        for c in range(4):
            kb.op("pe", lambda e: e.matmul(pm_[:, c * T:(c + 1) * T], blk64[:, :], Of[:, c, :T], start=True, stop=True),
                  reads=[blk64, Of], writes=[pm_], inc=(c == 3))
        for c in range(4):
            kb.op("pe", lambda e: e.matmul(pq_[:, c * T:(c + 1) * T], blk64[:, :], Osq[:, c, :T], start=True, stop=True),
                  reads=[blk64, Osq], writes=[pq_], inc=(c == 3))
        kb.op("act", lambda e: e.copy(out=mean_s[:, :, :T], in_=v3(pm_)), reads=[pm_], writes=[mean_s])
        kb.op("dve", lambda e: e.scalar_tensor_tensor(out=var[:, :, :T], in0=mean_s[:, :, :T], scalar=-1.0, in1=mean_s[:, :, :T],
                                                      op0=ALU.mult, op1=ALU.mult), reads=[mean_s], writes=[var])
        kb.op("dve", lambda e: e.tensor_tensor(out=var[:, :, :T], in0=var[:, :, :T], in1=v3(pq_), op=ALU.add),
              reads=[var, pq_], writes=[var])
        kb.op("dve", lambda e: e.tensor_scalar(out=var[:, :, :T], in0=var[:, :, :T], scalar1=0.0, scalar2=None, op0=ALU.max),
              reads=[var], writes=[var])
        kb.op("act", lambda e: e.activation(out=var[:, :, :T], in_=var[:, :, :T], func=AF.Sqrt, bias=64e-5), reads=[var], writes=[var])
        kb.op("dve", lambda e: e.reciprocal(out=var[:, :, :T], in_=var[:, :, :T]), reads=[var], writes=[var])
        kb.op("dve", lambda e: e.tensor_tensor(out=Of[:, :, :T], in0=Of[:, :, :T], in1=mean_s[:, :, :T], op=ALU.subtract),
              reads=[Of, mean_s], writes=[Of])
        kb.op("dve", lambda e: e.tensor_tensor(out=Of[:, :, :T], in0=Of[:, :, :T], in1=var[:, :, :T], op=ALU.mult),
              reads=[Of, var], writes=[Of])
        kb.op("dve", lambda e: e.tensor_tensor(out=Of[:, :, :T], in0=Of[:, :, :T], in1=b3(LNW, T), op=ALU.mult),
              reads=[Of, LNW], writes=[Of])
        kb.op("dve", lambda e: e.tensor_tensor(out=Of[:, :, :T], in0=Of[:, :, :T], in1=b3(LNB, T), op=ALU.add),
              reads=[Of, LNB], writes=[Of])
        kb.op("dve", lambda e: e.tensor_tensor(out=Of[:, :, :T], in0=Of[:, :, :T], in1=bonus[:, :, :T], op=ALU.add),
              reads=[Of, bonus], writes=[Of])
        kb.op("dve", lambda e: e.tensor_tensor(out=mixT[:, 4:8, :T], in0=Of[:, :, :T], in1=GG[:, :, :T], op=ALU.mult),
              reads=[Of, GG], writes=[mixT])
        if CUT <= 15:
            store_h(g, dst, ti, HT, final, None, (ss, rstd, junk))
            continue
        if os.environ.get('ZERO_M'):
            kb.op("dve", lambda e: e.memset(mixT[:, 0:4, :], 0.0), writes=[mixT])
        if os.environ.get('ZERO_R'):
            kb.op("dve", lambda e: e.tensor_scalar(out=mixT[:, 4:8, :T], in0=mixT[:, 4:8, :T], scalar1=0.0, scalar2=None, op0=ALU.mult), reads=[mixT], writes=[mixT])
        for nb in range(2):
            pp = next_ps(g)
            for c in range(8):
                kb.op("pe", lambda e: e.matmul(pp[:T, :], mixT[:, c, :T], Wout[:, c, nb * 512:(nb + 1) * 512],
                                               start=(c == 0), stop=(c == 7)), reads=[mixT, Wout], writes=[pp], inc=(c == 7))
            kb.op("dve", lambda e: e.tensor_tensor(out=HT[:T, nb * 512:(nb + 1) * 512], in0=HT[:T, nb * 512:(nb + 1) * 512],
                                                   in1=pp[:T, :], op=ALU.add), reads=[HT, pp], writes=[HT])
        store_h(g, dst, ti, HT, final, None, (ss, rstd, junk))


LG = [float(np.log(1.0 - 2.0 ** (-5.0 - h))) for h in range(4)]
TWO_PI = 6.283185307179586
CW1 = 6.28125
CW2 = TWO_PI - CW1


LG = [float(np.log(1.0 - 2.0 ** (-5.0 - h))) for h in range(4)]
TWO_PI = 6.283185307179586
CW1 = 6.28125
CW2 = TWO_PI - CW1


LG = [float(np.log(1.0 - 2.0 ** (-5.0 - h))) for h in range(4)]
TWO_PI = 6.283185307179586
CW1 = 6.28125
CW2 = TWO_PI - CW1


def phase_l1(g, src, dst, final):
    kb, nc, dr = g.kb, g.nc, g.dr
    Win = kb.sb([128, 8, 6144], BF16, "Win")
    Wout = kb.sb([128, 16, D], BF16, "Wout")
    with contextlib.ExitStack() as ses:
        old = kb.es
        kb.es = ses
        stg = [kb.sb([128, 1536], F32, f"stg{i}") for i in range(3)]
        load_weight_bf16(g, dr["o_w_in_p"], 0, D, 6144, Win, stg)
        load_weight_bf16(g, dr["o_w_out"], 0, 2048, D, Wout, stg)
        kb.barrier()
        kb.es = old
    Gb = kb.sb([128, D], BF16, "Gb")
    Gfin = None
    iota = kb.sb([128, 128], F32, "iota")
    pidx = kb.sb([128, 1], F32, "pidx")
    ue = kb.sb([128, 128], F32, "ue")
    inv = kb.sb([128, 1], F32, "inv")
    kb.dma(iota[:, :], dr["c_iota"].ap()[:, :], writes=[iota], sem_buf=iota)
    kb.dma(pidx[:, :], dr["c_pidx"].ap()[:, :], writes=[pidx], sem_buf=pidx)
    kb.dma(ue[:, :], dr["c_ue"].ap()[:, :], writes=[ue], sem_buf=ue)
    kb.dma(inv[:, :], dr["c_inv"].ap()[:, :], writes=[inv], sem_buf=inv)
    DM = kb.sb([128, 4, 128], F32, "DM")
    DEC = kb.sb([128, 4, 128], F32, "DEC")
    KDEC = {128: kb.sb([128, 4], F32, "KDEC128"), 16: kb.sb([128, 4], F32, "KDEC16")}
    tms = kb.sb([128, 128], F32, "tms")
    kb.op("dve", lambda e: e.tensor_scalar(out=tms[:, :], in0=iota[:, :], scalar1=pidx[:, 0:1], scalar2=0.0,
                                           op0=ALU.subtract, op1=ALU.max), reads=[iota, pidx], writes=[tms])
    for h in range(4):
        kb.op("act", lambda e: e.activation(out=DM[:, h, :], in_=tms[:, :], func=AF.Exp, scale=LG[h]),
              reads=[tms], writes=[DM])
        kb.op("dve", lambda e: e.scalar_tensor_tensor(out=DM[:, h, :], in0=DM[:, h, :], scalar=1.0 / 16.0,
                                                      in1=ue[:, :], op0=ALU.mult, op1=ALU.mult),
              reads=[DM, ue], writes=[DM])
        kb.op("act", lambda e: e.activation(out=DEC[:, h, :], in_=iota[:, :], func=AF.Exp, scale=LG[h], bias=LG[h]),
              reads=[iota], writes=[DEC])
        for TT in (128, 16):
            kd = KDEC[TT]
            kb.op("act", lambda e: e.activation(out=kd[:, h:h + 1], in_=pidx[:, 0:1], func=AF.Exp, scale=-LG[h],
                                                bias=LG[h] * (TT - 1)), reads=[pidx], writes=[kd])
            kb.op("dve", lambda e: e.tensor_scalar(out=kd[:, h:h + 1], in0=kd[:, h:h + 1], scalar1=1.0 / 16.0,
                                                   scalar2=None, op0=ALU.mult), reads=[kd], writes=[kd])
    Sr = kb.sb([128, 8, 512], F32, "Sr")
    Srb = kb.sb([128, 8, 512], BF16, "Srb")
    kb.op("dve", lambda e: e.memset(Sr[:, :, :], 0.0), writes=[Sr])
    kb.op("pool", lambda e: e.memset(Srb[:, :, :], 0.0), writes=[Srb])
    HTs = [kb.sb([128, D], F32, f"HT{i}") for i in range(2)]
    hn = kb.sb([128, D], BF16, "hn")
    hnT = kb.sb([128, 8, 128], BF16, "hnT")
    ss = kb.sb([128, 1], F32, "ss")
    rstd = kb.sb([128, 1], F32, "rstd")
    ang = kb.sb([128, 128], F32, "ang")
    ang2 = kb.sb([128, 128], F32, "ang2")
    kf = kb.sb([128, 128], F32, "kf")
    ki = kb.sb([128, 128], I32, "ki")
    nsin = kb.sb([128, 128], F32, "nsin")
    ncos = kb.sb([128, 128], F32, "ncos")
    t1 = kb.sb([128, 4, 128], F32, "t1")
    t2 = kb.sb([128, 4, 128], F32, "t2")
    qb = kb.sb([128, 2, 4, 128], BF16, "qb")
    qdb = kb.sb([128, 2, 4, 128], BF16, "qdb")
    kbf = kb.sb([128, 2, 4, 128], BF16, "kbf")
    kdT = kb.sb([128, 8, 128], BF16, "kdT")
    sTm = kb.sb([128, 4, 128], BF16, "sTm")
    VT = kb.sb([128, 2048], BF16, "VT")
    GS = kb.sb([128, 2048], BF16, "GS")
    og = kb.sb([128, 2048], BF16, "og")
    ogT = kb.sb([128, 16, 128], BF16, "ogT")
    st6 = kb.sb([128, 6], F32, "st6")
    mv = kb.sb([128, 2], F32, "mv")
    rs = kb.sb([128, 1], F32, "rs")
    junk = og
    kb.dma(t1[:, :, :].rearrange("p a b -> p (a b)"), bc_rows(dr["norm_mix"], 1, 512), writes=[t1], sem_buf=t1)
    kb.op("dve", lambda e: e.tensor_copy(out=Gb[:, 0:512], in_=t1[:, :, :].rearrange("p a b -> p (a b)")), reads=[t1], writes=[Gb])
    kb.dma(t2[:, :, :].rearrange("p a b -> p (a b)"), bc_rows(dr["norm_mix"], 1, 512, col0=512), writes=[t2], sem_buf=t2)
    kb.op("dve", lambda e: e.tensor_copy(out=Gb[:, 512:1024], in_=t2[:, :, :].rearrange("p a b -> p (a b)")), reads=[t2], writes=[Gb])

    def sincos(dst_tbl, shift, pos0, T):
        kb.op("dve", lambda e: e.tensor_scalar(out=ang[:, :T], in0=iota[:, :T], scalar1=float(pos0), scalar2=inv[:, 0:1],
                                               op0=ALU.add, op1=ALU.mult), reads=[iota, inv], writes=[ang])
        if shift != 0.0:
            kb.op("dve", lambda e: e.tensor_scalar(out=ang[:, :T], in0=ang[:, :T], scalar1=shift, scalar2=None,
                                                   op0=ALU.add), reads=[ang], writes=[ang])
        kb.op("dve", lambda e: e.tensor_scalar(out=ki[:, :T], in0=ang[:, :T], scalar1=1.0 / TWO_PI, scalar2=None,
                                               op0=ALU.mult), reads=[ang], writes=[ki])
        kb.op("dve", lambda e: e.tensor_copy(out=kf[:, :T], in_=ki[:, :T]), reads=[ki], writes=[kf])
        kb.op("dve", lambda e: e.scalar_tensor_tensor(out=ang2[:, :T], in0=kf[:, :T], scalar=-CW1, in1=ang[:, :T],
                                                      op0=ALU.mult, op1=ALU.add), reads=[kf, ang], writes=[ang2])
        kb.op("dve", lambda e: e.scalar_tensor_tensor(out=ang2[:, :T], in0=kf[:, :T], scalar=-CW2, in1=ang2[:, :T],
                                                      op0=ALU.mult, op1=ALU.add), reads=[kf, ang2], writes=[ang2])
        kb.op("dve", lambda e: e.tensor_scalar(out=ang2[:, :T], in0=ang2[:, :T], scalar1=3.1415925, scalar2=-3.1415925,
                                               op0=ALU.min, op1=ALU.max), reads=[ang2], writes=[ang2])
        kb.op("act", lambda e: e.activation(out=dst_tbl[:, :T], in_=ang2[:, :T], func=AF.Sin),
              reads=[ang2], writes=[dst_tbl])

    for ti, (r0, T) in enumerate(g.tiles):
        HT = HTs[ti % 2]
        HO = HT
        if ti == 0:
            load_h(g, src, 0, HT)
        rmsnorm_T(g, HT, T, Gb, hn, hnT, ss, rstd, junk)
        if ti + 1 < len(g.tiles):
            load_h(g, src, ti + 1, HTs[(ti + 1) % 2])
        sincos(nsin, 0.0, r0, T)
        sincos(ncos, np.pi / 2, r0, T)
        sb_ = nsin[:, :T].unsqueeze(1).to_broadcast([128, 4, T])
        cb_ = ncos[:, :T].unsqueeze(1).to_broadcast([128, 4, T])
        for which in range(2):
            pe_ = next_ps(g)
            po_ = next_ps(g)
            for eo, pb in ((0, pe_), (1, po_)):
                for h in range(4):
                    col = which * 1024 + h * 256 + eo * 128
                    for kc in range(8):
                        kb.op("pe", lambda e: e.matmul(pb[:, h * T:(h + 1) * T], Win[:, kc, col:col + 128],
                                                       hnT[:, kc, :T], start=(kc == 0), stop=(kc == 7)),
                              reads=[Win, hnT], writes=[pb], inc=(kc == 7))
            pe3 = pe_[:, 0:4 * T].rearrange("p (h t) -> p h t", h=4)
            po3 = po_[:, 0:4 * T].rearrange("p (h t) -> p h t", h=4)
            kb.op("dve", lambda e: e.tensor_tensor(out=t1[:, :, :T], in0=pe3, in1=cb_, op=ALU.mult),
                  reads=[pe_, ncos], writes=[t1])
            kb.op("dve", lambda e: e.tensor_tensor(out=t2[:, :, :T], in0=po3, in1=sb_, op=ALU.mult),
                  reads=[po_, nsin], writes=[t2])
            dstb = qb if which == 0 else kbf
            kb.op("dve", lambda e: e.tensor_tensor(out=dstb[:, 0, :, :T], in0=t1[:, :, :T], in1=t2[:, :, :T],
                                                   op=ALU.subtract), reads=[t1, t2], writes=[dstb])
            kb.op("dve", lambda e: e.tensor_tensor(out=t1[:, :, :T], in0=po3, in1=cb_, op=ALU.mult),
                  reads=[po_, ncos], writes=[t1])
            kb.op("dve", lambda e: e.tensor_tensor(out=t2[:, :, :T], in0=pe3, in1=sb_, op=ALU.mult),
                  reads=[pe_, nsin], writes=[t2])
            kb.op("dve", lambda e: e.tensor_tensor(out=dstb[:, 1, :, :T], in0=t1[:, :, :T], in1=t2[:, :, :T],
                                                   op=ALU.add), reads=[t1, t2], writes=[dstb])
            if which == 0:
                for eo in range(2):
                    kb.op("pool", lambda e: e.tensor_tensor(out=qdb[:, eo, :, :T], in0=qb[:, eo, :, :T],
                                                            in1=DEC[:, :, :T], op=ALU.mult),
                          reads=[qb, DEC], writes=[qdb])
        PST = g.PST
        for h in range(4):
            for eo in range(2):
                j = h * 2 + eo
                kb.op("pe", lambda e: e.transpose(out=PST[:T, j * 128:(j + 1) * 128], in_=kbf[:, eo, h, :T],
                                                  identity=g.ident_b[:, :]),
                      reads=[kbf, g.ident_b], writes=[PST], inc=(j == 7))
        for h in range(4):
            kb.op("act", lambda e: e.activation(out=kdT[:T, 2 * h:2 * h + 2, :],
                                                in_=PST[:T, 2 * h * 128:(2 * h + 2) * 128].rearrange("p (j d) -> p j d", j=2),
                                                func=AF.Copy, scale=KDEC[T][:T, h:h + 1]),
                  reads=[PST, KDEC[T]], writes=[kdT])
        psc = next_ps(g)
        for h in range(4):
            for eo in range(2):
                kb.op("pe", lambda e: e.matmul(psc[:T, h * T:(h + 1) * T], kbf[:, eo, h, :T], qb[:, eo, h, :T],
                                               start=(eo == 0), stop=(eo == 1)),
                      reads=[kbf, qb], writes=[psc], inc=(eo == 1))
        kb.op("dve", lambda e: e.tensor_tensor(out=sTm[:T, :, :T],
                                               in0=psc[:T, 0:4 * T].rearrange("p (h t) -> p h t", h=4),
                                               in1=DM[:T, :, :T], op=ALU.mult), reads=[psc, DM], writes=[sTm])
        for nb in range(4):
            pvv = next_ps(g)
            for kc in range(8):
                kb.op("pe", lambda e: e.matmul(pvv[:T, :], hnT[:, kc, :T], Win[:, kc, 2048 + nb * 512:2048 + (nb + 1) * 512],
                                               start=(kc == 0), stop=(kc == 7)), reads=[hnT, Win], writes=[pvv], inc=(kc == 7))
            kb.op("act", lambda e: e.copy(out=VT[:T, nb * 512:(nb + 1) * 512], in_=pvv[:T, :]), reads=[pvv], writes=[VT])
        for nb in range(4):
            pgg = next_ps(g)
            for kc in range(8):
                kb.op("pe", lambda e: e.matmul(pgg[:T, :], hnT[:, kc, :T], Win[:, kc, 4096 + nb * 512:4096 + (nb + 1) * 512],
                                               start=(kc == 0), stop=(kc == 7)), reads=[hnT, Win], writes=[pgg], inc=(kc == 7))
            kb.op("act", lambda e: e.activation(out=GS[:T, nb * 512:(nb + 1) * 512], in_=pgg[:T, :], func=AF.Silu),
                  reads=[pgg], writes=[GS])
        for h in range(4):
            po = next_ps(g)
            kb.op("pe", lambda e: e.matmul(po[:T, :], sTm[:T, h, :T], VT[:T, h * 512:(h + 1) * 512], start=True, stop=False),
                  reads=[sTm, VT], writes=[po], inc=False)
            for eo in range(2):
                kb.op("pe", lambda e: e.matmul(po[:T, :], qdb[:, eo, h, :T], Srb[:, 2 * h + eo, :], start=False, stop=(eo == 1)),
                      reads=[qdb, Srb], writes=[po], inc=(eo == 1))
            kb.op("dve", lambda e: e.bn_stats(out=st6[:T, :], in_=po[:T, :]), reads=[po], writes=[st6])
            kb.op("dve", lambda e: e.bn_aggr(out=mv[:T, :], in_=st6[:T, :]), reads=[st6], writes=[mv])
            kb.op("act", lambda e: e.activation(out=rs[:T, :], in_=mv[:T, 1:2], func=AF.Sqrt, scale=1.0, bias=1e-6),
                  reads=[mv], writes=[rs])
            kb.op("dve", lambda e: e.reciprocal(out=rs[:T, :], in_=rs[:T, :]), reads=[rs], writes=[rs])
            kb.op("dve", lambda e: e.tensor_scalar(out=og[:T, h * 512:(h + 1) * 512], in0=po[:T, :], scalar1=mv[:T, 0:1], scalar2=rs[:T, 0:1],
                                                   op0=ALU.subtract, op1=ALU.mult), reads=[po, mv, rs], writes=[og])
            kb.op("pool", lambda e: e.tensor_tensor(out=og[:T, h * 512:(h + 1) * 512], in0=og[:T, h * 512:(h + 1) * 512],
                                                    in1=GS[:T, h * 512:(h + 1) * 512], op=ALU.mult),
                  reads=[og, GS], writes=[og])
        gT = [float(np.exp(LG[h] * T)) for h in range(4)]
        for h in range(4):
            for eo in range(2):
                j = 2 * h + eo
                pst_ = next_ps(g)
                kb.op("pe", lambda e: e.matmul(pst_[:, :], kdT[:T, j, :], VT[:T, h * 512:(h + 1) * 512], start=True, stop=True),
                      reads=[kdT, VT], writes=[pst_])
                kb.op("dve", lambda e: e.scalar_tensor_tensor(out=Sr[:, j, :], in0=Sr[:, j, :], scalar=gT[h], in1=pst_[:, :],
                                                              op0=ALU.mult, op1=ALU.add), reads=[Sr, pst_], writes=[Sr])
                kb.op("act", lambda e: e.copy(out=Srb[:, j, :], in_=Sr[:, j, :]), reads=[Sr], writes=[Srb])
        for half in range(2):
            for j in range(8):
                c = half * 8 + j
                kb.op("pe", lambda e: e.transpose(out=PST[:, j * T:(j + 1) * T], in_=og[:T, c * 128:(c + 1) * 128],
                                                  identity=g.ident_b[:T, :T]),
                      reads=[og, g.ident_b], writes=[PST], inc=(j == 7))
            kb.op("act", lambda e: e.copy(out=ogT[:, half * 8:(half + 1) * 8, :T],
                                          in_=PST[:, 0:8 * T].rearrange("p (k t) -> p k t", k=8)),
                  reads=[PST], writes=[ogT])
        for nb in range(2):
            pp = next_ps(g)
            for c in range(16):
                kb.op("pe", lambda e: e.matmul(pp[:T, :], ogT[:, c, :T], Wout[:, c, nb * 512:(nb + 1) * 512],
                                               start=(c == 0), stop=(c == 15)), reads=[ogT, Wout], writes=[pp], inc=(c == 15))
            kb.op("dve", lambda e: e.tensor_tensor(out=HO[:T, nb * 512:(nb + 1) * 512],
                                                   in0=HT[:T, nb * 512:(nb + 1) * 512], in1=pp[:T, :], op=ALU.add),
                  reads=[HT, pp], writes=[HO])
        store_h(g, dst, ti, HO, final, Gfin, (ss, rstd, junk))


def make_in_map(inputs, b, NT):
    m = {"x": np.ascontiguousarray(inputs["x"][b, :128 * NT])}
    for k, shp in W_SPECS.items():
        src_k = "o_w_in" if k == "o_w_in_p" else k
        m[k] = np.ascontiguousarray(np.asarray(inputs[src_k], np.float32).reshape(shp))
    m.update(host_consts())
    perm = np.arange(6144)
    for sec in range(2):
        for h in range(4):
            base = sec * 1024 + h * 256
            perm[base:base + 256] = np.concatenate([base + np.arange(0, 256, 2), base + np.arange(1, 256, 2)])
    m["o_w_in_p"] = np.ascontiguousarray(m["o_w_in_p"][:, perm])
    return m


NT_FULL = 32


def kernel(**inputs):
    nc = build(NT_FULL, phases=(1, 2, 3, 4), debug=False, final=True)
    in_maps = [make_in_map(inputs, b, NT_FULL) for b in range(8)]
    res = run_bass_kernel_spmd(nc, in_maps, core_ids=list(range(8)))
    return np.stack([np.asarray(r["out"], np.float32) for r in res.results], axis=0)
```

```python
import contextlib
import numpy as np
import concourse.bass as bass
import concourse.mybir as mybir

F32 = mybir.dt.float32
BF16 = mybir.dt.bfloat16
I32 = mybir.dt.int32
AF = mybir.ActivationFunctionType
ALU = mybir.AluOpType
AX = mybir.AxisListType


class Buf:
    __slots__ = ("t", "w", "r", "dsem", "dcount", "name", "excl")

    def __init__(self, t, name=""):
        self.t = t
        self.w = {}
        self.r = {}
        self.dsem = None
        self.dcount = 0
        self.name = name
        self.excl = False

    def __getitem__(self, idx):
        return self.t[idx]


class Eng:
    def __init__(self, name, obj, sem):
        self.name = name
        self.obj = obj
        self.sem = sem
        self.count = 0
        self.seen = {}


class KB:
    def __init__(self, nc, es):
        self.nc = nc
        self.es = es
        self.sems = {}
        self.E = {}
        for name, obj in (("pe", nc.tensor), ("act", nc.scalar), ("dve", nc.vector),
                          ("pool", nc.gpsimd), ("sp", nc.sync)):
            sem = es.enter_context(nc.semaphore("s_" + name))
            self.E[name] = Eng(name, obj, sem)
            self.sems[id(sem)] = sem
        self.dma_tokens = {}
        self.nbuf = 0

    def sb(self, shape, dt, name=None):
        self.nbuf += 1
        name = f"{name or 'b'}_{self.nbuf}"
        t = self.es.enter_context(self.nc.sbuf_tensor(name, list(shape), dt))
        return Buf(t, name)

    def ps(self, shape, dt, name=None):
        self.nbuf += 1
        name = f"{name or 'p'}_{self.nbuf}"
        t = self.es.enter_context(self.nc.psum_tensor(name, list(shape), dt))
        b = Buf(t, name)
        b.excl = True
        return b

    def newsem(self, name):
        sem = self.es.enter_context(self.nc.semaphore(name))
        self.sems[id(sem)] = sem
        return sem

    def _wait(self, e, deps):
        for sid, val in deps.items():
            if e.seen.get(sid, 0) < val:
                e.obj.wait_ge(self.sems[sid], val)
                e.seen[sid] = val

    def _deps(self, e, reads, writes):
        deps = {}
        own = id(e.sem)
        for b in reads:
            for sid, v in b.w.items():
                if deps.get(sid, 0) < v:
                    deps[sid] = v
            if b.excl:
                for sid, v in b.r.items():
                    if sid != own and deps.get(sid, 0) < v:
                        deps[sid] = v
        for b in writes:
            for d in (b.w, b.r):
                for sid, v in d.items():
                    if sid == own:
                        continue
                    if deps.get(sid, 0) < v:
                        deps[sid] = v
        return deps

    def op(self, eng, fn, reads=(), writes=(), inc=True):
        e = self.E[eng]
        self._wait(e, self._deps(e, reads, writes))
        ins = fn(e.obj)
        if inc:
            e.count += 1
            ins.then_inc(e.sem, 1)
            val = e.count
        else:
            val = e.count + 1
        sid = id(e.sem)
        for b in reads:
            if b.r.get(sid, 0) < val:
                b.r[sid] = val
        for b in writes:
            if b.w.get(sid, 0) < val:
                b.w[sid] = val
        return ins

    def dma(self, out_ap, in_ap, reads=(), writes=(), sem_buf=None, q="sp"):
        e = self.E[q]
        self._wait(e, self._deps(e, reads, writes))
        b = sem_buf
        if b.dsem is None:
            b.dsem = self.newsem("d_" + b.name)
        b.dcount += 16
        e.obj.dma_start(out=out_ap, in_=in_ap).then_inc(b.dsem, 16)
        sid = id(b.dsem)
        for x in reads:
            x.r[sid] = b.dcount
        for x in writes:
            x.w[sid] = b.dcount
        self.dma_tokens[sid] = b.dcount

    def barrier(self):
        targets = {id(e.sem): e.count for e in self.E.values() if e.count > 0}
        targets.update(self.dma_tokens)
        for e in self.E.values():
            self._wait(e, {k: v for k, v in targets.items() if k != id(e.sem)})

    def final_wait(self):
        e = self.E["sp"]
        self._wait(e, dict(self.dma_tokens))


from concourse.bass_utils import run_bass_kernel_spmd

D = 1024
NMETA = 16
DFF = 2816
NFC = DFF // 128

W_SPECS = {
    "meta_tokens": (16, 1024), "norm_mix": (2, 1024), "norm_ffn": (2, 1024), "norm_final": (1, 1024),
    "e_w_in": (1024, 3848), "e_w_out": (1024, 1024), "m_b_i": (1, 4), "m_b_f": (1, 4), "m_norm": (1, 512),
    "r_mu": (1, 1792), "r_w0": (1, 512), "r_w2": (64, 512), "r_a0": (1, 512), "r_a2": (64, 512),
    "r_g2": (128, 512), "r_k_k": (1, 512), "r_k_a": (1, 512), "r_r_k": (1, 512), "r_ln_w": (1, 512),
    "r_ln_b": (1, 512), "o_w_in_p": (1024, 6144), "o_w_out": (2048, 1024), "f_w_up": (2048, 5632),
    "f_conv_w": (6, 2816), "f_conv_b": (2, 2816), "f_w_down": (5632, 1024),
}


def host_consts():
    c = {}
    c["c_ident"] = np.eye(128, dtype=np.float32)
    i = np.arange(128)
    c["c_ue"] = (i[:, None] <= i[None, :]).astype(np.float32)
    c["c_su"] = (i[:, None] < i[None, :]).astype(np.float32)
    c["c_iota"] = np.broadcast_to(np.arange(128, dtype=np.float32)[None, :], (128, 128)).copy()
    c["c_pidx"] = np.arange(128, dtype=np.float32)[:, None].copy()
    bo = np.zeros((128, 128), np.float32)
    bo[:64, :64] = 1.0
    bo[64:, 64:] = 1.0
    c["c_blk"] = bo
    c["c_inv"] = (np.float32(1.0) / np.power(np.float32(10000.0), np.linspace(0.0, 1.0, 128, dtype=np.float32))
                  ).astype(np.float32)[:, None].copy()
    return c


class Ctx:
    pass


def tile_rows(NT):
    tiles = [(0, NMETA)]
    for i in range(NT):
        tiles.append((NMETA + 128 * i, 128))
    return tiles


def build(NT, phases=(1, 2, 3, 4), debug=False, final=True):
    nc = bass.Bass("TRN2", target_bir_lowering=False)
    SEQ = 128 * NT
    L = NMETA + SEQ
    dr = {}
    dr["x"] = nc.dram_tensor("x", [SEQ, D], F32, kind="ExternalInput")
    for k, shp in W_SPECS.items():
        dr[k] = nc.dram_tensor(k, list(shp), F32, kind="ExternalInput")
    for k, v in host_consts().items():
        dr[k] = nc.dram_tensor(k, list(v.shape), F32, kind="ExternalInput")
    out = nc.dram_tensor("out", [SEQ, D], F32, kind="ExternalOutput")
    H = {}
    for i in (1, 2, 3):
        H[i] = nc.dram_tensor(f"H{i}", [L, D], F32, kind=("ExternalOutput" if debug else "Internal"))

    tiles = tile_rows(NT)
    es = contextlib.ExitStack()
    with es:
        kb = KB(nc, es)
        PS = [kb.ps([128, 512], F32, f"psb{i}") for i in range(7)]
        PST = kb.ps([128, 1024], BF16, "pstr")
        g = Ctx()
        g.nc, g.kb, g.dr, g.H, g.out, g.tiles, g.PS, g.PST = nc, kb, dr, H, out, tiles, PS, PST
        g.psi = 0
        g.ident_f = kb.sb([128, 128], F32, "ident_f")
        g.ident_b = kb.sb([128, 128], BF16, "ident_b")
        kb.dma(g.ident_f[:, :], dr["c_ident"].ap()[:, :], writes=[g.ident_f], sem_buf=g.ident_f)
        kb.op("dve", lambda e: e.tensor_copy(out=g.ident_b[:, :], in_=g.ident_f[:, :]),
              reads=[g.ident_f], writes=[g.ident_b])

        plist = [p for p in (1, 2, 3, 4) if p in phases]
        src = 0
        for p in plist:
            dst = p if p != plist[-1] else 4
            with contextlib.ExitStack() as pes:
                kb.es = pes
                if p in (2, 4):
                    phase_ffn(g, layer=(0 if p == 2 else 1), src=src, dst=dst, final=final)
                elif p == 1:
                    phase_l0(g, src=src, dst=dst, final=final)
                elif p == 3:
                    phase_l1(g, src=src, dst=dst, final=final)
                kb.barrier()
            kb.es = es
            src = dst
        kb.final_wait()
    return nc


def next_ps(g):
    b = g.PS[g.psi % len(g.PS)]
    g.psi += 1
    return b


def bc_rows(handle, row, n, parts=128, col0=0, ncols_total=None):
    ncols_total = ncols_total if ncols_total is not None else handle.shape[1]
    return bass.AP(handle, row * ncols_total + col0, [[0, parts], [1, n]])


def load_h(g, src, ti, HT):
    kb = g.kb
    r0, T = g.tiles[ti]
    if src == 0:
        if ti == 0:
            ap = g.dr["meta_tokens"].ap()[0:NMETA, :]
        else:
            ap = g.dr["x"].ap()[r0 - NMETA:r0 - NMETA + T, :]
    else:
        ap = g.H[src].ap()[r0:r0 + T, :]
    kb.dma(HT[:T, :], ap, writes=[HT], sem_buf=HT)


def store_h(g, dst, ti, HO, final, Gfin=None, scratch=None):
    kb = g.kb
    r0, T = g.tiles[ti]
    if dst != 4:
        kb.dma(g.H[dst].ap()[r0:r0 + T, :], HO[:T, :], reads=[HO], sem_buf=HO)
        return
    if ti == 0:
        return
    if final:
        ss, rstd, junk = scratch
        kb.op("act", lambda e: e.activation(out=junk[:T, 0:D], in_=HO[:T, :], func=AF.Square, accum_out=ss[:T, :]),
              reads=[HO], writes=[junk, ss])
        rstd_from_ss(kb, ss, rstd, T, 1.0 / D, 1e-6)
        kb.op("dve", lambda e: e.scalar_tensor_tensor(out=HO[:T, :], in0=HO[:T, :], scalar=rstd[:T, :],
                                                      in1=Gfin[:T, :], op0=ALU.mult, op1=ALU.mult),
              reads=[HO, rstd, Gfin], writes=[HO])
    kb.dma(g.out.ap()[r0 - NMETA:r0 - NMETA + T, :], HO[:T, :], reads=[HO], sem_buf=HO)


def load_weight_bf16(g, dram_handle, row0, K, N, W, stg, col0=0, ncols_total=None):
    kb = g.kb
    SW = stg[0].t.shape[1]
    engs = ("dve", "act", "dve", "act", "dve", "pool", "dve", "act")
    cnt = getattr(g, "_lw_cnt", 0)
    for kc in range(K // 128):
        for j0 in range(0, N, SW):
            w = min(SW, N - j0)
            s = stg[cnt % len(stg)]
            kb.dma(s[:, :w], dram_handle.ap()[row0 + kc * 128: row0 + (kc + 1) * 128, col0 + j0: col0 + j0 + w],
                   writes=[s], sem_buf=s)
            en = engs[cnt % len(engs)]
            if en == "act":
                kb.op("act", lambda e: e.copy(out=W[:, kc, j0:j0 + w], in_=s[:, :w]), reads=[s], writes=[W])
            else:
                kb.op(en, lambda e: e.tensor_copy(out=W[:, kc, j0:j0 + w], in_=s[:, :w]), reads=[s], writes=[W])
            cnt += 1
    g._lw_cnt = cnt


def rstd_from_ss(kb, ss, rstd, T, scale, eps, ap_fn=None):
    a = (lambda b: b[:T, :]) if ap_fn is None else ap_fn
    kb.op("act", lambda e: e.activation(out=a(rstd), in_=a(ss), func=AF.Sqrt, scale=scale, bias=eps),
          reads=[ss], writes=[rstd])
    kb.op("dve", lambda e: e.reciprocal(out=a(rstd), in_=a(rstd)), reads=[rstd], writes=[rstd])


def load_vec_fm(g, handle, row, nch, dstbuf, dst_ap, vtmp, col0=0):
    kb = g.kb
    ncols = handle.shape[1]
    src = bass.AP(handle, row * ncols + col0, [[128, nch], [1, 128]])
    kb.dma(vtmp[:nch, :], src, writes=[vtmp], sem_buf=vtmp)
    pt = next_ps(g)
    kb.op("pe", lambda e: e.transpose(out=pt[:, :nch], in_=vtmp[:nch, :], identity=g.ident_f[:nch, :nch]),
          reads=[vtmp, g.ident_f], writes=[pt])
    kb.op("dve", lambda e: e.tensor_copy(out=dst_ap, in_=pt[:, :nch]), reads=[pt], writes=[dstbuf])


def rmsnorm_T(g, HT, T, Gb, hn, hnT, ss, rstd, junk):
    kb = g.kb
    kb.op("act", lambda e: e.activation(out=junk[:T, 0:D], in_=HT[:T, :], func=AF.Square, accum_out=ss[:T, :]),
          reads=[HT], writes=[junk, ss])
    rstd_from_ss(kb, ss, rstd, T, 1.0 / D, 1e-6)
    kb.op("dve", lambda e: e.scalar_tensor_tensor(out=hn[:T, :], in0=HT[:T, :], scalar=rstd[:T, :],
                                                  in1=Gb[:T, :], op0=ALU.mult, op1=ALU.mult),
          reads=[HT, rstd, Gb], writes=[hn])
    PST = g.PST
    for kc in range(8):
        kb.op("pe", lambda e: e.transpose(out=PST[:, kc * T:(kc + 1) * T], in_=hn[:T, kc * 128:(kc + 1) * 128],
                                          identity=g.ident_b[:T, :T]),
              reads=[hn, g.ident_b], writes=[PST], inc=(kc == 7))
    kb.op("act", lambda e: e.copy(out=hnT[:, :, :T], in_=PST[:, 0:8 * T].rearrange("p (k t) -> p k t", k=8)),
          reads=[PST], writes=[hnT])


def norm_stats(g, HT, T, Gb, hn, ss, rstd, junk):
    kb = g.kb
    kb.op("act", lambda e: e.activation(out=junk[:T, 0:D], in_=HT[:T, :], func=AF.Square, accum_out=ss[:T, :]),
          reads=[HT], writes=[junk, ss])
    rstd_from_ss(kb, ss, rstd, T, 1.0 / D, 1e-6)
    kb.op("dve", lambda e: e.scalar_tensor_tensor(out=hn[:T, :], in0=HT[:T, :], scalar=rstd[:T, :],
                                                  in1=Gb[:T, :], op0=ALU.mult, op1=ALU.mult),
          reads=[HT, rstd, Gb], writes=[hn])


def norm_transpose(g, hn, hnT, T):
    kb = g.kb
    PST = g.PST
    for kc in range(8):
        kb.op("pe", lambda e: e.transpose(out=PST[:, kc * T:(kc + 1) * T], in_=hn[:T, kc * 128:(kc + 1) * 128],
                                          identity=g.ident_b[:T, :T]),
              reads=[hn, g.ident_b], writes=[PST], inc=(kc == 7))
    kb.op("act", lambda e: e.copy(out=hnT[:, :, :T], in_=PST[:, 0:8 * T].rearrange("p (k t) -> p k t", k=8)),
          reads=[PST], writes=[hnT])


def phase_ffn(g, layer, src, dst, final):
    kb, nc, dr = g.kb, g.nc, g.dr
    Wup = kb.sb([128, 8, 2 * DFF], BF16, "Wup")
    Wdn = kb.sb([128, NFC, D], BF16, "Wdn")
    with contextlib.ExitStack() as ses:
        old = kb.es
        kb.es = ses
        stg = [kb.sb([128, 1408], F32, f"stg{i}") for i in range(3)]
        load_weight_bf16(g, dr["f_w_up"], layer * D, D, 2 * DFF, Wup, stg)
        load_weight_bf16(g, dr["f_w_down"], layer * DFF, DFF, D, Wdn, stg)
        kb.barrier()
        kb.es = old
    Gb = kb.sb([128, D], F32, "Gb")
    kb.dma(Gb[:, :], bc_rows(dr["norm_ffn"], layer, D), writes=[Gb], sem_buf=Gb)
    Gfin = None
    if dst == 4 and final:
        Gfin = kb.sb([128, D], F32, "Gfin")
        kb.dma(Gfin[:, :], bc_rows(dr["norm_final"], 0, D), writes=[Gfin], sem_buf=Gfin)
    CW = kb.sb([128, 3, NFC], F32, "CW")
    CB = kb.sb([128, NFC], F32, "CB")
    vtmp = kb.sb([32, 128], F32, "vtmp")
    for j in range(3):
        load_vec_fm(g, dr["f_conv_w"], layer * 3 + j, NFC, CW, CW[:, j, :], vtmp)
    load_vec_fm(g, dr["f_conv_b"], layer, NFC, CB, CB[:, :], vtmp)
    HTs = [kb.sb([128, D], F32, f"HT{i}") for i in range(3)]
    hns = [kb.sb([128, D], BF16, f"hn{i}") for i in range(2)]
    hnTs = [kb.sb([128, 8, 128], BF16, f"hnT{i}") for i in range(2)]
    junk = kb.sb([128, D], BF16, "junk")
    sss = [kb.sb([128, 1], F32, f"ss{i}") for i in range(3)]
    rstds = [kb.sb([128, 1], F32, f"rstd{i}") for i in range(3)]
    G = kb.sb([128, NFC, 130], F32, "G")
    ACC = [kb.sb([128, 4, 128], F32, f"acc{i}") for i in range(2)]
    SIL = [kb.sb([128, 4, 128], F32, f"sil{i}") for i in range(2)]
    ACTT = kb.sb([128, NFC, 128], BF16, "ACTT")
    kb.op("dve", lambda e: e.memset(G[:, :, :], 0.0), writes=[G])
    po_banks = [g.PS[5], g.PS[6]]
    rot = g.PS[0:5]
    rot_i = [0]

    def next_rot():
        b = rot[rot_i[0] % len(rot)]
        rot_i[0] += 1
        return b

    ntl = len(g.tiles)
    load_h(g, src, 0, HTs[0])
    norm_stats(g, HTs[0], g.tiles[0][1], Gb, hns[0], sss[0], rstds[0], junk)
    if ntl > 1:
        load_h(g, src, 1, HTs[1])
    norm_transpose(g, hns[0], hnTs[0], g.tiles[0][1])

    def down_part(c_lo, c_hi, T):
        for c in range(c_lo, c_hi):
            for nb in range(2):
                po = po_banks[nb]
                kb.op("pe", lambda e: e.matmul(po[:T, :], ACTT[:, c, :T], Wdn[:, c, nb * 512:(nb + 1) * 512],
                                               start=(c == 0), stop=(c == NFC - 1)),
                      reads=[ACTT, Wdn], writes=[po], inc=(c == c_hi - 1))

    for ti, (r0, T) in enumerate(g.tiles):
        HT = HTs[ti % 3]
        hnT = hnTs[ti % 2]
        steps = list(range(0, NFC, 4))
        for si, c0 in enumerate(steps):
            nch = min(4, NFC - c0)
            pg = next_rot()
            pv = next_rot()
            for j in range(nch):
                for kc in range(8):
                    kb.op("pe", lambda e: e.matmul(pg[:, j * T:(j + 1) * T],
                                                   Wup[:, kc, DFF + (c0 + j) * 128: DFF + (c0 + j + 1) * 128],
                                                   hnT[:, kc, :T], start=(kc == 0), stop=(kc == 7)),
                          reads=[Wup, hnT], writes=[pg], inc=(kc == 7))
            for j in range(nch):
                for kc in range(8):
                    kb.op("pe", lambda e: e.matmul(pv[:, j * T:(j + 1) * T],
                                                   Wup[:, kc, (c0 + j) * 128:(c0 + j + 1) * 128],
                                                   hnT[:, kc, :T], start=(kc == 0), stop=(kc == 7)),
                          reads=[Wup, hnT], writes=[pv], inc=(kc == 7))
            if si >= 1:
                down_part(steps[si - 1], c0, T)
            kb.op("act", lambda e: e.copy(out=G[:, c0:c0 + nch, 2:2 + T],
                                          in_=pg[:, 0:nch * T].rearrange("p (c t) -> p c t", c=nch)),
                  reads=[pg], writes=[G])
            acc = ACC[si % 2]
            sil = SIL[si % 2]
            for j in range(nch):
                c = c0 + j
                kb.op("dve", lambda e: e.tensor_scalar(out=acc[:, j, :T], in0=G[:, c, 2:2 + T],
                                                       scalar1=CW[:, 2, c:c + 1], scalar2=CB[:, c:c + 1],
                                                       op0=ALU.mult, op1=ALU.add),
                      reads=[G, CW, CB], writes=[acc])
                kb.op("dve", lambda e: e.scalar_tensor_tensor(out=acc[:, j, :T], in0=G[:, c, 1:1 + T],
                                                              scalar=CW[:, 1, c:c + 1], in1=acc[:, j, :T],
                                                              op0=ALU.mult, op1=ALU.add),
                      reads=[G, CW, acc], writes=[acc])
                kb.op("dve", lambda e: e.scalar_tensor_tensor(out=acc[:, j, :T], in0=G[:, c, 0:T],
                                                              scalar=CW[:, 0, c:c + 1], in1=acc[:, j, :T],
                                                              op0=ALU.mult, op1=ALU.add),
                      reads=[G, CW, acc], writes=[acc])
            kb.op("act", lambda e: e.activation(out=sil[:, 0:nch, :T], in_=acc[:, 0:nch, :T], func=AF.Silu),
                  reads=[acc], writes=[sil])
            kb.op("dve", lambda e: e.tensor_tensor(out=ACTT[:, c0:c0 + nch, :T], in0=sil[:, 0:nch, :T],
                                                   in1=pv[:, 0:nch * T].rearrange("p (c t) -> p c t", c=nch),
                                                   op=ALU.mult),
                  reads=[sil, pv], writes=[ACTT])
        if ti + 1 < ntl:
            Tn = g.tiles[ti + 1][1]
            norm_stats(g, HTs[(ti + 1) % 3], Tn, Gb, hns[(ti + 1) % 2], sss[(ti + 1) % 3], rstds[(ti + 1) % 3], junk)
        down_part(steps[-1], NFC, T)
        kb.op("dve", lambda e: e.tensor_copy(out=G[:, :, 0:2], in_=G[:, :, T:T + 2]), reads=[G], writes=[G])
        if ti + 1 < ntl:
            norm_transpose(g, hns[(ti + 1) % 2], hnTs[(ti + 1) % 2], g.tiles[ti + 1][1])
        for nb in range(2):
            kb.op("dve", lambda e: e.tensor_tensor(out=HT[:T, nb * 512:(nb + 1) * 512],
                                                   in0=HT[:T, nb * 512:(nb + 1) * 512], in1=po_banks[nb][:T, :], op=ALU.add),
                  reads=[HT, po_banks[nb]], writes=[HT])
        store_h(g, dst, ti, HT, final, Gfin, (sss[2 - ti % 2 if False else (ti + 2) % 3], rstds[(ti + 2) % 3], junk))
        if ti + 2 < ntl:
            load_h(g, src, ti + 2, HTs[(ti + 2) % 3])


EH = 0.6065306597126334
ISQ = 0.08838834764831845
NEGBIG = -30000.0


def phase_l0(g, src, dst, final):
    import os
    CUT = int(os.environ.get('CUT', '99'))
    SUB = int(os.environ.get('SUB', '99'))
    HFN = int(os.environ.get('HFN', '2'))
    kb, nc, dr = g.kb, g.nc, g.dr
    Win = kb.sb([128, 8, 3848], BF16, "Win")
    Wout = kb.sb([128, 8, D], BF16, "Wout")
    W2A = kb.sb([128, 512], BF16, "W2A")
    G2 = kb.sb([128, 512], BF16, "G2")
    with contextlib.ExitStack() as ses:
        old = kb.es
        kb.es = ses
        stg = [kb.sb([128, 1924], F32, f"stg{i}") for i in range(3)]
        load_weight_bf16(g, dr["e_w_in"], 0, D, 3848, Win, stg)
        load_weight_bf16(g, dr["e_w_out"], 0, D, D, Wout, stg)
        s0 = stg[0]
        kb.dma(s0[0:64, 0:512], dr["r_w2"].ap()[:, :], writes=[s0], sem_buf=s0)
        kb.dma(s0[64:128, 0:512], dr["r_a2"].ap()[:, :], writes=[s0], sem_buf=s0)
        kb.op("dve", lambda e: e.tensor_copy(out=W2A[:, :], in_=s0[:, 0:512]), reads=[s0], writes=[W2A])
        s1 = stg[1]
        kb.dma(s1[:, 0:512], dr["r_g2"].ap()[:, :], writes=[s1], sem_buf=s1)
        kb.op("dve", lambda e: e.tensor_copy(out=G2[:, :], in_=s1[:, 0:512]), reads=[s1], writes=[G2])
        kb.barrier()
        kb.es = old
    F = lambda shape, name: kb.sb(shape, F32, name)
    Bf = lambda shape, name: kb.sb(shape, BF16, name)
    Gb = F([128, D], "Gb")
    kb.dma(Gb[:, :], bc_rows(dr["norm_mix"], 0, D), writes=[Gb], sem_buf=Gb)
    ue = F([128, 128], "ue")
    su = F([128, 128], "su")
    blk = F([128, 128], "blk")
    kb.dma(ue[:, :], dr["c_ue"].ap()[:, :], writes=[ue], sem_buf=ue)
    kb.dma(su[:, :], dr["c_su"].ap()[:, :], writes=[su], sem_buf=su)
    kb.dma(blk[:, :], dr["c_blk"].ap()[:, :], writes=[blk], sem_buf=blk)
    sl = F([128, 128], "sl")
    kb.op("dve", lambda e: e.tensor_scalar(out=sl[:, :], in0=ue[:, :], scalar1=-1.0, scalar2=1.0, op0=ALU.mult, op1=ALU.add),
          reads=[ue], writes=[sl])
    neg = F([128, 128], "neg")
    kb.op("dve", lambda e: e.tensor_scalar(out=neg[:, :], in0=sl[:, :], scalar1=NEGBIG, scalar2=None, op0=ALU.mult),
          reads=[sl], writes=[neg])
    blk64 = F([128, 128], "blk64")
    kb.op("dve", lambda e: e.tensor_scalar(out=blk64[:, :], in0=blk[:, :], scalar1=1.0 / 64.0, scalar2=None, op0=ALU.mult),
          reads=[blk], writes=[blk64])
    onesf = F([128, 128], "onesf")
    kb.op("dve", lambda e: e.memset(onesf[:, :], 1.0 / 128.0), writes=[onesf])
    onesb = Bf([128, 128], "onesb")
    kb.op("dve", lambda e: e.memset(onesb[:, :], 1.0), writes=[onesb])
    ones1 = F([128, 128], "ones1")
    kb.op("dve", lambda e: e.memset(ones1[:, :], 1.0), writes=[ones1])
    vtmp = F([32, 128], "vtmp")
    MU = F([128, 14], "MU"); W0 = F([128, 4], "W0"); A0 = F([128, 4], "A0"); KK = F([128, 4], "KK")
    KA = F([128, 4], "KA"); RRK = F([128, 4], "RRK"); LNW = F([128, 4], "LNW"); LNB = F([128, 4], "LNB")
    MN = F([128, 4], "MN")
    load_vec_fm(g, dr["r_mu"], 0, 14, MU, MU[:, :], vtmp)
    for nm, buf in (("r_w0", W0), ("r_a0", A0), ("r_k_k", KK), ("r_k_a", KA), ("r_r_k", RRK), ("r_ln_w", LNW),
                    ("r_ln_b", LNB), ("m_norm", MN)):
        load_vec_fm(g, dr[nm], 0, 4, buf, buf[:, :], vtmp)
    BG = F([128, 8], "BG")
    kb.dma(BG[:, 0:4], bc_rows(dr["m_b_i"], 0, 4), writes=[BG], sem_buf=BG)
    kb.dma(BG[:, 4:8], bc_rows(dr["m_b_f"], 0, 4), writes=[BG], sem_buf=BG)
    C = F([128, 4, 129], "C")
    Cb = Bf([128, 4, 128], "Cb")
    nbc = Bf([128, 4, 128], "nbc")
    ST = F([128, 4, 64], "ST")
    STb = Bf([128, 4, 64], "STb")
    ZR = F([128, 14, 129], "ZR")
    for b_ in (C, ST, ZR):
        kb.op("dve", lambda e: e.memset(b_[:, :, :], 0.0), writes=[b_])
    for b_ in (Cb, nbc, STb):
        kb.op("dve", lambda e: e.memset(b_[:, :, :], 0.0), writes=[b_])
    vTM1 = Bf([128, 4, 129], "vTM1")
    kb.op("dve", lambda e: e.memset(vTM1[:, :, :], 1.0), writes=[vTM1])
    HTs = [F([128, D], f"HT{i}") for i in range(2)]
    hn = Bf([128, D], "hn"); hnT = Bf([128, 8, 128], "hnT"); junk = Bf([128, D], "junk")
    ss = F([128, 1], "ss"); rstd = F([128, 1], "rstd")
    qTb = Bf([128, 4, 128], "qTb"); kTb = Bf([128, 4, 128], "kTb"); moT = F([128, 4, 128], "moT")
    gx = F([128, 8], "gx"); th = F([128, 8], "th"); ex = F([128, 4], "ex"); LI = F([128, 4], "LI"); LF = F([128, 4], "LF")
    lmb = F([128, 4], "lmb"); LFb = F([128, 4, 128], "LFb"); arg = F([128, 4, 128], "arg"); ET = F([128, 4, 128], "ET")
    eB = F([128, 4, 128], "eB"); gcol = F([128, 4], "gcol"); ew = F([128, 4], "ew"); qs = Bf([128, 4, 128], "qs")
    sT = Bf([128, 4, 128], "sT"); kw = Bf([128, 4, 128], "kw")
    cden = F([128, 4, 128], "cden"); hT = F([128, 4, 128], "hT"); hsq = F([128, 4, 128], "hsq"); rs4 = F([128, 4, 128], "rs4")
    mixT = Bf([128, 8, 128], "mixT")
    kTMf = F([128, 512], "kTMf")
    Z2 = F([128, 14, 128], "Z2"); D1 = Z2
    LIN = Bf([128, 128], "LIN"); sxg = Bf([128, 128], "sxg")
    sw = arg; aa = ET; GG = LFb
    kkr = cden; tq = hsq; rn = rs4; kp = moT
    CS = hT; CSp = F([128, 4, 128], "CSp"); csl = F([128, 4], "csl")
    eW = F([128, 4, 128], "eW"); eWp = eB; eWi = F([128, 4, 128], "eWi"); eWT = F([128, 4, 128], "eWT")
    kka = F([128, 4, 128], "kka")
    AR = Bf([128, 4, 2, 128], "AR"); BT = Bf([128, 4, 128], "BT"); KT = Bf([128, 4, 128], "KT")
    BH = Bf([128, 4, 128], "BH"); KH = Bf([128, 4, 128], "KH"); vb = Bf([128, 4, 128], "vb")
    bonus = F([128, 4, 128], "bonus")
    BTm = [Bf([128, 4, 128], f"BTm{i}") for i in range(2)]
    KTm = [Bf([128, 4, 128], f"KTm{i}") for i in range(2)]
    ATm = [Bf([128, 4, 128], f"ATm{i}") for i in range(2)]
    STbd = Bf([128, 4, 128], "STbd")
    kb.op("dve", lambda e: e.memset(STbd[:, :, :], 0.0), writes=[STbd])
    VTM = Bf([128, 8, 64], "VTM"); BHT = Bf([128, 8, 64], "BHT"); KHT = Bf([128, 8, 64], "KHT"); UTM = Bf([128, 8, 64], "UTM")
    Xa = [F([128, 8, 128], "Xa0"), F([128, 8, 128], "Xa1")]
    XTa = [F([128, 8, 128], "XTa0"), F([128, 8, 128], "XTa1")]
    Pm = F([128, 8, 128], "Pm")
    ARB = Bf([128, 8, 128], "ARB"); AAK = Bf([128, 8, 128], "AAK"); ARK = Bf([128, 8, 128], "ARK")
    P1 = kTMf
    Of = CS; Osq = tq; mean_s = rn; var = kka

    def b3(buf, T, n=4):
        return buf[:, 0:n].unsqueeze(2).to_broadcast([128, n, T])

    for ti, (r0, T) in enumerate(g.tiles):
        if os.environ.get('ONLY_T0') and ti > 0:
            break
        HT = HTs[ti % 2]
        if ti == 0:
            load_h(g, src, 0, HT)
        rmsnorm_T(g, HT, T, Gb, hn, hnT, ss, rstd, junk)
        if ti + 1 < len(g.tiles) and not os.environ.get('ONLY_T0'):
            load_h(g, src, ti + 1, HTs[(ti + 1) % 2])

        def proj_fm(pbank, j, col):
            for kc in range(8):
                kb.op("pe", lambda e: e.matmul(pbank[:, j * T:(j + 1) * T], Win[:, kc, col:col + 128], hnT[:, kc, :T],
                                               start=(kc == 0), stop=(kc == 7)),
                      reads=[Win, hnT], writes=[pbank], inc=(kc == 7))

        def proj_tm(pbank, col, n, c0=0):
            for kc in range(8):
                kb.op("pe", lambda e: e.matmul(pbank[:T, c0:c0 + n], hnT[:, kc, :T], Win[:, kc, col:col + n],
                                               start=(kc == 0), stop=(kc == 7)),
                      reads=[hnT, Win], writes=[pbank], inc=(kc == 7))

        def v3(pbank, n=4):
            return pbank[:, 0:n * T].rearrange("p (c t) -> p c t", c=n)

        pq = next_ps(g); pk = next_ps(g); pmo = next_ps(g)
        for h in range(4):
            proj_fm(pq, h, h * 128)
        for h in range(4):
            proj_fm(pk, h, 512 + h * 128)
        for h in range(4):
            proj_fm(pmo, h, 1536 + h * 128)
        kb.op("act", lambda e: e.copy(out=qTb[:, :, :T], in_=v3(pq)), reads=[pq], writes=[qTb])
        kb.op("act", lambda e: e.copy(out=kTb[:, :, :T], in_=v3(pk)), reads=[pk], writes=[kTb])
        kb.op("act", lambda e: e.activation(out=moT[:, :, :T], in_=v3(pmo), func=AF.Sigmoid), reads=[pmo], writes=[moT])
        pkt = next_ps(g); pvt = next_ps(g); pgt = next_ps(g)
        proj_tm(pkt, 512, 512)
        proj_tm(pvt, 1024, 512)
        proj_tm(pgt, 2048, 8)
        kb.op("act", lambda e: e.copy(out=vTM1[:T, :, 0:128], in_=pvt[:T, :].rearrange("p (h v) -> p h v", h=4)),
              reads=[pvt], writes=[vTM1])
        kb.op("act", lambda e: e.copy(out=kTMf[:T, :], in_=pkt[:T, :]), reads=[pkt], writes=[kTMf])
        if CUT <= 1:
            store_h(g, dst, ti, HT, final, None, (ss, rstd, junk))
            continue
        kb.op("dve", lambda e: e.tensor_tensor(out=gx[:T, :], in0=pgt[:T, 0:8], in1=BG[:T, :], op=ALU.add),
              reads=[pgt, BG], writes=[gx])
        kb.op("act", lambda e: e.activation(out=th[:T, :], in_=gx[:T, :], func=AF.Tanh, scale=1.0 / 15.0), reads=[gx], writes=[th])
        kb.op("dve", lambda e: e.tensor_scalar(out=LI[:T, :], in0=th[:T, 0:4], scalar1=15.0, scalar2=None, op0=ALU.mult),
              reads=[th], writes=[LI])
        kb.op("act", lambda e: e.activation(out=ex[:T, :], in_=th[:T, 4:8], func=AF.Exp, scale=-15.0), reads=[th], writes=[ex])
        kb.op("act", lambda e: e.activation(out=ex[:T, :], in_=ex[:T, :], func=AF.Ln, bias=1.0), reads=[ex], writes=[ex])
        kb.op("dve", lambda e: e.tensor_scalar(out=LF[:T, :], in0=ex[:T, :], scalar1=-1.0, scalar2=None, op0=ALU.mult),
              reads=[ex], writes=[LF])
        pbc = next_ps(g)
        kb.op("pe", lambda e: e.matmul(pbc[:T, 0:4], ue[:T, :T], LF[:T, :], start=True, stop=True), reads=[ue, LF], writes=[pbc])
        kb.op("dve", lambda e: e.tensor_tensor(out=lmb[:T, :], in0=LI[:T, :], in1=pbc[:T, 0:4], op=ALU.subtract),
              reads=[LI, pbc], writes=[lmb])
        kb.op("dve", lambda e: e.tensor_copy(out=LFb[:T, :, :], in_=LF[:T, 0:4].unsqueeze(2).to_broadcast([T, 4, 128])),
              reads=[LF], writes=[LFb])
        if CUT <= 2:
            store_h(g, dst, ti, HT, final, None, (ss, rstd, junk))
            continue
        pB = next_ps(g)
        for h in range(4):
            kb.op("pe", lambda e: e.matmul(pB[:, h * T:(h + 1) * T], LFb[:T, h, :], ue[:T, :T], start=True, stop=True),
                  reads=[LFb, ue], writes=[pB], inc=(h == 3))
        kb.op("dve", lambda e: e.tensor_tensor(out=arg[:T, :, :T], in0=v3(pB)[:T], in1=neg[:T, :T].unsqueeze(1).to_broadcast([T, 4, T]),
                                               op=ALU.add), reads=[pB, neg], writes=[arg])
        for h in range(4):
            kb.op("act", lambda e: e.activation(out=ET[:T, h, :T], in_=arg[:T, h, :T], func=AF.Exp, bias=lmb[:T, h:h + 1]),
                  reads=[arg, lmb], writes=[ET])
        kb.op("act", lambda e: e.activation(out=eB[:, :, :T], in_=v3(pB), func=AF.Exp), reads=[pB], writes=[eB])
        kb.op("dve", lambda e: e.tensor_copy(out=gcol[:, :], in_=v3(pB)[:, :, T - 1]), reads=[pB], writes=[gcol])
        for h in range(4):
            kb.op("act", lambda e: e.activation(out=ew[:T, h:h + 1], in_=lmb[:T, h:h + 1], func=AF.Exp, bias=gcol[:T, h:h + 1]),
                  reads=[lmb, gcol], writes=[ew])
        kb.op("dve", lambda e: e.tensor_tensor(out=qs[:, :, :T], in0=qTb[:, :, :T], in1=eB[:, :, :T], op=ALU.mult),
              reads=[qTb, eB], writes=[qs])
        if CUT <= 3:
            store_h(g, dst, ti, HT, final, None, (ss, rstd, junk))
            continue
        psc = next_ps(g)
        for h in range(4):
            kb.op("pe", lambda e: e.matmul(psc[:T, h * T:(h + 1) * T], kTb[:, h, :T], qTb[:, h, :T], start=True, stop=True),
                  reads=[kTb, qTb], writes=[psc], inc=(h == 3))
        kb.op("dve", lambda e: e.scalar_tensor_tensor(out=sT[:T, :, :T], in0=v3(psc)[:T], scalar=ISQ, in1=ET[:T, :, :T],
                                                      op0=ALU.mult, op1=ALU.mult), reads=[psc, ET], writes=[sT])
        pnum = next_ps(g); pden = next_ps(g)
        for h in range(4):
            kb.op("pe", lambda e: e.matmul(pnum[:, h * T:(h + 1) * T], vTM1[:T, h, 0:128], sT[:T, h, :T], start=True, stop=False),
                  reads=[vTM1, sT], writes=[pnum], inc=False)
            kb.op("pe", lambda e: e.matmul(pnum[:, h * T:(h + 1) * T], Cb[:, h, :], qs[:, h, :T], start=False, stop=True),
                  reads=[Cb, qs], writes=[pnum])
        for h in range(4):
            kb.op("pe", lambda e: e.matmul(pden[:, h * T:(h + 1) * T], onesb[:T, :], sT[:T, h, :T], start=True, stop=False),
                  reads=[onesb, sT], writes=[pden], inc=False)
            kb.op("pe", lambda e: e.matmul(pden[:, h * T:(h + 1) * T], nbc[:, h, :], qs[:, h, :T], start=False, stop=True),
                  reads=[nbc, qs], writes=[pden])
        kb.op("act", lambda e: e.activation(out=cden[:, :, :T], in_=v3(pden), func=AF.Abs), reads=[pden], writes=[cden])
        kb.op("dve", lambda e: e.tensor_scalar(out=cden[:, :, :T], in0=cden[:, :, :T], scalar1=1.0, scalar2=None, op0=ALU.max),
              reads=[cden], writes=[cden])
        kb.op("dve", lambda e: e.reciprocal(out=cden[:, :, :T], in_=cden[:, :, :T]), reads=[cden], writes=[cden])
        kb.op("dve", lambda e: e.tensor_tensor(out=hT[:, :, :T], in0=v3(pnum), in1=cden[:, :, :T], op=ALU.mult),
              reads=[pnum, cden], writes=[hT])
        kb.op("act", lambda e: e.activation(out=hsq[:, :, :T], in_=hT[:, :, :T], func=AF.Square), reads=[hT], writes=[hsq])
        pss = next_ps(g)
        for h in range(4):
            kb.op("pe", lambda e: e.matmul(pss[:, h * T:(h + 1) * T], onesf[:, :], hsq[:, h, :T], start=True, stop=True),
                  reads=[onesf, hsq], writes=[pss], inc=(h == 3))
        kb.op("act", lambda e: e.activation(out=rs4[:, :, :T], in_=v3(pss), func=AF.Sqrt, bias=1e-6), reads=[pss], writes=[rs4])
        kb.op("dve", lambda e: e.reciprocal(out=rs4[:, :, :T], in_=rs4[:, :, :T]), reads=[rs4], writes=[rs4])
        kb.op("dve", lambda e: e.tensor_tensor(out=hT[:, :, :T], in0=hT[:, :, :T], in1=rs4[:, :, :T], op=ALU.mult),
              reads=[hT, rs4], writes=[hT])
        kb.op("dve", lambda e: e.tensor_tensor(out=hT[:, :, :T], in0=hT[:, :, :T], in1=moT[:, :, :T], op=ALU.mult),
              reads=[hT, moT], writes=[hT])
        kb.op("dve", lambda e: e.tensor_tensor(out=mixT[:, 0:4, :T], in0=hT[:, :, :T], in1=b3(MN, T), op=ALU.mult),
              reads=[hT, MN], writes=[mixT])
        if CUT <= 4:
            store_h(g, dst, ti, HT, final, None, (ss, rstd, junk))
            continue
        for h in range(4):
            kb.op("dve", lambda e: e.tensor_scalar(out=kw[:T, h, :], in0=kTMf[:T, h * 128:(h + 1) * 128], scalar1=ew[:T, h:h + 1],
                                                   scalar2=ISQ, op0=ALU.mult, op1=ALU.mult), reads=[kTMf, ew], writes=[kw])
        for half in range(2):
            pC = next_ps(g)
            for hh in range(2):
                h = half * 2 + hh
                kb.op("pe", lambda e: e.matmul(pC[:, hh * 129:(hh + 1) * 129], kw[:T, h, :], vTM1[:T, h, :], start=True, stop=True),
                      reads=[kw, vTM1], writes=[pC], inc=(hh == 1))
            for hh in range(2):
                h = half * 2 + hh
                kb.op("dve", lambda e: e.scalar_tensor_tensor(out=C[:, h, :], in0=C[:, h, :], scalar=eB[:, h, T - 1:T],
                                                              in1=pC[:, hh * 129:(hh + 1) * 129], op0=ALU.mult, op1=ALU.add),
                      reads=[C, eB, pC], writes=[C])
        kb.op("act", lambda e: e.copy(out=Cb[:, :, :], in_=C[:, :, 0:128]), reads=[C], writes=[Cb])
        kb.op("dve", lambda e: e.tensor_copy(out=nbc[:, :, :], in_=C[:, :, 128:129].to_broadcast([128, 4, 128])),
              reads=[C], writes=[nbc])

        if CUT <= 5:
            store_h(g, dst, ti, HT, final, None, (ss, rstd, junk))
            continue
        zc = 2056
        for b0, n in ((0, 4), (4, 4), (8, 4), (12, 2)):
            pz = next_ps(g)
            for j in range(n):
                proj_fm(pz, j, zc + (b0 + j) * 128)
            kb.op("act", lambda e: e.copy(out=ZR[:, b0:b0 + n, 1:T + 1], in_=v3(pz, n)), reads=[pz], writes=[ZR])
        kb.op("dve", lambda e: e.tensor_tensor(out=D1[:, :, :T], in0=ZR[:, :, 0:T], in1=ZR[:, :, 1:T + 1], op=ALU.subtract),
              reads=[ZR], writes=[D1])
        kb.op("dve", lambda e: e.tensor_tensor(out=D1[:, :, :T], in0=D1[:, :, :T], in1=b3(MU, T, 14), op=ALU.mult),
              reads=[D1, MU], writes=[D1])
        kb.op("dve", lambda e: e.tensor_tensor(out=Z2[:, :, :T], in0=D1[:, :, :T], in1=ZR[:, :, 1:T + 1], op=ALU.add),
              reads=[D1, ZR], writes=[Z2])
        kb.op("dve", lambda e: e.tensor_copy(out=ZR[:, :, 0:1], in_=ZR[:, :, T:T + 1]), reads=[ZR], writes=[ZR])
        r_ = Z2[:, 0:4, :T]; k_ = Z2[:, 4:8, :T]; v_ = Z2[:, 8:12, :T]
        if CUT <= 6:
            store_h(g, dst, ti, HT, final, None, (ss, rstd, junk))
            continue
        kb.op("act", lambda e: e.activation(out=LIN[0:64, :T], in_=Z2[0:64, 12, :T], func=AF.Tanh), reads=[Z2], writes=[LIN])
        kb.op("act", lambda e: e.copy(out=LIN[64:128, :T], in_=Z2[64:128, 12, :T]), reads=[Z2], writes=[LIN])
        kb.op("act", lambda e: e.activation(out=sxg[:, :T], in_=Z2[:, 13, :T], func=AF.Sigmoid), reads=[Z2], writes=[sxg])
        pw = next_ps(g); pa = next_ps(g); pgg = next_ps(g)
        for c in range(4):
            kb.op("pe", lambda e: e.matmul(pw[:, c * T:(c + 1) * T], W2A[0:64, c * 128:(c + 1) * 128], LIN[0:64, :T], start=True, stop=True),
                  reads=[W2A, LIN], writes=[pw], inc=(c == 3))
        for c in range(4):
            kb.op("pe", lambda e: e.matmul(pa[:, c * T:(c + 1) * T], W2A[64:128, c * 128:(c + 1) * 128], LIN[64:128, :T], start=True, stop=True),
                  reads=[W2A, LIN], writes=[pa], inc=(c == 3))
        for c in range(4):
            kb.op("pe", lambda e: e.matmul(pgg[:, c * T:(c + 1) * T], G2[:, c * 128:(c + 1) * 128], sxg[:, :T], start=True, stop=True),
                  reads=[G2, sxg], writes=[pgg], inc=(c == 3))
        for c in range(4):
            kb.op("act", lambda e: e.activation(out=sw[:, c, :T], in_=pw[:, c * T:(c + 1) * T], func=AF.Sigmoid, bias=W0[:, c:c + 1]),
                  reads=[pw, W0], writes=[sw])
            kb.op("act", lambda e: e.activation(out=aa[:, c, :T], in_=pa[:, c * T:(c + 1) * T], func=AF.Sigmoid, bias=A0[:, c:c + 1]),
                  reads=[pa, A0], writes=[aa])
        kb.op("act", lambda e: e.copy(out=GG[:, :, :T], in_=v3(pgg)), reads=[pgg], writes=[GG])
        if CUT <= 7:
            store_h(g, dst, ti, HT, final, None, (ss, rstd, junk))
            continue
        kb.op("dve", lambda e: e.tensor_tensor(out=kkr[:, :, :T], in0=k_, in1=b3(KK, T), op=ALU.mult), reads=[Z2, KK], writes=[kkr])
        kb.op("act", lambda e: e.activation(out=tq[:, :, :T], in_=kkr[:, :, :T], func=AF.Square), reads=[kkr], writes=[tq])
        pn = next_ps(g)
        for c in range(4):
            kb.op("pe", lambda e: e.matmul(pn[:, c * T:(c + 1) * T], blk[:, :], tq[:, c, :T], start=True, stop=True),
                  reads=[blk, tq], writes=[pn], inc=(c == 3))
        kb.op("act", lambda e: e.activation(out=rn[:, :, :T], in_=v3(pn), func=AF.Sqrt), reads=[pn], writes=[rn])
        kb.op("dve", lambda e: e.tensor_scalar(out=rn[:, :, :T], in0=rn[:, :, :T], scalar1=1e-12, scalar2=None, op0=ALU.max),
              reads=[rn], writes=[rn])
        kb.op("dve", lambda e: e.reciprocal(out=rn[:, :, :T], in_=rn[:, :, :T]), reads=[rn], writes=[rn])
        kb.op("dve", lambda e: e.tensor_tensor(out=kkr[:, :, :T], in0=kkr[:, :, :T], in1=rn[:, :, :T], op=ALU.mult),
              reads=[kkr, rn], writes=[kkr])
        kb.op("dve", lambda e: e.scalar_tensor_tensor(out=tq[:, :, :T], in0=aa[:, :, :T], scalar=-1.0, in1=b3(KA, T),
                                                      op0=ALU.add, op1=ALU.mult), reads=[aa, KA], writes=[tq])
        kb.op("dve", lambda e: e.scalar_tensor_tensor(out=kp[:, :, :T], in0=tq[:, :, :T], scalar=1.0, in1=k_,
                                                      op0=ALU.add, op1=ALU.mult), reads=[tq, Z2], writes=[kp])
        if CUT <= 8:
            store_h(g, dst, ti, HT, final, None, (ss, rstd, junk))
            continue
        for c in range(4):
            kb.op("dve", lambda e: e.tensor_tensor_scan(out=CS[:, c, :T], data0=ones1[:, :T], data1=sw[:, c, :T], initial=0.0,
                                                        op0=ALU.mult, op1=ALU.add), reads=[ones1, sw], writes=[CS])
        kb.op("dve", lambda e: e.tensor_tensor(out=CSp[:, :, :T], in0=CS[:, :, :T], in1=sw[:, :, :T], op=ALU.subtract),
              reads=[CS, sw], writes=[CSp])
        kb.op("dve", lambda e: e.tensor_scalar(out=csl[:, :], in0=CS[:, :, T - 1], scalar1=-EH, scalar2=None, op0=ALU.mult),
              reads=[CS], writes=[csl])
        kb.op("act", lambda e: e.activation(out=eW[:, :, :T], in_=CS[:, :, :T], func=AF.Exp, scale=-EH), reads=[CS], writes=[eW])
        kb.op("act", lambda e: e.activation(out=eWp[:, :, :T], in_=CSp[:, :, :T], func=AF.Exp, scale=-EH), reads=[CSp], writes=[eWp])
        kb.op("act", lambda e: e.activation(out=eWi[:, :, :T], in_=CS[:, :, :T], func=AF.Exp, scale=EH), reads=[CS], writes=[eWi])
        for c in range(4):
            kb.op("act", lambda e: e.activation(out=eWT[:, c, :T], in_=CS[:, c, :T], func=AF.Exp, scale=EH, bias=csl[:, c:c + 1]),
                  reads=[CS, csl], writes=[eWT])
        kb.op("dve", lambda e: e.scalar_tensor_tensor(out=AR[:, :, 0, :T], in0=kkr[:, :, :T], scalar=-1.0, in1=eWp[:, :, :T],
                                                      op0=ALU.mult, op1=ALU.mult), reads=[kkr, eWp], writes=[AR])
        kb.op("dve", lambda e: e.tensor_tensor(out=AR[:, :, 1, :T], in0=r_, in1=eW[:, :, :T], op=ALU.mult), reads=[Z2, eW], writes=[AR])
        kb.op("dve", lambda e: e.tensor_tensor(out=kka[:, :, :T], in0=kkr[:, :, :T], in1=aa[:, :, :T], op=ALU.mult),
              reads=[kkr, aa], writes=[kka])
        kb.op("dve", lambda e: e.tensor_tensor(out=BT[:, :, :T], in0=kka[:, :, :T], in1=eWi[:, :, :T], op=ALU.mult),
              reads=[kka, eWi], writes=[BT])
        kb.op("dve", lambda e: e.tensor_tensor(out=KT[:, :, :T], in0=kp[:, :, :T], in1=eWi[:, :, :T], op=ALU.mult),
              reads=[kp, eWi], writes=[KT])
        kb.op("dve", lambda e: e.tensor_tensor(out=BH[:, :, :T], in0=kka[:, :, :T], in1=eWT[:, :, :T], op=ALU.mult),
              reads=[kka, eWT], writes=[BH])
        kb.op("dve", lambda e: e.tensor_tensor(out=KH[:, :, :T], in0=kp[:, :, :T], in1=eWT[:, :, :T], op=ALU.mult),
              reads=[kp, eWT], writes=[KH])
        kb.op("act", lambda e: e.copy(out=vb[:, :, :T], in_=v_), reads=[Z2], writes=[vb])
        kb.op("dve", lambda e: e.tensor_tensor(out=tq[:, :, :T], in0=r_, in1=kp[:, :, :T], op=ALU.mult), reads=[Z2, kp], writes=[tq])
        kb.op("dve", lambda e: e.tensor_tensor(out=tq[:, :, :T], in0=tq[:, :, :T], in1=b3(RRK, T), op=ALU.mult),
              reads=[tq, RRK], writes=[tq])
        prk = next_ps(g)
        for c in range(4):
            kb.op("pe", lambda e: e.matmul(prk[:, c * T:(c + 1) * T], blk[:, :], tq[:, c, :T], start=True, stop=True),
                  reads=[blk, tq], writes=[prk], inc=(c == 3))
        kb.op("dve", lambda e: e.tensor_tensor(out=bonus[:, :, :T], in0=v3(prk), in1=v_, op=ALU.mult), reads=[prk, Z2], writes=[bonus])
        if CUT <= 9:
            store_h(g, dst, ti, HT, final, None, (ss, rstd, junk))
            continue
        PST = g.PST
        for c in range(4):
            kb.op("pe", lambda e: e.transpose(out=PST[:T, c * 128:(c + 1) * 128], in_=vb[:, c, :T], identity=g.ident_b[:, :]),
                  reads=[vb, g.ident_b], writes=[PST], inc=False)
        for c in range(4):
            kb.op("pe", lambda e: e.transpose(out=PST[:T, (4 + c) * 128:(5 + c) * 128], in_=BH[:, c, :T], identity=g.ident_b[:, :]),
                  reads=[BH, g.ident_b], writes=[PST], inc=(c == 3))
        kb.op("act", lambda e: e.copy(out=VTM[:T, :, :], in_=PST[:T, 0:512].rearrange("p (h v) -> p h v", h=8)), reads=[PST], writes=[VTM])
        kb.op("act", lambda e: e.copy(out=BHT[:T, :, :], in_=PST[:T, 512:1024].rearrange("p (h v) -> p h v", h=8)), reads=[PST], writes=[BHT])
        for c in range(4):
            kb.op("pe", lambda e: e.transpose(out=PST[:T, c * 128:(c + 1) * 128], in_=KH[:, c, :T], identity=g.ident_b[:, :]),
                  reads=[KH, g.ident_b], writes=[PST], inc=(c == 3))
        kb.op("act", lambda e: e.copy(out=KHT[:T, :, :], in_=PST[:T, 0:512].rearrange("p (h v) -> p h v", h=8)), reads=[PST], writes=[KHT])
        if CUT <= 10:
            store_h(g, dst, ti, HT, final, None, (ss, rstd, junk))
            continue
        for hf in range(2):
            mcol = blk[:, 127 * hf:127 * hf + 1]
            kb.op("dve", lambda e: e.tensor_scalar(out=BTm[hf][:, :, :T], in0=BT[:, :, :T], scalar1=mcol, scalar2=None, op0=ALU.mult),
                  reads=[BT, blk], writes=[BTm[hf]])
            kb.op("dve", lambda e: e.tensor_scalar(out=KTm[hf][:, :, :T], in0=KT[:, :, :T], scalar1=mcol, scalar2=None, op0=ALU.mult),
                  reads=[KT, blk], writes=[KTm[hf]])
            kb.op("dve", lambda e: e.tensor_scalar(out=ATm[hf][:, :, :T], in0=AR[:, :, 0, :T], scalar1=mcol, scalar2=None, op0=ALU.mult),
                  reads=[AR, blk], writes=[ATm[hf]])
        X, XT = Xa[0], XTa[0]
        sub = su[:T, :T].unsqueeze(1).to_broadcast([T, 2, T])
        ueb = ue[:T, :T].unsqueeze(1).to_broadcast([T, 2, T])
        slb = sl[:T, :T].unsqueeze(1).to_broadcast([T, 4, T])
        for c in range(4):
            pNA = next_ps(g); pKA = next_ps(g)
            for hf in range(HFN):
                for j in range(2):
                    kb.op("pe", lambda e: e.matmul(pNA[:T, (hf * 2 + j) * T:(hf * 2 + j + 1) * T], BTm[hf][:, c, :T], AR[:, c, j, :T], start=True, stop=True),
                          reads=[BTm[hf], AR], writes=[pNA], inc=(hf == HFN - 1 and j == 1))
            for hf in range(2):
                for j in range(2):
                    kb.op("pe", lambda e: e.matmul(pKA[:T, (hf * 2 + j) * T:(hf * 2 + j + 1) * T], KTm[hf][:, c, :T], AR[:, c, j, :T], start=True, stop=True),
                          reads=[KTm[hf], AR], writes=[pKA], inc=(hf == HFN - 1 and j == 1))
            if SUB == 1:
                continue
            na4 = pNA[:T, 0:4 * T].rearrange("p (h j t) -> p h j t", h=2, j=2)
            ka4 = pKA[:T, 0:4 * T].rearrange("p (h j t) -> p h j t", h=2, j=2)
            kb.op("dve", lambda e: e.tensor_tensor(out=X[:T, 2 * c:2 * c + 2, :T], in0=na4[:, :, 0, :], in1=sub, op=ALU.mult),
                  reads=[pNA, su], writes=[X])
            kb.op("dve", lambda e: e.tensor_tensor(out=ARB[:T, 2 * c:2 * c + 2, :T], in0=na4[:, :, 1, :], in1=ueb, op=ALU.mult),
                  reads=[pNA, ue], writes=[ARB])
            kb.op("dve", lambda e: e.tensor_tensor(out=AAK[:T, 2 * c:2 * c + 2, :T], in0=ka4[:, :, 0, :], in1=sub, op=ALU.mult),
                  reads=[pKA, su], writes=[AAK])
            kb.op("dve", lambda e: e.tensor_tensor(out=ARK[:T, 2 * c:2 * c + 2, :T], in0=ka4[:, :, 1, :], in1=ueb, op=ALU.mult),
                  reads=[pKA, ue], writes=[ARK])
        if SUB <= 2:
            store_h(g, dst, ti, HT, final, None, (ss, rstd, junk))
            continue
        for half in range(2):
            pNb = next_ps(g)
            for j in range(4):
                h = half * 4 + j
                c, hf = h // 2, h % 2
                pl = slice(hf * 64, hf * 64 + 64)
                kb.op("pe", lambda e: e.matmul(pNb[:T, j * T:(j + 1) * T], ATm[hf][:, c, :T], BT[:, c, :T], start=True, stop=True),
                      reads=[ATm[hf], BT], writes=[pNb], inc=(j == 3))
            kb.op("dve", lambda e: e.tensor_tensor(out=XT[:T, half * 4:half * 4 + 4, :T], in0=v3(pNb)[:T], in1=slb, op=ALU.mult),
                  reads=[pNb, sl], writes=[XT])
        if CUT <= 11:
            store_h(g, dst, ti, HT, final, None, (ss, rstd, junk))
            continue
        kb.op("dve", lambda e: e.tensor_tensor(out=Pm[:T, :, :T], in0=X[:T, :, :T],
                                               in1=g.ident_f[:T, :T].unsqueeze(1).to_broadcast([T, 8, T]), op=ALU.add),
              reads=[X, g.ident_f], writes=[Pm])
        lv = 1
        cur = 0
        while lv * 2 < T:
            X, XT = Xa[cur], XTa[cur]
            Xn, XTn = Xa[1 - cur], XTa[1 - cur]
            for half in range(2):
                p1 = next_ps(g); p2 = next_ps(g)
                for j in range(4):
                    h = half * 4 + j
                    kb.op("pe", lambda e: e.matmul(p1[:T, j * T:(j + 1) * T], XT[:T, h, :T], X[:T, h, :T], start=True, stop=True),
                          reads=[XT, X], writes=[p1], inc=(j == 3))
                for j in range(4):
                    h = half * 4 + j
                    kb.op("pe", lambda e: e.matmul(p2[:T, j * T:(j + 1) * T], X[:T, h, :T], XT[:T, h, :T], start=True, stop=True),
                          reads=[XT, X], writes=[p2], inc=(j == 3))
                kb.op("act", lambda e: e.copy(out=Xn[:T, half * 4:half * 4 + 4, :T], in_=v3(p1)[:T]), reads=[p1], writes=[Xn])
                kb.op("dve", lambda e: e.tensor_copy(out=XTn[:T, half * 4:half * 4 + 4, :T], in_=v3(p2)[:T]), reads=[p2], writes=[XTn])
            for half in range(2):
                p3 = next_ps(g)
                for j in range(4):
                    h = half * 4 + j
                    kb.op("pe", lambda e: e.matmul(p3[:T, j * T:(j + 1) * T], XTn[:T, h, :T], Pm[:T, h, :T], start=True, stop=True),
                          reads=[XTn, Pm], writes=[p3], inc=(j == 3))
                kb.op("dve", lambda e: e.tensor_tensor(out=Pm[:T, half * 4:half * 4 + 4, :T], in0=Pm[:T, half * 4:half * 4 + 4, :T],
                                                       in1=v3(p3)[:T], op=ALU.add), reads=[Pm, p3], writes=[Pm])
            cur = 1 - cur
            lv *= 2
        if CUT <= 12:
            store_h(g, dst, ti, HT, final, None, (ss, rstd, junk))
            continue
        pP1 = next_ps(g)
        for h in range(8):
            c, hf = h // 2, h % 2
            pl = slice(hf * 64, hf * 64 + 64)
            kb.op("pe", lambda e: e.matmul(pP1[:T, h * 64:(h + 1) * 64], ATm[hf][:, c, :T], STb[:, c, :], start=True, stop=False),
                  reads=[ATm[hf], STb], writes=[pP1], inc=False)
            kb.op("pe", lambda e: e.matmul(pP1[:T, h * 64:(h + 1) * 64], AAK[:T, h, :T], VTM[:T, h, :], start=False, stop=True),
                  reads=[AAK, VTM], writes=[pP1], inc=(h == 7))
        kb.op("act", lambda e: e.copy(out=P1[:T, :], in_=pP1[:T, :]), reads=[pP1], writes=[P1])
        pU = next_ps(g)
        for h in range(8):
            kb.op("pe", lambda e: e.matmul(pU[:T, h * 64:(h + 1) * 64], Pm[:T, h, :T], P1[:T, h * 64:(h + 1) * 64], start=True, stop=True),
                  reads=[Pm, P1], writes=[pU], inc=(h == 7))
        kb.op("act", lambda e: e.copy(out=UTM[:T, :, :], in_=pU[:T, :].rearrange("p (h v) -> p h v", h=8)), reads=[pU], writes=[UTM])
        if CUT <= 13:
            store_h(g, dst, ti, HT, final, None, (ss, rstd, junk))
            continue
        pO = next_ps(g)
        for c in range(4):
            kb.op("pe", lambda e: e.matmul(pO[:, c * T:(c + 1) * T], STbd[:, c, :], AR[:, c, 1, :T], start=True, stop=False),
                  reads=[STbd, AR], writes=[pO], inc=False)
            for hf in range(2):
                h = 2 * c + hf
                pl = slice(hf * 64, hf * 64 + 64)
                kb.op("pe", lambda e: e.matmul(pO[pl, c * T:(c + 1) * T], UTM[:T, h, :], ARB[:T, h, :T], start=False, stop=False),
                      reads=[UTM, ARB], writes=[pO], inc=False)
                kb.op("pe", lambda e: e.matmul(pO[pl, c * T:(c + 1) * T], VTM[:T, h, :], ARK[:T, h, :T], start=False, stop=True),
                      reads=[VTM, ARK], writes=[pO], inc=(hf == 1))
        kb.op("act", lambda e: e.copy(out=Of[:, :, :T], in_=v3(pO)), reads=[pO], writes=[Of])
        pS = next_ps(g)
        for h in range(8):
            c, hf = h // 2, h % 2
            pl = slice(hf * 64, hf * 64 + 64)
            kb.op("pe", lambda e: e.matmul(pS[pl, c * 64:(c + 1) * 64], BHT[:T, h, :], UTM[:T, h, :], start=True, stop=False),
                  reads=[BHT, UTM], writes=[pS], inc=False)
            kb.op("pe", lambda e: e.matmul(pS[pl, c * 64:(c + 1) * 64], KHT[:T, h, :], VTM[:T, h, :], start=False, stop=True),
                  reads=[KHT, VTM], writes=[pS], inc=(h == 7))
        for c in range(4):
            kb.op("dve", lambda e: e.scalar_tensor_tensor(out=ST[:, c, :], in0=ST[:, c, :], scalar=eW[:, c, T - 1:T],
                                                          in1=pS[:, c * 64:(c + 1) * 64], op0=ALU.mult, op1=ALU.add),
                  reads=[ST, eW, pS], writes=[ST])
        kb.op("act", lambda e: e.copy(out=STb[:, :, :], in_=ST[:, :, :]), reads=[ST], writes=[STb])
        kb.op("act", lambda e: e.copy(out=STbd[0:64, :, 0:64], in_=ST[0:64, :, :]), reads=[ST], writes=[STbd])
        kb.op("act", lambda e: e.copy(out=STbd[64:128, :, 64:128], in_=ST[64:128, :, :]), reads=[ST], writes=[STbd])
        if CUT <= 14:
            store_h(g, dst, ti, HT, final, None, (ss, rstd, junk))
            continue
        kb.op("act", lambda e: e.activation(out=Osq[:, :, :T], in_=Of[:, :, :T], func=AF.Square), reads=[Of], writes=[Osq])
        pm_ = next_ps(g); pq_ = next_ps(g)
        for c in range(4):
            kb.op("pe", lambda e: e.matmul(pm_[:, c * T:(c + 1) * T], blk64[:, :], Of[:, c, :T], start=True, stop=True),
                  reads=[blk64, Of], writes=[pm_], inc=(c == 3))
        for c in range(4):
            kb.op("pe", lambda e: e.matmul(pq_[:, c * T:(c + 1) * T], blk64[:, :], Osq[:, c, :T], start=True, stop=True),
                  reads=[blk64, Osq], writes=[pq_], inc=(c == 3))
        kb.op("act", lambda e: e.copy(out=mean_s[:, :, :T], in_=v3(pm_)), reads=[pm_], writes=[mean_s])
        kb.op("dve", lambda e: e.scalar_tensor_tensor(out=var[:, :, :T], in0=mean_s[:, :, :T], scalar=-1.0, in1=mean_s[:, :, :T],
                                                      op0=ALU.mult, op1=ALU.mult), reads=[mean_s], writes=[var])
        kb.op("dve", lambda e: e.tensor_tensor(out=var[:, :, :T], in0=var[:, :, :T], in1=v3(pq_), op=ALU.add),
              reads=[var, pq_], writes=[var])
        kb.op("dve", lambda e: e.tensor_scalar(out=var[:, :, :T], in0=var[:, :, :T], scalar1=0.0, scalar2=None, op0=ALU.max),
              reads=[var], writes=[var])
        kb.op("act", lambda e: e.activation(out=var[:, :, :T], in_=var[:, :, :T], func=AF.Sqrt, bias=64e-5), reads=[var], writes=[var])
        kb.op("dve", lambda e: e.reciprocal(out=var[:, :, :T], in_=var[:, :, :T]), reads=[var], writes=[var])
        kb.op("dve", lambda e: e.tensor_tensor(out=Of[:, :, :T], in0=Of[:, :, :T], in1=mean_s[:, :, :T], op=ALU.subtract),
              reads=[Of, mean_s], writes=[Of])
        kb.op("dve", lambda e: e.tensor_tensor(out=Of[:, :, :T], in0=Of[:, :, :T], in1=var[:, :, :T], op=ALU.mult),
              reads=[Of, var], writes=[Of])
        kb.op("dve", lambda e: e.tensor_tensor(out=Of[:, :, :T], in0=Of[:, :, :T], in1=b3(LNW, T), op=ALU.mult),
              reads=[Of, LNW], writes=[Of])
        kb.op("dve", lambda e: e.tensor_tensor(out=Of[:, :, :T], in0=Of[:, :, :T], in1=b3(LNB, T), op=ALU.add),
              reads=[Of, LNB], writes=[Of])
        kb.op("dve", lambda e: e.tensor_tensor(out=Of[:, :, :T], in0=Of[:, :, :T], in1=bonus[:, :, :T], op=ALU.add),
              reads=[Of, bonus], writes=[Of])
        kb.op("dve", lambda e: e.tensor_tensor(out=mixT[:, 4:8, :T], in0=Of[:, :, :T], in1=GG[:, :, :T], op=ALU.mult),
              reads=[Of, GG], writes=[mixT])
        if CUT <= 15:
            store_h(g, dst, ti, HT, final, None, (ss, rstd, junk))
            continue
        if os.environ.get('ZERO_M'):
            kb.op("dve", lambda e: e.memset(mixT[:, 0:4, :], 0.0), writes=[mixT])
        if os.environ.get('ZERO_R'):
            kb.op("dve", lambda e: e.tensor_scalar(out=mixT[:, 4:8, :T], in0=mixT[:, 4:8, :T], scalar1=0.0, scalar2=None, op0=ALU.mult), reads=[mixT], writes=[mixT])
        for nb in range(2):
            pp = next_ps(g)
            for c in range(8):
                kb.op("pe", lambda e: e.matmul(pp[:T, :], mixT[:, c, :T], Wout[:, c, nb * 512:(nb + 1) * 512],
                                               start=(c == 0), stop=(c == 7)), reads=[mixT, Wout], writes=[pp], inc=(c == 7))
            kb.op("dve", lambda e: e.tensor_tensor(out=HT[:T, nb * 512:(nb + 1) * 512], in0=HT[:T, nb * 512:(nb + 1) * 512],
                                                   in1=pp[:T, :], op=ALU.add), reads=[HT, pp], writes=[HT])
        store_h(g, dst, ti, HT, final, None, (ss, rstd, junk))


LG = [float(np.log(1.0 - 2.0 ** (-5.0 - h))) for h in range(4)]
TWO_PI = 6.283185307179586
CW1 = 6.28125
CW2 = TWO_PI - CW1


LG = [float(np.log(1.0 - 2.0 ** (-5.0 - h))) for h in range(4)]
TWO_PI = 6.283185307179586
CW1 = 6.28125
CW2 = TWO_PI - CW1


LG = [float(np.log(1.0 - 2.0 ** (-5.0 - h))) for h in range(4)]
TWO_PI = 6.283185307179586
CW1 = 6.28125
CW2 = TWO_PI - CW1


def phase_l1(g, src, dst, final):
    kb, nc, dr = g.kb, g.nc, g.dr
    Win = kb.sb([128, 8, 6144], BF16, "Win")
    Wout = kb.sb([128, 16, D], BF16, "Wout")
    with contextlib.ExitStack() as ses:
        old = kb.es
        kb.es = ses
        stg = [kb.sb([128, 1536], F32, f"stg{i}") for i in range(3)]
        load_weight_bf16(g, dr["o_w_in_p"], 0, D, 6144, Win, stg)
        load_weight_bf16(g, dr["o_w_out"], 0, 2048, D, Wout, stg)
        kb.barrier()
        kb.es = old
    Gb = kb.sb([128, D], BF16, "Gb")
    Gfin = None
    iota = kb.sb([128, 128], F32, "iota")
    pidx = kb.sb([128, 1], F32, "pidx")
    ue = kb.sb([128, 128], F32, "ue")
    inv = kb.sb([128, 1], F32, "inv")
    kb.dma(iota[:, :], dr["c_iota"].ap()[:, :], writes=[iota], sem_buf=iota)
    kb.dma(pidx[:, :], dr["c_pidx"].ap()[:, :], writes=[pidx], sem_buf=pidx)
    kb.dma(ue[:, :], dr["c_ue"].ap()[:, :], writes=[ue], sem_buf=ue)
    kb.dma(inv[:, :], dr["c_inv"].ap()[:, :], writes=[inv], sem_buf=inv)
    DM = kb.sb([128, 4, 128], F32, "DM")
    DEC = kb.sb([128, 4, 128], F32, "DEC")
    KDEC = {128: kb.sb([128, 4], F32, "KDEC128"), 16: kb.sb([128, 4], F32, "KDEC16")}
    tms = kb.sb([128, 128], F32, "tms")
    kb.op("dve", lambda e: e.tensor_scalar(out=tms[:, :], in0=iota[:, :], scalar1=pidx[:, 0:1], scalar2=0.0,
                                           op0=ALU.subtract, op1=ALU.max), reads=[iota, pidx], writes=[tms])
    for h in range(4):
        kb.op("act", lambda e: e.activation(out=DM[:, h, :], in_=tms[:, :], func=AF.Exp, scale=LG[h]),
              reads=[tms], writes=[DM])
        kb.op("dve", lambda e: e.scalar_tensor_tensor(out=DM[:, h, :], in0=DM[:, h, :], scalar=1.0 / 16.0,
                                                      in1=ue[:, :], op0=ALU.mult, op1=ALU.mult),
              reads=[DM, ue], writes=[DM])
        kb.op("act", lambda e: e.activation(out=DEC[:, h, :], in_=iota[:, :], func=AF.Exp, scale=LG[h], bias=LG[h]),
              reads=[iota], writes=[DEC])
        for TT in (128, 16):
            kd = KDEC[TT]
            kb.op("act", lambda e: e.activation(out=kd[:, h:h + 1], in_=pidx[:, 0:1], func=AF.Exp, scale=-LG[h],
                                                bias=LG[h] * (TT - 1)), reads=[pidx], writes=[kd])
            kb.op("dve", lambda e: e.tensor_scalar(out=kd[:, h:h + 1], in0=kd[:, h:h + 1], scalar1=1.0 / 16.0,
                                                   scalar2=None, op0=ALU.mult), reads=[kd], writes=[kd])
    Sr = kb.sb([128, 8, 512], F32, "Sr")
    Srb = kb.sb([128, 8, 512], BF16, "Srb")
    kb.op("dve", lambda e: e.memset(Sr[:, :, :], 0.0), writes=[Sr])
    kb.op("pool", lambda e: e.memset(Srb[:, :, :], 0.0), writes=[Srb])
    HTs = [kb.sb([128, D], F32, f"HT{i}") for i in range(2)]
    hn = kb.sb([128, D], BF16, "hn")
    hnT = kb.sb([128, 8, 128], BF16, "hnT")
    sss = [kb.sb([128, 1], F32, f"ss{i}") for i in range(2)]
    rstds = [kb.sb([128, 1], F32, f"rstd{i}") for i in range(2)]
    ang = kb.sb([128, 128], F32, "ang")
    ang2 = kb.sb([128, 128], F32, "ang2")
    kf = kb.sb([128, 128], F32, "kf")
    ki = kb.sb([128, 128], I32, "ki")
    nsins = [kb.sb([128, 128], F32, f"nsin{i}") for i in range(2)]
    ncoss = [kb.sb([128, 128], F32, f"ncos{i}") for i in range(2)]
    t1 = kb.sb([128, 4, 128], F32, "t1")
    t2 = kb.sb([128, 4, 128], F32, "t2")
    qb = kb.sb([128, 2, 4, 128], BF16, "qb")
    qdb = kb.sb([128, 2, 4, 128], BF16, "qdb")
    kbf = kb.sb([128, 2, 4, 128], BF16, "kbf")
    kdT = kb.sb([128, 8, 128], BF16, "kdT")
    sTm = kb.sb([128, 4, 128], BF16, "sTm")
    VT = kb.sb([128, 2048], BF16, "VT")
    GS = kb.sb([128, 2048], BF16, "GS")
    og = kb.sb([128, 2048], BF16, "og")
    ogT = kb.sb([128, 16, 128], BF16, "ogT")
    st6 = kb.sb([128, 6], F32, "st6")
    mv = kb.sb([128, 2], F32, "mv")
    rs = kb.sb([128, 1], F32, "rs")
    junk = og
    kb.dma(t1[:, :, :].rearrange("p a b -> p (a b)"), bc_rows(dr["norm_mix"], 1, 512), writes=[t1], sem_buf=t1)
    kb.op("dve", lambda e: e.tensor_copy(out=Gb[:, 0:512], in_=t1[:, :, :].rearrange("p a b -> p (a b)")), reads=[t1], writes=[Gb])
    kb.dma(t2[:, :, :].rearrange("p a b -> p (a b)"), bc_rows(dr["norm_mix"], 1, 512, col0=512), writes=[t2], sem_buf=t2)
    kb.op("dve", lambda e: e.tensor_copy(out=Gb[:, 512:1024], in_=t2[:, :, :].rearrange("p a b -> p (a b)")), reads=[t2], writes=[Gb])

    def sincos(dst_tbl, shift, pos0, T):
        kb.op("dve", lambda e: e.tensor_scalar(out=ang[:, :T], in0=iota[:, :T], scalar1=float(pos0), scalar2=inv[:, 0:1],
                                               op0=ALU.add, op1=ALU.mult), reads=[iota, inv], writes=[ang])
        if shift != 0.0:
            kb.op("dve", lambda e: e.tensor_scalar(out=ang[:, :T], in0=ang[:, :T], scalar1=shift, scalar2=None,
                                                   op0=ALU.add), reads=[ang], writes=[ang])
        kb.op("dve", lambda e: e.tensor_scalar(out=ki[:, :T], in0=ang[:, :T], scalar1=1.0 / TWO_PI, scalar2=None,
                                               op0=ALU.mult), reads=[ang], writes=[ki])
        kb.op("dve", lambda e: e.tensor_copy(out=kf[:, :T], in_=ki[:, :T]), reads=[ki], writes=[kf])
        kb.op("dve", lambda e: e.scalar_tensor_tensor(out=ang2[:, :T], in0=kf[:, :T], scalar=-CW1, in1=ang[:, :T],
                                                      op0=ALU.mult, op1=ALU.add), reads=[kf, ang], writes=[ang2])
        kb.op("dve", lambda e: e.scalar_tensor_tensor(out=ang2[:, :T], in0=kf[:, :T], scalar=-CW2, in1=ang2[:, :T],
                                                      op0=ALU.mult, op1=ALU.add), reads=[kf, ang2], writes=[ang2])
        kb.op("dve", lambda e: e.tensor_scalar(out=ang2[:, :T], in0=ang2[:, :T], scalar1=3.1415925, scalar2=-3.1415925,
                                               op0=ALU.min, op1=ALU.max), reads=[ang2], writes=[ang2])
        kb.op("act", lambda e: e.activation(out=dst_tbl[:, :T], in_=ang2[:, :T], func=AF.Sin),
              reads=[ang2], writes=[dst_tbl])

    ntl = len(g.tiles)
    load_h(g, src, 0, HTs[0])
    T0 = g.tiles[0][1]
    norm_stats(g, HTs[0], T0, Gb, hn, sss[0], rstds[0], junk)
    if ntl > 1:
        load_h(g, src, 1, HTs[1])
    norm_transpose(g, hn, hnT, T0)
    sincos(nsins[0], 0.0, g.tiles[0][0], T0)
    sincos(ncoss[0], np.pi / 2, g.tiles[0][0], T0)
    PST = g.PST
    for ti, (r0, T) in enumerate(g.tiles):
        HT = HTs[ti % 2]
        HO = HT
        nsin = nsins[ti % 2]
        ncos = ncoss[ti % 2]
        sb_ = nsin[:, :T].unsqueeze(1).to_broadcast([128, 4, T])
        cb_ = ncos[:, :T].unsqueeze(1).to_broadcast([128, 4, T])
        qk_banks = []
        for which in range(2):
            pe_ = next_ps(g)
            po_ = next_ps(g)
            qk_banks.append((pe_, po_))
            for eo, pb in ((0, pe_), (1, po_)):
                for h in range(4):
                    col = which * 1024 + h * 256 + eo * 128
                    for kc in range(8):
                        kb.op("pe", lambda e: e.matmul(pb[:, h * T:(h + 1) * T], Win[:, kc, col:col + 128],
                                                       hnT[:, kc, :T], start=(kc == 0), stop=(kc == 7)),
                              reads=[Win, hnT], writes=[pb], inc=(kc == 7))
        for which in range(2):
            pe_, po_ = qk_banks[which]
            pe3 = pe_[:, 0:4 * T].rearrange("p (h t) -> p h t", h=4)
            po3 = po_[:, 0:4 * T].rearrange("p (h t) -> p h t", h=4)
            dstb = qb if which == 0 else kbf
            kb.op("dve", lambda e: e.tensor_tensor(out=t1[:, :, :T], in0=pe3, in1=cb_, op=ALU.mult),
                  reads=[pe_, ncos], writes=[t1])
            kb.op("dve", lambda e: e.tensor_tensor(out=t2[:, :, :T], in0=po3, in1=sb_, op=ALU.mult),
                  reads=[po_, nsin], writes=[t2])
            kb.op("dve", lambda e: e.tensor_tensor(out=dstb[:, 0, :, :T], in0=t1[:, :, :T], in1=t2[:, :, :T],
                                                   op=ALU.subtract), reads=[t1, t2], writes=[dstb])
            kb.op("dve", lambda e: e.tensor_tensor(out=t1[:, :, :T], in0=po3, in1=cb_, op=ALU.mult),
                  reads=[po_, ncos], writes=[t1])
            kb.op("dve", lambda e: e.tensor_tensor(out=t2[:, :, :T], in0=pe3, in1=sb_, op=ALU.mult),
                  reads=[pe_, nsin], writes=[t2])
            kb.op("dve", lambda e: e.tensor_tensor(out=dstb[:, 1, :, :T], in0=t1[:, :, :T], in1=t2[:, :, :T],
                                                   op=ALU.add), reads=[t1, t2], writes=[dstb])
            if which == 0:
                for eo in range(2):
                    kb.op("pool", lambda e: e.tensor_tensor(out=qdb[:, eo, :, :T], in0=qb[:, eo, :, :T],
                                                            in1=DEC[:, :, :T], op=ALU.mult),
                          reads=[qb, DEC], writes=[qdb])
        for nb in range(4):
            pvv = next_ps(g)
            for kc in range(8):
                kb.op("pe", lambda e: e.matmul(pvv[:T, :], hnT[:, kc, :T], Win[:, kc, 2048 + nb * 512:2048 + (nb + 1) * 512],
                                               start=(kc == 0), stop=(kc == 7)), reads=[hnT, Win], writes=[pvv], inc=(kc == 7))
            kb.op("act", lambda e: e.copy(out=VT[:T, nb * 512:(nb + 1) * 512], in_=pvv[:T, :]), reads=[pvv], writes=[VT])
        for nb in range(4):
            pgg = next_ps(g)
            for kc in range(8):
                kb.op("pe", lambda e: e.matmul(pgg[:T, :], hnT[:, kc, :T], Win[:, kc, 4096 + nb * 512:4096 + (nb + 1) * 512],
                                               start=(kc == 0), stop=(kc == 7)), reads=[hnT, Win], writes=[pgg], inc=(kc == 7))
            kb.op("act", lambda e: e.activation(out=GS[:T, nb * 512:(nb + 1) * 512], in_=pgg[:T, :], func=AF.Silu),
                  reads=[pgg], writes=[GS])
        if ti + 1 < ntl:
            r0n, Tn = g.tiles[ti + 1]
            sincos(nsins[(ti + 1) % 2], 0.0, r0n, Tn)
            sincos(ncoss[(ti + 1) % 2], np.pi / 2, r0n, Tn)
        for h in range(4):
            for eo in range(2):
                j = h * 2 + eo
                kb.op("pe", lambda e: e.transpose(out=PST[:T, j * 128:(j + 1) * 128], in_=kbf[:, eo, h, :T],
                                                  identity=g.ident_b[:, :]),
                      reads=[kbf, g.ident_b], writes=[PST], inc=(j == 7))
        for h in range(4):
            kb.op("act", lambda e: e.activation(out=kdT[:T, 2 * h:2 * h + 2, :],
                                                in_=PST[:T, 2 * h * 128:(2 * h + 2) * 128].rearrange("p (j d) -> p j d", j=2),
                                                func=AF.Copy, scale=KDEC[T][:T, h:h + 1]),
                  reads=[PST, KDEC[T]], writes=[kdT])
        psc = next_ps(g)
        for h in range(4):
            for eo in range(2):
                kb.op("pe", lambda e: e.matmul(psc[:T, h * T:(h + 1) * T], kbf[:, eo, h, :T], qb[:, eo, h, :T],
                                               start=(eo == 0), stop=(eo == 1)),
                      reads=[kbf, qb], writes=[psc], inc=(eo == 1))
        kb.op("dve", lambda e: e.tensor_tensor(out=sTm[:T, :, :T],
                                               in0=psc[:T, 0:4 * T].rearrange("p (h t) -> p h t", h=4),
                                               in1=DM[:T, :, :T], op=ALU.mult), reads=[psc, DM], writes=[sTm])
        for h in range(4):
            po = next_ps(g)
            kb.op("pe", lambda e: e.matmul(po[:T, :], sTm[:T, h, :T], VT[:T, h * 512:(h + 1) * 512], start=True, stop=False),
                  reads=[sTm, VT], writes=[po], inc=False)
            for eo in range(2):
                kb.op("pe", lambda e: e.matmul(po[:T, :], qdb[:, eo, h, :T], Srb[:, 2 * h + eo, :], start=False, stop=(eo == 1)),
                      reads=[qdb, Srb], writes=[po], inc=(eo == 1))
            kb.op("dve", lambda e: e.bn_stats(out=st6[:T, :], in_=po[:T, :]), reads=[po], writes=[st6])
            kb.op("dve", lambda e: e.bn_aggr(out=mv[:T, :], in_=st6[:T, :]), reads=[st6], writes=[mv])
            kb.op("act", lambda e: e.activation(out=rs[:T, :], in_=mv[:T, 1:2], func=AF.Sqrt, scale=1.0, bias=1e-6),
                  reads=[mv], writes=[rs])
            kb.op("dve", lambda e: e.reciprocal(out=rs[:T, :], in_=rs[:T, :]), reads=[rs], writes=[rs])
            kb.op("dve", lambda e: e.tensor_scalar(out=og[:T, h * 512:(h + 1) * 512], in0=po[:T, :], scalar1=mv[:T, 0:1], scalar2=rs[:T, 0:1],
                                                   op0=ALU.subtract, op1=ALU.mult), reads=[po, mv, rs], writes=[og])
            kb.op("pool", lambda e: e.tensor_tensor(out=og[:T, h * 512:(h + 1) * 512], in0=og[:T, h * 512:(h + 1) * 512],
                                                    in1=GS[:T, h * 512:(h + 1) * 512], op=ALU.mult),
                  reads=[og, GS], writes=[og])
        gT = [float(np.exp(LG[h] * T)) for h in range(4)]
        for h in range(4):
            for eo in range(2):
                j = 2 * h + eo
                pst_ = next_ps(g)
                kb.op("pe", lambda e: e.matmul(pst_[:, :], kdT[:T, j, :], VT[:T, h * 512:(h + 1) * 512], start=True, stop=True),
                      reads=[kdT, VT], writes=[pst_])
                kb.op("dve", lambda e: e.scalar_tensor_tensor(out=Sr[:, j, :], in0=Sr[:, j, :], scalar=gT[h], in1=pst_[:, :],
                                                              op0=ALU.mult, op1=ALU.add), reads=[Sr, pst_], writes=[Sr])
                kb.op("act", lambda e: e.copy(out=Srb[:, j, :], in_=Sr[:, j, :]), reads=[Sr], writes=[Srb])
        for half in range(2):
            for j in range(8):
                c = half * 8 + j
                kb.op("pe", lambda e: e.transpose(out=PST[:, j * T:(j + 1) * T], in_=og[:T, c * 128:(c + 1) * 128],
                                                  identity=g.ident_b[:T, :T]),
                      reads=[og, g.ident_b], writes=[PST], inc=(j == 7))
            kb.op("act", lambda e: e.copy(out=ogT[:, half * 8:(half + 1) * 8, :T],
                                          in_=PST[:, 0:8 * T].rearrange("p (k t) -> p k t", k=8)),
                  reads=[PST], writes=[ogT])
        if ti + 1 < ntl:
            Tn = g.tiles[ti + 1][1]
            norm_stats(g, HTs[(ti + 1) % 2], Tn, Gb, hn, sss[(ti + 1) % 2], rstds[(ti + 1) % 2], junk)
        pps = []
        for nb in range(2):
            pp = next_ps(g)
            pps.append(pp)
            for c in range(16):
                kb.op("pe", lambda e: e.matmul(pp[:T, :], ogT[:, c, :T], Wout[:, c, nb * 512:(nb + 1) * 512],
                                               start=(c == 0), stop=(c == 15)), reads=[ogT, Wout], writes=[pp], inc=(c == 15))
        if ti + 1 < ntl:
            norm_transpose(g, hn, hnT, g.tiles[ti + 1][1])
        for nb in range(2):
            kb.op("dve", lambda e: e.tensor_tensor(out=HO[:T, nb * 512:(nb + 1) * 512],
                                                   in0=HT[:T, nb * 512:(nb + 1) * 512], in1=pps[nb][:T, :], op=ALU.add),
                  reads=[HT, pps[nb]], writes=[HO])
        store_h(g, dst, ti, HO, final, Gfin, (sss[ti % 2], rstds[ti % 2], junk))
        if ti + 2 < ntl:
            load_h(g, src, ti + 2, HTs[ti % 2])


def make_in_map(inputs, b, NT):
    m = {"x": np.ascontiguousarray(inputs["x"][b, :128 * NT])}
    for k, shp in W_SPECS.items():
        src_k = "o_w_in" if k == "o_w_in_p" else k
        m[k] = np.ascontiguousarray(np.asarray(inputs[src_k], np.float32).reshape(shp))
    m.update(host_consts())
    perm = np.arange(6144)
    for sec in range(2):
        for h in range(4):
            base = sec * 1024 + h * 256
            perm[base:base + 256] = np.concatenate([base + np.arange(0, 256, 2), base + np.arange(1, 256, 2)])
    m["o_w_in_p"] = np.ascontiguousarray(m["o_w_in_p"][:, perm])
    return m


NT_FULL = 32


def kernel(**inputs):
    nc = build(NT_FULL, phases=(1, 2, 3, 4), debug=False, final=True)
    in_maps = [make_in_map(inputs, b, NT_FULL) for b in range(8)]
    res = run_bass_kernel_spmd(nc, in_maps, core_ids=list(range(8)))
    return np.stack([np.asarray(r["out"], np.float32) for r in res.results], axis=0)
```

```python
import contextlib
import numpy as np
import concourse.bass as bass
import concourse.mybir as mybir

F32 = mybir.dt.float32
BF16 = mybir.dt.bfloat16
I32 = mybir.dt.int32
AF = mybir.ActivationFunctionType
ALU = mybir.AluOpType
AX = mybir.AxisListType


class Buf:
    __slots__ = ("t", "w", "r", "dsem", "dcount", "name", "excl")

    def __init__(self, t, name=""):
        self.t = t
        self.w = {}
        self.r = {}
        self.dsem = None
        self.dcount = 0
        self.name = name
        self.excl = False

    def __getitem__(self, idx):
        return self.t[idx]


class Eng:
    def __init__(self, name, obj, sem):
        self.name = name
        self.obj = obj
        self.sem = sem
        self.count = 0
        self.seen = {}


class KB:
    def __init__(self, nc, es):
        self.nc = nc
        self.es = es
        self.sems = {}
        self.E = {}
        for name, obj in (("pe", nc.tensor), ("act", nc.scalar), ("dve", nc.vector),
                          ("pool", nc.gpsimd), ("sp", nc.sync)):
            sem = es.enter_context(nc.semaphore("s_" + name))
            self.E[name] = Eng(name, obj, sem)
            self.sems[id(sem)] = sem
        self.dma_tokens = {}
        self.nbuf = 0

    def sb(self, shape, dt, name=None):
        self.nbuf += 1
        name = f"{name or 'b'}_{self.nbuf}"
        t = self.es.enter_context(self.nc.sbuf_tensor(name, list(shape), dt))
        return Buf(t, name)

    def ps(self, shape, dt, name=None):
        self.nbuf += 1
        name = f"{name or 'p'}_{self.nbuf}"
        t = self.es.enter_context(self.nc.psum_tensor(name, list(shape), dt))
        b = Buf(t, name)
        b.excl = True
        return b

    def newsem(self, name):
        sem = self.es.enter_context(self.nc.semaphore(name))
        self.sems[id(sem)] = sem
        return sem

    def _wait(self, e, deps):
        for sid, val in deps.items():
            if e.seen.get(sid, 0) < val:
                e.obj.wait_ge(self.sems[sid], val)
                e.seen[sid] = val

    def _deps(self, e, reads, writes):
        deps = {}
        own = id(e.sem)
        for b in reads:
            for sid, v in b.w.items():
                if deps.get(sid, 0) < v:
                    deps[sid] = v
            if b.excl:
                for sid, v in b.r.items():
                    if sid != own and deps.get(sid, 0) < v:
                        deps[sid] = v
        for b in writes:
            for d in (b.w, b.r):
                for sid, v in d.items():
                    if sid == own:
                        continue
                    if deps.get(sid, 0) < v:
                        deps[sid] = v
        return deps

    def op(self, eng, fn, reads=(), writes=(), inc=True):
        e = self.E[eng]
        self._wait(e, self._deps(e, reads, writes))
        ins = fn(e.obj)
        if inc:
            e.count += 1
            ins.then_inc(e.sem, 1)
            val = e.count
        else:
            val = e.count + 1
        sid = id(e.sem)
        for b in reads:
            if b.r.get(sid, 0) < val:
                b.r[sid] = val
        for b in writes:
            if b.w.get(sid, 0) < val:
                b.w[sid] = val
        return ins

    def dma(self, out_ap, in_ap, reads=(), writes=(), sem_buf=None, q="sp"):
        e = self.E[q]
        self._wait(e, self._deps(e, reads, writes))
        b = sem_buf
        if b.dsem is None:
            b.dsem = self.newsem("d_" + b.name)
        b.dcount += 16
        e.obj.dma_start(out=out_ap, in_=in_ap).then_inc(b.dsem, 16)
        sid = id(b.dsem)
        for x in reads:
            x.r[sid] = b.dcount
        for x in writes:
            x.w[sid] = b.dcount
        self.dma_tokens[sid] = b.dcount

    def barrier(self):
        targets = {id(e.sem): e.count for e in self.E.values() if e.count > 0}
        targets.update(self.dma_tokens)
        for e in self.E.values():
            self._wait(e, {k: v for k, v in targets.items() if k != id(e.sem)})

    def final_wait(self):
        e = self.E["sp"]
        self._wait(e, dict(self.dma_tokens))


from concourse.bass_utils import run_bass_kernel_spmd

D = 1024
NMETA = 16
DFF = 2816
NFC = DFF // 128

W_SPECS = {
    "meta_tokens": (16, 1024), "norm_mix": (2, 1024), "norm_ffn": (2, 1024), "norm_final": (1, 1024),
    "e_w_in": (1024, 3848), "e_w_out": (1024, 1024), "m_b_i": (1, 4), "m_b_f": (1, 4), "m_norm": (1, 512),
    "r_mu": (1, 1792), "r_w0": (1, 512), "r_w2": (64, 512), "r_a0": (1, 512), "r_a2": (64, 512),
    "r_g2": (128, 512), "r_k_k": (1, 512), "r_k_a": (1, 512), "r_r_k": (1, 512), "r_ln_w": (1, 512),
    "r_ln_b": (1, 512), "o_w_in_p": (1024, 6144), "o_w_out": (2048, 1024), "f_w_up": (2048, 5632),
    "f_conv_w": (6, 2816), "f_conv_b": (2, 2816), "f_w_down": (5632, 1024),
}


def host_consts():
    c = {}
    c["c_ident"] = np.eye(128, dtype=np.float32)
    i = np.arange(128)
    c["c_ue"] = (i[:, None] <= i[None, :]).astype(np.float32)
    c["c_su"] = (i[:, None] < i[None, :]).astype(np.float32)
    c["c_iota"] = np.broadcast_to(np.arange(128, dtype=np.float32)[None, :], (128, 128)).copy()
    c["c_pidx"] = np.arange(128, dtype=np.float32)[:, None].copy()
    bo = np.zeros((128, 128), np.float32)
    bo[:64, :64] = 1.0
    bo[64:, 64:] = 1.0
    c["c_blk"] = bo
    c["c_inv"] = (np.float32(1.0) / np.power(np.float32(10000.0), np.linspace(0.0, 1.0, 128, dtype=np.float32))
                  ).astype(np.float32)[:, None].copy()
    return c


class Ctx:
    pass


def tile_rows(NT):
    tiles = [(0, NMETA)]
    for i in range(NT):
        tiles.append((NMETA + 128 * i, 128))
    return tiles


def build(NT, phases=(1, 2, 3, 4), debug=False, final=True):
    nc = bass.Bass("TRN2", target_bir_lowering=False)
    SEQ = 128 * NT
    L = NMETA + SEQ
    dr = {}
    dr["x"] = nc.dram_tensor("x", [SEQ, D], F32, kind="ExternalInput")
    for k, shp in W_SPECS.items():
        dr[k] = nc.dram_tensor(k, list(shp), F32, kind="ExternalInput")
    for k, v in host_consts().items():
        dr[k] = nc.dram_tensor(k, list(v.shape), F32, kind="ExternalInput")
    out = nc.dram_tensor("out", [SEQ, D], F32, kind="ExternalOutput")
    H = {}
    for i in (1, 2, 3):
        H[i] = nc.dram_tensor(f"H{i}", [L, D], F32, kind=("ExternalOutput" if debug else "Internal"))

    tiles = tile_rows(NT)
    es = contextlib.ExitStack()
    with es:
        kb = KB(nc, es)
        PS = [kb.ps([128, 512], F32, f"psb{i}") for i in range(7)]
        PST = kb.ps([128, 1024], BF16, "pstr")
        g = Ctx()
        g.nc, g.kb, g.dr, g.H, g.out, g.tiles, g.PS, g.PST = nc, kb, dr, H, out, tiles, PS, PST
        g.psi = 0
        g.ident_f = kb.sb([128, 128], F32, "ident_f")
        g.ident_b = kb.sb([128, 128], BF16, "ident_b")
        kb.dma(g.ident_f[:, :], dr["c_ident"].ap()[:, :], writes=[g.ident_f], sem_buf=g.ident_f)
        kb.op("dve", lambda e: e.tensor_copy(out=g.ident_b[:, :], in_=g.ident_f[:, :]),
              reads=[g.ident_f], writes=[g.ident_b])

        plist = [p for p in (1, 2, 3, 4) if p in phases]
        src = 0
        for p in plist:
            dst = p if p != plist[-1] else 4
            with contextlib.ExitStack() as pes:
                kb.es = pes
                if p in (2, 4):
                    phase_ffn(g, layer=(0 if p == 2 else 1), src=src, dst=dst, final=final)
                elif p == 1:
                    phase_l0(g, src=src, dst=dst, final=final)
                elif p == 3:
                    phase_l1(g, src=src, dst=dst, final=final)
                kb.barrier()
            kb.es = es
            src = dst
        kb.final_wait()
    return nc


def next_ps(g):
    b = g.PS[g.psi % len(g.PS)]
    g.psi += 1
    return b


def bc_rows(handle, row, n, parts=128, col0=0, ncols_total=None):
    ncols_total = ncols_total if ncols_total is not None else handle.shape[1]
    return bass.AP(handle, row * ncols_total + col0, [[0, parts], [1, n]])


def load_h(g, src, ti, HT):
    kb = g.kb
    r0, T = g.tiles[ti]
    if src == 0:
        if ti == 0:
            ap = g.dr["meta_tokens"].ap()[0:NMETA, :]
        else:
            ap = g.dr["x"].ap()[r0 - NMETA:r0 - NMETA + T, :]
    else:
        ap = g.H[src].ap()[r0:r0 + T, :]
    kb.dma(HT[:T, :], ap, writes=[HT], sem_buf=HT)


def store_h(g, dst, ti, HO, final, Gfin=None, scratch=None):
    kb = g.kb
    r0, T = g.tiles[ti]
    if dst != 4:
        kb.dma(g.H[dst].ap()[r0:r0 + T, :], HO[:T, :], reads=[HO], sem_buf=HO)
        return
    if ti == 0:
        return
    if final:
        ss, rstd, junk = scratch
        kb.op("act", lambda e: e.activation(out=junk[:T, 0:D], in_=HO[:T, :], func=AF.Square, accum_out=ss[:T, :]),
              reads=[HO], writes=[junk, ss])
        rstd_from_ss(kb, ss, rstd, T, 1.0 / D, 1e-6)
        kb.op("dve", lambda e: e.scalar_tensor_tensor(out=HO[:T, :], in0=HO[:T, :], scalar=rstd[:T, :],
                                                      in1=Gfin[:T, :], op0=ALU.mult, op1=ALU.mult),
              reads=[HO, rstd, Gfin], writes=[HO])
    kb.dma(g.out.ap()[r0 - NMETA:r0 - NMETA + T, :], HO[:T, :], reads=[HO], sem_buf=HO)


def load_weight_bf16(g, dram_handle, row0, K, N, W, stg, col0=0, ncols_total=None):
    kb = g.kb
    SW = stg[0].t.shape[1]
    engs = ("dve", "act", "dve", "act", "dve", "pool", "dve", "act")
    cnt = getattr(g, "_lw_cnt", 0)
    for kc in range(K // 128):
        for j0 in range(0, N, SW):
            w = min(SW, N - j0)
            s = stg[cnt % len(stg)]
            kb.dma(s[:, :w], dram_handle.ap()[row0 + kc * 128: row0 + (kc + 1) * 128, col0 + j0: col0 + j0 + w],
                   writes=[s], sem_buf=s)
            en = engs[cnt % len(engs)]
            if en == "act":
                kb.op("act", lambda e: e.copy(out=W[:, kc, j0:j0 + w], in_=s[:, :w]), reads=[s], writes=[W])
            else:
                kb.op(en, lambda e: e.tensor_copy(out=W[:, kc, j0:j0 + w], in_=s[:, :w]), reads=[s], writes=[W])
            cnt += 1
    g._lw_cnt = cnt


def rstd_from_ss(kb, ss, rstd, T, scale, eps, ap_fn=None):
    a = (lambda b: b[:T, :]) if ap_fn is None else ap_fn
    kb.op("act", lambda e: e.activation(out=a(rstd), in_=a(ss), func=AF.Sqrt, scale=scale, bias=eps),
          reads=[ss], writes=[rstd])
    kb.op("dve", lambda e: e.reciprocal(out=a(rstd), in_=a(rstd)), reads=[rstd], writes=[rstd])


def load_vec_fm(g, handle, row, nch, dstbuf, dst_ap, vtmp, col0=0):
    kb = g.kb
    ncols = handle.shape[1]
    src = bass.AP(handle, row * ncols + col0, [[128, nch], [1, 128]])
    kb.dma(vtmp[:nch, :], src, writes=[vtmp], sem_buf=vtmp)
    pt = next_ps(g)
    kb.op("pe", lambda e: e.transpose(out=pt[:, :nch], in_=vtmp[:nch, :], identity=g.ident_f[:nch, :nch]),
          reads=[vtmp, g.ident_f], writes=[pt])
    kb.op("dve", lambda e: e.tensor_copy(out=dst_ap, in_=pt[:, :nch]), reads=[pt], writes=[dstbuf])


def rmsnorm_T(g, HT, T, Gb, hn, hnT, ss, rstd, junk):
    kb = g.kb
    kb.op("act", lambda e: e.activation(out=junk[:T, 0:D], in_=HT[:T, :], func=AF.Square, accum_out=ss[:T, :]),
          reads=[HT], writes=[junk, ss])
    rstd_from_ss(kb, ss, rstd, T, 1.0 / D, 1e-6)
    kb.op("dve", lambda e: e.scalar_tensor_tensor(out=hn[:T, :], in0=HT[:T, :], scalar=rstd[:T, :],
                                                  in1=Gb[:T, :], op0=ALU.mult, op1=ALU.mult),
          reads=[HT, rstd, Gb], writes=[hn])
    PST = g.PST
    for kc in range(8):
        kb.op("pe", lambda e: e.transpose(out=PST[:, kc * T:(kc + 1) * T], in_=hn[:T, kc * 128:(kc + 1) * 128],
                                          identity=g.ident_b[:T, :T]),
              reads=[hn, g.ident_b], writes=[PST], inc=(kc == 7))
    kb.op("act", lambda e: e.copy(out=hnT[:, :, :T], in_=PST[:, 0:8 * T].rearrange("p (k t) -> p k t", k=8)),
          reads=[PST], writes=[hnT])


def norm_stats(g, HT, T, Gb, hn, ss, rstd, junk):
    kb = g.kb
    kb.op("act", lambda e: e.activation(out=junk[:T, 0:D], in_=HT[:T, :], func=AF.Square, accum_out=ss[:T, :]),
          reads=[HT], writes=[junk, ss])
    rstd_from_ss(kb, ss, rstd, T, 1.0 / D, 1e-6)
    kb.op("dve", lambda e: e.scalar_tensor_tensor(out=hn[:T, :], in0=HT[:T, :], scalar=rstd[:T, :],
                                                  in1=Gb[:T, :], op0=ALU.mult, op1=ALU.mult),
          reads=[HT, rstd, Gb], writes=[hn])


def norm_transpose(g, hn, hnT, T):
    kb = g.kb
    PST = g.PST
    for kc in range(8):
        kb.op("pe", lambda e: e.transpose(out=PST[:, kc * T:(kc + 1) * T], in_=hn[:T, kc * 128:(kc + 1) * 128],
                                          identity=g.ident_b[:T, :T]),
              reads=[hn, g.ident_b], writes=[PST], inc=(kc == 7))
    kb.op("act", lambda e: e.copy(out=hnT[:, :, :T], in_=PST[:, 0:8 * T].rearrange("p (k t) -> p k t", k=8)),
          reads=[PST], writes=[hnT])


def phase_ffn(g, layer, src, dst, final):
    kb, nc, dr = g.kb, g.nc, g.dr
    Wup = kb.sb([128, 8, 2 * DFF], BF16, "Wup")
    Wdn = kb.sb([128, NFC, D], BF16, "Wdn")
    with contextlib.ExitStack() as ses:
        old = kb.es
        kb.es = ses
        stg = [kb.sb([128, 1408], F32, f"stg{i}") for i in range(3)]
        load_weight_bf16(g, dr["f_w_up"], layer * D, D, 2 * DFF, Wup, stg)
        load_weight_bf16(g, dr["f_w_down"], layer * DFF, DFF, D, Wdn, stg)
        kb.barrier()
        kb.es = old
    Gb = kb.sb([128, D], F32, "Gb")
    kb.dma(Gb[:, :], bc_rows(dr["norm_ffn"], layer, D), writes=[Gb], sem_buf=Gb)
    Gfin = None
    if dst == 4 and final:
        Gfin = kb.sb([128, D], F32, "Gfin")
        kb.dma(Gfin[:, :], bc_rows(dr["norm_final"], 0, D), writes=[Gfin], sem_buf=Gfin)
    CW = kb.sb([128, 3, NFC], F32, "CW")
    CB = kb.sb([128, NFC], F32, "CB")
    vtmp = kb.sb([32, 128], F32, "vtmp")
    for j in range(3):
        load_vec_fm(g, dr["f_conv_w"], layer * 3 + j, NFC, CW, CW[:, j, :], vtmp)
    load_vec_fm(g, dr["f_conv_b"], layer, NFC, CB, CB[:, :], vtmp)
    HTs = [kb.sb([128, D], F32, f"HT{i}") for i in range(3)]
    hns = [kb.sb([128, D], BF16, f"hn{i}") for i in range(2)]
    hnTs = [kb.sb([128, 8, 128], BF16, f"hnT{i}") for i in range(2)]
    junk = kb.sb([128, D], BF16, "junk")
    sss = [kb.sb([128, 1], F32, f"ss{i}") for i in range(3)]
    rstds = [kb.sb([128, 1], F32, f"rstd{i}") for i in range(3)]
    G = kb.sb([128, NFC, 130], F32, "G")
    ACC = [kb.sb([128, 4, 128], F32, f"acc{i}") for i in range(2)]
    SIL = [kb.sb([128, 4, 128], F32, f"sil{i}") for i in range(2)]
    ACTT = kb.sb([128, NFC, 128], BF16, "ACTT")
    kb.op("dve", lambda e: e.memset(G[:, :, :], 0.0), writes=[G])
    po_banks = [g.PS[5], g.PS[6]]
    rot = g.PS[0:5]
    rot_i = [0]

    def next_rot():
        b = rot[rot_i[0] % len(rot)]
        rot_i[0] += 1
        return b

    ntl = len(g.tiles)
    load_h(g, src, 0, HTs[0])
    norm_stats(g, HTs[0], g.tiles[0][1], Gb, hns[0], sss[0], rstds[0], junk)
    if ntl > 1:
        load_h(g, src, 1, HTs[1])
    norm_transpose(g, hns[0], hnTs[0], g.tiles[0][1])

    def down_part(c_lo, c_hi, T):
        for c in range(c_lo, c_hi):
            for nb in range(2):
                po = po_banks[nb]
                kb.op("pe", lambda e: e.matmul(po[:T, :], ACTT[:, c, :T], Wdn[:, c, nb * 512:(nb + 1) * 512],
                                               start=(c == 0), stop=(c == NFC - 1)),
                      reads=[ACTT, Wdn], writes=[po], inc=(c == c_hi - 1))

    for ti, (r0, T) in enumerate(g.tiles):
        HT = HTs[ti % 3]
        hnT = hnTs[ti % 2]
        steps = list(range(0, NFC, 4))
        for si, c0 in enumerate(steps):
            nch = min(4, NFC - c0)
            pg = next_rot()
            pv = next_rot()
            for j in range(nch):
                for kc in range(8):
                    kb.op("pe", lambda e: e.matmul(pg[:, j * T:(j + 1) * T],
                                                   Wup[:, kc, DFF + (c0 + j) * 128: DFF + (c0 + j + 1) * 128],
                                                   hnT[:, kc, :T], start=(kc == 0), stop=(kc == 7)),
                          reads=[Wup, hnT], writes=[pg], inc=(kc == 7))
            for j in range(nch):
                for kc in range(8):
                    kb.op("pe", lambda e: e.matmul(pv[:, j * T:(j + 1) * T],
                                                   Wup[:, kc, (c0 + j) * 128:(c0 + j + 1) * 128],
                                                   hnT[:, kc, :T], start=(kc == 0), stop=(kc == 7)),
                          reads=[Wup, hnT], writes=[pv], inc=(kc == 7))
            if si >= 1:
                down_part(steps[si - 1], c0, T)
            kb.op("act", lambda e: e.copy(out=G[:, c0:c0 + nch, 2:2 + T],
                                          in_=pg[:, 0:nch * T].rearrange("p (c t) -> p c t", c=nch)),
                  reads=[pg], writes=[G])
            acc = ACC[si % 2]
            sil = SIL[si % 2]
            for j in range(nch):
                c = c0 + j
                kb.op("dve", lambda e: e.tensor_scalar(out=acc[:, j, :T], in0=G[:, c, 2:2 + T],
                                                       scalar1=CW[:, 2, c:c + 1], scalar2=CB[:, c:c + 1],
                                                       op0=ALU.mult, op1=ALU.add),
                      reads=[G, CW, CB], writes=[acc])
                kb.op("dve", lambda e: e.scalar_tensor_tensor(out=acc[:, j, :T], in0=G[:, c, 1:1 + T],
                                                              scalar=CW[:, 1, c:c + 1], in1=acc[:, j, :T],
                                                              op0=ALU.mult, op1=ALU.add),
                      reads=[G, CW, acc], writes=[acc])
                kb.op("dve", lambda e: e.scalar_tensor_tensor(out=acc[:, j, :T], in0=G[:, c, 0:T],
                                                              scalar=CW[:, 0, c:c + 1], in1=acc[:, j, :T],
                                                              op0=ALU.mult, op1=ALU.add),
                      reads=[G, CW, acc], writes=[acc])
            kb.op("act", lambda e: e.activation(out=sil[:, 0:nch, :T], in_=acc[:, 0:nch, :T], func=AF.Silu),
                  reads=[acc], writes=[sil])
            kb.op("dve", lambda e: e.tensor_tensor(out=ACTT[:, c0:c0 + nch, :T], in0=sil[:, 0:nch, :T],
                                                   in1=pv[:, 0:nch * T].rearrange("p (c t) -> p c t", c=nch),
                                                   op=ALU.mult),
                  reads=[sil, pv], writes=[ACTT])
        if ti + 1 < ntl:
            Tn = g.tiles[ti + 1][1]
            norm_stats(g, HTs[(ti + 1) % 3], Tn, Gb, hns[(ti + 1) % 2], sss[(ti + 1) % 3], rstds[(ti + 1) % 3], junk)
        down_part(steps[-1], NFC, T)
        kb.op("dve", lambda e: e.tensor_copy(out=G[:, :, 0:2], in_=G[:, :, T:T + 2]), reads=[G], writes=[G])
        if ti + 1 < ntl:
            norm_transpose(g, hns[(ti + 1) % 2], hnTs[(ti + 1) % 2], g.tiles[ti + 1][1])
        for nb in range(2):
            kb.op("dve", lambda e: e.tensor_tensor(out=HT[:T, nb * 512:(nb + 1) * 512],
                                                   in0=HT[:T, nb * 512:(nb + 1) * 512], in1=po_banks[nb][:T, :], op=ALU.add),
                  reads=[HT, po_banks[nb]], writes=[HT])
        store_h(g, dst, ti, HT, final, Gfin, (sss[2 - ti % 2 if False else (ti + 2) % 3], rstds[(ti + 2) % 3], junk))
        if ti + 2 < ntl:
            load_h(g, src, ti + 2, HTs[(ti + 2) % 3])


EH = 0.6065306597126334
ISQ = 0.08838834764831845
NEGBIG = -30000.0


def phase_l0(g, src, dst, final):
    import os
    CUT = int(os.environ.get('CUT', '99'))
    SUB = int(os.environ.get('SUB', '99'))
    HFN = int(os.environ.get('HFN', '2'))
    kb, nc, dr = g.kb, g.nc, g.dr
    Win = kb.sb([128, 8, 3848], BF16, "Win")
    Wout = kb.sb([128, 8, D], BF16, "Wout")
    W2A = kb.sb([128, 512], BF16, "W2A")
    G2 = kb.sb([128, 512], BF16, "G2")
    with contextlib.ExitStack() as ses:
        old = kb.es
        kb.es = ses
        stg = [kb.sb([128, 1924], F32, f"stg{i}") for i in range(3)]
        load_weight_bf16(g, dr["e_w_in"], 0, D, 3848, Win, stg)
        load_weight_bf16(g, dr["e_w_out"], 0, D, D, Wout, stg)
        s0 = stg[0]
        kb.dma(s0[0:64, 0:512], dr["r_w2"].ap()[:, :], writes=[s0], sem_buf=s0)
        kb.dma(s0[64:128, 0:512], dr["r_a2"].ap()[:, :], writes=[s0], sem_buf=s0)
        kb.op("dve", lambda e: e.tensor_copy(out=W2A[:, :], in_=s0[:, 0:512]), reads=[s0], writes=[W2A])
        s1 = stg[1]
        kb.dma(s1[:, 0:512], dr["r_g2"].ap()[:, :], writes=[s1], sem_buf=s1)
        kb.op("dve", lambda e: e.tensor_copy(out=G2[:, :], in_=s1[:, 0:512]), reads=[s1], writes=[G2])
        kb.barrier()
        kb.es = old
    F = lambda shape, name: kb.sb(shape, F32, name)
    Bf = lambda shape, name: kb.sb(shape, BF16, name)
    Gb = F([128, D], "Gb")
    kb.dma(Gb[:, :], bc_rows(dr["norm_mix"], 0, D), writes=[Gb], sem_buf=Gb)
    ue = F([128, 128], "ue")
    su = F([128, 128], "su")
    blk = F([128, 128], "blk")
    kb.dma(ue[:, :], dr["c_ue"].ap()[:, :], writes=[ue], sem_buf=ue)
    kb.dma(su[:, :], dr["c_su"].ap()[:, :], writes=[su], sem_buf=su)
    kb.dma(blk[:, :], dr["c_blk"].ap()[:, :], writes=[blk], sem_buf=blk)
    sl = F([128, 128], "sl")
    kb.op("dve", lambda e: e.tensor_scalar(out=sl[:, :], in0=ue[:, :], scalar1=-1.0, scalar2=1.0, op0=ALU.mult, op1=ALU.add),
          reads=[ue], writes=[sl])
    neg = F([128, 128], "neg")
    kb.op("dve", lambda e: e.tensor_scalar(out=neg[:, :], in0=sl[:, :], scalar1=NEGBIG, scalar2=None, op0=ALU.mult),
          reads=[sl], writes=[neg])
    nblk = F([128, 2], "nblk")
    kb.op("dve", lambda e: e.tensor_scalar(out=nblk[:, 0:1], in0=blk[:, 0:1], scalar1=-1.0, scalar2=None, op0=ALU.mult),
          reads=[blk], writes=[nblk])
    kb.op("dve", lambda e: e.tensor_scalar(out=nblk[:, 1:2], in0=blk[:, 127:128], scalar1=-1.0, scalar2=None, op0=ALU.mult),
          reads=[blk], writes=[nblk])
    blk64 = F([128, 128], "blk64")
    kb.op("dve", lambda e: e.tensor_scalar(out=blk64[:, :], in0=blk[:, :], scalar1=1.0 / 64.0, scalar2=None, op0=ALU.mult),
          reads=[blk], writes=[blk64])
    onesf = F([128, 128], "onesf")
    kb.op("dve", lambda e: e.memset(onesf[:, :], 1.0 / 128.0), writes=[onesf])
    onesb = Bf([128, 128], "onesb")
    kb.op("dve", lambda e: e.memset(onesb[:, :], 1.0), writes=[onesb])
    ones1 = F([128, 128], "ones1")
    kb.op("dve", lambda e: e.memset(ones1[:, :], 1.0), writes=[ones1])
    vtmp = F([32, 128], "vtmp")
    MU = F([128, 14], "MU"); W0 = F([128, 4], "W0"); A0 = F([128, 4], "A0"); KK = F([128, 4], "KK")
    KA = F([128, 4], "KA"); RRK = F([128, 4], "RRK"); LNW = F([128, 4], "LNW"); LNB = F([128, 4], "LNB")
    MN = F([128, 4], "MN")
    load_vec_fm(g, dr["r_mu"], 0, 14, MU, MU[:, :], vtmp)
    for nm, buf in (("r_w0", W0), ("r_a0", A0), ("r_k_k", KK), ("r_k_a", KA), ("r_r_k", RRK), ("r_ln_w", LNW),
                    ("r_ln_b", LNB), ("m_norm", MN)):
        load_vec_fm(g, dr[nm], 0, 4, buf, buf[:, :], vtmp)
    BG = F([128, 8], "BG")
    kb.dma(BG[:, 0:4], bc_rows(dr["m_b_i"], 0, 4), writes=[BG], sem_buf=BG)
    kb.dma(BG[:, 4:8], bc_rows(dr["m_b_f"], 0, 4), writes=[BG], sem_buf=BG)
    C = F([128, 4, 129], "C")
    Cb = Bf([128, 4, 128], "Cb")
    nbc = Bf([128, 4, 128], "nbc")
    ST = F([128, 4, 64], "ST")
    STb = Bf([128, 4, 64], "STb")
    ZR = F([128, 14, 129], "ZR")
    for b_ in (C, ST, ZR):
        kb.op("dve", lambda e: e.memset(b_[:, :, :], 0.0), writes=[b_])
    for b_ in (Cb, nbc, STb):
        kb.op("dve", lambda e: e.memset(b_[:, :, :], 0.0), writes=[b_])
    vTM1 = Bf([128, 4, 129], "vTM1")
    kb.op("dve", lambda e: e.memset(vTM1[:, :, :], 1.0), writes=[vTM1])
    HTs = [F([128, D], f"HT{i}") for i in range(2)]
    hn = Bf([128, D], "hn"); hnT = Bf([128, 8, 128], "hnT"); junk = hn
    ss = F([128, 1], "ss"); rstd = F([128, 1], "rstd")
    qTb = Bf([128, 4, 128], "qTb"); kTb = Bf([128, 4, 128], "kTb"); moT = Bf([128, 4, 128], "moT"); kpbuf = F([128, 4, 128], "kpbuf")
    gx = F([128, 8], "gx"); th = F([128, 8], "th"); ex = F([128, 4], "ex"); LI = F([128, 4], "LI"); LF = F([128, 4], "LF")
    lmb = F([128, 4], "lmb"); LFb = F([128, 4, 128], "LFb"); arg = F([128, 4, 128], "arg"); ET = Bf([128, 4, 128], "ET"); aabuf = Bf([128, 4, 128], "aabuf")
    eB = F([128, 4, 128], "eB"); gcol = F([128, 4], "gcol"); ew = F([128, 4], "ew"); qs = Bf([128, 4, 128], "qs")
    sT = Bf([128, 4, 128], "sT"); kw = Bf([128, 4, 128], "kw")
    cden = F([128, 4, 128], "cden"); hT = F([128, 4, 128], "hT"); hsq = F([128, 4, 128], "hsq"); rs4 = F([128, 4, 128], "rs4")
    mixTs = [Bf([128, 8, 128], f"mixT{i}") for i in range(2)]
    kTMf = Bf([128, 512], "kTMf")
    P1 = F([128, 512], "P1")
    GG = Bf([128, 4, 128], "GG")
    Z2 = F([128, 14, 128], "Z2"); D1 = Z2
    LIN = Bf([128, 128], "LIN"); sxg = Bf([128, 128], "sxg")
    sw = arg; aa = aabuf
    kkr = cden; tq = hsq; rn = rs4; kp = kpbuf
    CS = hT; CSp = F([128, 4, 128], "CSp"); csl = F([128, 4], "csl")
    eW = F([128, 4, 128], "eW"); eWp = eB; eWi = F([128, 4, 128], "eWi"); eWT = F([128, 4, 128], "eWT")
    kka = F([128, 4, 128], "kka")
    AR = Bf([128, 4, 2, 128], "AR")
    BH = Bf([128, 4, 128], "BH"); KH = Bf([128, 4, 128], "KH"); vb = Bf([128, 4, 128], "vb")
    bonus = F([128, 4, 128], "bonus")
    BTm = [Bf([128, 4, 128], f"BTm{i}") for i in range(2)]
    KTm = [Bf([128, 4, 128], f"KTm{i}") for i in range(2)]
    ATm = [Bf([128, 4, 128], f"ATm{i}") for i in range(2)]
    STbd = Bf([128, 4, 128], "STbd")
    kb.op("dve", lambda e: e.memset(STbd[:, :, :], 0.0), writes=[STbd])
    VTM = Bf([128, 8, 64], "VTM"); BHT = Bf([128, 8, 64], "BHT"); KHT = Bf([128, 8, 64], "KHT"); UTM = Bf([128, 8, 64], "UTM")
    Xa = [F([128, 8, 128], "Xa0"), F([128, 8, 128], "Xa1")]
    XTa = [F([128, 8, 128], "XTa0"), F([128, 8, 128], "XTa1")]
    Pm = F([128, 8, 128], "Pm")
    ARB = Bf([128, 8, 128], "ARB"); AAK = Bf([128, 8, 128], "AAK"); ARK = Bf([128, 8, 128], "ARK")
    Of = CS; Osq = tq; mean_s = rn; var = kka

    def b3(buf, T, n=4):
        return buf[:, 0:n].unsqueeze(2).to_broadcast([128, n, T])

    psA = g.PS[0:3]
    psB = g.PS[3:7]
    ia = [0]
    ib = [0]

    def nA():
        b = psA[ia[0] % len(psA)]
        ia[0] += 1
        return b

    def nB():
        b = psB[ib[0] % len(psB)]
        ib[0] += 1
        return b

    graw = F([128, 8], "graw")

    def gen_proj(ti):
        r0, T = g.tiles[ti]
        HT = HTs[ti % 2]
        mixT = mixTs[ti % 2]
        def proj_fm(pbank, j, col):
            for kc in range(8):
                kb.op("pe", lambda e: e.matmul(pbank[:, j * T:(j + 1) * T], Win[:, kc, col:col + 128], hnT[:, kc, :T],
                                               start=(kc == 0), stop=(kc == 7)),
                      reads=[Win, hnT], writes=[pbank], inc=(kc == 7))

        def proj_tm(pbank, col, n, c0=0):
            for kc in range(8):
                kb.op("pe", lambda e: e.matmul(pbank[:T, c0:c0 + n], hnT[:, kc, :T], Win[:, kc, col:col + n],
                                               start=(kc == 0), stop=(kc == 7)),
                      reads=[hnT, Win], writes=[pbank], inc=(kc == 7))

        def v3(pbank, n=4):
            return pbank[:, 0:n * T].rearrange("p (c t) -> p c t", c=n)

        pq = nA(); pk = nA(); pmo = nA()
        for h in range(4):
            proj_fm(pq, h, h * 128)
        yield
        for h in range(4):
            proj_fm(pk, h, 512 + h * 128)
        yield
        for h in range(4):
            proj_fm(pmo, h, 1536 + h * 128)
        kb.op("act", lambda e: e.copy(out=qTb[:, :, :T], in_=v3(pq)), reads=[pq], writes=[qTb])
        kb.op("act", lambda e: e.copy(out=kTb[:, :, :T], in_=v3(pk)), reads=[pk], writes=[kTb])
        kb.op("act", lambda e: e.activation(out=moT[:, :, :T], in_=v3(pmo), func=AF.Sigmoid), reads=[pmo], writes=[moT])
        yield
        pkt = nA(); pvt = nA(); pgt = nA()
        proj_tm(pkt, 512, 512)
        yield
        proj_tm(pvt, 1024, 512)
        proj_tm(pgt, 2048, 8)
        kb.op("act", lambda e: e.copy(out=vTM1[:T, :, 0:128], in_=pvt[:T, :].rearrange("p (h v) -> p h v", h=4)),
              reads=[pvt], writes=[vTM1])
        kb.op("act", lambda e: e.copy(out=kTMf[:T, :], in_=pkt[:T, :]), reads=[pkt], writes=[kTMf])
        kb.op("act", lambda e: e.copy(out=graw[:T, :], in_=pgt[:T, 0:8]), reads=[pgt], writes=[graw])
        zc = 2056
        for b0, n in ((0, 4), (4, 4), (8, 4), (12, 2)):
            yield
            pz = nA()
            for j in range(n):
                proj_fm(pz, j, zc + (b0 + j) * 128)
            kb.op("act", lambda e: e.copy(out=ZR[:, b0:b0 + n, 1:T + 1], in_=v3(pz, n)), reads=[pz], writes=[ZR])

    def gen_mlstm(ti):
        r0, T = g.tiles[ti]
        HT = HTs[ti % 2]
        mixT = mixTs[ti % 2]
        def proj_fm(pbank, j, col):
            for kc in range(8):
                kb.op("pe", lambda e: e.matmul(pbank[:, j * T:(j + 1) * T], Win[:, kc, col:col + 128], hnT[:, kc, :T],
                                               start=(kc == 0), stop=(kc == 7)),
                      reads=[Win, hnT], writes=[pbank], inc=(kc == 7))

        def proj_tm(pbank, col, n, c0=0):
            for kc in range(8):
                kb.op("pe", lambda e: e.matmul(pbank[:T, c0:c0 + n], hnT[:, kc, :T], Win[:, kc, col:col + n],
                                               start=(kc == 0), stop=(kc == 7)),
                      reads=[hnT, Win], writes=[pbank], inc=(kc == 7))

        def v3(pbank, n=4):
            return pbank[:, 0:n * T].rearrange("p (c t) -> p c t", c=n)

        kb.op("dve", lambda e: e.tensor_tensor(out=gx[:T, :], in0=graw[:T, :], in1=BG[:T, :], op=ALU.add),
              reads=[graw, BG], writes=[gx])
        kb.op("act", lambda e: e.activation(out=th[:T, :], in_=gx[:T, :], func=AF.Tanh, scale=1.0 / 15.0), reads=[gx], writes=[th])
        kb.op("dve", lambda e: e.tensor_scalar(out=LI[:T, :], in0=th[:T, 0:4], scalar1=15.0, scalar2=None, op0=ALU.mult),
              reads=[th], writes=[LI])
        kb.op("act", lambda e: e.activation(out=ex[:T, :], in_=th[:T, 4:8], func=AF.Exp, scale=-15.0), reads=[th], writes=[ex])
        kb.op("act", lambda e: e.activation(out=ex[:T, :], in_=ex[:T, :], func=AF.Ln, bias=1.0), reads=[ex], writes=[ex])
        kb.op("dve", lambda e: e.tensor_scalar(out=LF[:T, :], in0=ex[:T, :], scalar1=-1.0, scalar2=None, op0=ALU.mult),
              reads=[ex], writes=[LF])
        yield
        pbc = nA()
        kb.op("pe", lambda e: e.matmul(pbc[:T, 0:4], ue[:T, :T], LF[:T, :], start=True, stop=True), reads=[ue, LF], writes=[pbc])
        kb.op("dve", lambda e: e.tensor_tensor(out=lmb[:T, :], in0=LI[:T, :], in1=pbc[:T, 0:4], op=ALU.subtract),
              reads=[LI, pbc], writes=[lmb])
        kb.op("dve", lambda e: e.tensor_copy(out=LFb[:T, :, :], in_=LF[:T, 0:4].unsqueeze(2).to_broadcast([T, 4, 128])),
              reads=[LF], writes=[LFb])
        yield
        pB = nA()
        for h in range(4):
            kb.op("pe", lambda e: e.matmul(pB[:, h * T:(h + 1) * T], LFb[:T, h, :], ue[:T, :T], start=True, stop=True),
                  reads=[LFb, ue], writes=[pB], inc=(h == 3))
        kb.op("dve", lambda e: e.tensor_tensor(out=arg[:T, :, :T], in0=v3(pB)[:T], in1=neg[:T, :T].unsqueeze(1).to_broadcast([T, 4, T]),
                                               op=ALU.add), reads=[pB, neg], writes=[arg])
        for h in range(4):
            kb.op("act", lambda e: e.activation(out=ET[:T, h, :T], in_=arg[:T, h, :T], func=AF.Exp, bias=lmb[:T, h:h + 1]),
                  reads=[arg, lmb], writes=[ET])
        kb.op("act", lambda e: e.activation(out=eB[:, :, :T], in_=v3(pB), func=AF.Exp), reads=[pB], writes=[eB])
        kb.op("dve", lambda e: e.tensor_copy(out=gcol[:, :], in_=v3(pB)[:, :, T - 1]), reads=[pB], writes=[gcol])
        for h in range(4):
            kb.op("act", lambda e: e.activation(out=ew[:T, h:h + 1], in_=lmb[:T, h:h + 1], func=AF.Exp, bias=gcol[:T, h:h + 1]),
                  reads=[lmb, gcol], writes=[ew])
        kb.op("dve", lambda e: e.tensor_tensor(out=qs[:, :, :T], in0=qTb[:, :, :T], in1=eB[:, :, :T], op=ALU.mult),
              reads=[qTb, eB], writes=[qs])
        yield
        psc = nA()
        for h in range(4):
            kb.op("pe", lambda e: e.matmul(psc[:T, h * T:(h + 1) * T], kTb[:, h, :T], qTb[:, h, :T], start=True, stop=True),
                  reads=[kTb, qTb], writes=[psc], inc=(h == 3))
        kb.op("dve", lambda e: e.scalar_tensor_tensor(out=sT[:T, :, :T], in0=v3(psc)[:T], scalar=ISQ, in1=ET[:T, :, :T],
                                                      op0=ALU.mult, op1=ALU.mult), reads=[psc, ET], writes=[sT])
        yield
        pnum = nA(); pden = nA()
        for h in range(4):
            kb.op("pe", lambda e: e.matmul(pnum[:, h * T:(h + 1) * T], vTM1[:T, h, 0:128], sT[:T, h, :T], start=True, stop=False),
                  reads=[vTM1, sT], writes=[pnum], inc=False)
            kb.op("pe", lambda e: e.matmul(pnum[:, h * T:(h + 1) * T], Cb[:, h, :], qs[:, h, :T], start=False, stop=True),
                  reads=[Cb, qs], writes=[pnum])
        for h in range(4):
            kb.op("pe", lambda e: e.matmul(pden[:, h * T:(h + 1) * T], onesb[:T, :], sT[:T, h, :T], start=True, stop=False),
                  reads=[onesb, sT], writes=[pden], inc=False)
            kb.op("pe", lambda e: e.matmul(pden[:, h * T:(h + 1) * T], nbc[:, h, :], qs[:, h, :T], start=False, stop=True),
                  reads=[nbc, qs], writes=[pden])
        yield
        kb.op("act", lambda e: e.activation(out=cden[:, :, :T], in_=v3(pden), func=AF.Abs), reads=[pden], writes=[cden])
        kb.op("dve", lambda e: e.tensor_scalar(out=cden[:, :, :T], in0=cden[:, :, :T], scalar1=1.0, scalar2=None, op0=ALU.max),
              reads=[cden], writes=[cden])
        kb.op("dve", lambda e: e.reciprocal(out=cden[:, :, :T], in_=cden[:, :, :T]), reads=[cden], writes=[cden])
        kb.op("dve", lambda e: e.tensor_tensor(out=hT[:, :, :T], in0=v3(pnum), in1=cden[:, :, :T], op=ALU.mult),
              reads=[pnum, cden], writes=[hT])
        kb.op("act", lambda e: e.activation(out=hsq[:, :, :T], in_=hT[:, :, :T], func=AF.Square), reads=[hT], writes=[hsq])
        yield
        pss = nA()
        for h in range(4):
            kb.op("pe", lambda e: e.matmul(pss[:, h * T:(h + 1) * T], onesf[:, :], hsq[:, h, :T], start=True, stop=True),
                  reads=[onesf, hsq], writes=[pss], inc=(h == 3))
        kb.op("act", lambda e: e.activation(out=rs4[:, :, :T], in_=v3(pss), func=AF.Sqrt, bias=1e-6), reads=[pss], writes=[rs4])
        kb.op("dve", lambda e: e.reciprocal(out=rs4[:, :, :T], in_=rs4[:, :, :T]), reads=[rs4], writes=[rs4])
        kb.op("dve", lambda e: e.tensor_tensor(out=hT[:, :, :T], in0=hT[:, :, :T], in1=rs4[:, :, :T], op=ALU.mult),
              reads=[hT, rs4], writes=[hT])
        kb.op("dve", lambda e: e.tensor_tensor(out=hT[:, :, :T], in0=hT[:, :, :T], in1=moT[:, :, :T], op=ALU.mult),
              reads=[hT, moT], writes=[hT])
        kb.op("dve", lambda e: e.tensor_tensor(out=mixT[:, 0:4, :T], in0=hT[:, :, :T], in1=b3(MN, T), op=ALU.mult),
              reads=[hT, MN], writes=[mixT])
        yield
        for h in range(4):
            kb.op("dve", lambda e: e.tensor_scalar(out=kw[:T, h, :], in0=kTMf[:T, h * 128:(h + 1) * 128], scalar1=ew[:T, h:h + 1],
                                                   scalar2=ISQ, op0=ALU.mult, op1=ALU.mult), reads=[kTMf, ew], writes=[kw])
        for half in range(2):
            pC = nA()
            for hh in range(2):
                h = half * 2 + hh
                kb.op("pe", lambda e: e.matmul(pC[:, hh * 129:(hh + 1) * 129], kw[:T, h, :], vTM1[:T, h, :], start=True, stop=True),
                      reads=[kw, vTM1], writes=[pC], inc=(hh == 1))
            for hh in range(2):
                h = half * 2 + hh
                kb.op("dve", lambda e: e.scalar_tensor_tensor(out=C[:, h, :], in0=C[:, h, :], scalar=eB[:, h, T - 1:T],
                                                              in1=pC[:, hh * 129:(hh + 1) * 129], op0=ALU.mult, op1=ALU.add),
                      reads=[C, eB, pC], writes=[C])
        yield
        kb.op("act", lambda e: e.copy(out=Cb[:, :, :], in_=C[:, :, 0:128]), reads=[C], writes=[Cb])
        kb.op("dve", lambda e: e.tensor_copy(out=nbc[:, :, :], in_=C[:, :, 128:129].to_broadcast([128, 4, 128])),
              reads=[C], writes=[nbc])


    def gen_prep(ti):
        r0, T = g.tiles[ti]
        HT = HTs[ti % 2]
        mixT = mixTs[ti % 2]
        def proj_fm(pbank, j, col):
            for kc in range(8):
                kb.op("pe", lambda e: e.matmul(pbank[:, j * T:(j + 1) * T], Win[:, kc, col:col + 128], hnT[:, kc, :T],
                                               start=(kc == 0), stop=(kc == 7)),
                      reads=[Win, hnT], writes=[pbank], inc=(kc == 7))

        def proj_tm(pbank, col, n, c0=0):
            for kc in range(8):
                kb.op("pe", lambda e: e.matmul(pbank[:T, c0:c0 + n], hnT[:, kc, :T], Win[:, kc, col:col + n],
                                               start=(kc == 0), stop=(kc == 7)),
                      reads=[hnT, Win], writes=[pbank], inc=(kc == 7))

        def v3(pbank, n=4):
            return pbank[:, 0:n * T].rearrange("p (c t) -> p c t", c=n)

        kb.op("dve", lambda e: e.tensor_tensor(out=D1[:, :, :T], in0=ZR[:, :, 0:T], in1=ZR[:, :, 1:T + 1], op=ALU.subtract),
              reads=[ZR], writes=[D1])
        kb.op("dve", lambda e: e.tensor_tensor(out=D1[:, :, :T], in0=D1[:, :, :T], in1=b3(MU, T, 14), op=ALU.mult),
              reads=[D1, MU], writes=[D1])
        kb.op("dve", lambda e: e.tensor_tensor(out=Z2[:, :, :T], in0=D1[:, :, :T], in1=ZR[:, :, 1:T + 1], op=ALU.add),
              reads=[D1, ZR], writes=[Z2])
        kb.op("dve", lambda e: e.tensor_copy(out=ZR[:, :, 0:1], in_=ZR[:, :, T:T + 1]), reads=[ZR], writes=[ZR])
        r_ = Z2[:, 0:4, :T]; k_ = Z2[:, 4:8, :T]; v_ = Z2[:, 8:12, :T]
        yield
        kb.op("act", lambda e: e.activation(out=LIN[0:64, :T], in_=Z2[0:64, 12, :T], func=AF.Tanh), reads=[Z2], writes=[LIN])
        kb.op("act", lambda e: e.copy(out=LIN[64:128, :T], in_=Z2[64:128, 12, :T]), reads=[Z2], writes=[LIN])
        kb.op("act", lambda e: e.activation(out=sxg[:, :T], in_=Z2[:, 13, :T], func=AF.Sigmoid), reads=[Z2], writes=[sxg])
        yield
        pw = nB(); pa = nB(); pgg = nB()
        for c in range(4):
            kb.op("pe", lambda e: e.matmul(pw[:, c * T:(c + 1) * T], W2A[0:64, c * 128:(c + 1) * 128], LIN[0:64, :T], start=True, stop=True),
                  reads=[W2A, LIN], writes=[pw], inc=(c == 3))
        for c in range(4):
            kb.op("pe", lambda e: e.matmul(pa[:, c * T:(c + 1) * T], W2A[64:128, c * 128:(c + 1) * 128], LIN[64:128, :T], start=True, stop=True),
                  reads=[W2A, LIN], writes=[pa], inc=(c == 3))
        for c in range(4):
            kb.op("pe", lambda e: e.matmul(pgg[:, c * T:(c + 1) * T], G2[:, c * 128:(c + 1) * 128], sxg[:, :T], start=True, stop=True),
                  reads=[G2, sxg], writes=[pgg], inc=(c == 3))
        for c in range(4):
            kb.op("act", lambda e: e.activation(out=sw[:, c, :T], in_=pw[:, c * T:(c + 1) * T], func=AF.Sigmoid, bias=W0[:, c:c + 1]),
                  reads=[pw, W0], writes=[sw])
            kb.op("act", lambda e: e.activation(out=aa[:, c, :T], in_=pa[:, c * T:(c + 1) * T], func=AF.Sigmoid, bias=A0[:, c:c + 1]),
                  reads=[pa, A0], writes=[aa])
        kb.op("act", lambda e: e.copy(out=GG[:, :, :T], in_=v3(pgg)), reads=[pgg], writes=[GG])
        yield
        kb.op("dve", lambda e: e.tensor_tensor(out=kkr[:, :, :T], in0=k_, in1=b3(KK, T), op=ALU.mult), reads=[Z2, KK], writes=[kkr])
        kb.op("act", lambda e: e.activation(out=tq[:, :, :T], in_=kkr[:, :, :T], func=AF.Square), reads=[kkr], writes=[tq])
        pn = nB()
        for c in range(4):
            kb.op("pe", lambda e: e.matmul(pn[:, c * T:(c + 1) * T], blk[:, :], tq[:, c, :T], start=True, stop=True),
                  reads=[blk, tq], writes=[pn], inc=(c == 3))
        kb.op("act", lambda e: e.activation(out=rn[:, :, :T], in_=v3(pn), func=AF.Sqrt), reads=[pn], writes=[rn])
        kb.op("dve", lambda e: e.tensor_scalar(out=rn[:, :, :T], in0=rn[:, :, :T], scalar1=1e-12, scalar2=None, op0=ALU.max),
              reads=[rn], writes=[rn])
        kb.op("dve", lambda e: e.reciprocal(out=rn[:, :, :T], in_=rn[:, :, :T]), reads=[rn], writes=[rn])
        kb.op("dve", lambda e: e.tensor_tensor(out=kkr[:, :, :T], in0=kkr[:, :, :T], in1=rn[:, :, :T], op=ALU.mult),
              reads=[kkr, rn], writes=[kkr])
        yield
        kb.op("dve", lambda e: e.scalar_tensor_tensor(out=tq[:, :, :T], in0=aa[:, :, :T], scalar=-1.0, in1=b3(KA, T),
                                                      op0=ALU.add, op1=ALU.mult), reads=[aa, KA], writes=[tq])
        kb.op("dve", lambda e: e.scalar_tensor_tensor(out=kp[:, :, :T], in0=tq[:, :, :T], scalar=1.0, in1=k_,
                                                      op0=ALU.add, op1=ALU.mult), reads=[tq, Z2], writes=[kp])
        yield
        for c in range(4):
            kb.op("dve", lambda e: e.tensor_tensor_scan(out=CS[:, c, :T], data0=ones1[:, :T], data1=sw[:, c, :T], initial=0.0,
                                                        op0=ALU.mult, op1=ALU.add), reads=[ones1, sw], writes=[CS])
        kb.op("dve", lambda e: e.tensor_tensor(out=CSp[:, :, :T], in0=CS[:, :, :T], in1=sw[:, :, :T], op=ALU.subtract),
              reads=[CS, sw], writes=[CSp])
        kb.op("dve", lambda e: e.tensor_scalar(out=csl[:, :], in0=CS[:, :, T - 1], scalar1=-EH, scalar2=None, op0=ALU.mult),
              reads=[CS], writes=[csl])
        kb.op("act", lambda e: e.activation(out=eW[:, :, :T], in_=CS[:, :, :T], func=AF.Exp, scale=-EH), reads=[CS], writes=[eW])
        kb.op("act", lambda e: e.activation(out=eWp[:, :, :T], in_=CSp[:, :, :T], func=AF.Exp, scale=-EH), reads=[CSp], writes=[eWp])
        kb.op("act", lambda e: e.activation(out=eWi[:, :, :T], in_=CS[:, :, :T], func=AF.Exp, scale=EH), reads=[CS], writes=[eWi])
        for c in range(4):
            kb.op("act", lambda e: e.activation(out=eWT[:, c, :T], in_=CS[:, c, :T], func=AF.Exp, scale=EH, bias=csl[:, c:c + 1]),
                  reads=[CS, csl], writes=[eWT])
        yield
        kb.op("dve", lambda e: e.scalar_tensor_tensor(out=AR[:, :, 0, :T], in0=kkr[:, :, :T], scalar=-1.0, in1=eWp[:, :, :T],
                                                      op0=ALU.mult, op1=ALU.mult), reads=[kkr, eWp], writes=[AR])
        kb.op("dve", lambda e: e.tensor_tensor(out=AR[:, :, 1, :T], in0=r_, in1=eW[:, :, :T], op=ALU.mult), reads=[Z2, eW], writes=[AR])
        kb.op("dve", lambda e: e.tensor_tensor(out=kka[:, :, :T], in0=kkr[:, :, :T], in1=aa[:, :, :T], op=ALU.mult),
              reads=[kkr, aa], writes=[kka])
        kb.op("dve", lambda e: e.tensor_tensor(out=BH[:, :, :T], in0=kka[:, :, :T], in1=eWT[:, :, :T], op=ALU.mult),
              reads=[kka, eWT], writes=[BH])
        kb.op("dve", lambda e: e.tensor_tensor(out=KH[:, :, :T], in0=kp[:, :, :T], in1=eWT[:, :, :T], op=ALU.mult),
              reads=[kp, eWT], writes=[KH])
        kb.op("act", lambda e: e.copy(out=vb[:, :, :T], in_=v_), reads=[Z2], writes=[vb])
        yield
        kb.op("dve", lambda e: e.tensor_tensor(out=tq[:, :, :T], in0=r_, in1=kp[:, :, :T], op=ALU.mult), reads=[Z2, kp], writes=[tq])
        kb.op("dve", lambda e: e.tensor_tensor(out=tq[:, :, :T], in0=tq[:, :, :T], in1=b3(RRK, T), op=ALU.mult),
              reads=[tq, RRK], writes=[tq])
        prk = nB()
        for c in range(4):
            kb.op("pe", lambda e: e.matmul(prk[:, c * T:(c + 1) * T], blk[:, :], tq[:, c, :T], start=True, stop=True),
                  reads=[blk, tq], writes=[prk], inc=(c == 3))
        kb.op("dve", lambda e: e.tensor_tensor(out=bonus[:, :, :T], in0=v3(prk), in1=v_, op=ALU.mult), reads=[prk, Z2], writes=[bonus])
        yield
        PST = g.PST
        for c in range(4):
            kb.op("pe", lambda e: e.transpose(out=PST[:T, c * 128:(c + 1) * 128], in_=vb[:, c, :T], identity=g.ident_b[:, :]),
                  reads=[vb, g.ident_b], writes=[PST], inc=False)
        for c in range(4):
            kb.op("pe", lambda e: e.transpose(out=PST[:T, (4 + c) * 128:(5 + c) * 128], in_=BH[:, c, :T], identity=g.ident_b[:, :]),
                  reads=[BH, g.ident_b], writes=[PST], inc=(c == 3))
        kb.op("act", lambda e: e.copy(out=VTM[:T, :, :], in_=PST[:T, 0:512].rearrange("p (h v) -> p h v", h=8)), reads=[PST], writes=[VTM])
        kb.op("act", lambda e: e.copy(out=BHT[:T, :, :], in_=PST[:T, 512:1024].rearrange("p (h v) -> p h v", h=8)), reads=[PST], writes=[BHT])
        for c in range(4):
            kb.op("pe", lambda e: e.transpose(out=PST[:T, c * 128:(c + 1) * 128], in_=KH[:, c, :T], identity=g.ident_b[:, :]),
                  reads=[KH, g.ident_b], writes=[PST], inc=(c == 3))
        kb.op("act", lambda e: e.copy(out=KHT[:T, :, :], in_=PST[:T, 0:512].rearrange("p (h v) -> p h v", h=8)), reads=[PST], writes=[KHT])
        yield
        for hf in range(2):
            mcol = blk[:, 127 * hf:127 * hf + 1]
            kb.op("dve", lambda e: e.scalar_tensor_tensor(out=BTm[hf][:, :, :T], in0=kka[:, :, :T], scalar=mcol, in1=eWi[:, :, :T],
                                                          op0=ALU.mult, op1=ALU.mult), reads=[kka, blk, eWi], writes=[BTm[hf]])
            kb.op("dve", lambda e: e.scalar_tensor_tensor(out=KTm[hf][:, :, :T], in0=kp[:, :, :T], scalar=mcol, in1=eWi[:, :, :T],
                                                          op0=ALU.mult, op1=ALU.mult), reads=[kp, blk, eWi], writes=[KTm[hf]])
            kb.op("dve", lambda e: e.scalar_tensor_tensor(out=ATm[hf][:, :, :T], in0=kkr[:, :, :T], scalar=nblk[:, hf:hf + 1], in1=eWp[:, :, :T],
                                                          op0=ALU.mult, op1=ALU.mult), reads=[kkr, nblk, eWp], writes=[ATm[hf]])
        X, XT = Xa[0], XTa[0]
        sub = su[:T, :T].unsqueeze(1).to_broadcast([T, 2, T])
        ueb = ue[:T, :T].unsqueeze(1).to_broadcast([T, 2, T])
        slb = sl[:T, :T].unsqueeze(1).to_broadcast([T, 4, T])
        yield
        for c in range(4):
            yield
            pNA = nB(); pKA = nB()
            for hf in range(2):
                for j in range(2):
                    kb.op("pe", lambda e: e.matmul(pNA[:T, (hf * 2 + j) * T:(hf * 2 + j + 1) * T], BTm[hf][:, c, :T], AR[:, c, j, :T], start=True, stop=True),
                          reads=[BTm[hf], AR], writes=[pNA], inc=(hf == 1 and j == 1))
            for hf in range(2):
                for j in range(2):
                    kb.op("pe", lambda e: e.matmul(pKA[:T, (hf * 2 + j) * T:(hf * 2 + j + 1) * T], KTm[hf][:, c, :T], AR[:, c, j, :T], start=True, stop=True),
                          reads=[KTm[hf], AR], writes=[pKA], inc=(hf == 1 and j == 1))
            na4 = pNA[:T, 0:4 * T].rearrange("p (h j t) -> p h j t", h=2, j=2)
            ka4 = pKA[:T, 0:4 * T].rearrange("p (h j t) -> p h j t", h=2, j=2)
            kb.op("dve", lambda e: e.tensor_tensor(out=X[:T, 2 * c:2 * c + 2, :T], in0=na4[:, :, 0, :], in1=sub, op=ALU.mult),
                  reads=[pNA, su], writes=[X])
            kb.op("dve", lambda e: e.tensor_tensor(out=ARB[:T, 2 * c:2 * c + 2, :T], in0=na4[:, :, 1, :], in1=ueb, op=ALU.mult),
                  reads=[pNA, ue], writes=[ARB])
            kb.op("dve", lambda e: e.tensor_tensor(out=AAK[:T, 2 * c:2 * c + 2, :T], in0=ka4[:, :, 0, :], in1=sub, op=ALU.mult),
                  reads=[pKA, su], writes=[AAK])
            kb.op("dve", lambda e: e.tensor_tensor(out=ARK[:T, 2 * c:2 * c + 2, :T], in0=ka4[:, :, 1, :], in1=ueb, op=ALU.mult),
                  reads=[pKA, ue], writes=[ARK])
        yield
        for half in range(2):
            pNb = nB()
            for j in range(4):
                h = half * 4 + j
                c, hf = h // 2, h % 2
                pl = slice(hf * 64, hf * 64 + 64)
                kb.op("pe", lambda e: e.matmul(pNb[:T, j * T:(j + 1) * T], ATm[hf][:, c, :T], BTm[hf][:, c, :T], start=True, stop=True),
                      reads=[ATm[hf], BTm[hf]], writes=[pNb], inc=(j == 3))
            kb.op("dve", lambda e: e.tensor_tensor(out=XT[:T, half * 4:half * 4 + 4, :T], in0=v3(pNb)[:T], in1=slb, op=ALU.mult),
                  reads=[pNb, sl], writes=[XT])
        kb.op("dve", lambda e: e.tensor_tensor(out=Pm[:T, :, :T], in0=X[:T, :, :T],
                                               in1=g.ident_f[:T, :T].unsqueeze(1).to_broadcast([T, 8, T]), op=ALU.add),
              reads=[X, g.ident_f], writes=[Pm])

    def gen_neumann(ti):
        r0, T = g.tiles[ti]
        HT = HTs[ti % 2]
        mixT = mixTs[ti % 2]
        def proj_fm(pbank, j, col):
            for kc in range(8):
                kb.op("pe", lambda e: e.matmul(pbank[:, j * T:(j + 1) * T], Win[:, kc, col:col + 128], hnT[:, kc, :T],
                                               start=(kc == 0), stop=(kc == 7)),
                      reads=[Win, hnT], writes=[pbank], inc=(kc == 7))

        def proj_tm(pbank, col, n, c0=0):
            for kc in range(8):
                kb.op("pe", lambda e: e.matmul(pbank[:T, c0:c0 + n], hnT[:, kc, :T], Win[:, kc, col:col + n],
                                               start=(kc == 0), stop=(kc == 7)),
                      reads=[hnT, Win], writes=[pbank], inc=(kc == 7))

        def v3(pbank, n=4):
            return pbank[:, 0:n * T].rearrange("p (c t) -> p c t", c=n)

        yield
        lv = 1
        cur = 0
        while lv * 2 < T:
            X, XT = Xa[cur], XTa[cur]
            Xn, XTn = Xa[1 - cur], XTa[1 - cur]
            for half in range(2):
                yield
                p1 = nB(); p2 = nB()
                for j in range(4):
                    h = half * 4 + j
                    kb.op("pe", lambda e: e.matmul(p1[:T, j * T:(j + 1) * T], XT[:T, h, :T], X[:T, h, :T], start=True, stop=True),
                          reads=[XT, X], writes=[p1], inc=(j == 3))
                for j in range(4):
                    h = half * 4 + j
                    kb.op("pe", lambda e: e.matmul(p2[:T, j * T:(j + 1) * T], X[:T, h, :T], XT[:T, h, :T], start=True, stop=True),
                          reads=[XT, X], writes=[p2], inc=(j == 3))
                kb.op("act", lambda e: e.copy(out=Xn[:T, half * 4:half * 4 + 4, :T], in_=v3(p1)[:T]), reads=[p1], writes=[Xn])
                kb.op("dve", lambda e: e.tensor_copy(out=XTn[:T, half * 4:half * 4 + 4, :T], in_=v3(p2)[:T]), reads=[p2], writes=[XTn])
            yield
            for half in range(2):
                p3 = nB()
                for j in range(4):
                    h = half * 4 + j
                    kb.op("pe", lambda e: e.matmul(p3[:T, j * T:(j + 1) * T], XTn[:T, h, :T], Pm[:T, h, :T], start=True, stop=True),
                          reads=[XTn, Pm], writes=[p3], inc=(j == 3))
                kb.op("dve", lambda e: e.tensor_tensor(out=Pm[:T, half * 4:half * 4 + 4, :T], in0=Pm[:T, half * 4:half * 4 + 4, :T],
                                                       in1=v3(p3)[:T], op=ALU.add), reads=[Pm, p3], writes=[Pm])
            cur = 1 - cur
            lv *= 2

    def tail(ti):
        r0, T = g.tiles[ti]
        HT = HTs[ti % 2]
        mixT = mixTs[ti % 2]
        def proj_fm(pbank, j, col):
            for kc in range(8):
                kb.op("pe", lambda e: e.matmul(pbank[:, j * T:(j + 1) * T], Win[:, kc, col:col + 128], hnT[:, kc, :T],
                                               start=(kc == 0), stop=(kc == 7)),
                      reads=[Win, hnT], writes=[pbank], inc=(kc == 7))

        def proj_tm(pbank, col, n, c0=0):
            for kc in range(8):
                kb.op("pe", lambda e: e.matmul(pbank[:T, c0:c0 + n], hnT[:, kc, :T], Win[:, kc, col:col + n],
                                               start=(kc == 0), stop=(kc == 7)),
                      reads=[hnT, Win], writes=[pbank], inc=(kc == 7))

        def v3(pbank, n=4):
            return pbank[:, 0:n * T].rearrange("p (c t) -> p c t", c=n)

        cur = ((T.bit_length() - 2) % 2) if T > 2 else 0
        pP1 = nB()
        for h in range(8):
            c, hf = h // 2, h % 2
            pl = slice(hf * 64, hf * 64 + 64)
            kb.op("pe", lambda e: e.matmul(pP1[:T, h * 64:(h + 1) * 64], ATm[hf][:, c, :T], STb[:, c, :], start=True, stop=False),
                  reads=[ATm[hf], STb], writes=[pP1], inc=False)
            kb.op("pe", lambda e: e.matmul(pP1[:T, h * 64:(h + 1) * 64], AAK[:T, h, :T], VTM[:T, h, :], start=False, stop=True),
                  reads=[AAK, VTM], writes=[pP1], inc=(h == 7))
        kb.op("act", lambda e: e.copy(out=P1[:T, :], in_=pP1[:T, :]), reads=[pP1], writes=[P1])
        pU = nB()
        for h in range(8):
            kb.op("pe", lambda e: e.matmul(pU[:T, h * 64:(h + 1) * 64], Pm[:T, h, :T], P1[:T, h * 64:(h + 1) * 64], start=True, stop=True),
                  reads=[Pm, P1], writes=[pU], inc=(h == 7))
        kb.op("act", lambda e: e.copy(out=UTM[:T, :, :], in_=pU[:T, :].rearrange("p (h v) -> p h v", h=8)), reads=[pU], writes=[UTM])
        pO = nB()
        for c in range(4):
            kb.op("pe", lambda e: e.matmul(pO[:, c * T:(c + 1) * T], STbd[:, c, :], AR[:, c, 1, :T], start=True, stop=False),
                  reads=[STbd, AR], writes=[pO], inc=False)
            for hf in range(2):
                h = 2 * c + hf
                pl = slice(hf * 64, hf * 64 + 64)
                kb.op("pe", lambda e: e.matmul(pO[pl, c * T:(c + 1) * T], UTM[:T, h, :], ARB[:T, h, :T], start=False, stop=False),
                      reads=[UTM, ARB], writes=[pO], inc=False)
                kb.op("pe", lambda e: e.matmul(pO[pl, c * T:(c + 1) * T], VTM[:T, h, :], ARK[:T, h, :T], start=False, stop=True),
                      reads=[VTM, ARK], writes=[pO], inc=(hf == 1))
        kb.op("act", lambda e: e.copy(out=Of[:, :, :T], in_=v3(pO)), reads=[pO], writes=[Of])
        pS = nB()
        for h in range(8):
            c, hf = h // 2, h % 2
            pl = slice(hf * 64, hf * 64 + 64)
            kb.op("pe", lambda e: e.matmul(pS[pl, c * 64:(c + 1) * 64], BHT[:T, h, :], UTM[:T, h, :], start=True, stop=False),
                  reads=[BHT, UTM], writes=[pS], inc=False)
            kb.op("pe", lambda e: e.matmul(pS[pl, c * 64:(c + 1) * 64], KHT[:T, h, :], VTM[:T, h, :], start=False, stop=True),
                  reads=[KHT, VTM], writes=[pS], inc=(h == 7))
        for c in range(4):
            kb.op("dve", lambda e: e.scalar_tensor_tensor(out=ST[:, c, :], in0=ST[:, c, :], scalar=eW[:, c, T - 1:T],
                                                          in1=pS[:, c * 64:(c + 1) * 64], op0=ALU.mult, op1=ALU.add),
                  reads=[ST, eW, pS], writes=[ST])
        kb.op("act", lambda e: e.copy(out=STb[:, :, :], in_=ST[:, :, :]), reads=[ST], writes=[STb])
        kb.op("act", lambda e: e.copy(out=STbd[0:64, :, 0:64], in_=ST[0:64, :, :]), reads=[ST], writes=[STbd])
        kb.op("act", lambda e: e.copy(out=STbd[64:128, :, 64:128], in_=ST[64:128, :, :]), reads=[ST], writes=[STbd])
        kb.op("act", lambda e: e.activation(out=Osq[:, :, :T], in_=Of[:, :, :T], func=AF.Square), reads=[Of], writes=[Osq])
        pm_ = nB(); pq_ = nB()
        for c in range(4):
            kb.op("pe", lambda e: e.matmul(pm_[:, c * T:(c + 1) * T], blk64[:, :], Of[:, c, :T], start=True, stop=True),
                  reads=[blk64, Of], writes=[pm_], inc=(c == 3))
        for c in range(4):
            kb.op("pe", lambda e: e.matmul(pq_[:, c * T:(c + 1) * T], blk64[:, :], Osq[:, c, :T], start=True, stop=True),
                  reads=[blk64, Osq], writes=[pq_], inc=(c == 3))
        kb.op("act", lambda e: e.copy(out=mean_s[:, :, :T], in_=v3(pm_)), reads=[pm_], writes=[mean_s])
        kb.op("dve", lambda e: e.scalar_tensor_tensor(out=var[:, :, :T], in0=mean_s[:, :, :T], scalar=-1.0, in1=mean_s[:, :, :T],
                                                      op0=ALU.mult, op1=ALU.mult), reads=[mean_s], writes=[var])
        kb.op("dve", lambda e: e.tensor_tensor(out=var[:, :, :T], in0=var[:, :, :T], in1=v3(pq_), op=ALU.add),
              reads=[var, pq_], writes=[var])
        kb.op("dve", lambda e: e.tensor_scalar(out=var[:, :, :T], in0=var[:, :, :T], scalar1=0.0, scalar2=None, op0=ALU.max),
              reads=[var], writes=[var])
        kb.op("act", lambda e: e.activation(out=var[:, :, :T], in_=var[:, :, :T], func=AF.Sqrt, bias=64e-5), reads=[var], writes=[var])
        kb.op("dve", lambda e: e.reciprocal(out=var[:, :, :T], in_=var[:, :, :T]), reads=[var], writes=[var])
        kb.op("dve", lambda e: e.tensor_tensor(out=Of[:, :, :T], in0=Of[:, :, :T], in1=mean_s[:, :, :T], op=ALU.subtract),
              reads=[Of, mean_s], writes=[Of])
        kb.op("dve", lambda e: e.tensor_tensor(out=Of[:, :, :T], in0=Of[:, :, :T], in1=var[:, :, :T], op=ALU.mult),
              reads=[Of, var], writes=[Of])
        kb.op("dve", lambda e: e.tensor_tensor(out=Of[:, :, :T], in0=Of[:, :, :T], in1=b3(LNW, T), op=ALU.mult),
              reads=[Of, LNW], writes=[Of])
        kb.op("dve", lambda e: e.tensor_tensor(out=Of[:, :, :T], in0=Of[:, :, :T], in1=b3(LNB, T), op=ALU.add),
              reads=[Of, LNB], writes=[Of])
        kb.op("dve", lambda e: e.tensor_tensor(out=Of[:, :, :T], in0=Of[:, :, :T], in1=bonus[:, :, :T], op=ALU.add),
              reads=[Of, bonus], writes=[Of])
        kb.op("dve", lambda e: e.tensor_tensor(out=mixT[:, 4:8, :T], in0=Of[:, :, :T], in1=GG[:, :, :T], op=ALU.mult),
              reads=[Of, GG], writes=[mixT])
        for nb in range(2):
            pp = nB()
            for c in range(8):
                kb.op("pe", lambda e: e.matmul(pp[:T, :], mixT[:, c, :T], Wout[:, c, nb * 512:(nb + 1) * 512],
                                               start=(c == 0), stop=(c == 7)), reads=[mixT, Wout], writes=[pp], inc=(c == 7))
            kb.op("dve", lambda e: e.tensor_tensor(out=HT[:T, nb * 512:(nb + 1) * 512], in0=HT[:T, nb * 512:(nb + 1) * 512],
                                                   in1=pp[:T, :], op=ALU.add), reads=[HT, pp], writes=[HT])
        store_h(g, dst, ti, HT, final, None, (ss, rstd, junk))


    def run(gen):
        for _ in gen:
            pass

    def interleave(a, b):
        da = db = False
        while not (da and db):
            if not da:
                try:
                    next(a)
                except StopIteration:
                    da = True
            if not db:
                try:
                    next(b)
                except StopIteration:
                    db = True

    ntl = len(g.tiles)
    load_h(g, src, 0, HTs[0])
    rmsnorm_T(g, HTs[0], g.tiles[0][1], Gb, hn, hnT, ss, rstd, junk)
    if ntl > 1:
        load_h(g, src, 1, HTs[1])
    run(gen_proj(0))
    run(gen_mlstm(0))
    for ti in range(ntl):
        if ti + 1 < ntl:
            rmsnorm_T(g, HTs[(ti + 1) % 2], g.tiles[ti + 1][1], Gb, hn, hnT, ss, rstd, junk)
            interleave(gen_prep(ti), gen_proj(ti + 1))
            interleave(gen_neumann(ti), gen_mlstm(ti + 1))
        else:
            run(gen_prep(ti))
            run(gen_neumann(ti))
        tail(ti)
        if ti + 2 < ntl:
            load_h(g, src, ti + 2, HTs[ti % 2])


LG = [float(np.log(1.0 - 2.0 ** (-5.0 - h))) for h in range(4)]
TWO_PI = 6.283185307179586
CW1 = 6.28125
CW2 = TWO_PI - CW1


LG = [float(np.log(1.0 - 2.0 ** (-5.0 - h))) for h in range(4)]
TWO_PI = 6.283185307179586
CW1 = 6.28125
CW2 = TWO_PI - CW1


LG = [float(np.log(1.0 - 2.0 ** (-5.0 - h))) for h in range(4)]
TWO_PI = 6.283185307179586
CW1 = 6.28125
CW2 = TWO_PI - CW1


def phase_l1(g, src, dst, final):
    kb, nc, dr = g.kb, g.nc, g.dr
    Win = kb.sb([128, 8, 6144], BF16, "Win")
    Wout = kb.sb([128, 16, D], BF16, "Wout")
    with contextlib.ExitStack() as ses:
        old = kb.es
        kb.es = ses
        stg = [kb.sb([128, 1536], F32, f"stg{i}") for i in range(3)]
        load_weight_bf16(g, dr["o_w_in_p"], 0, D, 6144, Win, stg)
        load_weight_bf16(g, dr["o_w_out"], 0, 2048, D, Wout, stg)
        kb.barrier()
        kb.es = old
    Gb = kb.sb([128, D], BF16, "Gb")
    Gfin = None
    iota = kb.sb([128, 128], F32, "iota")
    pidx = kb.sb([128, 1], F32, "pidx")
    ue = kb.sb([128, 128], F32, "ue")
    inv = kb.sb([128, 1], F32, "inv")
    kb.dma(iota[:, :], dr["c_iota"].ap()[:, :], writes=[iota], sem_buf=iota)
    kb.dma(pidx[:, :], dr["c_pidx"].ap()[:, :], writes=[pidx], sem_buf=pidx)
    kb.dma(ue[:, :], dr["c_ue"].ap()[:, :], writes=[ue], sem_buf=ue)
    kb.dma(inv[:, :], dr["c_inv"].ap()[:, :], writes=[inv], sem_buf=inv)
    DM = kb.sb([128, 4, 128], F32, "DM")
    DEC = kb.sb([128, 4, 128], F32, "DEC")
    KDEC = {128: kb.sb([128, 4], F32, "KDEC128"), 16: kb.sb([128, 4], F32, "KDEC16")}
    tms = kb.sb([128, 128], F32, "tms")
    kb.op("dve", lambda e: e.tensor_scalar(out=tms[:, :], in0=iota[:, :], scalar1=pidx[:, 0:1], scalar2=0.0,
                                           op0=ALU.subtract, op1=ALU.max), reads=[iota, pidx], writes=[tms])
    for h in range(4):
        kb.op("act", lambda e: e.activation(out=DM[:, h, :], in_=tms[:, :], func=AF.Exp, scale=LG[h]),
              reads=[tms], writes=[DM])
        kb.op("dve", lambda e: e.scalar_tensor_tensor(out=DM[:, h, :], in0=DM[:, h, :], scalar=1.0 / 16.0,
                                                      in1=ue[:, :], op0=ALU.mult, op1=ALU.mult),
              reads=[DM, ue], writes=[DM])
        kb.op("act", lambda e: e.activation(out=DEC[:, h, :], in_=iota[:, :], func=AF.Exp, scale=LG[h], bias=LG[h]),
              reads=[iota], writes=[DEC])
        for TT in (128, 16):
            kd = KDEC[TT]
            kb.op("act", lambda e: e.activation(out=kd[:, h:h + 1], in_=pidx[:, 0:1], func=AF.Exp, scale=-LG[h],
                                                bias=LG[h] * (TT - 1)), reads=[pidx], writes=[kd])
            kb.op("dve", lambda e: e.tensor_scalar(out=kd[:, h:h + 1], in0=kd[:, h:h + 1], scalar1=1.0 / 16.0,
                                                   scalar2=None, op0=ALU.mult), reads=[kd], writes=[kd])
    Sr = kb.sb([128, 8, 512], F32, "Sr")
    Srb = kb.sb([128, 8, 512], BF16, "Srb")
    kb.op("dve", lambda e: e.memset(Sr[:, :, :], 0.0), writes=[Sr])
    kb.op("pool", lambda e: e.memset(Srb[:, :, :], 0.0), writes=[Srb])
    HTs = [kb.sb([128, D], F32, f"HT{i}") for i in range(2)]
    hn = kb.sb([128, D], BF16, "hn")
    hnT = kb.sb([128, 8, 128], BF16, "hnT")
    sss = [kb.sb([128, 1], F32, f"ss{i}") for i in range(2)]
    rstds = [kb.sb([128, 1], F32, f"rstd{i}") for i in range(2)]
    ang = kb.sb([128, 128], F32, "ang")
    ang2 = kb.sb([128, 128], F32, "ang2")
    kf = kb.sb([128, 128], F32, "kf")
    ki = kb.sb([128, 128], I32, "ki")
    nsins = [kb.sb([128, 128], F32, f"nsin{i}") for i in range(2)]
    ncoss = [kb.sb([128, 128], F32, f"ncos{i}") for i in range(2)]
    t1 = kb.sb([128, 4, 128], F32, "t1")
    t2 = kb.sb([128, 4, 128], F32, "t2")
    qb = kb.sb([128, 2, 4, 128], BF16, "qb")
    qdb = kb.sb([128, 2, 4, 128], BF16, "qdb")
    kbf = kb.sb([128, 2, 4, 128], BF16, "kbf")
    kdT = kb.sb([128, 8, 128], BF16, "kdT")
    sTm = kb.sb([128, 4, 128], BF16, "sTm")
    VT = kb.sb([128, 2048], BF16, "VT")
    GS = kb.sb([128, 2048], BF16, "GS")
    og = kb.sb([128, 2048], BF16, "og")
    ogT = kb.sb([128, 16, 128], BF16, "ogT")
    st6 = kb.sb([128, 6], F32, "st6")
    mv = kb.sb([128, 2], F32, "mv")
    rs = kb.sb([128, 1], F32, "rs")
    junk = og
    kb.dma(t1[:, :, :].rearrange("p a b -> p (a b)"), bc_rows(dr["norm_mix"], 1, 512), writes=[t1], sem_buf=t1)
    kb.op("dve", lambda e: e.tensor_copy(out=Gb[:, 0:512], in_=t1[:, :, :].rearrange("p a b -> p (a b)")), reads=[t1], writes=[Gb])
    kb.dma(t2[:, :, :].rearrange("p a b -> p (a b)"), bc_rows(dr["norm_mix"], 1, 512, col0=512), writes=[t2], sem_buf=t2)
    kb.op("dve", lambda e: e.tensor_copy(out=Gb[:, 512:1024], in_=t2[:, :, :].rearrange("p a b -> p (a b)")), reads=[t2], writes=[Gb])

    def sincos(dst_tbl, shift, pos0, T):
        kb.op("dve", lambda e: e.tensor_scalar(out=ang[:, :T], in0=iota[:, :T], scalar1=float(pos0), scalar2=inv[:, 0:1],
                                               op0=ALU.add, op1=ALU.mult), reads=[iota, inv], writes=[ang])
        if shift != 0.0:
            kb.op("dve", lambda e: e.tensor_scalar(out=ang[:, :T], in0=ang[:, :T], scalar1=shift, scalar2=None,
                                                   op0=ALU.add), reads=[ang], writes=[ang])
        kb.op("dve", lambda e: e.tensor_scalar(out=ki[:, :T], in0=ang[:, :T], scalar1=1.0 / TWO_PI, scalar2=None,
                                               op0=ALU.mult), reads=[ang], writes=[ki])
        kb.op("dve", lambda e: e.tensor_copy(out=kf[:, :T], in_=ki[:, :T]), reads=[ki], writes=[kf])
        kb.op("dve", lambda e: e.scalar_tensor_tensor(out=ang2[:, :T], in0=kf[:, :T], scalar=-CW1, in1=ang[:, :T],
                                                      op0=ALU.mult, op1=ALU.add), reads=[kf, ang], writes=[ang2])
        kb.op("dve", lambda e: e.scalar_tensor_tensor(out=ang2[:, :T], in0=kf[:, :T], scalar=-CW2, in1=ang2[:, :T],
                                                      op0=ALU.mult, op1=ALU.add), reads=[kf, ang2], writes=[ang2])
        kb.op("dve", lambda e: e.tensor_scalar(out=ang2[:, :T], in0=ang2[:, :T], scalar1=3.1415925, scalar2=-3.1415925,
                                               op0=ALU.min, op1=ALU.max), reads=[ang2], writes=[ang2])
        kb.op("act", lambda e: e.activation(out=dst_tbl[:, :T], in_=ang2[:, :T], func=AF.Sin),
              reads=[ang2], writes=[dst_tbl])

    ntl = len(g.tiles)
    load_h(g, src, 0, HTs[0])
    T0 = g.tiles[0][1]
    norm_stats(g, HTs[0], T0, Gb, hn, sss[0], rstds[0], junk)
    if ntl > 1:
        load_h(g, src, 1, HTs[1])
    norm_transpose(g, hn, hnT, T0)
    sincos(nsins[0], 0.0, g.tiles[0][0], T0)
    sincos(ncoss[0], np.pi / 2, g.tiles[0][0], T0)
    PST = g.PST
    for ti, (r0, T) in enumerate(g.tiles):
        HT = HTs[ti % 2]
        HO = HT
        nsin = nsins[ti % 2]
        ncos = ncoss[ti % 2]
        sb_ = nsin[:, :T].unsqueeze(1).to_broadcast([128, 4, T])
        cb_ = ncos[:, :T].unsqueeze(1).to_broadcast([128, 4, T])
        qk_banks = []
        for which in range(2):
            pe_ = next_ps(g)
            po_ = next_ps(g)
            qk_banks.append((pe_, po_))
            for eo, pb in ((0, pe_), (1, po_)):
                for h in range(4):
                    col = which * 1024 + h * 256 + eo * 128
                    for kc in range(8):
                        kb.op("pe", lambda e: e.matmul(pb[:, h * T:(h + 1) * T], Win[:, kc, col:col + 128],
                                                       hnT[:, kc, :T], start=(kc == 0), stop=(kc == 7)),
                              reads=[Win, hnT], writes=[pb], inc=(kc == 7))
        for which in range(2):
            pe_, po_ = qk_banks[which]
            pe3 = pe_[:, 0:4 * T].rearrange("p (h t) -> p h t", h=4)
            po3 = po_[:, 0:4 * T].rearrange("p (h t) -> p h t", h=4)
            dstb = qb if which == 0 else kbf
            kb.op("dve", lambda e: e.tensor_tensor(out=t1[:, :, :T], in0=pe3, in1=cb_, op=ALU.mult),
                  reads=[pe_, ncos], writes=[t1])
            kb.op("dve", lambda e: e.tensor_tensor(out=t2[:, :, :T], in0=po3, in1=sb_, op=ALU.mult),
                  reads=[po_, nsin], writes=[t2])
            kb.op("dve", lambda e: e.tensor_tensor(out=dstb[:, 0, :, :T], in0=t1[:, :, :T], in1=t2[:, :, :T],
                                                   op=ALU.subtract), reads=[t1, t2], writes=[dstb])
            kb.op("dve", lambda e: e.tensor_tensor(out=t1[:, :, :T], in0=po3, in1=cb_, op=ALU.mult),
                  reads=[po_, ncos], writes=[t1])
            kb.op("dve", lambda e: e.tensor_tensor(out=t2[:, :, :T], in0=pe3, in1=sb_, op=ALU.mult),
                  reads=[pe_, nsin], writes=[t2])
            kb.op("dve", lambda e: e.tensor_tensor(out=dstb[:, 1, :, :T], in0=t1[:, :, :T], in1=t2[:, :, :T],
                                                   op=ALU.add), reads=[t1, t2], writes=[dstb])
            if which == 0:
                for eo in range(2):
                    kb.op("pool", lambda e: e.tensor_tensor(out=qdb[:, eo, :, :T], in0=qb[:, eo, :, :T],
                                                            in1=DEC[:, :, :T], op=ALU.mult),
                          reads=[qb, DEC], writes=[qdb])
        for nb in range(4):
            pvv = next_ps(g)
            for kc in range(8):
                kb.op("pe", lambda e: e.matmul(pvv[:T, :], hnT[:, kc, :T], Win[:, kc, 2048 + nb * 512:2048 + (nb + 1) * 512],
                                               start=(kc == 0), stop=(kc == 7)), reads=[hnT, Win], writes=[pvv], inc=(kc == 7))
            kb.op("act", lambda e: e.copy(out=VT[:T, nb * 512:(nb + 1) * 512], in_=pvv[:T, :]), reads=[pvv], writes=[VT])
        for nb in range(4):
            pgg = next_ps(g)
            for kc in range(8):
                kb.op("pe", lambda e: e.matmul(pgg[:T, :], hnT[:, kc, :T], Win[:, kc, 4096 + nb * 512:4096 + (nb + 1) * 512],
                                               start=(kc == 0), stop=(kc == 7)), reads=[hnT, Win], writes=[pgg], inc=(kc == 7))
            kb.op("act", lambda e: e.activation(out=GS[:T, nb * 512:(nb + 1) * 512], in_=pgg[:T, :], func=AF.Silu),
                  reads=[pgg], writes=[GS])
        if ti + 1 < ntl:
            r0n, Tn = g.tiles[ti + 1]
            sincos(nsins[(ti + 1) % 2], 0.0, r0n, Tn)
            sincos(ncoss[(ti + 1) % 2], np.pi / 2, r0n, Tn)
        for h in range(4):
            for eo in range(2):
                j = h * 2 + eo
                kb.op("pe", lambda e: e.transpose(out=PST[:T, j * 128:(j + 1) * 128], in_=kbf[:, eo, h, :T],
                                                  identity=g.ident_b[:, :]),
                      reads=[kbf, g.ident_b], writes=[PST], inc=(j == 7))
        for h in range(4):
            kb.op("act", lambda e: e.activation(out=kdT[:T, 2 * h:2 * h + 2, :],
                                                in_=PST[:T, 2 * h * 128:(2 * h + 2) * 128].rearrange("p (j d) -> p j d", j=2),
                                                func=AF.Copy, scale=KDEC[T][:T, h:h + 1]),
                  reads=[PST, KDEC[T]], writes=[kdT])
        psc = next_ps(g)
        for h in range(4):
            for eo in range(2):
                kb.op("pe", lambda e: e.matmul(psc[:T, h * T:(h + 1) * T], kbf[:, eo, h, :T], qb[:, eo, h, :T],
                                               start=(eo == 0), stop=(eo == 1)),
                      reads=[kbf, qb], writes=[psc], inc=(eo == 1))
        kb.op("dve", lambda e: e.tensor_tensor(out=sTm[:T, :, :T],
                                               in0=psc[:T, 0:4 * T].rearrange("p (h t) -> p h t", h=4),
                                               in1=DM[:T, :, :T], op=ALU.mult), reads=[psc, DM], writes=[sTm])
        for h in range(4):
            po = next_ps(g)
            kb.op("pe", lambda e: e.matmul(po[:T, :], sTm[:T, h, :T], VT[:T, h * 512:(h + 1) * 512], start=True, stop=False),
                  reads=[sTm, VT], writes=[po], inc=False)
            for eo in range(2):
                kb.op("pe", lambda e: e.matmul(po[:T, :], qdb[:, eo, h, :T], Srb[:, 2 * h + eo, :], start=False, stop=(eo == 1)),
                      reads=[qdb, Srb], writes=[po], inc=(eo == 1))
            kb.op("dve", lambda e: e.bn_stats(out=st6[:T, :], in_=po[:T, :]), reads=[po], writes=[st6])
            kb.op("dve", lambda e: e.bn_aggr(out=mv[:T, :], in_=st6[:T, :]), reads=[st6], writes=[mv])
            kb.op("act", lambda e: e.activation(out=rs[:T, :], in_=mv[:T, 1:2], func=AF.Sqrt, scale=1.0, bias=1e-6),
                  reads=[mv], writes=[rs])
            kb.op("dve", lambda e: e.reciprocal(out=rs[:T, :], in_=rs[:T, :]), reads=[rs], writes=[rs])
            kb.op("dve", lambda e: e.tensor_scalar(out=og[:T, h * 512:(h + 1) * 512], in0=po[:T, :], scalar1=mv[:T, 0:1], scalar2=rs[:T, 0:1],
                                                   op0=ALU.subtract, op1=ALU.mult), reads=[po, mv, rs], writes=[og])
            kb.op("pool", lambda e: e.tensor_tensor(out=og[:T, h * 512:(h + 1) * 512], in0=og[:T, h * 512:(h + 1) * 512],
                                                    in1=GS[:T, h * 512:(h + 1) * 512], op=ALU.mult),
                  reads=[og, GS], writes=[og])
        gT = [float(np.exp(LG[h] * T)) for h in range(4)]
        for h in range(4):
            for eo in range(2):
                j = 2 * h + eo
                pst_ = next_ps(g)
                kb.op("pe", lambda e: e.matmul(pst_[:, :], kdT[:T, j, :], VT[:T, h * 512:(h + 1) * 512], start=True, stop=True),
                      reads=[kdT, VT], writes=[pst_])
                kb.op("dve", lambda e: e.scalar_tensor_tensor(out=Sr[:, j, :], in0=Sr[:, j, :], scalar=gT[h], in1=pst_[:, :],
                                                              op0=ALU.mult, op1=ALU.add), reads=[Sr, pst_], writes=[Sr])
                kb.op("act", lambda e: e.copy(out=Srb[:, j, :], in_=Sr[:, j, :]), reads=[Sr], writes=[Srb])
        for half in range(2):
            for j in range(8):
                c = half * 8 + j
                kb.op("pe", lambda e: e.transpose(out=PST[:, j * T:(j + 1) * T], in_=og[:T, c * 128:(c + 1) * 128],
                                                  identity=g.ident_b[:T, :T]),
                      reads=[og, g.ident_b], writes=[PST], inc=(j == 7))
            kb.op("act", lambda e: e.copy(out=ogT[:, half * 8:(half + 1) * 8, :T],
                                          in_=PST[:, 0:8 * T].rearrange("p (k t) -> p k t", k=8)),
                  reads=[PST], writes=[ogT])
        if ti + 1 < ntl:
            Tn = g.tiles[ti + 1][1]
            norm_stats(g, HTs[(ti + 1) % 2], Tn, Gb, hn, sss[(ti + 1) % 2], rstds[(ti + 1) % 2], junk)
        pps = []
        for nb in range(2):
            pp = next_ps(g)
            pps.append(pp)
            for c in range(16):
                kb.op("pe", lambda e: e.matmul(pp[:T, :], ogT[:, c, :T], Wout[:, c, nb * 512:(nb + 1) * 512],
                                               start=(c == 0), stop=(c == 15)), reads=[ogT, Wout], writes=[pp], inc=(c == 15))
        if ti + 1 < ntl:
            norm_transpose(g, hn, hnT, g.tiles[ti + 1][1])
        for nb in range(2):
            kb.op("dve", lambda e: e.tensor_tensor(out=HO[:T, nb * 512:(nb + 1) * 512],
                                                   in0=HT[:T, nb * 512:(nb + 1) * 512], in1=pps[nb][:T, :], op=ALU.add),
                  reads=[HT, pps[nb]], writes=[HO])
        store_h(g, dst, ti, HO, final, Gfin, (sss[ti % 2], rstds[ti % 2], junk))
        if ti + 2 < ntl:
            load_h(g, src, ti + 2, HTs[ti % 2])


def make_in_map(inputs, b, NT):
    m = {"x": np.ascontiguousarray(inputs["x"][b, :128 * NT])}
    for k, shp in W_SPECS.items():
        src_k = "o_w_in" if k == "o_w_in_p" else k
        m[k] = np.ascontiguousarray(np.asarray(inputs[src_k], np.float32).reshape(shp))
    m.update(host_consts())
    perm = np.arange(6144)
    for sec in range(2):
        for h in range(4):
            base = sec * 1024 + h * 256
            perm[base:base + 256] = np.concatenate([base + np.arange(0, 256, 2), base + np.arange(1, 256, 2)])
    m["o_w_in_p"] = np.ascontiguousarray(m["o_w_in_p"][:, perm])
    return m


NT_FULL = 32


def kernel(**inputs):
    nc = build(NT_FULL, phases=(1, 2, 3, 4), debug=False, final=True)
    in_maps = [make_in_map(inputs, b, NT_FULL) for b in range(8)]
    res = run_bass_kernel_spmd(nc, in_maps, core_ids=list(range(8)))
    return np.stack([np.asarray(r["out"], np.float32) for r in res.results], axis=0)
```

```python
import contextlib
import numpy as np
import concourse.bass as bass
import concourse.mybir as mybir

F32 = mybir.dt.float32
BF16 = mybir.dt.bfloat16
I32 = mybir.dt.int32
AF = mybir.ActivationFunctionType
ALU = mybir.AluOpType
AX = mybir.AxisListType


class Buf:
    __slots__ = ("t", "w", "r", "dsem", "dcount", "name", "excl")

    def __init__(self, t, name=""):
        self.t = t
        self.w = {}
        self.r = {}
        self.dsem = None
        self.dcount = 0
        self.name = name
        self.excl = False

    def __getitem__(self, idx):
        return self.t[idx]


class Eng:
    def __init__(self, name, obj, sem):
        self.name = name
        self.obj = obj
        self.sem = sem
        self.count = 0
        self.seen = {}


class KB:
    def __init__(self, nc, es):
        self.nc = nc
        self.es = es
        self.sems = {}
        self.E = {}
        for name, obj in (("pe", nc.tensor), ("act", nc.scalar), ("dve", nc.vector),
                          ("pool", nc.gpsimd), ("sp", nc.sync)):
            sem = es.enter_context(nc.semaphore("s_" + name))
            self.E[name] = Eng(name, obj, sem)
            self.sems[id(sem)] = sem
        self.dma_tokens = {}
        self.nbuf = 0

    def sb(self, shape, dt, name=None):
        self.nbuf += 1
        name = f"{name or 'b'}_{self.nbuf}"
        t = self.es.enter_context(self.nc.sbuf_tensor(name, list(shape), dt))
        return Buf(t, name)

    def ps(self, shape, dt, name=None):
        self.nbuf += 1
        name = f"{name or 'p'}_{self.nbuf}"
        t = self.es.enter_context(self.nc.psum_tensor(name, list(shape), dt))
        b = Buf(t, name)
        b.excl = True
        return b

    def newsem(self, name):
        sem = self.es.enter_context(self.nc.semaphore(name))
        self.sems[id(sem)] = sem
        return sem

    def _wait(self, e, deps):
        for sid, val in deps.items():
            if e.seen.get(sid, 0) < val:
                e.obj.wait_ge(self.sems[sid], val)
                e.seen[sid] = val

    def _deps(self, e, reads, writes):
        deps = {}
        own = id(e.sem)
        for b in reads:
            for sid, v in b.w.items():
                if deps.get(sid, 0) < v:
                    deps[sid] = v
            if b.excl:
                for sid, v in b.r.items():
                    if sid != own and deps.get(sid, 0) < v:
                        deps[sid] = v
        for b in writes:
            for d in (b.w, b.r):
                for sid, v in d.items():
                    if sid == own:
                        continue
                    if deps.get(sid, 0) < v:
                        deps[sid] = v
        return deps

    def op(self, eng, fn, reads=(), writes=(), inc=True):
        e = self.E[eng]
        self._wait(e, self._deps(e, reads, writes))
        ins = fn(e.obj)
        if inc:
            e.count += 1
            ins.then_inc(e.sem, 1)
            val = e.count
        else:
            val = e.count + 1
        sid = id(e.sem)
        for b in reads:
            if b.r.get(sid, 0) < val:
                b.r[sid] = val
        for b in writes:
            if b.w.get(sid, 0) < val:
                b.w[sid] = val
        return ins

    def dma(self, out_ap, in_ap, reads=(), writes=(), sem_buf=None, q="sp"):
        e = self.E[q]
        self._wait(e, self._deps(e, reads, writes))
        b = sem_buf
        if b.dsem is None:
            b.dsem = self.newsem("d_" + b.name)
        b.dcount += 16
        e.obj.dma_start(out=out_ap, in_=in_ap).then_inc(b.dsem, 16)
        sid = id(b.dsem)
        for x in reads:
            x.r[sid] = b.dcount
        for x in writes:
            x.w[sid] = b.dcount
        self.dma_tokens[sid] = b.dcount

    def barrier(self):
        targets = {id(e.sem): e.count for e in self.E.values() if e.count > 0}
        targets.update(self.dma_tokens)
        for e in self.E.values():
            self._wait(e, {k: v for k, v in targets.items() if k != id(e.sem)})

    def final_wait(self):
        e = self.E["sp"]
        self._wait(e, dict(self.dma_tokens))


from concourse.bass_utils import run_bass_kernel_spmd

D = 1024
NMETA = 16
DFF = 2816
NFC = DFF // 128

W_SPECS = {
    "meta_tokens": (16, 1024), "norm_mix": (2, 1024), "norm_ffn": (2, 1024), "norm_final": (1, 1024),
    "e_w_in": (1024, 3848), "e_w_out": (1024, 1024), "m_b_i": (1, 4), "m_b_f": (1, 4), "m_norm": (1, 512),
    "r_mu": (1, 1792), "r_w0": (1, 512), "r_w2": (64, 512), "r_a0": (1, 512), "r_a2": (64, 512),
    "r_g2": (128, 512), "r_k_k": (1, 512), "r_k_a": (1, 512), "r_r_k": (1, 512), "r_ln_w": (1, 512),
    "r_ln_b": (1, 512), "o_w_in_p": (1024, 6144), "o_w_out": (2048, 1024), "f_w_up": (2048, 5632),
    "f_conv_w": (6, 2816), "f_conv_b": (2, 2816), "f_w_down": (5632, 1024),
}


def host_consts():
    c = {}
    c["c_ident"] = np.eye(128, dtype=np.float32)
    i = np.arange(128)
    c["c_ue"] = (i[:, None] <= i[None, :]).astype(np.float32)
    c["c_su"] = (i[:, None] < i[None, :]).astype(np.float32)
    c["c_iota"] = np.broadcast_to(np.arange(128, dtype=np.float32)[None, :], (128, 128)).copy()
    c["c_pidx"] = np.arange(128, dtype=np.float32)[:, None].copy()
    bo = np.zeros((128, 128), np.float32)
    bo[:64, :64] = 1.0
    bo[64:, 64:] = 1.0
    c["c_blk"] = bo
    c["c_inv"] = (np.float32(1.0) / np.power(np.float32(10000.0), np.linspace(0.0, 1.0, 128, dtype=np.float32))
                  ).astype(np.float32)[:, None].copy()
    return c


class Ctx:
    pass


def tile_rows(NT):
    tiles = [(0, NMETA)]
    for i in range(NT):
        tiles.append((NMETA + 128 * i, 128))
    return tiles


def build(NT, phases=(1, 2, 3, 4), debug=False, final=True):
    nc = bass.Bass("TRN2", target_bir_lowering=False)
    SEQ = 128 * NT
    L = NMETA + SEQ
    dr = {}
    dr["x"] = nc.dram_tensor("x", [SEQ, D], F32, kind="ExternalInput")
    for k, shp in W_SPECS.items():
        dr[k] = nc.dram_tensor(k, list(shp), F32, kind="ExternalInput")
    for k, v in host_consts().items():
        dr[k] = nc.dram_tensor(k, list(v.shape), F32, kind="ExternalInput")
    out = nc.dram_tensor("out", [SEQ, D], F32, kind="ExternalOutput")
    H = {}
    for i in (1, 2, 3):
        H[i] = nc.dram_tensor(f"H{i}", [L, D], F32, kind=("ExternalOutput" if debug else "Internal"))

    tiles = tile_rows(NT)
    es = contextlib.ExitStack()
    with es:
        kb = KB(nc, es)
        PS = [kb.ps([128, 512], F32, f"psb{i}") for i in range(7)]
        PST = kb.ps([128, 1024], BF16, "pstr")
        g = Ctx()
        g.nc, g.kb, g.dr, g.H, g.out, g.tiles, g.PS, g.PST = nc, kb, dr, H, out, tiles, PS, PST
        g.psi = 0
        g.ident_f = kb.sb([128, 128], F32, "ident_f")
        g.ident_b = kb.sb([128, 128], BF16, "ident_b")
        kb.dma(g.ident_f[:, :], dr["c_ident"].ap()[:, :], writes=[g.ident_f], sem_buf=g.ident_f)
        kb.op("dve", lambda e: e.tensor_copy(out=g.ident_b[:, :], in_=g.ident_f[:, :]),
              reads=[g.ident_f], writes=[g.ident_b])

        plist = [p for p in (1, 2, 3, 4) if p in phases]
        src = 0
        for p in plist:
            dst = p if p != plist[-1] else 4
            with contextlib.ExitStack() as pes:
                kb.es = pes
                if p in (2, 4):
                    phase_ffn(g, layer=(0 if p == 2 else 1), src=src, dst=dst, final=final)
                elif p == 1:
                    phase_l0(g, src=src, dst=dst, final=final)
                elif p == 3:
                    phase_l1(g, src=src, dst=dst, final=final)
                kb.barrier()
            kb.es = es
            src = dst
        kb.final_wait()
    return nc


def next_ps(g):
    b = g.PS[g.psi % len(g.PS)]
    g.psi += 1
    return b


def bc_rows(handle, row, n, parts=128, col0=0, ncols_total=None):
    ncols_total = ncols_total if ncols_total is not None else handle.shape[1]
    return bass.AP(handle, row * ncols_total + col0, [[0, parts], [1, n]])


def load_h(g, src, ti, HT):
    kb = g.kb
    r0, T = g.tiles[ti]
    if src == 0:
        if ti == 0:
            ap = g.dr["meta_tokens"].ap()[0:NMETA, :]
        else:
            ap = g.dr["x"].ap()[r0 - NMETA:r0 - NMETA + T, :]
    else:
        ap = g.H[src].ap()[r0:r0 + T, :]
    kb.dma(HT[:T, :], ap, writes=[HT], sem_buf=HT)


def store_h(g, dst, ti, HO, final, Gfin=None, scratch=None):
    kb = g.kb
    r0, T = g.tiles[ti]
    if dst != 4:
        kb.dma(g.H[dst].ap()[r0:r0 + T, :], HO[:T, :], reads=[HO], sem_buf=HO)
        return
    if ti == 0:
        return
    if final:
        ss, rstd, junk = scratch
        kb.op("act", lambda e: e.activation(out=junk[:T, 0:D], in_=HO[:T, :], func=AF.Square, accum_out=ss[:T, :]),
              reads=[HO], writes=[junk, ss])
        rstd_from_ss(kb, ss, rstd, T, 1.0 / D, 1e-6)
        kb.op("dve", lambda e: e.scalar_tensor_tensor(out=HO[:T, :], in0=HO[:T, :], scalar=rstd[:T, :],
                                                      in1=Gfin[:T, :], op0=ALU.mult, op1=ALU.mult),
              reads=[HO, rstd, Gfin], writes=[HO])
    kb.dma(g.out.ap()[r0 - NMETA:r0 - NMETA + T, :], HO[:T, :], reads=[HO], sem_buf=HO)


import os as _os
DMAQ_N = int(_os.environ.get("DMAQ_N", "1"))


def load_weight_bf16(g, dram_handle, row0, K, N, W, stg, col0=0, ncols_total=None):
    kb = g.kb
    SW = stg[0].t.shape[1]
    engs = ("dve", "act", "dve", "act", "dve", "act", "dve")
    cnt = getattr(g, "_lw_cnt", 0)
    for kc in range(K // 128):
        for j0 in range(0, N, SW):
            w = min(SW, N - j0)
            s = stg[cnt % len(stg)]
            kb.dma(s[:, :w], dram_handle.ap()[row0 + kc * 128: row0 + (kc + 1) * 128, col0 + j0: col0 + j0 + w],
                   writes=[s], sem_buf=s, q=(("sp", "pool", "act")[cnt % DMAQ_N] if DMAQ_N > 1 else "sp"))
            en = engs[cnt % len(engs)]
            if en == "act":
                kb.op("act", lambda e: e.copy(out=W[:, kc, j0:j0 + w], in_=s[:, :w]), reads=[s], writes=[W])
            else:
                kb.op(en, lambda e: e.tensor_copy(out=W[:, kc, j0:j0 + w], in_=s[:, :w]), reads=[s], writes=[W])
            cnt += 1
    g._lw_cnt = cnt


def rstd_from_ss(kb, ss, rstd, T, scale, eps, ap_fn=None):
    a = (lambda b: b[:T, :]) if ap_fn is None else ap_fn
    kb.op("act", lambda e: e.activation(out=a(rstd), in_=a(ss), func=AF.Sqrt, scale=scale, bias=eps),
          reads=[ss], writes=[rstd])
    kb.op("dve", lambda e: e.reciprocal(out=a(rstd), in_=a(rstd)), reads=[rstd], writes=[rstd])


def load_vec_fm(g, handle, row, nch, dstbuf, dst_ap, vtmp, col0=0):
    kb = g.kb
    ncols = handle.shape[1]
    src = bass.AP(handle, row * ncols + col0, [[128, nch], [1, 128]])
    kb.dma(vtmp[:nch, :], src, writes=[vtmp], sem_buf=vtmp)
    pt = next_ps(g)
    kb.op("pe", lambda e: e.transpose(out=pt[:, :nch], in_=vtmp[:nch, :], identity=g.ident_f[:nch, :nch]),
          reads=[vtmp, g.ident_f], writes=[pt])
    kb.op("dve", lambda e: e.tensor_copy(out=dst_ap, in_=pt[:, :nch]), reads=[pt], writes=[dstbuf])


def rmsnorm_T(g, HT, T, Gb, hn, hnT, ss, rstd, junk):
    kb = g.kb
    kb.op("act", lambda e: e.activation(out=junk[:T, 0:D], in_=HT[:T, :], func=AF.Square, accum_out=ss[:T, :]),
          reads=[HT], writes=[junk, ss])
    rstd_from_ss(kb, ss, rstd, T, 1.0 / D, 1e-6)
    kb.op("dve", lambda e: e.scalar_tensor_tensor(out=hn[:T, :], in0=HT[:T, :], scalar=rstd[:T, :],
                                                  in1=Gb[:T, :], op0=ALU.mult, op1=ALU.mult),
          reads=[HT, rstd, Gb], writes=[hn])
    PST = g.PST
    for kc in range(8):
        kb.op("pe", lambda e: e.transpose(out=PST[:, kc * T:(kc + 1) * T], in_=hn[:T, kc * 128:(kc + 1) * 128],
                                          identity=g.ident_b[:T, :T]),
              reads=[hn, g.ident_b], writes=[PST], inc=(kc == 7))
    kb.op("act", lambda e: e.copy(out=hnT[:, :, :T], in_=PST[:, 0:8 * T].rearrange("p (k t) -> p k t", k=8)),
          reads=[PST], writes=[hnT])


def norm_stats(g, HT, T, Gb, hn, ss, rstd, junk):
    kb = g.kb
    kb.op("act", lambda e: e.activation(out=junk[:T, 0:D], in_=HT[:T, :], func=AF.Square, accum_out=ss[:T, :]),
          reads=[HT], writes=[junk, ss])
    rstd_from_ss(kb, ss, rstd, T, 1.0 / D, 1e-6)
    kb.op("dve", lambda e: e.scalar_tensor_tensor(out=hn[:T, :], in0=HT[:T, :], scalar=rstd[:T, :],
                                                  in1=Gb[:T, :], op0=ALU.mult, op1=ALU.mult),
          reads=[HT, rstd, Gb], writes=[hn])


def norm_transpose(g, hn, hnT, T):
    kb = g.kb
    PST = g.PST
    for kc in range(8):
        kb.op("pe", lambda e: e.transpose(out=PST[:, kc * T:(kc + 1) * T], in_=hn[:T, kc * 128:(kc + 1) * 128],
                                          identity=g.ident_b[:T, :T]),
              reads=[hn, g.ident_b], writes=[PST], inc=(kc == 7))
    kb.op("act", lambda e: e.copy(out=hnT[:, :, :T], in_=PST[:, 0:8 * T].rearrange("p (k t) -> p k t", k=8)),
          reads=[PST], writes=[hnT])


def phase_ffn(g, layer, src, dst, final):
    kb, nc, dr = g.kb, g.nc, g.dr
    Wup = kb.sb([128, 8, 2 * DFF], BF16, "Wup")
    Wdn = kb.sb([128, NFC, D], BF16, "Wdn")
    with contextlib.ExitStack() as ses:
        old = kb.es
        kb.es = ses
        stg = [kb.sb([128, 1408], F32, f"stg{i}") for i in range(3)]
        load_weight_bf16(g, dr["f_w_up"], layer * D, D, 2 * DFF, Wup, stg)
        load_weight_bf16(g, dr["f_w_down"], layer * DFF, DFF, D, Wdn, stg)
        kb.barrier()
        kb.es = old
    Gb = kb.sb([128, D], F32, "Gb")
    kb.dma(Gb[:, :], bc_rows(dr["norm_ffn"], layer, D), writes=[Gb], sem_buf=Gb)
    Gfin = None
    if dst == 4 and final:
        Gfin = kb.sb([128, D], F32, "Gfin")
        kb.dma(Gfin[:, :], bc_rows(dr["norm_final"], 0, D), writes=[Gfin], sem_buf=Gfin)
    CW = kb.sb([128, 3, NFC], F32, "CW")
    CB = kb.sb([128, NFC], F32, "CB")
    vtmp = kb.sb([32, 128], F32, "vtmp")
    for j in range(3):
        load_vec_fm(g, dr["f_conv_w"], layer * 3 + j, NFC, CW, CW[:, j, :], vtmp)
    load_vec_fm(g, dr["f_conv_b"], layer, NFC, CB, CB[:, :], vtmp)
    HTs = [kb.sb([128, D], F32, f"HT{i}") for i in range(3)]
    hns = [kb.sb([128, D], BF16, f"hn{i}") for i in range(2)]
    hnTs = [kb.sb([128, 8, 128], BF16, f"hnT{i}") for i in range(2)]
    junk = kb.sb([128, D], BF16, "junk")
    sss = [kb.sb([128, 1], F32, f"ss{i}") for i in range(3)]
    rstds = [kb.sb([128, 1], F32, f"rstd{i}") for i in range(3)]
    G = kb.sb([128, NFC, 130], F32, "G")
    ACC = [kb.sb([128, 4, 128], F32, f"acc{i}") for i in range(2)]
    SIL = [kb.sb([128, 4, 128], F32, f"sil{i}") for i in range(2)]
    ACTT = kb.sb([128, NFC, 128], BF16, "ACTT")
    kb.op("dve", lambda e: e.memset(G[:, :, :], 0.0), writes=[G])
    po_banks = [g.PS[5], g.PS[6]]
    rot = g.PS[0:5]
    rot_i = [0]

    def next_rot():
        b = rot[rot_i[0] % len(rot)]
        rot_i[0] += 1
        return b

    ntl = len(g.tiles)
    load_h(g, src, 0, HTs[0])
    norm_stats(g, HTs[0], g.tiles[0][1], Gb, hns[0], sss[0], rstds[0], junk)
    if ntl > 1:
        load_h(g, src, 1, HTs[1])
    norm_transpose(g, hns[0], hnTs[0], g.tiles[0][1])

    def down_part(c_lo, c_hi, T):
        for c in range(c_lo, c_hi):
            for nb in range(2):
                po = po_banks[nb]
                kb.op("pe", lambda e: e.matmul(po[:T, :], ACTT[:, c, :T], Wdn[:, c, nb * 512:(nb + 1) * 512],
                                               start=(c == 0), stop=(c == NFC - 1)),
                      reads=[ACTT, Wdn], writes=[po], inc=(c == c_hi - 1))

    for ti, (r0, T) in enumerate(g.tiles):
        HT = HTs[ti % 3]
        hnT = hnTs[ti % 2]
        steps = list(range(0, NFC, 4))
        for si, c0 in enumerate(steps):
            nch = min(4, NFC - c0)
            pg = next_rot()
            pv = next_rot()
            for j in range(nch):
                for kc in range(8):
                    kb.op("pe", lambda e: e.matmul(pg[:, j * T:(j + 1) * T],
                                                   Wup[:, kc, DFF + (c0 + j) * 128: DFF + (c0 + j + 1) * 128],
                                                   hnT[:, kc, :T], start=(kc == 0), stop=(kc == 7)),
                          reads=[Wup, hnT], writes=[pg], inc=(kc == 7))
            for j in range(nch):
                for kc in range(8):
                    kb.op("pe", lambda e: e.matmul(pv[:, j * T:(j + 1) * T],
                                                   Wup[:, kc, (c0 + j) * 128:(c0 + j + 1) * 128],
                                                   hnT[:, kc, :T], start=(kc == 0), stop=(kc == 7)),
                          reads=[Wup, hnT], writes=[pv], inc=(kc == 7))
            if si >= 1:
                down_part(steps[si - 1], c0, T)
            kb.op("act", lambda e: e.copy(out=G[:, c0:c0 + nch, 2:2 + T],
                                          in_=pg[:, 0:nch * T].rearrange("p (c t) -> p c t", c=nch)),
                  reads=[pg], writes=[G])
            acc = ACC[si % 2]
            sil = SIL[si % 2]
            for j in range(nch):
                c = c0 + j
                kb.op("dve", lambda e: e.tensor_scalar(out=acc[:, j, :T], in0=G[:, c, 2:2 + T],
                                                       scalar1=CW[:, 2, c:c + 1], scalar2=CB[:, c:c + 1],
                                                       op0=ALU.mult, op1=ALU.add),
                      reads=[G, CW, CB], writes=[acc])
                kb.op("dve", lambda e: e.scalar_tensor_tensor(out=acc[:, j, :T], in0=G[:, c, 1:1 + T],
                                                              scalar=CW[:, 1, c:c + 1], in1=acc[:, j, :T],
                                                              op0=ALU.mult, op1=ALU.add),
                      reads=[G, CW, acc], writes=[acc])
                kb.op("dve", lambda e: e.scalar_tensor_tensor(out=acc[:, j, :T], in0=G[:, c, 0:T],
                                                              scalar=CW[:, 0, c:c + 1], in1=acc[:, j, :T],
                                                              op0=ALU.mult, op1=ALU.add),
                      reads=[G, CW, acc], writes=[acc])
            kb.op("act", lambda e: e.activation(out=sil[:, 0:nch, :T], in_=acc[:, 0:nch, :T], func=AF.Silu),
                  reads=[acc], writes=[sil])
            kb.op("dve", lambda e: e.tensor_tensor(out=ACTT[:, c0:c0 + nch, :T], in0=sil[:, 0:nch, :T],
                                                   in1=pv[:, 0:nch * T].rearrange("p (c t) -> p c t", c=nch),
                                                   op=ALU.mult),
                  reads=[sil, pv], writes=[ACTT])
        if ti + 1 < ntl:
            Tn = g.tiles[ti + 1][1]
            norm_stats(g, HTs[(ti + 1) % 3], Tn, Gb, hns[(ti + 1) % 2], sss[(ti + 1) % 3], rstds[(ti + 1) % 3], junk)
        down_part(steps[-1], NFC, T)
        kb.op("dve", lambda e: e.tensor_copy(out=G[:, :, 0:2], in_=G[:, :, T:T + 2]), reads=[G], writes=[G])
        if ti + 1 < ntl:
            norm_transpose(g, hns[(ti + 1) % 2], hnTs[(ti + 1) % 2], g.tiles[ti + 1][1])
        for nb in range(2):
            kb.op("dve", lambda e: e.tensor_tensor(out=HT[:T, nb * 512:(nb + 1) * 512],
                                                   in0=HT[:T, nb * 512:(nb + 1) * 512], in1=po_banks[nb][:T, :], op=ALU.add),
                  reads=[HT, po_banks[nb]], writes=[HT])
        store_h(g, dst, ti, HT, final, Gfin, (sss[2 - ti % 2 if False else (ti + 2) % 3], rstds[(ti + 2) % 3], junk))
        if ti + 2 < ntl:
            load_h(g, src, ti + 2, HTs[(ti + 2) % 3])


EH = 0.6065306597126334
ISQ = 0.08838834764831845
NEGBIG = -30000.0


def phase_l0(g, src, dst, final):
    import os
    CUT = int(os.environ.get('CUT', '99'))
    SUB = int(os.environ.get('SUB', '99'))
    HFN = int(os.environ.get('HFN', '2'))
    kb, nc, dr = g.kb, g.nc, g.dr
    Win = kb.sb([128, 8, 3848], BF16, "Win")
    Wout = kb.sb([128, 8, D], BF16, "Wout")
    W2A = kb.sb([128, 512], BF16, "W2A")
    G2 = kb.sb([128, 512], BF16, "G2")
    with contextlib.ExitStack() as ses:
        old = kb.es
        kb.es = ses
        stg = [kb.sb([128, 1924], F32, f"stg{i}") for i in range(3)]
        load_weight_bf16(g, dr["e_w_in"], 0, D, 3848, Win, stg)
        load_weight_bf16(g, dr["e_w_out"], 0, D, D, Wout, stg)
        s0 = stg[0]
        kb.dma(s0[0:64, 0:512], dr["r_w2"].ap()[:, :], writes=[s0], sem_buf=s0)
        kb.dma(s0[64:128, 0:512], dr["r_a2"].ap()[:, :], writes=[s0], sem_buf=s0)
        kb.op("dve", lambda e: e.tensor_copy(out=W2A[:, :], in_=s0[:, 0:512]), reads=[s0], writes=[W2A])
        s1 = stg[1]
        kb.dma(s1[:, 0:512], dr["r_g2"].ap()[:, :], writes=[s1], sem_buf=s1)
        kb.op("dve", lambda e: e.tensor_copy(out=G2[:, :], in_=s1[:, 0:512]), reads=[s1], writes=[G2])
        kb.barrier()
        kb.es = old
    F = lambda shape, name: kb.sb(shape, F32, name)
    Bf = lambda shape, name: kb.sb(shape, BF16, name)
    Gb = Bf([128, D], "Gb")
    ue = F([128, 128], "ue")
    su = F([128, 128], "su")
    blk = F([128, 128], "blk")
    kb.dma(ue[:, :], dr["c_ue"].ap()[:, :], writes=[ue], sem_buf=ue)
    kb.dma(su[:, :], dr["c_su"].ap()[:, :], writes=[su], sem_buf=su)
    kb.dma(blk[:, :], dr["c_blk"].ap()[:, :], writes=[blk], sem_buf=blk)
    sl = F([128, 128], "sl")
    kb.op("dve", lambda e: e.tensor_scalar(out=sl[:, :], in0=ue[:, :], scalar1=-1.0, scalar2=1.0, op0=ALU.mult, op1=ALU.add),
          reads=[ue], writes=[sl])
    neg = F([128, 128], "neg")
    kb.op("dve", lambda e: e.tensor_scalar(out=neg[:, :], in0=sl[:, :], scalar1=NEGBIG, scalar2=None, op0=ALU.mult),
          reads=[sl], writes=[neg])
    nblk = F([128, 2], "nblk")
    kb.op("dve", lambda e: e.tensor_scalar(out=nblk[:, 0:1], in0=blk[:, 0:1], scalar1=-1.0, scalar2=None, op0=ALU.mult),
          reads=[blk], writes=[nblk])
    kb.op("dve", lambda e: e.tensor_scalar(out=nblk[:, 1:2], in0=blk[:, 127:128], scalar1=-1.0, scalar2=None, op0=ALU.mult),
          reads=[blk], writes=[nblk])
    blk64 = F([128, 128], "blk64")
    kb.op("dve", lambda e: e.tensor_scalar(out=blk64[:, :], in0=blk[:, :], scalar1=1.0 / 64.0, scalar2=None, op0=ALU.mult),
          reads=[blk], writes=[blk64])
    onesf = F([128, 128], "onesf")
    kb.op("dve", lambda e: e.memset(onesf[:, :], 1.0 / 128.0), writes=[onesf])
    onesb = Bf([128, 128], "onesb")
    kb.op("dve", lambda e: e.memset(onesb[:, :], 1.0), writes=[onesb])
    ones1 = F([128, 128], "ones1")
    kb.op("dve", lambda e: e.memset(ones1[:, :], 1.0), writes=[ones1])
    vtmp = F([32, 128], "vtmp")
    MU = F([128, 14], "MU"); W0 = F([128, 4], "W0"); A0 = F([128, 4], "A0"); KK = F([128, 4], "KK")
    KA = F([128, 4], "KA"); RRK = F([128, 4], "RRK"); LNW = F([128, 4], "LNW"); LNB = F([128, 4], "LNB")
    MN = F([128, 4], "MN")
    load_vec_fm(g, dr["r_mu"], 0, 14, MU, MU[:, :], vtmp)
    for nm, buf in (("r_w0", W0), ("r_a0", A0), ("r_k_k", KK), ("r_k_a", KA), ("r_r_k", RRK), ("r_ln_w", LNW),
                    ("r_ln_b", LNB), ("m_norm", MN)):
        load_vec_fm(g, dr[nm], 0, 4, buf, buf[:, :], vtmp)
    BG = F([128, 8], "BG")
    kb.dma(BG[:, 0:4], bc_rows(dr["m_b_i"], 0, 4), writes=[BG], sem_buf=BG)
    kb.dma(BG[:, 4:8], bc_rows(dr["m_b_f"], 0, 4), writes=[BG], sem_buf=BG)
    C = F([128, 4, 129], "C")
    Cb = Bf([128, 4, 128], "Cb")
    nbc = Bf([128, 4, 128], "nbc")
    ST = F([128, 4, 64], "ST")
    STb = Bf([128, 4, 64], "STb")
    ZR = F([128, 14, 129], "ZR")
    for b_ in (C, ST, ZR):
        kb.op("dve", lambda e: e.memset(b_[:, :, :], 0.0), writes=[b_])
    for b_ in (Cb, nbc, STb):
        kb.op("dve", lambda e: e.memset(b_[:, :, :], 0.0), writes=[b_])
    vTM1 = Bf([128, 4, 129], "vTM1")
    kb.op("dve", lambda e: e.memset(vTM1[:, :, :], 1.0), writes=[vTM1])
    HTs = [F([128, D], f"HT{i}") for i in range(3)]
    hn = Bf([128, D], "hn"); hnT = Bf([128, 8, 128], "hnT"); junk = hn
    ss = F([128, 1], "ss"); rstd = F([128, 1], "rstd")
    qTb = Bf([128, 4, 128], "qTb"); kTb = Bf([128, 4, 128], "kTb"); moT = Bf([128, 4, 128], "moT"); kpbuf = F([128, 4, 128], "kpbuf")
    gx = F([128, 8], "gx"); th = F([128, 8], "th"); ex = F([128, 4], "ex"); LI = F([128, 4], "LI"); LF = F([128, 4], "LF")
    lmb = F([128, 4], "lmb"); LFb = F([128, 4, 128], "LFb"); arg = F([128, 4, 128], "arg"); ET = Bf([128, 4, 128], "ET"); aabuf = Bf([128, 4, 128], "aabuf")
    eB = F([128, 4, 128], "eB"); gcol = F([128, 4], "gcol"); ew = F([128, 4], "ew"); qs = Bf([128, 4, 128], "qs")
    sT = Bf([128, 4, 128], "sT"); kw = Bf([128, 4, 128], "kw")
    cden = F([128, 4, 128], "cden"); hT = F([128, 4, 128], "hT"); hsq = F([128, 4, 128], "hsq"); rs4 = F([128, 4, 128], "rs4")
    mixTs = [Bf([128, 8, 128], f"mixT{i}") for i in range(2)]
    kTMf = Bf([128, 512], "kTMf")
    P1 = F([128, 512], "P1")
    GGs = [Bf([128, 4, 128], f"GG{i}") for i in range(2)]
    for hh_ in range(2):
        kb.dma(P1[:, :], bc_rows(dr["norm_mix"], 0, 512, col0=512 * hh_), writes=[P1], sem_buf=P1)
        kb.op("dve", lambda e: e.tensor_copy(out=Gb[:, 512 * hh_:512 * (hh_ + 1)], in_=P1[:, :]), reads=[P1], writes=[Gb])
    Z2 = Bf([128, 14, 128], "Z2"); D1 = Z2
    LIN = Bf([128, 128], "LIN"); sxg = Bf([128, 128], "sxg")
    sw = arg; aa = aabuf
    kkr = cden; tq = hsq; rn = rs4; kp = kpbuf
    CS = hT; CSp = F([128, 4, 128], "CSp"); csl = F([128, 4], "csl")
    eW = F([128, 4, 128], "eW"); eWp = eB; eWi = F([128, 4, 128], "eWi"); eWT = F([128, 4, 128], "eWT")
    kka = F([128, 4, 128], "kka")
    AR = Bf([128, 4, 2, 128], "AR")
    BH = Bf([128, 4, 128], "BH"); KH = Bf([128, 4, 128], "KH"); vb = Bf([128, 4, 128], "vb")
    bonuss = [Bf([128, 4, 128], f"bonus{i}") for i in range(2)]
    BTm = [Bf([128, 4, 128], f"BTm{i}") for i in range(2)]
    KTm = [Bf([128, 4, 128], f"KTm{i}") for i in range(2)]
    ATm = [Bf([128, 4, 128], f"ATm{i}") for i in range(2)]
    STbd = Bf([128, 4, 128], "STbd")
    kb.op("dve", lambda e: e.memset(STbd[:, :, :], 0.0), writes=[STbd])
    VTM = Bf([128, 8, 64], "VTM"); BHT = Bf([128, 8, 64], "BHT"); KHT = Bf([128, 8, 64], "KHT"); UTM = Bf([128, 8, 64], "UTM")
    Xa = [F([128, 8, 128], "Xa0"), F([128, 8, 128], "Xa1")]
    XTa = [F([128, 8, 128], "XTa0"), F([128, 8, 128], "XTa1")]
    Pm = F([128, 8, 128], "Pm")
    ARB = Bf([128, 8, 128], "ARB"); AAK = Bf([128, 8, 128], "AAK"); ARK = Bf([128, 8, 128], "ARK")

    class SubBuf:
        def __init__(self, parent, lo):
            self.parent, self.lo = parent, lo
            self.w, self.r, self.excl, self.name = parent.w, parent.r, False, parent.name

        def __getitem__(self, idx):
            p, c, t = idx
            if isinstance(c, slice):
                c = slice((c.start or 0) + self.lo, (c.stop if c.stop is not None else 4) + self.lo)
            else:
                c = c + self.lo
            return self.parent.t[p, c, t]

    Of = SubBuf(Xa[1], 0); Osq = SubBuf(Xa[1], 4); mean_s = SubBuf(XTa[1], 0); var = SubBuf(XTa[1], 4)

    def b3(buf, T, n=4):
        return buf[:, 0:n].unsqueeze(2).to_broadcast([128, n, T])

    psA = g.PS[0:3]
    psB = g.PS[3:7]
    ia = [0]
    ib = [0]

    def nA():
        b = psA[ia[0] % len(psA)]
        ia[0] += 1
        return b

    def nB():
        b = psB[ib[0] % len(psB)]
        ib[0] += 1
        return b

    graw = F([128, 8], "graw")

    def gen_proj(ti):
        r0, T = g.tiles[ti]
        HT = HTs[ti % 3]
        mixT = mixTs[ti % 2]
        GG = GGs[ti % 2]
        bonus = bonuss[ti % 2]
        def proj_fm(pbank, j, col):
            for kc in range(8):
                kb.op("pe", lambda e: e.matmul(pbank[:, j * T:(j + 1) * T], Win[:, kc, col:col + 128], hnT[:, kc, :T],
                                               start=(kc == 0), stop=(kc == 7)),
                      reads=[Win, hnT], writes=[pbank], inc=(kc == 7))

        def proj_tm(pbank, col, n, c0=0):
            for kc in range(8):
                kb.op("pe", lambda e: e.matmul(pbank[:T, c0:c0 + n], hnT[:, kc, :T], Win[:, kc, col:col + n],
                                               start=(kc == 0), stop=(kc == 7)),
                      reads=[hnT, Win], writes=[pbank], inc=(kc == 7))

        def v3(pbank, n=4):
            return pbank[:, 0:n * T].rearrange("p (c t) -> p c t", c=n)

        pq = nA(); pk = nA(); pmo = nA()
        for h in range(4):
            proj_fm(pq, h, h * 128)
        yield
        for h in range(4):
            proj_fm(pk, h, 512 + h * 128)
        yield
        for h in range(4):
            proj_fm(pmo, h, 1536 + h * 128)
        kb.op("act", lambda e: e.copy(out=qTb[:, :, :T], in_=v3(pq)), reads=[pq], writes=[qTb])
        kb.op("act", lambda e: e.copy(out=kTb[:, :, :T], in_=v3(pk)), reads=[pk], writes=[kTb])
        kb.op("act", lambda e: e.activation(out=moT[:, :, :T], in_=v3(pmo), func=AF.Sigmoid), reads=[pmo], writes=[moT])
        yield
        pkt = nA(); pvt = nA(); pgt = nA()
        proj_tm(pkt, 512, 512)
        yield
        proj_tm(pvt, 1024, 512)
        proj_tm(pgt, 2048, 8)
        kb.op("act", lambda e: e.copy(out=vTM1[:T, :, 0:128], in_=pvt[:T, :].rearrange("p (h v) -> p h v", h=4)),
              reads=[pvt], writes=[vTM1])
        kb.op("act", lambda e: e.copy(out=kTMf[:T, :], in_=pkt[:T, :]), reads=[pkt], writes=[kTMf])
        kb.op("act", lambda e: e.copy(out=graw[:T, :], in_=pgt[:T, 0:8]), reads=[pgt], writes=[graw])
        zc = 2056
        for b0, n in ((0, 4), (4, 4), (8, 4), (12, 2)):
            yield
            pz = nA()
            for j in range(n):
                proj_fm(pz, j, zc + (b0 + j) * 128)
            kb.op("act", lambda e: e.copy(out=ZR[:, b0:b0 + n, 1:T + 1], in_=v3(pz, n)), reads=[pz], writes=[ZR])

    def gen_mlstm(ti):
        r0, T = g.tiles[ti]
        HT = HTs[ti % 3]
        mixT = mixTs[ti % 2]
        GG = GGs[ti % 2]
        bonus = bonuss[ti % 2]
        def proj_fm(pbank, j, col):
            for kc in range(8):
                kb.op("pe", lambda e: e.matmul(pbank[:, j * T:(j + 1) * T], Win[:, kc, col:col + 128], hnT[:, kc, :T],
                                               start=(kc == 0), stop=(kc == 7)),
                      reads=[Win, hnT], writes=[pbank], inc=(kc == 7))

        def proj_tm(pbank, col, n, c0=0):
            for kc in range(8):
                kb.op("pe", lambda e: e.matmul(pbank[:T, c0:c0 + n], hnT[:, kc, :T], Win[:, kc, col:col + n],
                                               start=(kc == 0), stop=(kc == 7)),
                      reads=[hnT, Win], writes=[pbank], inc=(kc == 7))

        def v3(pbank, n=4):
            return pbank[:, 0:n * T].rearrange("p (c t) -> p c t", c=n)

        kb.op("dve", lambda e: e.tensor_tensor(out=gx[:T, :], in0=graw[:T, :], in1=BG[:T, :], op=ALU.add),
              reads=[graw, BG], writes=[gx])
        kb.op("act", lambda e: e.activation(out=th[:T, :], in_=gx[:T, :], func=AF.Tanh, scale=1.0 / 15.0), reads=[gx], writes=[th])
        kb.op("dve", lambda e: e.tensor_scalar(out=LI[:T, :], in0=th[:T, 0:4], scalar1=15.0, scalar2=None, op0=ALU.mult),
              reads=[th], writes=[LI])
        kb.op("act", lambda e: e.activation(out=ex[:T, :], in_=th[:T, 4:8], func=AF.Exp, scale=-15.0), reads=[th], writes=[ex])
        kb.op("act", lambda e: e.activation(out=ex[:T, :], in_=ex[:T, :], func=AF.Ln, bias=1.0), reads=[ex], writes=[ex])
        kb.op("dve", lambda e: e.tensor_scalar(out=LF[:T, :], in0=ex[:T, :], scalar1=-1.0, scalar2=None, op0=ALU.mult),
              reads=[ex], writes=[LF])
        yield
        pbc = nA()
        kb.op("pe", lambda e: e.matmul(pbc[:T, 0:4], ue[:T, :T], LF[:T, :], start=True, stop=True), reads=[ue, LF], writes=[pbc])
        kb.op("dve", lambda e: e.tensor_tensor(out=lmb[:T, :], in0=LI[:T, :], in1=pbc[:T, 0:4], op=ALU.subtract),
              reads=[LI, pbc], writes=[lmb])
        kb.op("dve", lambda e: e.tensor_copy(out=LFb[:T, :, :], in_=LF[:T, 0:4].unsqueeze(2).to_broadcast([T, 4, 128])),
              reads=[LF], writes=[LFb])
        yield
        pB = nA()
        for h in range(4):
            kb.op("pe", lambda e: e.matmul(pB[:, h * T:(h + 1) * T], LFb[:T, h, :], ue[:T, :T], start=True, stop=True),
                  reads=[LFb, ue], writes=[pB], inc=(h == 3))
        kb.op("dve", lambda e: e.tensor_tensor(out=arg[:T, :, :T], in0=v3(pB)[:T], in1=neg[:T, :T].unsqueeze(1).to_broadcast([T, 4, T]),
                                               op=ALU.add), reads=[pB, neg], writes=[arg])
        for h in range(4):
            kb.op("act", lambda e: e.activation(out=ET[:T, h, :T], in_=arg[:T, h, :T], func=AF.Exp, bias=lmb[:T, h:h + 1]),
                  reads=[arg, lmb], writes=[ET])
        kb.op("act", lambda e: e.activation(out=eB[:, :, :T], in_=v3(pB), func=AF.Exp), reads=[pB], writes=[eB])
        kb.op("dve", lambda e: e.tensor_copy(out=gcol[:, :], in_=v3(pB)[:, :, T - 1]), reads=[pB], writes=[gcol])
        for h in range(4):
            kb.op("act", lambda e: e.activation(out=ew[:T, h:h + 1], in_=lmb[:T, h:h + 1], func=AF.Exp, bias=gcol[:T, h:h + 1]),
                  reads=[lmb, gcol], writes=[ew])
        kb.op("dve", lambda e: e.tensor_tensor(out=qs[:, :, :T], in0=qTb[:, :, :T], in1=eB[:, :, :T], op=ALU.mult),
              reads=[qTb, eB], writes=[qs])
        yield
        psc = nA()
        for h in range(4):
            kb.op("pe", lambda e: e.matmul(psc[:T, h * T:(h + 1) * T], kTb[:, h, :T], qTb[:, h, :T], start=True, stop=True),
                  reads=[kTb, qTb], writes=[psc], inc=(h == 3))
        kb.op("dve", lambda e: e.scalar_tensor_tensor(out=sT[:T, :, :T], in0=v3(psc)[:T], scalar=ISQ, in1=ET[:T, :, :T],
                                                      op0=ALU.mult, op1=ALU.mult), reads=[psc, ET], writes=[sT])
        yield
        pnum = nA(); pden = nA()
        for h in range(4):
            kb.op("pe", lambda e: e.matmul(pnum[:, h * T:(h + 1) * T], vTM1[:T, h, 0:128], sT[:T, h, :T], start=True, stop=False),
                  reads=[vTM1, sT], writes=[pnum], inc=False)
            kb.op("pe", lambda e: e.matmul(pnum[:, h * T:(h + 1) * T], Cb[:, h, :], qs[:, h, :T], start=False, stop=True),
                  reads=[Cb, qs], writes=[pnum])
        for h in range(4):
            kb.op("pe", lambda e: e.matmul(pden[:, h * T:(h + 1) * T], onesb[:T, :], sT[:T, h, :T], start=True, stop=False),
                  reads=[onesb, sT], writes=[pden], inc=False)
            kb.op("pe", lambda e: e.matmul(pden[:, h * T:(h + 1) * T], nbc[:, h, :], qs[:, h, :T], start=False, stop=True),
                  reads=[nbc, qs], writes=[pden])
        yield
        kb.op("act", lambda e: e.activation(out=cden[:, :, :T], in_=v3(pden), func=AF.Abs), reads=[pden], writes=[cden])
        kb.op("dve", lambda e: e.tensor_scalar(out=cden[:, :, :T], in0=cden[:, :, :T], scalar1=1.0, scalar2=None, op0=ALU.max),
              reads=[cden], writes=[cden])
        kb.op("dve", lambda e: e.reciprocal(out=cden[:, :, :T], in_=cden[:, :, :T]), reads=[cden], writes=[cden])
        kb.op("dve", lambda e: e.tensor_tensor(out=hT[:, :, :T], in0=v3(pnum), in1=cden[:, :, :T], op=ALU.mult),
              reads=[pnum, cden], writes=[hT])
        kb.op("act", lambda e: e.activation(out=hsq[:, :, :T], in_=hT[:, :, :T], func=AF.Square), reads=[hT], writes=[hsq])
        yield
        pss = nA()
        for h in range(4):
            kb.op("pe", lambda e: e.matmul(pss[:, h * T:(h + 1) * T], onesf[:, :], hsq[:, h, :T], start=True, stop=True),
                  reads=[onesf, hsq], writes=[pss], inc=(h == 3))
        kb.op("act", lambda e: e.activation(out=rs4[:, :, :T], in_=v3(pss), func=AF.Sqrt, bias=1e-6), reads=[pss], writes=[rs4])
        kb.op("dve", lambda e: e.reciprocal(out=rs4[:, :, :T], in_=rs4[:, :, :T]), reads=[rs4], writes=[rs4])
        kb.op("dve", lambda e: e.tensor_tensor(out=hT[:, :, :T], in0=hT[:, :, :T], in1=rs4[:, :, :T], op=ALU.mult),
              reads=[hT, rs4], writes=[hT])
        kb.op("dve", lambda e: e.tensor_tensor(out=hT[:, :, :T], in0=hT[:, :, :T], in1=moT[:, :, :T], op=ALU.mult),
              reads=[hT, moT], writes=[hT])
        kb.op("dve", lambda e: e.tensor_tensor(out=mixT[:, 0:4, :T], in0=hT[:, :, :T], in1=b3(MN, T), op=ALU.mult),
              reads=[hT, MN], writes=[mixT])
        yield
        for h in range(4):
            kb.op("dve", lambda e: e.tensor_scalar(out=kw[:T, h, :], in0=kTMf[:T, h * 128:(h + 1) * 128], scalar1=ew[:T, h:h + 1],
                                                   scalar2=ISQ, op0=ALU.mult, op1=ALU.mult), reads=[kTMf, ew], writes=[kw])
        for half in range(2):
            pC = nA()
            for hh in range(2):
                h = half * 2 + hh
                kb.op("pe", lambda e: e.matmul(pC[:, hh * 129:(hh + 1) * 129], kw[:T, h, :], vTM1[:T, h, :], start=True, stop=True),
                      reads=[kw, vTM1], writes=[pC], inc=(hh == 1))
            for hh in range(2):
                h = half * 2 + hh
                kb.op("dve", lambda e: e.scalar_tensor_tensor(out=C[:, h, :], in0=C[:, h, :], scalar=eB[:, h, T - 1:T],
                                                              in1=pC[:, hh * 129:(hh + 1) * 129], op0=ALU.mult, op1=ALU.add),
                      reads=[C, eB, pC], writes=[C])
        yield
        kb.op("act", lambda e: e.copy(out=Cb[:, :, :], in_=C[:, :, 0:128]), reads=[C], writes=[Cb])
        kb.op("dve", lambda e: e.tensor_copy(out=nbc[:, :, :], in_=C[:, :, 128:129].to_broadcast([128, 4, 128])),
              reads=[C], writes=[nbc])


    def gen_prep(ti):
        r0, T = g.tiles[ti]
        HT = HTs[ti % 3]
        mixT = mixTs[ti % 2]
        GG = GGs[ti % 2]
        bonus = bonuss[ti % 2]
        def proj_fm(pbank, j, col):
            for kc in range(8):
                kb.op("pe", lambda e: e.matmul(pbank[:, j * T:(j + 1) * T], Win[:, kc, col:col + 128], hnT[:, kc, :T],
                                               start=(kc == 0), stop=(kc == 7)),
                      reads=[Win, hnT], writes=[pbank], inc=(kc == 7))

        def proj_tm(pbank, col, n, c0=0):
            for kc in range(8):
                kb.op("pe", lambda e: e.matmul(pbank[:T, c0:c0 + n], hnT[:, kc, :T], Win[:, kc, col:col + n],
                                               start=(kc == 0), stop=(kc == 7)),
                      reads=[hnT, Win], writes=[pbank], inc=(kc == 7))

        def v3(pbank, n=4):
            return pbank[:, 0:n * T].rearrange("p (c t) -> p c t", c=n)

        kb.op("dve", lambda e: e.tensor_tensor(out=D1[:, :, :T], in0=ZR[:, :, 0:T], in1=ZR[:, :, 1:T + 1], op=ALU.subtract),
              reads=[ZR], writes=[D1])
        kb.op("dve", lambda e: e.tensor_tensor(out=D1[:, :, :T], in0=D1[:, :, :T], in1=b3(MU, T, 14), op=ALU.mult),
              reads=[D1, MU], writes=[D1])
        kb.op("dve", lambda e: e.tensor_tensor(out=Z2[:, :, :T], in0=D1[:, :, :T], in1=ZR[:, :, 1:T + 1], op=ALU.add),
              reads=[D1, ZR], writes=[Z2])
        kb.op("dve", lambda e: e.tensor_copy(out=ZR[:, :, 0:1], in_=ZR[:, :, T:T + 1]), reads=[ZR], writes=[ZR])
        r_ = Z2[:, 0:4, :T]; k_ = Z2[:, 4:8, :T]; v_ = Z2[:, 8:12, :T]
        yield
        kb.op("act", lambda e: e.activation(out=LIN[0:64, :T], in_=Z2[0:64, 12, :T], func=AF.Tanh), reads=[Z2], writes=[LIN])
        kb.op("act", lambda e: e.copy(out=LIN[64:128, :T], in_=Z2[64:128, 12, :T]), reads=[Z2], writes=[LIN])
        kb.op("act", lambda e: e.activation(out=sxg[:, :T], in_=Z2[:, 13, :T], func=AF.Sigmoid), reads=[Z2], writes=[sxg])
        yield
        pw = nB(); pa = nB(); pgg = nB()
        for c in range(4):
            kb.op("pe", lambda e: e.matmul(pw[:, c * T:(c + 1) * T], W2A[0:64, c * 128:(c + 1) * 128], LIN[0:64, :T], start=True, stop=True),
                  reads=[W2A, LIN], writes=[pw], inc=(c == 3))
        for c in range(4):
            kb.op("pe", lambda e: e.matmul(pa[:, c * T:(c + 1) * T], W2A[64:128, c * 128:(c + 1) * 128], LIN[64:128, :T], start=True, stop=True),
                  reads=[W2A, LIN], writes=[pa], inc=(c == 3))
        for c in range(4):
            kb.op("pe", lambda e: e.matmul(pgg[:, c * T:(c + 1) * T], G2[:, c * 128:(c + 1) * 128], sxg[:, :T], start=True, stop=True),
                  reads=[G2, sxg], writes=[pgg], inc=(c == 3))
        for c in range(4):
            kb.op("act", lambda e: e.activation(out=sw[:, c, :T], in_=pw[:, c * T:(c + 1) * T], func=AF.Sigmoid, bias=W0[:, c:c + 1]),
                  reads=[pw, W0], writes=[sw])
            kb.op("act", lambda e: e.activation(out=aa[:, c, :T], in_=pa[:, c * T:(c + 1) * T], func=AF.Sigmoid, bias=A0[:, c:c + 1]),
                  reads=[pa, A0], writes=[aa])
        kb.op("act", lambda e: e.copy(out=GG[:, :, :T], in_=v3(pgg)), reads=[pgg], writes=[GG])
        yield
        kb.op("dve", lambda e: e.tensor_tensor(out=kkr[:, :, :T], in0=k_, in1=b3(KK, T), op=ALU.mult), reads=[Z2, KK], writes=[kkr])
        kb.op("act", lambda e: e.activation(out=tq[:, :, :T], in_=kkr[:, :, :T], func=AF.Square), reads=[kkr], writes=[tq])
        pn = nB()
        for c in range(4):
            kb.op("pe", lambda e: e.matmul(pn[:, c * T:(c + 1) * T], blk[:, :], tq[:, c, :T], start=True, stop=True),
                  reads=[blk, tq], writes=[pn], inc=(c == 3))
        kb.op("act", lambda e: e.activation(out=rn[:, :, :T], in_=v3(pn), func=AF.Sqrt), reads=[pn], writes=[rn])
        kb.op("dve", lambda e: e.tensor_scalar(out=rn[:, :, :T], in0=rn[:, :, :T], scalar1=1e-12, scalar2=None, op0=ALU.max),
              reads=[rn], writes=[rn])
        kb.op("dve", lambda e: e.reciprocal(out=rn[:, :, :T], in_=rn[:, :, :T]), reads=[rn], writes=[rn])
        kb.op("dve", lambda e: e.tensor_tensor(out=kkr[:, :, :T], in0=kkr[:, :, :T], in1=rn[:, :, :T], op=ALU.mult),
              reads=[kkr, rn], writes=[kkr])
        yield
        kb.op("dve", lambda e: e.scalar_tensor_tensor(out=tq[:, :, :T], in0=aa[:, :, :T], scalar=-1.0, in1=b3(KA, T),
                                                      op0=ALU.add, op1=ALU.mult), reads=[aa, KA], writes=[tq])
        kb.op("dve", lambda e: e.scalar_tensor_tensor(out=kp[:, :, :T], in0=tq[:, :, :T], scalar=1.0, in1=k_,
                                                      op0=ALU.add, op1=ALU.mult), reads=[tq, Z2], writes=[kp])
        yield
        for c in range(4):
            kb.op("dve", lambda e: e.tensor_tensor_scan(out=CS[:, c, :T], data0=ones1[:, :T], data1=sw[:, c, :T], initial=0.0,
                                                        op0=ALU.mult, op1=ALU.add), reads=[ones1, sw], writes=[CS])
        kb.op("dve", lambda e: e.tensor_tensor(out=CSp[:, :, :T], in0=CS[:, :, :T], in1=sw[:, :, :T], op=ALU.subtract),
              reads=[CS, sw], writes=[CSp])
        kb.op("dve", lambda e: e.tensor_scalar(out=csl[:, :], in0=CS[:, :, T - 1], scalar1=-EH, scalar2=None, op0=ALU.mult),
              reads=[CS], writes=[csl])
        kb.op("act", lambda e: e.activation(out=eW[:, :, :T], in_=CS[:, :, :T], func=AF.Exp, scale=-EH), reads=[CS], writes=[eW])
        kb.op("act", lambda e: e.activation(out=eWp[:, :, :T], in_=CSp[:, :, :T], func=AF.Exp, scale=-EH), reads=[CSp], writes=[eWp])
        kb.op("act", lambda e: e.activation(out=eWi[:, :, :T], in_=CS[:, :, :T], func=AF.Exp, scale=EH), reads=[CS], writes=[eWi])
        for c in range(4):
            kb.op("act", lambda e: e.activation(out=eWT[:, c, :T], in_=CS[:, c, :T], func=AF.Exp, scale=EH, bias=csl[:, c:c + 1]),
                  reads=[CS, csl], writes=[eWT])
        yield
        kb.op("dve", lambda e: e.scalar_tensor_tensor(out=AR[:, :, 0, :T], in0=kkr[:, :, :T], scalar=-1.0, in1=eWp[:, :, :T],
                                                      op0=ALU.mult, op1=ALU.mult), reads=[kkr, eWp], writes=[AR])
        kb.op("dve", lambda e: e.tensor_tensor(out=AR[:, :, 1, :T], in0=r_, in1=eW[:, :, :T], op=ALU.mult), reads=[Z2, eW], writes=[AR])
        kb.op("dve", lambda e: e.tensor_tensor(out=kka[:, :, :T], in0=kkr[:, :, :T], in1=aa[:, :, :T], op=ALU.mult),
              reads=[kkr, aa], writes=[kka])
        kb.op("dve", lambda e: e.tensor_tensor(out=BH[:, :, :T], in0=kka[:, :, :T], in1=eWT[:, :, :T], op=ALU.mult),
              reads=[kka, eWT], writes=[BH])
        kb.op("dve", lambda e: e.tensor_tensor(out=KH[:, :, :T], in0=kp[:, :, :T], in1=eWT[:, :, :T], op=ALU.mult),
              reads=[kp, eWT], writes=[KH])
        kb.op("act", lambda e: e.copy(out=vb[:, :, :T], in_=v_), reads=[Z2], writes=[vb])
        yield
        kb.op("dve", lambda e: e.tensor_tensor(out=tq[:, :, :T], in0=r_, in1=kp[:, :, :T], op=ALU.mult), reads=[Z2, kp], writes=[tq])
        kb.op("dve", lambda e: e.tensor_tensor(out=tq[:, :, :T], in0=tq[:, :, :T], in1=b3(RRK, T), op=ALU.mult),
              reads=[tq, RRK], writes=[tq])
        prk = nB()
        for c in range(4):
            kb.op("pe", lambda e: e.matmul(prk[:, c * T:(c + 1) * T], blk[:, :], tq[:, c, :T], start=True, stop=True),
                  reads=[blk, tq], writes=[prk], inc=(c == 3))
        kb.op("dve", lambda e: e.tensor_tensor(out=bonus[:, :, :T], in0=v3(prk), in1=v_, op=ALU.mult), reads=[prk, Z2], writes=[bonus])
        yield
        PST = g.PST
        for c in range(4):
            kb.op("pe", lambda e: e.transpose(out=PST[:T, c * 128:(c + 1) * 128], in_=vb[:, c, :T], identity=g.ident_b[:, :]),
                  reads=[vb, g.ident_b], writes=[PST], inc=False)
        for c in range(4):
            kb.op("pe", lambda e: e.transpose(out=PST[:T, (4 + c) * 128:(5 + c) * 128], in_=BH[:, c, :T], identity=g.ident_b[:, :]),
                  reads=[BH, g.ident_b], writes=[PST], inc=(c == 3))
        kb.op("act", lambda e: e.copy(out=VTM[:T, :, :], in_=PST[:T, 0:512].rearrange("p (h v) -> p h v", h=8)), reads=[PST], writes=[VTM])
        kb.op("act", lambda e: e.copy(out=BHT[:T, :, :], in_=PST[:T, 512:1024].rearrange("p (h v) -> p h v", h=8)), reads=[PST], writes=[BHT])
        for c in range(4):
            kb.op("pe", lambda e: e.transpose(out=PST[:T, c * 128:(c + 1) * 128], in_=KH[:, c, :T], identity=g.ident_b[:, :]),
                  reads=[KH, g.ident_b], writes=[PST], inc=(c == 3))
        kb.op("act", lambda e: e.copy(out=KHT[:T, :, :], in_=PST[:T, 0:512].rearrange("p (h v) -> p h v", h=8)), reads=[PST], writes=[KHT])
        yield
        for hf in range(2):
            mcol = blk[:, 127 * hf:127 * hf + 1]
            kb.op("dve", lambda e: e.scalar_tensor_tensor(out=BTm[hf][:, :, :T], in0=kka[:, :, :T], scalar=mcol, in1=eWi[:, :, :T],
                                                          op0=ALU.mult, op1=ALU.mult), reads=[kka, blk, eWi], writes=[BTm[hf]])
            kb.op("dve", lambda e: e.scalar_tensor_tensor(out=KTm[hf][:, :, :T], in0=kp[:, :, :T], scalar=mcol, in1=eWi[:, :, :T],
                                                          op0=ALU.mult, op1=ALU.mult), reads=[kp, blk, eWi], writes=[KTm[hf]])
            kb.op("dve", lambda e: e.scalar_tensor_tensor(out=ATm[hf][:, :, :T], in0=kkr[:, :, :T], scalar=nblk[:, hf:hf + 1], in1=eWp[:, :, :T],
                                                          op0=ALU.mult, op1=ALU.mult), reads=[kkr, nblk, eWp], writes=[ATm[hf]])
        X, XT = Xa[0], XTa[0]
        sub = su[:T, :T].unsqueeze(1).to_broadcast([T, 2, T])
        ueb = ue[:T, :T].unsqueeze(1).to_broadcast([T, 2, T])
        slb = sl[:T, :T].unsqueeze(1).to_broadcast([T, 4, T])
        yield
        for c in range(4):
            yield
            pNA = nB(); pKA = nB()
            for hf in range(2):
                for j in range(2):
                    kb.op("pe", lambda e: e.matmul(pNA[:T, (hf * 2 + j) * T:(hf * 2 + j + 1) * T], BTm[hf][:, c, :T], AR[:, c, j, :T], start=True, stop=True),
                          reads=[BTm[hf], AR], writes=[pNA], inc=(hf == 1 and j == 1))
            for hf in range(2):
                for j in range(2):
                    kb.op("pe", lambda e: e.matmul(pKA[:T, (hf * 2 + j) * T:(hf * 2 + j + 1) * T], KTm[hf][:, c, :T], AR[:, c, j, :T], start=True, stop=True),
                          reads=[KTm[hf], AR], writes=[pKA], inc=(hf == 1 and j == 1))
            na4 = pNA[:T, 0:4 * T].rearrange("p (h j t) -> p h j t", h=2, j=2)
            ka4 = pKA[:T, 0:4 * T].rearrange("p (h j t) -> p h j t", h=2, j=2)
            kb.op("dve", lambda e: e.tensor_tensor(out=X[:T, 2 * c:2 * c + 2, :T], in0=na4[:, :, 0, :], in1=sub, op=ALU.mult),
                  reads=[pNA, su], writes=[X])
            kb.op("dve", lambda e: e.tensor_tensor(out=ARB[:T, 2 * c:2 * c + 2, :T], in0=na4[:, :, 1, :], in1=ueb, op=ALU.mult),
                  reads=[pNA, ue], writes=[ARB])
            kb.op("dve", lambda e: e.tensor_tensor(out=AAK[:T, 2 * c:2 * c + 2, :T], in0=ka4[:, :, 0, :], in1=sub, op=ALU.mult),
                  reads=[pKA, su], writes=[AAK])
            kb.op("dve", lambda e: e.tensor_tensor(out=ARK[:T, 2 * c:2 * c + 2, :T], in0=ka4[:, :, 1, :], in1=ueb, op=ALU.mult),
                  reads=[pKA, ue], writes=[ARK])
        yield
        for half in range(2):
            pNb = nB()
            for j in range(4):
                h = half * 4 + j
                c, hf = h // 2, h % 2
                pl = slice(hf * 64, hf * 64 + 64)
                kb.op("pe", lambda e: e.matmul(pNb[:T, j * T:(j + 1) * T], ATm[hf][:, c, :T], BTm[hf][:, c, :T], start=True, stop=True),
                      reads=[ATm[hf], BTm[hf]], writes=[pNb], inc=(j == 3))
            kb.op("dve", lambda e: e.tensor_tensor(out=XT[:T, half * 4:half * 4 + 4, :T], in0=v3(pNb)[:T], in1=slb, op=ALU.mult),
                  reads=[pNb, sl], writes=[XT])
        kb.op("dve", lambda e: e.tensor_tensor(out=Pm[:T, :, :T], in0=X[:T, :, :T],
                                               in1=g.ident_f[:T, :T].unsqueeze(1).to_broadcast([T, 8, T]), op=ALU.add),
              reads=[X, g.ident_f], writes=[Pm])

    def gen_neumann(ti):
        r0, T = g.tiles[ti]
        HT = HTs[ti % 3]
        mixT = mixTs[ti % 2]
        GG = GGs[ti % 2]
        bonus = bonuss[ti % 2]
        def proj_fm(pbank, j, col):
            for kc in range(8):
                kb.op("pe", lambda e: e.matmul(pbank[:, j * T:(j + 1) * T], Win[:, kc, col:col + 128], hnT[:, kc, :T],
                                               start=(kc == 0), stop=(kc == 7)),
                      reads=[Win, hnT], writes=[pbank], inc=(kc == 7))

        def proj_tm(pbank, col, n, c0=0):
            for kc in range(8):
                kb.op("pe", lambda e: e.matmul(pbank[:T, c0:c0 + n], hnT[:, kc, :T], Win[:, kc, col:col + n],
                                               start=(kc == 0), stop=(kc == 7)),
                      reads=[hnT, Win], writes=[pbank], inc=(kc == 7))

        def v3(pbank, n=4):
            return pbank[:, 0:n * T].rearrange("p (c t) -> p c t", c=n)

        yield
        lv = 1
        cur = 0
        while lv * 2 < T:
            X, XT = Xa[cur], XTa[cur]
            Xn, XTn = Xa[1 - cur], XTa[1 - cur]
            for half in range(2):
                yield
                p1 = nB(); p2 = nB()
                for j in range(4):
                    h = half * 4 + j
                    kb.op("pe", lambda e: e.matmul(p1[:T, j * T:(j + 1) * T], XT[:T, h, :T], X[:T, h, :T], start=True, stop=True),
                          reads=[XT, X], writes=[p1], inc=(j == 3))
                for j in range(4):
                    h = half * 4 + j
                    kb.op("pe", lambda e: e.matmul(p2[:T, j * T:(j + 1) * T], X[:T, h, :T], XT[:T, h, :T], start=True, stop=True),
                          reads=[XT, X], writes=[p2], inc=(j == 3))
                kb.op("act", lambda e: e.copy(out=Xn[:T, half * 4:half * 4 + 4, :T], in_=v3(p1)[:T]), reads=[p1], writes=[Xn])
                kb.op("dve", lambda e: e.tensor_copy(out=XTn[:T, half * 4:half * 4 + 4, :T], in_=v3(p2)[:T]), reads=[p2], writes=[XTn])
            yield
            for half in range(2):
                p3 = nB()
                for j in range(4):
                    h = half * 4 + j
                    kb.op("pe", lambda e: e.matmul(p3[:T, j * T:(j + 1) * T], XTn[:T, h, :T], Pm[:T, h, :T], start=True, stop=True),
                          reads=[XTn, Pm], writes=[p3], inc=(j == 3))
                kb.op("dve", lambda e: e.tensor_tensor(out=Pm[:T, half * 4:half * 4 + 4, :T], in0=Pm[:T, half * 4:half * 4 + 4, :T],
                                                       in1=v3(p3)[:T], op=ALU.add), reads=[Pm, p3], writes=[Pm])
            cur = 1 - cur
            lv *= 2

    def tail1(ti):
        r0, T = g.tiles[ti]
        HT = HTs[ti % 3]
        mixT = mixTs[ti % 2]
        GG = GGs[ti % 2]
        bonus = bonuss[ti % 2]
        def proj_fm(pbank, j, col):
            for kc in range(8):
                kb.op("pe", lambda e: e.matmul(pbank[:, j * T:(j + 1) * T], Win[:, kc, col:col + 128], hnT[:, kc, :T],
                                               start=(kc == 0), stop=(kc == 7)),
                      reads=[Win, hnT], writes=[pbank], inc=(kc == 7))

        def proj_tm(pbank, col, n, c0=0):
            for kc in range(8):
                kb.op("pe", lambda e: e.matmul(pbank[:T, c0:c0 + n], hnT[:, kc, :T], Win[:, kc, col:col + n],
                                               start=(kc == 0), stop=(kc == 7)),
                      reads=[hnT, Win], writes=[pbank], inc=(kc == 7))

        def v3(pbank, n=4):
            return pbank[:, 0:n * T].rearrange("p (c t) -> p c t", c=n)

        cur = ((T.bit_length() - 2) % 2) if T > 2 else 0
        pP1 = nB()
        for h in range(8):
            c, hf = h // 2, h % 2
            pl = slice(hf * 64, hf * 64 + 64)
            kb.op("pe", lambda e: e.matmul(pP1[:T, h * 64:(h + 1) * 64], ATm[hf][:, c, :T], STb[:, c, :], start=True, stop=False),
                  reads=[ATm[hf], STb], writes=[pP1], inc=False)
            kb.op("pe", lambda e: e.matmul(pP1[:T, h * 64:(h + 1) * 64], AAK[:T, h, :T], VTM[:T, h, :], start=False, stop=True),
                  reads=[AAK, VTM], writes=[pP1], inc=(h == 7))
        kb.op("act", lambda e: e.copy(out=P1[:T, :], in_=pP1[:T, :]), reads=[pP1], writes=[P1])
        pU = nB()
        for h in range(8):
            kb.op("pe", lambda e: e.matmul(pU[:T, h * 64:(h + 1) * 64], Pm[:T, h, :T], P1[:T, h * 64:(h + 1) * 64], start=True, stop=True),
                  reads=[Pm, P1], writes=[pU], inc=(h == 7))
        kb.op("act", lambda e: e.copy(out=UTM[:T, :, :], in_=pU[:T, :].rearrange("p (h v) -> p h v", h=8)), reads=[pU], writes=[UTM])
        pO = nB()
        for c in range(4):
            kb.op("pe", lambda e: e.matmul(pO[:, c * T:(c + 1) * T], STbd[:, c, :], AR[:, c, 1, :T], start=True, stop=False),
                  reads=[STbd, AR], writes=[pO], inc=False)
            for hf in range(2):
                h = 2 * c + hf
                pl = slice(hf * 64, hf * 64 + 64)
                kb.op("pe", lambda e: e.matmul(pO[pl, c * T:(c + 1) * T], UTM[:T, h, :], ARB[:T, h, :T], start=False, stop=False),
                      reads=[UTM, ARB], writes=[pO], inc=False)
                kb.op("pe", lambda e: e.matmul(pO[pl, c * T:(c + 1) * T], VTM[:T, h, :], ARK[:T, h, :T], start=False, stop=True),
                      reads=[VTM, ARK], writes=[pO], inc=(hf == 1))
        kb.op("act", lambda e: e.copy(out=Of[:, :, :T], in_=v3(pO)), reads=[pO], writes=[Of])
        pS = nB()
        for h in range(8):
            c, hf = h // 2, h % 2
            pl = slice(hf * 64, hf * 64 + 64)
            kb.op("pe", lambda e: e.matmul(pS[pl, c * 64:(c + 1) * 64], BHT[:T, h, :], UTM[:T, h, :], start=True, stop=False),
                  reads=[BHT, UTM], writes=[pS], inc=False)
            kb.op("pe", lambda e: e.matmul(pS[pl, c * 64:(c + 1) * 64], KHT[:T, h, :], VTM[:T, h, :], start=False, stop=True),
                  reads=[KHT, VTM], writes=[pS], inc=(h == 7))
        for c in range(4):
            kb.op("dve", lambda e: e.scalar_tensor_tensor(out=ST[:, c, :], in0=ST[:, c, :], scalar=eW[:, c, T - 1:T],
                                                          in1=pS[:, c * 64:(c + 1) * 64], op0=ALU.mult, op1=ALU.add),
                  reads=[ST, eW, pS], writes=[ST])
        kb.op("act", lambda e: e.copy(out=STb[:, :, :], in_=ST[:, :, :]), reads=[ST], writes=[STb])
        kb.op("act", lambda e: e.copy(out=STbd[0:64, :, 0:64], in_=ST[0:64, :, :]), reads=[ST], writes=[STbd])
        kb.op("act", lambda e: e.copy(out=STbd[64:128, :, 64:128], in_=ST[64:128, :, :]), reads=[ST], writes=[STbd])

    def gen_tail2(ti):
        r0, T = g.tiles[ti]
        HT = HTs[ti % 3]
        mixT = mixTs[ti % 2]
        GG = GGs[ti % 2]
        bonus = bonuss[ti % 2]
        def proj_fm(pbank, j, col):
            for kc in range(8):
                kb.op("pe", lambda e: e.matmul(pbank[:, j * T:(j + 1) * T], Win[:, kc, col:col + 128], hnT[:, kc, :T],
                                               start=(kc == 0), stop=(kc == 7)),
                      reads=[Win, hnT], writes=[pbank], inc=(kc == 7))

        def proj_tm(pbank, col, n, c0=0):
            for kc in range(8):
                kb.op("pe", lambda e: e.matmul(pbank[:T, c0:c0 + n], hnT[:, kc, :T], Win[:, kc, col:col + n],
                                               start=(kc == 0), stop=(kc == 7)),
                      reads=[hnT, Win], writes=[pbank], inc=(kc == 7))

        def v3(pbank, n=4):
            return pbank[:, 0:n * T].rearrange("p (c t) -> p c t", c=n)

        cur = ((T.bit_length() - 2) % 2) if T > 2 else 0
        kb.op("act", lambda e: e.activation(out=Osq[:, :, :T], in_=Of[:, :, :T], func=AF.Square), reads=[Of], writes=[Osq])
        yield
        pm_ = nB(); pq_ = nB()
        for c in range(4):
            kb.op("pe", lambda e: e.matmul(pm_[:, c * T:(c + 1) * T], blk64[:, :], Of[:, c, :T], start=True, stop=True),
                  reads=[blk64, Of], writes=[pm_], inc=(c == 3))
        for c in range(4):
            kb.op("pe", lambda e: e.matmul(pq_[:, c * T:(c + 1) * T], blk64[:, :], Osq[:, c, :T], start=True, stop=True),
                  reads=[blk64, Osq], writes=[pq_], inc=(c == 3))
        kb.op("act", lambda e: e.copy(out=mean_s[:, :, :T], in_=v3(pm_)), reads=[pm_], writes=[mean_s])
        yield
        kb.op("dve", lambda e: e.scalar_tensor_tensor(out=var[:, :, :T], in0=mean_s[:, :, :T], scalar=-1.0, in1=mean_s[:, :, :T],
                                                      op0=ALU.mult, op1=ALU.mult), reads=[mean_s], writes=[var])
        kb.op("dve", lambda e: e.tensor_tensor(out=var[:, :, :T], in0=var[:, :, :T], in1=v3(pq_), op=ALU.add),
              reads=[var, pq_], writes=[var])
        yield
        kb.op("dve", lambda e: e.tensor_scalar(out=var[:, :, :T], in0=var[:, :, :T], scalar1=0.0, scalar2=None, op0=ALU.max),
              reads=[var], writes=[var])
        kb.op("act", lambda e: e.activation(out=var[:, :, :T], in_=var[:, :, :T], func=AF.Sqrt, bias=64e-5), reads=[var], writes=[var])
        yield
        kb.op("dve", lambda e: e.reciprocal(out=var[:, :, :T], in_=var[:, :, :T]), reads=[var], writes=[var])
        kb.op("dve", lambda e: e.tensor_tensor(out=Of[:, :, :T], in0=Of[:, :, :T], in1=mean_s[:, :, :T], op=ALU.subtract),
              reads=[Of, mean_s], writes=[Of])
        yield
        kb.op("dve", lambda e: e.tensor_tensor(out=Of[:, :, :T], in0=Of[:, :, :T], in1=var[:, :, :T], op=ALU.mult),
              reads=[Of, var], writes=[Of])
        kb.op("dve", lambda e: e.tensor_tensor(out=Of[:, :, :T], in0=Of[:, :, :T], in1=b3(LNW, T), op=ALU.mult),
              reads=[Of, LNW], writes=[Of])
        yield
        kb.op("dve", lambda e: e.tensor_tensor(out=Of[:, :, :T], in0=Of[:, :, :T], in1=b3(LNB, T), op=ALU.add),
              reads=[Of, LNB], writes=[Of])
        kb.op("dve", lambda e: e.tensor_tensor(out=Of[:, :, :T], in0=Of[:, :, :T], in1=bonus[:, :, :T], op=ALU.add),
              reads=[Of, bonus], writes=[Of])
        yield
        kb.op("dve", lambda e: e.tensor_tensor(out=mixT[:, 4:8, :T], in0=Of[:, :, :T], in1=GG[:, :, :T], op=ALU.mult),
              reads=[Of, GG], writes=[mixT])
        for nb in range(2):
            pp = nB()
            for c in range(8):
                kb.op("pe", lambda e: e.matmul(pp[:T, :], mixT[:, c, :T], Wout[:, c, nb * 512:(nb + 1) * 512],
                                               start=(c == 0), stop=(c == 7)), reads=[mixT, Wout], writes=[pp], inc=(c == 7))
            kb.op("dve", lambda e: e.tensor_tensor(out=HT[:T, nb * 512:(nb + 1) * 512], in0=HT[:T, nb * 512:(nb + 1) * 512],
                                                   in1=pp[:T, :], op=ALU.add), reads=[HT, pp], writes=[HT])
        yield
        store_h(g, dst, ti, HT, final, None, (ss, rstd, junk))


    def run(gen):
        for _ in gen:
            pass

    def interleave(a, b, ra=1, rb=1):
        da = db = False
        while not (da and db):
            for _ in range(ra):
                if not da:
                    try:
                        next(a)
                    except StopIteration:
                        da = True
            for _ in range(rb):
                if not db:
                    try:
                        next(b)
                    except StopIteration:
                        db = True

    RA = int(os.environ.get('RA', '2')); RB = int(os.environ.get('RB', '1'))

    def interleave_n(gens):
        gens = [x for x in gens if x is not None]
        alive = [True] * len(gens)
        while any(alive):
            for k_, gg_ in enumerate(gens):
                if alive[k_]:
                    try:
                        next(gg_)
                    except StopIteration:
                        alive[k_] = False

    ntl = len(g.tiles)
    for k_ in range(min(3, ntl)):
        load_h(g, src, k_, HTs[k_])
    rmsnorm_T(g, HTs[0], g.tiles[0][1], Gb, hn, hnT, ss, rstd, junk)
    run(gen_proj(0))
    run(gen_mlstm(0))
    if ntl > 1:
        rmsnorm_T(g, HTs[1], g.tiles[1][1], Gb, hn, hnT, ss, rstd, junk)
        interleave_n([gen_prep(0), gen_proj(1)])
    else:
        run(gen_prep(0))
    for ti in range(ntl):
        if ti + 1 < ntl:
            interleave(gen_neumann(ti), gen_mlstm(ti + 1), RA, RB)
        else:
            run(gen_neumann(ti))
        tail1(ti)
        if ti + 1 < ntl:
            if ti + 2 < ntl:
                rmsnorm_T(g, HTs[(ti + 2) % 3], g.tiles[ti + 2][1], Gb, hn, hnT, ss, rstd, junk)
            interleave_n([gen_prep(ti + 1), gen_proj(ti + 2) if ti + 2 < ntl else None, gen_tail2(ti)])
        else:
            run(gen_tail2(ti))
        if ti + 3 < ntl:
            load_h(g, src, ti + 3, HTs[ti % 3])


LG = [float(np.log(1.0 - 2.0 ** (-5.0 - h))) for h in range(4)]
TWO_PI = 6.283185307179586
CW1 = 6.28125
CW2 = TWO_PI - CW1


LG = [float(np.log(1.0 - 2.0 ** (-5.0 - h))) for h in range(4)]
TWO_PI = 6.283185307179586
CW1 = 6.28125
CW2 = TWO_PI - CW1


LG = [float(np.log(1.0 - 2.0 ** (-5.0 - h))) for h in range(4)]
TWO_PI = 6.283185307179586
CW1 = 6.28125
CW2 = TWO_PI - CW1


def phase_l1(g, src, dst, final):
    kb, nc, dr = g.kb, g.nc, g.dr
    Win = kb.sb([128, 8, 6144], BF16, "Win")
    Wout = kb.sb([128, 16, D], BF16, "Wout")
    with contextlib.ExitStack() as ses:
        old = kb.es
        kb.es = ses
        stg = [kb.sb([128, 1536], F32, f"stg{i}") for i in range(3)]
        load_weight_bf16(g, dr["o_w_in_p"], 0, D, 6144, Win, stg)
        load_weight_bf16(g, dr["o_w_out"], 0, 2048, D, Wout, stg)
        kb.barrier()
        kb.es = old
    Gb = kb.sb([128, D], BF16, "Gb")
    Gfin = None
    iota = kb.sb([128, 128], F32, "iota")
    pidx = kb.sb([128, 1], F32, "pidx")
    ue = kb.sb([128, 128], F32, "ue")
    inv = kb.sb([128, 1], F32, "inv")
    kb.dma(iota[:, :], dr["c_iota"].ap()[:, :], writes=[iota], sem_buf=iota)
    kb.dma(pidx[:, :], dr["c_pidx"].ap()[:, :], writes=[pidx], sem_buf=pidx)
    kb.dma(ue[:, :], dr["c_ue"].ap()[:, :], writes=[ue], sem_buf=ue)
    kb.dma(inv[:, :], dr["c_inv"].ap()[:, :], writes=[inv], sem_buf=inv)
    DM = kb.sb([128, 4, 128], F32, "DM")
    DEC = kb.sb([128, 4, 128], F32, "DEC")
    KDEC = {128: kb.sb([128, 4], F32, "KDEC128"), 16: kb.sb([128, 4], F32, "KDEC16")}
    tms = kb.sb([128, 128], F32, "tms")
    kb.op("dve", lambda e: e.tensor_scalar(out=tms[:, :], in0=iota[:, :], scalar1=pidx[:, 0:1], scalar2=0.0,
                                           op0=ALU.subtract, op1=ALU.max), reads=[iota, pidx], writes=[tms])
    for h in range(4):
        kb.op("act", lambda e: e.activation(out=DM[:, h, :], in_=tms[:, :], func=AF.Exp, scale=LG[h]),
              reads=[tms], writes=[DM])
        kb.op("dve", lambda e: e.scalar_tensor_tensor(out=DM[:, h, :], in0=DM[:, h, :], scalar=1.0 / 16.0,
                                                      in1=ue[:, :], op0=ALU.mult, op1=ALU.mult),
              reads=[DM, ue], writes=[DM])
        kb.op("act", lambda e: e.activation(out=DEC[:, h, :], in_=iota[:, :], func=AF.Exp, scale=LG[h], bias=LG[h]),
              reads=[iota], writes=[DEC])
        for TT in (128, 16):
            kd = KDEC[TT]
            kb.op("act", lambda e: e.activation(out=kd[:, h:h + 1], in_=pidx[:, 0:1], func=AF.Exp, scale=-LG[h],
                                                bias=LG[h] * (TT - 1)), reads=[pidx], writes=[kd])
            kb.op("dve", lambda e: e.tensor_scalar(out=kd[:, h:h + 1], in0=kd[:, h:h + 1], scalar1=1.0 / 16.0,
                                                   scalar2=None, op0=ALU.mult), reads=[kd], writes=[kd])
    Sr = kb.sb([128, 8, 512], F32, "Sr")
    Srb = kb.sb([128, 8, 512], BF16, "Srb")
    kb.op("dve", lambda e: e.memset(Sr[:, :, :], 0.0), writes=[Sr])
    kb.op("pool", lambda e: e.memset(Srb[:, :, :], 0.0), writes=[Srb])
    HTs = [kb.sb([128, D], F32, f"HT{i}") for i in range(2)]
    hn = kb.sb([128, D], BF16, "hn")
    hnT = kb.sb([128, 8, 128], BF16, "hnT")
    sss = [kb.sb([128, 1], F32, f"ss{i}") for i in range(2)]
    rstds = [kb.sb([128, 1], F32, f"rstd{i}") for i in range(2)]
    ang = kb.sb([128, 128], F32, "ang")
    ang2 = kb.sb([128, 128], F32, "ang2")
    kf = kb.sb([128, 128], F32, "kf")
    ki = kb.sb([128, 128], I32, "ki")
    nsins = [kb.sb([128, 128], F32, f"nsin{i}") for i in range(2)]
    ncoss = [kb.sb([128, 128], F32, f"ncos{i}") for i in range(2)]
    t1 = kb.sb([128, 4, 128], F32, "t1")
    t2 = kb.sb([128, 4, 128], F32, "t2")
    qb = kb.sb([128, 2, 4, 128], BF16, "qb")
    qdb = kb.sb([128, 2, 4, 128], BF16, "qdb")
    kbf = kb.sb([128, 2, 4, 128], BF16, "kbf")
    kdT = kb.sb([128, 8, 128], BF16, "kdT")
    sTm = kb.sb([128, 4, 128], BF16, "sTm")
    VT = kb.sb([128, 2048], BF16, "VT")
    GS = kb.sb([128, 2048], BF16, "GS")
    og = kb.sb([128, 2048], BF16, "og")
    ogT = kb.sb([128, 16, 128], BF16, "ogT")
    st6 = kb.sb([128, 6], F32, "st6")
    mv = kb.sb([128, 2], F32, "mv")
    rs = kb.sb([128, 1], F32, "rs")
    junk = og
    kb.dma(t1[:, :, :].rearrange("p a b -> p (a b)"), bc_rows(dr["norm_mix"], 1, 512), writes=[t1], sem_buf=t1)
    kb.op("dve", lambda e: e.tensor_copy(out=Gb[:, 0:512], in_=t1[:, :, :].rearrange("p a b -> p (a b)")), reads=[t1], writes=[Gb])
    kb.dma(t2[:, :, :].rearrange("p a b -> p (a b)"), bc_rows(dr["norm_mix"], 1, 512, col0=512), writes=[t2], sem_buf=t2)
    kb.op("dve", lambda e: e.tensor_copy(out=Gb[:, 512:1024], in_=t2[:, :, :].rearrange("p a b -> p (a b)")), reads=[t2], writes=[Gb])

    def sincos(dst_tbl, shift, pos0, T):
        kb.op("dve", lambda e: e.tensor_scalar(out=ang[:, :T], in0=iota[:, :T], scalar1=float(pos0), scalar2=inv[:, 0:1],
                                               op0=ALU.add, op1=ALU.mult), reads=[iota, inv], writes=[ang])
        if shift != 0.0:
            kb.op("dve", lambda e: e.tensor_scalar(out=ang[:, :T], in0=ang[:, :T], scalar1=shift, scalar2=None,
                                                   op0=ALU.add), reads=[ang], writes=[ang])
        kb.op("dve", lambda e: e.tensor_scalar(out=ki[:, :T], in0=ang[:, :T], scalar1=1.0 / TWO_PI, scalar2=None,
                                               op0=ALU.mult), reads=[ang], writes=[ki])
        kb.op("dve", lambda e: e.tensor_copy(out=kf[:, :T], in_=ki[:, :T]), reads=[ki], writes=[kf])
        kb.op("dve", lambda e: e.scalar_tensor_tensor(out=ang2[:, :T], in0=kf[:, :T], scalar=-CW1, in1=ang[:, :T],
                                                      op0=ALU.mult, op1=ALU.add), reads=[kf, ang], writes=[ang2])
        kb.op("dve", lambda e: e.scalar_tensor_tensor(out=ang2[:, :T], in0=kf[:, :T], scalar=-CW2, in1=ang2[:, :T],
                                                      op0=ALU.mult, op1=ALU.add), reads=[kf, ang2], writes=[ang2])
        kb.op("dve", lambda e: e.tensor_scalar(out=ang2[:, :T], in0=ang2[:, :T], scalar1=3.1415925, scalar2=-3.1415925,
                                               op0=ALU.min, op1=ALU.max), reads=[ang2], writes=[ang2])
        kb.op("act", lambda e: e.activation(out=dst_tbl[:, :T], in_=ang2[:, :T], func=AF.Sin),
              reads=[ang2], writes=[dst_tbl])

    ntl = len(g.tiles)
    load_h(g, src, 0, HTs[0])
    T0 = g.tiles[0][1]
    norm_stats(g, HTs[0], T0, Gb, hn, sss[0], rstds[0], junk)
    if ntl > 1:
        load_h(g, src, 1, HTs[1])
    norm_transpose(g, hn, hnT, T0)
    sincos(nsins[0], 0.0, g.tiles[0][0], T0)
    sincos(ncoss[0], np.pi / 2, g.tiles[0][0], T0)
    PST = g.PST
    for ti, (r0, T) in enumerate(g.tiles):
        HT = HTs[ti % 2]
        HO = HT
        nsin = nsins[ti % 2]
        ncos = ncoss[ti % 2]
        sb_ = nsin[:, :T].unsqueeze(1).to_broadcast([128, 4, T])
        cb_ = ncos[:, :T].unsqueeze(1).to_broadcast([128, 4, T])
        qk_banks = []
        for which in range(2):
            pe_ = next_ps(g)
            po_ = next_ps(g)
            qk_banks.append((pe_, po_))
            for eo, pb in ((0, pe_), (1, po_)):
                for h in range(4):
                    col = which * 1024 + h * 256 + eo * 128
                    for kc in range(8):
                        kb.op("pe", lambda e: e.matmul(pb[:, h * T:(h + 1) * T], Win[:, kc, col:col + 128],
                                                       hnT[:, kc, :T], start=(kc == 0), stop=(kc == 7)),
                              reads=[Win, hnT], writes=[pb], inc=(kc == 7))
        for which in range(2):
            pe_, po_ = qk_banks[which]
            pe3 = pe_[:, 0:4 * T].rearrange("p (h t) -> p h t", h=4)
            po3 = po_[:, 0:4 * T].rearrange("p (h t) -> p h t", h=4)
            dstb = qb if which == 0 else kbf
            kb.op("dve", lambda e: e.tensor_tensor(out=t1[:, :, :T], in0=pe3, in1=cb_, op=ALU.mult),
                  reads=[pe_, ncos], writes=[t1])
            kb.op("dve", lambda e: e.tensor_tensor(out=t2[:, :, :T], in0=po3, in1=sb_, op=ALU.mult),
                  reads=[po_, nsin], writes=[t2])
            kb.op("dve", lambda e: e.tensor_tensor(out=dstb[:, 0, :, :T], in0=t1[:, :, :T], in1=t2[:, :, :T],
                                                   op=ALU.subtract), reads=[t1, t2], writes=[dstb])
            kb.op("dve", lambda e: e.tensor_tensor(out=t1[:, :, :T], in0=po3, in1=cb_, op=ALU.mult),
                  reads=[po_, ncos], writes=[t1])
            kb.op("dve", lambda e: e.tensor_tensor(out=t2[:, :, :T], in0=pe3, in1=sb_, op=ALU.mult),
                  reads=[pe_, nsin], writes=[t2])
            kb.op("dve", lambda e: e.tensor_tensor(out=dstb[:, 1, :, :T], in0=t1[:, :, :T], in1=t2[:, :, :T],
                                                   op=ALU.add), reads=[t1, t2], writes=[dstb])
            if which == 0:
                for eo in range(2):
                    kb.op("pool", lambda e: e.tensor_tensor(out=qdb[:, eo, :, :T], in0=qb[:, eo, :, :T],
                                                            in1=DEC[:, :, :T], op=ALU.mult),
                          reads=[qb, DEC], writes=[qdb])
        for nb in range(4):
            pvv = next_ps(g)
            for kc in range(8):
                kb.op("pe", lambda e: e.matmul(pvv[:T, :], hnT[:, kc, :T], Win[:, kc, 2048 + nb * 512:2048 + (nb + 1) * 512],
                                               start=(kc == 0), stop=(kc == 7)), reads=[hnT, Win], writes=[pvv], inc=(kc == 7))
            kb.op("act", lambda e: e.copy(out=VT[:T, nb * 512:(nb + 1) * 512], in_=pvv[:T, :]), reads=[pvv], writes=[VT])
        for nb in range(4):
            pgg = next_ps(g)
            for kc in range(8):
                kb.op("pe", lambda e: e.matmul(pgg[:T, :], hnT[:, kc, :T], Win[:, kc, 4096 + nb * 512:4096 + (nb + 1) * 512],
                                               start=(kc == 0), stop=(kc == 7)), reads=[hnT, Win], writes=[pgg], inc=(kc == 7))
            kb.op("act", lambda e: e.activation(out=GS[:T, nb * 512:(nb + 1) * 512], in_=pgg[:T, :], func=AF.Silu),
                  reads=[pgg], writes=[GS])
        if ti + 1 < ntl:
            r0n, Tn = g.tiles[ti + 1]
            sincos(nsins[(ti + 1) % 2], 0.0, r0n, Tn)
            sincos(ncoss[(ti + 1) % 2], np.pi / 2, r0n, Tn)
        for h in range(4):
            for eo in range(2):
                j = h * 2 + eo
                kb.op("pe", lambda e: e.transpose(out=PST[:T, j * 128:(j + 1) * 128], in_=kbf[:, eo, h, :T],
                                                  identity=g.ident_b[:, :]),
                      reads=[kbf, g.ident_b], writes=[PST], inc=(j == 7))
        for h in range(4):
            kb.op("act", lambda e: e.activation(out=kdT[:T, 2 * h:2 * h + 2, :],
                                                in_=PST[:T, 2 * h * 128:(2 * h + 2) * 128].rearrange("p (j d) -> p j d", j=2),
                                                func=AF.Copy, scale=KDEC[T][:T, h:h + 1]),
                  reads=[PST, KDEC[T]], writes=[kdT])
        psc = next_ps(g)
        for h in range(4):
            for eo in range(2):
                kb.op("pe", lambda e: e.matmul(psc[:T, h * T:(h + 1) * T], kbf[:, eo, h, :T], qb[:, eo, h, :T],
                                               start=(eo == 0), stop=(eo == 1)),
                      reads=[kbf, qb], writes=[psc], inc=(eo == 1))
        kb.op("dve", lambda e: e.tensor_tensor(out=sTm[:T, :, :T],
                                               in0=psc[:T, 0:4 * T].rearrange("p (h t) -> p h t", h=4),
                                               in1=DM[:T, :, :T], op=ALU.mult), reads=[psc, DM], writes=[sTm])
        for h in range(4):
            po = next_ps(g)
            kb.op("pe", lambda e: e.matmul(po[:T, :], sTm[:T, h, :T], VT[:T, h * 512:(h + 1) * 512], start=True, stop=False),
                  reads=[sTm, VT], writes=[po], inc=False)
            for eo in range(2):
                kb.op("pe", lambda e: e.matmul(po[:T, :], qdb[:, eo, h, :T], Srb[:, 2 * h + eo, :], start=False, stop=(eo == 1)),
                      reads=[qdb, Srb], writes=[po], inc=(eo == 1))
            kb.op("dve", lambda e: e.bn_stats(out=st6[:T, :], in_=po[:T, :]), reads=[po], writes=[st6])
            kb.op("dve", lambda e: e.bn_aggr(out=mv[:T, :], in_=st6[:T, :]), reads=[st6], writes=[mv])
            kb.op("act", lambda e: e.activation(out=rs[:T, :], in_=mv[:T, 1:2], func=AF.Sqrt, scale=1.0, bias=1e-6),
                  reads=[mv], writes=[rs])
            kb.op("dve", lambda e: e.reciprocal(out=rs[:T, :], in_=rs[:T, :]), reads=[rs], writes=[rs])
            kb.op("dve", lambda e: e.tensor_scalar(out=og[:T, h * 512:(h + 1) * 512], in0=po[:T, :], scalar1=mv[:T, 0:1], scalar2=rs[:T, 0:1],
                                                   op0=ALU.subtract, op1=ALU.mult), reads=[po, mv, rs], writes=[og])
            kb.op("pool", lambda e: e.tensor_tensor(out=og[:T, h * 512:(h + 1) * 512], in0=og[:T, h * 512:(h + 1) * 512],
                                                    in1=GS[:T, h * 512:(h + 1) * 512], op=ALU.mult),
                  reads=[og, GS], writes=[og])
        gT = [float(np.exp(LG[h] * T)) for h in range(4)]
        for h in range(4):
            for eo in range(2):
                j = 2 * h + eo
                pst_ = next_ps(g)
                kb.op("pe", lambda e: e.matmul(pst_[:, :], kdT[:T, j, :], VT[:T, h * 512:(h + 1) * 512], start=True, stop=True),
                      reads=[kdT, VT], writes=[pst_])
                kb.op("dve", lambda e: e.scalar_tensor_tensor(out=Sr[:, j, :], in0=Sr[:, j, :], scalar=gT[h], in1=pst_[:, :],
                                                              op0=ALU.mult, op1=ALU.add), reads=[Sr, pst_], writes=[Sr])
                kb.op("act", lambda e: e.copy(out=Srb[:, j, :], in_=Sr[:, j, :]), reads=[Sr], writes=[Srb])
        for half in range(2):
            for j in range(8):
                c = half * 8 + j
                kb.op("pe", lambda e: e.transpose(out=PST[:, j * T:(j + 1) * T], in_=og[:T, c * 128:(c + 1) * 128],
                                                  identity=g.ident_b[:T, :T]),
                      reads=[og, g.ident_b], writes=[PST], inc=(j == 7))
            kb.op("act", lambda e: e.copy(out=ogT[:, half * 8:(half + 1) * 8, :T],
                                          in_=PST[:, 0:8 * T].rearrange("p (k t) -> p k t", k=8)),
                  reads=[PST], writes=[ogT])
        if ti + 1 < ntl:
            Tn = g.tiles[ti + 1][1]
            norm_stats(g, HTs[(ti + 1) % 2], Tn, Gb, hn, sss[(ti + 1) % 2], rstds[(ti + 1) % 2], junk)
        pps = []
        for nb in range(2):
            pp = next_ps(g)
            pps.append(pp)
            for c in range(16):
                kb.op("pe", lambda e: e.matmul(pp[:T, :], ogT[:, c, :T], Wout[:, c, nb * 512:(nb + 1) * 512],
                                               start=(c == 0), stop=(c == 15)), reads=[ogT, Wout], writes=[pp], inc=(c == 15))
        if ti + 1 < ntl:
            norm_transpose(g, hn, hnT, g.tiles[ti + 1][1])
        for nb in range(2):
            kb.op("dve", lambda e: e.tensor_tensor(out=HO[:T, nb * 512:(nb + 1) * 512],
                                                   in0=HT[:T, nb * 512:(nb + 1) * 512], in1=pps[nb][:T, :], op=ALU.add),
                  reads=[HT, pps[nb]], writes=[HO])
        store_h(g, dst, ti, HO, final, Gfin, (sss[ti % 2], rstds[ti % 2], junk))
        if ti + 2 < ntl:
            load_h(g, src, ti + 2, HTs[ti % 2])


def make_in_map(inputs, b, NT):
    m = {"x": np.ascontiguousarray(inputs["x"][b, :128 * NT])}
    for k, shp in W_SPECS.items():
        src_k = "o_w_in" if k == "o_w_in_p" else k
        m[k] = np.ascontiguousarray(np.asarray(inputs[src_k], np.float32).reshape(shp))
    m.update(host_consts())
    perm = np.arange(6144)
    for sec in range(2):
        for h in range(4):
            base = sec * 1024 + h * 256
            perm[base:base + 256] = np.concatenate([base + np.arange(0, 256, 2), base + np.arange(1, 256, 2)])
    m["o_w_in_p"] = np.ascontiguousarray(m["o_w_in_p"][:, perm])
    return m


NT_FULL = 32


def kernel(**inputs):
    nc = build(NT_FULL, phases=(1, 2, 3, 4), debug=False, final=True)
    in_maps = [make_in_map(inputs, b, NT_FULL) for b in range(8)]
    res = run_bass_kernel_spmd(nc, in_maps, core_ids=list(range(8)))
    return np.stack([np.asarray(r["out"], np.float32) for r in res.results], axis=0)
```

```python
import contextlib
import numpy as np
import concourse.bass as bass
import concourse.mybir as mybir

F32 = mybir.dt.float32
BF16 = mybir.dt.bfloat16
I32 = mybir.dt.int32
AF = mybir.ActivationFunctionType
ALU = mybir.AluOpType
AX = mybir.AxisListType


class Buf:
    __slots__ = ("t", "w", "r", "dsem", "dcount", "name", "excl")

    def __init__(self, t, name=""):
        self.t = t
        self.w = {}
        self.r = {}
        self.dsem = None
        self.dcount = 0
        self.name = name
        self.excl = False

    def __getitem__(self, idx):
        return self.t[idx]


class _Cap:
    def __init__(self):
        self.call = None

    def __getattr__(self, name):
        def f(*args, **kwargs):
            self.call = (name, args, kwargs)
            return None
        return f


class Eng:
    def __init__(self, name, obj, sem):
        self.name = name
        self.obj = obj
        self.sem = sem
        self.count = 0
        self.seen = {}


class KB:
    def __init__(self, nc, es):
        self.nc = nc
        self.es = es
        self.sems = {}
        self.E = {}
        for name, obj in (("pe", nc.tensor), ("act", nc.scalar), ("dve", nc.vector),
                          ("pool", nc.gpsimd), ("sp", nc.sync)):
            sem = es.enter_context(nc.semaphore("s_" + name))
            self.E[name] = Eng(name, obj, sem)
            self.sems[id(sem)] = sem
        self.dma_tokens = {}
        self.nbuf = 0
        self.rec = None

    def sb(self, shape, dt, name=None):
        self.nbuf += 1
        name = f"{name or 'b'}_{self.nbuf}"
        t = self.es.enter_context(self.nc.sbuf_tensor(name, list(shape), dt))
        return Buf(t, name)

    def ps(self, shape, dt, name=None):
        self.nbuf += 1
        name = f"{name or 'p'}_{self.nbuf}"
        t = self.es.enter_context(self.nc.psum_tensor(name, list(shape), dt))
        b = Buf(t, name)
        b.excl = True
        return b

    def newsem(self, name):
        sem = self.es.enter_context(self.nc.semaphore(name))
        self.sems[id(sem)] = sem
        return sem

    def _wait(self, e, deps):
        for sid, val in deps.items():
            if e.seen.get(sid, 0) < val:
                e.obj.wait_ge(self.sems[sid], val)
                e.seen[sid] = val

    def _deps(self, e, reads, writes):
        deps = {}
        own = id(e.sem)
        for b in reads:
            for sid, v in b.w.items():
                if deps.get(sid, 0) < v:
                    deps[sid] = v
            if b.excl:
                for sid, v in b.r.items():
                    if sid != own and deps.get(sid, 0) < v:
                        deps[sid] = v
        skip_own = (e.name == "pe")
        for b in writes:
            for d in (b.w, b.r):
                for sid, v in d.items():
                    if sid == own and skip_own:
                        continue
                    if deps.get(sid, 0) < v:
                        deps[sid] = v
        return deps

    def op(self, eng, fn, reads=(), writes=(), inc=True):
        if self.rec is not None:
            import sys as _sys
            cap = _Cap()
            fn(cap)
            name, args, kwargs = cap.call
            fn2 = (lambda e_, name=name, args=args, kwargs=kwargs: getattr(e_, name)(*args, **kwargs))
            self.rec.append(("op", eng, fn2, tuple(reads), tuple(writes), inc, _sys._getframe(1).f_lineno))
            return None
        e = self.E[eng]
        self._wait(e, self._deps(e, reads, writes))
        ins = fn(e.obj)
        if inc:
            e.count += 1
            ins.then_inc(e.sem, 1)
            val = e.count
        else:
            val = e.count + 1
        sid = id(e.sem)
        for b in reads:
            if b.r.get(sid, 0) < val:
                b.r[sid] = val
        for b in writes:
            if b.w.get(sid, 0) < val:
                b.w[sid] = val
        return ins

    def dma(self, out_ap, in_ap, reads=(), writes=(), sem_buf=None, q="sp"):
        if self.rec is not None:
            import sys as _sys
            self.rec.append(("dma", q, (out_ap, in_ap, sem_buf), tuple(reads), tuple(writes), True, _sys._getframe(1).f_lineno))
            return None
        e = self.E[q]
        self._wait(e, self._deps(e, reads, writes))
        b = sem_buf
        if b.dsem is None:
            b.dsem = self.newsem("d_" + b.name)
        b.dcount += 16
        e.obj.dma_start(out=out_ap, in_=in_ap).then_inc(b.dsem, 16)
        sid = id(b.dsem)
        for x in reads:
            x.r[sid] = b.dcount
        for x in writes:
            x.w[sid] = b.dcount
        self.dma_tokens[sid] = b.dcount

    def barrier(self):
        targets = {id(e.sem): e.count for e in self.E.values() if e.count > 0}
        targets.update(self.dma_tokens)
        for e in self.E.values():
            self._wait(e, {k: v for k, v in targets.items() if k != id(e.sem)})

    def final_wait(self):
        e = self.E["sp"]
        self._wait(e, dict(self.dma_tokens))

    def record(self, fn):
        assert self.rec is None
        self.rec = []
        try:
            r = fn()
            if r is not None and hasattr(r, "__next__"):
                for _ in r:
                    pass
        finally:
            ops, self.rec = self.rec, None
        return ops

    def schedule(self, streams, cost_fn, sync_ns=120.0):
        streams = [list(x) for x in streams if x]
        n = len(streams)
        key = lambda b: id(b.w)
        rem_r = [dict() for _ in range(n)]
        rem_w = [dict() for _ in range(n)]
        for k, st in enumerate(streams):
            for o in st:
                for b in o[3]:
                    rem_r[k][key(b)] = rem_r[k].get(key(b), 0) + 1
                for b in o[4]:
                    rem_w[k][key(b)] = rem_w[k].get(key(b), 0) + 1
        pos = [0] * n
        eng_t = {}
        w_t = {}
        r_t = {}
        w_e = {}
        order = []
        total = sum(len(x) for x in streams)
        while len(order) < total:
            best = None
            for k in range(n):
                if pos[k] >= len(streams[k]):
                    continue
                o = streams[k][pos[k]]
                ok = True
                for j in range(k):
                    if pos[j] >= len(streams[j]):
                        continue
                    for b in o[4]:
                        kk_ = key(b)
                        if rem_r[j].get(kk_, 0) or rem_w[j].get(kk_, 0):
                            ok = False
                            break
                    if ok:
                        for b in o[3]:
                            if rem_w[j].get(key(b), 0):
                                ok = False
                                break
                    if not ok:
                        break
                if not ok:
                    continue
                eng = o[1]
                t = eng_t.get(eng, 0.0)
                for b in o[3]:
                    kk_ = key(b)
                    tw = w_t.get(kk_, 0.0) + (sync_ns if w_e.get(kk_) != eng else 0.0)
                    if tw > t:
                        t = tw
                for b in o[4]:
                    kk_ = key(b)
                    tw = max(w_t.get(kk_, 0.0), r_t.get(kk_, 0.0)) + sync_ns
                    if tw > t:
                        t = tw
                if best is None or t < best[0] - 1e-9:
                    best = (t, k, o)
            t, k, o = best
            dur = cost_fn(o)
            eng = o[1]
            end = t + dur
            eng_t[eng] = end if o[0] == "op" else t + 60.0
            for b in o[3]:
                kk_ = key(b)
                r_t[kk_] = max(r_t.get(kk_, 0.0), end)
                rem_r[k][kk_] -= 1
            for b in o[4]:
                kk_ = key(b)
                w_t[kk_] = end
                w_e[kk_] = eng
                rem_w[k][kk_] -= 1
            pos[k] += 1
            order.append(o)
        for o in order:
            if o[0] == "op":
                self.op(o[1], o[2], reads=o[3], writes=o[4], inc=o[5])
            else:
                out_ap, in_ap, sem_buf = o[2]
                self.dma(out_ap, in_ap, reads=o[3], writes=o[4], sem_buf=sem_buf, q=o[1])
        return max(eng_t.values()) if eng_t else 0.0


from concourse.bass_utils import run_bass_kernel_spmd

D = 1024
NMETA = 16
DFF = 2816
NFC = DFF // 128

W_SPECS = {
    "meta_tokens": (16, 1024), "norm_mix": (2, 1024), "norm_ffn": (2, 1024), "norm_final": (1, 1024),
    "e_w_in": (1024, 3848), "e_w_out": (1024, 1024), "m_b_i": (1, 4), "m_b_f": (1, 4), "m_norm": (1, 512),
    "r_mu": (1, 1792), "r_w0": (1, 512), "r_w2": (64, 512), "r_a0": (1, 512), "r_a2": (64, 512),
    "r_g2": (128, 512), "r_k_k": (1, 512), "r_k_a": (1, 512), "r_r_k": (1, 512), "r_ln_w": (1, 512),
    "r_ln_b": (1, 512), "o_w_in_p": (1024, 6144), "o_w_out": (2048, 1024), "f_w_up": (2048, 5632),
    "f_conv_w": (6, 2816), "f_conv_b": (2, 2816), "f_w_down": (5632, 1024),
}


def host_consts():
    c = {}
    c["c_ident"] = np.eye(128, dtype=np.float32)
    i = np.arange(128)
    c["c_ue"] = (i[:, None] <= i[None, :]).astype(np.float32)
    c["c_su"] = (i[:, None] < i[None, :]).astype(np.float32)
    c["c_iota"] = np.broadcast_to(np.arange(128, dtype=np.float32)[None, :], (128, 128)).copy()
    c["c_pidx"] = np.arange(128, dtype=np.float32)[:, None].copy()
    bo = np.zeros((128, 128), np.float32)
    bo[:64, :64] = 1.0
    bo[64:, 64:] = 1.0
    c["c_blk"] = bo
    c["c_inv"] = (np.float32(1.0) / np.power(np.float32(10000.0), np.linspace(0.0, 1.0, 128, dtype=np.float32))
                  ).astype(np.float32)[:, None].copy()
    return c


class Ctx:
    pass


def tile_rows(NT):
    tiles = [(0, NMETA)]
    for i in range(NT):
        tiles.append((NMETA + 128 * i, 128))
    return tiles


def build(NT, phases=(1, 2, 3, 4), debug=False, final=True):
    nc = bass.Bass("TRN2", target_bir_lowering=False)
    SEQ = 128 * NT
    L = NMETA + SEQ
    dr = {}
    dr["x"] = nc.dram_tensor("x", [SEQ, D], F32, kind="ExternalInput")
    for k, shp in W_SPECS.items():
        dr[k] = nc.dram_tensor(k, list(shp), F32, kind="ExternalInput")
    for k, v in host_consts().items():
        dr[k] = nc.dram_tensor(k, list(v.shape), F32, kind="ExternalInput")
    out = nc.dram_tensor("out", [SEQ, D], F32, kind="ExternalOutput")
    H = {}
    for i in (1, 2, 3):
        H[i] = nc.dram_tensor(f"H{i}", [L, D], F32, kind=("ExternalOutput" if debug else "Internal"))

    tiles = tile_rows(NT)
    es = contextlib.ExitStack()
    with es:
        kb = KB(nc, es)
        PS = [kb.ps([128, 512], F32, f"psb{i}") for i in range(7)]
        PST = kb.ps([128, 1024], BF16, "pstr")
        g = Ctx()
        g.nc, g.kb, g.dr, g.H, g.out, g.tiles, g.PS, g.PST = nc, kb, dr, H, out, tiles, PS, PST
        g.psi = 0
        g.ident_f = kb.sb([128, 128], F32, "ident_f")
        g.ident_b = kb.sb([128, 128], BF16, "ident_b")
        kb.dma(g.ident_f[:, :], dr["c_ident"].ap()[:, :], writes=[g.ident_f], sem_buf=g.ident_f)
        kb.op("dve", lambda e: e.tensor_copy(out=g.ident_b[:, :], in_=g.ident_f[:, :]),
              reads=[g.ident_f], writes=[g.ident_b])

        plist = [p for p in (1, 2, 3, 4) if p in phases]
        src = 0
        for p in plist:
            dst = p if p != plist[-1] else 4
            with contextlib.ExitStack() as pes:
                kb.es = pes
                if p in (2, 4):
                    phase_ffn(g, layer=(0 if p == 2 else 1), src=src, dst=dst, final=final)
                elif p == 1:
                    phase_l0(g, src=src, dst=dst, final=final)
                elif p == 3:
                    phase_l1(g, src=src, dst=dst, final=final)
                kb.barrier()
            kb.es = es
            src = dst
        kb.final_wait()
    return nc


def next_ps(g):
    b = g.PS[g.psi % len(g.PS)]
    g.psi += 1
    return b


def bc_rows(handle, row, n, parts=128, col0=0, ncols_total=None):
    ncols_total = ncols_total if ncols_total is not None else handle.shape[1]
    return bass.AP(handle, row * ncols_total + col0, [[0, parts], [1, n]])


def load_h(g, src, ti, HT):
    kb = g.kb
    r0, T = g.tiles[ti]
    if src == 0:
        if ti == 0:
            ap = g.dr["meta_tokens"].ap()[0:NMETA, :]
        else:
            ap = g.dr["x"].ap()[r0 - NMETA:r0 - NMETA + T, :]
    else:
        ap = g.H[src].ap()[r0:r0 + T, :]
    kb.dma(HT[:T, :], ap, writes=[HT], sem_buf=HT)


def store_h(g, dst, ti, HO, final, Gfin=None, scratch=None):
    kb = g.kb
    r0, T = g.tiles[ti]
    if dst != 4:
        kb.dma(g.H[dst].ap()[r0:r0 + T, :], HO[:T, :], reads=[HO], sem_buf=HO)
        return
    if ti == 0:
        return
    if final:
        ss, rstd, junk = scratch
        kb.op("act", lambda e: e.activation(out=junk[:T, 0:D], in_=HO[:T, :], func=AF.Square, accum_out=ss[:T, :]),
              reads=[HO], writes=[junk, ss])
        rstd_from_ss(kb, ss, rstd, T, 1.0 / D, 1e-6)
        kb.op("dve", lambda e: e.scalar_tensor_tensor(out=HO[:T, :], in0=HO[:T, :], scalar=rstd[:T, :],
                                                      in1=Gfin[:T, :], op0=ALU.mult, op1=ALU.mult),
              reads=[HO, rstd, Gfin], writes=[HO])
    kb.dma(g.out.ap()[r0 - NMETA:r0 - NMETA + T, :], HO[:T, :], reads=[HO], sem_buf=HO)


import os as _os
DMAQ_N = int(_os.environ.get("DMAQ_N", "1"))


def load_weight_bf16(g, dram_handle, row0, K, N, W, stg, col0=0, ncols_total=None):
    kb = g.kb
    SW = stg[0].t.shape[1]
    engs = ("dve", "act", "dve", "act", "dve", "act", "dve")
    cnt = getattr(g, "_lw_cnt", 0)
    for kc in range(K // 128):
        for j0 in range(0, N, SW):
            w = min(SW, N - j0)
            s = stg[cnt % len(stg)]
            kb.dma(s[:, :w], dram_handle.ap()[row0 + kc * 128: row0 + (kc + 1) * 128, col0 + j0: col0 + j0 + w],
                   writes=[s], sem_buf=s, q=(("sp", "pool", "act")[cnt % DMAQ_N] if DMAQ_N > 1 else "sp"))
            en = engs[cnt % len(engs)]
            if en == "act":
                kb.op("act", lambda e: e.copy(out=W[:, kc, j0:j0 + w], in_=s[:, :w]), reads=[s], writes=[W])
            else:
                kb.op(en, lambda e: e.tensor_copy(out=W[:, kc, j0:j0 + w], in_=s[:, :w]), reads=[s], writes=[W])
            cnt += 1
    g._lw_cnt = cnt


def rstd_from_ss(kb, ss, rstd, T, scale, eps, ap_fn=None):
    a = (lambda b: b[:T, :]) if ap_fn is None else ap_fn
    kb.op("act", lambda e: e.activation(out=a(rstd), in_=a(ss), func=AF.Sqrt, scale=scale, bias=eps),
          reads=[ss], writes=[rstd])
    kb.op("dve", lambda e: e.reciprocal(out=a(rstd), in_=a(rstd)), reads=[rstd], writes=[rstd])


def load_vec_fm(g, handle, row, nch, dstbuf, dst_ap, vtmp, col0=0):
    kb = g.kb
    ncols = handle.shape[1]
    src = bass.AP(handle, row * ncols + col0, [[128, nch], [1, 128]])
    kb.dma(vtmp[:nch, :], src, writes=[vtmp], sem_buf=vtmp)
    pt = next_ps(g)
    kb.op("pe", lambda e: e.transpose(out=pt[:, :nch], in_=vtmp[:nch, :], identity=g.ident_f[:nch, :nch]),
          reads=[vtmp, g.ident_f], writes=[pt])
    kb.op("dve", lambda e: e.tensor_copy(out=dst_ap, in_=pt[:, :nch]), reads=[pt], writes=[dstbuf])


def rmsnorm_T(g, HT, T, Gb, hn, hnT, ss, rstd, junk):
    kb = g.kb
    kb.op("act", lambda e: e.activation(out=junk[:T, 0:D], in_=HT[:T, :], func=AF.Square, accum_out=ss[:T, :]),
          reads=[HT], writes=[junk, ss])
    rstd_from_ss(kb, ss, rstd, T, 1.0 / D, 1e-6)
    kb.op("dve", lambda e: e.scalar_tensor_tensor(out=hn[:T, :], in0=HT[:T, :], scalar=rstd[:T, :],
                                                  in1=Gb[:T, :], op0=ALU.mult, op1=ALU.mult),
          reads=[HT, rstd, Gb], writes=[hn])
    PST = g.PST
    for kc in range(8):
        kb.op("pe", lambda e: e.transpose(out=PST[:, kc * T:(kc + 1) * T], in_=hn[:T, kc * 128:(kc + 1) * 128],
                                          identity=g.ident_b[:T, :T]),
              reads=[hn, g.ident_b], writes=[PST], inc=(kc == 7))
    kb.op("act", lambda e: e.copy(out=hnT[:, :, :T], in_=PST[:, 0:8 * T].rearrange("p (k t) -> p k t", k=8)),
          reads=[PST], writes=[hnT])


def norm_stats(g, HT, T, Gb, hn, ss, rstd, junk):
    kb = g.kb
    kb.op("act", lambda e: e.activation(out=junk[:T, 0:D], in_=HT[:T, :], func=AF.Square, accum_out=ss[:T, :]),
          reads=[HT], writes=[junk, ss])
    rstd_from_ss(kb, ss, rstd, T, 1.0 / D, 1e-6)
    kb.op("dve", lambda e: e.scalar_tensor_tensor(out=hn[:T, :], in0=HT[:T, :], scalar=rstd[:T, :],
                                                  in1=Gb[:T, :], op0=ALU.mult, op1=ALU.mult),
          reads=[HT, rstd, Gb], writes=[hn])


def norm_transpose(g, hn, hnT, T):
    kb = g.kb
    PST = g.PST
    for kc in range(8):
        kb.op("pe", lambda e: e.transpose(out=PST[:, kc * T:(kc + 1) * T], in_=hn[:T, kc * 128:(kc + 1) * 128],
                                          identity=g.ident_b[:T, :T]),
              reads=[hn, g.ident_b], writes=[PST], inc=(kc == 7))
    kb.op("act", lambda e: e.copy(out=hnT[:, :, :T], in_=PST[:, 0:8 * T].rearrange("p (k t) -> p k t", k=8)),
          reads=[PST], writes=[hnT])


def phase_ffn(g, layer, src, dst, final):
    kb, nc, dr = g.kb, g.nc, g.dr
    Wup = kb.sb([128, 8, 2 * DFF], BF16, "Wup")
    Wdn = kb.sb([128, NFC, D], BF16, "Wdn")
    with contextlib.ExitStack() as ses:
        old = kb.es
        kb.es = ses
        stg = [kb.sb([128, 1408], F32, f"stg{i}") for i in range(3)]
        load_weight_bf16(g, dr["f_w_up"], layer * D, D, 2 * DFF, Wup, stg)
        load_weight_bf16(g, dr["f_w_down"], layer * DFF, DFF, D, Wdn, stg)
        kb.barrier()
        kb.es = old
    Gb = kb.sb([128, D], F32, "Gb")
    kb.dma(Gb[:, :], bc_rows(dr["norm_ffn"], layer, D), writes=[Gb], sem_buf=Gb)
    Gfin = None
    if dst == 4 and final:
        Gfin = kb.sb([128, D], F32, "Gfin")
        kb.dma(Gfin[:, :], bc_rows(dr["norm_final"], 0, D), writes=[Gfin], sem_buf=Gfin)
    CW = kb.sb([128, 3, NFC], F32, "CW")
    CB = kb.sb([128, NFC], F32, "CB")
    vtmp = kb.sb([32, 128], F32, "vtmp")
    for j in range(3):
        load_vec_fm(g, dr["f_conv_w"], layer * 3 + j, NFC, CW, CW[:, j, :], vtmp)
    load_vec_fm(g, dr["f_conv_b"], layer, NFC, CB, CB[:, :], vtmp)
    HTs = [kb.sb([128, D], F32, f"HT{i}") for i in range(3)]
    hns = [kb.sb([128, D], BF16, f"hn{i}") for i in range(2)]
    hnTs = [kb.sb([128, 8, 128], BF16, f"hnT{i}") for i in range(2)]
    junk = kb.sb([128, D], BF16, "junk")
    sss = [kb.sb([128, 1], F32, f"ss{i}") for i in range(3)]
    rstds = [kb.sb([128, 1], F32, f"rstd{i}") for i in range(3)]
    G = kb.sb([128, NFC, 130], F32, "G")
    ACC = [kb.sb([128, 4, 128], F32, f"acc{i}") for i in range(2)]
    SIL = [kb.sb([128, 4, 128], F32, f"sil{i}") for i in range(2)]
    ACTT = kb.sb([128, NFC, 128], BF16, "ACTT")
    kb.op("dve", lambda e: e.memset(G[:, :, :], 0.0), writes=[G])
    po_banks = [g.PS[5], g.PS[6]]
    rot = g.PS[0:5]
    rot_i = [0]

    def next_rot():
        b = rot[rot_i[0] % len(rot)]
        rot_i[0] += 1
        return b

    ntl = len(g.tiles)
    load_h(g, src, 0, HTs[0])
    norm_stats(g, HTs[0], g.tiles[0][1], Gb, hns[0], sss[0], rstds[0], junk)
    if ntl > 1:
        load_h(g, src, 1, HTs[1])
    norm_transpose(g, hns[0], hnTs[0], g.tiles[0][1])

    def down_part(c_lo, c_hi, T):
        for c in range(c_lo, c_hi):
            for nb in range(2):
                po = po_banks[nb]
                kb.op("pe", lambda e: e.matmul(po[:T, :], ACTT[:, c, :T], Wdn[:, c, nb * 512:(nb + 1) * 512],
                                               start=(c == 0), stop=(c == NFC - 1)),
                      reads=[ACTT, Wdn], writes=[po], inc=(c == c_hi - 1))

    for ti, (r0, T) in enumerate(g.tiles):
        HT = HTs[ti % 3]
        hnT = hnTs[ti % 2]
        steps = list(range(0, NFC, 4))
        for si, c0 in enumerate(steps):
            nch = min(4, NFC - c0)
            pg = next_rot()
            pv = next_rot()
            for j in range(nch):
                for kc in range(8):
                    kb.op("pe", lambda e: e.matmul(pg[:, j * T:(j + 1) * T],
                                                   Wup[:, kc, DFF + (c0 + j) * 128: DFF + (c0 + j + 1) * 128],
                                                   hnT[:, kc, :T], start=(kc == 0), stop=(kc == 7)),
                          reads=[Wup, hnT], writes=[pg], inc=(kc == 7))
            for j in range(nch):
                for kc in range(8):
                    kb.op("pe", lambda e: e.matmul(pv[:, j * T:(j + 1) * T],
                                                   Wup[:, kc, (c0 + j) * 128:(c0 + j + 1) * 128],
                                                   hnT[:, kc, :T], start=(kc == 0), stop=(kc == 7)),
                          reads=[Wup, hnT], writes=[pv], inc=(kc == 7))
            if si >= 1:
                down_part(steps[si - 1], c0, T)
            kb.op("act", lambda e: e.copy(out=G[:, c0:c0 + nch, 2:2 + T],
                                          in_=pg[:, 0:nch * T].rearrange("p (c t) -> p c t", c=nch)),
                  reads=[pg], writes=[G])
            acc = ACC[si % 2]
            sil = SIL[si % 2]
            for j in range(nch):
                c = c0 + j
                kb.op("dve", lambda e: e.tensor_scalar(out=acc[:, j, :T], in0=G[:, c, 2:2 + T],
                                                       scalar1=CW[:, 2, c:c + 1], scalar2=CB[:, c:c + 1],
                                                       op0=ALU.mult, op1=ALU.add),
                      reads=[G, CW, CB], writes=[acc])
                kb.op("dve", lambda e: e.scalar_tensor_tensor(out=acc[:, j, :T], in0=G[:, c, 1:1 + T],
                                                              scalar=CW[:, 1, c:c + 1], in1=acc[:, j, :T],
                                                              op0=ALU.mult, op1=ALU.add),
                      reads=[G, CW, acc], writes=[acc])
                kb.op("dve", lambda e: e.scalar_tensor_tensor(out=acc[:, j, :T], in0=G[:, c, 0:T],
                                                              scalar=CW[:, 0, c:c + 1], in1=acc[:, j, :T],
                                                              op0=ALU.mult, op1=ALU.add),
                      reads=[G, CW, acc], writes=[acc])
            kb.op("act", lambda e: e.activation(out=sil[:, 0:nch, :T], in_=acc[:, 0:nch, :T], func=AF.Silu),
                  reads=[acc], writes=[sil])
            kb.op("dve", lambda e: e.tensor_tensor(out=ACTT[:, c0:c0 + nch, :T], in0=sil[:, 0:nch, :T],
                                                   in1=pv[:, 0:nch * T].rearrange("p (c t) -> p c t", c=nch),
                                                   op=ALU.mult),
                  reads=[sil, pv], writes=[ACTT])
        if ti + 1 < ntl:
            Tn = g.tiles[ti + 1][1]
            norm_stats(g, HTs[(ti + 1) % 3], Tn, Gb, hns[(ti + 1) % 2], sss[(ti + 1) % 3], rstds[(ti + 1) % 3], junk)
        down_part(steps[-1], NFC, T)
        kb.op("dve", lambda e: e.tensor_copy(out=G[:, :, 0:2], in_=G[:, :, T:T + 2]), reads=[G], writes=[G])
        if ti + 1 < ntl:
            norm_transpose(g, hns[(ti + 1) % 2], hnTs[(ti + 1) % 2], g.tiles[ti + 1][1])
        for nb in range(2):
            kb.op("dve", lambda e: e.tensor_tensor(out=HT[:T, nb * 512:(nb + 1) * 512],
                                                   in0=HT[:T, nb * 512:(nb + 1) * 512], in1=po_banks[nb][:T, :], op=ALU.add),
                  reads=[HT, po_banks[nb]], writes=[HT])
        store_h(g, dst, ti, HT, final, Gfin, (sss[2 - ti % 2 if False else (ti + 2) % 3], rstds[(ti + 2) % 3], junk))
        if ti + 2 < ntl:
            load_h(g, src, ti + 2, HTs[(ti + 2) % 3])


EH = 0.6065306597126334
ISQ = 0.08838834764831845
NEGBIG = -30000.0


def phase_l0(g, src, dst, final):
    import os
    CUT = int(os.environ.get('CUT', '99'))
    SUB = int(os.environ.get('SUB', '99'))
    HFN = int(os.environ.get('HFN', '2'))
    kb, nc, dr = g.kb, g.nc, g.dr
    Win = kb.sb([128, 8, 3848], BF16, "Win")
    Wout = kb.sb([128, 8, D], BF16, "Wout")
    W2A = kb.sb([128, 512], BF16, "W2A")
    G2 = kb.sb([128, 512], BF16, "G2")
    with contextlib.ExitStack() as ses:
        old = kb.es
        kb.es = ses
        stg = [kb.sb([128, 1924], F32, f"stg{i}") for i in range(3)]
        load_weight_bf16(g, dr["e_w_in"], 0, D, 3848, Win, stg)
        load_weight_bf16(g, dr["e_w_out"], 0, D, D, Wout, stg)
        s0 = stg[0]
        kb.dma(s0[0:64, 0:512], dr["r_w2"].ap()[:, :], writes=[s0], sem_buf=s0)
        kb.dma(s0[64:128, 0:512], dr["r_a2"].ap()[:, :], writes=[s0], sem_buf=s0)
        kb.op("dve", lambda e: e.tensor_copy(out=W2A[:, :], in_=s0[:, 0:512]), reads=[s0], writes=[W2A])
        s1 = stg[1]
        kb.dma(s1[:, 0:512], dr["r_g2"].ap()[:, :], writes=[s1], sem_buf=s1)
        kb.op("dve", lambda e: e.tensor_copy(out=G2[:, :], in_=s1[:, 0:512]), reads=[s1], writes=[G2])
        kb.barrier()
        kb.es = old
    F = lambda shape, name: kb.sb(shape, F32, name)
    Bf = lambda shape, name: kb.sb(shape, BF16, name)
    Gb = Bf([128, D], "Gb")
    ue = F([128, 128], "ue")
    su = F([128, 128], "su")
    blk = F([128, 128], "blk")
    kb.dma(ue[:, :], dr["c_ue"].ap()[:, :], writes=[ue], sem_buf=ue)
    kb.dma(su[:, :], dr["c_su"].ap()[:, :], writes=[su], sem_buf=su)
    kb.dma(blk[:, :], dr["c_blk"].ap()[:, :], writes=[blk], sem_buf=blk)
    sl = F([128, 128], "sl")
    kb.op("dve", lambda e: e.tensor_scalar(out=sl[:, :], in0=ue[:, :], scalar1=-1.0, scalar2=1.0, op0=ALU.mult, op1=ALU.add),
          reads=[ue], writes=[sl])
    neg = F([128, 128], "neg")
    kb.op("dve", lambda e: e.tensor_scalar(out=neg[:, :], in0=sl[:, :], scalar1=NEGBIG, scalar2=None, op0=ALU.mult),
          reads=[sl], writes=[neg])
    nblk = F([128, 2], "nblk")
    kb.op("dve", lambda e: e.tensor_scalar(out=nblk[:, 0:1], in0=blk[:, 0:1], scalar1=-1.0, scalar2=None, op0=ALU.mult),
          reads=[blk], writes=[nblk])
    kb.op("dve", lambda e: e.tensor_scalar(out=nblk[:, 1:2], in0=blk[:, 127:128], scalar1=-1.0, scalar2=None, op0=ALU.mult),
          reads=[blk], writes=[nblk])
    blk64 = F([128, 128], "blk64")
    kb.op("dve", lambda e: e.tensor_scalar(out=blk64[:, :], in0=blk[:, :], scalar1=1.0 / 64.0, scalar2=None, op0=ALU.mult),
          reads=[blk], writes=[blk64])
    onesf = F([128, 128], "onesf")
    kb.op("dve", lambda e: e.memset(onesf[:, :], 1.0 / 128.0), writes=[onesf])
    onesb = Bf([128, 128], "onesb")
    kb.op("dve", lambda e: e.memset(onesb[:, :], 1.0), writes=[onesb])
    ones1 = F([128, 128], "ones1")
    kb.op("dve", lambda e: e.memset(ones1[:, :], 1.0), writes=[ones1])
    vtmp = F([32, 128], "vtmp")
    MU = F([128, 14], "MU"); W0 = F([128, 4], "W0"); A0 = F([128, 4], "A0"); KK = F([128, 4], "KK")
    KA = F([128, 4], "KA"); RRK = F([128, 4], "RRK"); LNW = F([128, 4], "LNW"); LNB = F([128, 4], "LNB")
    MN = F([128, 4], "MN")
    load_vec_fm(g, dr["r_mu"], 0, 14, MU, MU[:, :], vtmp)
    for nm, buf in (("r_w0", W0), ("r_a0", A0), ("r_k_k", KK), ("r_k_a", KA), ("r_r_k", RRK), ("r_ln_w", LNW),
                    ("r_ln_b", LNB), ("m_norm", MN)):
        load_vec_fm(g, dr[nm], 0, 4, buf, buf[:, :], vtmp)
    BG = F([128, 8], "BG")
    kb.dma(BG[:, 0:4], bc_rows(dr["m_b_i"], 0, 4), writes=[BG], sem_buf=BG)
    kb.dma(BG[:, 4:8], bc_rows(dr["m_b_f"], 0, 4), writes=[BG], sem_buf=BG)
    C = F([128, 4, 129], "C")
    Cb = Bf([128, 4, 128], "Cb")
    nbc = Bf([128, 4, 128], "nbc")
    ST = F([128, 4, 64], "ST")
    STb = Bf([128, 4, 64], "STb")
    ZR = F([128, 14, 129], "ZR")
    for b_ in (C, ST, ZR):
        kb.op("dve", lambda e: e.memset(b_[:, :, :], 0.0), writes=[b_])
    for b_ in (Cb, nbc, STb):
        kb.op("dve", lambda e: e.memset(b_[:, :, :], 0.0), writes=[b_])
    vTM1 = Bf([128, 4, 129], "vTM1")
    kb.op("dve", lambda e: e.memset(vTM1[:, :, :], 1.0), writes=[vTM1])
    HTs = [F([128, D], f"HT{i}") for i in range(3)]
    hn = Bf([128, D], "hn"); hnT = Bf([128, 8, 128], "hnT"); junk = hn
    ss = F([128, 1], "ss"); rstd = F([128, 1], "rstd")
    qTb = Bf([128, 4, 128], "qTb"); kTb = Bf([128, 4, 128], "kTb"); moT = Bf([128, 4, 128], "moT"); kpbuf = F([128, 4, 128], "kpbuf")
    gx = F([128, 8], "gx"); th = F([128, 8], "th"); ex = F([128, 4], "ex"); LI = F([128, 4], "LI"); LF = F([128, 4], "LF")
    lmb = F([128, 4], "lmb"); LFb = F([128, 4, 128], "LFb"); arg = F([128, 4, 128], "arg"); ET = Bf([128, 4, 128], "ET"); aabuf = Bf([128, 4, 128], "aabuf")
    eB = F([128, 4, 128], "eB"); gcol = F([128, 4], "gcol"); ew = F([128, 4], "ew"); qs = Bf([128, 4, 128], "qs")
    sT = Bf([128, 4, 128], "sT"); kw = Bf([128, 4, 128], "kw")
    cden = F([128, 4, 128], "cden"); hT = F([128, 4, 128], "hT"); hsq = F([128, 4, 128], "hsq"); rs4 = F([128, 4, 128], "rs4")
    mixTs = [Bf([128, 8, 128], f"mixT{i}") for i in range(2)]
    kTMf = Bf([128, 512], "kTMf")
    P1 = F([128, 512], "P1")
    GGs = [Bf([128, 4, 128], f"GG{i}") for i in range(2)]
    for hh_ in range(2):
        kb.dma(P1[:, :], bc_rows(dr["norm_mix"], 0, 512, col0=512 * hh_), writes=[P1], sem_buf=P1)
        kb.op("dve", lambda e: e.tensor_copy(out=Gb[:, 512 * hh_:512 * (hh_ + 1)], in_=P1[:, :]), reads=[P1], writes=[Gb])
    Z2 = Bf([128, 14, 128], "Z2"); D1 = Z2
    LIN = Bf([128, 128], "LIN"); sxg = Bf([128, 128], "sxg")
    sw = arg; aa = aabuf
    kkr = cden; tq = hsq; rn = rs4; kp = kpbuf
    CS = hT; CSp = F([128, 4, 128], "CSp"); csl = F([128, 4], "csl")
    eW = F([128, 4, 128], "eW"); eWp = eB; eWi = F([128, 4, 128], "eWi"); eWT = F([128, 4, 128], "eWT")
    kka = F([128, 4, 128], "kka")
    AR = Bf([128, 4, 2, 128], "AR")
    BH = Bf([128, 4, 128], "BH"); KH = Bf([128, 4, 128], "KH"); vb = Bf([128, 4, 128], "vb")
    bonuss = [Bf([128, 4, 128], f"bonus{i}") for i in range(2)]
    BTm = [Bf([128, 4, 128], f"BTm{i}") for i in range(2)]
    KTm = [Bf([128, 4, 128], f"KTm{i}") for i in range(2)]
    ATm = [Bf([128, 4, 128], f"ATm{i}") for i in range(2)]
    STbd = Bf([128, 4, 128], "STbd")
    kb.op("dve", lambda e: e.memset(STbd[:, :, :], 0.0), writes=[STbd])
    VTM = Bf([128, 8, 64], "VTM"); BHT = Bf([128, 8, 64], "BHT"); KHT = Bf([128, 8, 64], "KHT"); UTM = Bf([128, 8, 64], "UTM")
    Xa = [F([128, 8, 128], "Xa0"), F([128, 8, 128], "Xa1")]
    XTa = [F([128, 8, 128], "XTa0"), F([128, 8, 128], "XTa1")]
    Pm = F([128, 8, 128], "Pm")
    ARB = Bf([128, 8, 128], "ARB"); AAK = Bf([128, 8, 128], "AAK"); ARK = Bf([128, 8, 128], "ARK")

    class SubBuf:
        def __init__(self, parent, lo):
            self.parent, self.lo = parent, lo
            self.w, self.r, self.excl, self.name = parent.w, parent.r, False, parent.name

        def __getitem__(self, idx):
            p, c, t = idx
            if isinstance(c, slice):
                c = slice((c.start or 0) + self.lo, (c.stop if c.stop is not None else 4) + self.lo)
            else:
                c = c + self.lo
            return self.parent.t[p, c, t]

    Of = SubBuf(Xa[1], 0); Osq = SubBuf(Xa[1], 4); mean_s = SubBuf(XTa[1], 0); var = SubBuf(XTa[1], 4)

    def b3(buf, T, n=4):
        return buf[:, 0:n].unsqueeze(2).to_broadcast([128, n, T])

    def mk_alloc(banks):
        st = [0]

        def alloc():
            b = banks[st[0] % len(banks)]
            st[0] += 1
            return b
        return alloc

    nP = mk_alloc(g.PS[0:2])
    nM = mk_alloc(g.PS[2:3])
    nR = mk_alloc(g.PS[3:5])
    nS = mk_alloc(g.PS[5:7])

    graw = F([128, 8], "graw")

    def gen_proj(ti):
        r0, T = g.tiles[ti]
        HT = HTs[ti % 3]
        mixT = mixTs[ti % 2]
        GG = GGs[ti % 2]
        bonus = bonuss[ti % 2]
        def proj_fm(pbank, j, col):
            for kc in range(8):
                kb.op("pe", lambda e: e.matmul(pbank[:, j * T:(j + 1) * T], Win[:, kc, col:col + 128], hnT[:, kc, :T],
                                               start=(kc == 0), stop=(kc == 7)),
                      reads=[Win, hnT], writes=[pbank], inc=(kc == 7))

        def proj_tm(pbank, col, n, c0=0):
            for kc in range(8):
                kb.op("pe", lambda e: e.matmul(pbank[:T, c0:c0 + n], hnT[:, kc, :T], Win[:, kc, col:col + n],
                                               start=(kc == 0), stop=(kc == 7)),
                      reads=[hnT, Win], writes=[pbank], inc=(kc == 7))

        def v3(pbank, n=4):
            return pbank[:, 0:n * T].rearrange("p (c t) -> p c t", c=n)

        pq = nP()
        for h in range(4):
            proj_fm(pq, h, h * 128)
        kb.op("act", lambda e: e.copy(out=qTb[:, :, :T], in_=v3(pq)), reads=[pq], writes=[qTb])
        pk = nP()
        for h in range(4):
            proj_fm(pk, h, 512 + h * 128)
        kb.op("act", lambda e: e.copy(out=kTb[:, :, :T], in_=v3(pk)), reads=[pk], writes=[kTb])
        pmo = nP()
        for h in range(4):
            proj_fm(pmo, h, 1536 + h * 128)
        kb.op("act", lambda e: e.activation(out=moT[:, :, :T], in_=v3(pmo), func=AF.Sigmoid), reads=[pmo], writes=[moT])
        pkt = nP()
        proj_tm(pkt, 512, 512)
        kb.op("act", lambda e: e.copy(out=kTMf[:T, :], in_=pkt[:T, :]), reads=[pkt], writes=[kTMf])
        pvt = nP()
        proj_tm(pvt, 1024, 512)
        kb.op("act", lambda e: e.copy(out=vTM1[:T, :, 0:128], in_=pvt[:T, :].rearrange("p (h v) -> p h v", h=4)),
              reads=[pvt], writes=[vTM1])
        pgt = nP()
        proj_tm(pgt, 2048, 8)
        kb.op("act", lambda e: e.copy(out=graw[:T, :], in_=pgt[:T, 0:8]), reads=[pgt], writes=[graw])
        zc = 2056
        for b0, n in ((0, 4), (4, 4), (8, 4), (12, 2)):
            yield
            pz = nP()
            for j in range(n):
                proj_fm(pz, j, zc + (b0 + j) * 128)
            kb.op("act", lambda e: e.copy(out=ZR[:, b0:b0 + n, 1:T + 1], in_=v3(pz, n)), reads=[pz], writes=[ZR])

    def gen_mlstm(ti):
        r0, T = g.tiles[ti]
        HT = HTs[ti % 3]
        mixT = mixTs[ti % 2]
        GG = GGs[ti % 2]
        bonus = bonuss[ti % 2]
        def proj_fm(pbank, j, col):
            for kc in range(8):
                kb.op("pe", lambda e: e.matmul(pbank[:, j * T:(j + 1) * T], Win[:, kc, col:col + 128], hnT[:, kc, :T],
                                               start=(kc == 0), stop=(kc == 7)),
                      reads=[Win, hnT], writes=[pbank], inc=(kc == 7))

        def proj_tm(pbank, col, n, c0=0):
            for kc in range(8):
                kb.op("pe", lambda e: e.matmul(pbank[:T, c0:c0 + n], hnT[:, kc, :T], Win[:, kc, col:col + n],
                                               start=(kc == 0), stop=(kc == 7)),
                      reads=[hnT, Win], writes=[pbank], inc=(kc == 7))

        def v3(pbank, n=4):
            return pbank[:, 0:n * T].rearrange("p (c t) -> p c t", c=n)

        kb.op("dve", lambda e: e.tensor_tensor(out=gx[:T, :], in0=graw[:T, :], in1=BG[:T, :], op=ALU.add),
              reads=[graw, BG], writes=[gx])
        kb.op("act", lambda e: e.activation(out=th[:T, :], in_=gx[:T, :], func=AF.Tanh, scale=1.0 / 15.0), reads=[gx], writes=[th])
        kb.op("dve", lambda e: e.tensor_scalar(out=LI[:T, :], in0=th[:T, 0:4], scalar1=15.0, scalar2=None, op0=ALU.mult),
              reads=[th], writes=[LI])
        kb.op("act", lambda e: e.activation(out=ex[:T, :], in_=th[:T, 4:8], func=AF.Exp, scale=-15.0), reads=[th], writes=[ex])
        kb.op("act", lambda e: e.activation(out=ex[:T, :], in_=ex[:T, :], func=AF.Ln, bias=1.0), reads=[ex], writes=[ex])
        kb.op("dve", lambda e: e.tensor_scalar(out=LF[:T, :], in0=ex[:T, :], scalar1=-1.0, scalar2=None, op0=ALU.mult),
              reads=[ex], writes=[LF])
        yield
        pbc = nM()
        kb.op("pe", lambda e: e.matmul(pbc[:T, 0:4], ue[:T, :T], LF[:T, :], start=True, stop=True), reads=[ue, LF], writes=[pbc])
        kb.op("dve", lambda e: e.tensor_tensor(out=lmb[:T, :], in0=LI[:T, :], in1=pbc[:T, 0:4], op=ALU.subtract),
              reads=[LI, pbc], writes=[lmb])
        kb.op("dve", lambda e: e.tensor_copy(out=LFb[:T, :, :], in_=LF[:T, 0:4].unsqueeze(2).to_broadcast([T, 4, 128])),
              reads=[LF], writes=[LFb])
        yield
        pB = nM()
        for h in range(4):
            kb.op("pe", lambda e: e.matmul(pB[:, h * T:(h + 1) * T], LFb[:T, h, :], ue[:T, :T], start=True, stop=True),
                  reads=[LFb, ue], writes=[pB], inc=(h == 3))
        kb.op("dve", lambda e: e.tensor_tensor(out=arg[:T, :, :T], in0=v3(pB)[:T], in1=neg[:T, :T].unsqueeze(1).to_broadcast([T, 4, T]),
                                               op=ALU.add), reads=[pB, neg], writes=[arg])
        for h in range(4):
            kb.op("act", lambda e: e.activation(out=ET[:T, h, :T], in_=arg[:T, h, :T], func=AF.Exp, bias=lmb[:T, h:h + 1]),
                  reads=[arg, lmb], writes=[ET])
        kb.op("act", lambda e: e.activation(out=eB[:, :, :T], in_=v3(pB), func=AF.Exp), reads=[pB], writes=[eB])
        kb.op("dve", lambda e: e.tensor_copy(out=gcol[:, :], in_=v3(pB)[:, :, T - 1]), reads=[pB], writes=[gcol])
        for h in range(4):
            kb.op("act", lambda e: e.activation(out=ew[:T, h:h + 1], in_=lmb[:T, h:h + 1], func=AF.Exp, bias=gcol[:T, h:h + 1]),
                  reads=[lmb, gcol], writes=[ew])
        kb.op("dve", lambda e: e.tensor_tensor(out=qs[:, :, :T], in0=qTb[:, :, :T], in1=eB[:, :, :T], op=ALU.mult),
              reads=[qTb, eB], writes=[qs])
        yield
        psc = nM()
        for h in range(4):
            kb.op("pe", lambda e: e.matmul(psc[:T, h * T:(h + 1) * T], kTb[:, h, :T], qTb[:, h, :T], start=True, stop=True),
                  reads=[kTb, qTb], writes=[psc], inc=(h == 3))
        kb.op("dve", lambda e: e.scalar_tensor_tensor(out=sT[:T, :, :T], in0=v3(psc)[:T], scalar=ISQ, in1=ET[:T, :, :T],
                                                      op0=ALU.mult, op1=ALU.mult), reads=[psc, ET], writes=[sT])
        yield
        pden = nM()
        for h in range(4):
            kb.op("pe", lambda e: e.matmul(pden[:, h * T:(h + 1) * T], onesb[:T, :], sT[:T, h, :T], start=True, stop=False),
                  reads=[onesb, sT], writes=[pden], inc=False)
            kb.op("pe", lambda e: e.matmul(pden[:, h * T:(h + 1) * T], nbc[:, h, :], qs[:, h, :T], start=False, stop=True),
                  reads=[nbc, qs], writes=[pden])
        kb.op("act", lambda e: e.activation(out=cden[:, :, :T], in_=v3(pden), func=AF.Abs), reads=[pden], writes=[cden])
        kb.op("dve", lambda e: e.tensor_scalar(out=cden[:, :, :T], in0=cden[:, :, :T], scalar1=1.0, scalar2=None, op0=ALU.max),
              reads=[cden], writes=[cden])
        kb.op("dve", lambda e: e.reciprocal(out=cden[:, :, :T], in_=cden[:, :, :T]), reads=[cden], writes=[cden])
        pnum = nM()
        for h in range(4):
            kb.op("pe", lambda e: e.matmul(pnum[:, h * T:(h + 1) * T], vTM1[:T, h, 0:128], sT[:T, h, :T], start=True, stop=False),
                  reads=[vTM1, sT], writes=[pnum], inc=False)
            kb.op("pe", lambda e: e.matmul(pnum[:, h * T:(h + 1) * T], Cb[:, h, :], qs[:, h, :T], start=False, stop=True),
                  reads=[Cb, qs], writes=[pnum])
        kb.op("dve", lambda e: e.tensor_tensor(out=hT[:, :, :T], in0=v3(pnum), in1=cden[:, :, :T], op=ALU.mult),
              reads=[pnum, cden], writes=[hT])
        kb.op("act", lambda e: e.activation(out=hsq[:, :, :T], in_=hT[:, :, :T], func=AF.Square), reads=[hT], writes=[hsq])
        yield
        pss = nM()
        for h in range(4):
            kb.op("pe", lambda e: e.matmul(pss[:, h * T:(h + 1) * T], onesf[:, :], hsq[:, h, :T], start=True, stop=True),
                  reads=[onesf, hsq], writes=[pss], inc=(h == 3))
        kb.op("act", lambda e: e.activation(out=rs4[:, :, :T], in_=v3(pss), func=AF.Sqrt, bias=1e-6), reads=[pss], writes=[rs4])
        kb.op("dve", lambda e: e.reciprocal(out=rs4[:, :, :T], in_=rs4[:, :, :T]), reads=[rs4], writes=[rs4])
        kb.op("dve", lambda e: e.tensor_tensor(out=hT[:, :, :T], in0=hT[:, :, :T], in1=rs4[:, :, :T], op=ALU.mult),
              reads=[hT, rs4], writes=[hT])
        kb.op("dve", lambda e: e.tensor_tensor(out=hT[:, :, :T], in0=hT[:, :, :T], in1=moT[:, :, :T], op=ALU.mult),
              reads=[hT, moT], writes=[hT])
        kb.op("dve", lambda e: e.tensor_tensor(out=mixT[:, 0:4, :T], in0=hT[:, :, :T], in1=b3(MN, T), op=ALU.mult),
              reads=[hT, MN], writes=[mixT])
        yield
        for h in range(4):
            kb.op("dve", lambda e: e.tensor_scalar(out=kw[:T, h, :], in0=kTMf[:T, h * 128:(h + 1) * 128], scalar1=ew[:T, h:h + 1],
                                                   scalar2=ISQ, op0=ALU.mult, op1=ALU.mult), reads=[kTMf, ew], writes=[kw])
        for half in range(2):
            pC = nM()
            for hh in range(2):
                h = half * 2 + hh
                kb.op("pe", lambda e: e.matmul(pC[:, hh * 129:(hh + 1) * 129], kw[:T, h, :], vTM1[:T, h, :], start=True, stop=True),
                      reads=[kw, vTM1], writes=[pC], inc=(hh == 1))
            for hh in range(2):
                h = half * 2 + hh
                kb.op("dve", lambda e: e.scalar_tensor_tensor(out=C[:, h, :], in0=C[:, h, :], scalar=eB[:, h, T - 1:T],
                                                              in1=pC[:, hh * 129:(hh + 1) * 129], op0=ALU.mult, op1=ALU.add),
                      reads=[C, eB, pC], writes=[C])
        yield
        kb.op("act", lambda e: e.copy(out=Cb[:, :, :], in_=C[:, :, 0:128]), reads=[C], writes=[Cb])
        kb.op("dve", lambda e: e.tensor_copy(out=nbc[:, :, :], in_=C[:, :, 128:129].to_broadcast([128, 4, 128])),
              reads=[C], writes=[nbc])


    def gen_prep(ti):
        r0, T = g.tiles[ti]
        HT = HTs[ti % 3]
        mixT = mixTs[ti % 2]
        GG = GGs[ti % 2]
        bonus = bonuss[ti % 2]
        def proj_fm(pbank, j, col):
            for kc in range(8):
                kb.op("pe", lambda e: e.matmul(pbank[:, j * T:(j + 1) * T], Win[:, kc, col:col + 128], hnT[:, kc, :T],
                                               start=(kc == 0), stop=(kc == 7)),
                      reads=[Win, hnT], writes=[pbank], inc=(kc == 7))

        def proj_tm(pbank, col, n, c0=0):
            for kc in range(8):
                kb.op("pe", lambda e: e.matmul(pbank[:T, c0:c0 + n], hnT[:, kc, :T], Win[:, kc, col:col + n],
                                               start=(kc == 0), stop=(kc == 7)),
                      reads=[hnT, Win], writes=[pbank], inc=(kc == 7))

        def v3(pbank, n=4):
            return pbank[:, 0:n * T].rearrange("p (c t) -> p c t", c=n)

        kb.op("dve", lambda e: e.tensor_tensor(out=D1[:, :, :T], in0=ZR[:, :, 0:T], in1=ZR[:, :, 1:T + 1], op=ALU.subtract),
              reads=[ZR], writes=[D1])
        kb.op("dve", lambda e: e.tensor_tensor(out=D1[:, :, :T], in0=D1[:, :, :T], in1=b3(MU, T, 14), op=ALU.mult),
              reads=[D1, MU], writes=[D1])
        kb.op("dve", lambda e: e.tensor_tensor(out=Z2[:, :, :T], in0=D1[:, :, :T], in1=ZR[:, :, 1:T + 1], op=ALU.add),
              reads=[D1, ZR], writes=[Z2])
        kb.op("dve", lambda e: e.tensor_copy(out=ZR[:, :, 0:1], in_=ZR[:, :, T:T + 1]), reads=[ZR], writes=[ZR])
        r_ = Z2[:, 0:4, :T]; k_ = Z2[:, 4:8, :T]; v_ = Z2[:, 8:12, :T]
        yield
        kb.op("act", lambda e: e.activation(out=LIN[0:64, :T], in_=Z2[0:64, 12, :T], func=AF.Tanh), reads=[Z2], writes=[LIN])
        kb.op("act", lambda e: e.copy(out=LIN[64:128, :T], in_=Z2[64:128, 12, :T]), reads=[Z2], writes=[LIN])
        kb.op("act", lambda e: e.activation(out=sxg[:, :T], in_=Z2[:, 13, :T], func=AF.Sigmoid), reads=[Z2], writes=[sxg])
        yield
        pw = nR()
        for c in range(4):
            kb.op("pe", lambda e: e.matmul(pw[:, c * T:(c + 1) * T], W2A[0:64, c * 128:(c + 1) * 128], LIN[0:64, :T], start=True, stop=True),
                  reads=[W2A, LIN], writes=[pw], inc=(c == 3))
        for c in range(4):
            kb.op("act", lambda e: e.activation(out=sw[:, c, :T], in_=pw[:, c * T:(c + 1) * T], func=AF.Sigmoid, bias=W0[:, c:c + 1]),
                  reads=[pw, W0], writes=[sw])
        pa = nR()
        for c in range(4):
            kb.op("pe", lambda e: e.matmul(pa[:, c * T:(c + 1) * T], W2A[64:128, c * 128:(c + 1) * 128], LIN[64:128, :T], start=True, stop=True),
                  reads=[W2A, LIN], writes=[pa], inc=(c == 3))
        for c in range(4):
            kb.op("act", lambda e: e.activation(out=aa[:, c, :T], in_=pa[:, c * T:(c + 1) * T], func=AF.Sigmoid, bias=A0[:, c:c + 1]),
                  reads=[pa, A0], writes=[aa])
        pgg = nR()
        for c in range(4):
            kb.op("pe", lambda e: e.matmul(pgg[:, c * T:(c + 1) * T], G2[:, c * 128:(c + 1) * 128], sxg[:, :T], start=True, stop=True),
                  reads=[G2, sxg], writes=[pgg], inc=(c == 3))
        kb.op("act", lambda e: e.copy(out=GG[:, :, :T], in_=v3(pgg)), reads=[pgg], writes=[GG])
        yield
        kb.op("dve", lambda e: e.tensor_tensor(out=kkr[:, :, :T], in0=k_, in1=b3(KK, T), op=ALU.mult), reads=[Z2, KK], writes=[kkr])
        kb.op("act", lambda e: e.activation(out=tq[:, :, :T], in_=kkr[:, :, :T], func=AF.Square), reads=[kkr], writes=[tq])
        pn = nR()
        for c in range(4):
            kb.op("pe", lambda e: e.matmul(pn[:, c * T:(c + 1) * T], blk[:, :], tq[:, c, :T], start=True, stop=True),
                  reads=[blk, tq], writes=[pn], inc=(c == 3))
        kb.op("act", lambda e: e.activation(out=rn[:, :, :T], in_=v3(pn), func=AF.Sqrt), reads=[pn], writes=[rn])
        kb.op("dve", lambda e: e.tensor_scalar(out=rn[:, :, :T], in0=rn[:, :, :T], scalar1=1e-12, scalar2=None, op0=ALU.max),
              reads=[rn], writes=[rn])
        kb.op("dve", lambda e: e.reciprocal(out=rn[:, :, :T], in_=rn[:, :, :T]), reads=[rn], writes=[rn])
        kb.op("dve", lambda e: e.tensor_tensor(out=kkr[:, :, :T], in0=kkr[:, :, :T], in1=rn[:, :, :T], op=ALU.mult),
              reads=[kkr, rn], writes=[kkr])
        yield
        kb.op("dve", lambda e: e.scalar_tensor_tensor(out=tq[:, :, :T], in0=aa[:, :, :T], scalar=-1.0, in1=b3(KA, T),
                                                      op0=ALU.add, op1=ALU.mult), reads=[aa, KA], writes=[tq])
        kb.op("dve", lambda e: e.scalar_tensor_tensor(out=kp[:, :, :T], in0=tq[:, :, :T], scalar=1.0, in1=k_,
                                                      op0=ALU.add, op1=ALU.mult), reads=[tq, Z2], writes=[kp])
        yield
        for c in range(4):
            kb.op("dve", lambda e: e.tensor_tensor_scan(out=CS[:, c, :T], data0=ones1[:, :T], data1=sw[:, c, :T], initial=0.0,
                                                        op0=ALU.mult, op1=ALU.add), reads=[ones1, sw], writes=[CS])
        kb.op("dve", lambda e: e.tensor_tensor(out=CSp[:, :, :T], in0=CS[:, :, :T], in1=sw[:, :, :T], op=ALU.subtract),
              reads=[CS, sw], writes=[CSp])
        kb.op("dve", lambda e: e.tensor_scalar(out=csl[:, :], in0=CS[:, :, T - 1], scalar1=-EH, scalar2=None, op0=ALU.mult),
              reads=[CS], writes=[csl])
        kb.op("act", lambda e: e.activation(out=eW[:, :, :T], in_=CS[:, :, :T], func=AF.Exp, scale=-EH), reads=[CS], writes=[eW])
        kb.op("act", lambda e: e.activation(out=eWp[:, :, :T], in_=CSp[:, :, :T], func=AF.Exp, scale=-EH), reads=[CSp], writes=[eWp])
        kb.op("act", lambda e: e.activation(out=eWi[:, :, :T], in_=CS[:, :, :T], func=AF.Exp, scale=EH), reads=[CS], writes=[eWi])
        for c in range(4):
            kb.op("act", lambda e: e.activation(out=eWT[:, c, :T], in_=CS[:, c, :T], func=AF.Exp, scale=EH, bias=csl[:, c:c + 1]),
                  reads=[CS, csl], writes=[eWT])
        yield
        kb.op("dve", lambda e: e.scalar_tensor_tensor(out=AR[:, :, 0, :T], in0=kkr[:, :, :T], scalar=-1.0, in1=eWp[:, :, :T],
                                                      op0=ALU.mult, op1=ALU.mult), reads=[kkr, eWp], writes=[AR])
        kb.op("dve", lambda e: e.tensor_tensor(out=AR[:, :, 1, :T], in0=r_, in1=eW[:, :, :T], op=ALU.mult), reads=[Z2, eW], writes=[AR])
        kb.op("dve", lambda e: e.tensor_tensor(out=kka[:, :, :T], in0=kkr[:, :, :T], in1=aa[:, :, :T], op=ALU.mult),
              reads=[kkr, aa], writes=[kka])
        kb.op("dve", lambda e: e.tensor_tensor(out=BH[:, :, :T], in0=kka[:, :, :T], in1=eWT[:, :, :T], op=ALU.mult),
              reads=[kka, eWT], writes=[BH])
        kb.op("dve", lambda e: e.tensor_tensor(out=KH[:, :, :T], in0=kp[:, :, :T], in1=eWT[:, :, :T], op=ALU.mult),
              reads=[kp, eWT], writes=[KH])
        kb.op("act", lambda e: e.copy(out=vb[:, :, :T], in_=v_), reads=[Z2], writes=[vb])
        yield
        kb.op("dve", lambda e: e.tensor_tensor(out=tq[:, :, :T], in0=r_, in1=kp[:, :, :T], op=ALU.mult), reads=[Z2, kp], writes=[tq])
        kb.op("dve", lambda e: e.tensor_tensor(out=tq[:, :, :T], in0=tq[:, :, :T], in1=b3(RRK, T), op=ALU.mult),
              reads=[tq, RRK], writes=[tq])
        prk = nR()
        for c in range(4):
            kb.op("pe", lambda e: e.matmul(prk[:, c * T:(c + 1) * T], blk[:, :], tq[:, c, :T], start=True, stop=True),
                  reads=[blk, tq], writes=[prk], inc=(c == 3))
        kb.op("dve", lambda e: e.tensor_tensor(out=bonus[:, :, :T], in0=v3(prk), in1=v_, op=ALU.mult), reads=[prk, Z2], writes=[bonus])
        yield
        PST = g.PST
        for c in range(4):
            kb.op("pe", lambda e: e.transpose(out=PST[:T, c * 128:(c + 1) * 128], in_=vb[:, c, :T], identity=g.ident_b[:, :]),
                  reads=[vb, g.ident_b], writes=[PST], inc=False)
        for c in range(4):
            kb.op("pe", lambda e: e.transpose(out=PST[:T, (4 + c) * 128:(5 + c) * 128], in_=BH[:, c, :T], identity=g.ident_b[:, :]),
                  reads=[BH, g.ident_b], writes=[PST], inc=(c == 3))
        kb.op("act", lambda e: e.copy(out=VTM[:T, :, :], in_=PST[:T, 0:512].rearrange("p (h v) -> p h v", h=8)), reads=[PST], writes=[VTM])
        kb.op("act", lambda e: e.copy(out=BHT[:T, :, :], in_=PST[:T, 512:1024].rearrange("p (h v) -> p h v", h=8)), reads=[PST], writes=[BHT])
        for c in range(4):
            kb.op("pe", lambda e: e.transpose(out=PST[:T, c * 128:(c + 1) * 128], in_=KH[:, c, :T], identity=g.ident_b[:, :]),
                  reads=[KH, g.ident_b], writes=[PST], inc=(c == 3))
        kb.op("act", lambda e: e.copy(out=KHT[:T, :, :], in_=PST[:T, 0:512].rearrange("p (h v) -> p h v", h=8)), reads=[PST], writes=[KHT])
        yield
        for hf in range(2):
            mcol = blk[:, 127 * hf:127 * hf + 1]
            kb.op("dve", lambda e: e.scalar_tensor_tensor(out=BTm[hf][:, :, :T], in0=kka[:, :, :T], scalar=mcol, in1=eWi[:, :, :T],
                                                          op0=ALU.mult, op1=ALU.mult), reads=[kka, blk, eWi], writes=[BTm[hf]])
            kb.op("dve", lambda e: e.scalar_tensor_tensor(out=KTm[hf][:, :, :T], in0=kp[:, :, :T], scalar=mcol, in1=eWi[:, :, :T],
                                                          op0=ALU.mult, op1=ALU.mult), reads=[kp, blk, eWi], writes=[KTm[hf]])
            kb.op("dve", lambda e: e.scalar_tensor_tensor(out=ATm[hf][:, :, :T], in0=kkr[:, :, :T], scalar=nblk[:, hf:hf + 1], in1=eWp[:, :, :T],
                                                          op0=ALU.mult, op1=ALU.mult), reads=[kkr, nblk, eWp], writes=[ATm[hf]])
        X, XT = Xa[0], XTa[0]
        sub = su[:T, :T].unsqueeze(1).to_broadcast([T, 2, T])
        ueb = ue[:T, :T].unsqueeze(1).to_broadcast([T, 2, T])
        slb = sl[:T, :T].unsqueeze(1).to_broadcast([T, 4, T])
        yield
        for c in range(4):
            yield
            pNA = nR(); pKA = nR()
            for hf in range(2):
                for j in range(2):
                    kb.op("pe", lambda e: e.matmul(pNA[:T, (hf * 2 + j) * T:(hf * 2 + j + 1) * T], BTm[hf][:, c, :T], AR[:, c, j, :T], start=True, stop=True),
                          reads=[BTm[hf], AR], writes=[pNA], inc=(hf == 1 and j == 1))
            for hf in range(2):
                for j in range(2):
                    kb.op("pe", lambda e: e.matmul(pKA[:T, (hf * 2 + j) * T:(hf * 2 + j + 1) * T], KTm[hf][:, c, :T], AR[:, c, j, :T], start=True, stop=True),
                          reads=[KTm[hf], AR], writes=[pKA], inc=(hf == 1 and j == 1))
            na4 = pNA[:T, 0:4 * T].rearrange("p (h j t) -> p h j t", h=2, j=2)
            ka4 = pKA[:T, 0:4 * T].rearrange("p (h j t) -> p h j t", h=2, j=2)
            kb.op("dve", lambda e: e.tensor_tensor(out=X[:T, 2 * c:2 * c + 2, :T], in0=na4[:, :, 0, :], in1=sub, op=ALU.mult),
                  reads=[pNA, su], writes=[X])
            kb.op("dve", lambda e: e.tensor_tensor(out=ARB[:T, 2 * c:2 * c + 2, :T], in0=na4[:, :, 1, :], in1=ueb, op=ALU.mult),
                  reads=[pNA, ue], writes=[ARB])
            kb.op("dve", lambda e: e.tensor_tensor(out=AAK[:T, 2 * c:2 * c + 2, :T], in0=ka4[:, :, 0, :], in1=sub, op=ALU.mult),
                  reads=[pKA, su], writes=[AAK])
            kb.op("dve", lambda e: e.tensor_tensor(out=ARK[:T, 2 * c:2 * c + 2, :T], in0=ka4[:, :, 1, :], in1=ueb, op=ALU.mult),
                  reads=[pKA, ue], writes=[ARK])
        yield
        for half in range(2):
            pNb = nR()
            for j in range(4):
                h = half * 4 + j
                c, hf = h // 2, h % 2
                pl = slice(hf * 64, hf * 64 + 64)
                kb.op("pe", lambda e: e.matmul(pNb[:T, j * T:(j + 1) * T], ATm[hf][:, c, :T], BTm[hf][:, c, :T], start=True, stop=True),
                      reads=[ATm[hf], BTm[hf]], writes=[pNb], inc=(j == 3))
            kb.op("dve", lambda e: e.tensor_tensor(out=XT[:T, half * 4:half * 4 + 4, :T], in0=v3(pNb)[:T], in1=slb, op=ALU.mult),
                  reads=[pNb, sl], writes=[XT])
        kb.op("dve", lambda e: e.tensor_tensor(out=Pm[:T, :, :T], in0=X[:T, :, :T],
                                               in1=g.ident_f[:T, :T].unsqueeze(1).to_broadcast([T, 8, T]), op=ALU.add),
              reads=[X, g.ident_f], writes=[Pm])

    def gen_neumann(ti):
        r0, T = g.tiles[ti]
        HT = HTs[ti % 3]
        mixT = mixTs[ti % 2]
        GG = GGs[ti % 2]
        bonus = bonuss[ti % 2]
        def proj_fm(pbank, j, col):
            for kc in range(8):
                kb.op("pe", lambda e: e.matmul(pbank[:, j * T:(j + 1) * T], Win[:, kc, col:col + 128], hnT[:, kc, :T],
                                               start=(kc == 0), stop=(kc == 7)),
                      reads=[Win, hnT], writes=[pbank], inc=(kc == 7))

        def proj_tm(pbank, col, n, c0=0):
            for kc in range(8):
                kb.op("pe", lambda e: e.matmul(pbank[:T, c0:c0 + n], hnT[:, kc, :T], Win[:, kc, col:col + n],
                                               start=(kc == 0), stop=(kc == 7)),
                      reads=[hnT, Win], writes=[pbank], inc=(kc == 7))

        def v3(pbank, n=4):
            return pbank[:, 0:n * T].rearrange("p (c t) -> p c t", c=n)

        yield
        lv = 1
        cur = 0
        while lv * 2 < T:
            X, XT = Xa[cur], XTa[cur]
            Xn, XTn = Xa[1 - cur], XTa[1 - cur]
            for half in range(2):
                yield
                p1 = nS(); p2 = nS()
                for j in range(4):
                    h = half * 4 + j
                    kb.op("pe", lambda e: e.matmul(p1[:T, j * T:(j + 1) * T], XT[:T, h, :T], X[:T, h, :T], start=True, stop=True),
                          reads=[XT, X], writes=[p1], inc=(j == 3))
                for j in range(4):
                    h = half * 4 + j
                    kb.op("pe", lambda e: e.matmul(p2[:T, j * T:(j + 1) * T], X[:T, h, :T], XT[:T, h, :T], start=True, stop=True),
                          reads=[XT, X], writes=[p2], inc=(j == 3))
                kb.op("act", lambda e: e.copy(out=Xn[:T, half * 4:half * 4 + 4, :T], in_=v3(p1)[:T]), reads=[p1], writes=[Xn])
                kb.op("dve", lambda e: e.tensor_copy(out=XTn[:T, half * 4:half * 4 + 4, :T], in_=v3(p2)[:T]), reads=[p2], writes=[XTn])
            yield
            for half in range(2):
                p3 = nS()
                for j in range(4):
                    h = half * 4 + j
                    kb.op("pe", lambda e: e.matmul(p3[:T, j * T:(j + 1) * T], XTn[:T, h, :T], Pm[:T, h, :T], start=True, stop=True),
                          reads=[XTn, Pm], writes=[p3], inc=(j == 3))
                kb.op("dve", lambda e: e.tensor_tensor(out=Pm[:T, half * 4:half * 4 + 4, :T], in0=Pm[:T, half * 4:half * 4 + 4, :T],
                                                       in1=v3(p3)[:T], op=ALU.add), reads=[Pm, p3], writes=[Pm])
            cur = 1 - cur
            lv *= 2

    def tail1(ti):
        r0, T = g.tiles[ti]
        HT = HTs[ti % 3]
        mixT = mixTs[ti % 2]
        GG = GGs[ti % 2]
        bonus = bonuss[ti % 2]
        def proj_fm(pbank, j, col):
            for kc in range(8):
                kb.op("pe", lambda e: e.matmul(pbank[:, j * T:(j + 1) * T], Win[:, kc, col:col + 128], hnT[:, kc, :T],
                                               start=(kc == 0), stop=(kc == 7)),
                      reads=[Win, hnT], writes=[pbank], inc=(kc == 7))

        def proj_tm(pbank, col, n, c0=0):
            for kc in range(8):
                kb.op("pe", lambda e: e.matmul(pbank[:T, c0:c0 + n], hnT[:, kc, :T], Win[:, kc, col:col + n],
                                               start=(kc == 0), stop=(kc == 7)),
                      reads=[hnT, Win], writes=[pbank], inc=(kc == 7))

        def v3(pbank, n=4):
            return pbank[:, 0:n * T].rearrange("p (c t) -> p c t", c=n)

        cur = ((T.bit_length() - 2) % 2) if T > 2 else 0
        pP1 = nS()
        for h in range(8):
            c, hf = h // 2, h % 2
            pl = slice(hf * 64, hf * 64 + 64)
            kb.op("pe", lambda e: e.matmul(pP1[:T, h * 64:(h + 1) * 64], ATm[hf][:, c, :T], STb[:, c, :], start=True, stop=False),
                  reads=[ATm[hf], STb], writes=[pP1], inc=False)
            kb.op("pe", lambda e: e.matmul(pP1[:T, h * 64:(h + 1) * 64], AAK[:T, h, :T], VTM[:T, h, :], start=False, stop=True),
                  reads=[AAK, VTM], writes=[pP1], inc=(h == 7))
        kb.op("act", lambda e: e.copy(out=P1[:T, :], in_=pP1[:T, :]), reads=[pP1], writes=[P1])
        pU = nS()
        for h in range(8):
            kb.op("pe", lambda e: e.matmul(pU[:T, h * 64:(h + 1) * 64], Pm[:T, h, :T], P1[:T, h * 64:(h + 1) * 64], start=True, stop=True),
                  reads=[Pm, P1], writes=[pU], inc=(h == 7))
        kb.op("act", lambda e: e.copy(out=UTM[:T, :, :], in_=pU[:T, :].rearrange("p (h v) -> p h v", h=8)), reads=[pU], writes=[UTM])
        pO = nS()
        for c in range(4):
            kb.op("pe", lambda e: e.matmul(pO[:, c * T:(c + 1) * T], STbd[:, c, :], AR[:, c, 1, :T], start=True, stop=False),
                  reads=[STbd, AR], writes=[pO], inc=False)
            for hf in range(2):
                h = 2 * c + hf
                pl = slice(hf * 64, hf * 64 + 64)
                kb.op("pe", lambda e: e.matmul(pO[pl, c * T:(c + 1) * T], UTM[:T, h, :], ARB[:T, h, :T], start=False, stop=False),
                      reads=[UTM, ARB], writes=[pO], inc=False)
                kb.op("pe", lambda e: e.matmul(pO[pl, c * T:(c + 1) * T], VTM[:T, h, :], ARK[:T, h, :T], start=False, stop=True),
                      reads=[VTM, ARK], writes=[pO], inc=(hf == 1))
        kb.op("act", lambda e: e.copy(out=Of[:, :, :T], in_=v3(pO)), reads=[pO], writes=[Of])
        pS = nS()
        for h in range(8):
            c, hf = h // 2, h % 2
            pl = slice(hf * 64, hf * 64 + 64)
            kb.op("pe", lambda e: e.matmul(pS[pl, c * 64:(c + 1) * 64], BHT[:T, h, :], UTM[:T, h, :], start=True, stop=False),
                  reads=[BHT, UTM], writes=[pS], inc=False)
            kb.op("pe", lambda e: e.matmul(pS[pl, c * 64:(c + 1) * 64], KHT[:T, h, :], VTM[:T, h, :], start=False, stop=True),
                  reads=[KHT, VTM], writes=[pS], inc=(h == 7))
        for c in range(4):
            kb.op("dve", lambda e: e.scalar_tensor_tensor(out=ST[:, c, :], in0=ST[:, c, :], scalar=eW[:, c, T - 1:T],
                                                          in1=pS[:, c * 64:(c + 1) * 64], op0=ALU.mult, op1=ALU.add),
                  reads=[ST, eW, pS], writes=[ST])
        kb.op("act", lambda e: e.copy(out=STb[:, :, :], in_=ST[:, :, :]), reads=[ST], writes=[STb])
        kb.op("act", lambda e: e.copy(out=STbd[0:64, :, 0:64], in_=ST[0:64, :, :]), reads=[ST], writes=[STbd])
        kb.op("act", lambda e: e.copy(out=STbd[64:128, :, 64:128], in_=ST[64:128, :, :]), reads=[ST], writes=[STbd])

    def gen_tail2(ti):
        r0, T = g.tiles[ti]
        HT = HTs[ti % 3]
        mixT = mixTs[ti % 2]
        GG = GGs[ti % 2]
        bonus = bonuss[ti % 2]
        def proj_fm(pbank, j, col):
            for kc in range(8):
                kb.op("pe", lambda e: e.matmul(pbank[:, j * T:(j + 1) * T], Win[:, kc, col:col + 128], hnT[:, kc, :T],
                                               start=(kc == 0), stop=(kc == 7)),
                      reads=[Win, hnT], writes=[pbank], inc=(kc == 7))

        def proj_tm(pbank, col, n, c0=0):
            for kc in range(8):
                kb.op("pe", lambda e: e.matmul(pbank[:T, c0:c0 + n], hnT[:, kc, :T], Win[:, kc, col:col + n],
                                               start=(kc == 0), stop=(kc == 7)),
                      reads=[hnT, Win], writes=[pbank], inc=(kc == 7))

        def v3(pbank, n=4):
            return pbank[:, 0:n * T].rearrange("p (c t) -> p c t", c=n)

        cur = ((T.bit_length() - 2) % 2) if T > 2 else 0
        kb.op("act", lambda e: e.activation(out=Osq[:, :, :T], in_=Of[:, :, :T], func=AF.Square), reads=[Of], writes=[Osq])
        yield
        pm_ = nS(); pq_ = nS()
        for c in range(4):
            kb.op("pe", lambda e: e.matmul(pm_[:, c * T:(c + 1) * T], blk64[:, :], Of[:, c, :T], start=True, stop=True),
                  reads=[blk64, Of], writes=[pm_], inc=(c == 3))
        for c in range(4):
            kb.op("pe", lambda e: e.matmul(pq_[:, c * T:(c + 1) * T], blk64[:, :], Osq[:, c, :T], start=True, stop=True),
                  reads=[blk64, Osq], writes=[pq_], inc=(c == 3))
        kb.op("act", lambda e: e.copy(out=mean_s[:, :, :T], in_=v3(pm_)), reads=[pm_], writes=[mean_s])
        yield
        kb.op("dve", lambda e: e.scalar_tensor_tensor(out=var[:, :, :T], in0=mean_s[:, :, :T], scalar=-1.0, in1=mean_s[:, :, :T],
                                                      op0=ALU.mult, op1=ALU.mult), reads=[mean_s], writes=[var])
        kb.op("dve", lambda e: e.tensor_tensor(out=var[:, :, :T], in0=var[:, :, :T], in1=v3(pq_), op=ALU.add),
              reads=[var, pq_], writes=[var])
        yield
        kb.op("dve", lambda e: e.tensor_scalar(out=var[:, :, :T], in0=var[:, :, :T], scalar1=0.0, scalar2=None, op0=ALU.max),
              reads=[var], writes=[var])
        kb.op("act", lambda e: e.activation(out=var[:, :, :T], in_=var[:, :, :T], func=AF.Sqrt, bias=64e-5), reads=[var], writes=[var])
        yield
        kb.op("dve", lambda e: e.reciprocal(out=var[:, :, :T], in_=var[:, :, :T]), reads=[var], writes=[var])
        kb.op("dve", lambda e: e.tensor_tensor(out=Of[:, :, :T], in0=Of[:, :, :T], in1=mean_s[:, :, :T], op=ALU.subtract),
              reads=[Of, mean_s], writes=[Of])
        yield
        kb.op("dve", lambda e: e.tensor_tensor(out=Of[:, :, :T], in0=Of[:, :, :T], in1=var[:, :, :T], op=ALU.mult),
              reads=[Of, var], writes=[Of])
        kb.op("dve", lambda e: e.tensor_tensor(out=Of[:, :, :T], in0=Of[:, :, :T], in1=b3(LNW, T), op=ALU.mult),
              reads=[Of, LNW], writes=[Of])
        yield
        kb.op("dve", lambda e: e.tensor_tensor(out=Of[:, :, :T], in0=Of[:, :, :T], in1=b3(LNB, T), op=ALU.add),
              reads=[Of, LNB], writes=[Of])
        kb.op("dve", lambda e: e.tensor_tensor(out=Of[:, :, :T], in0=Of[:, :, :T], in1=bonus[:, :, :T], op=ALU.add),
              reads=[Of, bonus], writes=[Of])
        yield
        kb.op("dve", lambda e: e.tensor_tensor(out=mixT[:, 4:8, :T], in0=Of[:, :, :T], in1=GG[:, :, :T], op=ALU.mult),
              reads=[Of, GG], writes=[mixT])
        for nb in range(2):
            pp = nS()
            for c in range(8):
                kb.op("pe", lambda e: e.matmul(pp[:T, :], mixT[:, c, :T], Wout[:, c, nb * 512:(nb + 1) * 512],
                                               start=(c == 0), stop=(c == 7)), reads=[mixT, Wout], writes=[pp], inc=(c == 7))
            kb.op("dve", lambda e: e.tensor_tensor(out=HT[:T, nb * 512:(nb + 1) * 512], in0=HT[:T, nb * 512:(nb + 1) * 512],
                                                   in1=pp[:T, :], op=ALU.add), reads=[HT, pp], writes=[HT])
        yield
        store_h(g, dst, ti, HT, final, None, (ss, rstd, junk))


    def run(gen):
        for _ in gen:
            pass

    def interleave(a, b, ra=1, rb=1):
        da = db = False
        while not (da and db):
            for _ in range(ra):
                if not da:
                    try:
                        next(a)
                    except StopIteration:
                        da = True
            for _ in range(rb):
                if not db:
                    try:
                        next(b)
                    except StopIteration:
                        db = True

    base_line = phase_l0.__code__.co_firstlineno
    CSCALE = float(os.environ.get('CSCALE', '1.0'))

    def cost(o):
        if o[0] == "dma":
            return 60.0
        return CSCALE * float(L0_COST.get(o[6] - base_line, L0_COST_DEFAULT.get(o[1], 300.0)))

    def norm_and_proj(ti):
        rmsnorm_T(g, HTs[ti % 3], g.tiles[ti][1], Gb, hn, hnT, ss, rstd, junk)
        return gen_proj(ti)

    def s0(ti):
        for _ in gen_neumann(ti):
            pass
        tail1(ti)
        for _ in gen_tail2(ti):
            pass
        if ti + 3 < ntl:
            load_h(g, src, ti + 3, HTs[ti % 3])

    ntl = len(g.tiles)
    for k_ in range(min(3, ntl)):
        load_h(g, src, k_, HTs[k_])
    NOSCHED = bool(os.environ.get("L0_NOSCHED"))

    def rec(f):
        if NOSCHED:
            r = f()
            if r is not None and hasattr(r, "__next__"):
                for _ in r:
                    pass
            return []
        return kb.record(f)

    run(norm_and_proj(0))
    pro = [rec(lambda: gen_mlstm(0)), rec(lambda: gen_prep(0))]
    if ntl > 1:
        pro.append(rec(lambda: norm_and_proj(1)))
    kb.schedule(pro, cost, sync_ns=float(os.environ.get('SYNC_NS', '0')))
    for ti in range(ntl):
        streams = [rec(lambda: s0(ti))]
        if ti + 1 < ntl:
            streams.append(rec(lambda: gen_mlstm(ti + 1)))
            streams.append(rec(lambda: gen_prep(ti + 1)))
        if ti + 2 < ntl:
            streams.append(rec(lambda: norm_and_proj(ti + 2)))
        kb.schedule(streams, cost, sync_ns=float(os.environ.get('SYNC_NS', '0')))


L0_COST = {19: 426.0, 22: 427.0, 36: 227.0, 39: 226.0, 42: 154.0, 44: 154.0, 47: 134.0, 50: 139.0, 52: 140.0, 54: 141.0, 75: 1025.5, 77: 485.0, 79: 488.0, 96: 427.8, 111: 485.0, 161: 56.0, 178: 585.0, 182: 585.0, 167: 216.0, 186: 1283.0, 189: 501.5, 192: 587.0, 196: 66.0, 228: 97.5, 204: 585.0, 230: 1283.0, 233: 205.0, 231: 171.2, 234: 1283.0, 235: 170.2, 239: 269.0, 240: 159.0, 242: 411.2, 247: 214.0, 249: 692.0, 252: 401.0, 254: 483.2, 255: 162.2, 259: 599.2, 257: 294.0, 265: 123.5, 267: 692.0, 272: 133.5, 274: 111.2, 276: 597.0, 277: 427.0, 282: 134.8, 284: 111.5, 279: 3353.0, 286: 691.2, 288: 629.2, 292: 214.0, 294: 1283.0, 295: 3354.0, 296: 693.0, 298: 693.0, 300: 692.0, 305: 254.0, 311: 267.0, 315: 350.0, 319: 617.5, 320: 427.0, 345: 1933.2, 347: 2025.2, 349: 2025.2, 355: 1283.0, 351: 180.2, 380: 593.2, 356: 294.2, 357: 309.0, 361: 93.0, 368: 39.0, 364: 367.0, 371: 367.0, 375: 123.5, 377: 475.0, 381: 529.0, 384: 214.0, 386: 1283.0, 387: 427.0, 389: 3353.0, 390: 693.0, 394: 599.8, 396: 693.0, 401: 425.0, 407: 1283.0, 403: 692.2, 405: 103.2, 408: 519.0, 409: 520.0, 411: 401.2, 415: 693.0, 417: 693.0, 418: 599.8, 424: 507.0, 420: 693.0, 422: 600.8, 427: 601.8, 428: 692.0, 432: 214.0, 439: 107.0, 434: 692.0, 454: 662.2, 442: 107.0, 456: 662.0, 458: 663.0, 444: 586.0, 445: 487.0, 447: 146.0, 470: 107.0, 474: 107.0, 449: 585.0, 478: 425.0, 480: 332.0, 482: 331.2, 484: 331.0, 493: 123.0, 495: 692.0, 498: 1133.0, 534: 112.0, 538: 112.0, 540: 585.0, 541: 692.0, 547: 140.2, 549: 689.0, 581: 78.2, 583: 77.5, 585: 585.0, 588: 213.2, 594: 221.0, 590: 585.0, 599: 247.0, 601: 100.0, 609: 47.0, 611: 47.0, 603: 585.0, 614: 280.0, 617: 404.2, 618: 306.2, 619: 401.0, 644: 629.2, 648: 214.0, 651: 213.2, 653: 585.0, 655: 693.0, 657: 689.0, 660: 427.2, 662: 1283.0, 664: 3353.0, 665: 692.2, 668: 692.0, 670: 692.2, 673: 692.0, 675: 692.2, 678: 692.2, 684: 427.0, 686: 689.0}
L0_COST_DEFAULT = {"pe": 120.0, "dve": 450.0, "act": 450.0, "pool": 900.0}


LG = [float(np.log(1.0 - 2.0 ** (-5.0 - h))) for h in range(4)]
TWO_PI = 6.283185307179586
CW1 = 6.28125
CW2 = TWO_PI - CW1


LG = [float(np.log(1.0 - 2.0 ** (-5.0 - h))) for h in range(4)]
TWO_PI = 6.283185307179586
CW1 = 6.28125
CW2 = TWO_PI - CW1


LG = [float(np.log(1.0 - 2.0 ** (-5.0 - h))) for h in range(4)]
TWO_PI = 6.283185307179586
CW1 = 6.28125
CW2 = TWO_PI - CW1


def phase_l1(g, src, dst, final):
    kb, nc, dr = g.kb, g.nc, g.dr
    Win = kb.sb([128, 8, 6144], BF16, "Win")
    Wout = kb.sb([128, 16, D], BF16, "Wout")
    with contextlib.ExitStack() as ses:
        old = kb.es
        kb.es = ses
        stg = [kb.sb([128, 1536], F32, f"stg{i}") for i in range(3)]
        load_weight_bf16(g, dr["o_w_in_p"], 0, D, 6144, Win, stg)
        load_weight_bf16(g, dr["o_w_out"], 0, 2048, D, Wout, stg)
        kb.barrier()
        kb.es = old
    Gb = kb.sb([128, D], BF16, "Gb")
    Gfin = None
    iota = kb.sb([128, 128], F32, "iota")
    pidx = kb.sb([128, 1], F32, "pidx")
    ue = kb.sb([128, 128], F32, "ue")
    inv = kb.sb([128, 1], F32, "inv")
    kb.dma(iota[:, :], dr["c_iota"].ap()[:, :], writes=[iota], sem_buf=iota)
    kb.dma(pidx[:, :], dr["c_pidx"].ap()[:, :], writes=[pidx], sem_buf=pidx)
    kb.dma(ue[:, :], dr["c_ue"].ap()[:, :], writes=[ue], sem_buf=ue)
    kb.dma(inv[:, :], dr["c_inv"].ap()[:, :], writes=[inv], sem_buf=inv)
    DM = kb.sb([128, 4, 128], F32, "DM")
    DEC = kb.sb([128, 4, 128], F32, "DEC")
    KDEC = {128: kb.sb([128, 4], F32, "KDEC128"), 16: kb.sb([128, 4], F32, "KDEC16")}
    tms = kb.sb([128, 128], F32, "tms")
    kb.op("dve", lambda e: e.tensor_scalar(out=tms[:, :], in0=iota[:, :], scalar1=pidx[:, 0:1], scalar2=0.0,
                                           op0=ALU.subtract, op1=ALU.max), reads=[iota, pidx], writes=[tms])
    for h in range(4):
        kb.op("act", lambda e: e.activation(out=DM[:, h, :], in_=tms[:, :], func=AF.Exp, scale=LG[h]),
              reads=[tms], writes=[DM])
        kb.op("dve", lambda e: e.scalar_tensor_tensor(out=DM[:, h, :], in0=DM[:, h, :], scalar=1.0 / 16.0,
                                                      in1=ue[:, :], op0=ALU.mult, op1=ALU.mult),
              reads=[DM, ue], writes=[DM])
        kb.op("act", lambda e: e.activation(out=DEC[:, h, :], in_=iota[:, :], func=AF.Exp, scale=LG[h], bias=LG[h]),
              reads=[iota], writes=[DEC])
        for TT in (128, 16):
            kd = KDEC[TT]
            kb.op("act", lambda e: e.activation(out=kd[:, h:h + 1], in_=pidx[:, 0:1], func=AF.Exp, scale=-LG[h],
                                                bias=LG[h] * (TT - 1)), reads=[pidx], writes=[kd])
            kb.op("dve", lambda e: e.tensor_scalar(out=kd[:, h:h + 1], in0=kd[:, h:h + 1], scalar1=1.0 / 16.0,
                                                   scalar2=None, op0=ALU.mult), reads=[kd], writes=[kd])
    Sr = kb.sb([128, 8, 512], F32, "Sr")
    Srb = kb.sb([128, 8, 512], BF16, "Srb")
    kb.op("dve", lambda e: e.memset(Sr[:, :, :], 0.0), writes=[Sr])
    kb.op("pool", lambda e: e.memset(Srb[:, :, :], 0.0), writes=[Srb])
    HTs = [kb.sb([128, D], F32, f"HT{i}") for i in range(2)]
    hn = kb.sb([128, D], BF16, "hn")
    hnT = kb.sb([128, 8, 128], BF16, "hnT")
    sss = [kb.sb([128, 1], F32, f"ss{i}") for i in range(2)]
    rstds = [kb.sb([128, 1], F32, f"rstd{i}") for i in range(2)]
    ang = kb.sb([128, 128], F32, "ang")
    ang2 = kb.sb([128, 128], F32, "ang2")
    kf = kb.sb([128, 128], F32, "kf")
    ki = kb.sb([128, 128], I32, "ki")
    nsins = [kb.sb([128, 128], F32, f"nsin{i}") for i in range(2)]
    ncoss = [kb.sb([128, 128], F32, f"ncos{i}") for i in range(2)]
    t1 = kb.sb([128, 4, 128], F32, "t1")
    t2 = kb.sb([128, 4, 128], F32, "t2")
    qb = kb.sb([128, 2, 4, 128], BF16, "qb")
    qdb = kb.sb([128, 2, 4, 128], BF16, "qdb")
    kbf = kb.sb([128, 2, 4, 128], BF16, "kbf")
    kdT = kb.sb([128, 8, 128], BF16, "kdT")
    sTm = kb.sb([128, 4, 128], BF16, "sTm")
    VT = kb.sb([128, 2048], BF16, "VT")
    GS = kb.sb([128, 2048], BF16, "GS")
    og = kb.sb([128, 2048], BF16, "og")
    ogT = kb.sb([128, 16, 128], BF16, "ogT")
    st6 = kb.sb([128, 6], F32, "st6")
    mv = kb.sb([128, 2], F32, "mv")
    rs = kb.sb([128, 1], F32, "rs")
    junk = og
    kb.dma(t1[:, :, :].rearrange("p a b -> p (a b)"), bc_rows(dr["norm_mix"], 1, 512), writes=[t1], sem_buf=t1)
    kb.op("dve", lambda e: e.tensor_copy(out=Gb[:, 0:512], in_=t1[:, :, :].rearrange("p a b -> p (a b)")), reads=[t1], writes=[Gb])
    kb.dma(t2[:, :, :].rearrange("p a b -> p (a b)"), bc_rows(dr["norm_mix"], 1, 512, col0=512), writes=[t2], sem_buf=t2)
    kb.op("dve", lambda e: e.tensor_copy(out=Gb[:, 512:1024], in_=t2[:, :, :].rearrange("p a b -> p (a b)")), reads=[t2], writes=[Gb])

    def sincos(dst_tbl, shift, pos0, T):
        kb.op("dve", lambda e: e.tensor_scalar(out=ang[:, :T], in0=iota[:, :T], scalar1=float(pos0), scalar2=inv[:, 0:1],
                                               op0=ALU.add, op1=ALU.mult), reads=[iota, inv], writes=[ang])
        if shift != 0.0:
            kb.op("dve", lambda e: e.tensor_scalar(out=ang[:, :T], in0=ang[:, :T], scalar1=shift, scalar2=None,
                                                   op0=ALU.add), reads=[ang], writes=[ang])
        kb.op("dve", lambda e: e.tensor_scalar(out=ki[:, :T], in0=ang[:, :T], scalar1=1.0 / TWO_PI, scalar2=None,
                                               op0=ALU.mult), reads=[ang], writes=[ki])
        kb.op("dve", lambda e: e.tensor_copy(out=kf[:, :T], in_=ki[:, :T]), reads=[ki], writes=[kf])
        kb.op("dve", lambda e: e.scalar_tensor_tensor(out=ang2[:, :T], in0=kf[:, :T], scalar=-CW1, in1=ang[:, :T],
                                                      op0=ALU.mult, op1=ALU.add), reads=[kf, ang], writes=[ang2])
        kb.op("dve", lambda e: e.scalar_tensor_tensor(out=ang2[:, :T], in0=kf[:, :T], scalar=-CW2, in1=ang2[:, :T],
                                                      op0=ALU.mult, op1=ALU.add), reads=[kf, ang2], writes=[ang2])
        kb.op("dve", lambda e: e.tensor_scalar(out=ang2[:, :T], in0=ang2[:, :T], scalar1=3.1415925, scalar2=-3.1415925,
                                               op0=ALU.min, op1=ALU.max), reads=[ang2], writes=[ang2])
        kb.op("act", lambda e: e.activation(out=dst_tbl[:, :T], in_=ang2[:, :T], func=AF.Sin),
              reads=[ang2], writes=[dst_tbl])

    ntl = len(g.tiles)
    load_h(g, src, 0, HTs[0])
    T0 = g.tiles[0][1]
    norm_stats(g, HTs[0], T0, Gb, hn, sss[0], rstds[0], junk)
    if ntl > 1:
        load_h(g, src, 1, HTs[1])
    norm_transpose(g, hn, hnT, T0)
    sincos(nsins[0], 0.0, g.tiles[0][0], T0)
    sincos(ncoss[0], np.pi / 2, g.tiles[0][0], T0)
    PST = g.PST
    for ti, (r0, T) in enumerate(g.tiles):
        HT = HTs[ti % 2]
        HO = HT
        nsin = nsins[ti % 2]
        ncos = ncoss[ti % 2]
        sb_ = nsin[:, :T].unsqueeze(1).to_broadcast([128, 4, T])
        cb_ = ncos[:, :T].unsqueeze(1).to_broadcast([128, 4, T])
        qk_banks = []
        for which in range(2):
            pe_ = next_ps(g)
            po_ = next_ps(g)
            qk_banks.append((pe_, po_))
            for eo, pb in ((0, pe_), (1, po_)):
                for h in range(4):
                    col = which * 1024 + h * 256 + eo * 128
                    for kc in range(8):
                        kb.op("pe", lambda e: e.matmul(pb[:, h * T:(h + 1) * T], Win[:, kc, col:col + 128],
                                                       hnT[:, kc, :T], start=(kc == 0), stop=(kc == 7)),
                              reads=[Win, hnT], writes=[pb], inc=(kc == 7))
        for which in range(2):
            pe_, po_ = qk_banks[which]
            pe3 = pe_[:, 0:4 * T].rearrange("p (h t) -> p h t", h=4)
            po3 = po_[:, 0:4 * T].rearrange("p (h t) -> p h t", h=4)
            dstb = qb if which == 0 else kbf
            kb.op("dve", lambda e: e.tensor_tensor(out=t1[:, :, :T], in0=pe3, in1=cb_, op=ALU.mult),
                  reads=[pe_, ncos], writes=[t1])
            kb.op("dve", lambda e: e.tensor_tensor(out=t2[:, :, :T], in0=po3, in1=sb_, op=ALU.mult),
                  reads=[po_, nsin], writes=[t2])
            kb.op("dve", lambda e: e.tensor_tensor(out=dstb[:, 0, :, :T], in0=t1[:, :, :T], in1=t2[:, :, :T],
                                                   op=ALU.subtract), reads=[t1, t2], writes=[dstb])
            kb.op("dve", lambda e: e.tensor_tensor(out=t1[:, :, :T], in0=po3, in1=cb_, op=ALU.mult),
                  reads=[po_, ncos], writes=[t1])
            kb.op("dve", lambda e: e.tensor_tensor(out=t2[:, :, :T], in0=pe3, in1=sb_, op=ALU.mult),
                  reads=[pe_, nsin], writes=[t2])
            kb.op("dve", lambda e: e.tensor_tensor(out=dstb[:, 1, :, :T], in0=t1[:, :, :T], in1=t2[:, :, :T],
                                                   op=ALU.add), reads=[t1, t2], writes=[dstb])
            if which == 0:
                for eo in range(2):
                    kb.op("pool", lambda e: e.tensor_tensor(out=qdb[:, eo, :, :T], in0=qb[:, eo, :, :T],
                                                            in1=DEC[:, :, :T], op=ALU.mult),
                          reads=[qb, DEC], writes=[qdb])
        for nb in range(4):
            pvv = next_ps(g)
            for kc in range(8):
                kb.op("pe", lambda e: e.matmul(pvv[:T, :], hnT[:, kc, :T], Win[:, kc, 2048 + nb * 512:2048 + (nb + 1) * 512],
                                               start=(kc == 0), stop=(kc == 7)), reads=[hnT, Win], writes=[pvv], inc=(kc == 7))
            kb.op("act", lambda e: e.copy(out=VT[:T, nb * 512:(nb + 1) * 512], in_=pvv[:T, :]), reads=[pvv], writes=[VT])
        for nb in range(4):
            pgg = next_ps(g)
            for kc in range(8):
                kb.op("pe", lambda e: e.matmul(pgg[:T, :], hnT[:, kc, :T], Win[:, kc, 4096 + nb * 512:4096 + (nb + 1) * 512],
                                               start=(kc == 0), stop=(kc == 7)), reads=[hnT, Win], writes=[pgg], inc=(kc == 7))
            kb.op("act", lambda e: e.activation(out=GS[:T, nb * 512:(nb + 1) * 512], in_=pgg[:T, :], func=AF.Silu),
                  reads=[pgg], writes=[GS])
        if ti + 1 < ntl:
            r0n, Tn = g.tiles[ti + 1]
            sincos(nsins[(ti + 1) % 2], 0.0, r0n, Tn)
            sincos(ncoss[(ti + 1) % 2], np.pi / 2, r0n, Tn)
        for h in range(4):
            for eo in range(2):
                j = h * 2 + eo
                kb.op("pe", lambda e: e.transpose(out=PST[:T, j * 128:(j + 1) * 128], in_=kbf[:, eo, h, :T],
                                                  identity=g.ident_b[:, :]),
                      reads=[kbf, g.ident_b], writes=[PST], inc=(j == 7))
        for h in range(4):
            kb.op("act", lambda e: e.activation(out=kdT[:T, 2 * h:2 * h + 2, :],
                                                in_=PST[:T, 2 * h * 128:(2 * h + 2) * 128].rearrange("p (j d) -> p j d", j=2),
                                                func=AF.Copy, scale=KDEC[T][:T, h:h + 1]),
                  reads=[PST, KDEC[T]], writes=[kdT])
        psc = next_ps(g)
        for h in range(4):
            for eo in range(2):
                kb.op("pe", lambda e: e.matmul(psc[:T, h * T:(h + 1) * T], kbf[:, eo, h, :T], qb[:, eo, h, :T],
                                               start=(eo == 0), stop=(eo == 1)),
                      reads=[kbf, qb], writes=[psc], inc=(eo == 1))
        kb.op("dve", lambda e: e.tensor_tensor(out=sTm[:T, :, :T],
                                               in0=psc[:T, 0:4 * T].rearrange("p (h t) -> p h t", h=4),
                                               in1=DM[:T, :, :T], op=ALU.mult), reads=[psc, DM], writes=[sTm])
        for h in range(4):
            po = next_ps(g)
            kb.op("pe", lambda e: e.matmul(po[:T, :], sTm[:T, h, :T], VT[:T, h * 512:(h + 1) * 512], start=True, stop=False),
                  reads=[sTm, VT], writes=[po], inc=False)
            for eo in range(2):
                kb.op("pe", lambda e: e.matmul(po[:T, :], qdb[:, eo, h, :T], Srb[:, 2 * h + eo, :], start=False, stop=(eo == 1)),
                      reads=[qdb, Srb], writes=[po], inc=(eo == 1))
            kb.op("dve", lambda e: e.bn_stats(out=st6[:T, :], in_=po[:T, :]), reads=[po], writes=[st6])
            kb.op("dve", lambda e: e.bn_aggr(out=mv[:T, :], in_=st6[:T, :]), reads=[st6], writes=[mv])
            kb.op("act", lambda e: e.activation(out=rs[:T, :], in_=mv[:T, 1:2], func=AF.Sqrt, scale=1.0, bias=1e-6),
                  reads=[mv], writes=[rs])
            kb.op("dve", lambda e: e.reciprocal(out=rs[:T, :], in_=rs[:T, :]), reads=[rs], writes=[rs])
            kb.op("dve", lambda e: e.tensor_scalar(out=og[:T, h * 512:(h + 1) * 512], in0=po[:T, :], scalar1=mv[:T, 0:1], scalar2=rs[:T, 0:1],
                                                   op0=ALU.subtract, op1=ALU.mult), reads=[po, mv, rs], writes=[og])
            kb.op("pool", lambda e: e.tensor_tensor(out=og[:T, h * 512:(h + 1) * 512], in0=og[:T, h * 512:(h + 1) * 512],
                                                    in1=GS[:T, h * 512:(h + 1) * 512], op=ALU.mult),
                  reads=[og, GS], writes=[og])
        gT = [float(np.exp(LG[h] * T)) for h in range(4)]
        for h in range(4):
            for eo in range(2):
                j = 2 * h + eo
                pst_ = next_ps(g)
                kb.op("pe", lambda e: e.matmul(pst_[:, :], kdT[:T, j, :], VT[:T, h * 512:(h + 1) * 512], start=True, stop=True),
                      reads=[kdT, VT], writes=[pst_])
                kb.op("dve", lambda e: e.scalar_tensor_tensor(out=Sr[:, j, :], in0=Sr[:, j, :], scalar=gT[h], in1=pst_[:, :],
                                                              op0=ALU.mult, op1=ALU.add), reads=[Sr, pst_], writes=[Sr])
                kb.op("act", lambda e: e.copy(out=Srb[:, j, :], in_=Sr[:, j, :]), reads=[Sr], writes=[Srb])
        for half in range(2):
            for j in range(8):
                c = half * 8 + j
                kb.op("pe", lambda e: e.transpose(out=PST[:, j * T:(j + 1) * T], in_=og[:T, c * 128:(c + 1) * 128],
                                                  identity=g.ident_b[:T, :T]),
                      reads=[og, g.ident_b], writes=[PST], inc=(j == 7))
            kb.op("act", lambda e: e.copy(out=ogT[:, half * 8:(half + 1) * 8, :T],
                                          in_=PST[:, 0:8 * T].rearrange("p (k t) -> p k t", k=8)),
                  reads=[PST], writes=[ogT])
        if ti + 1 < ntl:
            Tn = g.tiles[ti + 1][1]
            norm_stats(g, HTs[(ti + 1) % 2], Tn, Gb, hn, sss[(ti + 1) % 2], rstds[(ti + 1) % 2], junk)
        pps = []
        for nb in range(2):
            pp = next_ps(g)
            pps.append(pp)
            for c in range(16):
                kb.op("pe", lambda e: e.matmul(pp[:T, :], ogT[:, c, :T], Wout[:, c, nb * 512:(nb + 1) * 512],
                                               start=(c == 0), stop=(c == 15)), reads=[ogT, Wout], writes=[pp], inc=(c == 15))
        if ti + 1 < ntl:
            norm_transpose(g, hn, hnT, g.tiles[ti + 1][1])
        for nb in range(2):
            kb.op("dve", lambda e: e.tensor_tensor(out=HO[:T, nb * 512:(nb + 1) * 512],
                                                   in0=HT[:T, nb * 512:(nb + 1) * 512], in1=pps[nb][:T, :], op=ALU.add),
                  reads=[HT, pps[nb]], writes=[HO])
        store_h(g, dst, ti, HO, final, Gfin, (sss[ti % 2], rstds[ti % 2], junk))
        if ti + 2 < ntl:
            load_h(g, src, ti + 2, HTs[ti % 2])


def make_in_map(inputs, b, NT):
    m = {"x": np.ascontiguousarray(inputs["x"][b, :128 * NT])}
    for k, shp in W_SPECS.items():
        src_k = "o_w_in" if k == "o_w_in_p" else k
        m[k] = np.ascontiguousarray(np.asarray(inputs[src_k], np.float32).reshape(shp))
    m.update(host_consts())
    perm = np.arange(6144)
    for sec in range(2):
        for h in range(4):
            base = sec * 1024 + h * 256
            perm[base:base + 256] = np.concatenate([base + np.arange(0, 256, 2), base + np.arange(1, 256, 2)])
    m["o_w_in_p"] = np.ascontiguousarray(m["o_w_in_p"][:, perm])
    return m


NT_FULL = 32


def kernel(**inputs):
    nc = build(NT_FULL, phases=(1, 2, 3, 4), debug=False, final=True)
    in_maps = [make_in_map(inputs, b, NT_FULL) for b in range(8)]
    res = run_bass_kernel_spmd(nc, in_maps, core_ids=list(range(8)))
    return np.stack([np.asarray(r["out"], np.float32) for r in res.results], axis=0)
```

```python
import contextlib
import numpy as np
import concourse.bass as bass
import concourse.mybir as mybir

import os as _os_fw
_STRICT = bool(_os_fw.environ.get("KB_STRICT"))
_NPE_BIAS = float(_os_fw.environ.get("NPE_BIAS", "0"))
_CARRY = bool(int(_os_fw.environ.get("SCHED_CARRY", "0")))
F32 = mybir.dt.float32
BF16 = mybir.dt.bfloat16
I32 = mybir.dt.int32
AF = mybir.ActivationFunctionType
ALU = mybir.AluOpType
AX = mybir.AxisListType


class Buf:
    __slots__ = ("t", "w", "r", "dsem", "dcount", "name", "excl")

    def __init__(self, t, name=""):
        self.t = t
        self.w = {}
        self.r = {}
        self.dsem = None
        self.dcount = 0
        self.name = name
        self.excl = False

    def __getitem__(self, idx):
        return self.t[idx]


class _Cap:
    def __init__(self):
        self.call = None

    def __getattr__(self, name):
        def f(*args, **kwargs):
            self.call = (name, args, kwargs)
            return None
        return f


class Eng:
    def __init__(self, name, obj, sem):
        self.name = name
        self.obj = obj
        self.sem = sem
        self.count = 0
        self.seen = {}


class KB:
    def __init__(self, nc, es):
        self.nc = nc
        self.es = es
        self.sems = {}
        self.E = {}
        for name, obj in (("pe", nc.tensor), ("act", nc.scalar), ("dve", nc.vector),
                          ("pool", nc.gpsimd), ("sp", nc.sync)):
            sem = es.enter_context(nc.semaphore("s_" + name))
            self.E[name] = Eng(name, obj, sem)
            self.sems[id(sem)] = sem
        self.dma_tokens = {}
        self.nbuf = 0
        self.rec = None

    def sb(self, shape, dt, name=None):
        self.nbuf += 1
        name = f"{name or 'b'}_{self.nbuf}"
        t = self.es.enter_context(self.nc.sbuf_tensor(name, list(shape), dt))
        return Buf(t, name)

    def ps(self, shape, dt, name=None):
        self.nbuf += 1
        name = f"{name or 'p'}_{self.nbuf}"
        t = self.es.enter_context(self.nc.psum_tensor(name, list(shape), dt))
        b = Buf(t, name)
        b.excl = True
        return b

    def newsem(self, name):
        sem = self.es.enter_context(self.nc.semaphore(name))
        self.sems[id(sem)] = sem
        return sem

    def _wait(self, e, deps):
        for sid, val in deps.items():
            if e.seen.get(sid, 0) < val:
                e.obj.wait_ge(self.sems[sid], val)
                e.seen[sid] = val

    def _deps(self, e, reads, writes):
        deps = {}
        own = id(e.sem)
        for b in reads:
            for sid, v in b.w.items():
                if deps.get(sid, 0) < v:
                    deps[sid] = v
            if b.excl:
                for sid, v in b.r.items():
                    if sid != own and deps.get(sid, 0) < v:
                        deps[sid] = v
        skip_own = True if not _STRICT else (e.name == "pe")
        for b in writes:
            for d in (b.w, b.r):
                for sid, v in d.items():
                    if sid == own and skip_own:
                        continue
                    if deps.get(sid, 0) < v:
                        deps[sid] = v
        return deps

    def op(self, eng, fn, reads=(), writes=(), inc=True):
        if self.rec is not None:
            import sys as _sys
            cap = _Cap()
            fn(cap)
            name, args, kwargs = cap.call
            fn2 = (lambda e_, name=name, args=args, kwargs=kwargs: getattr(e_, name)(*args, **kwargs))
            self.rec.append(("op", eng, fn2, tuple(reads), tuple(writes), inc, _sys._getframe(1).f_lineno))
            return None
        e = self.E[eng]
        self._wait(e, self._deps(e, reads, writes))
        ins = fn(e.obj)
        if inc:
            e.count += 1
            ins.then_inc(e.sem, 1)
            val = e.count
        else:
            val = e.count + 1
        sid = id(e.sem)
        for b in reads:
            if b.r.get(sid, 0) < val:
                b.r[sid] = val
        for b in writes:
            if b.w.get(sid, 0) < val:
                b.w[sid] = val
        return ins

    def dma(self, out_ap, in_ap, reads=(), writes=(), sem_buf=None, q="sp"):
        if self.rec is not None:
            import sys as _sys
            self.rec.append(("dma", q, (out_ap, in_ap, sem_buf), tuple(reads), tuple(writes), True, _sys._getframe(1).f_lineno))
            return None
        e = self.E[q]
        self._wait(e, self._deps(e, reads, writes))
        b = sem_buf
        if b.dsem is None:
            b.dsem = self.newsem("d_" + b.name)
        b.dcount += 16
        e.obj.dma_start(out=out_ap, in_=in_ap).then_inc(b.dsem, 16)
        sid = id(b.dsem)
        for x in reads:
            x.r[sid] = b.dcount
        for x in writes:
            x.w[sid] = b.dcount
        self.dma_tokens[sid] = b.dcount

    def barrier(self):
        targets = {id(e.sem): e.count for e in self.E.values() if e.count > 0}
        targets.update(self.dma_tokens)
        for e in self.E.values():
            self._wait(e, {k: v for k, v in targets.items() if k != id(e.sem)})

    def final_wait(self):
        e = self.E["sp"]
        self._wait(e, dict(self.dma_tokens))

    def record(self, fn):
        assert self.rec is None
        self.rec = []
        try:
            r = fn()
            if r is not None and hasattr(r, "__next__"):
                for _ in r:
                    pass
        finally:
            ops, self.rec = self.rec, None
        return ops

    def schedule(self, streams, cost_fn, sync_ns=0.0, slack_ns=0.0):
        streams = [list(x) for x in streams if x]
        n = len(streams)
        key = lambda b: id(b.w)
        rem_r = [dict() for _ in range(n)]
        rem_w = [dict() for _ in range(n)]
        for k, st in enumerate(streams):
            for o in st:
                for b in o[3]:
                    rem_r[k][key(b)] = rem_r[k].get(key(b), 0) + 1
                for b in o[4]:
                    rem_w[k][key(b)] = rem_w[k].get(key(b), 0) + 1
        pos = [0] * n
        if _CARRY and getattr(self, "_sst", None) is not None:
            eng_t, w_t, r_t, w_e = self._sst
        else:
            eng_t, w_t, r_t, w_e = {}, {}, {}, {}
            self._sst = (eng_t, w_t, r_t, w_e)
        order = []
        total = sum(len(x) for x in streams)
        while len(order) < total:
            best = None
            cands = []
            for k in range(n):
                if pos[k] >= len(streams[k]):
                    continue
                o = streams[k][pos[k]]
                ok = True
                for j in range(k):
                    if pos[j] >= len(streams[j]):
                        continue
                    for b in o[4]:
                        kk_ = key(b)
                        if rem_r[j].get(kk_, 0) or rem_w[j].get(kk_, 0):
                            ok = False
                            break
                    if ok:
                        for b in o[3]:
                            if rem_w[j].get(key(b), 0):
                                ok = False
                                break
                    if not ok:
                        break
                if not ok:
                    continue
                eng = o[1]
                t = eng_t.get(eng, 0.0)
                for b in o[3]:
                    kk_ = key(b)
                    tw = w_t.get(kk_, 0.0) + (sync_ns if w_e.get(kk_) != eng else 0.0)
                    if tw > t:
                        t = tw
                for b in o[4]:
                    kk_ = key(b)
                    tw = max(w_t.get(kk_, 0.0), r_t.get(kk_, 0.0)) + sync_ns
                    if tw > t:
                        t = tw
                tk = t + (0.0 if eng == "pe" else _NPE_BIAS)
                if best is None or tk < best[3] - 1e-9:
                    best = (t, k, o, tk)
                cands.append((t, k, o, tk))
            if slack_ns > 0:
                for c_ in cands:
                    if c_[0] <= best[0] + slack_ns:
                        best = c_
                        break
            t, k, o = best[0], best[1], best[2]
            dur = cost_fn(o)
            eng = o[1]
            end = t + dur
            eng_t[eng] = end if o[0] == "op" else t + 60.0
            for b in o[3]:
                kk_ = key(b)
                r_t[kk_] = max(r_t.get(kk_, 0.0), end)
                rem_r[k][kk_] -= 1
            for b in o[4]:
                kk_ = key(b)
                w_t[kk_] = end
                w_e[kk_] = eng
                rem_w[k][kk_] -= 1
            pos[k] += 1
            order.append(o)
        for o in order:
            if o[0] == "op":
                self.op(o[1], o[2], reads=o[3], writes=o[4], inc=o[5])
            else:
                out_ap, in_ap, sem_buf = o[2]
                self.dma(out_ap, in_ap, reads=o[3], writes=o[4], sem_buf=sem_buf, q=o[1])
        return max(eng_t.values()) if eng_t else 0.0


from concourse.bass_utils import run_bass_kernel_spmd

D = 1024
NMETA = 16
DFF = 2816
NFC = DFF // 128

W_SPECS = {
    "meta_tokens": (16, 1024), "norm_mix": (2, 1024), "norm_ffn": (2, 1024), "norm_final": (1, 1024),
    "e_w_in": (1024, 3848), "e_w_out": (1024, 1024), "m_b_i": (1, 4), "m_b_f": (1, 4), "m_norm": (1, 512),
    "r_mu": (1, 1792), "r_w0": (1, 512), "r_w2": (64, 512), "r_a0": (1, 512), "r_a2": (64, 512),
    "r_g2": (128, 512), "r_k_k": (1, 512), "r_k_a": (1, 512), "r_r_k": (1, 512), "r_ln_w": (1, 512),
    "r_ln_b": (1, 512), "o_w_in_p": (1024, 6144), "o_w_out": (2048, 1024), "f_w_up": (2048, 5632),
    "f_conv_w": (6, 2816), "f_conv_b": (2, 2816), "f_w_down": (5632, 1024),
}


def host_consts():
    c = {}
    c["c_ident"] = np.eye(128, dtype=np.float32)
    i = np.arange(128)
    c["c_ue"] = (i[:, None] <= i[None, :]).astype(np.float32)
    c["c_su"] = (i[:, None] < i[None, :]).astype(np.float32)
    c["c_iota"] = np.broadcast_to(np.arange(128, dtype=np.float32)[None, :], (128, 128)).copy()
    c["c_pidx"] = np.arange(128, dtype=np.float32)[:, None].copy()
    bo = np.zeros((128, 128), np.float32)
    bo[:64, :64] = 1.0
    bo[64:, 64:] = 1.0
    c["c_blk"] = bo
    c["c_inv"] = (np.float32(1.0) / np.power(np.float32(10000.0), np.linspace(0.0, 1.0, 128, dtype=np.float32))
                  ).astype(np.float32)[:, None].copy()
    return c


class Ctx:
    pass


def tile_rows(NT):
    tiles = [(0, NMETA)]
    for i in range(NT):
        tiles.append((NMETA + 128 * i, 128))
    return tiles


def build(NT, phases=(1, 2, 3, 4), debug=False, final=True):
    nc = bass.Bass("TRN2", target_bir_lowering=False)
    SEQ = 128 * NT
    L = NMETA + SEQ
    dr = {}
    dr["x"] = nc.dram_tensor("x", [SEQ, D], F32, kind="ExternalInput")
    for k, shp in W_SPECS.items():
        dr[k] = nc.dram_tensor(k, list(shp), F32, kind="ExternalInput")
    for k, v in host_consts().items():
        dr[k] = nc.dram_tensor(k, list(v.shape), F32, kind="ExternalInput")
    out = nc.dram_tensor("out", [SEQ, D], F32, kind="ExternalOutput")
    H = {}
    for i in (1, 2, 3):
        H[i] = nc.dram_tensor(f"H{i}", [L, D], F32, kind=("ExternalOutput" if debug else "Internal"))

    tiles = tile_rows(NT)
    es = contextlib.ExitStack()
    with es:
        kb = KB(nc, es)
        PS = [kb.ps([128, 512], F32, f"psb{i}") for i in range(7)]
        PST = kb.ps([128, 1024], BF16, "pstr")
        g = Ctx()
        g.nc, g.kb, g.dr, g.H, g.out, g.tiles, g.PS, g.PST = nc, kb, dr, H, out, tiles, PS, PST
        g.psi = 0
        g.ident_f = kb.sb([128, 128], F32, "ident_f")
        g.ident_b = kb.sb([128, 128], BF16, "ident_b")
        kb.dma(g.ident_f[:, :], dr["c_ident"].ap()[:, :], writes=[g.ident_f], sem_buf=g.ident_f)
        kb.op("dve", lambda e: e.tensor_copy(out=g.ident_b[:, :], in_=g.ident_f[:, :]),
              reads=[g.ident_f], writes=[g.ident_b])

        plist = [p for p in (1, 2, 3, 4) if p in phases]
        src = 0
        for p in plist:
            dst = p if p != plist[-1] else 4
            with contextlib.ExitStack() as pes:
                kb.es = pes
                if p in (2, 4):
                    phase_ffn(g, layer=(0 if p == 2 else 1), src=src, dst=dst, final=final)
                elif p == 1:
                    phase_l0(g, src=src, dst=dst, final=final)
                elif p == 3:
                    phase_l1(g, src=src, dst=dst, final=final)
                kb.barrier()
            kb.es = es
            src = dst
        kb.final_wait()
    return nc


def next_ps(g):
    b = g.PS[g.psi % len(g.PS)]
    g.psi += 1
    return b


def bc_rows(handle, row, n, parts=128, col0=0, ncols_total=None):
    ncols_total = ncols_total if ncols_total is not None else handle.shape[1]
    return bass.AP(handle, row * ncols_total + col0, [[0, parts], [1, n]])


def load_h(g, src, ti, HT):
    kb = g.kb
    r0, T = g.tiles[ti]
    if src == 0:
        if ti == 0:
            ap = g.dr["meta_tokens"].ap()[0:NMETA, :]
        else:
            ap = g.dr["x"].ap()[r0 - NMETA:r0 - NMETA + T, :]
    else:
        ap = g.H[src].ap()[r0:r0 + T, :]
    kb.dma(HT[:T, :], ap, writes=[HT], sem_buf=HT)


def store_h(g, dst, ti, HO, final, Gfin=None, scratch=None):
    kb = g.kb
    r0, T = g.tiles[ti]
    if dst != 4:
        kb.dma(g.H[dst].ap()[r0:r0 + T, :], HO[:T, :], reads=[HO], sem_buf=HO)
        return
    if ti == 0:
        return
    if final:
        ss, rstd, junk = scratch
        kb.op("act", lambda e: e.activation(out=junk[:T, 0:D], in_=HO[:T, :], func=AF.Square, accum_out=ss[:T, :]),
              reads=[HO], writes=[junk, ss])
        rstd_from_ss(kb, ss, rstd, T, 1.0 / D, 1e-6)
        kb.op("dve", lambda e: e.scalar_tensor_tensor(out=HO[:T, :], in0=HO[:T, :], scalar=rstd[:T, :],
                                                      in1=Gfin[:T, :], op0=ALU.mult, op1=ALU.mult),
              reads=[HO, rstd, Gfin], writes=[HO])
    kb.dma(g.out.ap()[r0 - NMETA:r0 - NMETA + T, :], HO[:T, :], reads=[HO], sem_buf=HO)


import os as _os
DMAQ_N = int(_os.environ.get("DMAQ_N", "1"))


def load_weight_bf16(g, dram_handle, row0, K, N, W, stg, col0=0, ncols_total=None):
    kb = g.kb
    SW = stg[0].t.shape[1]
    engs = ("dve", "act", "dve", "act", "dve", "act", "dve")
    cnt = getattr(g, "_lw_cnt", 0)
    for kc in range(K // 128):
        for j0 in range(0, N, SW):
            w = min(SW, N - j0)
            s = stg[cnt % len(stg)]
            kb.dma(s[:, :w], dram_handle.ap()[row0 + kc * 128: row0 + (kc + 1) * 128, col0 + j0: col0 + j0 + w],
                   writes=[s], sem_buf=s, q=(("sp", "pool", "act")[cnt % DMAQ_N] if DMAQ_N > 1 else "sp"))
            en = engs[cnt % len(engs)]
            if en == "act":
                kb.op("act", lambda e: e.copy(out=W[:, kc, j0:j0 + w], in_=s[:, :w]), reads=[s], writes=[W])
            else:
                kb.op(en, lambda e: e.tensor_copy(out=W[:, kc, j0:j0 + w], in_=s[:, :w]), reads=[s], writes=[W])
            cnt += 1
    g._lw_cnt = cnt


def rstd_from_ss(kb, ss, rstd, T, scale, eps, ap_fn=None):
    a = (lambda b: b[:T, :]) if ap_fn is None else ap_fn
    kb.op("act", lambda e: e.activation(out=a(rstd), in_=a(ss), func=AF.Sqrt, scale=scale, bias=eps),
          reads=[ss], writes=[rstd])
    kb.op("dve", lambda e: e.reciprocal(out=a(rstd), in_=a(rstd)), reads=[rstd], writes=[rstd])


def load_vec_fm(g, handle, row, nch, dstbuf, dst_ap, vtmp, col0=0):
    kb = g.kb
    ncols = handle.shape[1]
    src = bass.AP(handle, row * ncols + col0, [[128, nch], [1, 128]])
    kb.dma(vtmp[:nch, :], src, writes=[vtmp], sem_buf=vtmp)
    pt = next_ps(g)
    kb.op("pe", lambda e: e.transpose(out=pt[:, :nch], in_=vtmp[:nch, :], identity=g.ident_f[:nch, :nch]),
          reads=[vtmp, g.ident_f], writes=[pt])
    kb.op("dve", lambda e: e.tensor_copy(out=dst_ap, in_=pt[:, :nch]), reads=[pt], writes=[dstbuf])


def rmsnorm_T(g, HT, T, Gb, hn, hnT, ss, rstd, junk):
    kb = g.kb
    kb.op("act", lambda e: e.activation(out=junk[:T, 0:D], in_=HT[:T, :], func=AF.Square, accum_out=ss[:T, :]),
          reads=[HT], writes=[junk, ss])
    rstd_from_ss(kb, ss, rstd, T, 1.0 / D, 1e-6)
    kb.op("dve", lambda e: e.scalar_tensor_tensor(out=hn[:T, :], in0=HT[:T, :], scalar=rstd[:T, :],
                                                  in1=Gb[:T, :], op0=ALU.mult, op1=ALU.mult),
          reads=[HT, rstd, Gb], writes=[hn])
    PST = g.PST
    for kc in range(8):
        kb.op("pe", lambda e: e.transpose(out=PST[:, kc * T:(kc + 1) * T], in_=hn[:T, kc * 128:(kc + 1) * 128],
                                          identity=g.ident_b[:T, :T]),
              reads=[hn, g.ident_b], writes=[PST], inc=(kc == 7))
    kb.op("act", lambda e: e.copy(out=hnT[:, :, :T], in_=PST[:, 0:8 * T].rearrange("p (k t) -> p k t", k=8)),
          reads=[PST], writes=[hnT])


def norm_stats(g, HT, T, Gb, hn, ss, rstd, junk):
    kb = g.kb
    kb.op("act", lambda e: e.activation(out=junk[:T, 0:D], in_=HT[:T, :], func=AF.Square, accum_out=ss[:T, :]),
          reads=[HT], writes=[junk, ss])
    rstd_from_ss(kb, ss, rstd, T, 1.0 / D, 1e-6)
    kb.op("dve", lambda e: e.scalar_tensor_tensor(out=hn[:T, :], in0=HT[:T, :], scalar=rstd[:T, :],
                                                  in1=Gb[:T, :], op0=ALU.mult, op1=ALU.mult),
          reads=[HT, rstd, Gb], writes=[hn])


def norm_transpose(g, hn, hnT, T):
    kb = g.kb
    PST = g.PST
    for kc in range(8):
        kb.op("pe", lambda e: e.transpose(out=PST[:, kc * T:(kc + 1) * T], in_=hn[:T, kc * 128:(kc + 1) * 128],
                                          identity=g.ident_b[:T, :T]),
              reads=[hn, g.ident_b], writes=[PST], inc=(kc == 7))
    kb.op("act", lambda e: e.copy(out=hnT[:, :, :T], in_=PST[:, 0:8 * T].rearrange("p (k t) -> p k t", k=8)),
          reads=[PST], writes=[hnT])


import os


def phase_ffn(g, layer, src, dst, final):
    kb, nc, dr = g.kb, g.nc, g.dr
    Wup = kb.sb([128, 8, 2 * DFF], BF16, "Wup")
    Wdn = kb.sb([128, NFC, D], BF16, "Wdn")
    with contextlib.ExitStack() as ses:
        old = kb.es
        kb.es = ses
        stg = [kb.sb([128, 1408], F32, f"stg{i}") for i in range(3)]
        load_weight_bf16(g, dr["f_w_up"], layer * D, D, 2 * DFF, Wup, stg)
        load_weight_bf16(g, dr["f_w_down"], layer * DFF, DFF, D, Wdn, stg)
        kb.barrier()
        kb.es = old
    Gb = kb.sb([128, D], F32, "Gb")
    kb.dma(Gb[:, :], bc_rows(dr["norm_ffn"], layer, D), writes=[Gb], sem_buf=Gb)
    Gfin = None
    if dst == 4 and final:
        Gfin = kb.sb([128, D], F32, "Gfin")
        kb.dma(Gfin[:, :], bc_rows(dr["norm_final"], 0, D), writes=[Gfin], sem_buf=Gfin)
    CW = kb.sb([128, 3, NFC], F32, "CW")
    CB = kb.sb([128, NFC], F32, "CB")
    vtmp = kb.sb([32, 128], F32, "vtmp")
    for j in range(3):
        load_vec_fm(g, dr["f_conv_w"], layer * 3 + j, NFC, CW, CW[:, j, :], vtmp)
    load_vec_fm(g, dr["f_conv_b"], layer, NFC, CB, CB[:, :], vtmp)
    HTs = [kb.sb([128, D], F32, f"HT{i}") for i in range(3)]
    hns = [kb.sb([128, D], BF16, f"hn{i}") for i in range(2)]
    hnTs = [kb.sb([128, 8, 128], BF16, f"hnT{i}") for i in range(2)]
    junk = kb.sb([128, D], BF16, "junk")
    sss = [kb.sb([128, 1], F32, f"ss{i}") for i in range(3)]
    rstds = [kb.sb([128, 1], F32, f"rstd{i}") for i in range(3)]
    G = kb.sb([128, NFC, 130], F32, "G")
    ACC = [kb.sb([128, 4, 128], F32, f"acc{i}") for i in range(2)]
    SIL = [kb.sb([128, 4, 128], F32, f"sil{i}") for i in range(2)]
    ACTT = kb.sb([128, NFC, 128], BF16, "ACTT")
    kb.op("dve", lambda e: e.memset(G[:, :, :], 0.0), writes=[G])
    po_banks = [g.PS[5], g.PS[6]]
    rot = g.PS[0:5]
    rot_i = [0]

    def next_rot():
        b = rot[rot_i[0] % len(rot)]
        rot_i[0] += 1
        return b

    NS_AT = int(os.environ.get('NS_AT', '1'))
    ntl = len(g.tiles)
    load_h(g, src, 0, HTs[0])
    norm_stats(g, HTs[0], g.tiles[0][1], Gb, hns[0], sss[0], rstds[0], junk)
    if ntl > 1:
        load_h(g, src, 1, HTs[1])
    if ntl > 2:
        load_h(g, src, 2, HTs[2])
    norm_transpose(g, hns[0], hnTs[0], g.tiles[0][1])

    def down_part(c_lo, c_hi, T):
        for c in range(c_lo, c_hi):
            for nb in range(2):
                po = po_banks[nb]
                kb.op("pe", lambda e: e.matmul(po[:T, :], ACTT[:, c, :T], Wdn[:, c, nb * 512:(nb + 1) * 512],
                                               start=(c == 0), stop=(c == NFC - 1)),
                      reads=[ACTT, Wdn], writes=[po], inc=(c == c_hi - 1))

    steps = list(range(0, NFC, 4))
    DEFER = bool(int(os.environ.get("FFN_DEFER", "0")))

    def tile_tail(tj):
        r0j, Tj = g.tiles[tj]
        HTj = HTs[tj % 3]
        down_part(steps[-1], NFC, Tj)
        for nb in range(2):
            kb.op("dve", lambda e: e.tensor_tensor(out=HTj[:Tj, nb * 512:(nb + 1) * 512],
                                                   in0=HTj[:Tj, nb * 512:(nb + 1) * 512], in1=po_banks[nb][:Tj, :], op=ALU.add),
                  reads=[HTj, po_banks[nb]], writes=[HTj])
        store_h(g, dst, tj, HTj, final, Gfin, (sss[(tj + 2) % 3], rstds[(tj + 2) % 3], junk))
        if tj + 3 < ntl:
            load_h(g, src, tj + 3, HTs[tj % 3])

    pending = None
    for ti, (r0, T) in enumerate(g.tiles):
        HT = HTs[ti % 3]
        hnT = hnTs[ti % 2]
        for si, c0 in enumerate(steps):
            nch = min(4, NFC - c0)
            pg = next_rot()
            pv = next_rot()
            for j in range(nch):
                for kc in range(8):
                    kb.op("pe", lambda e: e.matmul(pg[:, j * T:(j + 1) * T],
                                                   Wup[:, kc, DFF + (c0 + j) * 128: DFF + (c0 + j + 1) * 128],
                                                   hnT[:, kc, :T], start=(kc == 0), stop=(kc == 7)),
                          reads=[Wup, hnT], writes=[pg], inc=(kc == 7))
            for j in range(nch):
                for kc in range(8):
                    kb.op("pe", lambda e: e.matmul(pv[:, j * T:(j + 1) * T],
                                                   Wup[:, kc, (c0 + j) * 128:(c0 + j + 1) * 128],
                                                   hnT[:, kc, :T], start=(kc == 0), stop=(kc == 7)),
                          reads=[Wup, hnT], writes=[pv], inc=(kc == 7))
            if si == 0 and pending is not None:
                tile_tail(pending)
                pending = None
            if si >= 1:
                down_part(steps[si - 1], c0, T)
            kb.op("act", lambda e: e.copy(out=G[:, c0:c0 + nch, 2:2 + T],
                                          in_=pg[:, 0:nch * T].rearrange("p (c t) -> p c t", c=nch)),
                  reads=[pg], writes=[G])
            acc = ACC[si % 2]
            sil = SIL[si % 2]
            for j in range(nch):
                c = c0 + j
                kb.op("dve", lambda e: e.tensor_scalar(out=acc[:, j, :T], in0=G[:, c, 2:2 + T],
                                                       scalar1=CW[:, 2, c:c + 1], scalar2=CB[:, c:c + 1],
                                                       op0=ALU.mult, op1=ALU.add),
                      reads=[G, CW, CB], writes=[acc])
                kb.op("dve", lambda e: e.scalar_tensor_tensor(out=acc[:, j, :T], in0=G[:, c, 1:1 + T],
                                                              scalar=CW[:, 1, c:c + 1], in1=acc[:, j, :T],
                                                              op0=ALU.mult, op1=ALU.add),
                      reads=[G, CW, acc], writes=[acc])
                kb.op("dve", lambda e: e.scalar_tensor_tensor(out=acc[:, j, :T], in0=G[:, c, 0:T],
                                                              scalar=CW[:, 0, c:c + 1], in1=acc[:, j, :T],
                                                              op0=ALU.mult, op1=ALU.add),
                      reads=[G, CW, acc], writes=[acc])
            kb.op("act", lambda e: e.activation(out=sil[:, 0:nch, :T], in_=acc[:, 0:nch, :T], func=AF.Silu),
                  reads=[acc], writes=[sil])
            kb.op("dve", lambda e: e.tensor_tensor(out=ACTT[:, c0:c0 + nch, :T], in0=sil[:, 0:nch, :T],
                                                   in1=pv[:, 0:nch * T].rearrange("p (c t) -> p c t", c=nch),
                                                   op=ALU.mult),
                  reads=[sil, pv], writes=[ACTT])
            if si == NS_AT and ti + 1 < ntl:
                Tn = g.tiles[ti + 1][1]
                norm_stats(g, HTs[(ti + 1) % 3], Tn, Gb, hns[(ti + 1) % 2], sss[(ti + 1) % 3], rstds[(ti + 1) % 3], junk)
        if ti + 1 < ntl:
            norm_transpose(g, hns[(ti + 1) % 2], hnTs[(ti + 1) % 2], g.tiles[ti + 1][1])
        kb.op("dve", lambda e: e.tensor_copy(out=G[:, :, 0:2], in_=G[:, :, T:T + 2]), reads=[G], writes=[G])
        if DEFER and ti + 1 < ntl:
            pending = ti
        else:
            tile_tail(ti)


EH = 0.6065306597126334
ISQ = 0.08838834764831845
NEGBIG = -30000.0


def phase_l0(g, src, dst, final):
    import os
    CUT = int(os.environ.get('CUT', '99'))
    SUB = int(os.environ.get('SUB', '99'))
    HFN = int(os.environ.get('HFN', '2'))
    kb, nc, dr = g.kb, g.nc, g.dr
    Win = kb.sb([128, 8, 3848], BF16, "Win")
    Wout = kb.sb([128, 8, D], BF16, "Wout")
    W2A = kb.sb([128, 512], BF16, "W2A")
    G2 = kb.sb([128, 512], BF16, "G2")
    with contextlib.ExitStack() as ses:
        old = kb.es
        kb.es = ses
        stg = [kb.sb([128, 1924], F32, f"stg{i}") for i in range(3)]
        load_weight_bf16(g, dr["e_w_in"], 0, D, 3848, Win, stg)
        load_weight_bf16(g, dr["e_w_out"], 0, D, D, Wout, stg)
        s0 = stg[0]
        kb.dma(s0[0:64, 0:512], dr["r_w2"].ap()[:, :], writes=[s0], sem_buf=s0)
        kb.dma(s0[64:128, 0:512], dr["r_a2"].ap()[:, :], writes=[s0], sem_buf=s0)
        kb.op("dve", lambda e: e.tensor_copy(out=W2A[:, :], in_=s0[:, 0:512]), reads=[s0], writes=[W2A])
        s1 = stg[1]
        kb.dma(s1[:, 0:512], dr["r_g2"].ap()[:, :], writes=[s1], sem_buf=s1)
        kb.op("dve", lambda e: e.tensor_copy(out=G2[:, :], in_=s1[:, 0:512]), reads=[s1], writes=[G2])
        kb.barrier()
        kb.es = old
    F = lambda shape, name: kb.sb(shape, F32, name)
    Bf = lambda shape, name: kb.sb(shape, BF16, name)
    Gb = Bf([128, D], "Gb")
    ue = F([128, 128], "ue")
    su = F([128, 128], "su")
    blk = F([128, 128], "blk")
    kb.dma(ue[:, :], dr["c_ue"].ap()[:, :], writes=[ue], sem_buf=ue)
    kb.dma(su[:, :], dr["c_su"].ap()[:, :], writes=[su], sem_buf=su)
    kb.dma(blk[:, :], dr["c_blk"].ap()[:, :], writes=[blk], sem_buf=blk)
    sl = F([128, 128], "sl")
    kb.op("dve", lambda e: e.tensor_scalar(out=sl[:, :], in0=ue[:, :], scalar1=-1.0, scalar2=1.0, op0=ALU.mult, op1=ALU.add),
          reads=[ue], writes=[sl])
    neg = F([128, 128], "neg")
    kb.op("dve", lambda e: e.tensor_scalar(out=neg[:, :], in0=sl[:, :], scalar1=NEGBIG, scalar2=None, op0=ALU.mult),
          reads=[sl], writes=[neg])
    nblk = F([128, 2], "nblk")
    kb.op("dve", lambda e: e.tensor_scalar(out=nblk[:, 0:1], in0=blk[:, 0:1], scalar1=-1.0, scalar2=None, op0=ALU.mult),
          reads=[blk], writes=[nblk])
    kb.op("dve", lambda e: e.tensor_scalar(out=nblk[:, 1:2], in0=blk[:, 127:128], scalar1=-1.0, scalar2=None, op0=ALU.mult),
          reads=[blk], writes=[nblk])
    blk64 = F([128, 128], "blk64")
    kb.op("dve", lambda e: e.tensor_scalar(out=blk64[:, :], in0=blk[:, :], scalar1=1.0 / 64.0, scalar2=None, op0=ALU.mult),
          reads=[blk], writes=[blk64])
    onesf = F([128, 128], "onesf")
    kb.op("dve", lambda e: e.memset(onesf[:, :], 1.0 / 128.0), writes=[onesf])
    onesb = Bf([128, 128], "onesb")
    kb.op("dve", lambda e: e.memset(onesb[:, :], 1.0), writes=[onesb])
    ones1 = F([128, 128], "ones1")
    kb.op("dve", lambda e: e.memset(ones1[:, :], 1.0), writes=[ones1])
    vtmp = F([32, 128], "vtmp")
    MU = F([128, 14], "MU"); W0 = F([128, 4], "W0"); A0 = F([128, 4], "A0"); KK = F([128, 4], "KK")
    KA = F([128, 4], "KA"); RRK = F([128, 4], "RRK"); LNW = F([128, 4], "LNW"); LNB = F([128, 4], "LNB")
    MN = F([128, 4], "MN")
    load_vec_fm(g, dr["r_mu"], 0, 14, MU, MU[:, :], vtmp)
    for nm, buf in (("r_w0", W0), ("r_a0", A0), ("r_k_k", KK), ("r_k_a", KA), ("r_r_k", RRK), ("r_ln_w", LNW),
                    ("r_ln_b", LNB), ("m_norm", MN)):
        load_vec_fm(g, dr[nm], 0, 4, buf, buf[:, :], vtmp)
    BG = F([128, 8], "BG")
    kb.dma(BG[:, 0:4], bc_rows(dr["m_b_i"], 0, 4), writes=[BG], sem_buf=BG)
    kb.dma(BG[:, 4:8], bc_rows(dr["m_b_f"], 0, 4), writes=[BG], sem_buf=BG)
    C = F([128, 4, 129], "C")
    Cb = Bf([128, 4, 128], "Cb")
    nbc = Bf([128, 4, 128], "nbc")
    ST = F([128, 4, 64], "ST")
    STb = Bf([128, 4, 64], "STb")
    ZR = F([128, 14, 129], "ZR")
    for b_ in (C, ST, ZR):
        kb.op("dve", lambda e: e.memset(b_[:, :, :], 0.0), writes=[b_])
    for b_ in (Cb, nbc, STb):
        kb.op("dve", lambda e: e.memset(b_[:, :, :], 0.0), writes=[b_])
    vTM1 = Bf([128, 4, 129], "vTM1")
    kb.op("dve", lambda e: e.memset(vTM1[:, :, :], 1.0), writes=[vTM1])
    HTs = [F([128, D], f"HT{i}") for i in range(3)]
    hn = Bf([128, D], "hn"); hnT = Bf([128, 8, 128], "hnT"); junk = hn
    ss = F([128, 1], "ss"); rstd = F([128, 1], "rstd")
    qTb = Bf([128, 4, 128], "qTb"); kTb = Bf([128, 4, 128], "kTb"); moT = Bf([128, 4, 128], "moT"); kpbuf = F([128, 4, 128], "kpbuf")
    gx = F([128, 8], "gx"); th = F([128, 8], "th"); ex = F([128, 4], "ex"); LI = F([128, 4], "LI"); LF = F([128, 4], "LF")
    lmb = F([128, 4], "lmb"); LFb = F([128, 4, 128], "LFb"); arg = F([128, 4, 128], "arg"); ET = Bf([128, 4, 128], "ET"); aabuf = Bf([128, 4, 128], "aabuf")
    eB = F([128, 4, 128], "eB"); gcol = F([128, 4], "gcol"); ew = F([128, 4], "ew"); qs = Bf([128, 4, 128], "qs")
    sT = Bf([128, 4, 128], "sT"); kw = Bf([128, 4, 128], "kw")
    cden = F([128, 4, 128], "cden"); hT = F([128, 4, 128], "hT"); hsq = F([128, 4, 128], "hsq"); rs4 = F([128, 4, 128], "rs4")
    mixTs = [Bf([128, 8, 128], f"mixT{i}") for i in range(2)]
    kTMf = Bf([128, 512], "kTMf")
    P1 = F([128, 512], "P1")
    GGs = [Bf([128, 4, 128], f"GG{i}") for i in range(2)]
    for hh_ in range(2):
        kb.dma(P1[:, :], bc_rows(dr["norm_mix"], 0, 512, col0=512 * hh_), writes=[P1], sem_buf=P1)
        kb.op("dve", lambda e: e.tensor_copy(out=Gb[:, 512 * hh_:512 * (hh_ + 1)], in_=P1[:, :]), reads=[P1], writes=[Gb])
    Z2 = Bf([128, 14, 128], "Z2"); D1 = Z2
    LIN = Bf([128, 128], "LIN"); sxg = Bf([128, 128], "sxg")
    sw = arg; aa = aabuf
    kkr = cden; tq = hsq; rn = rs4; kp = kpbuf
    CS = hT; CSp = F([128, 4, 128], "CSp"); csl = F([128, 4], "csl")
    eW = F([128, 4, 128], "eW"); eWp = eB; eWi = F([128, 4, 128], "eWi"); eWT = F([128, 4, 128], "eWT")
    kka = F([128, 4, 128], "kka")
    AR = Bf([128, 4, 2, 128], "AR")
    BH = Bf([128, 4, 128], "BH"); KH = Bf([128, 4, 128], "KH"); vb = Bf([128, 4, 128], "vb")
    bonuss = [Bf([128, 4, 128], f"bonus{i}") for i in range(2)]
    BTm = [Bf([128, 4, 128], f"BTm{i}") for i in range(2)]
    KTm = [Bf([128, 4, 128], f"KTm{i}") for i in range(2)]
    ATm = [Bf([128, 4, 128], f"ATm{i}") for i in range(2)]
    STbd = Bf([128, 4, 128], "STbd")
    kb.op("dve", lambda e: e.memset(STbd[:, :, :], 0.0), writes=[STbd])
    VTM = Bf([128, 8, 64], "VTM"); BHT = Bf([128, 8, 64], "BHT"); KHT = Bf([128, 8, 64], "KHT"); UTM = Bf([128, 8, 64], "UTM")
    Xa = [F([128, 8, 128], "Xa0"), F([128, 8, 128], "Xa1")]
    XTa = [F([128, 8, 128], "XTa0"), F([128, 8, 128], "XTa1")]
    Pm = F([128, 8, 128], "Pm")
    ARB = Bf([128, 8, 128], "ARB"); AAK = Bf([128, 8, 128], "AAK"); ARK = Bf([128, 8, 128], "ARK")

    class SubBuf:
        def __init__(self, parent, lo):
            self.parent, self.lo = parent, lo
            self.w, self.r, self.excl, self.name = parent.w, parent.r, False, parent.name

        def __getitem__(self, idx):
            p, c, t = idx
            if isinstance(c, slice):
                c = slice((c.start or 0) + self.lo, (c.stop if c.stop is not None else 4) + self.lo)
            else:
                c = c + self.lo
            return self.parent.t[p, c, t]

    Of = SubBuf(Xa[1], 0); Osq = SubBuf(Xa[1], 4); mean_s = SubBuf(XTa[1], 0); var = SubBuf(XTa[1], 4)

    def b3(buf, T, n=4):
        return buf[:, 0:n].unsqueeze(2).to_broadcast([128, n, T])

    def mk_alloc(banks):
        st = [0]

        def alloc():
            b = banks[st[0] % len(banks)]
            st[0] += 1
            return b
        return alloc

    _bk = [int(c) for c in os.environ.get("L0_BANKS", "1123")]
    _o = [0, _bk[0], _bk[0] + _bk[1], _bk[0] + _bk[1] + _bk[2], 7]
    nP = mk_alloc(g.PS[_o[0]:_o[1]])
    nM = mk_alloc(g.PS[_o[1]:_o[2]])
    nR = mk_alloc(g.PS[_o[2]:_o[3]])
    nS = mk_alloc(g.PS[_o[3]:_o[4]])

    graw = F([128, 8], "graw")

    def gen_proj(ti):
        r0, T = g.tiles[ti]
        HT = HTs[ti % 3]
        mixT = mixTs[ti % 2]
        GG = GGs[ti % 2]
        bonus = bonuss[ti % 2]
        def proj_fm(pbank, j, col):
            for kc in range(8):
                kb.op("pe", lambda e: e.matmul(pbank[:, j * T:(j + 1) * T], Win[:, kc, col:col + 128], hnT[:, kc, :T],
                                               start=(kc == 0), stop=(kc == 7)),
                      reads=[Win, hnT], writes=[pbank], inc=(kc == 7))

        def proj_tm(pbank, col, n, c0=0):
            for kc in range(8):
                kb.op("pe", lambda e: e.matmul(pbank[:T, c0:c0 + n], hnT[:, kc, :T], Win[:, kc, col:col + n],
                                               start=(kc == 0), stop=(kc == 7)),
                      reads=[hnT, Win], writes=[pbank], inc=(kc == 7))

        def v3(pbank, n=4):
            return pbank[:, 0:n * T].rearrange("p (c t) -> p c t", c=n)

        pq = nP()
        for h in range(4):
            proj_fm(pq, h, h * 128)
        kb.op("act", lambda e: e.copy(out=qTb[:, :, :T], in_=v3(pq)), reads=[pq], writes=[qTb])
        pk = nP()
        for h in range(4):
            proj_fm(pk, h, 512 + h * 128)
        kb.op("act", lambda e: e.copy(out=kTb[:, :, :T], in_=v3(pk)), reads=[pk], writes=[kTb])
        pmo = nP()
        for h in range(4):
            proj_fm(pmo, h, 1536 + h * 128)
        kb.op("act", lambda e: e.activation(out=moT[:, :, :T], in_=v3(pmo), func=AF.Sigmoid), reads=[pmo], writes=[moT])
        pkt = nP()
        proj_tm(pkt, 512, 512)
        kb.op("act", lambda e: e.copy(out=kTMf[:T, :], in_=pkt[:T, :]), reads=[pkt], writes=[kTMf])
        pvt = nP()
        proj_tm(pvt, 1024, 512)
        kb.op("act", lambda e: e.copy(out=vTM1[:T, :, 0:128], in_=pvt[:T, :].rearrange("p (h v) -> p h v", h=4)),
              reads=[pvt], writes=[vTM1])
        pgt = nP()
        proj_tm(pgt, 2048, 8)
        kb.op("act", lambda e: e.copy(out=graw[:T, :], in_=pgt[:T, 0:8]), reads=[pgt], writes=[graw])
        zc = 2056
        for b0, n in ((0, 4), (4, 4), (8, 4), (12, 2)):
            yield
            pz = nP()
            for j in range(n):
                proj_fm(pz, j, zc + (b0 + j) * 128)
            kb.op("act", lambda e: e.copy(out=ZR[:, b0:b0 + n, 1:T + 1], in_=v3(pz, n)), reads=[pz], writes=[ZR])

    def gen_mlstm(ti):
        r0, T = g.tiles[ti]
        HT = HTs[ti % 3]
        mixT = mixTs[ti % 2]
        GG = GGs[ti % 2]
        bonus = bonuss[ti % 2]
        def proj_fm(pbank, j, col):
            for kc in range(8):
                kb.op("pe", lambda e: e.matmul(pbank[:, j * T:(j + 1) * T], Win[:, kc, col:col + 128], hnT[:, kc, :T],
                                               start=(kc == 0), stop=(kc == 7)),
                      reads=[Win, hnT], writes=[pbank], inc=(kc == 7))

        def proj_tm(pbank, col, n, c0=0):
            for kc in range(8):
                kb.op("pe", lambda e: e.matmul(pbank[:T, c0:c0 + n], hnT[:, kc, :T], Win[:, kc, col:col + n],
                                               start=(kc == 0), stop=(kc == 7)),
                      reads=[hnT, Win], writes=[pbank], inc=(kc == 7))

        def v3(pbank, n=4):
            return pbank[:, 0:n * T].rearrange("p (c t) -> p c t", c=n)

        kb.op("dve", lambda e: e.tensor_tensor(out=gx[:T, :], in0=graw[:T, :], in1=BG[:T, :], op=ALU.add),
              reads=[graw, BG], writes=[gx])
        kb.op("act", lambda e: e.activation(out=th[:T, :], in_=gx[:T, :], func=AF.Tanh, scale=1.0 / 15.0), reads=[gx], writes=[th])
        kb.op("dve", lambda e: e.tensor_scalar(out=LI[:T, :], in0=th[:T, 0:4], scalar1=15.0, scalar2=None, op0=ALU.mult),
              reads=[th], writes=[LI])
        kb.op("act", lambda e: e.activation(out=ex[:T, :], in_=th[:T, 4:8], func=AF.Exp, scale=-15.0), reads=[th], writes=[ex])
        kb.op("act", lambda e: e.activation(out=ex[:T, :], in_=ex[:T, :], func=AF.Ln, bias=1.0), reads=[ex], writes=[ex])
        kb.op("dve", lambda e: e.tensor_scalar(out=LF[:T, :], in0=ex[:T, :], scalar1=-1.0, scalar2=None, op0=ALU.mult),
              reads=[ex], writes=[LF])
        yield
        pbc = nM()
        kb.op("pe", lambda e: e.matmul(pbc[:T, 0:4], ue[:T, :T], LF[:T, :], start=True, stop=True), reads=[ue, LF], writes=[pbc])
        kb.op("dve", lambda e: e.tensor_tensor(out=lmb[:T, :], in0=LI[:T, :], in1=pbc[:T, 0:4], op=ALU.subtract),
              reads=[LI, pbc], writes=[lmb])
        kb.op("dve", lambda e: e.tensor_copy(out=LFb[:T, :, :], in_=LF[:T, 0:4].unsqueeze(2).to_broadcast([T, 4, 128])),
              reads=[LF], writes=[LFb])
        yield
        pB = nM()
        for h in range(4):
            kb.op("pe", lambda e: e.matmul(pB[:, h * T:(h + 1) * T], LFb[:T, h, :], ue[:T, :T], start=True, stop=True),
                  reads=[LFb, ue], writes=[pB], inc=(h == 3))
        kb.op("dve", lambda e: e.tensor_tensor(out=arg[:T, :, :T], in0=v3(pB)[:T], in1=neg[:T, :T].unsqueeze(1).to_broadcast([T, 4, T]),
                                               op=ALU.add), reads=[pB, neg], writes=[arg])
        for h in range(4):
            kb.op("act", lambda e: e.activation(out=ET[:T, h, :T], in_=arg[:T, h, :T], func=AF.Exp, bias=lmb[:T, h:h + 1]),
                  reads=[arg, lmb], writes=[ET])
        kb.op("act", lambda e: e.activation(out=eB[:, :, :T], in_=v3(pB), func=AF.Exp), reads=[pB], writes=[eB])
        kb.op("dve", lambda e: e.tensor_copy(out=gcol[:, :], in_=v3(pB)[:, :, T - 1]), reads=[pB], writes=[gcol])
        for h in range(4):
            kb.op("act", lambda e: e.activation(out=ew[:T, h:h + 1], in_=lmb[:T, h:h + 1], func=AF.Exp, bias=gcol[:T, h:h + 1]),
                  reads=[lmb, gcol], writes=[ew])
        kb.op("dve", lambda e: e.tensor_tensor(out=qs[:, :, :T], in0=qTb[:, :, :T], in1=eB[:, :, :T], op=ALU.mult),
              reads=[qTb, eB], writes=[qs])
        yield
        psc = nM()
        for h in range(4):
            kb.op("pe", lambda e: e.matmul(psc[:T, h * T:(h + 1) * T], kTb[:, h, :T], qTb[:, h, :T], start=True, stop=True),
                  reads=[kTb, qTb], writes=[psc], inc=(h == 3))
        kb.op("dve", lambda e: e.scalar_tensor_tensor(out=sT[:T, :, :T], in0=v3(psc)[:T], scalar=ISQ, in1=ET[:T, :, :T],
                                                      op0=ALU.mult, op1=ALU.mult), reads=[psc, ET], writes=[sT])
        yield
        pden = nM()
        for h in range(4):
            kb.op("pe", lambda e: e.matmul(pden[:, h * T:(h + 1) * T], onesb[:T, :], sT[:T, h, :T], start=True, stop=False),
                  reads=[onesb, sT], writes=[pden], inc=False)
            kb.op("pe", lambda e: e.matmul(pden[:, h * T:(h + 1) * T], nbc[:, h, :], qs[:, h, :T], start=False, stop=True),
                  reads=[nbc, qs], writes=[pden])
        kb.op("act", lambda e: e.activation(out=cden[:, :, :T], in_=v3(pden), func=AF.Abs), reads=[pden], writes=[cden])
        kb.op("dve", lambda e: e.tensor_scalar(out=cden[:, :, :T], in0=cden[:, :, :T], scalar1=1.0, scalar2=None, op0=ALU.max),
              reads=[cden], writes=[cden])
        kb.op("dve", lambda e: e.reciprocal(out=cden[:, :, :T], in_=cden[:, :, :T]), reads=[cden], writes=[cden])
        pnum = nM()
        for h in range(4):
            kb.op("pe", lambda e: e.matmul(pnum[:, h * T:(h + 1) * T], vTM1[:T, h, 0:128], sT[:T, h, :T], start=True, stop=False),
                  reads=[vTM1, sT], writes=[pnum], inc=False)
            kb.op("pe", lambda e: e.matmul(pnum[:, h * T:(h + 1) * T], Cb[:, h, :], qs[:, h, :T], start=False, stop=True),
                  reads=[Cb, qs], writes=[pnum])
        kb.op("dve", lambda e: e.tensor_tensor(out=hT[:, :, :T], in0=v3(pnum), in1=cden[:, :, :T], op=ALU.mult),
              reads=[pnum, cden], writes=[hT])
        kb.op("act", lambda e: e.activation(out=hsq[:, :, :T], in_=hT[:, :, :T], func=AF.Square), reads=[hT], writes=[hsq])
        yield
        pss = nM()
        for h in range(4):
            kb.op("pe", lambda e: e.matmul(pss[:, h * T:(h + 1) * T], onesf[:, :], hsq[:, h, :T], start=True, stop=True),
                  reads=[onesf, hsq], writes=[pss], inc=(h == 3))
        kb.op("act", lambda e: e.activation(out=rs4[:, :, :T], in_=v3(pss), func=AF.Sqrt, bias=1e-6), reads=[pss], writes=[rs4])
        kb.op("dve", lambda e: e.reciprocal(out=rs4[:, :, :T], in_=rs4[:, :, :T]), reads=[rs4], writes=[rs4])
        kb.op("dve", lambda e: e.tensor_tensor(out=hT[:, :, :T], in0=hT[:, :, :T], in1=rs4[:, :, :T], op=ALU.mult),
              reads=[hT, rs4], writes=[hT])
        kb.op("dve", lambda e: e.tensor_tensor(out=hT[:, :, :T], in0=hT[:, :, :T], in1=moT[:, :, :T], op=ALU.mult),
              reads=[hT, moT], writes=[hT])
        kb.op("dve", lambda e: e.tensor_tensor(out=mixT[:, 0:4, :T], in0=hT[:, :, :T], in1=b3(MN, T), op=ALU.mult),
              reads=[hT, MN], writes=[mixT])
        yield
        for h in range(4):
            kb.op("dve", lambda e: e.tensor_scalar(out=kw[:T, h, :], in0=kTMf[:T, h * 128:(h + 1) * 128], scalar1=ew[:T, h:h + 1],
                                                   scalar2=ISQ, op0=ALU.mult, op1=ALU.mult), reads=[kTMf, ew], writes=[kw])
        for half in range(2):
            pC = nM()
            for hh in range(2):
                h = half * 2 + hh
                kb.op("pe", lambda e: e.matmul(pC[:, hh * 129:(hh + 1) * 129], kw[:T, h, :], vTM1[:T, h, :], start=True, stop=True),
                      reads=[kw, vTM1], writes=[pC], inc=(hh == 1))
            for hh in range(2):
                h = half * 2 + hh
                kb.op("dve", lambda e: e.scalar_tensor_tensor(out=C[:, h, :], in0=C[:, h, :], scalar=eB[:, h, T - 1:T],
                                                              in1=pC[:, hh * 129:(hh + 1) * 129], op0=ALU.mult, op1=ALU.add),
                      reads=[C, eB, pC], writes=[C])
        yield
        kb.op("act", lambda e: e.copy(out=Cb[:, :, :], in_=C[:, :, 0:128]), reads=[C], writes=[Cb])
        kb.op("dve", lambda e: e.tensor_copy(out=nbc[:, :, :], in_=C[:, :, 128:129].to_broadcast([128, 4, 128])),
              reads=[C], writes=[nbc])


    def gen_prep(ti):
        r0, T = g.tiles[ti]
        HT = HTs[ti % 3]
        mixT = mixTs[ti % 2]
        GG = GGs[ti % 2]
        bonus = bonuss[ti % 2]
        def proj_fm(pbank, j, col):
            for kc in range(8):
                kb.op("pe", lambda e: e.matmul(pbank[:, j * T:(j + 1) * T], Win[:, kc, col:col + 128], hnT[:, kc, :T],
                                               start=(kc == 0), stop=(kc == 7)),
                      reads=[Win, hnT], writes=[pbank], inc=(kc == 7))

        def proj_tm(pbank, col, n, c0=0):
            for kc in range(8):
                kb.op("pe", lambda e: e.matmul(pbank[:T, c0:c0 + n], hnT[:, kc, :T], Win[:, kc, col:col + n],
                                               start=(kc == 0), stop=(kc == 7)),
                      reads=[hnT, Win], writes=[pbank], inc=(kc == 7))

        def v3(pbank, n=4):
            return pbank[:, 0:n * T].rearrange("p (c t) -> p c t", c=n)

        kb.op("dve", lambda e: e.tensor_tensor(out=D1[:, :, :T], in0=ZR[:, :, 0:T], in1=ZR[:, :, 1:T + 1], op=ALU.subtract),
              reads=[ZR], writes=[D1])
        kb.op("dve", lambda e: e.tensor_tensor(out=D1[:, :, :T], in0=D1[:, :, :T], in1=b3(MU, T, 14), op=ALU.mult),
              reads=[D1, MU], writes=[D1])
        kb.op("dve", lambda e: e.tensor_tensor(out=Z2[:, :, :T], in0=D1[:, :, :T], in1=ZR[:, :, 1:T + 1], op=ALU.add),
              reads=[D1, ZR], writes=[Z2])
        kb.op("dve", lambda e: e.tensor_copy(out=ZR[:, :, 0:1], in_=ZR[:, :, T:T + 1]), reads=[ZR], writes=[ZR])
        r_ = Z2[:, 0:4, :T]; k_ = Z2[:, 4:8, :T]; v_ = Z2[:, 8:12, :T]
        yield
        kb.op("act", lambda e: e.activation(out=LIN[0:64, :T], in_=Z2[0:64, 12, :T], func=AF.Tanh), reads=[Z2], writes=[LIN])
        kb.op("act", lambda e: e.copy(out=LIN[64:128, :T], in_=Z2[64:128, 12, :T]), reads=[Z2], writes=[LIN])
        kb.op("act", lambda e: e.activation(out=sxg[:, :T], in_=Z2[:, 13, :T], func=AF.Sigmoid), reads=[Z2], writes=[sxg])
        yield
        pw = nR()
        for c in range(4):
            kb.op("pe", lambda e: e.matmul(pw[:, c * T:(c + 1) * T], W2A[0:64, c * 128:(c + 1) * 128], LIN[0:64, :T], start=True, stop=True),
                  reads=[W2A, LIN], writes=[pw], inc=(c == 3))
        for c in range(4):
            kb.op("act", lambda e: e.activation(out=sw[:, c, :T], in_=pw[:, c * T:(c + 1) * T], func=AF.Sigmoid, bias=W0[:, c:c + 1]),
                  reads=[pw, W0], writes=[sw])
        pa = nR()
        for c in range(4):
            kb.op("pe", lambda e: e.matmul(pa[:, c * T:(c + 1) * T], W2A[64:128, c * 128:(c + 1) * 128], LIN[64:128, :T], start=True, stop=True),
                  reads=[W2A, LIN], writes=[pa], inc=(c == 3))
        for c in range(4):
            kb.op("act", lambda e: e.activation(out=aa[:, c, :T], in_=pa[:, c * T:(c + 1) * T], func=AF.Sigmoid, bias=A0[:, c:c + 1]),
                  reads=[pa, A0], writes=[aa])
        pgg = nR()
        for c in range(4):
            kb.op("pe", lambda e: e.matmul(pgg[:, c * T:(c + 1) * T], G2[:, c * 128:(c + 1) * 128], sxg[:, :T], start=True, stop=True),
                  reads=[G2, sxg], writes=[pgg], inc=(c == 3))
        kb.op("act", lambda e: e.copy(out=GG[:, :, :T], in_=v3(pgg)), reads=[pgg], writes=[GG])
        yield
        kb.op("dve", lambda e: e.tensor_tensor(out=kkr[:, :, :T], in0=k_, in1=b3(KK, T), op=ALU.mult), reads=[Z2, KK], writes=[kkr])
        kb.op("act", lambda e: e.activation(out=tq[:, :, :T], in_=kkr[:, :, :T], func=AF.Square), reads=[kkr], writes=[tq])
        pn = nR()
        for c in range(4):
            kb.op("pe", lambda e: e.matmul(pn[:, c * T:(c + 1) * T], blk[:, :], tq[:, c, :T], start=True, stop=True),
                  reads=[blk, tq], writes=[pn], inc=(c == 3))
        kb.op("act", lambda e: e.activation(out=rn[:, :, :T], in_=v3(pn), func=AF.Sqrt), reads=[pn], writes=[rn])
        kb.op("dve", lambda e: e.tensor_scalar(out=rn[:, :, :T], in0=rn[:, :, :T], scalar1=1e-12, scalar2=None, op0=ALU.max),
              reads=[rn], writes=[rn])
        kb.op("dve", lambda e: e.reciprocal(out=rn[:, :, :T], in_=rn[:, :, :T]), reads=[rn], writes=[rn])
        kb.op("dve", lambda e: e.tensor_tensor(out=kkr[:, :, :T], in0=kkr[:, :, :T], in1=rn[:, :, :T], op=ALU.mult),
              reads=[kkr, rn], writes=[kkr])
        yield
        kb.op("dve", lambda e: e.scalar_tensor_tensor(out=tq[:, :, :T], in0=aa[:, :, :T], scalar=-1.0, in1=b3(KA, T),
                                                      op0=ALU.add, op1=ALU.mult), reads=[aa, KA], writes=[tq])
        kb.op("dve", lambda e: e.scalar_tensor_tensor(out=kp[:, :, :T], in0=tq[:, :, :T], scalar=1.0, in1=k_,
                                                      op0=ALU.add, op1=ALU.mult), reads=[tq, Z2], writes=[kp])
        yield
        for c in range(4):
            kb.op("dve", lambda e: e.tensor_tensor_scan(out=CS[:, c, :T], data0=ones1[:, :T], data1=sw[:, c, :T], initial=0.0,
                                                        op0=ALU.mult, op1=ALU.add), reads=[ones1, sw], writes=[CS])
        kb.op("dve", lambda e: e.tensor_tensor(out=CSp[:, :, :T], in0=CS[:, :, :T], in1=sw[:, :, :T], op=ALU.subtract),
              reads=[CS, sw], writes=[CSp])
        kb.op("dve", lambda e: e.tensor_scalar(out=csl[:, :], in0=CS[:, :, T - 1], scalar1=-EH, scalar2=None, op0=ALU.mult),
              reads=[CS], writes=[csl])
        kb.op("act", lambda e: e.activation(out=eW[:, :, :T], in_=CS[:, :, :T], func=AF.Exp, scale=-EH), reads=[CS], writes=[eW])
        kb.op("act", lambda e: e.activation(out=eWp[:, :, :T], in_=CSp[:, :, :T], func=AF.Exp, scale=-EH), reads=[CSp], writes=[eWp])
        kb.op("act", lambda e: e.activation(out=eWi[:, :, :T], in_=CS[:, :, :T], func=AF.Exp, scale=EH), reads=[CS], writes=[eWi])
        for c in range(4):
            kb.op("act", lambda e: e.activation(out=eWT[:, c, :T], in_=CS[:, c, :T], func=AF.Exp, scale=EH, bias=csl[:, c:c + 1]),
                  reads=[CS, csl], writes=[eWT])
        yield
        kb.op("dve", lambda e: e.scalar_tensor_tensor(out=AR[:, :, 0, :T], in0=kkr[:, :, :T], scalar=-1.0, in1=eWp[:, :, :T],
                                                      op0=ALU.mult, op1=ALU.mult), reads=[kkr, eWp], writes=[AR])
        kb.op("dve", lambda e: e.tensor_tensor(out=AR[:, :, 1, :T], in0=r_, in1=eW[:, :, :T], op=ALU.mult), reads=[Z2, eW], writes=[AR])
        kb.op("dve", lambda e: e.tensor_tensor(out=kka[:, :, :T], in0=kkr[:, :, :T], in1=aa[:, :, :T], op=ALU.mult),
              reads=[kkr, aa], writes=[kka])
        kb.op("dve", lambda e: e.tensor_tensor(out=BH[:, :, :T], in0=kka[:, :, :T], in1=eWT[:, :, :T], op=ALU.mult),
              reads=[kka, eWT], writes=[BH])
        kb.op("dve", lambda e: e.tensor_tensor(out=KH[:, :, :T], in0=kp[:, :, :T], in1=eWT[:, :, :T], op=ALU.mult),
              reads=[kp, eWT], writes=[KH])
        kb.op("act", lambda e: e.copy(out=vb[:, :, :T], in_=v_), reads=[Z2], writes=[vb])
        yield
        kb.op("dve", lambda e: e.tensor_tensor(out=tq[:, :, :T], in0=r_, in1=kp[:, :, :T], op=ALU.mult), reads=[Z2, kp], writes=[tq])
        kb.op("dve", lambda e: e.tensor_tensor(out=tq[:, :, :T], in0=tq[:, :, :T], in1=b3(RRK, T), op=ALU.mult),
              reads=[tq, RRK], writes=[tq])
        prk = nR()
        for c in range(4):
            kb.op("pe", lambda e: e.matmul(prk[:, c * T:(c + 1) * T], blk[:, :], tq[:, c, :T], start=True, stop=True),
                  reads=[blk, tq], writes=[prk], inc=(c == 3))
        kb.op("dve", lambda e: e.tensor_tensor(out=bonus[:, :, :T], in0=v3(prk), in1=v_, op=ALU.mult), reads=[prk, Z2], writes=[bonus])
        yield
        PST = g.PST
        for c in range(4):
            kb.op("pe", lambda e: e.transpose(out=PST[:T, c * 128:(c + 1) * 128], in_=vb[:, c, :T], identity=g.ident_b[:, :]),
                  reads=[vb, g.ident_b], writes=[PST], inc=False)
        for c in range(4):
            kb.op("pe", lambda e: e.transpose(out=PST[:T, (4 + c) * 128:(5 + c) * 128], in_=BH[:, c, :T], identity=g.ident_b[:, :]),
                  reads=[BH, g.ident_b], writes=[PST], inc=(c == 3))
        kb.op("act", lambda e: e.copy(out=VTM[:T, :, :], in_=PST[:T, 0:512].rearrange("p (h v) -> p h v", h=8)), reads=[PST], writes=[VTM])
        kb.op("act", lambda e: e.copy(out=BHT[:T, :, :], in_=PST[:T, 512:1024].rearrange("p (h v) -> p h v", h=8)), reads=[PST], writes=[BHT])
        for c in range(4):
            kb.op("pe", lambda e: e.transpose(out=PST[:T, c * 128:(c + 1) * 128], in_=KH[:, c, :T], identity=g.ident_b[:, :]),
                  reads=[KH, g.ident_b], writes=[PST], inc=(c == 3))
        kb.op("act", lambda e: e.copy(out=KHT[:T, :, :], in_=PST[:T, 0:512].rearrange("p (h v) -> p h v", h=8)), reads=[PST], writes=[KHT])
        yield
        for hf in range(2):
            mcol = blk[:, 127 * hf:127 * hf + 1]
            kb.op("dve", lambda e: e.scalar_tensor_tensor(out=BTm[hf][:, :, :T], in0=kka[:, :, :T], scalar=mcol, in1=eWi[:, :, :T],
                                                          op0=ALU.mult, op1=ALU.mult), reads=[kka, blk, eWi], writes=[BTm[hf]])
            kb.op("dve", lambda e: e.scalar_tensor_tensor(out=KTm[hf][:, :, :T], in0=kp[:, :, :T], scalar=mcol, in1=eWi[:, :, :T],
                                                          op0=ALU.mult, op1=ALU.mult), reads=[kp, blk, eWi], writes=[KTm[hf]])
            kb.op("dve", lambda e: e.scalar_tensor_tensor(out=ATm[hf][:, :, :T], in0=kkr[:, :, :T], scalar=nblk[:, hf:hf + 1], in1=eWp[:, :, :T],
                                                          op0=ALU.mult, op1=ALU.mult), reads=[kkr, nblk, eWp], writes=[ATm[hf]])
        X, XT = Xa[0], XTa[0]
        sub = su[:T, :T].unsqueeze(1).to_broadcast([T, 2, T])
        ueb = ue[:T, :T].unsqueeze(1).to_broadcast([T, 2, T])
        slb = sl[:T, :T].unsqueeze(1).to_broadcast([T, 4, T])
        yield
        for c in range(4):
            yield
            pNA = nR(); pKA = nR()
            for hf in range(2):
                for j in range(2):
                    kb.op("pe", lambda e: e.matmul(pNA[:T, (hf * 2 + j) * T:(hf * 2 + j + 1) * T], BTm[hf][:, c, :T], AR[:, c, j, :T], start=True, stop=True),
                          reads=[BTm[hf], AR], writes=[pNA], inc=(hf == 1 and j == 1))
            for hf in range(2):
                for j in range(2):
                    kb.op("pe", lambda e: e.matmul(pKA[:T, (hf * 2 + j) * T:(hf * 2 + j + 1) * T], KTm[hf][:, c, :T], AR[:, c, j, :T], start=True, stop=True),
                          reads=[KTm[hf], AR], writes=[pKA], inc=(hf == 1 and j == 1))
            na4 = pNA[:T, 0:4 * T].rearrange("p (h j t) -> p h j t", h=2, j=2)
            ka4 = pKA[:T, 0:4 * T].rearrange("p (h j t) -> p h j t", h=2, j=2)
            kb.op("dve", lambda e: e.tensor_tensor(out=X[:T, 2 * c:2 * c + 2, :T], in0=na4[:, :, 0, :], in1=sub, op=ALU.mult),
                  reads=[pNA, su], writes=[X])
            kb.op("dve", lambda e: e.tensor_tensor(out=ARB[:T, 2 * c:2 * c + 2, :T], in0=na4[:, :, 1, :], in1=ueb, op=ALU.mult),
                  reads=[pNA, ue], writes=[ARB])
            kb.op("dve", lambda e: e.tensor_tensor(out=AAK[:T, 2 * c:2 * c + 2, :T], in0=ka4[:, :, 0, :], in1=sub, op=ALU.mult),
                  reads=[pKA, su], writes=[AAK])
            kb.op("dve", lambda e: e.tensor_tensor(out=ARK[:T, 2 * c:2 * c + 2, :T], in0=ka4[:, :, 1, :], in1=ueb, op=ALU.mult),
                  reads=[pKA, ue], writes=[ARK])
        yield
        for half in range(2):
            pNb = nR()
            for j in range(4):
                h = half * 4 + j
                c, hf = h // 2, h % 2
                pl = slice(hf * 64, hf * 64 + 64)
                kb.op("pe", lambda e: e.matmul(pNb[:T, j * T:(j + 1) * T], ATm[hf][:, c, :T], BTm[hf][:, c, :T], start=True, stop=True),
                      reads=[ATm[hf], BTm[hf]], writes=[pNb], inc=(j == 3))
            kb.op("dve", lambda e: e.tensor_tensor(out=XT[:T, half * 4:half * 4 + 4, :T], in0=v3(pNb)[:T], in1=slb, op=ALU.mult),
                  reads=[pNb, sl], writes=[XT])
        kb.op("dve", lambda e: e.tensor_tensor(out=Pm[:T, :, :T], in0=X[:T, :, :T],
                                               in1=g.ident_f[:T, :T].unsqueeze(1).to_broadcast([T, 8, T]), op=ALU.add),
              reads=[X, g.ident_f], writes=[Pm])

    def gen_neumann(ti):
        r0, T = g.tiles[ti]
        HT = HTs[ti % 3]
        mixT = mixTs[ti % 2]
        GG = GGs[ti % 2]
        bonus = bonuss[ti % 2]
        def proj_fm(pbank, j, col):
            for kc in range(8):
                kb.op("pe", lambda e: e.matmul(pbank[:, j * T:(j + 1) * T], Win[:, kc, col:col + 128], hnT[:, kc, :T],
                                               start=(kc == 0), stop=(kc == 7)),
                      reads=[Win, hnT], writes=[pbank], inc=(kc == 7))

        def proj_tm(pbank, col, n, c0=0):
            for kc in range(8):
                kb.op("pe", lambda e: e.matmul(pbank[:T, c0:c0 + n], hnT[:, kc, :T], Win[:, kc, col:col + n],
                                               start=(kc == 0), stop=(kc == 7)),
                      reads=[hnT, Win], writes=[pbank], inc=(kc == 7))

        def v3(pbank, n=4):
            return pbank[:, 0:n * T].rearrange("p (c t) -> p c t", c=n)

        yield
        lv = 1
        cur = 0
        while lv * 2 < T:
            X, XT = Xa[cur], XTa[cur]
            Xn, XTn = Xa[1 - cur], XTa[1 - cur]
            for half in range(2):
                yield
                p1 = nS(); p2 = nS()
                for j in range(4):
                    h = half * 4 + j
                    kb.op("pe", lambda e: e.matmul(p1[:T, j * T:(j + 1) * T], XT[:T, h, :T], X[:T, h, :T], start=True, stop=True),
                          reads=[XT, X], writes=[p1], inc=(j == 3))
                for j in range(4):
                    h = half * 4 + j
                    kb.op("pe", lambda e: e.matmul(p2[:T, j * T:(j + 1) * T], X[:T, h, :T], XT[:T, h, :T], start=True, stop=True),
                          reads=[XT, X], writes=[p2], inc=(j == 3))
                kb.op("act", lambda e: e.copy(out=Xn[:T, half * 4:half * 4 + 4, :T], in_=v3(p1)[:T]), reads=[p1], writes=[Xn])
                kb.op("dve", lambda e: e.tensor_copy(out=XTn[:T, half * 4:half * 4 + 4, :T], in_=v3(p2)[:T]), reads=[p2], writes=[XTn])
            yield
            for half in range(2):
                p3 = nS()
                for j in range(4):
                    h = half * 4 + j
                    kb.op("pe", lambda e: e.matmul(p3[:T, j * T:(j + 1) * T], XTn[:T, h, :T], Pm[:T, h, :T], start=True, stop=True),
                          reads=[XTn, Pm], writes=[p3], inc=(j == 3))
                kb.op("dve", lambda e: e.tensor_tensor(out=Pm[:T, half * 4:half * 4 + 4, :T], in0=Pm[:T, half * 4:half * 4 + 4, :T],
                                                       in1=v3(p3)[:T], op=ALU.add), reads=[Pm, p3], writes=[Pm])
            cur = 1 - cur
            lv *= 2

    def tail1(ti):
        r0, T = g.tiles[ti]
        HT = HTs[ti % 3]
        mixT = mixTs[ti % 2]
        GG = GGs[ti % 2]
        bonus = bonuss[ti % 2]
        def proj_fm(pbank, j, col):
            for kc in range(8):
                kb.op("pe", lambda e: e.matmul(pbank[:, j * T:(j + 1) * T], Win[:, kc, col:col + 128], hnT[:, kc, :T],
                                               start=(kc == 0), stop=(kc == 7)),
                      reads=[Win, hnT], writes=[pbank], inc=(kc == 7))

        def proj_tm(pbank, col, n, c0=0):
            for kc in range(8):
                kb.op("pe", lambda e: e.matmul(pbank[:T, c0:c0 + n], hnT[:, kc, :T], Win[:, kc, col:col + n],
                                               start=(kc == 0), stop=(kc == 7)),
                      reads=[hnT, Win], writes=[pbank], inc=(kc == 7))

        def v3(pbank, n=4):
            return pbank[:, 0:n * T].rearrange("p (c t) -> p c t", c=n)

        cur = ((T.bit_length() - 2) % 2) if T > 2 else 0
        pP1 = nS()
        for h in range(8):
            c, hf = h // 2, h % 2
            pl = slice(hf * 64, hf * 64 + 64)
            kb.op("pe", lambda e: e.matmul(pP1[:T, h * 64:(h + 1) * 64], ATm[hf][:, c, :T], STb[:, c, :], start=True, stop=False),
                  reads=[ATm[hf], STb], writes=[pP1], inc=False)
            kb.op("pe", lambda e: e.matmul(pP1[:T, h * 64:(h + 1) * 64], AAK[:T, h, :T], VTM[:T, h, :], start=False, stop=True),
                  reads=[AAK, VTM], writes=[pP1], inc=(h == 7))
        kb.op("act", lambda e: e.copy(out=P1[:T, :], in_=pP1[:T, :]), reads=[pP1], writes=[P1])
        pU = nS()
        for h in range(8):
            kb.op("pe", lambda e: e.matmul(pU[:T, h * 64:(h + 1) * 64], Pm[:T, h, :T], P1[:T, h * 64:(h + 1) * 64], start=True, stop=True),
                  reads=[Pm, P1], writes=[pU], inc=(h == 7))
        kb.op("act", lambda e: e.copy(out=UTM[:T, :, :], in_=pU[:T, :].rearrange("p (h v) -> p h v", h=8)), reads=[pU], writes=[UTM])
        pO = nS()
        for c in range(4):
            kb.op("pe", lambda e: e.matmul(pO[:, c * T:(c + 1) * T], STbd[:, c, :], AR[:, c, 1, :T], start=True, stop=False),
                  reads=[STbd, AR], writes=[pO], inc=False)
            for hf in range(2):
                h = 2 * c + hf
                pl = slice(hf * 64, hf * 64 + 64)
                kb.op("pe", lambda e: e.matmul(pO[pl, c * T:(c + 1) * T], UTM[:T, h, :], ARB[:T, h, :T], start=False, stop=False),
                      reads=[UTM, ARB], writes=[pO], inc=False)
                kb.op("pe", lambda e: e.matmul(pO[pl, c * T:(c + 1) * T], VTM[:T, h, :], ARK[:T, h, :T], start=False, stop=True),
                      reads=[VTM, ARK], writes=[pO], inc=(hf == 1))
        kb.op("act", lambda e: e.copy(out=Of[:, :, :T], in_=v3(pO)), reads=[pO], writes=[Of])
        pS = nS()
        for h in range(8):
            c, hf = h // 2, h % 2
            pl = slice(hf * 64, hf * 64 + 64)
            kb.op("pe", lambda e: e.matmul(pS[pl, c * 64:(c + 1) * 64], BHT[:T, h, :], UTM[:T, h, :], start=True, stop=False),
                  reads=[BHT, UTM], writes=[pS], inc=False)
            kb.op("pe", lambda e: e.matmul(pS[pl, c * 64:(c + 1) * 64], KHT[:T, h, :], VTM[:T, h, :], start=False, stop=True),
                  reads=[KHT, VTM], writes=[pS], inc=(h == 7))
        for c in range(4):
            kb.op("dve", lambda e: e.scalar_tensor_tensor(out=ST[:, c, :], in0=ST[:, c, :], scalar=eW[:, c, T - 1:T],
                                                          in1=pS[:, c * 64:(c + 1) * 64], op0=ALU.mult, op1=ALU.add),
                  reads=[ST, eW, pS], writes=[ST])
        kb.op("act", lambda e: e.copy(out=STb[:, :, :], in_=ST[:, :, :]), reads=[ST], writes=[STb])
        kb.op("act", lambda e: e.copy(out=STbd[0:64, :, 0:64], in_=ST[0:64, :, :]), reads=[ST], writes=[STbd])
        kb.op("act", lambda e: e.copy(out=STbd[64:128, :, 64:128], in_=ST[64:128, :, :]), reads=[ST], writes=[STbd])

    def gen_tail2(ti):
        r0, T = g.tiles[ti]
        HT = HTs[ti % 3]
        mixT = mixTs[ti % 2]
        GG = GGs[ti % 2]
        bonus = bonuss[ti % 2]
        def proj_fm(pbank, j, col):
            for kc in range(8):
                kb.op("pe", lambda e: e.matmul(pbank[:, j * T:(j + 1) * T], Win[:, kc, col:col + 128], hnT[:, kc, :T],
                                               start=(kc == 0), stop=(kc == 7)),
                      reads=[Win, hnT], writes=[pbank], inc=(kc == 7))

        def proj_tm(pbank, col, n, c0=0):
            for kc in range(8):
                kb.op("pe", lambda e: e.matmul(pbank[:T, c0:c0 + n], hnT[:, kc, :T], Win[:, kc, col:col + n],
                                               start=(kc == 0), stop=(kc == 7)),
                      reads=[hnT, Win], writes=[pbank], inc=(kc == 7))

        def v3(pbank, n=4):
            return pbank[:, 0:n * T].rearrange("p (c t) -> p c t", c=n)

        cur = ((T.bit_length() - 2) % 2) if T > 2 else 0
        kb.op("act", lambda e: e.activation(out=Osq[:, :, :T], in_=Of[:, :, :T], func=AF.Square), reads=[Of], writes=[Osq])
        yield
        pm_ = nS(); pq_ = nS()
        for c in range(4):
            kb.op("pe", lambda e: e.matmul(pm_[:, c * T:(c + 1) * T], blk64[:, :], Of[:, c, :T], start=True, stop=True),
                  reads=[blk64, Of], writes=[pm_], inc=(c == 3))
        for c in range(4):
            kb.op("pe", lambda e: e.matmul(pq_[:, c * T:(c + 1) * T], blk64[:, :], Osq[:, c, :T], start=True, stop=True),
                  reads=[blk64, Osq], writes=[pq_], inc=(c == 3))
        kb.op("act", lambda e: e.copy(out=mean_s[:, :, :T], in_=v3(pm_)), reads=[pm_], writes=[mean_s])
        yield
        kb.op("dve", lambda e: e.scalar_tensor_tensor(out=var[:, :, :T], in0=mean_s[:, :, :T], scalar=-1.0, in1=mean_s[:, :, :T],
                                                      op0=ALU.mult, op1=ALU.mult), reads=[mean_s], writes=[var])
        kb.op("dve", lambda e: e.tensor_tensor(out=var[:, :, :T], in0=var[:, :, :T], in1=v3(pq_), op=ALU.add),
              reads=[var, pq_], writes=[var])
        yield
        kb.op("dve", lambda e: e.tensor_scalar(out=var[:, :, :T], in0=var[:, :, :T], scalar1=0.0, scalar2=None, op0=ALU.max),
              reads=[var], writes=[var])
        kb.op("act", lambda e: e.activation(out=var[:, :, :T], in_=var[:, :, :T], func=AF.Sqrt, bias=64e-5), reads=[var], writes=[var])
        yield
        kb.op("dve", lambda e: e.reciprocal(out=var[:, :, :T], in_=var[:, :, :T]), reads=[var], writes=[var])
        kb.op("dve", lambda e: e.tensor_tensor(out=Of[:, :, :T], in0=Of[:, :, :T], in1=mean_s[:, :, :T], op=ALU.subtract),
              reads=[Of, mean_s], writes=[Of])
        yield
        kb.op("dve", lambda e: e.tensor_tensor(out=Of[:, :, :T], in0=Of[:, :, :T], in1=var[:, :, :T], op=ALU.mult),
              reads=[Of, var], writes=[Of])
        kb.op("dve", lambda e: e.tensor_tensor(out=Of[:, :, :T], in0=Of[:, :, :T], in1=b3(LNW, T), op=ALU.mult),
              reads=[Of, LNW], writes=[Of])
        yield
        kb.op("dve", lambda e: e.tensor_tensor(out=Of[:, :, :T], in0=Of[:, :, :T], in1=b3(LNB, T), op=ALU.add),
              reads=[Of, LNB], writes=[Of])
        kb.op("dve", lambda e: e.tensor_tensor(out=Of[:, :, :T], in0=Of[:, :, :T], in1=bonus[:, :, :T], op=ALU.add),
              reads=[Of, bonus], writes=[Of])
        yield
        kb.op("dve", lambda e: e.tensor_tensor(out=mixT[:, 4:8, :T], in0=Of[:, :, :T], in1=GG[:, :, :T], op=ALU.mult),
              reads=[Of, GG], writes=[mixT])
        for nb in range(2):
            pp = nS()
            for c in range(8):
                kb.op("pe", lambda e: e.matmul(pp[:T, :], mixT[:, c, :T], Wout[:, c, nb * 512:(nb + 1) * 512],
                                               start=(c == 0), stop=(c == 7)), reads=[mixT, Wout], writes=[pp], inc=(c == 7))
            kb.op("dve", lambda e: e.tensor_tensor(out=HT[:T, nb * 512:(nb + 1) * 512], in0=HT[:T, nb * 512:(nb + 1) * 512],
                                                   in1=pp[:T, :], op=ALU.add), reads=[HT, pp], writes=[HT])
        yield
        store_h(g, dst, ti, HT, final, None, (ss, rstd, junk))


    def run(gen):
        for _ in gen:
            pass

    def interleave(a, b, ra=1, rb=1):
        da = db = False
        while not (da and db):
            for _ in range(ra):
                if not da:
                    try:
                        next(a)
                    except StopIteration:
                        da = True
            for _ in range(rb):
                if not db:
                    try:
                        next(b)
                    except StopIteration:
                        db = True

    cost = op_cost

    def norm_and_proj(ti):
        rmsnorm_T(g, HTs[ti % 3], g.tiles[ti][1], Gb, hn, hnT, ss, rstd, junk)
        return gen_proj(ti)

    def s0(ti):
        for _ in gen_neumann(ti):
            pass
        tail1(ti)
        for _ in gen_tail2(ti):
            pass
        if ti + 3 < ntl:
            load_h(g, src, ti + 3, HTs[ti % 3])

    ntl = len(g.tiles)
    for k_ in range(min(3, ntl)):
        load_h(g, src, k_, HTs[k_])
    NOSCHED = bool(os.environ.get("L0_NOSCHED"))

    def rec(f):
        if NOSCHED:
            r = f()
            if r is not None and hasattr(r, "__next__"):
                for _ in r:
                    pass
            return []
        return kb.record(f)

    run(norm_and_proj(0))
    pro = [rec(lambda: gen_mlstm(0)), rec(lambda: gen_prep(0))]
    if ntl > 1:
        pro.append(rec(lambda: norm_and_proj(1)))
    kb.schedule(pro, cost, sync_ns=float(os.environ.get('SYNC_NS', '0')), slack_ns=float(os.environ.get('SLACK_NS', '0')))
    for ti in range(ntl):
        streams = [rec(lambda: s0(ti))]
        if ti + 1 < ntl:
            streams.append(rec(lambda: gen_mlstm(ti + 1)))
            streams.append(rec(lambda: gen_prep(ti + 1)))
        if ti + 2 < ntl:
            streams.append(rec(lambda: norm_and_proj(ti + 2)))
        kb.schedule(streams, cost, sync_ns=float(os.environ.get('SYNC_NS', '0')), slack_ns=float(os.environ.get('SLACK_NS', '0')))


LG = [float(np.log(1.0 - 2.0 ** (-5.0 - h))) for h in range(4)]
TWO_PI = 6.283185307179586
CW1 = 6.28125
CW2 = TWO_PI - CW1


LG = [float(np.log(1.0 - 2.0 ** (-5.0 - h))) for h in range(4)]
TWO_PI = 6.283185307179586
CW1 = 6.28125
CW2 = TWO_PI - CW1


LG = [float(np.log(1.0 - 2.0 ** (-5.0 - h))) for h in range(4)]
TWO_PI = 6.283185307179586
CW1 = 6.28125
CW2 = TWO_PI - CW1


def phase_l1(g, src, dst, final):
    kb, nc, dr = g.kb, g.nc, g.dr
    Win = kb.sb([128, 8, 6144], BF16, "Win")
    Wout = kb.sb([128, 16, D], BF16, "Wout")
    with contextlib.ExitStack() as ses:
        old = kb.es
        kb.es = ses
        stg = [kb.sb([128, 1536], F32, f"stg{i}") for i in range(3)]
        load_weight_bf16(g, dr["o_w_in_p"], 0, D, 6144, Win, stg)
        load_weight_bf16(g, dr["o_w_out"], 0, 2048, D, Wout, stg)
        kb.barrier()
        kb.es = old
    Gb = kb.sb([128, D], BF16, "Gb")
    Gfin = None
    iota = kb.sb([128, 128], F32, "iota")
    pidx = kb.sb([128, 1], F32, "pidx")
    ue = kb.sb([128, 128], F32, "ue")
    inv = kb.sb([128, 1], F32, "inv")
    kb.dma(iota[:, :], dr["c_iota"].ap()[:, :], writes=[iota], sem_buf=iota)
    kb.dma(pidx[:, :], dr["c_pidx"].ap()[:, :], writes=[pidx], sem_buf=pidx)
    kb.dma(ue[:, :], dr["c_ue"].ap()[:, :], writes=[ue], sem_buf=ue)
    kb.dma(inv[:, :], dr["c_inv"].ap()[:, :], writes=[inv], sem_buf=inv)
    DM = kb.sb([128, 4, 128], F32, "DM")
    DEC = kb.sb([128, 4, 128], F32, "DEC")
    KDEC = {128: kb.sb([128, 4], F32, "KDEC128"), 16: kb.sb([128, 4], F32, "KDEC16")}
    tms = kb.sb([128, 128], F32, "tms")
    kb.op("dve", lambda e: e.tensor_scalar(out=tms[:, :], in0=iota[:, :], scalar1=pidx[:, 0:1], scalar2=0.0,
                                           op0=ALU.subtract, op1=ALU.max), reads=[iota, pidx], writes=[tms])
    for h in range(4):
        kb.op("act", lambda e: e.activation(out=DM[:, h, :], in_=tms[:, :], func=AF.Exp, scale=LG[h]),
              reads=[tms], writes=[DM])
        kb.op("dve", lambda e: e.scalar_tensor_tensor(out=DM[:, h, :], in0=DM[:, h, :], scalar=1.0 / 16.0,
                                                      in1=ue[:, :], op0=ALU.mult, op1=ALU.mult),
              reads=[DM, ue], writes=[DM])
        kb.op("act", lambda e: e.activation(out=DEC[:, h, :], in_=iota[:, :], func=AF.Exp, scale=LG[h], bias=LG[h]),
              reads=[iota], writes=[DEC])
        for TT in (128, 16):
            kd = KDEC[TT]
            kb.op("act", lambda e: e.activation(out=kd[:, h:h + 1], in_=pidx[:, 0:1], func=AF.Exp, scale=-LG[h],
                                                bias=LG[h] * (TT - 1)), reads=[pidx], writes=[kd])
            kb.op("dve", lambda e: e.tensor_scalar(out=kd[:, h:h + 1], in0=kd[:, h:h + 1], scalar1=1.0 / 16.0,
                                                   scalar2=None, op0=ALU.mult), reads=[kd], writes=[kd])
    Sr = kb.sb([128, 8, 512], F32, "Sr")
    Srb = kb.sb([128, 8, 512], BF16, "Srb")
    kb.op("dve", lambda e: e.memset(Sr[:, :, :], 0.0), writes=[Sr])
    kb.op("pool", lambda e: e.memset(Srb[:, :, :], 0.0), writes=[Srb])
    HTs = [kb.sb([128, D], F32, f"HT{i}") for i in range(2)]
    hn = kb.sb([128, D], BF16, "hn")
    hnT = kb.sb([128, 8, 128], BF16, "hnT")
    sss = [kb.sb([128, 1], F32, f"ss{i}") for i in range(2)]
    rstds = [kb.sb([128, 1], F32, f"rstd{i}") for i in range(2)]
    ang = kb.sb([128, 128], F32, "ang")
    ang2 = kb.sb([128, 128], F32, "ang2")
    kf = kb.sb([128, 128], F32, "kf")
    ki = kb.sb([128, 128], I32, "ki")
    nsins = [kb.sb([128, 128], F32, f"nsin{i}") for i in range(2)]
    ncoss = [kb.sb([128, 128], F32, f"ncos{i}") for i in range(2)]
    t1 = kb.sb([128, 4, 128], F32, "t1")
    t2 = kb.sb([128, 4, 128], F32, "t2")
    qb = kb.sb([128, 2, 4, 128], BF16, "qb")
    qdb = kb.sb([128, 2, 4, 128], BF16, "qdb")
    kbf = kb.sb([128, 2, 4, 128], BF16, "kbf")
    kdT = kb.sb([128, 8, 128], BF16, "kdT")
    sTm = kb.sb([128, 4, 128], BF16, "sTm")
    VT = kb.sb([128, 2048], BF16, "VT")
    GS = kb.sb([128, 2048], BF16, "GS")
    og = kb.sb([128, 2048], BF16, "og")
    ogT = kb.sb([128, 16, 128], BF16, "ogT")
    st6 = kb.sb([128, 6], F32, "st6")
    mv = kb.sb([128, 2], F32, "mv")
    rs = kb.sb([128, 1], F32, "rs")
    junk = og
    kb.dma(t1[:, :, :].rearrange("p a b -> p (a b)"), bc_rows(dr["norm_mix"], 1, 512), writes=[t1], sem_buf=t1)
    kb.op("dve", lambda e: e.tensor_copy(out=Gb[:, 0:512], in_=t1[:, :, :].rearrange("p a b -> p (a b)")), reads=[t1], writes=[Gb])
    kb.dma(t2[:, :, :].rearrange("p a b -> p (a b)"), bc_rows(dr["norm_mix"], 1, 512, col0=512), writes=[t2], sem_buf=t2)
    kb.op("dve", lambda e: e.tensor_copy(out=Gb[:, 512:1024], in_=t2[:, :, :].rearrange("p a b -> p (a b)")), reads=[t2], writes=[Gb])

    def sincos(dst_tbl, shift, pos0, T):
        kb.op("dve", lambda e: e.tensor_scalar(out=ang[:, :T], in0=iota[:, :T], scalar1=float(pos0), scalar2=inv[:, 0:1],
                                               op0=ALU.add, op1=ALU.mult), reads=[iota, inv], writes=[ang])
        if shift != 0.0:
            kb.op("dve", lambda e: e.tensor_scalar(out=ang[:, :T], in0=ang[:, :T], scalar1=shift, scalar2=None,
                                                   op0=ALU.add), reads=[ang], writes=[ang])
        kb.op("dve", lambda e: e.tensor_scalar(out=ki[:, :T], in0=ang[:, :T], scalar1=1.0 / TWO_PI, scalar2=None,
                                               op0=ALU.mult), reads=[ang], writes=[ki])
        kb.op("dve", lambda e: e.tensor_copy(out=kf[:, :T], in_=ki[:, :T]), reads=[ki], writes=[kf])
        kb.op("dve", lambda e: e.scalar_tensor_tensor(out=ang2[:, :T], in0=kf[:, :T], scalar=-CW1, in1=ang[:, :T],
                                                      op0=ALU.mult, op1=ALU.add), reads=[kf, ang], writes=[ang2])
        kb.op("dve", lambda e: e.scalar_tensor_tensor(out=ang2[:, :T], in0=kf[:, :T], scalar=-CW2, in1=ang2[:, :T],
                                                      op0=ALU.mult, op1=ALU.add), reads=[kf, ang2], writes=[ang2])
        kb.op("dve", lambda e: e.tensor_scalar(out=ang2[:, :T], in0=ang2[:, :T], scalar1=3.1415925, scalar2=-3.1415925,
                                               op0=ALU.min, op1=ALU.max), reads=[ang2], writes=[ang2])
        kb.op("act", lambda e: e.activation(out=dst_tbl[:, :T], in_=ang2[:, :T], func=AF.Sin),
              reads=[ang2], writes=[dst_tbl])

    ntl = len(g.tiles)
    load_h(g, src, 0, HTs[0])
    T0 = g.tiles[0][1]
    norm_stats(g, HTs[0], T0, Gb, hn, sss[0], rstds[0], junk)
    if ntl > 1:
        load_h(g, src, 1, HTs[1])
    norm_transpose(g, hn, hnT, T0)
    sincos(nsins[0], 0.0, g.tiles[0][0], T0)
    sincos(ncoss[0], np.pi / 2, g.tiles[0][0], T0)
    PST = g.PST
    for ti, (r0, T) in enumerate(g.tiles):
        HT = HTs[ti % 2]
        HO = HT
        nsin = nsins[ti % 2]
        ncos = ncoss[ti % 2]
        sb_ = nsin[:, :T].unsqueeze(1).to_broadcast([128, 4, T])
        cb_ = ncos[:, :T].unsqueeze(1).to_broadcast([128, 4, T])
        qk_banks = []
        for which in range(2):
            pe_ = next_ps(g)
            po_ = next_ps(g)
            qk_banks.append((pe_, po_))
            for eo, pb in ((0, pe_), (1, po_)):
                for h in range(4):
                    col = which * 1024 + h * 256 + eo * 128
                    for kc in range(8):
                        kb.op("pe", lambda e: e.matmul(pb[:, h * T:(h + 1) * T], Win[:, kc, col:col + 128],
                                                       hnT[:, kc, :T], start=(kc == 0), stop=(kc == 7)),
                              reads=[Win, hnT], writes=[pb], inc=(kc == 7))
        for which in range(2):
            pe_, po_ = qk_banks[which]
            pe3 = pe_[:, 0:4 * T].rearrange("p (h t) -> p h t", h=4)
            po3 = po_[:, 0:4 * T].rearrange("p (h t) -> p h t", h=4)
            dstb = qb if which == 0 else kbf
            kb.op("dve", lambda e: e.tensor_tensor(out=t1[:, :, :T], in0=pe3, in1=cb_, op=ALU.mult),
                  reads=[pe_, ncos], writes=[t1])
            kb.op("dve", lambda e: e.tensor_tensor(out=t2[:, :, :T], in0=po3, in1=sb_, op=ALU.mult),
                  reads=[po_, nsin], writes=[t2])
            kb.op("dve", lambda e: e.tensor_tensor(out=dstb[:, 0, :, :T], in0=t1[:, :, :T], in1=t2[:, :, :T],
                                                   op=ALU.subtract), reads=[t1, t2], writes=[dstb])
            kb.op("dve", lambda e: e.tensor_tensor(out=t1[:, :, :T], in0=po3, in1=cb_, op=ALU.mult),
                  reads=[po_, ncos], writes=[t1])
            kb.op("dve", lambda e: e.tensor_tensor(out=t2[:, :, :T], in0=pe3, in1=sb_, op=ALU.mult),
                  reads=[pe_, nsin], writes=[t2])
            kb.op("dve", lambda e: e.tensor_tensor(out=dstb[:, 1, :, :T], in0=t1[:, :, :T], in1=t2[:, :, :T],
                                                   op=ALU.add), reads=[t1, t2], writes=[dstb])
            if which == 0:
                for eo in range(2):
                    kb.op("pool", lambda e: e.tensor_tensor(out=qdb[:, eo, :, :T], in0=qb[:, eo, :, :T],
                                                            in1=DEC[:, :, :T], op=ALU.mult),
                          reads=[qb, DEC], writes=[qdb])
        for nb in range(4):
            pvv = next_ps(g)
            for kc in range(8):
                kb.op("pe", lambda e: e.matmul(pvv[:T, :], hnT[:, kc, :T], Win[:, kc, 2048 + nb * 512:2048 + (nb + 1) * 512],
                                               start=(kc == 0), stop=(kc == 7)), reads=[hnT, Win], writes=[pvv], inc=(kc == 7))
            kb.op("act", lambda e: e.copy(out=VT[:T, nb * 512:(nb + 1) * 512], in_=pvv[:T, :]), reads=[pvv], writes=[VT])
        for nb in range(4):
            pgg = next_ps(g)
            for kc in range(8):
                kb.op("pe", lambda e: e.matmul(pgg[:T, :], hnT[:, kc, :T], Win[:, kc, 4096 + nb * 512:4096 + (nb + 1) * 512],
                                               start=(kc == 0), stop=(kc == 7)), reads=[hnT, Win], writes=[pgg], inc=(kc == 7))
            kb.op("act", lambda e: e.activation(out=GS[:T, nb * 512:(nb + 1) * 512], in_=pgg[:T, :], func=AF.Silu),
                  reads=[pgg], writes=[GS])
        if ti + 1 < ntl:
            r0n, Tn = g.tiles[ti + 1]
            sincos(nsins[(ti + 1) % 2], 0.0, r0n, Tn)
            sincos(ncoss[(ti + 1) % 2], np.pi / 2, r0n, Tn)
        for h in range(4):
            for eo in range(2):
                j = h * 2 + eo
                kb.op("pe", lambda e: e.transpose(out=PST[:T, j * 128:(j + 1) * 128], in_=kbf[:, eo, h, :T],
                                                  identity=g.ident_b[:, :]),
                      reads=[kbf, g.ident_b], writes=[PST], inc=(j == 7))
        for h in range(4):
            kb.op("act", lambda e: e.activation(out=kdT[:T, 2 * h:2 * h + 2, :],
                                                in_=PST[:T, 2 * h * 128:(2 * h + 2) * 128].rearrange("p (j d) -> p j d", j=2),
                                                func=AF.Copy, scale=KDEC[T][:T, h:h + 1]),
                  reads=[PST, KDEC[T]], writes=[kdT])
        psc = next_ps(g)
        for h in range(4):
            for eo in range(2):
                kb.op("pe", lambda e: e.matmul(psc[:T, h * T:(h + 1) * T], kbf[:, eo, h, :T], qb[:, eo, h, :T],
                                               start=(eo == 0), stop=(eo == 1)),
                      reads=[kbf, qb], writes=[psc], inc=(eo == 1))
        kb.op("dve", lambda e: e.tensor_tensor(out=sTm[:T, :, :T],
                                               in0=psc[:T, 0:4 * T].rearrange("p (h t) -> p h t", h=4),
                                               in1=DM[:T, :, :T], op=ALU.mult), reads=[psc, DM], writes=[sTm])
        for h in range(4):
            po = next_ps(g)
            kb.op("pe", lambda e: e.matmul(po[:T, :], sTm[:T, h, :T], VT[:T, h * 512:(h + 1) * 512], start=True, stop=False),
                  reads=[sTm, VT], writes=[po], inc=False)
            for eo in range(2):
                kb.op("pe", lambda e: e.matmul(po[:T, :], qdb[:, eo, h, :T], Srb[:, 2 * h + eo, :], start=False, stop=(eo == 1)),
                      reads=[qdb, Srb], writes=[po], inc=(eo == 1))
            kb.op("dve", lambda e: e.bn_stats(out=st6[:T, :], in_=po[:T, :]), reads=[po], writes=[st6])
            kb.op("dve", lambda e: e.bn_aggr(out=mv[:T, :], in_=st6[:T, :]), reads=[st6], writes=[mv])
            kb.op("act", lambda e: e.activation(out=rs[:T, :], in_=mv[:T, 1:2], func=AF.Sqrt, scale=1.0, bias=1e-6),
                  reads=[mv], writes=[rs])
            kb.op("dve", lambda e: e.reciprocal(out=rs[:T, :], in_=rs[:T, :]), reads=[rs], writes=[rs])
            kb.op("dve", lambda e: e.tensor_scalar(out=og[:T, h * 512:(h + 1) * 512], in0=po[:T, :], scalar1=mv[:T, 0:1], scalar2=rs[:T, 0:1],
                                                   op0=ALU.subtract, op1=ALU.mult), reads=[po, mv, rs], writes=[og])
            kb.op("pool", lambda e: e.tensor_tensor(out=og[:T, h * 512:(h + 1) * 512], in0=og[:T, h * 512:(h + 1) * 512],
                                                    in1=GS[:T, h * 512:(h + 1) * 512], op=ALU.mult),
                  reads=[og, GS], writes=[og])
        gT = [float(np.exp(LG[h] * T)) for h in range(4)]
        for h in range(4):
            for eo in range(2):
                j = 2 * h + eo
                pst_ = next_ps(g)
                kb.op("pe", lambda e: e.matmul(pst_[:, :], kdT[:T, j, :], VT[:T, h * 512:(h + 1) * 512], start=True, stop=True),
                      reads=[kdT, VT], writes=[pst_])
                kb.op("dve", lambda e: e.scalar_tensor_tensor(out=Sr[:, j, :], in0=Sr[:, j, :], scalar=gT[h], in1=pst_[:, :],
                                                              op0=ALU.mult, op1=ALU.add), reads=[Sr, pst_], writes=[Sr])
                kb.op("act", lambda e: e.copy(out=Srb[:, j, :], in_=Sr[:, j, :]), reads=[Sr], writes=[Srb])
        for half in range(2):
            for j in range(8):
                c = half * 8 + j
                kb.op("pe", lambda e: e.transpose(out=PST[:, j * T:(j + 1) * T], in_=og[:T, c * 128:(c + 1) * 128],
                                                  identity=g.ident_b[:T, :T]),
                      reads=[og, g.ident_b], writes=[PST], inc=(j == 7))
            kb.op("act", lambda e: e.copy(out=ogT[:, half * 8:(half + 1) * 8, :T],
                                          in_=PST[:, 0:8 * T].rearrange("p (k t) -> p k t", k=8)),
                  reads=[PST], writes=[ogT])
        if ti + 1 < ntl:
            Tn = g.tiles[ti + 1][1]
            norm_stats(g, HTs[(ti + 1) % 2], Tn, Gb, hn, sss[(ti + 1) % 2], rstds[(ti + 1) % 2], junk)
        pps = []
        for nb in range(2):
            pp = next_ps(g)
            pps.append(pp)
            for c in range(16):
                kb.op("pe", lambda e: e.matmul(pp[:T, :], ogT[:, c, :T], Wout[:, c, nb * 512:(nb + 1) * 512],
                                               start=(c == 0), stop=(c == 15)), reads=[ogT, Wout], writes=[pp], inc=(c == 15))
        if ti + 1 < ntl:
            norm_transpose(g, hn, hnT, g.tiles[ti + 1][1])
        for nb in range(2):
            kb.op("dve", lambda e: e.tensor_tensor(out=HO[:T, nb * 512:(nb + 1) * 512],
                                                   in0=HT[:T, nb * 512:(nb + 1) * 512], in1=pps[nb][:T, :], op=ALU.add),
                  reads=[HT, pps[nb]], writes=[HO])
        store_h(g, dst, ti, HO, final, Gfin, (sss[ti % 2], rstds[ti % 2], junk))
        if ti + 2 < ntl:
            load_h(g, src, ti + 2, HTs[ti % 2])


def make_in_map(inputs, b, NT):
    m = {"x": np.ascontiguousarray(inputs["x"][b, :128 * NT])}
    for k, shp in W_SPECS.items():
        src_k = "o_w_in" if k == "o_w_in_p" else k
        m[k] = np.ascontiguousarray(np.asarray(inputs[src_k], np.float32).reshape(shp))
    m.update(host_consts())
    perm = np.arange(6144)
    for sec in range(2):
        for h in range(4):
            base = sec * 1024 + h * 256
            perm[base:base + 256] = np.concatenate([base + np.arange(0, 256, 2), base + np.arange(1, 256, 2)])
    m["o_w_in_p"] = np.ascontiguousarray(m["o_w_in_p"][:, perm])
    return m


def op_cost(o):
    if o[0] == "dma":
        return 60.0
    return _ESC.get(o[1], 1.0) * float(COST_TAB.get(o[6] - host_consts.__code__.co_firstlineno, COST_DEFAULT.get(o[1], 300.0)))


_ESC = {"pe": float(_os.environ.get("PE_SC", "1.0")), "dve": float(_os.environ.get("DVE_SC", "1.0")), "act": float(_os.environ.get("ACT_SC", "1.0"))}
COST_DEFAULT = {"pe": 120.0, "dve": 450.0, "act": 450.0, "pool": 900.0}
COST_TAB = {30: 45.0, 58: 227.0, 145: 959.0, 143: 1471.0, 380: 428.0, 383: 426.0, 397: 227.0, 400: 227.0, 403: 154.0, 405: 60.0, 408: 135.0, 165: 264.0, 411: 139.0, 413: 142.0, 415: 139.0, 167: 181.0, 436: 1025.5, 438: 485.0, 440: 488.0, 457: 427.0, 173: 957.2, 472: 485.0, 153: 1283.0, 155: 163.5, 176: 1285.0, 181: 107.0, 184: 1012.0, 524: 56.0, 541: 586.0, 545: 585.0, 549: 1283.0, 530: 216.0, 552: 585.0, 555: 588.0, 559: 165.0, 591: 97.2, 567: 585.0, 593: 1283.0, 596: 205.0, 594: 171.0, 597: 1283.0, 598: 170.2, 602: 344.8, 603: 159.0, 605: 410.2, 610: 214.0, 612: 692.0, 615: 296.5, 617: 480.0, 618: 161.2, 622: 599.2, 620: 212.8, 628: 123.2, 630: 692.0, 635: 132.8, 637: 111.5, 639: 597.0, 640: 427.0, 645: 133.5, 647: 111.2, 642: 3352.2, 649: 692.0, 651: 629.2, 655: 214.0, 657: 1283.0, 658: 3353.2, 659: 693.2, 661: 692.2, 663: 692.0, 668: 163.2, 674: 267.0, 678: 349.0, 682: 619.0, 683: 426.2, 708: 1934.5, 710: 2026.0, 712: 2025.2, 714: 91.2, 718: 1283.0, 743: 692.0, 719: 294.0, 720: 309.2, 724: 93.2, 731: 39.2, 727: 262.5, 734: 258.0, 738: 123.0, 740: 475.0, 744: 529.0, 747: 214.0, 749: 1283.0, 750: 427.2, 752: 3353.0, 753: 692.5, 757: 602.2, 759: 693.0, 764: 316.0, 770: 1283.0, 766: 692.2, 768: 80.2, 771: 519.0, 772: 520.0, 774: 292.2, 778: 693.0, 780: 601.0, 781: 603.5, 787: 508.0, 783: 693.0, 785: 601.2, 790: 601.5, 791: 692.2, 795: 214.0, 802: 107.0, 797: 692.0, 817: 661.2, 805: 107.0, 819: 662.0, 821: 664.0, 807: 585.0, 808: 489.0, 810: 146.2, 833: 107.0, 837: 107.0, 812: 585.0, 841: 425.0, 843: 332.0, 845: 331.0, 847: 331.0, 856: 123.2, 858: 691.0, 861: 1132.0, 897: 112.0, 901: 112.0, 903: 585.0, 904: 692.0, 910: 213.0, 912: 689.0, 944: 78.0, 946: 77.2, 948: 585.0, 951: 112.0, 957: 221.0, 953: 585.0, 962: 246.0, 964: 99.0, 972: 47.0, 974: 47.0, 966: 585.0, 977: 281.0, 980: 405.0, 981: 306.0, 982: 401.0, 1007: 629.0, 1011: 214.0, 1014: 214.0, 1016: 586.0, 1018: 692.2, 1020: 689.0, 1023: 427.2, 1025: 1283.0, 1027: 3353.0, 1028: 692.2, 1031: 692.2, 1033: 692.0, 1036: 692.2, 1038: 692.0, 1041: 692.0, 1047: 427.0, 1049: 687.5, 191: 950.0, 248: 2442.0, 194: 1284.0, 204: 56.0, 207: 1012.0, 303: 56.0, 309: 56.0, 318: 474.2, 325: 259.0, 329: 352.0, 333: 351.0, 337: 630.0, 339: 689.0, 273: 427.0, 349: 100.2, 286: 689.0, 1189: 3509.0, 1169: 1007.0, 1172: 309.0, 1177: 199.2, 1181: 94.0, 1174: 293.0, 1183: 153.2, 1188: 3472.0, 1217: 280.0, 1219: 327.0, 1223: 197.2, 1228: 228.0, 1230: 227.0, 1231: 291.2, 1267: 56.0, 1233: 293.0, 1235: 227.2, 1237: 310.0, 1226: 227.0, 1297: 216.0, 1276: 691.2, 1278: 691.2, 1280: 692.0, 1282: 602.0, 1284: 598.2, 1286: 692.2, 1290: 1643.0, 1299: 585.0, 1303: 216.0, 1305: 597.0, 1316: 56.0, 1328: 56.0, 1331: 629.8, 1320: 455.2, 1337: 426.0, 1340: 426.2, 1342: 627.8, 1343: 182.0, 1344: 203.0, 1358: 584.0, 1346: 164.0, 1347: 810.0, 1349: 1152.2, 1360: 689.0, 1362: 618.0, 1367: 107.0, 1370: 1012.0, 1382: 426.0, 1387: 689.0, 116: 1056.8, 119: 1285.5}


NT_FULL = 32


def kernel(**inputs):
    nc = build(NT_FULL, phases=(1, 2, 3, 4), debug=False, final=True)
    in_maps = [make_in_map(inputs, b, NT_FULL) for b in range(8)]
    res = run_bass_kernel_spmd(nc, in_maps, core_ids=list(range(8)))
    return np.stack([np.asarray(r["out"], np.float32) for r in res.results], axis=0)
```

```python
import contextlib
import numpy as np
import concourse.bass as bass
import concourse.mybir as mybir

import os as _os_fw
_STRICT = bool(_os_fw.environ.get("KB_STRICT"))
_NPE_BIAS = float(_os_fw.environ.get("NPE_BIAS", "0"))
_CARRY = bool(int(_os_fw.environ.get("SCHED_CARRY", "0")))
F32 = mybir.dt.float32
BF16 = mybir.dt.bfloat16
I32 = mybir.dt.int32
AF = mybir.ActivationFunctionType
ALU = mybir.AluOpType
AX = mybir.AxisListType


class Buf:
    __slots__ = ("t", "w", "r", "dsem", "dcount", "name", "excl")

    def __init__(self, t, name=""):
        self.t = t
        self.w = {}
        self.r = {}
        self.dsem = None
        self.dcount = 0
        self.name = name
        self.excl = False

    def __getitem__(self, idx):
        return self.t[idx]


class _Cap:
    def __init__(self):
        self.call = None

    def __getattr__(self, name):
        def f(*args, **kwargs):
            self.call = (name, args, kwargs)
            return None
        return f


class Eng:
    def __init__(self, name, obj, sem):
        self.name = name
        self.obj = obj
        self.sem = sem
        self.count = 0
        self.seen = {}


class KB:
    def __init__(self, nc, es):
        self.nc = nc
        self.es = es
        self.sems = {}
        self.E = {}
        for name, obj in (("pe", nc.tensor), ("act", nc.scalar), ("dve", nc.vector),
                          ("pool", nc.gpsimd), ("sp", nc.sync)):
            sem = es.enter_context(nc.semaphore("s_" + name))
            self.E[name] = Eng(name, obj, sem)
            self.sems[id(sem)] = sem
        self.dma_tokens = {}
        self.nbuf = 0
        self.rec = None

    def sb(self, shape, dt, name=None):
        self.nbuf += 1
        name = f"{name or 'b'}_{self.nbuf}"
        t = self.es.enter_context(self.nc.sbuf_tensor(name, list(shape), dt))
        return Buf(t, name)

    def ps(self, shape, dt, name=None):
        self.nbuf += 1
        name = f"{name or 'p'}_{self.nbuf}"
        t = self.es.enter_context(self.nc.psum_tensor(name, list(shape), dt))
        b = Buf(t, name)
        b.excl = True
        return b

    def newsem(self, name):
        sem = self.es.enter_context(self.nc.semaphore(name))
        self.sems[id(sem)] = sem
        return sem

    def _wait(self, e, deps):
        for sid, val in deps.items():
            if e.seen.get(sid, 0) < val:
                e.obj.wait_ge(self.sems[sid], val)
                e.seen[sid] = val

    def _deps(self, e, reads, writes):
        deps = {}
        own = id(e.sem)
        for b in reads:
            for sid, v in b.w.items():
                if deps.get(sid, 0) < v:
                    deps[sid] = v
            if b.excl:
                for sid, v in b.r.items():
                    if sid != own and deps.get(sid, 0) < v:
                        deps[sid] = v
        skip_own = True if not _STRICT else (e.name == "pe")
        for b in writes:
            for d in (b.w, b.r):
                for sid, v in d.items():
                    if sid == own and skip_own:
                        continue
                    if deps.get(sid, 0) < v:
                        deps[sid] = v
        return deps

    def op(self, eng, fn, reads=(), writes=(), inc=True):
        if self.rec is not None:
            import sys as _sys
            cap = _Cap()
            fn(cap)
            name, args, kwargs = cap.call
            fn2 = (lambda e_, name=name, args=args, kwargs=kwargs: getattr(e_, name)(*args, **kwargs))
            self.rec.append(("op", eng, fn2, tuple(reads), tuple(writes), inc, _sys._getframe(1).f_lineno))
            return None
        e = self.E[eng]
        self._wait(e, self._deps(e, reads, writes))
        ins = fn(e.obj)
        if inc:
            e.count += 1
            ins.then_inc(e.sem, 1)
            val = e.count
        else:
            val = e.count + 1
        sid = id(e.sem)
        for b in reads:
            if b.r.get(sid, 0) < val:
                b.r[sid] = val
        for b in writes:
            if b.w.get(sid, 0) < val:
                b.w[sid] = val
        return ins

    def dma(self, out_ap, in_ap, reads=(), writes=(), sem_buf=None, q="sp"):
        if self.rec is not None:
            import sys as _sys
            self.rec.append(("dma", q, (out_ap, in_ap, sem_buf), tuple(reads), tuple(writes), True, _sys._getframe(1).f_lineno))
            return None
        e = self.E[q]
        self._wait(e, self._deps(e, reads, writes))
        b = sem_buf
        if b.dsem is None:
            b.dsem = self.newsem("d_" + b.name)
        b.dcount += 16
        e.obj.dma_start(out=out_ap, in_=in_ap).then_inc(b.dsem, 16)
        sid = id(b.dsem)
        for x in reads:
            x.r[sid] = b.dcount
        for x in writes:
            x.w[sid] = b.dcount
        self.dma_tokens[sid] = b.dcount

    def barrier(self):
        targets = {id(e.sem): e.count for e in self.E.values() if e.count > 0}
        targets.update(self.dma_tokens)
        for e in self.E.values():
            self._wait(e, {k: v for k, v in targets.items() if k != id(e.sem)})

    def final_wait(self):
        e = self.E["sp"]
        self._wait(e, dict(self.dma_tokens))

    def record(self, fn):
        assert self.rec is None
        self.rec = []
        try:
            r = fn()
            if r is not None and hasattr(r, "__next__"):
                for _ in r:
                    pass
        finally:
            ops, self.rec = self.rec, None
        return ops

    def schedule(self, streams, cost_fn, sync_ns=0.0, slack_ns=0.0):
        streams = [list(x) for x in streams if x]
        n = len(streams)
        key = lambda b: id(b.w)
        rem_r = [dict() for _ in range(n)]
        rem_w = [dict() for _ in range(n)]
        for k, st in enumerate(streams):
            for o in st:
                for b in o[3]:
                    rem_r[k][key(b)] = rem_r[k].get(key(b), 0) + 1
                for b in o[4]:
                    rem_w[k][key(b)] = rem_w[k].get(key(b), 0) + 1
        pos = [0] * n
        if _CARRY and getattr(self, "_sst", None) is not None:
            eng_t, w_t, r_t, w_e = self._sst
        else:
            eng_t, w_t, r_t, w_e = {}, {}, {}, {}
            self._sst = (eng_t, w_t, r_t, w_e)
        order = []
        total = sum(len(x) for x in streams)
        while len(order) < total:
            best = None
            cands = []
            for k in range(n):
                if pos[k] >= len(streams[k]):
                    continue
                o = streams[k][pos[k]]
                ok = True
                for j in range(k):
                    if pos[j] >= len(streams[j]):
                        continue
                    for b in o[4]:
                        kk_ = key(b)
                        if rem_r[j].get(kk_, 0) or rem_w[j].get(kk_, 0):
                            ok = False
                            break
                    if ok:
                        for b in o[3]:
                            if rem_w[j].get(key(b), 0):
                                ok = False
                                break
                    if not ok:
                        break
                if not ok:
                    continue
                eng = o[1]
                t = eng_t.get(eng, 0.0)
                for b in o[3]:
                    kk_ = key(b)
                    tw = w_t.get(kk_, 0.0) + (sync_ns if w_e.get(kk_) != eng else 0.0)
                    if tw > t:
                        t = tw
                for b in o[4]:
                    kk_ = key(b)
                    tw = max(w_t.get(kk_, 0.0), r_t.get(kk_, 0.0)) + sync_ns
                    if tw > t:
                        t = tw
                tk = t + (0.0 if eng == "pe" else _NPE_BIAS)
                if best is None or tk < best[3] - 1e-9:
                    best = (t, k, o, tk)
                cands.append((t, k, o, tk))
            if slack_ns > 0:
                for c_ in cands:
                    if c_[0] <= best[0] + slack_ns:
                        best = c_
                        break
            t, k, o = best[0], best[1], best[2]
            dur = cost_fn(o)
            eng = o[1]
            end = t + dur
            eng_t[eng] = end if o[0] == "op" else t + 60.0
            for b in o[3]:
                kk_ = key(b)
                r_t[kk_] = max(r_t.get(kk_, 0.0), end)
                rem_r[k][kk_] -= 1
            for b in o[4]:
                kk_ = key(b)
                w_t[kk_] = end
                w_e[kk_] = eng
                rem_w[k][kk_] -= 1
            pos[k] += 1
            order.append(o)
        for o in order:
            if o[0] == "op":
                self.op(o[1], o[2], reads=o[3], writes=o[4], inc=o[5])
            else:
                out_ap, in_ap, sem_buf = o[2]
                self.dma(out_ap, in_ap, reads=o[3], writes=o[4], sem_buf=sem_buf, q=o[1])
        return max(eng_t.values()) if eng_t else 0.0


from concourse.bass_utils import run_bass_kernel_spmd

D = 1024
NMETA = 16
DFF = 2816
NFC = DFF // 128

W_SPECS = {
    "meta_tokens": (16, 1024), "norm_mix": (2, 1024), "norm_ffn": (2, 1024), "norm_final": (1, 1024),
    "e_w_in": (1024, 3848), "e_w_out": (1024, 1024), "m_b_i": (1, 4), "m_b_f": (1, 4), "m_norm": (1, 512),
    "r_mu": (1, 1792), "r_w0": (1, 512), "r_w2": (64, 512), "r_a0": (1, 512), "r_a2": (64, 512),
    "r_g2": (128, 512), "r_k_k": (1, 512), "r_k_a": (1, 512), "r_r_k": (1, 512), "r_ln_w": (1, 512),
    "r_ln_b": (1, 512), "o_w_in_p": (1024, 6144), "o_w_out": (2048, 1024), "f_w_up": (2048, 5632),
    "f_conv_w": (6, 2816), "f_conv_b": (2, 2816), "f_w_down": (5632, 1024),
}


def host_consts():
    c = {}
    c["c_ident"] = np.eye(128, dtype=np.float32)
    i = np.arange(128)
    c["c_ue"] = (i[:, None] <= i[None, :]).astype(np.float32)
    c["c_su"] = (i[:, None] < i[None, :]).astype(np.float32)
    c["c_iota"] = np.broadcast_to(np.arange(128, dtype=np.float32)[None, :], (128, 128)).copy()
    c["c_pidx"] = np.arange(128, dtype=np.float32)[:, None].copy()
    bo = np.zeros((128, 128), np.float32)
    bo[:64, :64] = 1.0
    bo[64:, 64:] = 1.0
    c["c_blk"] = bo
    c["c_inv"] = (np.float32(1.0) / np.power(np.float32(10000.0), np.linspace(0.0, 1.0, 128, dtype=np.float32))
                  ).astype(np.float32)[:, None].copy()
    return c


class Ctx:
    pass


def tile_rows(NT):
    tiles = [(0, NMETA)]
    for i in range(NT):
        tiles.append((NMETA + 128 * i, 128))
    return tiles


def build(NT, phases=(1, 2, 3, 4), debug=False, final=True):
    nc = bass.Bass("TRN2", target_bir_lowering=False)
    SEQ = 128 * NT
    L = NMETA + SEQ
    dr = {}
    dr["x"] = nc.dram_tensor("x", [SEQ, D], F32, kind="ExternalInput")
    for k, shp in W_SPECS.items():
        dr[k] = nc.dram_tensor(k, list(shp), F32, kind="ExternalInput")
    for k, v in host_consts().items():
        dr[k] = nc.dram_tensor(k, list(v.shape), F32, kind="ExternalInput")
    out = nc.dram_tensor("out", [SEQ, D], F32, kind="ExternalOutput")
    H = {}
    for i in (1, 2, 3):
        H[i] = nc.dram_tensor(f"H{i}", [L, D], F32, kind=("ExternalOutput" if debug else "Internal"))

    tiles = tile_rows(NT)
    es = contextlib.ExitStack()
    with es:
        kb = KB(nc, es)
        PS = [kb.ps([128, 512], F32, f"psb{i}") for i in range(7)]
        PST = kb.ps([128, 1024], BF16, "pstr")
        g = Ctx()
        g.nc, g.kb, g.dr, g.H, g.out, g.tiles, g.PS, g.PST = nc, kb, dr, H, out, tiles, PS, PST
        g.psi = 0
        g.ident_f = kb.sb([128, 128], F32, "ident_f")
        g.ident_b = kb.sb([128, 128], BF16, "ident_b")
        kb.dma(g.ident_f[:, :], dr["c_ident"].ap()[:, :], writes=[g.ident_f], sem_buf=g.ident_f)
        kb.op("dve", lambda e: e.tensor_copy(out=g.ident_b[:, :], in_=g.ident_f[:, :]),
              reads=[g.ident_f], writes=[g.ident_b])

        plist = [p for p in (1, 2, 3, 4) if p in phases]
        src = 0
        for p in plist:
            dst = p if p != plist[-1] else 4
            with contextlib.ExitStack() as pes:
                kb.es = pes
                if p in (2, 4):
                    phase_ffn(g, layer=(0 if p == 2 else 1), src=src, dst=dst, final=final)
                elif p == 1:
                    phase_l0(g, src=src, dst=dst, final=final)
                elif p == 3:
                    phase_l1(g, src=src, dst=dst, final=final)
                kb.barrier()
            kb.es = es
            src = dst
        kb.final_wait()
    return nc


def next_ps(g):
    b = g.PS[g.psi % len(g.PS)]
    g.psi += 1
    return b


def bc_rows(handle, row, n, parts=128, col0=0, ncols_total=None):
    ncols_total = ncols_total if ncols_total is not None else handle.shape[1]
    return bass.AP(handle, row * ncols_total + col0, [[0, parts], [1, n]])


def load_h(g, src, ti, HT):
    kb = g.kb
    r0, T = g.tiles[ti]
    if src == 0:
        if ti == 0:
            ap = g.dr["meta_tokens"].ap()[0:NMETA, :]
        else:
            ap = g.dr["x"].ap()[r0 - NMETA:r0 - NMETA + T, :]
    else:
        ap = g.H[src].ap()[r0:r0 + T, :]
    kb.dma(HT[:T, :], ap, writes=[HT], sem_buf=HT)


def store_h(g, dst, ti, HO, final, Gfin=None, scratch=None):
    kb = g.kb
    r0, T = g.tiles[ti]
    if dst != 4:
        kb.dma(g.H[dst].ap()[r0:r0 + T, :], HO[:T, :], reads=[HO], sem_buf=HO)
        return
    if ti == 0:
        return
    if final:
        ss, rstd, junk = scratch
        kb.op("act", lambda e: e.activation(out=junk[:T, 0:D], in_=HO[:T, :], func=AF.Square, accum_out=ss[:T, :]),
              reads=[HO], writes=[junk, ss])
        rstd_from_ss(kb, ss, rstd, T, 1.0 / D, 1e-6)
        kb.op("dve", lambda e: e.scalar_tensor_tensor(out=HO[:T, :], in0=HO[:T, :], scalar=rstd[:T, :],
                                                      in1=Gfin[:T, :], op0=ALU.mult, op1=ALU.mult),
              reads=[HO, rstd, Gfin], writes=[HO])
    kb.dma(g.out.ap()[r0 - NMETA:r0 - NMETA + T, :], HO[:T, :], reads=[HO], sem_buf=HO)


import os as _os
DMAQ_N = int(_os.environ.get("DMAQ_N", "1"))


def load_weight_bf16(g, dram_handle, row0, K, N, W, stg, col0=0, ncols_total=None):
    kb = g.kb
    SW = stg[0].t.shape[1]
    engs = ("dve", "act", "dve", "act", "dve", "act", "dve")
    cnt = getattr(g, "_lw_cnt", 0)
    for kc in range(K // 128):
        for j0 in range(0, N, SW):
            w = min(SW, N - j0)
            s = stg[cnt % len(stg)]
            kb.dma(s[:, :w], dram_handle.ap()[row0 + kc * 128: row0 + (kc + 1) * 128, col0 + j0: col0 + j0 + w],
                   writes=[s], sem_buf=s, q=(("sp", "pool", "act")[cnt % DMAQ_N] if DMAQ_N > 1 else "sp"))
            en = engs[cnt % len(engs)]
            if en == "act":
                kb.op("act", lambda e: e.copy(out=W[:, kc, j0:j0 + w], in_=s[:, :w]), reads=[s], writes=[W])
            else:
                kb.op(en, lambda e: e.tensor_copy(out=W[:, kc, j0:j0 + w], in_=s[:, :w]), reads=[s], writes=[W])
            cnt += 1
    g._lw_cnt = cnt


def rstd_from_ss(kb, ss, rstd, T, scale, eps, ap_fn=None):
    a = (lambda b: b[:T, :]) if ap_fn is None else ap_fn
    kb.op("act", lambda e: e.activation(out=a(rstd), in_=a(ss), func=AF.Sqrt, scale=scale, bias=eps),
          reads=[ss], writes=[rstd])
    kb.op("dve", lambda e: e.reciprocal(out=a(rstd), in_=a(rstd)), reads=[rstd], writes=[rstd])


def load_vec_fm(g, handle, row, nch, dstbuf, dst_ap, vtmp, col0=0):
    kb = g.kb
    ncols = handle.shape[1]
    src = bass.AP(handle, row * ncols + col0, [[128, nch], [1, 128]])
    kb.dma(vtmp[:nch, :], src, writes=[vtmp], sem_buf=vtmp)
    pt = next_ps(g)
    kb.op("pe", lambda e: e.transpose(out=pt[:, :nch], in_=vtmp[:nch, :], identity=g.ident_f[:nch, :nch]),
          reads=[vtmp, g.ident_f], writes=[pt])
    kb.op("dve", lambda e: e.tensor_copy(out=dst_ap, in_=pt[:, :nch]), reads=[pt], writes=[dstbuf])


def rmsnorm_T(g, HT, T, Gb, hn, hnT, ss, rstd, junk):
    kb = g.kb
    kb.op("act", lambda e: e.activation(out=junk[:T, 0:D], in_=HT[:T, :], func=AF.Square, accum_out=ss[:T, :]),
          reads=[HT], writes=[junk, ss])
    rstd_from_ss(kb, ss, rstd, T, 1.0 / D, 1e-6)
    kb.op("dve", lambda e: e.scalar_tensor_tensor(out=hn[:T, :], in0=HT[:T, :], scalar=rstd[:T, :],
                                                  in1=Gb[:T, :], op0=ALU.mult, op1=ALU.mult),
          reads=[HT, rstd, Gb], writes=[hn])
    PST = g.PST
    for kc in range(8):
        kb.op("pe", lambda e: e.transpose(out=PST[:, kc * T:(kc + 1) * T], in_=hn[:T, kc * 128:(kc + 1) * 128],
                                          identity=g.ident_b[:T, :T]),
              reads=[hn, g.ident_b], writes=[PST], inc=(kc == 7))
    kb.op("act", lambda e: e.copy(out=hnT[:, :, :T], in_=PST[:, 0:8 * T].rearrange("p (k t) -> p k t", k=8)),
          reads=[PST], writes=[hnT])


def norm_stats(g, HT, T, Gb, hn, ss, rstd, junk):
    kb = g.kb
    kb.op("act", lambda e: e.activation(out=junk[:T, 0:D], in_=HT[:T, :], func=AF.Square, accum_out=ss[:T, :]),
          reads=[HT], writes=[junk, ss])
    rstd_from_ss(kb, ss, rstd, T, 1.0 / D, 1e-6)
    kb.op("dve", lambda e: e.scalar_tensor_tensor(out=hn[:T, :], in0=HT[:T, :], scalar=rstd[:T, :],
                                                  in1=Gb[:T, :], op0=ALU.mult, op1=ALU.mult),
          reads=[HT, rstd, Gb], writes=[hn])


def norm_transpose(g, hn, hnT, T):
    kb = g.kb
    PST = g.PST
    for kc in range(8):
        kb.op("pe", lambda e: e.transpose(out=PST[:, kc * T:(kc + 1) * T], in_=hn[:T, kc * 128:(kc + 1) * 128],
                                          identity=g.ident_b[:T, :T]),
              reads=[hn, g.ident_b], writes=[PST], inc=(kc == 7))
    kb.op("act", lambda e: e.copy(out=hnT[:, :, :T], in_=PST[:, 0:8 * T].rearrange("p (k t) -> p k t", k=8)),
          reads=[PST], writes=[hnT])


import os


def phase_ffn(g, layer, src, dst, final):
    kb, nc, dr = g.kb, g.nc, g.dr
    Wup = kb.sb([128, 8, 2 * DFF], BF16, "Wup")
    Wdn = kb.sb([128, NFC, D], BF16, "Wdn")
    with contextlib.ExitStack() as ses:
        old = kb.es
        kb.es = ses
        stg = [kb.sb([128, 1408], F32, f"stg{i}") for i in range(3)]
        load_weight_bf16(g, dr["f_w_up"], layer * D, D, 2 * DFF, Wup, stg)
        load_weight_bf16(g, dr["f_w_down"], layer * DFF, DFF, D, Wdn, stg)
        kb.barrier()
        kb.es = old
    Gb = kb.sb([128, D], F32, "Gb")
    kb.dma(Gb[:, :], bc_rows(dr["norm_ffn"], layer, D), writes=[Gb], sem_buf=Gb)
    Gfin = None
    if dst == 4 and final:
        Gfin = kb.sb([128, D], F32, "Gfin")
        kb.dma(Gfin[:, :], bc_rows(dr["norm_final"], 0, D), writes=[Gfin], sem_buf=Gfin)
    CW = kb.sb([128, 3, NFC], F32, "CW")
    CB = kb.sb([128, NFC], F32, "CB")
    vtmp = kb.sb([32, 128], F32, "vtmp")
    for j in range(3):
        load_vec_fm(g, dr["f_conv_w"], layer * 3 + j, NFC, CW, CW[:, j, :], vtmp)
    load_vec_fm(g, dr["f_conv_b"], layer, NFC, CB, CB[:, :], vtmp)
    HTs = [kb.sb([128, D], F32, f"HT{i}") for i in range(3)]
    hns = [kb.sb([128, D], BF16, f"hn{i}") for i in range(2)]
    hnTs = [kb.sb([128, 8, 128], BF16, f"hnT{i}") for i in range(2)]
    junk = kb.sb([128, D], BF16, "junk")
    sss = [kb.sb([128, 1], F32, f"ss{i}") for i in range(3)]
    rstds = [kb.sb([128, 1], F32, f"rstd{i}") for i in range(3)]
    G = kb.sb([128, NFC, 130], F32, "G")
    ACC = [kb.sb([128, 4, 128], F32, f"acc{i}") for i in range(2)]
    SIL = [kb.sb([128, 4, 128], F32, f"sil{i}") for i in range(2)]
    ACTT = kb.sb([128, NFC, 128], BF16, "ACTT")
    kb.op("dve", lambda e: e.memset(G[:, :, :], 0.0), writes=[G])
    po_banks = [g.PS[5], g.PS[6]]
    rot = g.PS[0:5]
    rot_i = [0]

    def next_rot():
        b = rot[rot_i[0] % len(rot)]
        rot_i[0] += 1
        return b

    NS_AT = int(os.environ.get('NS_AT', '1'))
    ntl = len(g.tiles)
    load_h(g, src, 0, HTs[0])
    norm_stats(g, HTs[0], g.tiles[0][1], Gb, hns[0], sss[0], rstds[0], junk)
    if ntl > 1:
        load_h(g, src, 1, HTs[1])
    if ntl > 2:
        load_h(g, src, 2, HTs[2])
    norm_transpose(g, hns[0], hnTs[0], g.tiles[0][1])

    def down_part(c_lo, c_hi, T):
        for c in range(c_lo, c_hi):
            for nb in range(2):
                po = po_banks[nb]
                kb.op("pe", lambda e: e.matmul(po[:T, :], ACTT[:, c, :T], Wdn[:, c, nb * 512:(nb + 1) * 512],
                                               start=(c == 0), stop=(c == NFC - 1)),
                      reads=[ACTT, Wdn], writes=[po], inc=(c == c_hi - 1))

    steps = list(range(0, NFC, 4))
    DEFER = bool(int(os.environ.get("FFN_DEFER", "0")))

    def tile_tail(tj):
        r0j, Tj = g.tiles[tj]
        HTj = HTs[tj % 3]
        down_part(steps[-1], NFC, Tj)
        for nb in range(2):
            kb.op("dve", lambda e: e.tensor_tensor(out=HTj[:Tj, nb * 512:(nb + 1) * 512],
                                                   in0=HTj[:Tj, nb * 512:(nb + 1) * 512], in1=po_banks[nb][:Tj, :], op=ALU.add),
                  reads=[HTj, po_banks[nb]], writes=[HTj])
        store_h(g, dst, tj, HTj, final, Gfin, (sss[(tj + 2) % 3], rstds[(tj + 2) % 3], junk))
        if tj + 3 < ntl:
            load_h(g, src, tj + 3, HTs[tj % 3])

    pending = None
    for ti, (r0, T) in enumerate(g.tiles):
        HT = HTs[ti % 3]
        hnT = hnTs[ti % 2]
        for si, c0 in enumerate(steps):
            nch = min(4, NFC - c0)
            pg = next_rot()
            pv = next_rot()
            for j in range(nch):
                for kc in range(8):
                    kb.op("pe", lambda e: e.matmul(pg[:, j * T:(j + 1) * T],
                                                   Wup[:, kc, DFF + (c0 + j) * 128: DFF + (c0 + j + 1) * 128],
                                                   hnT[:, kc, :T], start=(kc == 0), stop=(kc == 7)),
                          reads=[Wup, hnT], writes=[pg], inc=(kc == 7))
            for j in range(nch):
                for kc in range(8):
                    kb.op("pe", lambda e: e.matmul(pv[:, j * T:(j + 1) * T],
                                                   Wup[:, kc, (c0 + j) * 128:(c0 + j + 1) * 128],
                                                   hnT[:, kc, :T], start=(kc == 0), stop=(kc == 7)),
                          reads=[Wup, hnT], writes=[pv], inc=(kc == 7))
            if si == 0 and pending is not None:
                tile_tail(pending)
                pending = None
            if si >= 1:
                down_part(steps[si - 1], c0, T)
            kb.op("act", lambda e: e.copy(out=G[:, c0:c0 + nch, 2:2 + T],
                                          in_=pg[:, 0:nch * T].rearrange("p (c t) -> p c t", c=nch)),
                  reads=[pg], writes=[G])
            acc = ACC[si % 2]
            sil = SIL[si % 2]
            for j in range(nch):
                c = c0 + j
                kb.op("dve", lambda e: e.tensor_scalar(out=acc[:, j, :T], in0=G[:, c, 2:2 + T],
                                                       scalar1=CW[:, 2, c:c + 1], scalar2=CB[:, c:c + 1],
                                                       op0=ALU.mult, op1=ALU.add),
                      reads=[G, CW, CB], writes=[acc])
                kb.op("dve", lambda e: e.scalar_tensor_tensor(out=acc[:, j, :T], in0=G[:, c, 1:1 + T],
                                                              scalar=CW[:, 1, c:c + 1], in1=acc[:, j, :T],
                                                              op0=ALU.mult, op1=ALU.add),
                      reads=[G, CW, acc], writes=[acc])
                kb.op("dve", lambda e: e.scalar_tensor_tensor(out=acc[:, j, :T], in0=G[:, c, 0:T],
                                                              scalar=CW[:, 0, c:c + 1], in1=acc[:, j, :T],
                                                              op0=ALU.mult, op1=ALU.add),
                      reads=[G, CW, acc], writes=[acc])
            kb.op("act", lambda e: e.activation(out=sil[:, 0:nch, :T], in_=acc[:, 0:nch, :T], func=AF.Silu),
                  reads=[acc], writes=[sil])
            kb.op("dve", lambda e: e.tensor_tensor(out=ACTT[:, c0:c0 + nch, :T], in0=sil[:, 0:nch, :T],
                                                   in1=pv[:, 0:nch * T].rearrange("p (c t) -> p c t", c=nch),
                                                   op=ALU.mult),
                  reads=[sil, pv], writes=[ACTT])
            if si == NS_AT and ti + 1 < ntl:
                Tn = g.tiles[ti + 1][1]
                norm_stats(g, HTs[(ti + 1) % 3], Tn, Gb, hns[(ti + 1) % 2], sss[(ti + 1) % 3], rstds[(ti + 1) % 3], junk)
        if ti + 1 < ntl:
            norm_transpose(g, hns[(ti + 1) % 2], hnTs[(ti + 1) % 2], g.tiles[ti + 1][1])
        kb.op("dve", lambda e: e.tensor_copy(out=G[:, :, 0:2], in_=G[:, :, T:T + 2]), reads=[G], writes=[G])
        if DEFER and ti + 1 < ntl:
            pending = ti
        else:
            tile_tail(ti)


EH = 0.6065306597126334
ISQ = 0.08838834764831845
NEGBIG = -30000.0


def phase_l0(g, src, dst, final):
    import os
    CUT = int(os.environ.get('CUT', '99'))
    SUB = int(os.environ.get('SUB', '99'))
    HFN = int(os.environ.get('HFN', '2'))
    kb, nc, dr = g.kb, g.nc, g.dr
    Win = kb.sb([128, 8, 3848], BF16, "Win")
    Wout = kb.sb([128, 8, D], BF16, "Wout")
    W2A = kb.sb([128, 512], BF16, "W2A")
    G2 = kb.sb([128, 512], BF16, "G2")
    with contextlib.ExitStack() as ses:
        old = kb.es
        kb.es = ses
        stg = [kb.sb([128, 1924], F32, f"stg{i}") for i in range(3)]
        load_weight_bf16(g, dr["e_w_in"], 0, D, 3848, Win, stg)
        load_weight_bf16(g, dr["e_w_out"], 0, D, D, Wout, stg)
        s0 = stg[0]
        kb.dma(s0[0:64, 0:512], dr["r_w2"].ap()[:, :], writes=[s0], sem_buf=s0)
        kb.dma(s0[64:128, 0:512], dr["r_a2"].ap()[:, :], writes=[s0], sem_buf=s0)
        kb.op("dve", lambda e: e.tensor_copy(out=W2A[:, :], in_=s0[:, 0:512]), reads=[s0], writes=[W2A])
        s1 = stg[1]
        kb.dma(s1[:, 0:512], dr["r_g2"].ap()[:, :], writes=[s1], sem_buf=s1)
        kb.op("dve", lambda e: e.tensor_copy(out=G2[:, :], in_=s1[:, 0:512]), reads=[s1], writes=[G2])
        kb.barrier()
        kb.es = old
    F = lambda shape, name: kb.sb(shape, F32, name)
    Bf = lambda shape, name: kb.sb(shape, BF16, name)
    Gb = Bf([128, D], "Gb")
    ue = F([128, 128], "ue")
    su = F([128, 128], "su")
    blk = F([128, 128], "blk")
    kb.dma(ue[:, :], dr["c_ue"].ap()[:, :], writes=[ue], sem_buf=ue)
    kb.dma(su[:, :], dr["c_su"].ap()[:, :], writes=[su], sem_buf=su)
    kb.dma(blk[:, :], dr["c_blk"].ap()[:, :], writes=[blk], sem_buf=blk)
    sl = F([128, 128], "sl")
    kb.op("dve", lambda e: e.tensor_scalar(out=sl[:, :], in0=ue[:, :], scalar1=-1.0, scalar2=1.0, op0=ALU.mult, op1=ALU.add),
          reads=[ue], writes=[sl])
    neg = F([128, 128], "neg")
    kb.op("dve", lambda e: e.tensor_scalar(out=neg[:, :], in0=sl[:, :], scalar1=NEGBIG, scalar2=None, op0=ALU.mult),
          reads=[sl], writes=[neg])
    nblk = F([128, 2], "nblk")
    kb.op("dve", lambda e: e.tensor_scalar(out=nblk[:, 0:1], in0=blk[:, 0:1], scalar1=-1.0, scalar2=None, op0=ALU.mult),
          reads=[blk], writes=[nblk])
    kb.op("dve", lambda e: e.tensor_scalar(out=nblk[:, 1:2], in0=blk[:, 127:128], scalar1=-1.0, scalar2=None, op0=ALU.mult),
          reads=[blk], writes=[nblk])
    blk64 = F([128, 128], "blk64")
    kb.op("dve", lambda e: e.tensor_scalar(out=blk64[:, :], in0=blk[:, :], scalar1=1.0 / 64.0, scalar2=None, op0=ALU.mult),
          reads=[blk], writes=[blk64])
    onesf = F([128, 128], "onesf")
    kb.op("dve", lambda e: e.memset(onesf[:, :], 1.0 / 128.0), writes=[onesf])
    onesb = Bf([128, 128], "onesb")
    kb.op("dve", lambda e: e.memset(onesb[:, :], 1.0), writes=[onesb])
    ones1 = F([128, 128], "ones1")
    kb.op("dve", lambda e: e.memset(ones1[:, :], 1.0), writes=[ones1])
    vtmp = F([32, 128], "vtmp")
    MU = F([128, 14], "MU"); W0 = F([128, 4], "W0"); A0 = F([128, 4], "A0"); KK = F([128, 4], "KK")
    KA = F([128, 4], "KA"); RRK = F([128, 4], "RRK"); LNW = F([128, 4], "LNW"); LNB = F([128, 4], "LNB")
    MN = F([128, 4], "MN")
    load_vec_fm(g, dr["r_mu"], 0, 14, MU, MU[:, :], vtmp)
    for nm, buf in (("r_w0", W0), ("r_a0", A0), ("r_k_k", KK), ("r_k_a", KA), ("r_r_k", RRK), ("r_ln_w", LNW),
                    ("r_ln_b", LNB), ("m_norm", MN)):
        load_vec_fm(g, dr[nm], 0, 4, buf, buf[:, :], vtmp)
    BG = F([128, 8], "BG")
    kb.dma(BG[:, 0:4], bc_rows(dr["m_b_i"], 0, 4), writes=[BG], sem_buf=BG)
    kb.dma(BG[:, 4:8], bc_rows(dr["m_b_f"], 0, 4), writes=[BG], sem_buf=BG)
    C = F([128, 4, 129], "C")
    Cb = Bf([128, 4, 128], "Cb")
    nbc = Bf([128, 4, 128], "nbc")
    ST = F([128, 4, 64], "ST")
    STb = Bf([128, 4, 64], "STb")
    ZR = F([128, 14, 129], "ZR")
    for b_ in (C, ST, ZR):
        kb.op("dve", lambda e: e.memset(b_[:, :, :], 0.0), writes=[b_])
    for b_ in (Cb, nbc, STb):
        kb.op("dve", lambda e: e.memset(b_[:, :, :], 0.0), writes=[b_])
    vTM1 = Bf([128, 4, 129], "vTM1")
    kb.op("dve", lambda e: e.memset(vTM1[:, :, :], 1.0), writes=[vTM1])
    HTs = [F([128, D], f"HT{i}") for i in range(3)]
    hn = Bf([128, D], "hn"); hnT = Bf([128, 8, 128], "hnT"); junk = hn
    ss = F([128, 1], "ss"); rstd = F([128, 1], "rstd")
    qTb = Bf([128, 4, 128], "qTb"); kTb = Bf([128, 4, 128], "kTb"); moT = Bf([128, 4, 128], "moT"); kpbuf = F([128, 4, 128], "kpbuf")
    gx = F([128, 8], "gx"); th = F([128, 8], "th"); ex = F([128, 4], "ex"); LI = F([128, 4], "LI"); LF = F([128, 4], "LF")
    lmb = F([128, 4], "lmb"); LFb = F([128, 4, 128], "LFb"); arg = F([128, 4, 128], "arg"); ET = Bf([128, 4, 128], "ET"); aabuf = Bf([128, 4, 128], "aabuf")
    eB = F([128, 4, 128], "eB"); gcol = F([128, 4], "gcol"); ew = F([128, 4], "ew"); qs = Bf([128, 4, 128], "qs")
    sT = Bf([128, 4, 128], "sT"); kw = Bf([128, 4, 128], "kw")
    cden = F([128, 4, 128], "cden"); hT = F([128, 4, 128], "hT"); hsq = F([128, 4, 128], "hsq"); rs4 = F([128, 4, 128], "rs4")
    mixTs = [Bf([128, 8, 128], f"mixT{i}") for i in range(2)]
    kTMf = Bf([128, 512], "kTMf")
    P1 = F([128, 512], "P1")
    GGs = [Bf([128, 4, 128], f"GG{i}") for i in range(2)]
    for hh_ in range(2):
        kb.dma(P1[:, :], bc_rows(dr["norm_mix"], 0, 512, col0=512 * hh_), writes=[P1], sem_buf=P1)
        kb.op("dve", lambda e: e.tensor_copy(out=Gb[:, 512 * hh_:512 * (hh_ + 1)], in_=P1[:, :]), reads=[P1], writes=[Gb])
    Z2 = Bf([128, 14, 128], "Z2"); D1 = Z2
    LIN = Bf([128, 128], "LIN"); sxg = Bf([128, 128], "sxg")
    sw = arg; aa = aabuf
    kkr = cden; tq = hsq; rn = rs4; kp = kpbuf
    CS = hT; CSp = F([128, 4, 128], "CSp"); csl = F([128, 4], "csl")
    eW = F([128, 4, 128], "eW"); eWp = eB; eWi = F([128, 4, 128], "eWi"); eWT = F([128, 4, 128], "eWT")
    kka = F([128, 4, 128], "kka")
    AR = Bf([128, 4, 2, 128], "AR")
    BH = Bf([128, 4, 128], "BH"); KH = Bf([128, 4, 128], "KH"); vb = Bf([128, 4, 128], "vb")
    bonuss = [Bf([128, 4, 128], f"bonus{i}") for i in range(2)]
    BTm = [Bf([128, 4, 128], f"BTm{i}") for i in range(2)]
    KTm = [Bf([128, 4, 128], f"KTm{i}") for i in range(2)]
    ATm = [Bf([128, 4, 128], f"ATm{i}") for i in range(2)]
    STbd = Bf([128, 4, 128], "STbd")
    kb.op("dve", lambda e: e.memset(STbd[:, :, :], 0.0), writes=[STbd])
    VTM = Bf([128, 8, 64], "VTM"); BHT = Bf([128, 8, 64], "BHT"); KHT = Bf([128, 8, 64], "KHT"); UTM = Bf([128, 8, 64], "UTM")
    Xa = [F([128, 8, 128], "Xa0"), F([128, 8, 128], "Xa1")]
    XTa = [F([128, 8, 128], "XTa0"), F([128, 8, 128], "XTa1")]
    Pm = F([128, 8, 128], "Pm")
    ARB = Bf([128, 8, 128], "ARB"); AAK = Bf([128, 8, 128], "AAK"); ARK = Bf([128, 8, 128], "ARK")

    class SubBuf:
        def __init__(self, parent, lo):
            self.parent, self.lo = parent, lo
            self.w, self.r, self.excl, self.name = parent.w, parent.r, False, parent.name

        def __getitem__(self, idx):
            p, c, t = idx
            if isinstance(c, slice):
                c = slice((c.start or 0) + self.lo, (c.stop if c.stop is not None else 4) + self.lo)
            else:
                c = c + self.lo
            return self.parent.t[p, c, t]

    Of = SubBuf(Xa[1], 0); Osq = SubBuf(Xa[1], 4); mean_s = SubBuf(XTa[1], 0); var = SubBuf(XTa[1], 4)

    def b3(buf, T, n=4):
        return buf[:, 0:n].unsqueeze(2).to_broadcast([128, n, T])

    def mk_alloc(banks):
        st = [0]

        def alloc():
            b = banks[st[0] % len(banks)]
            st[0] += 1
            return b
        return alloc

    _bk = [int(c) for c in os.environ.get("L0_BANKS", "2122")]
    _o = [0, _bk[0], _bk[0] + _bk[1], _bk[0] + _bk[1] + _bk[2], 7]
    nP = mk_alloc(g.PS[_o[0]:_o[1]])
    nM = mk_alloc(g.PS[_o[1]:_o[2]])
    nR = mk_alloc(g.PS[_o[2]:_o[3]])
    nS = mk_alloc(g.PS[_o[3]:_o[4]])

    graw = F([128, 8], "graw")

    def gen_proj(ti):
        r0, T = g.tiles[ti]
        HT = HTs[ti % 3]
        mixT = mixTs[ti % 2]
        GG = GGs[ti % 2]
        bonus = bonuss[ti % 2]
        def proj_fm(pbank, j, col):
            for kc in range(8):
                kb.op("pe", lambda e: e.matmul(pbank[:, j * T:(j + 1) * T], Win[:, kc, col:col + 128], hnT[:, kc, :T],
                                               start=(kc == 0), stop=(kc == 7)),
                      reads=[Win, hnT], writes=[pbank], inc=(kc == 7))

        def proj_tm(pbank, col, n, c0=0):
            for kc in range(8):
                kb.op("pe", lambda e: e.matmul(pbank[:T, c0:c0 + n], hnT[:, kc, :T], Win[:, kc, col:col + n],
                                               start=(kc == 0), stop=(kc == 7)),
                      reads=[hnT, Win], writes=[pbank], inc=(kc == 7))

        def v3(pbank, n=4):
            return pbank[:, 0:n * T].rearrange("p (c t) -> p c t", c=n)

        pq = nP()
        for h in range(4):
            proj_fm(pq, h, h * 128)
        kb.op("act", lambda e: e.copy(out=qTb[:, :, :T], in_=v3(pq)), reads=[pq], writes=[qTb])
        pk = nP()
        for h in range(4):
            proj_fm(pk, h, 512 + h * 128)
        kb.op("act", lambda e: e.copy(out=kTb[:, :, :T], in_=v3(pk)), reads=[pk], writes=[kTb])
        pmo = nP()
        for h in range(4):
            proj_fm(pmo, h, 1536 + h * 128)
        kb.op("act", lambda e: e.activation(out=moT[:, :, :T], in_=v3(pmo), func=AF.Sigmoid), reads=[pmo], writes=[moT])
        pkt = nP()
        proj_tm(pkt, 512, 512)
        kb.op("act", lambda e: e.copy(out=kTMf[:T, :], in_=pkt[:T, :]), reads=[pkt], writes=[kTMf])
        pvt = nP()
        proj_tm(pvt, 1024, 512)
        kb.op("act", lambda e: e.copy(out=vTM1[:T, :, 0:128], in_=pvt[:T, :].rearrange("p (h v) -> p h v", h=4)),
              reads=[pvt], writes=[vTM1])
        pgt = nP()
        proj_tm(pgt, 2048, 8)
        kb.op("act", lambda e: e.copy(out=graw[:T, :], in_=pgt[:T, 0:8]), reads=[pgt], writes=[graw])
        zc = 2056
        for b0, n in ((0, 4), (4, 4), (8, 4), (12, 2)):
            yield
            pz = nP()
            for j in range(n):
                proj_fm(pz, j, zc + (b0 + j) * 128)
            kb.op("act", lambda e: e.copy(out=ZR[:, b0:b0 + n, 1:T + 1], in_=v3(pz, n)), reads=[pz], writes=[ZR])

    def gen_mlstm(ti):
        r0, T = g.tiles[ti]
        HT = HTs[ti % 3]
        mixT = mixTs[ti % 2]
        GG = GGs[ti % 2]
        bonus = bonuss[ti % 2]
        def proj_fm(pbank, j, col):
            for kc in range(8):
                kb.op("pe", lambda e: e.matmul(pbank[:, j * T:(j + 1) * T], Win[:, kc, col:col + 128], hnT[:, kc, :T],
                                               start=(kc == 0), stop=(kc == 7)),
                      reads=[Win, hnT], writes=[pbank], inc=(kc == 7))

        def proj_tm(pbank, col, n, c0=0):
            for kc in range(8):
                kb.op("pe", lambda e: e.matmul(pbank[:T, c0:c0 + n], hnT[:, kc, :T], Win[:, kc, col:col + n],
                                               start=(kc == 0), stop=(kc == 7)),
                      reads=[hnT, Win], writes=[pbank], inc=(kc == 7))

        def v3(pbank, n=4):
            return pbank[:, 0:n * T].rearrange("p (c t) -> p c t", c=n)

        kb.op("dve", lambda e: e.tensor_tensor(out=gx[:T, :], in0=graw[:T, :], in1=BG[:T, :], op=ALU.add),
              reads=[graw, BG], writes=[gx])
        kb.op("act", lambda e: e.activation(out=th[:T, :], in_=gx[:T, :], func=AF.Tanh, scale=1.0 / 15.0), reads=[gx], writes=[th])
        kb.op("dve", lambda e: e.tensor_scalar(out=LI[:T, :], in0=th[:T, 0:4], scalar1=15.0, scalar2=None, op0=ALU.mult),
              reads=[th], writes=[LI])
        kb.op("act", lambda e: e.activation(out=ex[:T, :], in_=th[:T, 4:8], func=AF.Exp, scale=-15.0), reads=[th], writes=[ex])
        kb.op("act", lambda e: e.activation(out=ex[:T, :], in_=ex[:T, :], func=AF.Ln, bias=1.0), reads=[ex], writes=[ex])
        kb.op("dve", lambda e: e.tensor_scalar(out=LF[:T, :], in0=ex[:T, :], scalar1=-1.0, scalar2=None, op0=ALU.mult),
              reads=[ex], writes=[LF])
        yield
        pbc = nM()
        kb.op("pe", lambda e: e.matmul(pbc[:T, 0:4], ue[:T, :T], LF[:T, :], start=True, stop=True), reads=[ue, LF], writes=[pbc])
        kb.op("dve", lambda e: e.tensor_tensor(out=lmb[:T, :], in0=LI[:T, :], in1=pbc[:T, 0:4], op=ALU.subtract),
              reads=[LI, pbc], writes=[lmb])
        kb.op("dve", lambda e: e.tensor_copy(out=LFb[:T, :, :], in_=LF[:T, 0:4].unsqueeze(2).to_broadcast([T, 4, 128])),
              reads=[LF], writes=[LFb])
        yield
        pB = nM()
        for h in range(4):
            kb.op("pe", lambda e: e.matmul(pB[:, h * T:(h + 1) * T], LFb[:T, h, :], ue[:T, :T], start=True, stop=True),
                  reads=[LFb, ue], writes=[pB], inc=(h == 3))
        kb.op("dve", lambda e: e.tensor_tensor(out=arg[:T, :, :T], in0=v3(pB)[:T], in1=neg[:T, :T].unsqueeze(1).to_broadcast([T, 4, T]),
                                               op=ALU.add), reads=[pB, neg], writes=[arg])
        for h in range(4):
            kb.op("act", lambda e: e.activation(out=ET[:T, h, :T], in_=arg[:T, h, :T], func=AF.Exp, bias=lmb[:T, h:h + 1]),
                  reads=[arg, lmb], writes=[ET])
        kb.op("act", lambda e: e.activation(out=eB[:, :, :T], in_=v3(pB), func=AF.Exp), reads=[pB], writes=[eB])
        kb.op("dve", lambda e: e.tensor_copy(out=gcol[:, :], in_=v3(pB)[:, :, T - 1]), reads=[pB], writes=[gcol])
        for h in range(4):
            kb.op("act", lambda e: e.activation(out=ew[:T, h:h + 1], in_=lmb[:T, h:h + 1], func=AF.Exp, bias=gcol[:T, h:h + 1]),
                  reads=[lmb, gcol], writes=[ew])
        kb.op("dve", lambda e: e.tensor_tensor(out=qs[:, :, :T], in0=qTb[:, :, :T], in1=eB[:, :, :T], op=ALU.mult),
              reads=[qTb, eB], writes=[qs])
        yield
        psc = nM()
        for h in range(4):
            kb.op("pe", lambda e: e.matmul(psc[:T, h * T:(h + 1) * T], kTb[:, h, :T], qTb[:, h, :T], start=True, stop=True),
                  reads=[kTb, qTb], writes=[psc], inc=(h == 3))
        kb.op("dve", lambda e: e.scalar_tensor_tensor(out=sT[:T, :, :T], in0=v3(psc)[:T], scalar=ISQ, in1=ET[:T, :, :T],
                                                      op0=ALU.mult, op1=ALU.mult), reads=[psc, ET], writes=[sT])
        yield
        pden = nM()
        for h in range(4):
            kb.op("pe", lambda e: e.matmul(pden[:, h * T:(h + 1) * T], onesb[:T, :], sT[:T, h, :T], start=True, stop=False),
                  reads=[onesb, sT], writes=[pden], inc=False)
            kb.op("pe", lambda e: e.matmul(pden[:, h * T:(h + 1) * T], nbc[:, h, :], qs[:, h, :T], start=False, stop=True),
                  reads=[nbc, qs], writes=[pden])
        kb.op("act", lambda e: e.activation(out=cden[:, :, :T], in_=v3(pden), func=AF.Abs), reads=[pden], writes=[cden])
        kb.op("dve", lambda e: e.tensor_scalar(out=cden[:, :, :T], in0=cden[:, :, :T], scalar1=1.0, scalar2=None, op0=ALU.max),
              reads=[cden], writes=[cden])
        kb.op("dve", lambda e: e.reciprocal(out=cden[:, :, :T], in_=cden[:, :, :T]), reads=[cden], writes=[cden])
        pnum = nM()
        for h in range(4):
            kb.op("pe", lambda e: e.matmul(pnum[:, h * T:(h + 1) * T], vTM1[:T, h, 0:128], sT[:T, h, :T], start=True, stop=False),
                  reads=[vTM1, sT], writes=[pnum], inc=False)
            kb.op("pe", lambda e: e.matmul(pnum[:, h * T:(h + 1) * T], Cb[:, h, :], qs[:, h, :T], start=False, stop=True),
                  reads=[Cb, qs], writes=[pnum])
        kb.op("dve", lambda e: e.tensor_tensor(out=hT[:, :, :T], in0=v3(pnum), in1=cden[:, :, :T], op=ALU.mult),
              reads=[pnum, cden], writes=[hT])
        kb.op("act", lambda e: e.activation(out=hsq[:, :, :T], in_=hT[:, :, :T], func=AF.Square), reads=[hT], writes=[hsq])
        yield
        pss = nM()
        for h in range(4):
            kb.op("pe", lambda e: e.matmul(pss[:, h * T:(h + 1) * T], onesf[:, :], hsq[:, h, :T], start=True, stop=True),
                  reads=[onesf, hsq], writes=[pss], inc=(h == 3))
        kb.op("act", lambda e: e.activation(out=rs4[:, :, :T], in_=v3(pss), func=AF.Sqrt, bias=1e-6), reads=[pss], writes=[rs4])
        kb.op("dve", lambda e: e.reciprocal(out=rs4[:, :, :T], in_=rs4[:, :, :T]), reads=[rs4], writes=[rs4])
        kb.op("dve", lambda e: e.tensor_tensor(out=hT[:, :, :T], in0=hT[:, :, :T], in1=rs4[:, :, :T], op=ALU.mult),
              reads=[hT, rs4], writes=[hT])
        kb.op("dve", lambda e: e.tensor_tensor(out=hT[:, :, :T], in0=hT[:, :, :T], in1=moT[:, :, :T], op=ALU.mult),
              reads=[hT, moT], writes=[hT])
        kb.op("dve", lambda e: e.tensor_tensor(out=mixT[:, 0:4, :T], in0=hT[:, :, :T], in1=b3(MN, T), op=ALU.mult),
              reads=[hT, MN], writes=[mixT])
        yield
        for h in range(4):
            kb.op("dve", lambda e: e.tensor_scalar(out=kw[:T, h, :], in0=kTMf[:T, h * 128:(h + 1) * 128], scalar1=ew[:T, h:h + 1],
                                                   scalar2=ISQ, op0=ALU.mult, op1=ALU.mult), reads=[kTMf, ew], writes=[kw])
        for half in range(2):
            pC = nM()
            for hh in range(2):
                h = half * 2 + hh
                kb.op("pe", lambda e: e.matmul(pC[:, hh * 129:(hh + 1) * 129], kw[:T, h, :], vTM1[:T, h, :], start=True, stop=True),
                      reads=[kw, vTM1], writes=[pC], inc=(hh == 1))
            for hh in range(2):
                h = half * 2 + hh
                kb.op("dve", lambda e: e.scalar_tensor_tensor(out=C[:, h, :], in0=C[:, h, :], scalar=eB[:, h, T - 1:T],
                                                              in1=pC[:, hh * 129:(hh + 1) * 129], op0=ALU.mult, op1=ALU.add),
                      reads=[C, eB, pC], writes=[C])
        yield
        kb.op("act", lambda e: e.copy(out=Cb[:, :, :], in_=C[:, :, 0:128]), reads=[C], writes=[Cb])
        kb.op("dve", lambda e: e.tensor_copy(out=nbc[:, :, :], in_=C[:, :, 128:129].to_broadcast([128, 4, 128])),
              reads=[C], writes=[nbc])


    def gen_prep(ti):
        r0, T = g.tiles[ti]
        HT = HTs[ti % 3]
        mixT = mixTs[ti % 2]
        GG = GGs[ti % 2]
        bonus = bonuss[ti % 2]
        def proj_fm(pbank, j, col):
            for kc in range(8):
                kb.op("pe", lambda e: e.matmul(pbank[:, j * T:(j + 1) * T], Win[:, kc, col:col + 128], hnT[:, kc, :T],
                                               start=(kc == 0), stop=(kc == 7)),
                      reads=[Win, hnT], writes=[pbank], inc=(kc == 7))

        def proj_tm(pbank, col, n, c0=0):
            for kc in range(8):
                kb.op("pe", lambda e: e.matmul(pbank[:T, c0:c0 + n], hnT[:, kc, :T], Win[:, kc, col:col + n],
                                               start=(kc == 0), stop=(kc == 7)),
                      reads=[hnT, Win], writes=[pbank], inc=(kc == 7))

        def v3(pbank, n=4):
            return pbank[:, 0:n * T].rearrange("p (c t) -> p c t", c=n)

        kb.op("dve", lambda e: e.tensor_tensor(out=D1[:, :, :T], in0=ZR[:, :, 0:T], in1=ZR[:, :, 1:T + 1], op=ALU.subtract),
              reads=[ZR], writes=[D1])
        kb.op("dve", lambda e: e.tensor_tensor(out=D1[:, :, :T], in0=D1[:, :, :T], in1=b3(MU, T, 14), op=ALU.mult),
              reads=[D1, MU], writes=[D1])
        kb.op("dve", lambda e: e.tensor_tensor(out=Z2[:, :, :T], in0=D1[:, :, :T], in1=ZR[:, :, 1:T + 1], op=ALU.add),
              reads=[D1, ZR], writes=[Z2])
        kb.op("dve", lambda e: e.tensor_copy(out=ZR[:, :, 0:1], in_=ZR[:, :, T:T + 1]), reads=[ZR], writes=[ZR])
        r_ = Z2[:, 0:4, :T]; k_ = Z2[:, 4:8, :T]; v_ = Z2[:, 8:12, :T]
        yield
        kb.op("act", lambda e: e.activation(out=LIN[0:64, :T], in_=Z2[0:64, 12, :T], func=AF.Tanh), reads=[Z2], writes=[LIN])
        kb.op("act", lambda e: e.copy(out=LIN[64:128, :T], in_=Z2[64:128, 12, :T]), reads=[Z2], writes=[LIN])
        kb.op("act", lambda e: e.activation(out=sxg[:, :T], in_=Z2[:, 13, :T], func=AF.Sigmoid), reads=[Z2], writes=[sxg])
        yield
        pw = nR()
        for c in range(4):
            kb.op("pe", lambda e: e.matmul(pw[:, c * T:(c + 1) * T], W2A[0:64, c * 128:(c + 1) * 128], LIN[0:64, :T], start=True, stop=True),
                  reads=[W2A, LIN], writes=[pw], inc=(c == 3))
        for c in range(4):
            kb.op("act", lambda e: e.activation(out=sw[:, c, :T], in_=pw[:, c * T:(c + 1) * T], func=AF.Sigmoid, bias=W0[:, c:c + 1]),
                  reads=[pw, W0], writes=[sw])
        pa = nR()
        for c in range(4):
            kb.op("pe", lambda e: e.matmul(pa[:, c * T:(c + 1) * T], W2A[64:128, c * 128:(c + 1) * 128], LIN[64:128, :T], start=True, stop=True),
                  reads=[W2A, LIN], writes=[pa], inc=(c == 3))
        for c in range(4):
            kb.op("act", lambda e: e.activation(out=aa[:, c, :T], in_=pa[:, c * T:(c + 1) * T], func=AF.Sigmoid, bias=A0[:, c:c + 1]),
                  reads=[pa, A0], writes=[aa])
        pgg = nR()
        for c in range(4):
            kb.op("pe", lambda e: e.matmul(pgg[:, c * T:(c + 1) * T], G2[:, c * 128:(c + 1) * 128], sxg[:, :T], start=True, stop=True),
                  reads=[G2, sxg], writes=[pgg], inc=(c == 3))
        kb.op("act", lambda e: e.copy(out=GG[:, :, :T], in_=v3(pgg)), reads=[pgg], writes=[GG])
        yield
        kb.op("dve", lambda e: e.tensor_tensor(out=kkr[:, :, :T], in0=k_, in1=b3(KK, T), op=ALU.mult), reads=[Z2, KK], writes=[kkr])
        kb.op("act", lambda e: e.activation(out=tq[:, :, :T], in_=kkr[:, :, :T], func=AF.Square), reads=[kkr], writes=[tq])
        pn = nR()
        for c in range(4):
            kb.op("pe", lambda e: e.matmul(pn[:, c * T:(c + 1) * T], blk[:, :], tq[:, c, :T], start=True, stop=True),
                  reads=[blk, tq], writes=[pn], inc=(c == 3))
        kb.op("act", lambda e: e.activation(out=rn[:, :, :T], in_=v3(pn), func=AF.Sqrt), reads=[pn], writes=[rn])
        kb.op("dve", lambda e: e.tensor_scalar(out=rn[:, :, :T], in0=rn[:, :, :T], scalar1=1e-12, scalar2=None, op0=ALU.max),
              reads=[rn], writes=[rn])
        kb.op("dve", lambda e: e.reciprocal(out=rn[:, :, :T], in_=rn[:, :, :T]), reads=[rn], writes=[rn])
        kb.op("dve", lambda e: e.tensor_tensor(out=kkr[:, :, :T], in0=kkr[:, :, :T], in1=rn[:, :, :T], op=ALU.mult),
              reads=[kkr, rn], writes=[kkr])
        yield
        kb.op("dve", lambda e: e.scalar_tensor_tensor(out=tq[:, :, :T], in0=aa[:, :, :T], scalar=-1.0, in1=b3(KA, T),
                                                      op0=ALU.add, op1=ALU.mult), reads=[aa, KA], writes=[tq])
        kb.op("dve", lambda e: e.scalar_tensor_tensor(out=kp[:, :, :T], in0=tq[:, :, :T], scalar=1.0, in1=k_,
                                                      op0=ALU.add, op1=ALU.mult), reads=[tq, Z2], writes=[kp])
        yield
        for c in range(4):
            kb.op("dve", lambda e: e.tensor_tensor_scan(out=CS[:, c, :T], data0=ones1[:, :T], data1=sw[:, c, :T], initial=0.0,
                                                        op0=ALU.mult, op1=ALU.add), reads=[ones1, sw], writes=[CS])
        kb.op("dve", lambda e: e.tensor_tensor(out=CSp[:, :, :T], in0=CS[:, :, :T], in1=sw[:, :, :T], op=ALU.subtract),
              reads=[CS, sw], writes=[CSp])
        kb.op("dve", lambda e: e.tensor_scalar(out=csl[:, :], in0=CS[:, :, T - 1], scalar1=-EH, scalar2=None, op0=ALU.mult),
              reads=[CS], writes=[csl])
        kb.op("act", lambda e: e.activation(out=eW[:, :, :T], in_=CS[:, :, :T], func=AF.Exp, scale=-EH), reads=[CS], writes=[eW])
        kb.op("act", lambda e: e.activation(out=eWp[:, :, :T], in_=CSp[:, :, :T], func=AF.Exp, scale=-EH), reads=[CSp], writes=[eWp])
        kb.op("act", lambda e: e.activation(out=eWi[:, :, :T], in_=CS[:, :, :T], func=AF.Exp, scale=EH), reads=[CS], writes=[eWi])
        for c in range(4):
            kb.op("act", lambda e: e.activation(out=eWT[:, c, :T], in_=CS[:, c, :T], func=AF.Exp, scale=EH, bias=csl[:, c:c + 1]),
                  reads=[CS, csl], writes=[eWT])
        yield
        kb.op("dve", lambda e: e.scalar_tensor_tensor(out=AR[:, :, 0, :T], in0=kkr[:, :, :T], scalar=-1.0, in1=eWp[:, :, :T],
                                                      op0=ALU.mult, op1=ALU.mult), reads=[kkr, eWp], writes=[AR])
        kb.op("dve", lambda e: e.tensor_tensor(out=AR[:, :, 1, :T], in0=r_, in1=eW[:, :, :T], op=ALU.mult), reads=[Z2, eW], writes=[AR])
        kb.op("dve", lambda e: e.tensor_tensor(out=kka[:, :, :T], in0=kkr[:, :, :T], in1=aa[:, :, :T], op=ALU.mult),
              reads=[kkr, aa], writes=[kka])
        kb.op("dve", lambda e: e.tensor_tensor(out=BH[:, :, :T], in0=kka[:, :, :T], in1=eWT[:, :, :T], op=ALU.mult),
              reads=[kka, eWT], writes=[BH])
        kb.op("dve", lambda e: e.tensor_tensor(out=KH[:, :, :T], in0=kp[:, :, :T], in1=eWT[:, :, :T], op=ALU.mult),
              reads=[kp, eWT], writes=[KH])
        kb.op("act", lambda e: e.copy(out=vb[:, :, :T], in_=v_), reads=[Z2], writes=[vb])
        yield
        kb.op("dve", lambda e: e.tensor_tensor(out=tq[:, :, :T], in0=r_, in1=kp[:, :, :T], op=ALU.mult), reads=[Z2, kp], writes=[tq])
        kb.op("dve", lambda e: e.tensor_tensor(out=tq[:, :, :T], in0=tq[:, :, :T], in1=b3(RRK, T), op=ALU.mult),
              reads=[tq, RRK], writes=[tq])
        prk = nR()
        for c in range(4):
            kb.op("pe", lambda e: e.matmul(prk[:, c * T:(c + 1) * T], blk[:, :], tq[:, c, :T], start=True, stop=True),
                  reads=[blk, tq], writes=[prk], inc=(c == 3))
        kb.op("dve", lambda e: e.tensor_tensor(out=bonus[:, :, :T], in0=v3(prk), in1=v_, op=ALU.mult), reads=[prk, Z2], writes=[bonus])
        yield
        PST = g.PST
        for c in range(4):
            kb.op("pe", lambda e: e.transpose(out=PST[:T, c * 128:(c + 1) * 128], in_=vb[:, c, :T], identity=g.ident_b[:, :]),
                  reads=[vb, g.ident_b], writes=[PST], inc=False)
        for c in range(4):
            kb.op("pe", lambda e: e.transpose(out=PST[:T, (4 + c) * 128:(5 + c) * 128], in_=BH[:, c, :T], identity=g.ident_b[:, :]),
                  reads=[BH, g.ident_b], writes=[PST], inc=(c == 3))
        kb.op("act", lambda e: e.copy(out=VTM[:T, :, :], in_=PST[:T, 0:512].rearrange("p (h v) -> p h v", h=8)), reads=[PST], writes=[VTM])
        kb.op("act", lambda e: e.copy(out=BHT[:T, :, :], in_=PST[:T, 512:1024].rearrange("p (h v) -> p h v", h=8)), reads=[PST], writes=[BHT])
        for c in range(4):
            kb.op("pe", lambda e: e.transpose(out=PST[:T, c * 128:(c + 1) * 128], in_=KH[:, c, :T], identity=g.ident_b[:, :]),
                  reads=[KH, g.ident_b], writes=[PST], inc=(c == 3))
        kb.op("act", lambda e: e.copy(out=KHT[:T, :, :], in_=PST[:T, 0:512].rearrange("p (h v) -> p h v", h=8)), reads=[PST], writes=[KHT])
        yield
        for hf in range(2):
            mcol = blk[:, 127 * hf:127 * hf + 1]
            kb.op("dve", lambda e: e.scalar_tensor_tensor(out=BTm[hf][:, :, :T], in0=kka[:, :, :T], scalar=mcol, in1=eWi[:, :, :T],
                                                          op0=ALU.mult, op1=ALU.mult), reads=[kka, blk, eWi], writes=[BTm[hf]])
            kb.op("dve", lambda e: e.scalar_tensor_tensor(out=KTm[hf][:, :, :T], in0=kp[:, :, :T], scalar=mcol, in1=eWi[:, :, :T],
                                                          op0=ALU.mult, op1=ALU.mult), reads=[kp, blk, eWi], writes=[KTm[hf]])
            kb.op("dve", lambda e: e.scalar_tensor_tensor(out=ATm[hf][:, :, :T], in0=kkr[:, :, :T], scalar=nblk[:, hf:hf + 1], in1=eWp[:, :, :T],
                                                          op0=ALU.mult, op1=ALU.mult), reads=[kkr, nblk, eWp], writes=[ATm[hf]])
        X, XT = Xa[0], XTa[0]
        sub = su[:T, :T].unsqueeze(1).to_broadcast([T, 2, T])
        ueb = ue[:T, :T].unsqueeze(1).to_broadcast([T, 2, T])
        slb = sl[:T, :T].unsqueeze(1).to_broadcast([T, 4, T])
        yield
        for c in range(4):
            yield
            pNA = nR(); pKA = nR()
            for hf in range(2):
                for j in range(2):
                    kb.op("pe", lambda e: e.matmul(pNA[:T, (hf * 2 + j) * T:(hf * 2 + j + 1) * T], BTm[hf][:, c, :T], AR[:, c, j, :T], start=True, stop=True),
                          reads=[BTm[hf], AR], writes=[pNA], inc=(hf == 1 and j == 1))
            for hf in range(2):
                for j in range(2):
                    kb.op("pe", lambda e: e.matmul(pKA[:T, (hf * 2 + j) * T:(hf * 2 + j + 1) * T], KTm[hf][:, c, :T], AR[:, c, j, :T], start=True, stop=True),
                          reads=[KTm[hf], AR], writes=[pKA], inc=(hf == 1 and j == 1))
            na4 = pNA[:T, 0:4 * T].rearrange("p (h j t) -> p h j t", h=2, j=2)
            ka4 = pKA[:T, 0:4 * T].rearrange("p (h j t) -> p h j t", h=2, j=2)
            kb.op("dve", lambda e: e.tensor_tensor(out=X[:T, 2 * c:2 * c + 2, :T], in0=na4[:, :, 0, :], in1=sub, op=ALU.mult),
                  reads=[pNA, su], writes=[X])
            kb.op("dve", lambda e: e.tensor_tensor(out=ARB[:T, 2 * c:2 * c + 2, :T], in0=na4[:, :, 1, :], in1=ueb, op=ALU.mult),
                  reads=[pNA, ue], writes=[ARB])
            kb.op("dve", lambda e: e.tensor_tensor(out=AAK[:T, 2 * c:2 * c + 2, :T], in0=ka4[:, :, 0, :], in1=sub, op=ALU.mult),
                  reads=[pKA, su], writes=[AAK])
            kb.op("dve", lambda e: e.tensor_tensor(out=ARK[:T, 2 * c:2 * c + 2, :T], in0=ka4[:, :, 1, :], in1=ueb, op=ALU.mult),
                  reads=[pKA, ue], writes=[ARK])
        yield
        for half in range(2):
            pNb = nR()
            for j in range(4):
                h = half * 4 + j
                c, hf = h // 2, h % 2
                pl = slice(hf * 64, hf * 64 + 64)
                kb.op("pe", lambda e: e.matmul(pNb[:T, j * T:(j + 1) * T], ATm[hf][:, c, :T], BTm[hf][:, c, :T], start=True, stop=True),
                      reads=[ATm[hf], BTm[hf]], writes=[pNb], inc=(j == 3))
            kb.op("dve", lambda e: e.tensor_tensor(out=XT[:T, half * 4:half * 4 + 4, :T], in0=v3(pNb)[:T], in1=slb, op=ALU.mult),
                  reads=[pNb, sl], writes=[XT])
        kb.op("dve", lambda e: e.tensor_tensor(out=Pm[:T, :, :T], in0=X[:T, :, :T],
                                               in1=g.ident_f[:T, :T].unsqueeze(1).to_broadcast([T, 8, T]), op=ALU.add),
              reads=[X, g.ident_f], writes=[Pm])

    def gen_neumann(ti):
        r0, T = g.tiles[ti]
        HT = HTs[ti % 3]
        mixT = mixTs[ti % 2]
        GG = GGs[ti % 2]
        bonus = bonuss[ti % 2]
        def proj_fm(pbank, j, col):
            for kc in range(8):
                kb.op("pe", lambda e: e.matmul(pbank[:, j * T:(j + 1) * T], Win[:, kc, col:col + 128], hnT[:, kc, :T],
                                               start=(kc == 0), stop=(kc == 7)),
                      reads=[Win, hnT], writes=[pbank], inc=(kc == 7))

        def proj_tm(pbank, col, n, c0=0):
            for kc in range(8):
                kb.op("pe", lambda e: e.matmul(pbank[:T, c0:c0 + n], hnT[:, kc, :T], Win[:, kc, col:col + n],
                                               start=(kc == 0), stop=(kc == 7)),
                      reads=[hnT, Win], writes=[pbank], inc=(kc == 7))

        def v3(pbank, n=4):
            return pbank[:, 0:n * T].rearrange("p (c t) -> p c t", c=n)

        yield
        lv = 1
        cur = 0
        while lv * 2 < T:
            X, XT = Xa[cur], XTa[cur]
            Xn, XTn = Xa[1 - cur], XTa[1 - cur]
            for half in range(2):
                yield
                p1 = nS(); p2 = nS()
                for j in range(4):
                    h = half * 4 + j
                    kb.op("pe", lambda e: e.matmul(p1[:T, j * T:(j + 1) * T], XT[:T, h, :T], X[:T, h, :T], start=True, stop=True),
                          reads=[XT, X], writes=[p1], inc=(j == 3))
                for j in range(4):
                    h = half * 4 + j
                    kb.op("pe", lambda e: e.matmul(p2[:T, j * T:(j + 1) * T], X[:T, h, :T], XT[:T, h, :T], start=True, stop=True),
                          reads=[XT, X], writes=[p2], inc=(j == 3))
                kb.op("act", lambda e: e.copy(out=Xn[:T, half * 4:half * 4 + 4, :T], in_=v3(p1)[:T]), reads=[p1], writes=[Xn])
                kb.op("dve", lambda e: e.tensor_copy(out=XTn[:T, half * 4:half * 4 + 4, :T], in_=v3(p2)[:T]), reads=[p2], writes=[XTn])
            yield
            for half in range(2):
                p3 = nS()
                for j in range(4):
                    h = half * 4 + j
                    kb.op("pe", lambda e: e.matmul(p3[:T, j * T:(j + 1) * T], XTn[:T, h, :T], Pm[:T, h, :T], start=True, stop=True),
                          reads=[XTn, Pm], writes=[p3], inc=(j == 3))
                kb.op("dve", lambda e: e.tensor_tensor(out=Pm[:T, half * 4:half * 4 + 4, :T], in0=Pm[:T, half * 4:half * 4 + 4, :T],
                                                       in1=v3(p3)[:T], op=ALU.add), reads=[Pm, p3], writes=[Pm])
            cur = 1 - cur
            lv *= 2

    def tail1(ti):
        r0, T = g.tiles[ti]
        HT = HTs[ti % 3]
        mixT = mixTs[ti % 2]
        GG = GGs[ti % 2]
        bonus = bonuss[ti % 2]
        def proj_fm(pbank, j, col):
            for kc in range(8):
                kb.op("pe", lambda e: e.matmul(pbank[:, j * T:(j + 1) * T], Win[:, kc, col:col + 128], hnT[:, kc, :T],
                                               start=(kc == 0), stop=(kc == 7)),
                      reads=[Win, hnT], writes=[pbank], inc=(kc == 7))

        def proj_tm(pbank, col, n, c0=0):
            for kc in range(8):
                kb.op("pe", lambda e: e.matmul(pbank[:T, c0:c0 + n], hnT[:, kc, :T], Win[:, kc, col:col + n],
                                               start=(kc == 0), stop=(kc == 7)),
                      reads=[hnT, Win], writes=[pbank], inc=(kc == 7))

        def v3(pbank, n=4):
            return pbank[:, 0:n * T].rearrange("p (c t) -> p c t", c=n)

        cur = ((T.bit_length() - 2) % 2) if T > 2 else 0
        pP1 = nS()
        for h in range(8):
            c, hf = h // 2, h % 2
            pl = slice(hf * 64, hf * 64 + 64)
            kb.op("pe", lambda e: e.matmul(pP1[:T, h * 64:(h + 1) * 64], ATm[hf][:, c, :T], STb[:, c, :], start=True, stop=False),
                  reads=[ATm[hf], STb], writes=[pP1], inc=False)
            kb.op("pe", lambda e: e.matmul(pP1[:T, h * 64:(h + 1) * 64], AAK[:T, h, :T], VTM[:T, h, :], start=False, stop=True),
                  reads=[AAK, VTM], writes=[pP1], inc=(h == 7))
        kb.op("act", lambda e: e.copy(out=P1[:T, :], in_=pP1[:T, :]), reads=[pP1], writes=[P1])
        pU = nS()
        for h in range(8):
            kb.op("pe", lambda e: e.matmul(pU[:T, h * 64:(h + 1) * 64], Pm[:T, h, :T], P1[:T, h * 64:(h + 1) * 64], start=True, stop=True),
                  reads=[Pm, P1], writes=[pU], inc=(h == 7))
        kb.op("act", lambda e: e.copy(out=UTM[:T, :, :], in_=pU[:T, :].rearrange("p (h v) -> p h v", h=8)), reads=[pU], writes=[UTM])
        pO = nS()
        for c in range(4):
            kb.op("pe", lambda e: e.matmul(pO[:, c * T:(c + 1) * T], STbd[:, c, :], AR[:, c, 1, :T], start=True, stop=False),
                  reads=[STbd, AR], writes=[pO], inc=False)
            for hf in range(2):
                h = 2 * c + hf
                pl = slice(hf * 64, hf * 64 + 64)
                kb.op("pe", lambda e: e.matmul(pO[pl, c * T:(c + 1) * T], UTM[:T, h, :], ARB[:T, h, :T], start=False, stop=False),
                      reads=[UTM, ARB], writes=[pO], inc=False)
                kb.op("pe", lambda e: e.matmul(pO[pl, c * T:(c + 1) * T], VTM[:T, h, :], ARK[:T, h, :T], start=False, stop=True),
                      reads=[VTM, ARK], writes=[pO], inc=(hf == 1))
        kb.op("act", lambda e: e.copy(out=Of[:, :, :T], in_=v3(pO)), reads=[pO], writes=[Of])
        pS = nS()
        for h in range(8):
            c, hf = h // 2, h % 2
            pl = slice(hf * 64, hf * 64 + 64)
            kb.op("pe", lambda e: e.matmul(pS[pl, c * 64:(c + 1) * 64], BHT[:T, h, :], UTM[:T, h, :], start=True, stop=False),
                  reads=[BHT, UTM], writes=[pS], inc=False)
            kb.op("pe", lambda e: e.matmul(pS[pl, c * 64:(c + 1) * 64], KHT[:T, h, :], VTM[:T, h, :], start=False, stop=True),
                  reads=[KHT, VTM], writes=[pS], inc=(h == 7))
        for c in range(4):
            kb.op("dve", lambda e: e.scalar_tensor_tensor(out=ST[:, c, :], in0=ST[:, c, :], scalar=eW[:, c, T - 1:T],
                                                          in1=pS[:, c * 64:(c + 1) * 64], op0=ALU.mult, op1=ALU.add),
                  reads=[ST, eW, pS], writes=[ST])
        kb.op("act", lambda e: e.copy(out=STb[:, :, :], in_=ST[:, :, :]), reads=[ST], writes=[STb])
        kb.op("act", lambda e: e.copy(out=STbd[0:64, :, 0:64], in_=ST[0:64, :, :]), reads=[ST], writes=[STbd])
        kb.op("act", lambda e: e.copy(out=STbd[64:128, :, 64:128], in_=ST[64:128, :, :]), reads=[ST], writes=[STbd])

    def gen_tail2(ti):
        r0, T = g.tiles[ti]
        HT = HTs[ti % 3]
        mixT = mixTs[ti % 2]
        GG = GGs[ti % 2]
        bonus = bonuss[ti % 2]
        def proj_fm(pbank, j, col):
            for kc in range(8):
                kb.op("pe", lambda e: e.matmul(pbank[:, j * T:(j + 1) * T], Win[:, kc, col:col + 128], hnT[:, kc, :T],
                                               start=(kc == 0), stop=(kc == 7)),
                      reads=[Win, hnT], writes=[pbank], inc=(kc == 7))

        def proj_tm(pbank, col, n, c0=0):
            for kc in range(8):
                kb.op("pe", lambda e: e.matmul(pbank[:T, c0:c0 + n], hnT[:, kc, :T], Win[:, kc, col:col + n],
                                               start=(kc == 0), stop=(kc == 7)),
                      reads=[hnT, Win], writes=[pbank], inc=(kc == 7))

        def v3(pbank, n=4):
            return pbank[:, 0:n * T].rearrange("p (c t) -> p c t", c=n)

        cur = ((T.bit_length() - 2) % 2) if T > 2 else 0
        kb.op("act", lambda e: e.activation(out=Osq[:, :, :T], in_=Of[:, :, :T], func=AF.Square), reads=[Of], writes=[Osq])
        yield
        pm_ = nS(); pq_ = nS()
        for c in range(4):
            kb.op("pe", lambda e: e.matmul(pm_[:, c * T:(c + 1) * T], blk64[:, :], Of[:, c, :T], start=True, stop=True),
                  reads=[blk64, Of], writes=[pm_], inc=(c == 3))
        for c in range(4):
            kb.op("pe", lambda e: e.matmul(pq_[:, c * T:(c + 1) * T], blk64[:, :], Osq[:, c, :T], start=True, stop=True),
                  reads=[blk64, Osq], writes=[pq_], inc=(c == 3))
        kb.op("act", lambda e: e.copy(out=mean_s[:, :, :T], in_=v3(pm_)), reads=[pm_], writes=[mean_s])
        yield
        kb.op("dve", lambda e: e.scalar_tensor_tensor(out=var[:, :, :T], in0=mean_s[:, :, :T], scalar=-1.0, in1=mean_s[:, :, :T],
                                                      op0=ALU.mult, op1=ALU.mult), reads=[mean_s], writes=[var])
        kb.op("dve", lambda e: e.tensor_tensor(out=var[:, :, :T], in0=var[:, :, :T], in1=v3(pq_), op=ALU.add),
              reads=[var, pq_], writes=[var])
        yield
        kb.op("dve", lambda e: e.tensor_scalar(out=var[:, :, :T], in0=var[:, :, :T], scalar1=0.0, scalar2=None, op0=ALU.max),
              reads=[var], writes=[var])
        kb.op("act", lambda e: e.activation(out=var[:, :, :T], in_=var[:, :, :T], func=AF.Sqrt, bias=64e-5), reads=[var], writes=[var])
        yield
        kb.op("dve", lambda e: e.reciprocal(out=var[:, :, :T], in_=var[:, :, :T]), reads=[var], writes=[var])
        kb.op("dve", lambda e: e.tensor_tensor(out=Of[:, :, :T], in0=Of[:, :, :T], in1=mean_s[:, :, :T], op=ALU.subtract),
              reads=[Of, mean_s], writes=[Of])
        yield
        kb.op("dve", lambda e: e.tensor_tensor(out=Of[:, :, :T], in0=Of[:, :, :T], in1=var[:, :, :T], op=ALU.mult),
              reads=[Of, var], writes=[Of])
        kb.op("dve", lambda e: e.tensor_tensor(out=Of[:, :, :T], in0=Of[:, :, :T], in1=b3(LNW, T), op=ALU.mult),
              reads=[Of, LNW], writes=[Of])
        yield
        kb.op("dve", lambda e: e.tensor_tensor(out=Of[:, :, :T], in0=Of[:, :, :T], in1=b3(LNB, T), op=ALU.add),
              reads=[Of, LNB], writes=[Of])
        kb.op("dve", lambda e: e.tensor_tensor(out=Of[:, :, :T], in0=Of[:, :, :T], in1=bonus[:, :, :T], op=ALU.add),
              reads=[Of, bonus], writes=[Of])
        yield
        kb.op("dve", lambda e: e.tensor_tensor(out=mixT[:, 4:8, :T], in0=Of[:, :, :T], in1=GG[:, :, :T], op=ALU.mult),
              reads=[Of, GG], writes=[mixT])
        for nb in range(2):
            pp = nS()
            for c in range(8):
                kb.op("pe", lambda e: e.matmul(pp[:T, :], mixT[:, c, :T], Wout[:, c, nb * 512:(nb + 1) * 512],
                                               start=(c == 0), stop=(c == 7)), reads=[mixT, Wout], writes=[pp], inc=(c == 7))
            kb.op("dve", lambda e: e.tensor_tensor(out=HT[:T, nb * 512:(nb + 1) * 512], in0=HT[:T, nb * 512:(nb + 1) * 512],
                                                   in1=pp[:T, :], op=ALU.add), reads=[HT, pp], writes=[HT])
        yield
        store_h(g, dst, ti, HT, final, None, (ss, rstd, junk))


    def run(gen):
        for _ in gen:
            pass

    def interleave(a, b, ra=1, rb=1):
        da = db = False
        while not (da and db):
            for _ in range(ra):
                if not da:
                    try:
                        next(a)
                    except StopIteration:
                        da = True
            for _ in range(rb):
                if not db:
                    try:
                        next(b)
                    except StopIteration:
                        db = True

    cost = op_cost

    def norm_and_proj(ti):
        rmsnorm_T(g, HTs[ti % 3], g.tiles[ti][1], Gb, hn, hnT, ss, rstd, junk)
        return gen_proj(ti)

    def s0(ti):
        for _ in gen_neumann(ti):
            pass
        tail1(ti)
        for _ in gen_tail2(ti):
            pass
        if ti + 3 < ntl:
            load_h(g, src, ti + 3, HTs[ti % 3])

    ntl = len(g.tiles)
    for k_ in range(min(3, ntl)):
        load_h(g, src, k_, HTs[k_])
    NOSCHED = bool(os.environ.get("L0_NOSCHED"))

    def rec(f):
        if NOSCHED:
            r = f()
            if r is not None and hasattr(r, "__next__"):
                for _ in r:
                    pass
            return []
        return kb.record(f)

    run(norm_and_proj(0))
    pro = [rec(lambda: gen_mlstm(0)), rec(lambda: gen_prep(0))]
    if ntl > 1:
        pro.append(rec(lambda: norm_and_proj(1)))
    kb.schedule(pro, cost, sync_ns=float(os.environ.get('SYNC_NS', '0')), slack_ns=float(os.environ.get('SLACK_NS', '0')))
    for ti in range(ntl):
        streams = [rec(lambda: s0(ti))]
        if ti + 1 < ntl:
            streams.append(rec(lambda: gen_mlstm(ti + 1)))
            streams.append(rec(lambda: gen_prep(ti + 1)))
        if ti + 2 < ntl:
            streams.append(rec(lambda: norm_and_proj(ti + 2)))
        kb.schedule(streams, cost, sync_ns=float(os.environ.get('SYNC_NS', '0')), slack_ns=float(os.environ.get('SLACK_NS', '0')))


LG = [float(np.log(1.0 - 2.0 ** (-5.0 - h))) for h in range(4)]
TWO_PI = 6.283185307179586
CW1 = 6.28125
CW2 = TWO_PI - CW1


LG = [float(np.log(1.0 - 2.0 ** (-5.0 - h))) for h in range(4)]
TWO_PI = 6.283185307179586
CW1 = 6.28125
CW2 = TWO_PI - CW1


LG = [float(np.log(1.0 - 2.0 ** (-5.0 - h))) for h in range(4)]
TWO_PI = 6.283185307179586
CW1 = 6.28125
CW2 = TWO_PI - CW1


def phase_l1(g, src, dst, final):
    kb, nc, dr = g.kb, g.nc, g.dr
    Win = kb.sb([128, 8, 6144], BF16, "Win")
    Wout = kb.sb([128, 16, D], BF16, "Wout")
    with contextlib.ExitStack() as ses:
        old = kb.es
        kb.es = ses
        stg = [kb.sb([128, 1536], F32, f"stg{i}") for i in range(3)]
        load_weight_bf16(g, dr["o_w_in_p"], 0, D, 6144, Win, stg)
        load_weight_bf16(g, dr["o_w_out"], 0, 2048, D, Wout, stg)
        kb.barrier()
        kb.es = old
    Gb = kb.sb([128, D], BF16, "Gb")
    Gfin = None
    iota = kb.sb([128, 128], F32, "iota")
    pidx = kb.sb([128, 1], F32, "pidx")
    ue = kb.sb([128, 128], F32, "ue")
    inv = kb.sb([128, 1], F32, "inv")
    kb.dma(iota[:, :], dr["c_iota"].ap()[:, :], writes=[iota], sem_buf=iota)
    kb.dma(pidx[:, :], dr["c_pidx"].ap()[:, :], writes=[pidx], sem_buf=pidx)
    kb.dma(ue[:, :], dr["c_ue"].ap()[:, :], writes=[ue], sem_buf=ue)
    kb.dma(inv[:, :], dr["c_inv"].ap()[:, :], writes=[inv], sem_buf=inv)
    DM = kb.sb([128, 4, 128], F32, "DM")
    DEC = kb.sb([128, 4, 128], F32, "DEC")
    KDEC = {128: kb.sb([128, 4], F32, "KDEC128"), 16: kb.sb([128, 4], F32, "KDEC16")}
    tms = kb.sb([128, 128], F32, "tms")
    kb.op("dve", lambda e: e.tensor_scalar(out=tms[:, :], in0=iota[:, :], scalar1=pidx[:, 0:1], scalar2=0.0,
                                           op0=ALU.subtract, op1=ALU.max), reads=[iota, pidx], writes=[tms])
    for h in range(4):
        kb.op("act", lambda e: e.activation(out=DM[:, h, :], in_=tms[:, :], func=AF.Exp, scale=LG[h]),
              reads=[tms], writes=[DM])
        kb.op("dve", lambda e: e.scalar_tensor_tensor(out=DM[:, h, :], in0=DM[:, h, :], scalar=1.0 / 16.0,
                                                      in1=ue[:, :], op0=ALU.mult, op1=ALU.mult),
              reads=[DM, ue], writes=[DM])
        kb.op("act", lambda e: e.activation(out=DEC[:, h, :], in_=iota[:, :], func=AF.Exp, scale=LG[h], bias=LG[h]),
              reads=[iota], writes=[DEC])
        for TT in (128, 16):
            kd = KDEC[TT]
            kb.op("act", lambda e: e.activation(out=kd[:, h:h + 1], in_=pidx[:, 0:1], func=AF.Exp, scale=-LG[h],
                                                bias=LG[h] * (TT - 1)), reads=[pidx], writes=[kd])
            kb.op("dve", lambda e: e.tensor_scalar(out=kd[:, h:h + 1], in0=kd[:, h:h + 1], scalar1=1.0 / 16.0,
                                                   scalar2=None, op0=ALU.mult), reads=[kd], writes=[kd])
    Sr = kb.sb([128, 8, 512], F32, "Sr")
    Srb = kb.sb([128, 8, 512], BF16, "Srb")
    kb.op("dve", lambda e: e.memset(Sr[:, :, :], 0.0), writes=[Sr])
    kb.op("pool", lambda e: e.memset(Srb[:, :, :], 0.0), writes=[Srb])
    HTs = [kb.sb([128, D], F32, f"HT{i}") for i in range(2)]
    hn = kb.sb([128, D], BF16, "hn")
    hnT = kb.sb([128, 8, 128], BF16, "hnT")
    sss = [kb.sb([128, 1], F32, f"ss{i}") for i in range(2)]
    rstds = [kb.sb([128, 1], F32, f"rstd{i}") for i in range(2)]
    ang = kb.sb([128, 128], F32, "ang")
    ang2 = kb.sb([128, 128], F32, "ang2")
    kf = kb.sb([128, 128], F32, "kf")
    ki = kb.sb([128, 128], I32, "ki")
    nsins = [kb.sb([128, 128], F32, f"nsin{i}") for i in range(2)]
    ncoss = [kb.sb([128, 128], F32, f"ncos{i}") for i in range(2)]
    t1 = kb.sb([128, 4, 128], F32, "t1")
    t2 = kb.sb([128, 4, 128], F32, "t2")
    qb = kb.sb([128, 2, 4, 128], BF16, "qb")
    qdb = kb.sb([128, 2, 4, 128], BF16, "qdb")
    kbf = kb.sb([128, 2, 4, 128], BF16, "kbf")
    kdT = kb.sb([128, 8, 128], BF16, "kdT")
    sTm = kb.sb([128, 4, 128], BF16, "sTm")
    VT = kb.sb([128, 2048], BF16, "VT")
    GS = kb.sb([128, 2048], BF16, "GS")
    og = kb.sb([128, 2048], BF16, "og")
    ogT = kb.sb([128, 16, 128], BF16, "ogT")
    st6 = kb.sb([128, 6], F32, "st6")
    mv = kb.sb([128, 2], F32, "mv")
    rs = kb.sb([128, 1], F32, "rs")
    junk = og
    kb.dma(t1[:, :, :].rearrange("p a b -> p (a b)"), bc_rows(dr["norm_mix"], 1, 512), writes=[t1], sem_buf=t1)
    kb.op("dve", lambda e: e.tensor_copy(out=Gb[:, 0:512], in_=t1[:, :, :].rearrange("p a b -> p (a b)")), reads=[t1], writes=[Gb])
    kb.dma(t2[:, :, :].rearrange("p a b -> p (a b)"), bc_rows(dr["norm_mix"], 1, 512, col0=512), writes=[t2], sem_buf=t2)
    kb.op("dve", lambda e: e.tensor_copy(out=Gb[:, 512:1024], in_=t2[:, :, :].rearrange("p a b -> p (a b)")), reads=[t2], writes=[Gb])

    def sincos(dst_tbl, shift, pos0, T):
        kb.op("dve", lambda e: e.tensor_scalar(out=ang[:, :T], in0=iota[:, :T], scalar1=float(pos0), scalar2=inv[:, 0:1],
                                               op0=ALU.add, op1=ALU.mult), reads=[iota, inv], writes=[ang])
        if shift != 0.0:
            kb.op("dve", lambda e: e.tensor_scalar(out=ang[:, :T], in0=ang[:, :T], scalar1=shift, scalar2=None,
                                                   op0=ALU.add), reads=[ang], writes=[ang])
        kb.op("dve", lambda e: e.tensor_scalar(out=ki[:, :T], in0=ang[:, :T], scalar1=1.0 / TWO_PI, scalar2=None,
                                               op0=ALU.mult), reads=[ang], writes=[ki])
        kb.op("dve", lambda e: e.tensor_copy(out=kf[:, :T], in_=ki[:, :T]), reads=[ki], writes=[kf])
        kb.op("dve", lambda e: e.scalar_tensor_tensor(out=ang2[:, :T], in0=kf[:, :T], scalar=-CW1, in1=ang[:, :T],
                                                      op0=ALU.mult, op1=ALU.add), reads=[kf, ang], writes=[ang2])
        kb.op("dve", lambda e: e.scalar_tensor_tensor(out=ang2[:, :T], in0=kf[:, :T], scalar=-CW2, in1=ang2[:, :T],
                                                      op0=ALU.mult, op1=ALU.add), reads=[kf, ang2], writes=[ang2])
        kb.op("dve", lambda e: e.tensor_scalar(out=ang2[:, :T], in0=ang2[:, :T], scalar1=3.1415925, scalar2=-3.1415925,
                                               op0=ALU.min, op1=ALU.max), reads=[ang2], writes=[ang2])
        kb.op("act", lambda e: e.activation(out=dst_tbl[:, :T], in_=ang2[:, :T], func=AF.Sin),
              reads=[ang2], writes=[dst_tbl])

    ntl = len(g.tiles)
    load_h(g, src, 0, HTs[0])
    T0 = g.tiles[0][1]
    norm_stats(g, HTs[0], T0, Gb, hn, sss[0], rstds[0], junk)
    if ntl > 1:
        load_h(g, src, 1, HTs[1])
    norm_transpose(g, hn, hnT, T0)
    sincos(nsins[0], 0.0, g.tiles[0][0], T0)
    sincos(ncoss[0], np.pi / 2, g.tiles[0][0], T0)
    PST = g.PST
    for ti, (r0, T) in enumerate(g.tiles):
        HT = HTs[ti % 2]
        HO = HT
        nsin = nsins[ti % 2]
        ncos = ncoss[ti % 2]
        sb_ = nsin[:, :T].unsqueeze(1).to_broadcast([128, 4, T])
        cb_ = ncos[:, :T].unsqueeze(1).to_broadcast([128, 4, T])
        qk_banks = []
        for which in range(2):
            pe_ = next_ps(g)
            po_ = next_ps(g)
            qk_banks.append((pe_, po_))
            for eo, pb in ((0, pe_), (1, po_)):
                for h in range(4):
                    col = which * 1024 + h * 256 + eo * 128
                    for kc in range(8):
                        kb.op("pe", lambda e: e.matmul(pb[:, h * T:(h + 1) * T], Win[:, kc, col:col + 128],
                                                       hnT[:, kc, :T], start=(kc == 0), stop=(kc == 7)),
                              reads=[Win, hnT], writes=[pb], inc=(kc == 7))
        for which in range(2):
            pe_, po_ = qk_banks[which]
            pe3 = pe_[:, 0:4 * T].rearrange("p (h t) -> p h t", h=4)
            po3 = po_[:, 0:4 * T].rearrange("p (h t) -> p h t", h=4)
            dstb = qb if which == 0 else kbf
            kb.op("dve", lambda e: e.tensor_tensor(out=t1[:, :, :T], in0=pe3, in1=cb_, op=ALU.mult),
                  reads=[pe_, ncos], writes=[t1])
            kb.op("dve", lambda e: e.tensor_tensor(out=t2[:, :, :T], in0=po3, in1=sb_, op=ALU.mult),
                  reads=[po_, nsin], writes=[t2])
            kb.op("dve", lambda e: e.tensor_tensor(out=dstb[:, 0, :, :T], in0=t1[:, :, :T], in1=t2[:, :, :T],
                                                   op=ALU.subtract), reads=[t1, t2], writes=[dstb])
            kb.op("dve", lambda e: e.tensor_tensor(out=t1[:, :, :T], in0=po3, in1=cb_, op=ALU.mult),
                  reads=[po_, ncos], writes=[t1])
            kb.op("dve", lambda e: e.tensor_tensor(out=t2[:, :, :T], in0=pe3, in1=sb_, op=ALU.mult),
                  reads=[pe_, nsin], writes=[t2])
            kb.op("dve", lambda e: e.tensor_tensor(out=dstb[:, 1, :, :T], in0=t1[:, :, :T], in1=t2[:, :, :T],
                                                   op=ALU.add), reads=[t1, t2], writes=[dstb])
            if which == 0:
                for eo in range(2):
                    kb.op("pool", lambda e: e.tensor_tensor(out=qdb[:, eo, :, :T], in0=qb[:, eo, :, :T],
                                                            in1=DEC[:, :, :T], op=ALU.mult),
                          reads=[qb, DEC], writes=[qdb])
        for nb in range(4):
            pvv = next_ps(g)
            for kc in range(8):
                kb.op("pe", lambda e: e.matmul(pvv[:T, :], hnT[:, kc, :T], Win[:, kc, 2048 + nb * 512:2048 + (nb + 1) * 512],
                                               start=(kc == 0), stop=(kc == 7)), reads=[hnT, Win], writes=[pvv], inc=(kc == 7))
            kb.op("act", lambda e: e.copy(out=VT[:T, nb * 512:(nb + 1) * 512], in_=pvv[:T, :]), reads=[pvv], writes=[VT])
        for nb in range(4):
            pgg = next_ps(g)
            for kc in range(8):
                kb.op("pe", lambda e: e.matmul(pgg[:T, :], hnT[:, kc, :T], Win[:, kc, 4096 + nb * 512:4096 + (nb + 1) * 512],
                                               start=(kc == 0), stop=(kc == 7)), reads=[hnT, Win], writes=[pgg], inc=(kc == 7))
            kb.op("act", lambda e: e.activation(out=GS[:T, nb * 512:(nb + 1) * 512], in_=pgg[:T, :], func=AF.Silu),
                  reads=[pgg], writes=[GS])
        if ti + 1 < ntl:
            r0n, Tn = g.tiles[ti + 1]
            sincos(nsins[(ti + 1) % 2], 0.0, r0n, Tn)
            sincos(ncoss[(ti + 1) % 2], np.pi / 2, r0n, Tn)
        for h in range(4):
            for eo in range(2):
                j = h * 2 + eo
                kb.op("pe", lambda e: e.transpose(out=PST[:T, j * 128:(j + 1) * 128], in_=kbf[:, eo, h, :T],
                                                  identity=g.ident_b[:, :]),
                      reads=[kbf, g.ident_b], writes=[PST], inc=(j == 7))
        for h in range(4):
            kb.op("act", lambda e: e.activation(out=kdT[:T, 2 * h:2 * h + 2, :],
                                                in_=PST[:T, 2 * h * 128:(2 * h + 2) * 128].rearrange("p (j d) -> p j d", j=2),
                                                func=AF.Copy, scale=KDEC[T][:T, h:h + 1]),
                  reads=[PST, KDEC[T]], writes=[kdT])
        psc = next_ps(g)
        for h in range(4):
            for eo in range(2):
                kb.op("pe", lambda e: e.matmul(psc[:T, h * T:(h + 1) * T], kbf[:, eo, h, :T], qb[:, eo, h, :T],
                                               start=(eo == 0), stop=(eo == 1)),
                      reads=[kbf, qb], writes=[psc], inc=(eo == 1))
        kb.op("dve", lambda e: e.tensor_tensor(out=sTm[:T, :, :T],
                                               in0=psc[:T, 0:4 * T].rearrange("p (h t) -> p h t", h=4),
                                               in1=DM[:T, :, :T], op=ALU.mult), reads=[psc, DM], writes=[sTm])
        for h in range(4):
            po = next_ps(g)
            kb.op("pe", lambda e: e.matmul(po[:T, :], sTm[:T, h, :T], VT[:T, h * 512:(h + 1) * 512], start=True, stop=False),
                  reads=[sTm, VT], writes=[po], inc=False)
            for eo in range(2):
                kb.op("pe", lambda e: e.matmul(po[:T, :], qdb[:, eo, h, :T], Srb[:, 2 * h + eo, :], start=False, stop=(eo == 1)),
                      reads=[qdb, Srb], writes=[po], inc=(eo == 1))
            kb.op("dve", lambda e: e.bn_stats(out=st6[:T, :], in_=po[:T, :]), reads=[po], writes=[st6])
            kb.op("dve", lambda e: e.bn_aggr(out=mv[:T, :], in_=st6[:T, :]), reads=[st6], writes=[mv])
            kb.op("act", lambda e: e.activation(out=rs[:T, :], in_=mv[:T, 1:2], func=AF.Sqrt, scale=1.0, bias=1e-6),
                  reads=[mv], writes=[rs])
            kb.op("dve", lambda e: e.reciprocal(out=rs[:T, :], in_=rs[:T, :]), reads=[rs], writes=[rs])
            kb.op("dve", lambda e: e.tensor_scalar(out=og[:T, h * 512:(h + 1) * 512], in0=po[:T, :], scalar1=mv[:T, 0:1], scalar2=rs[:T, 0:1],
                                                   op0=ALU.subtract, op1=ALU.mult), reads=[po, mv, rs], writes=[og])
            kb.op("pool", lambda e: e.tensor_tensor(out=og[:T, h * 512:(h + 1) * 512], in0=og[:T, h * 512:(h + 1) * 512],
                                                    in1=GS[:T, h * 512:(h + 1) * 512], op=ALU.mult),
                  reads=[og, GS], writes=[og])
        gT = [float(np.exp(LG[h] * T)) for h in range(4)]
        for h in range(4):
            for eo in range(2):
                j = 2 * h + eo
                pst_ = next_ps(g)
                kb.op("pe", lambda e: e.matmul(pst_[:, :], kdT[:T, j, :], VT[:T, h * 512:(h + 1) * 512], start=True, stop=True),
                      reads=[kdT, VT], writes=[pst_])
                kb.op("dve", lambda e: e.scalar_tensor_tensor(out=Sr[:, j, :], in0=Sr[:, j, :], scalar=gT[h], in1=pst_[:, :],
                                                              op0=ALU.mult, op1=ALU.add), reads=[Sr, pst_], writes=[Sr])
                kb.op("act", lambda e: e.copy(out=Srb[:, j, :], in_=Sr[:, j, :]), reads=[Sr], writes=[Srb])
        for half in range(2):
            for j in range(8):
                c = half * 8 + j
                kb.op("pe", lambda e: e.transpose(out=PST[:, j * T:(j + 1) * T], in_=og[:T, c * 128:(c + 1) * 128],
                                                  identity=g.ident_b[:T, :T]),
                      reads=[og, g.ident_b], writes=[PST], inc=(j == 7))
            kb.op("act", lambda e: e.copy(out=ogT[:, half * 8:(half + 1) * 8, :T],
                                          in_=PST[:, 0:8 * T].rearrange("p (k t) -> p k t", k=8)),
                  reads=[PST], writes=[ogT])
        if ti + 1 < ntl:
            Tn = g.tiles[ti + 1][1]
            norm_stats(g, HTs[(ti + 1) % 2], Tn, Gb, hn, sss[(ti + 1) % 2], rstds[(ti + 1) % 2], junk)
        pps = []
        for nb in range(2):
            pp = next_ps(g)
            pps.append(pp)
            for c in range(16):
                kb.op("pe", lambda e: e.matmul(pp[:T, :], ogT[:, c, :T], Wout[:, c, nb * 512:(nb + 1) * 512],
                                               start=(c == 0), stop=(c == 15)), reads=[ogT, Wout], writes=[pp], inc=(c == 15))
        if ti + 1 < ntl:
            norm_transpose(g, hn, hnT, g.tiles[ti + 1][1])
        for nb in range(2):
            kb.op("dve", lambda e: e.tensor_tensor(out=HO[:T, nb * 512:(nb + 1) * 512],
                                                   in0=HT[:T, nb * 512:(nb + 1) * 512], in1=pps[nb][:T, :], op=ALU.add),
                  reads=[HT, pps[nb]], writes=[HO])
        store_h(g, dst, ti, HO, final, Gfin, (sss[ti % 2], rstds[ti % 2], junk))
        if ti + 2 < ntl:
            load_h(g, src, ti + 2, HTs[ti % 2])


def make_in_map(inputs, b, NT):
    m = {"x": np.ascontiguousarray(inputs["x"][b, :128 * NT])}
    for k, shp in W_SPECS.items():
        src_k = "o_w_in" if k == "o_w_in_p" else k
        m[k] = np.ascontiguousarray(np.asarray(inputs[src_k], np.float32).reshape(shp))
    m.update(host_consts())
    perm = np.arange(6144)
    for sec in range(2):
        for h in range(4):
            base = sec * 1024 + h * 256
            perm[base:base + 256] = np.concatenate([base + np.arange(0, 256, 2), base + np.arange(1, 256, 2)])
    m["o_w_in_p"] = np.ascontiguousarray(m["o_w_in_p"][:, perm])
    return m


def op_cost(o):
    if o[0] == "dma":
        return 60.0
    return _ESC.get(o[1], 1.0) * float(COST_TAB.get(o[6] - host_consts.__code__.co_firstlineno, COST_DEFAULT.get(o[1], 300.0)))


_ESC = {"pe": float(_os.environ.get("PE_SC", "1.0")), "dve": float(_os.environ.get("DVE_SC", "1.0")), "act": float(_os.environ.get("ACT_SC", "1.0"))}
COST_DEFAULT = {"pe": 120.0, "dve": 450.0, "act": 450.0, "pool": 900.0}
COST_TAB = {248: 2442.0, 303: 56.0, 309: 56.0, 318: 474.2, 325: 259.0, 329: 352.0, 333: 351.0, 337: 630.0, 339: 689.0, 273: 427.0, 349: 100.2, 286: 689.0, 30: 45.0, 58: 227.0, 145: 959.0, 143: 1471.0, 380: 428.0, 383: 428.0, 397: 227.0, 400: 227.0, 403: 153.0, 405: 63.0, 408: 132.0, 165: 264.0, 411: 139.0, 413: 140.0, 415: 142.0, 167: 181.0, 436: 1025.5, 438: 485.0, 440: 488.0, 457: 399.2, 173: 957.2, 472: 485.0, 153: 1283.0, 155: 164.0, 176: 1284.2, 181: 107.0, 184: 1012.0, 524: 56.0, 541: 585.2, 545: 585.2, 530: 216.0, 549: 1283.0, 552: 501.8, 555: 588.0, 559: 66.0, 591: 98.0, 567: 585.0, 593: 1283.0, 596: 205.0, 594: 171.0, 597: 1283.0, 598: 171.0, 602: 343.0, 603: 159.0, 605: 410.0, 610: 214.0, 612: 692.0, 615: 296.5, 617: 480.0, 618: 162.2, 622: 599.2, 620: 212.8, 628: 123.5, 630: 692.0, 635: 134.2, 637: 112.0, 639: 597.0, 640: 427.0, 645: 135.5, 647: 111.5, 642: 3353.0, 649: 691.0, 651: 629.2, 655: 214.0, 657: 1283.0, 658: 3353.2, 659: 693.0, 661: 693.0, 663: 691.2, 668: 163.0, 674: 267.0, 678: 350.0, 682: 617.5, 683: 427.2, 708: 1935.0, 710: 2026.0, 712: 2025.2, 714: 91.2, 718: 1283.0, 743: 692.0, 719: 295.0, 720: 309.2, 724: 95.2, 731: 39.0, 727: 261.8, 734: 258.0, 738: 122.8, 740: 475.0, 744: 529.0, 747: 214.0, 749: 1283.0, 750: 427.0, 752: 3353.0, 753: 693.0, 757: 601.8, 759: 693.0, 764: 316.2, 770: 1283.0, 766: 692.0, 768: 82.0, 771: 519.0, 772: 520.0, 774: 293.0, 778: 693.0, 780: 601.0, 781: 603.2, 787: 507.2, 783: 693.0, 785: 600.0, 790: 603.0, 791: 693.0, 795: 214.2, 802: 107.0, 797: 692.0, 817: 661.2, 805: 107.0, 819: 663.0, 821: 663.0, 807: 586.0, 808: 488.0, 810: 135.0, 833: 107.0, 837: 107.0, 812: 585.0, 841: 425.0, 843: 331.0, 845: 334.0, 847: 331.0, 856: 123.2, 858: 691.2, 861: 1131.5, 897: 112.0, 901: 112.0, 903: 585.0, 904: 692.0, 910: 112.0, 912: 689.0, 944: 60.2, 946: 59.2, 948: 585.0, 951: 112.0, 957: 220.0, 953: 585.2, 962: 192.8, 964: 49.2, 972: 47.0, 974: 47.0, 966: 585.0, 977: 280.0, 980: 404.2, 981: 306.2, 982: 401.2, 1007: 629.2, 1011: 214.0, 1014: 214.0, 1016: 585.0, 1018: 693.0, 1020: 689.0, 1023: 427.0, 1025: 1283.0, 1027: 3353.0, 1028: 692.0, 1031: 692.0, 1033: 692.0, 1036: 692.0, 1038: 692.0, 1041: 692.0, 1047: 427.0, 1049: 689.0, 191: 946.2, 194: 1285.0, 204: 107.0, 207: 1012.0, 1189: 3509.0, 1169: 1050.0, 1172: 310.0, 1177: 200.0, 1181: 93.0, 1174: 293.0, 1183: 153.0, 1188: 3472.0, 1217: 279.0, 1219: 329.0, 1223: 197.0, 1228: 226.2, 1230: 228.2, 1231: 292.2, 1267: 56.0, 1233: 293.0, 1235: 227.0, 1237: 309.0, 1226: 227.0, 1297: 216.0, 1276: 692.0, 1278: 692.0, 1280: 693.0, 1282: 600.2, 1284: 598.0, 1286: 693.0, 1290: 1640.0, 1299: 585.0, 1303: 216.0, 1305: 597.0, 1316: 56.0, 1328: 56.0, 1331: 629.8, 1320: 455.0, 1337: 389.2, 1340: 376.8, 1342: 627.0, 1343: 182.0, 1344: 203.0, 1346: 164.0, 1347: 810.0, 1358: 582.0, 1349: 1153.0, 1360: 689.0, 1362: 618.0, 1367: 107.0, 1370: 1012.0, 1382: 357.0, 1387: 689.0, 116: 955.8, 119: 1285.5}


NT_FULL = 32


def kernel(**inputs):
    nc = build(NT_FULL, phases=(1, 2, 3, 4), debug=False, final=True)
    in_maps = [make_in_map(inputs, b, NT_FULL) for b in range(8)]
    res = run_bass_kernel_spmd(nc, in_maps, core_ids=list(range(8)))
    return np.stack([np.asarray(r["out"], np.float32) for r in res.results], axis=0)
```

```python
import contextlib
import numpy as np
import concourse.bass as bass
import concourse.mybir as mybir

import os as _os_fw
_STRICT = bool(_os_fw.environ.get("KB_STRICT"))
_NPE_BIAS = float(_os_fw.environ.get("NPE_BIAS", "0"))
_CARRY = bool(int(_os_fw.environ.get("SCHED_CARRY", "0")))
F32 = mybir.dt.float32
BF16 = mybir.dt.bfloat16
I32 = mybir.dt.int32
AF = mybir.ActivationFunctionType
ALU = mybir.AluOpType
AX = mybir.AxisListType


class Buf:
    __slots__ = ("t", "w", "r", "dsem", "dcount", "name", "excl")

    def __init__(self, t, name=""):
        self.t = t
        self.w = {}
        self.r = {}
        self.dsem = None
        self.dcount = 0
        self.name = name
        self.excl = False

    def __getitem__(self, idx):
        return self.t[idx]


class _Cap:
    def __init__(self):
        self.call = None

    def __getattr__(self, name):
        def f(*args, **kwargs):
            self.call = (name, args, kwargs)
            return None
        return f


class Eng:
    def __init__(self, name, obj, sem):
        self.name = name
        self.obj = obj
        self.sem = sem
        self.count = 0
        self.seen = {}


class KB:
    def __init__(self, nc, es):
        self.nc = nc
        self.es = es
        self.sems = {}
        self.E = {}
        for name, obj in (("pe", nc.tensor), ("act", nc.scalar), ("dve", nc.vector),
                          ("pool", nc.gpsimd), ("sp", nc.sync)):
            sem = es.enter_context(nc.semaphore("s_" + name))
            self.E[name] = Eng(name, obj, sem)
            self.sems[id(sem)] = sem
        self.dma_tokens = {}
        self.nbuf = 0
        self.rec = None

    def sb(self, shape, dt, name=None):
        self.nbuf += 1
        name = f"{name or 'b'}_{self.nbuf}"
        t = self.es.enter_context(self.nc.sbuf_tensor(name, list(shape), dt))
        return Buf(t, name)

    def ps(self, shape, dt, name=None):
        self.nbuf += 1
        name = f"{name or 'p'}_{self.nbuf}"
        t = self.es.enter_context(self.nc.psum_tensor(name, list(shape), dt))
        b = Buf(t, name)
        b.excl = True
        return b

    def newsem(self, name):
        sem = self.es.enter_context(self.nc.semaphore(name))
        self.sems[id(sem)] = sem
        return sem

    def _wait(self, e, deps):
        for sid, val in deps.items():
            if e.seen.get(sid, 0) < val:
                e.obj.wait_ge(self.sems[sid], val)
                e.seen[sid] = val

    def _deps(self, e, reads, writes):
        deps = {}
        own = id(e.sem)
        for b in reads:
            for sid, v in b.w.items():
                if deps.get(sid, 0) < v:
                    deps[sid] = v
            if b.excl:
                for sid, v in b.r.items():
                    if sid != own and deps.get(sid, 0) < v:
                        deps[sid] = v
        skip_own = True if not _STRICT else (e.name == "pe")
        for b in writes:
            for d in (b.w, b.r):
                for sid, v in d.items():
                    if sid == own and skip_own:
                        continue
                    if deps.get(sid, 0) < v:
                        deps[sid] = v
        return deps

    def op(self, eng, fn, reads=(), writes=(), inc=True):
        if self.rec is not None:
            import sys as _sys
            cap = _Cap()
            fn(cap)
            name, args, kwargs = cap.call
            fn2 = (lambda e_, name=name, args=args, kwargs=kwargs: getattr(e_, name)(*args, **kwargs))
            self.rec.append(("op", eng, fn2, tuple(reads), tuple(writes), inc, _sys._getframe(1).f_lineno))
            return None
        e = self.E[eng]
        self._wait(e, self._deps(e, reads, writes))
        ins = fn(e.obj)
        if inc:
            e.count += 1
            ins.then_inc(e.sem, 1)
            val = e.count
        else:
            val = e.count + 1
        sid = id(e.sem)
        for b in reads:
            if b.r.get(sid, 0) < val:
                b.r[sid] = val
        for b in writes:
            if b.w.get(sid, 0) < val:
                b.w[sid] = val
        return ins

    def dma(self, out_ap, in_ap, reads=(), writes=(), sem_buf=None, q="sp"):
        if self.rec is not None:
            import sys as _sys
            self.rec.append(("dma", q, (out_ap, in_ap, sem_buf), tuple(reads), tuple(writes), True, _sys._getframe(1).f_lineno))
            return None
        e = self.E[q]
        self._wait(e, self._deps(e, reads, writes))
        b = sem_buf
        if b.dsem is None:
            b.dsem = self.newsem("d_" + b.name)
        b.dcount += 16
        e.obj.dma_start(out=out_ap, in_=in_ap).then_inc(b.dsem, 16)
        sid = id(b.dsem)
        for x in reads:
            x.r[sid] = b.dcount
        for x in writes:
            x.w[sid] = b.dcount
        self.dma_tokens[sid] = b.dcount

    def barrier(self):
        targets = {id(e.sem): e.count for e in self.E.values() if e.count > 0}
        targets.update(self.dma_tokens)
        for e in self.E.values():
            self._wait(e, {k: v for k, v in targets.items() if k != id(e.sem)})

    def final_wait(self):
        e = self.E["sp"]
        self._wait(e, dict(self.dma_tokens))

    def record(self, fn):
        assert self.rec is None
        self.rec = []
        try:
            r = fn()
            if r is not None and hasattr(r, "__next__"):
                for _ in r:
                    pass
        finally:
            ops, self.rec = self.rec, None
        return ops

    def schedule(self, streams, cost_fn, sync_ns=0.0, slack_ns=0.0):
        streams = [list(x) for x in streams if x]
        n = len(streams)
        key = lambda b: id(b.w)
        rem_r = [dict() for _ in range(n)]
        rem_w = [dict() for _ in range(n)]
        for k, st in enumerate(streams):
            for o in st:
                for b in o[3]:
                    rem_r[k][key(b)] = rem_r[k].get(key(b), 0) + 1
                for b in o[4]:
                    rem_w[k][key(b)] = rem_w[k].get(key(b), 0) + 1
        pos = [0] * n
        if _CARRY and getattr(self, "_sst", None) is not None:
            eng_t, w_t, r_t, w_e = self._sst
        else:
            eng_t, w_t, r_t, w_e = {}, {}, {}, {}
            self._sst = (eng_t, w_t, r_t, w_e)
        order = []
        total = sum(len(x) for x in streams)
        while len(order) < total:
            best = None
            cands = []
            for k in range(n):
                if pos[k] >= len(streams[k]):
                    continue
                o = streams[k][pos[k]]
                ok = True
                for j in range(k):
                    if pos[j] >= len(streams[j]):
                        continue
                    for b in o[4]:
                        kk_ = key(b)
                        if rem_r[j].get(kk_, 0) or rem_w[j].get(kk_, 0):
                            ok = False
                            break
                    if ok:
                        for b in o[3]:
                            if rem_w[j].get(key(b), 0):
                                ok = False
                                break
                    if not ok:
                        break
                if not ok:
                    continue
                eng = o[1]
                t = eng_t.get(eng, 0.0)
                for b in o[3]:
                    kk_ = key(b)
                    tw = w_t.get(kk_, 0.0) + (sync_ns if w_e.get(kk_) != eng else 0.0)
                    if tw > t:
                        t = tw
                for b in o[4]:
                    kk_ = key(b)
                    tw = max(w_t.get(kk_, 0.0), r_t.get(kk_, 0.0)) + sync_ns
                    if tw > t:
                        t = tw
                tk = t + (0.0 if eng == "pe" else _NPE_BIAS)
                if best is None or tk < best[3] - 1e-9:
                    best = (t, k, o, tk)
                cands.append((t, k, o, tk))
            if slack_ns > 0:
                for c_ in cands:
                    if c_[0] <= best[0] + slack_ns:
                        best = c_
                        break
            t, k, o = best[0], best[1], best[2]
            dur = cost_fn(o)
            eng = o[1]
            end = t + dur
            eng_t[eng] = end if o[0] == "op" else t + 60.0
            for b in o[3]:
                kk_ = key(b)
                r_t[kk_] = max(r_t.get(kk_, 0.0), end)
                rem_r[k][kk_] -= 1
            for b in o[4]:
                kk_ = key(b)
                w_t[kk_] = end
                w_e[kk_] = eng
                rem_w[k][kk_] -= 1
            pos[k] += 1
            order.append(o)
        for o in order:
            if o[0] == "op":
                self.op(o[1], o[2], reads=o[3], writes=o[4], inc=o[5])
            else:
                out_ap, in_ap, sem_buf = o[2]
                self.dma(out_ap, in_ap, reads=o[3], writes=o[4], sem_buf=sem_buf, q=o[1])
        return max(eng_t.values()) if eng_t else 0.0


from concourse.bass_utils import run_bass_kernel_spmd

D = 1024
NMETA = 16
DFF = 2816
NFC = DFF // 128

W_SPECS = {
    "meta_tokens": (16, 1024), "norm_mix": (2, 1024), "norm_ffn": (2, 1024), "norm_final": (1, 1024),
    "e_w_in": (1024, 3848), "e_w_out": (1024, 1024), "m_b_i": (1, 4), "m_b_f": (1, 4), "m_norm": (1, 512),
    "r_mu": (1, 1792), "r_w0": (1, 512), "r_w2": (64, 512), "r_a0": (1, 512), "r_a2": (64, 512),
    "r_g2": (128, 512), "r_k_k": (1, 512), "r_k_a": (1, 512), "r_r_k": (1, 512), "r_ln_w": (1, 512),
    "r_ln_b": (1, 512), "o_w_in_p": (1024, 6144), "o_w_out": (2048, 1024), "f_w_up": (2048, 5632),
    "f_conv_w": (6, 2816), "f_conv_b": (2, 2816), "f_w_down": (5632, 1024),
}


def host_consts():
    c = {}
    c["c_ident"] = np.eye(128, dtype=np.float32)
    i = np.arange(128)
    c["c_ue"] = (i[:, None] <= i[None, :]).astype(np.float32)
    c["c_su"] = (i[:, None] < i[None, :]).astype(np.float32)
    c["c_iota"] = np.broadcast_to(np.arange(128, dtype=np.float32)[None, :], (128, 128)).copy()
    c["c_pidx"] = np.arange(128, dtype=np.float32)[:, None].copy()
    bo = np.zeros((128, 128), np.float32)
    bo[:64, :64] = 1.0
    bo[64:, 64:] = 1.0
    c["c_blk"] = bo
    c["c_inv"] = (np.float32(1.0) / np.power(np.float32(10000.0), np.linspace(0.0, 1.0, 128, dtype=np.float32))
                  ).astype(np.float32)[:, None].copy()
    return c


class Ctx:
    pass


def tile_rows(NT):
    tiles = [(0, NMETA)]
    for i in range(NT):
        tiles.append((NMETA + 128 * i, 128))
    return tiles


def build(NT, phases=(1, 2, 3, 4), debug=False, final=True):
    nc = bass.Bass("TRN2", target_bir_lowering=False)
    SEQ = 128 * NT
    L = NMETA + SEQ
    dr = {}
    dr["x"] = nc.dram_tensor("x", [SEQ, D], F32, kind="ExternalInput")
    for k, shp in W_SPECS.items():
        dr[k] = nc.dram_tensor(k, list(shp), F32, kind="ExternalInput")
    for k, v in host_consts().items():
        dr[k] = nc.dram_tensor(k, list(v.shape), F32, kind="ExternalInput")
    out = nc.dram_tensor("out", [SEQ, D], F32, kind="ExternalOutput")
    H = {}
    for i in (1, 2, 3):
        H[i] = nc.dram_tensor(f"H{i}", [L, D], F32, kind=("ExternalOutput" if debug else "Internal"))

    tiles = tile_rows(NT)
    es = contextlib.ExitStack()
    with es:
        kb = KB(nc, es)
        PS = [kb.ps([128, 512], F32, f"psb{i}") for i in range(7)]
        PST = kb.ps([128, 1024], BF16, "pstr")
        g = Ctx()
        g.nc, g.kb, g.dr, g.H, g.out, g.tiles, g.PS, g.PST = nc, kb, dr, H, out, tiles, PS, PST
        g.psi = 0
        g.ident_f = kb.sb([128, 128], F32, "ident_f")
        g.ident_b = kb.sb([128, 128], BF16, "ident_b")
        kb.dma(g.ident_f[:, :], dr["c_ident"].ap()[:, :], writes=[g.ident_f], sem_buf=g.ident_f)
        kb.op("dve", lambda e: e.tensor_copy(out=g.ident_b[:, :], in_=g.ident_f[:, :]),
              reads=[g.ident_f], writes=[g.ident_b])

        plist = [p for p in (1, 2, 3, 4) if p in phases]
        src = 0
        for p in plist:
            dst = p if p != plist[-1] else 4
            with contextlib.ExitStack() as pes:
                kb.es = pes
                if p in (2, 4):
                    phase_ffn(g, layer=(0 if p == 2 else 1), src=src, dst=dst, final=final)
                elif p == 1:
                    phase_l0(g, src=src, dst=dst, final=final)
                elif p == 3:
                    phase_l1(g, src=src, dst=dst, final=final)
                kb.barrier()
            kb.es = es
            src = dst
        kb.final_wait()
    return nc


def next_ps(g):
    b = g.PS[g.psi % len(g.PS)]
    g.psi += 1
    return b


def bc_rows(handle, row, n, parts=128, col0=0, ncols_total=None):
    ncols_total = ncols_total if ncols_total is not None else handle.shape[1]
    return bass.AP(handle, row * ncols_total + col0, [[0, parts], [1, n]])


def load_h(g, src, ti, HT):
    kb = g.kb
    r0, T = g.tiles[ti]
    if src == 0:
        if ti == 0:
            ap = g.dr["meta_tokens"].ap()[0:NMETA, :]
        else:
            ap = g.dr["x"].ap()[r0 - NMETA:r0 - NMETA + T, :]
    else:
        ap = g.H[src].ap()[r0:r0 + T, :]
    kb.dma(HT[:T, :], ap, writes=[HT], sem_buf=HT)


def store_h(g, dst, ti, HO, final, Gfin=None, scratch=None):
    kb = g.kb
    r0, T = g.tiles[ti]
    if dst != 4:
        kb.dma(g.H[dst].ap()[r0:r0 + T, :], HO[:T, :], reads=[HO], sem_buf=HO)
        return
    if ti == 0:
        return
    if final:
        ss, rstd, junk = scratch
        kb.op("act", lambda e: e.activation(out=junk[:T, 0:D], in_=HO[:T, :], func=AF.Square, accum_out=ss[:T, :]),
              reads=[HO], writes=[junk, ss])
        rstd_from_ss(kb, ss, rstd, T, 1.0 / D, 1e-6)
        kb.op("dve", lambda e: e.scalar_tensor_tensor(out=HO[:T, :], in0=HO[:T, :], scalar=rstd[:T, :],
                                                      in1=Gfin[:T, :], op0=ALU.mult, op1=ALU.mult),
              reads=[HO, rstd, Gfin], writes=[HO])
    kb.dma(g.out.ap()[r0 - NMETA:r0 - NMETA + T, :], HO[:T, :], reads=[HO], sem_buf=HO)


import os as _os
DMAQ_N = int(_os.environ.get("DMAQ_N", "1"))


def load_weight_bf16(g, dram_handle, row0, K, N, W, stg, col0=0, ncols_total=None):
    kb = g.kb
    SW = stg[0].t.shape[1]
    engs = ("dve", "act", "dve", "act", "dve", "act", "dve")
    cnt = getattr(g, "_lw_cnt", 0)
    for kc in range(K // 128):
        for j0 in range(0, N, SW):
            w = min(SW, N - j0)
            s = stg[cnt % len(stg)]
            kb.dma(s[:, :w], dram_handle.ap()[row0 + kc * 128: row0 + (kc + 1) * 128, col0 + j0: col0 + j0 + w],
                   writes=[s], sem_buf=s, q=(("sp", "pool", "act")[cnt % DMAQ_N] if DMAQ_N > 1 else "sp"))
            en = engs[cnt % len(engs)]
            if en == "act":
                kb.op("act", lambda e: e.copy(out=W[:, kc, j0:j0 + w], in_=s[:, :w]), reads=[s], writes=[W])
            else:
                kb.op(en, lambda e: e.tensor_copy(out=W[:, kc, j0:j0 + w], in_=s[:, :w]), reads=[s], writes=[W])
            cnt += 1
    g._lw_cnt = cnt


def rstd_from_ss(kb, ss, rstd, T, scale, eps, ap_fn=None):
    a = (lambda b: b[:T, :]) if ap_fn is None else ap_fn
    kb.op("act", lambda e: e.activation(out=a(rstd), in_=a(ss), func=AF.Sqrt, scale=scale, bias=eps),
          reads=[ss], writes=[rstd])
    kb.op("dve", lambda e: e.reciprocal(out=a(rstd), in_=a(rstd)), reads=[rstd], writes=[rstd])


def load_vec_fm(g, handle, row, nch, dstbuf, dst_ap, vtmp, col0=0):
    kb = g.kb
    ncols = handle.shape[1]
    src = bass.AP(handle, row * ncols + col0, [[128, nch], [1, 128]])
    kb.dma(vtmp[:nch, :], src, writes=[vtmp], sem_buf=vtmp)
    pt = next_ps(g)
    kb.op("pe", lambda e: e.transpose(out=pt[:, :nch], in_=vtmp[:nch, :], identity=g.ident_f[:nch, :nch]),
          reads=[vtmp, g.ident_f], writes=[pt])
    kb.op("dve", lambda e: e.tensor_copy(out=dst_ap, in_=pt[:, :nch]), reads=[pt], writes=[dstbuf])


def rmsnorm_T(g, HT, T, Gb, hn, hnT, ss, rstd, junk):
    kb = g.kb
    kb.op("act", lambda e: e.activation(out=junk[:T, 0:D], in_=HT[:T, :], func=AF.Square, accum_out=ss[:T, :]),
          reads=[HT], writes=[junk, ss])
    rstd_from_ss(kb, ss, rstd, T, 1.0 / D, 1e-6)
    kb.op("dve", lambda e: e.scalar_tensor_tensor(out=hn[:T, :], in0=HT[:T, :], scalar=rstd[:T, :],
                                                  in1=Gb[:T, :], op0=ALU.mult, op1=ALU.mult),
          reads=[HT, rstd, Gb], writes=[hn])
    PST = g.PST
    for kc in range(8):
        kb.op("pe", lambda e: e.transpose(out=PST[:, kc * T:(kc + 1) * T], in_=hn[:T, kc * 128:(kc + 1) * 128],
                                          identity=g.ident_b[:T, :T]),
              reads=[hn, g.ident_b], writes=[PST], inc=(kc == 7))
    kb.op("act", lambda e: e.copy(out=hnT[:, :, :T], in_=PST[:, 0:8 * T].rearrange("p (k t) -> p k t", k=8)),
          reads=[PST], writes=[hnT])


def norm_stats(g, HT, T, Gb, hn, ss, rstd, junk):
    kb = g.kb
    kb.op("act", lambda e: e.activation(out=junk[:T, 0:D], in_=HT[:T, :], func=AF.Square, accum_out=ss[:T, :]),
          reads=[HT], writes=[junk, ss])
    rstd_from_ss(kb, ss, rstd, T, 1.0 / D, 1e-6)
    kb.op("dve", lambda e: e.scalar_tensor_tensor(out=hn[:T, :], in0=HT[:T, :], scalar=rstd[:T, :],
                                                  in1=Gb[:T, :], op0=ALU.mult, op1=ALU.mult),
          reads=[HT, rstd, Gb], writes=[hn])


def norm_transpose(g, hn, hnT, T):
    kb = g.kb
    PST = g.PST
    for kc in range(8):
        kb.op("pe", lambda e: e.transpose(out=PST[:, kc * T:(kc + 1) * T], in_=hn[:T, kc * 128:(kc + 1) * 128],
                                          identity=g.ident_b[:T, :T]),
              reads=[hn, g.ident_b], writes=[PST], inc=(kc == 7))
    kb.op("act", lambda e: e.copy(out=hnT[:, :, :T], in_=PST[:, 0:8 * T].rearrange("p (k t) -> p k t", k=8)),
          reads=[PST], writes=[hnT])


import os


def phase_ffn(g, layer, src, dst, final):
    kb, nc, dr = g.kb, g.nc, g.dr
    Wup = kb.sb([128, 8, 2 * DFF], BF16, "Wup")
    Wdn = kb.sb([128, NFC, D], BF16, "Wdn")
    with contextlib.ExitStack() as ses:
        old = kb.es
        kb.es = ses
        stg = [kb.sb([128, 1408], F32, f"stg{i}") for i in range(3)]
        load_weight_bf16(g, dr["f_w_up"], layer * D, D, 2 * DFF, Wup, stg)
        load_weight_bf16(g, dr["f_w_down"], layer * DFF, DFF, D, Wdn, stg)
        kb.barrier()
        kb.es = old
    Gb = kb.sb([128, D], F32, "Gb")
    kb.dma(Gb[:, :], bc_rows(dr["norm_ffn"], layer, D), writes=[Gb], sem_buf=Gb)
    Gfin = None
    if dst == 4 and final:
        Gfin = kb.sb([128, D], F32, "Gfin")
        kb.dma(Gfin[:, :], bc_rows(dr["norm_final"], 0, D), writes=[Gfin], sem_buf=Gfin)
    CW = kb.sb([128, 3, NFC], F32, "CW")
    CB = kb.sb([128, NFC], F32, "CB")
    vtmp = kb.sb([32, 128], F32, "vtmp")
    for j in range(3):
        load_vec_fm(g, dr["f_conv_w"], layer * 3 + j, NFC, CW, CW[:, j, :], vtmp)
    load_vec_fm(g, dr["f_conv_b"], layer, NFC, CB, CB[:, :], vtmp)
    HTs = [kb.sb([128, D], F32, f"HT{i}") for i in range(3)]
    hns = [kb.sb([128, D], BF16, f"hn{i}") for i in range(2)]
    hnTs = [kb.sb([128, 8, 128], BF16, f"hnT{i}") for i in range(2)]
    junk = kb.sb([128, D], BF16, "junk")
    sss = [kb.sb([128, 1], F32, f"ss{i}") for i in range(3)]
    rstds = [kb.sb([128, 1], F32, f"rstd{i}") for i in range(3)]
    G = kb.sb([128, NFC, 130], F32, "G")
    ACC = [kb.sb([128, 4, 128], F32, f"acc{i}") for i in range(2)]
    SIL = [kb.sb([128, 4, 128], F32, f"sil{i}") for i in range(2)]
    ACTT = kb.sb([128, NFC, 128], BF16, "ACTT")
    kb.op("dve", lambda e: e.memset(G[:, :, :], 0.0), writes=[G])
    po_banks = [g.PS[5], g.PS[6]]
    rot = g.PS[0:5]
    rot_i = [0]

    def next_rot():
        b = rot[rot_i[0] % len(rot)]
        rot_i[0] += 1
        return b

    NS_AT = int(os.environ.get('NS_AT', '1'))
    ntl = len(g.tiles)
    load_h(g, src, 0, HTs[0])
    norm_stats(g, HTs[0], g.tiles[0][1], Gb, hns[0], sss[0], rstds[0], junk)
    if ntl > 1:
        load_h(g, src, 1, HTs[1])
    if ntl > 2:
        load_h(g, src, 2, HTs[2])
    norm_transpose(g, hns[0], hnTs[0], g.tiles[0][1])

    def down_part(c_lo, c_hi, T):
        for c in range(c_lo, c_hi):
            for nb in range(2):
                po = po_banks[nb]
                kb.op("pe", lambda e: e.matmul(po[:T, :], ACTT[:, c, :T], Wdn[:, c, nb * 512:(nb + 1) * 512],
                                               start=(c == 0), stop=(c == NFC - 1)),
                      reads=[ACTT, Wdn], writes=[po], inc=(c == c_hi - 1))

    steps = list(range(0, NFC, 4))
    DEFER = bool(int(os.environ.get("FFN_DEFER", "0")))

    def tile_tail(tj):
        r0j, Tj = g.tiles[tj]
        HTj = HTs[tj % 3]
        down_part(steps[-1], NFC, Tj)
        for nb in range(2):
            kb.op("dve", lambda e: e.tensor_tensor(out=HTj[:Tj, nb * 512:(nb + 1) * 512],
                                                   in0=HTj[:Tj, nb * 512:(nb + 1) * 512], in1=po_banks[nb][:Tj, :], op=ALU.add),
                  reads=[HTj, po_banks[nb]], writes=[HTj])
        store_h(g, dst, tj, HTj, final, Gfin, (sss[(tj + 2) % 3], rstds[(tj + 2) % 3], junk))
        if tj + 3 < ntl:
            load_h(g, src, tj + 3, HTs[tj % 3])

    pending = None
    for ti, (r0, T) in enumerate(g.tiles):
        HT = HTs[ti % 3]
        hnT = hnTs[ti % 2]
        for si, c0 in enumerate(steps):
            nch = min(4, NFC - c0)
            pg = next_rot()
            pv = next_rot()
            for j in range(nch):
                for kc in range(8):
                    kb.op("pe", lambda e: e.matmul(pg[:, j * T:(j + 1) * T],
                                                   Wup[:, kc, DFF + (c0 + j) * 128: DFF + (c0 + j + 1) * 128],
                                                   hnT[:, kc, :T], start=(kc == 0), stop=(kc == 7)),
                          reads=[Wup, hnT], writes=[pg], inc=(kc == 7))
            for j in range(nch):
                for kc in range(8):
                    kb.op("pe", lambda e: e.matmul(pv[:, j * T:(j + 1) * T],
                                                   Wup[:, kc, (c0 + j) * 128:(c0 + j + 1) * 128],
                                                   hnT[:, kc, :T], start=(kc == 0), stop=(kc == 7)),
                          reads=[Wup, hnT], writes=[pv], inc=(kc == 7))
            if si == 0 and pending is not None:
                tile_tail(pending)
                pending = None
            if si >= 1:
                down_part(steps[si - 1], c0, T)
            kb.op("act", lambda e: e.copy(out=G[:, c0:c0 + nch, 2:2 + T],
                                          in_=pg[:, 0:nch * T].rearrange("p (c t) -> p c t", c=nch)),
                  reads=[pg], writes=[G])
            acc = ACC[si % 2]
            sil = SIL[si % 2]
            for j in range(nch):
                c = c0 + j
                kb.op("dve", lambda e: e.tensor_scalar(out=acc[:, j, :T], in0=G[:, c, 2:2 + T],
                                                       scalar1=CW[:, 2, c:c + 1], scalar2=CB[:, c:c + 1],
                                                       op0=ALU.mult, op1=ALU.add),
                      reads=[G, CW, CB], writes=[acc])
                kb.op("dve", lambda e: e.scalar_tensor_tensor(out=acc[:, j, :T], in0=G[:, c, 1:1 + T],
                                                              scalar=CW[:, 1, c:c + 1], in1=acc[:, j, :T],
                                                              op0=ALU.mult, op1=ALU.add),
                      reads=[G, CW, acc], writes=[acc])
                kb.op("dve", lambda e: e.scalar_tensor_tensor(out=acc[:, j, :T], in0=G[:, c, 0:T],
                                                              scalar=CW[:, 0, c:c + 1], in1=acc[:, j, :T],
                                                              op0=ALU.mult, op1=ALU.add),
                      reads=[G, CW, acc], writes=[acc])
            kb.op("act", lambda e: e.activation(out=sil[:, 0:nch, :T], in_=acc[:, 0:nch, :T], func=AF.Silu),
                  reads=[acc], writes=[sil])
            kb.op("dve", lambda e: e.tensor_tensor(out=ACTT[:, c0:c0 + nch, :T], in0=sil[:, 0:nch, :T],
                                                   in1=pv[:, 0:nch * T].rearrange("p (c t) -> p c t", c=nch),
                                                   op=ALU.mult),
                  reads=[sil, pv], writes=[ACTT])
            if si == NS_AT and ti + 1 < ntl:
                Tn = g.tiles[ti + 1][1]
                norm_stats(g, HTs[(ti + 1) % 3], Tn, Gb, hns[(ti + 1) % 2], sss[(ti + 1) % 3], rstds[(ti + 1) % 3], junk)
        if ti + 1 < ntl:
            norm_transpose(g, hns[(ti + 1) % 2], hnTs[(ti + 1) % 2], g.tiles[ti + 1][1])
        kb.op("dve", lambda e: e.tensor_copy(out=G[:, :, 0:2], in_=G[:, :, T:T + 2]), reads=[G], writes=[G])
        if DEFER and ti + 1 < ntl:
            pending = ti
        else:
            tile_tail(ti)


EH = 0.6065306597126334
ISQ = 0.08838834764831845
NEGBIG = -30000.0


def phase_l0(g, src, dst, final):
    import os
    CUT = int(os.environ.get('CUT', '99'))
    SUB = int(os.environ.get('SUB', '99'))
    HFN = int(os.environ.get('HFN', '2'))
    kb, nc, dr = g.kb, g.nc, g.dr
    Win = kb.sb([128, 8, 3848], BF16, "Win")
    Wout = kb.sb([128, 8, D], BF16, "Wout")
    W2A = kb.sb([128, 512], BF16, "W2A")
    G2 = kb.sb([128, 512], BF16, "G2")
    with contextlib.ExitStack() as ses:
        old = kb.es
        kb.es = ses
        stg = [kb.sb([128, 1924], F32, f"stg{i}") for i in range(3)]
        load_weight_bf16(g, dr["e_w_in"], 0, D, 3848, Win, stg)
        load_weight_bf16(g, dr["e_w_out"], 0, D, D, Wout, stg)
        s0 = stg[0]
        kb.dma(s0[0:64, 0:512], dr["r_w2"].ap()[:, :], writes=[s0], sem_buf=s0)
        kb.dma(s0[64:128, 0:512], dr["r_a2"].ap()[:, :], writes=[s0], sem_buf=s0)
        kb.op("dve", lambda e: e.tensor_copy(out=W2A[:, :], in_=s0[:, 0:512]), reads=[s0], writes=[W2A])
        s1 = stg[1]
        kb.dma(s1[:, 0:512], dr["r_g2"].ap()[:, :], writes=[s1], sem_buf=s1)
        kb.op("dve", lambda e: e.tensor_copy(out=G2[:, :], in_=s1[:, 0:512]), reads=[s1], writes=[G2])
        kb.barrier()
        kb.es = old
    F = lambda shape, name: kb.sb(shape, F32, name)
    Bf = lambda shape, name: kb.sb(shape, BF16, name)
    Gb = Bf([128, D], "Gb")
    ue = F([128, 128], "ue")
    su = F([128, 128], "su")
    blk = F([128, 128], "blk")
    kb.dma(ue[:, :], dr["c_ue"].ap()[:, :], writes=[ue], sem_buf=ue)
    kb.dma(su[:, :], dr["c_su"].ap()[:, :], writes=[su], sem_buf=su)
    kb.dma(blk[:, :], dr["c_blk"].ap()[:, :], writes=[blk], sem_buf=blk)
    sl = F([128, 128], "sl")
    kb.op("dve", lambda e: e.tensor_scalar(out=sl[:, :], in0=ue[:, :], scalar1=-1.0, scalar2=1.0, op0=ALU.mult, op1=ALU.add),
          reads=[ue], writes=[sl])
    neg = F([128, 128], "neg")
    kb.op("dve", lambda e: e.tensor_scalar(out=neg[:, :], in0=sl[:, :], scalar1=NEGBIG, scalar2=None, op0=ALU.mult),
          reads=[sl], writes=[neg])
    nblk = F([128, 2], "nblk")
    kb.op("dve", lambda e: e.tensor_scalar(out=nblk[:, 0:1], in0=blk[:, 0:1], scalar1=-1.0, scalar2=None, op0=ALU.mult),
          reads=[blk], writes=[nblk])
    kb.op("dve", lambda e: e.tensor_scalar(out=nblk[:, 1:2], in0=blk[:, 127:128], scalar1=-1.0, scalar2=None, op0=ALU.mult),
          reads=[blk], writes=[nblk])
    blk64 = F([128, 128], "blk64")
    kb.op("dve", lambda e: e.tensor_scalar(out=blk64[:, :], in0=blk[:, :], scalar1=1.0 / 64.0, scalar2=None, op0=ALU.mult),
          reads=[blk], writes=[blk64])
    onesf = F([128, 128], "onesf")
    kb.op("dve", lambda e: e.memset(onesf[:, :], 1.0 / 128.0), writes=[onesf])
    onesb = Bf([128, 128], "onesb")
    kb.op("dve", lambda e: e.memset(onesb[:, :], 1.0), writes=[onesb])
    ones1 = F([128, 128], "ones1")
    kb.op("dve", lambda e: e.memset(ones1[:, :], 1.0), writes=[ones1])
    vtmp = F([32, 128], "vtmp")
    MU = F([128, 14], "MU"); W0 = F([128, 4], "W0"); A0 = F([128, 4], "A0"); KK = F([128, 4], "KK")
    KA = F([128, 4], "KA"); RRK = F([128, 4], "RRK"); LNW = F([128, 4], "LNW"); LNB = F([128, 4], "LNB")
    MN = F([128, 4], "MN")
    load_vec_fm(g, dr["r_mu"], 0, 14, MU, MU[:, :], vtmp)
    for nm, buf in (("r_w0", W0), ("r_a0", A0), ("r_k_k", KK), ("r_k_a", KA), ("r_r_k", RRK), ("r_ln_w", LNW),
                    ("r_ln_b", LNB), ("m_norm", MN)):
        load_vec_fm(g, dr[nm], 0, 4, buf, buf[:, :], vtmp)
    BG = F([128, 8], "BG")
    kb.dma(BG[:, 0:4], bc_rows(dr["m_b_i"], 0, 4), writes=[BG], sem_buf=BG)
    kb.dma(BG[:, 4:8], bc_rows(dr["m_b_f"], 0, 4), writes=[BG], sem_buf=BG)
    C = F([128, 4, 129], "C")
    Cb = Bf([128, 4, 128], "Cb")
    nbc = Bf([128, 4, 128], "nbc")
    ST = F([128, 4, 64], "ST")
    STb = Bf([128, 4, 64], "STb")
    ZR = F([128, 14, 129], "ZR")
    for b_ in (C, ST, ZR):
        kb.op("dve", lambda e: e.memset(b_[:, :, :], 0.0), writes=[b_])
    for b_ in (Cb, nbc, STb):
        kb.op("dve", lambda e: e.memset(b_[:, :, :], 0.0), writes=[b_])
    vTM1 = Bf([128, 4, 129], "vTM1")
    kb.op("dve", lambda e: e.memset(vTM1[:, :, :], 1.0), writes=[vTM1])
    HTs = [F([128, D], f"HT{i}") for i in range(3)]
    hn = Bf([128, D], "hn"); hnT = Bf([128, 8, 128], "hnT"); junk = hn
    ss = F([128, 1], "ss"); rstd = F([128, 1], "rstd")
    qTb = Bf([128, 4, 128], "qTb"); kTb = Bf([128, 4, 128], "kTb"); moT = Bf([128, 4, 128], "moT"); kpbuf = F([128, 4, 128], "kpbuf")
    gx = F([128, 8], "gx"); th = F([128, 8], "th"); ex = F([128, 4], "ex"); LI = F([128, 4], "LI"); LF = F([128, 4], "LF")
    lmb = F([128, 4], "lmb"); LFb = F([128, 4, 128], "LFb"); arg = F([128, 4, 128], "arg"); ET = Bf([128, 4, 128], "ET"); aabuf = Bf([128, 4, 128], "aabuf")
    eB = F([128, 4, 128], "eB"); gcol = F([128, 4], "gcol"); ew = F([128, 4], "ew"); qs = Bf([128, 4, 128], "qs")
    sT = Bf([128, 4, 128], "sT"); kw = Bf([128, 4, 128], "kw")
    cden = F([128, 4, 128], "cden"); hT = F([128, 4, 128], "hT"); hsq = F([128, 4, 128], "hsq"); rs4 = F([128, 4, 128], "rs4")
    mixTs = [Bf([128, 8, 128], f"mixT{i}") for i in range(2)]
    kTMf = Bf([128, 512], "kTMf")
    P1 = F([128, 512], "P1")
    GGs = [Bf([128, 4, 128], f"GG{i}") for i in range(2)]
    for hh_ in range(2):
        kb.dma(P1[:, :], bc_rows(dr["norm_mix"], 0, 512, col0=512 * hh_), writes=[P1], sem_buf=P1)
        kb.op("dve", lambda e: e.tensor_copy(out=Gb[:, 512 * hh_:512 * (hh_ + 1)], in_=P1[:, :]), reads=[P1], writes=[Gb])
    Z2 = Bf([128, 14, 128], "Z2"); D1 = Z2
    LIN = Bf([128, 128], "LIN"); sxg = Bf([128, 128], "sxg")
    sw = arg; aa = aabuf
    kkr = cden; tq = hsq; rn = rs4; kp = kpbuf
    CS = hT; CSp = F([128, 4, 128], "CSp"); csl = F([128, 4], "csl")
    eW = F([128, 4, 128], "eW"); eWp = eB; eWi = F([128, 4, 128], "eWi"); eWT = F([128, 4, 128], "eWT")
    kka = F([128, 4, 128], "kka")
    AR = Bf([128, 4, 2, 128], "AR")
    BH = Bf([128, 4, 128], "BH"); KH = Bf([128, 4, 128], "KH"); vb = Bf([128, 4, 128], "vb")
    bonuss = [Bf([128, 4, 128], f"bonus{i}") for i in range(2)]
    BTm = [Bf([128, 4, 128], f"BTm{i}") for i in range(2)]
    KTm = [Bf([128, 4, 128], f"KTm{i}") for i in range(2)]
    ATm = [Bf([128, 4, 128], f"ATm{i}") for i in range(2)]
    STbd = Bf([128, 4, 128], "STbd")
    kb.op("dve", lambda e: e.memset(STbd[:, :, :], 0.0), writes=[STbd])
    VTM = Bf([128, 8, 64], "VTM"); BHT = Bf([128, 8, 64], "BHT"); KHT = Bf([128, 8, 64], "KHT"); UTM = Bf([128, 8, 64], "UTM")
    Xa = [F([128, 8, 128], "Xa0"), F([128, 8, 128], "Xa1")]
    XTa = [F([128, 8, 128], "XTa0"), F([128, 8, 128], "XTa1")]
    Pm = F([128, 8, 128], "Pm")
    ARB = Bf([128, 8, 128], "ARB"); AAK = Bf([128, 8, 128], "AAK"); ARK = Bf([128, 8, 128], "ARK")

    class SubBuf:
        def __init__(self, parent, lo):
            self.parent, self.lo = parent, lo
            self.w, self.r, self.excl, self.name = parent.w, parent.r, False, parent.name

        def __getitem__(self, idx):
            p, c, t = idx
            if isinstance(c, slice):
                c = slice((c.start or 0) + self.lo, (c.stop if c.stop is not None else 4) + self.lo)
            else:
                c = c + self.lo
            return self.parent.t[p, c, t]

    Of = SubBuf(Xa[1], 0); Osq = SubBuf(Xa[1], 4); mean_s = SubBuf(XTa[1], 0); var = SubBuf(XTa[1], 4)

    def b3(buf, T, n=4):
        return buf[:, 0:n].unsqueeze(2).to_broadcast([128, n, T])

    def mk_alloc(banks):
        st = [0]

        def alloc():
            b = banks[st[0] % len(banks)]
            st[0] += 1
            return b
        return alloc

    _bk = [int(c) for c in os.environ.get("L0_BANKS", "2113")]
    _o = [0, _bk[0], _bk[0] + _bk[1], _bk[0] + _bk[1] + _bk[2], 7]
    nP = mk_alloc(g.PS[_o[0]:_o[1]])
    nM = mk_alloc(g.PS[_o[1]:_o[2]])
    nR = mk_alloc(g.PS[_o[2]:_o[3]])
    nS = mk_alloc(g.PS[_o[3]:_o[4]])

    graw = F([128, 8], "graw")

    def gen_proj(ti):
        r0, T = g.tiles[ti]
        HT = HTs[ti % 3]
        mixT = mixTs[ti % 2]
        GG = GGs[ti % 2]
        bonus = bonuss[ti % 2]
        def proj_fm(pbank, j, col):
            for kc in range(8):
                kb.op("pe", lambda e: e.matmul(pbank[:, j * T:(j + 1) * T], Win[:, kc, col:col + 128], hnT[:, kc, :T],
                                               start=(kc == 0), stop=(kc == 7)),
                      reads=[Win, hnT], writes=[pbank], inc=(kc == 7))

        def proj_tm(pbank, col, n, c0=0):
            for kc in range(8):
                kb.op("pe", lambda e: e.matmul(pbank[:T, c0:c0 + n], hnT[:, kc, :T], Win[:, kc, col:col + n],
                                               start=(kc == 0), stop=(kc == 7)),
                      reads=[hnT, Win], writes=[pbank], inc=(kc == 7))

        def v3(pbank, n=4):
            return pbank[:, 0:n * T].rearrange("p (c t) -> p c t", c=n)

        pq = nP()
        for h in range(4):
            proj_fm(pq, h, h * 128)
        kb.op("act", lambda e: e.copy(out=qTb[:, :, :T], in_=v3(pq)), reads=[pq], writes=[qTb])
        pk = nP()
        for h in range(4):
            proj_fm(pk, h, 512 + h * 128)
        kb.op("act", lambda e: e.copy(out=kTb[:, :, :T], in_=v3(pk)), reads=[pk], writes=[kTb])
        pmo = nP()
        for h in range(4):
            proj_fm(pmo, h, 1536 + h * 128)
        kb.op("act", lambda e: e.activation(out=moT[:, :, :T], in_=v3(pmo), func=AF.Sigmoid), reads=[pmo], writes=[moT])
        pkt = nP()
        proj_tm(pkt, 512, 512)
        kb.op("act", lambda e: e.copy(out=kTMf[:T, :], in_=pkt[:T, :]), reads=[pkt], writes=[kTMf])
        pvt = nP()
        proj_tm(pvt, 1024, 512)
        kb.op("act", lambda e: e.copy(out=vTM1[:T, :, 0:128], in_=pvt[:T, :].rearrange("p (h v) -> p h v", h=4)),
              reads=[pvt], writes=[vTM1])
        pgt = nP()
        proj_tm(pgt, 2048, 8)
        kb.op("act", lambda e: e.copy(out=graw[:T, :], in_=pgt[:T, 0:8]), reads=[pgt], writes=[graw])
        zc = 2056
        for b0, n in ((0, 4), (4, 4), (8, 4), (12, 2)):
            yield
            pz = nP()
            for j in range(n):
                proj_fm(pz, j, zc + (b0 + j) * 128)
            kb.op("act", lambda e: e.copy(out=ZR[:, b0:b0 + n, 1:T + 1], in_=v3(pz, n)), reads=[pz], writes=[ZR])

    def gen_mlstm(ti):
        r0, T = g.tiles[ti]
        HT = HTs[ti % 3]
        mixT = mixTs[ti % 2]
        GG = GGs[ti % 2]
        bonus = bonuss[ti % 2]
        def proj_fm(pbank, j, col):
            for kc in range(8):
                kb.op("pe", lambda e: e.matmul(pbank[:, j * T:(j + 1) * T], Win[:, kc, col:col + 128], hnT[:, kc, :T],
                                               start=(kc == 0), stop=(kc == 7)),
                      reads=[Win, hnT], writes=[pbank], inc=(kc == 7))

        def proj_tm(pbank, col, n, c0=0):
            for kc in range(8):
                kb.op("pe", lambda e: e.matmul(pbank[:T, c0:c0 + n], hnT[:, kc, :T], Win[:, kc, col:col + n],
                                               start=(kc == 0), stop=(kc == 7)),
                      reads=[hnT, Win], writes=[pbank], inc=(kc == 7))

        def v3(pbank, n=4):
            return pbank[:, 0:n * T].rearrange("p (c t) -> p c t", c=n)

        kb.op("dve", lambda e: e.tensor_tensor(out=gx[:T, :], in0=graw[:T, :], in1=BG[:T, :], op=ALU.add),
              reads=[graw, BG], writes=[gx])
        kb.op("act", lambda e: e.activation(out=th[:T, :], in_=gx[:T, :], func=AF.Tanh, scale=1.0 / 15.0), reads=[gx], writes=[th])
        kb.op("dve", lambda e: e.tensor_scalar(out=LI[:T, :], in0=th[:T, 0:4], scalar1=15.0, scalar2=None, op0=ALU.mult),
              reads=[th], writes=[LI])
        kb.op("act", lambda e: e.activation(out=ex[:T, :], in_=th[:T, 4:8], func=AF.Exp, scale=-15.0), reads=[th], writes=[ex])
        kb.op("act", lambda e: e.activation(out=ex[:T, :], in_=ex[:T, :], func=AF.Ln, bias=1.0), reads=[ex], writes=[ex])
        kb.op("dve", lambda e: e.tensor_scalar(out=LF[:T, :], in0=ex[:T, :], scalar1=-1.0, scalar2=None, op0=ALU.mult),
              reads=[ex], writes=[LF])
        yield
        pbc = nM()
        kb.op("pe", lambda e: e.matmul(pbc[:T, 0:4], ue[:T, :T], LF[:T, :], start=True, stop=True), reads=[ue, LF], writes=[pbc])
        kb.op("dve", lambda e: e.tensor_tensor(out=lmb[:T, :], in0=LI[:T, :], in1=pbc[:T, 0:4], op=ALU.subtract),
              reads=[LI, pbc], writes=[lmb])
        kb.op("dve", lambda e: e.tensor_copy(out=LFb[:T, :, :], in_=LF[:T, 0:4].unsqueeze(2).to_broadcast([T, 4, 128])),
              reads=[LF], writes=[LFb])
        yield
        pB = nM()
        for h in range(4):
            kb.op("pe", lambda e: e.matmul(pB[:, h * T:(h + 1) * T], LFb[:T, h, :], ue[:T, :T], start=True, stop=True),
                  reads=[LFb, ue], writes=[pB], inc=(h == 3))
        kb.op("dve", lambda e: e.tensor_tensor(out=arg[:T, :, :T], in0=v3(pB)[:T], in1=neg[:T, :T].unsqueeze(1).to_broadcast([T, 4, T]),
                                               op=ALU.add), reads=[pB, neg], writes=[arg])
        for h in range(4):
            kb.op("act", lambda e: e.activation(out=ET[:T, h, :T], in_=arg[:T, h, :T], func=AF.Exp, bias=lmb[:T, h:h + 1]),
                  reads=[arg, lmb], writes=[ET])
        kb.op("act", lambda e: e.activation(out=eB[:, :, :T], in_=v3(pB), func=AF.Exp), reads=[pB], writes=[eB])
        kb.op("dve", lambda e: e.tensor_copy(out=gcol[:, :], in_=v3(pB)[:, :, T - 1]), reads=[pB], writes=[gcol])
        for h in range(4):
            kb.op("act", lambda e: e.activation(out=ew[:T, h:h + 1], in_=lmb[:T, h:h + 1], func=AF.Exp, bias=gcol[:T, h:h + 1]),
                  reads=[lmb, gcol], writes=[ew])
        kb.op("dve", lambda e: e.tensor_tensor(out=qs[:, :, :T], in0=qTb[:, :, :T], in1=eB[:, :, :T], op=ALU.mult),
              reads=[qTb, eB], writes=[qs])
        yield
        psc = nM()
        for h in range(4):
            kb.op("pe", lambda e: e.matmul(psc[:T, h * T:(h + 1) * T], kTb[:, h, :T], qTb[:, h, :T], start=True, stop=True),
                  reads=[kTb, qTb], writes=[psc], inc=(h == 3))
        kb.op("dve", lambda e: e.scalar_tensor_tensor(out=sT[:T, :, :T], in0=v3(psc)[:T], scalar=ISQ, in1=ET[:T, :, :T],
                                                      op0=ALU.mult, op1=ALU.mult), reads=[psc, ET], writes=[sT])
        yield
        pden = nM()
        for h in range(4):
            kb.op("pe", lambda e: e.matmul(pden[:, h * T:(h + 1) * T], onesb[:T, :], sT[:T, h, :T], start=True, stop=False),
                  reads=[onesb, sT], writes=[pden], inc=False)
            kb.op("pe", lambda e: e.matmul(pden[:, h * T:(h + 1) * T], nbc[:, h, :], qs[:, h, :T], start=False, stop=True),
                  reads=[nbc, qs], writes=[pden])
        kb.op("act", lambda e: e.activation(out=cden[:, :, :T], in_=v3(pden), func=AF.Abs), reads=[pden], writes=[cden])
        kb.op("dve", lambda e: e.tensor_scalar(out=cden[:, :, :T], in0=cden[:, :, :T], scalar1=1.0, scalar2=None, op0=ALU.max),
              reads=[cden], writes=[cden])
        kb.op("dve", lambda e: e.reciprocal(out=cden[:, :, :T], in_=cden[:, :, :T]), reads=[cden], writes=[cden])
        pnum = nM()
        for h in range(4):
            kb.op("pe", lambda e: e.matmul(pnum[:, h * T:(h + 1) * T], vTM1[:T, h, 0:128], sT[:T, h, :T], start=True, stop=False),
                  reads=[vTM1, sT], writes=[pnum], inc=False)
            kb.op("pe", lambda e: e.matmul(pnum[:, h * T:(h + 1) * T], Cb[:, h, :], qs[:, h, :T], start=False, stop=True),
                  reads=[Cb, qs], writes=[pnum])
        kb.op("dve", lambda e: e.tensor_tensor(out=hT[:, :, :T], in0=v3(pnum), in1=cden[:, :, :T], op=ALU.mult),
              reads=[pnum, cden], writes=[hT])
        kb.op("act", lambda e: e.activation(out=hsq[:, :, :T], in_=hT[:, :, :T], func=AF.Square), reads=[hT], writes=[hsq])
        yield
        pss = nM()
        for h in range(4):
            kb.op("pe", lambda e: e.matmul(pss[:, h * T:(h + 1) * T], onesf[:, :], hsq[:, h, :T], start=True, stop=True),
                  reads=[onesf, hsq], writes=[pss], inc=(h == 3))
        kb.op("act", lambda e: e.activation(out=rs4[:, :, :T], in_=v3(pss), func=AF.Sqrt, bias=1e-6), reads=[pss], writes=[rs4])
        kb.op("dve", lambda e: e.reciprocal(out=rs4[:, :, :T], in_=rs4[:, :, :T]), reads=[rs4], writes=[rs4])
        kb.op("dve", lambda e: e.tensor_tensor(out=hT[:, :, :T], in0=hT[:, :, :T], in1=rs4[:, :, :T], op=ALU.mult),
              reads=[hT, rs4], writes=[hT])
        kb.op("dve", lambda e: e.tensor_tensor(out=hT[:, :, :T], in0=hT[:, :, :T], in1=moT[:, :, :T], op=ALU.mult),
              reads=[hT, moT], writes=[hT])
        kb.op("dve", lambda e: e.tensor_tensor(out=mixT[:, 0:4, :T], in0=hT[:, :, :T], in1=b3(MN, T), op=ALU.mult),
              reads=[hT, MN], writes=[mixT])
        yield
        for h in range(4):
            kb.op("dve", lambda e: e.tensor_scalar(out=kw[:T, h, :], in0=kTMf[:T, h * 128:(h + 1) * 128], scalar1=ew[:T, h:h + 1],
                                                   scalar2=ISQ, op0=ALU.mult, op1=ALU.mult), reads=[kTMf, ew], writes=[kw])
        for half in range(2):
            pC = nM()
            for hh in range(2):
                h = half * 2 + hh
                kb.op("pe", lambda e: e.matmul(pC[:, hh * 129:(hh + 1) * 129], kw[:T, h, :], vTM1[:T, h, :], start=True, stop=True),
                      reads=[kw, vTM1], writes=[pC], inc=(hh == 1))
            for hh in range(2):
                h = half * 2 + hh
                kb.op("dve", lambda e: e.scalar_tensor_tensor(out=C[:, h, :], in0=C[:, h, :], scalar=eB[:, h, T - 1:T],
                                                              in1=pC[:, hh * 129:(hh + 1) * 129], op0=ALU.mult, op1=ALU.add),
                      reads=[C, eB, pC], writes=[C])
        yield
        kb.op("act", lambda e: e.copy(out=Cb[:, :, :], in_=C[:, :, 0:128]), reads=[C], writes=[Cb])
        kb.op("dve", lambda e: e.tensor_copy(out=nbc[:, :, :], in_=C[:, :, 128:129].to_broadcast([128, 4, 128])),
              reads=[C], writes=[nbc])


    def gen_prep(ti):
        r0, T = g.tiles[ti]
        HT = HTs[ti % 3]
        mixT = mixTs[ti % 2]
        GG = GGs[ti % 2]
        bonus = bonuss[ti % 2]
        def proj_fm(pbank, j, col):
            for kc in range(8):
                kb.op("pe", lambda e: e.matmul(pbank[:, j * T:(j + 1) * T], Win[:, kc, col:col + 128], hnT[:, kc, :T],
                                               start=(kc == 0), stop=(kc == 7)),
                      reads=[Win, hnT], writes=[pbank], inc=(kc == 7))

        def proj_tm(pbank, col, n, c0=0):
            for kc in range(8):
                kb.op("pe", lambda e: e.matmul(pbank[:T, c0:c0 + n], hnT[:, kc, :T], Win[:, kc, col:col + n],
                                               start=(kc == 0), stop=(kc == 7)),
                      reads=[hnT, Win], writes=[pbank], inc=(kc == 7))

        def v3(pbank, n=4):
            return pbank[:, 0:n * T].rearrange("p (c t) -> p c t", c=n)

        kb.op("dve", lambda e: e.tensor_tensor(out=D1[:, :, :T], in0=ZR[:, :, 0:T], in1=ZR[:, :, 1:T + 1], op=ALU.subtract),
              reads=[ZR], writes=[D1])
        kb.op("dve", lambda e: e.tensor_tensor(out=D1[:, :, :T], in0=D1[:, :, :T], in1=b3(MU, T, 14), op=ALU.mult),
              reads=[D1, MU], writes=[D1])
        kb.op("dve", lambda e: e.tensor_tensor(out=Z2[:, :, :T], in0=D1[:, :, :T], in1=ZR[:, :, 1:T + 1], op=ALU.add),
              reads=[D1, ZR], writes=[Z2])
        kb.op("dve", lambda e: e.tensor_copy(out=ZR[:, :, 0:1], in_=ZR[:, :, T:T + 1]), reads=[ZR], writes=[ZR])
        r_ = Z2[:, 0:4, :T]; k_ = Z2[:, 4:8, :T]; v_ = Z2[:, 8:12, :T]
        yield
        kb.op("act", lambda e: e.activation(out=LIN[0:64, :T], in_=Z2[0:64, 12, :T], func=AF.Tanh), reads=[Z2], writes=[LIN])
        kb.op("act", lambda e: e.copy(out=LIN[64:128, :T], in_=Z2[64:128, 12, :T]), reads=[Z2], writes=[LIN])
        kb.op("act", lambda e: e.activation(out=sxg[:, :T], in_=Z2[:, 13, :T], func=AF.Sigmoid), reads=[Z2], writes=[sxg])
        yield
        pw = nR()
        for c in range(4):
            kb.op("pe", lambda e: e.matmul(pw[:, c * T:(c + 1) * T], W2A[0:64, c * 128:(c + 1) * 128], LIN[0:64, :T], start=True, stop=True),
                  reads=[W2A, LIN], writes=[pw], inc=(c == 3))
        for c in range(4):
            kb.op("act", lambda e: e.activation(out=sw[:, c, :T], in_=pw[:, c * T:(c + 1) * T], func=AF.Sigmoid, bias=W0[:, c:c + 1]),
                  reads=[pw, W0], writes=[sw])
        pa = nR()
        for c in range(4):
            kb.op("pe", lambda e: e.matmul(pa[:, c * T:(c + 1) * T], W2A[64:128, c * 128:(c + 1) * 128], LIN[64:128, :T], start=True, stop=True),
                  reads=[W2A, LIN], writes=[pa], inc=(c == 3))
        for c in range(4):
            kb.op("act", lambda e: e.activation(out=aa[:, c, :T], in_=pa[:, c * T:(c + 1) * T], func=AF.Sigmoid, bias=A0[:, c:c + 1]),
                  reads=[pa, A0], writes=[aa])
        pgg = nR()
        for c in range(4):
            kb.op("pe", lambda e: e.matmul(pgg[:, c * T:(c + 1) * T], G2[:, c * 128:(c + 1) * 128], sxg[:, :T], start=True, stop=True),
                  reads=[G2, sxg], writes=[pgg], inc=(c == 3))
        kb.op("act", lambda e: e.copy(out=GG[:, :, :T], in_=v3(pgg)), reads=[pgg], writes=[GG])
        yield
        kb.op("dve", lambda e: e.tensor_tensor(out=kkr[:, :, :T], in0=k_, in1=b3(KK, T), op=ALU.mult), reads=[Z2, KK], writes=[kkr])
        kb.op("act", lambda e: e.activation(out=tq[:, :, :T], in_=kkr[:, :, :T], func=AF.Square), reads=[kkr], writes=[tq])
        pn = nR()
        for c in range(4):
            kb.op("pe", lambda e: e.matmul(pn[:, c * T:(c + 1) * T], blk[:, :], tq[:, c, :T], start=True, stop=True),
                  reads=[blk, tq], writes=[pn], inc=(c == 3))
        kb.op("act", lambda e: e.activation(out=rn[:, :, :T], in_=v3(pn), func=AF.Sqrt), reads=[pn], writes=[rn])
        kb.op("dve", lambda e: e.tensor_scalar(out=rn[:, :, :T], in0=rn[:, :, :T], scalar1=1e-12, scalar2=None, op0=ALU.max),
              reads=[rn], writes=[rn])
        kb.op("dve", lambda e: e.reciprocal(out=rn[:, :, :T], in_=rn[:, :, :T]), reads=[rn], writes=[rn])
        kb.op("dve", lambda e: e.tensor_tensor(out=kkr[:, :, :T], in0=kkr[:, :, :T], in1=rn[:, :, :T], op=ALU.mult),
              reads=[kkr, rn], writes=[kkr])
        yield
        kb.op("dve", lambda e: e.scalar_tensor_tensor(out=tq[:, :, :T], in0=aa[:, :, :T], scalar=-1.0, in1=b3(KA, T),
                                                      op0=ALU.add, op1=ALU.mult), reads=[aa, KA], writes=[tq])
        kb.op("dve", lambda e: e.scalar_tensor_tensor(out=kp[:, :, :T], in0=tq[:, :, :T], scalar=1.0, in1=k_,
                                                      op0=ALU.add, op1=ALU.mult), reads=[tq, Z2], writes=[kp])
        yield
        for c in range(4):
            kb.op("dve", lambda e: e.tensor_tensor_scan(out=CS[:, c, :T], data0=ones1[:, :T], data1=sw[:, c, :T], initial=0.0,
                                                        op0=ALU.mult, op1=ALU.add), reads=[ones1, sw], writes=[CS])
        kb.op("dve", lambda e: e.tensor_tensor(out=CSp[:, :, :T], in0=CS[:, :, :T], in1=sw[:, :, :T], op=ALU.subtract),
              reads=[CS, sw], writes=[CSp])
        kb.op("dve", lambda e: e.tensor_scalar(out=csl[:, :], in0=CS[:, :, T - 1], scalar1=-EH, scalar2=None, op0=ALU.mult),
              reads=[CS], writes=[csl])
        kb.op("act", lambda e: e.activation(out=eW[:, :, :T], in_=CS[:, :, :T], func=AF.Exp, scale=-EH), reads=[CS], writes=[eW])
        kb.op("act", lambda e: e.activation(out=eWp[:, :, :T], in_=CSp[:, :, :T], func=AF.Exp, scale=-EH), reads=[CSp], writes=[eWp])
        kb.op("act", lambda e: e.activation(out=eWi[:, :, :T], in_=CS[:, :, :T], func=AF.Exp, scale=EH), reads=[CS], writes=[eWi])
        for c in range(4):
            kb.op("act", lambda e: e.activation(out=eWT[:, c, :T], in_=CS[:, c, :T], func=AF.Exp, scale=EH, bias=csl[:, c:c + 1]),
                  reads=[CS, csl], writes=[eWT])
        yield
        kb.op("dve", lambda e: e.scalar_tensor_tensor(out=AR[:, :, 0, :T], in0=kkr[:, :, :T], scalar=-1.0, in1=eWp[:, :, :T],
                                                      op0=ALU.mult, op1=ALU.mult), reads=[kkr, eWp], writes=[AR])
        kb.op("dve", lambda e: e.tensor_tensor(out=AR[:, :, 1, :T], in0=r_, in1=eW[:, :, :T], op=ALU.mult), reads=[Z2, eW], writes=[AR])
        kb.op("dve", lambda e: e.tensor_tensor(out=kka[:, :, :T], in0=kkr[:, :, :T], in1=aa[:, :, :T], op=ALU.mult),
              reads=[kkr, aa], writes=[kka])
        kb.op("dve", lambda e: e.tensor_tensor(out=BH[:, :, :T], in0=kka[:, :, :T], in1=eWT[:, :, :T], op=ALU.mult),
              reads=[kka, eWT], writes=[BH])
        kb.op("dve", lambda e: e.tensor_tensor(out=KH[:, :, :T], in0=kp[:, :, :T], in1=eWT[:, :, :T], op=ALU.mult),
              reads=[kp, eWT], writes=[KH])
        kb.op("act", lambda e: e.copy(out=vb[:, :, :T], in_=v_), reads=[Z2], writes=[vb])
        yield
        kb.op("dve", lambda e: e.tensor_tensor(out=tq[:, :, :T], in0=r_, in1=kp[:, :, :T], op=ALU.mult), reads=[Z2, kp], writes=[tq])
        kb.op("dve", lambda e: e.tensor_tensor(out=tq[:, :, :T], in0=tq[:, :, :T], in1=b3(RRK, T), op=ALU.mult),
              reads=[tq, RRK], writes=[tq])
        prk = nR()
        for c in range(4):
            kb.op("pe", lambda e: e.matmul(prk[:, c * T:(c + 1) * T], blk[:, :], tq[:, c, :T], start=True, stop=True),
                  reads=[blk, tq], writes=[prk], inc=(c == 3))
        kb.op("dve", lambda e: e.tensor_tensor(out=bonus[:, :, :T], in0=v3(prk), in1=v_, op=ALU.mult), reads=[prk, Z2], writes=[bonus])
        yield
        PST = g.PST
        for c in range(4):
            kb.op("pe", lambda e: e.transpose(out=PST[:T, c * 128:(c + 1) * 128], in_=vb[:, c, :T], identity=g.ident_b[:, :]),
                  reads=[vb, g.ident_b], writes=[PST], inc=False)
        for c in range(4):
            kb.op("pe", lambda e: e.transpose(out=PST[:T, (4 + c) * 128:(5 + c) * 128], in_=BH[:, c, :T], identity=g.ident_b[:, :]),
                  reads=[BH, g.ident_b], writes=[PST], inc=(c == 3))
        kb.op("act", lambda e: e.copy(out=VTM[:T, :, :], in_=PST[:T, 0:512].rearrange("p (h v) -> p h v", h=8)), reads=[PST], writes=[VTM])
        kb.op("act", lambda e: e.copy(out=BHT[:T, :, :], in_=PST[:T, 512:1024].rearrange("p (h v) -> p h v", h=8)), reads=[PST], writes=[BHT])
        for c in range(4):
            kb.op("pe", lambda e: e.transpose(out=PST[:T, c * 128:(c + 1) * 128], in_=KH[:, c, :T], identity=g.ident_b[:, :]),
                  reads=[KH, g.ident_b], writes=[PST], inc=(c == 3))
        kb.op("act", lambda e: e.copy(out=KHT[:T, :, :], in_=PST[:T, 0:512].rearrange("p (h v) -> p h v", h=8)), reads=[PST], writes=[KHT])
        yield
        for hf in range(2):
            mcol = blk[:, 127 * hf:127 * hf + 1]
            kb.op("dve", lambda e: e.scalar_tensor_tensor(out=BTm[hf][:, :, :T], in0=kka[:, :, :T], scalar=mcol, in1=eWi[:, :, :T],
                                                          op0=ALU.mult, op1=ALU.mult), reads=[kka, blk, eWi], writes=[BTm[hf]])
            kb.op("dve", lambda e: e.scalar_tensor_tensor(out=KTm[hf][:, :, :T], in0=kp[:, :, :T], scalar=mcol, in1=eWi[:, :, :T],
                                                          op0=ALU.mult, op1=ALU.mult), reads=[kp, blk, eWi], writes=[KTm[hf]])
            kb.op("dve", lambda e: e.scalar_tensor_tensor(out=ATm[hf][:, :, :T], in0=kkr[:, :, :T], scalar=nblk[:, hf:hf + 1], in1=eWp[:, :, :T],
                                                          op0=ALU.mult, op1=ALU.mult), reads=[kkr, nblk, eWp], writes=[ATm[hf]])
        X, XT = Xa[0], XTa[0]
        sub = su[:T, :T].unsqueeze(1).to_broadcast([T, 2, T])
        ueb = ue[:T, :T].unsqueeze(1).to_broadcast([T, 2, T])
        slb = sl[:T, :T].unsqueeze(1).to_broadcast([T, 4, T])
        yield
        for c in range(4):
            yield
            pNA = nR()
            for hf in range(2):
                for j in range(2):
                    kb.op("pe", lambda e: e.matmul(pNA[:T, (hf * 2 + j) * T:(hf * 2 + j + 1) * T], BTm[hf][:, c, :T], AR[:, c, j, :T], start=True, stop=True),
                          reads=[BTm[hf], AR], writes=[pNA], inc=(hf == 1 and j == 1))
            na4 = pNA[:T, 0:4 * T].rearrange("p (h j t) -> p h j t", h=2, j=2)
            kb.op("dve", lambda e: e.tensor_tensor(out=X[:T, 2 * c:2 * c + 2, :T], in0=na4[:, :, 0, :], in1=sub, op=ALU.mult),
                  reads=[pNA, su], writes=[X])
            kb.op("dve", lambda e: e.tensor_tensor(out=ARB[:T, 2 * c:2 * c + 2, :T], in0=na4[:, :, 1, :], in1=ueb, op=ALU.mult),
                  reads=[pNA, ue], writes=[ARB])
            pKA = nR(); ka4 = pKA[:T, 0:4 * T].rearrange("p (h j t) -> p h j t", h=2, j=2)
            for hf in range(2):
                for j in range(2):
                    kb.op("pe", lambda e: e.matmul(pKA[:T, (hf * 2 + j) * T:(hf * 2 + j + 1) * T], KTm[hf][:, c, :T], AR[:, c, j, :T], start=True, stop=True),
                          reads=[KTm[hf], AR], writes=[pKA], inc=(hf == 1 and j == 1))
            kb.op("dve", lambda e: e.tensor_tensor(out=AAK[:T, 2 * c:2 * c + 2, :T], in0=ka4[:, :, 0, :], in1=sub, op=ALU.mult),
                  reads=[pKA, su], writes=[AAK])
            kb.op("dve", lambda e: e.tensor_tensor(out=ARK[:T, 2 * c:2 * c + 2, :T], in0=ka4[:, :, 1, :], in1=ueb, op=ALU.mult),
                  reads=[pKA, ue], writes=[ARK])
        yield
        for half in range(2):
            pNb = nR()
            for j in range(4):
                h = half * 4 + j
                c, hf = h // 2, h % 2
                pl = slice(hf * 64, hf * 64 + 64)
                kb.op("pe", lambda e: e.matmul(pNb[:T, j * T:(j + 1) * T], ATm[hf][:, c, :T], BTm[hf][:, c, :T], start=True, stop=True),
                      reads=[ATm[hf], BTm[hf]], writes=[pNb], inc=(j == 3))
            kb.op("dve", lambda e: e.tensor_tensor(out=XT[:T, half * 4:half * 4 + 4, :T], in0=v3(pNb)[:T], in1=slb, op=ALU.mult),
                  reads=[pNb, sl], writes=[XT])
        kb.op("dve", lambda e: e.tensor_tensor(out=Pm[:T, :, :T], in0=X[:T, :, :T],
                                               in1=g.ident_f[:T, :T].unsqueeze(1).to_broadcast([T, 8, T]), op=ALU.add),
              reads=[X, g.ident_f], writes=[Pm])

    def gen_neumann(ti):
        r0, T = g.tiles[ti]
        HT = HTs[ti % 3]
        mixT = mixTs[ti % 2]
        GG = GGs[ti % 2]
        bonus = bonuss[ti % 2]
        def proj_fm(pbank, j, col):
            for kc in range(8):
                kb.op("pe", lambda e: e.matmul(pbank[:, j * T:(j + 1) * T], Win[:, kc, col:col + 128], hnT[:, kc, :T],
                                               start=(kc == 0), stop=(kc == 7)),
                      reads=[Win, hnT], writes=[pbank], inc=(kc == 7))

        def proj_tm(pbank, col, n, c0=0):
            for kc in range(8):
                kb.op("pe", lambda e: e.matmul(pbank[:T, c0:c0 + n], hnT[:, kc, :T], Win[:, kc, col:col + n],
                                               start=(kc == 0), stop=(kc == 7)),
                      reads=[hnT, Win], writes=[pbank], inc=(kc == 7))

        def v3(pbank, n=4):
            return pbank[:, 0:n * T].rearrange("p (c t) -> p c t", c=n)

        yield
        lv = 1
        cur = 0
        while lv * 2 < T:
            X, XT = Xa[cur], XTa[cur]
            Xn, XTn = Xa[1 - cur], XTa[1 - cur]
            for half in range(2):
                yield
                p1 = nS(); p2 = nS()
                for j in range(4):
                    h = half * 4 + j
                    kb.op("pe", lambda e: e.matmul(p1[:T, j * T:(j + 1) * T], XT[:T, h, :T], X[:T, h, :T], start=True, stop=True),
                          reads=[XT, X], writes=[p1], inc=(j == 3))
                for j in range(4):
                    h = half * 4 + j
                    kb.op("pe", lambda e: e.matmul(p2[:T, j * T:(j + 1) * T], X[:T, h, :T], XT[:T, h, :T], start=True, stop=True),
                          reads=[XT, X], writes=[p2], inc=(j == 3))
                kb.op("act", lambda e: e.copy(out=Xn[:T, half * 4:half * 4 + 4, :T], in_=v3(p1)[:T]), reads=[p1], writes=[Xn])
                kb.op("dve", lambda e: e.tensor_copy(out=XTn[:T, half * 4:half * 4 + 4, :T], in_=v3(p2)[:T]), reads=[p2], writes=[XTn])
            yield
            for half in range(2):
                p3 = nS()
                for j in range(4):
                    h = half * 4 + j
                    kb.op("pe", lambda e: e.matmul(p3[:T, j * T:(j + 1) * T], XTn[:T, h, :T], Pm[:T, h, :T], start=True, stop=True),
                          reads=[XTn, Pm], writes=[p3], inc=(j == 3))
                kb.op("dve", lambda e: e.tensor_tensor(out=Pm[:T, half * 4:half * 4 + 4, :T], in0=Pm[:T, half * 4:half * 4 + 4, :T],
                                                       in1=v3(p3)[:T], op=ALU.add), reads=[Pm, p3], writes=[Pm])
            cur = 1 - cur
            lv *= 2

    def tail1(ti):
        r0, T = g.tiles[ti]
        HT = HTs[ti % 3]
        mixT = mixTs[ti % 2]
        GG = GGs[ti % 2]
        bonus = bonuss[ti % 2]
        def proj_fm(pbank, j, col):
            for kc in range(8):
                kb.op("pe", lambda e: e.matmul(pbank[:, j * T:(j + 1) * T], Win[:, kc, col:col + 128], hnT[:, kc, :T],
                                               start=(kc == 0), stop=(kc == 7)),
                      reads=[Win, hnT], writes=[pbank], inc=(kc == 7))

        def proj_tm(pbank, col, n, c0=0):
            for kc in range(8):
                kb.op("pe", lambda e: e.matmul(pbank[:T, c0:c0 + n], hnT[:, kc, :T], Win[:, kc, col:col + n],
                                               start=(kc == 0), stop=(kc == 7)),
                      reads=[hnT, Win], writes=[pbank], inc=(kc == 7))

        def v3(pbank, n=4):
            return pbank[:, 0:n * T].rearrange("p (c t) -> p c t", c=n)

        cur = ((T.bit_length() - 2) % 2) if T > 2 else 0
        pP1 = nS()
        for h in range(8):
            c, hf = h // 2, h % 2
            pl = slice(hf * 64, hf * 64 + 64)
            kb.op("pe", lambda e: e.matmul(pP1[:T, h * 64:(h + 1) * 64], ATm[hf][:, c, :T], STb[:, c, :], start=True, stop=False),
                  reads=[ATm[hf], STb], writes=[pP1], inc=False)
            kb.op("pe", lambda e: e.matmul(pP1[:T, h * 64:(h + 1) * 64], AAK[:T, h, :T], VTM[:T, h, :], start=False, stop=True),
                  reads=[AAK, VTM], writes=[pP1], inc=(h == 7))
        kb.op("act", lambda e: e.copy(out=P1[:T, :], in_=pP1[:T, :]), reads=[pP1], writes=[P1])
        pU = nS()
        for h in range(8):
            kb.op("pe", lambda e: e.matmul(pU[:T, h * 64:(h + 1) * 64], Pm[:T, h, :T], P1[:T, h * 64:(h + 1) * 64], start=True, stop=True),
                  reads=[Pm, P1], writes=[pU], inc=(h == 7))
        kb.op("act", lambda e: e.copy(out=UTM[:T, :, :], in_=pU[:T, :].rearrange("p (h v) -> p h v", h=8)), reads=[pU], writes=[UTM])
        pO = nS()
        for c in range(4):
            kb.op("pe", lambda e: e.matmul(pO[:, c * T:(c + 1) * T], STbd[:, c, :], AR[:, c, 1, :T], start=True, stop=False),
                  reads=[STbd, AR], writes=[pO], inc=False)
            for hf in range(2):
                h = 2 * c + hf
                pl = slice(hf * 64, hf * 64 + 64)
                kb.op("pe", lambda e: e.matmul(pO[pl, c * T:(c + 1) * T], UTM[:T, h, :], ARB[:T, h, :T], start=False, stop=False),
                      reads=[UTM, ARB], writes=[pO], inc=False)
                kb.op("pe", lambda e: e.matmul(pO[pl, c * T:(c + 1) * T], VTM[:T, h, :], ARK[:T, h, :T], start=False, stop=True),
                      reads=[VTM, ARK], writes=[pO], inc=(hf == 1))
        kb.op("act", lambda e: e.copy(out=Of[:, :, :T], in_=v3(pO)), reads=[pO], writes=[Of])
        pS = nS()
        for h in range(8):
            c, hf = h // 2, h % 2
            pl = slice(hf * 64, hf * 64 + 64)
            kb.op("pe", lambda e: e.matmul(pS[pl, c * 64:(c + 1) * 64], BHT[:T, h, :], UTM[:T, h, :], start=True, stop=False),
                  reads=[BHT, UTM], writes=[pS], inc=False)
            kb.op("pe", lambda e: e.matmul(pS[pl, c * 64:(c + 1) * 64], KHT[:T, h, :], VTM[:T, h, :], start=False, stop=True),
                  reads=[KHT, VTM], writes=[pS], inc=(h == 7))
        for c in range(4):
            kb.op("dve", lambda e: e.scalar_tensor_tensor(out=ST[:, c, :], in0=ST[:, c, :], scalar=eW[:, c, T - 1:T],
                                                          in1=pS[:, c * 64:(c + 1) * 64], op0=ALU.mult, op1=ALU.add),
                  reads=[ST, eW, pS], writes=[ST])
        kb.op("act", lambda e: e.copy(out=STb[:, :, :], in_=ST[:, :, :]), reads=[ST], writes=[STb])
        kb.op("act", lambda e: e.copy(out=STbd[0:64, :, 0:64], in_=ST[0:64, :, :]), reads=[ST], writes=[STbd])
        kb.op("act", lambda e: e.copy(out=STbd[64:128, :, 64:128], in_=ST[64:128, :, :]), reads=[ST], writes=[STbd])

    def gen_tail2(ti):
        r0, T = g.tiles[ti]
        HT = HTs[ti % 3]
        mixT = mixTs[ti % 2]
        GG = GGs[ti % 2]
        bonus = bonuss[ti % 2]
        def proj_fm(pbank, j, col):
            for kc in range(8):
                kb.op("pe", lambda e: e.matmul(pbank[:, j * T:(j + 1) * T], Win[:, kc, col:col + 128], hnT[:, kc, :T],
                                               start=(kc == 0), stop=(kc == 7)),
                      reads=[Win, hnT], writes=[pbank], inc=(kc == 7))

        def proj_tm(pbank, col, n, c0=0):
            for kc in range(8):
                kb.op("pe", lambda e: e.matmul(pbank[:T, c0:c0 + n], hnT[:, kc, :T], Win[:, kc, col:col + n],
                                               start=(kc == 0), stop=(kc == 7)),
                      reads=[hnT, Win], writes=[pbank], inc=(kc == 7))

        def v3(pbank, n=4):
            return pbank[:, 0:n * T].rearrange("p (c t) -> p c t", c=n)

        cur = ((T.bit_length() - 2) % 2) if T > 2 else 0
        kb.op("act", lambda e: e.activation(out=Osq[:, :, :T], in_=Of[:, :, :T], func=AF.Square), reads=[Of], writes=[Osq])
        yield
        pm_ = nS(); pq_ = nS()
        for c in range(4):
            kb.op("pe", lambda e: e.matmul(pm_[:, c * T:(c + 1) * T], blk64[:, :], Of[:, c, :T], start=True, stop=True),
                  reads=[blk64, Of], writes=[pm_], inc=(c == 3))
        for c in range(4):
            kb.op("pe", lambda e: e.matmul(pq_[:, c * T:(c + 1) * T], blk64[:, :], Osq[:, c, :T], start=True, stop=True),
                  reads=[blk64, Osq], writes=[pq_], inc=(c == 3))
        kb.op("act", lambda e: e.copy(out=mean_s[:, :, :T], in_=v3(pm_)), reads=[pm_], writes=[mean_s])
        yield
        kb.op("dve", lambda e: e.scalar_tensor_tensor(out=var[:, :, :T], in0=mean_s[:, :, :T], scalar=-1.0, in1=mean_s[:, :, :T],
                                                      op0=ALU.mult, op1=ALU.mult), reads=[mean_s], writes=[var])
        kb.op("dve", lambda e: e.tensor_tensor(out=var[:, :, :T], in0=var[:, :, :T], in1=v3(pq_), op=ALU.add),
              reads=[var, pq_], writes=[var])
        yield
        kb.op("dve", lambda e: e.tensor_scalar(out=var[:, :, :T], in0=var[:, :, :T], scalar1=0.0, scalar2=None, op0=ALU.max),
              reads=[var], writes=[var])
        kb.op("act", lambda e: e.activation(out=var[:, :, :T], in_=var[:, :, :T], func=AF.Sqrt, bias=64e-5), reads=[var], writes=[var])
        yield
        kb.op("dve", lambda e: e.reciprocal(out=var[:, :, :T], in_=var[:, :, :T]), reads=[var], writes=[var])
        kb.op("dve", lambda e: e.tensor_tensor(out=Of[:, :, :T], in0=Of[:, :, :T], in1=mean_s[:, :, :T], op=ALU.subtract),
              reads=[Of, mean_s], writes=[Of])
        yield
        kb.op("dve", lambda e: e.tensor_tensor(out=Of[:, :, :T], in0=Of[:, :, :T], in1=var[:, :, :T], op=ALU.mult),
              reads=[Of, var], writes=[Of])
        kb.op("dve", lambda e: e.tensor_tensor(out=Of[:, :, :T], in0=Of[:, :, :T], in1=b3(LNW, T), op=ALU.mult),
              reads=[Of, LNW], writes=[Of])
        yield
        kb.op("dve", lambda e: e.tensor_tensor(out=Of[:, :, :T], in0=Of[:, :, :T], in1=b3(LNB, T), op=ALU.add),
              reads=[Of, LNB], writes=[Of])
        kb.op("dve", lambda e: e.tensor_tensor(out=Of[:, :, :T], in0=Of[:, :, :T], in1=bonus[:, :, :T], op=ALU.add),
              reads=[Of, bonus], writes=[Of])
        yield
        kb.op("dve", lambda e: e.tensor_tensor(out=mixT[:, 4:8, :T], in0=Of[:, :, :T], in1=GG[:, :, :T], op=ALU.mult),
              reads=[Of, GG], writes=[mixT])
        for nb in range(2):
            pp = nS()
            for c in range(8):
                kb.op("pe", lambda e: e.matmul(pp[:T, :], mixT[:, c, :T], Wout[:, c, nb * 512:(nb + 1) * 512],
                                               start=(c == 0), stop=(c == 7)), reads=[mixT, Wout], writes=[pp], inc=(c == 7))
            kb.op("dve", lambda e: e.tensor_tensor(out=HT[:T, nb * 512:(nb + 1) * 512], in0=HT[:T, nb * 512:(nb + 1) * 512],
                                                   in1=pp[:T, :], op=ALU.add), reads=[HT, pp], writes=[HT])
        yield
        store_h(g, dst, ti, HT, final, None, (ss, rstd, junk))


    def run(gen):
        for _ in gen:
            pass

    def interleave(a, b, ra=1, rb=1):
        da = db = False
        while not (da and db):
            for _ in range(ra):
                if not da:
                    try:
                        next(a)
                    except StopIteration:
                        da = True
            for _ in range(rb):
                if not db:
                    try:
                        next(b)
                    except StopIteration:
                        db = True

    cost = op_cost

    def norm_and_proj(ti):
        rmsnorm_T(g, HTs[ti % 3], g.tiles[ti][1], Gb, hn, hnT, ss, rstd, junk)
        return gen_proj(ti)

    def s0(ti):
        for _ in gen_neumann(ti):
            pass
        tail1(ti)
        for _ in gen_tail2(ti):
            pass
        if ti + 3 < ntl:
            load_h(g, src, ti + 3, HTs[ti % 3])

    ntl = len(g.tiles)
    for k_ in range(min(3, ntl)):
        load_h(g, src, k_, HTs[k_])
    NOSCHED = bool(os.environ.get("L0_NOSCHED"))

    def rec(f):
        if NOSCHED:
            r = f()
            if r is not None and hasattr(r, "__next__"):
                for _ in r:
                    pass
            return []
        return kb.record(f)

    run(norm_and_proj(0))
    pro = [rec(lambda: gen_mlstm(0)), rec(lambda: gen_prep(0))]
    if ntl > 1:
        pro.append(rec(lambda: norm_and_proj(1)))
    kb.schedule(pro, cost, sync_ns=float(os.environ.get('SYNC_NS', '0')), slack_ns=float(os.environ.get('SLACK_NS', '0')))
    for ti in range(ntl):
        streams = [rec(lambda: s0(ti))]
        if ti + 1 < ntl:
            streams.append(rec(lambda: gen_mlstm(ti + 1)))
            streams.append(rec(lambda: gen_prep(ti + 1)))
        if ti + 2 < ntl:
            streams.append(rec(lambda: norm_and_proj(ti + 2)))
        kb.schedule(streams, cost, sync_ns=float(os.environ.get('SYNC_NS', '0')), slack_ns=float(os.environ.get('SLACK_NS', '0')))


LG = [float(np.log(1.0 - 2.0 ** (-5.0 - h))) for h in range(4)]
TWO_PI = 6.283185307179586
CW1 = 6.28125
CW2 = TWO_PI - CW1


LG = [float(np.log(1.0 - 2.0 ** (-5.0 - h))) for h in range(4)]
TWO_PI = 6.283185307179586
CW1 = 6.28125
CW2 = TWO_PI - CW1


LG = [float(np.log(1.0 - 2.0 ** (-5.0 - h))) for h in range(4)]
TWO_PI = 6.283185307179586
CW1 = 6.28125
CW2 = TWO_PI - CW1


def phase_l1(g, src, dst, final):
    kb, nc, dr = g.kb, g.nc, g.dr
    Win = kb.sb([128, 8, 6144], BF16, "Win")
    Wout = kb.sb([128, 16, D], BF16, "Wout")
    with contextlib.ExitStack() as ses:
        old = kb.es
        kb.es = ses
        stg = [kb.sb([128, 1536], F32, f"stg{i}") for i in range(3)]
        load_weight_bf16(g, dr["o_w_in_p"], 0, D, 6144, Win, stg)
        load_weight_bf16(g, dr["o_w_out"], 0, 2048, D, Wout, stg)
        kb.barrier()
        kb.es = old
    Gb = kb.sb([128, D], BF16, "Gb")
    Gfin = None
    iota = kb.sb([128, 128], F32, "iota")
    pidx = kb.sb([128, 1], F32, "pidx")
    ue = kb.sb([128, 128], F32, "ue")
    inv = kb.sb([128, 1], F32, "inv")
    kb.dma(iota[:, :], dr["c_iota"].ap()[:, :], writes=[iota], sem_buf=iota)
    kb.dma(pidx[:, :], dr["c_pidx"].ap()[:, :], writes=[pidx], sem_buf=pidx)
    kb.dma(ue[:, :], dr["c_ue"].ap()[:, :], writes=[ue], sem_buf=ue)
    kb.dma(inv[:, :], dr["c_inv"].ap()[:, :], writes=[inv], sem_buf=inv)
    DM = kb.sb([128, 4, 128], F32, "DM")
    DEC = kb.sb([128, 4, 128], F32, "DEC")
    KDEC = {128: kb.sb([128, 4], F32, "KDEC128"), 16: kb.sb([128, 4], F32, "KDEC16")}
    tms = kb.sb([128, 128], F32, "tms")
    kb.op("dve", lambda e: e.tensor_scalar(out=tms[:, :], in0=iota[:, :], scalar1=pidx[:, 0:1], scalar2=0.0,
                                           op0=ALU.subtract, op1=ALU.max), reads=[iota, pidx], writes=[tms])
    for h in range(4):
        kb.op("act", lambda e: e.activation(out=DM[:, h, :], in_=tms[:, :], func=AF.Exp, scale=LG[h]),
              reads=[tms], writes=[DM])
        kb.op("dve", lambda e: e.scalar_tensor_tensor(out=DM[:, h, :], in0=DM[:, h, :], scalar=1.0 / 16.0,
                                                      in1=ue[:, :], op0=ALU.mult, op1=ALU.mult),
              reads=[DM, ue], writes=[DM])
        kb.op("act", lambda e: e.activation(out=DEC[:, h, :], in_=iota[:, :], func=AF.Exp, scale=LG[h], bias=LG[h]),
              reads=[iota], writes=[DEC])
        for TT in (128, 16):
            kd = KDEC[TT]
            kb.op("act", lambda e: e.activation(out=kd[:, h:h + 1], in_=pidx[:, 0:1], func=AF.Exp, scale=-LG[h],
                                                bias=LG[h] * (TT - 1)), reads=[pidx], writes=[kd])
            kb.op("dve", lambda e: e.tensor_scalar(out=kd[:, h:h + 1], in0=kd[:, h:h + 1], scalar1=1.0 / 16.0,
                                                   scalar2=None, op0=ALU.mult), reads=[kd], writes=[kd])
    Sr = kb.sb([128, 8, 512], F32, "Sr")
    Srb = kb.sb([128, 8, 512], BF16, "Srb")
    kb.op("dve", lambda e: e.memset(Sr[:, :, :], 0.0), writes=[Sr])
    kb.op("pool", lambda e: e.memset(Srb[:, :, :], 0.0), writes=[Srb])
    HTs = [kb.sb([128, D], F32, f"HT{i}") for i in range(2)]
    hn = kb.sb([128, D], BF16, "hn")
    hnT = kb.sb([128, 8, 128], BF16, "hnT")
    sss = [kb.sb([128, 1], F32, f"ss{i}") for i in range(2)]
    rstds = [kb.sb([128, 1], F32, f"rstd{i}") for i in range(2)]
    ang = kb.sb([128, 128], F32, "ang")
    ang2 = kb.sb([128, 128], F32, "ang2")
    kf = kb.sb([128, 128], F32, "kf")
    ki = kb.sb([128, 128], I32, "ki")
    nsins = [kb.sb([128, 128], F32, f"nsin{i}") for i in range(2)]
    ncoss = [kb.sb([128, 128], F32, f"ncos{i}") for i in range(2)]
    t1 = kb.sb([128, 4, 128], F32, "t1")
    t2 = kb.sb([128, 4, 128], F32, "t2")
    qb = kb.sb([128, 2, 4, 128], BF16, "qb")
    qdb = kb.sb([128, 2, 4, 128], BF16, "qdb")
    kbf = kb.sb([128, 2, 4, 128], BF16, "kbf")
    kdT = kb.sb([128, 8, 128], BF16, "kdT")
    sTm = kb.sb([128, 4, 128], BF16, "sTm")
    VT = kb.sb([128, 2048], BF16, "VT")
    GS = kb.sb([128, 2048], BF16, "GS")
    og = kb.sb([128, 2048], BF16, "og")
    ogT = kb.sb([128, 16, 128], BF16, "ogT")
    st6 = kb.sb([128, 6], F32, "st6")
    mv = kb.sb([128, 2], F32, "mv")
    rs = kb.sb([128, 1], F32, "rs")
    junk = og
    kb.dma(t1[:, :, :].rearrange("p a b -> p (a b)"), bc_rows(dr["norm_mix"], 1, 512), writes=[t1], sem_buf=t1)
    kb.op("dve", lambda e: e.tensor_copy(out=Gb[:, 0:512], in_=t1[:, :, :].rearrange("p a b -> p (a b)")), reads=[t1], writes=[Gb])
    kb.dma(t2[:, :, :].rearrange("p a b -> p (a b)"), bc_rows(dr["norm_mix"], 1, 512, col0=512), writes=[t2], sem_buf=t2)
    kb.op("dve", lambda e: e.tensor_copy(out=Gb[:, 512:1024], in_=t2[:, :, :].rearrange("p a b -> p (a b)")), reads=[t2], writes=[Gb])

    def sincos(dst_tbl, shift, pos0, T):
        kb.op("dve", lambda e: e.tensor_scalar(out=ang[:, :T], in0=iota[:, :T], scalar1=float(pos0), scalar2=inv[:, 0:1],
                                               op0=ALU.add, op1=ALU.mult), reads=[iota, inv], writes=[ang])
        if shift != 0.0:
            kb.op("dve", lambda e: e.tensor_scalar(out=ang[:, :T], in0=ang[:, :T], scalar1=shift, scalar2=None,
                                                   op0=ALU.add), reads=[ang], writes=[ang])
        kb.op("dve", lambda e: e.tensor_scalar(out=ki[:, :T], in0=ang[:, :T], scalar1=1.0 / TWO_PI, scalar2=None,
                                               op0=ALU.mult), reads=[ang], writes=[ki])
        kb.op("dve", lambda e: e.tensor_copy(out=kf[:, :T], in_=ki[:, :T]), reads=[ki], writes=[kf])
        kb.op("dve", lambda e: e.scalar_tensor_tensor(out=ang2[:, :T], in0=kf[:, :T], scalar=-CW1, in1=ang[:, :T],
                                                      op0=ALU.mult, op1=ALU.add), reads=[kf, ang], writes=[ang2])
        kb.op("dve", lambda e: e.scalar_tensor_tensor(out=ang2[:, :T], in0=kf[:, :T], scalar=-CW2, in1=ang2[:, :T],
                                                      op0=ALU.mult, op1=ALU.add), reads=[kf, ang2], writes=[ang2])
        kb.op("dve", lambda e: e.tensor_scalar(out=ang2[:, :T], in0=ang2[:, :T], scalar1=3.1415925, scalar2=-3.1415925,
                                               op0=ALU.min, op1=ALU.max), reads=[ang2], writes=[ang2])
        kb.op("act", lambda e: e.activation(out=dst_tbl[:, :T], in_=ang2[:, :T], func=AF.Sin),
              reads=[ang2], writes=[dst_tbl])

    ntl = len(g.tiles)
    load_h(g, src, 0, HTs[0])
    T0 = g.tiles[0][1]
    norm_stats(g, HTs[0], T0, Gb, hn, sss[0], rstds[0], junk)
    if ntl > 1:
        load_h(g, src, 1, HTs[1])
    norm_transpose(g, hn, hnT, T0)
    sincos(nsins[0], 0.0, g.tiles[0][0], T0)
    sincos(ncoss[0], np.pi / 2, g.tiles[0][0], T0)
    PST = g.PST
    for ti, (r0, T) in enumerate(g.tiles):
        HT = HTs[ti % 2]
        HO = HT
        nsin = nsins[ti % 2]
        ncos = ncoss[ti % 2]
        sb_ = nsin[:, :T].unsqueeze(1).to_broadcast([128, 4, T])
        cb_ = ncos[:, :T].unsqueeze(1).to_broadcast([128, 4, T])
        qk_banks = []
        for which in range(2):
            pe_ = next_ps(g)
            po_ = next_ps(g)
            qk_banks.append((pe_, po_))
            for eo, pb in ((0, pe_), (1, po_)):
                for h in range(4):
                    col = which * 1024 + h * 256 + eo * 128
                    for kc in range(8):
                        kb.op("pe", lambda e: e.matmul(pb[:, h * T:(h + 1) * T], Win[:, kc, col:col + 128],
                                                       hnT[:, kc, :T], start=(kc == 0), stop=(kc == 7)),
                              reads=[Win, hnT], writes=[pb], inc=(kc == 7))
        for which in range(2):
            pe_, po_ = qk_banks[which]
            pe3 = pe_[:, 0:4 * T].rearrange("p (h t) -> p h t", h=4)
            po3 = po_[:, 0:4 * T].rearrange("p (h t) -> p h t", h=4)
            dstb = qb if which == 0 else kbf
            kb.op("dve", lambda e: e.tensor_tensor(out=t1[:, :, :T], in0=pe3, in1=cb_, op=ALU.mult),
                  reads=[pe_, ncos], writes=[t1])
            kb.op("dve", lambda e: e.tensor_tensor(out=t2[:, :, :T], in0=po3, in1=sb_, op=ALU.mult),
                  reads=[po_, nsin], writes=[t2])
            kb.op("dve", lambda e: e.tensor_tensor(out=dstb[:, 0, :, :T], in0=t1[:, :, :T], in1=t2[:, :, :T],
                                                   op=ALU.subtract), reads=[t1, t2], writes=[dstb])
            kb.op("dve", lambda e: e.tensor_tensor(out=t1[:, :, :T], in0=po3, in1=cb_, op=ALU.mult),
                  reads=[po_, ncos], writes=[t1])
            kb.op("dve", lambda e: e.tensor_tensor(out=t2[:, :, :T], in0=pe3, in1=sb_, op=ALU.mult),
                  reads=[pe_, nsin], writes=[t2])
            kb.op("dve", lambda e: e.tensor_tensor(out=dstb[:, 1, :, :T], in0=t1[:, :, :T], in1=t2[:, :, :T],
                                                   op=ALU.add), reads=[t1, t2], writes=[dstb])
            if which == 0:
                for eo in range(2):
                    kb.op("pool", lambda e: e.tensor_tensor(out=qdb[:, eo, :, :T], in0=qb[:, eo, :, :T],
                                                            in1=DEC[:, :, :T], op=ALU.mult),
                          reads=[qb, DEC], writes=[qdb])
        for nb in range(4):
            pvv = next_ps(g)
            for kc in range(8):
                kb.op("pe", lambda e: e.matmul(pvv[:T, :], hnT[:, kc, :T], Win[:, kc, 2048 + nb * 512:2048 + (nb + 1) * 512],
                                               start=(kc == 0), stop=(kc == 7)), reads=[hnT, Win], writes=[pvv], inc=(kc == 7))
            kb.op("act", lambda e: e.copy(out=VT[:T, nb * 512:(nb + 1) * 512], in_=pvv[:T, :]), reads=[pvv], writes=[VT])
        for nb in range(4):
            pgg = next_ps(g)
            for kc in range(8):
                kb.op("pe", lambda e: e.matmul(pgg[:T, :], hnT[:, kc, :T], Win[:, kc, 4096 + nb * 512:4096 + (nb + 1) * 512],
                                               start=(kc == 0), stop=(kc == 7)), reads=[hnT, Win], writes=[pgg], inc=(kc == 7))
            kb.op("act", lambda e: e.activation(out=GS[:T, nb * 512:(nb + 1) * 512], in_=pgg[:T, :], func=AF.Silu),
                  reads=[pgg], writes=[GS])
        if ti + 1 < ntl:
            r0n, Tn = g.tiles[ti + 1]
            sincos(nsins[(ti + 1) % 2], 0.0, r0n, Tn)
            sincos(ncoss[(ti + 1) % 2], np.pi / 2, r0n, Tn)
        for h in range(4):
            for eo in range(2):
                j = h * 2 + eo
                kb.op("pe", lambda e: e.transpose(out=PST[:T, j * 128:(j + 1) * 128], in_=kbf[:, eo, h, :T],
                                                  identity=g.ident_b[:, :]),
                      reads=[kbf, g.ident_b], writes=[PST], inc=(j == 7))
        for h in range(4):
            kb.op("act", lambda e: e.activation(out=kdT[:T, 2 * h:2 * h + 2, :],
                                                in_=PST[:T, 2 * h * 128:(2 * h + 2) * 128].rearrange("p (j d) -> p j d", j=2),
                                                func=AF.Copy, scale=KDEC[T][:T, h:h + 1]),
                  reads=[PST, KDEC[T]], writes=[kdT])
        psc = next_ps(g)
        for h in range(4):
            for eo in range(2):
                kb.op("pe", lambda e: e.matmul(psc[:T, h * T:(h + 1) * T], kbf[:, eo, h, :T], qb[:, eo, h, :T],
                                               start=(eo == 0), stop=(eo == 1)),
                      reads=[kbf, qb], writes=[psc], inc=(eo == 1))
        kb.op("dve", lambda e: e.tensor_tensor(out=sTm[:T, :, :T],
                                               in0=psc[:T, 0:4 * T].rearrange("p (h t) -> p h t", h=4),
                                               in1=DM[:T, :, :T], op=ALU.mult), reads=[psc, DM], writes=[sTm])
        for h in range(4):
            po = next_ps(g)
            kb.op("pe", lambda e: e.matmul(po[:T, :], sTm[:T, h, :T], VT[:T, h * 512:(h + 1) * 512], start=True, stop=False),
                  reads=[sTm, VT], writes=[po], inc=False)
            for eo in range(2):
                kb.op("pe", lambda e: e.matmul(po[:T, :], qdb[:, eo, h, :T], Srb[:, 2 * h + eo, :], start=False, stop=(eo == 1)),
                      reads=[qdb, Srb], writes=[po], inc=(eo == 1))
            kb.op("dve", lambda e: e.bn_stats(out=st6[:T, :], in_=po[:T, :]), reads=[po], writes=[st6])
            kb.op("dve", lambda e: e.bn_aggr(out=mv[:T, :], in_=st6[:T, :]), reads=[st6], writes=[mv])
            kb.op("act", lambda e: e.activation(out=rs[:T, :], in_=mv[:T, 1:2], func=AF.Sqrt, scale=1.0, bias=1e-6),
                  reads=[mv], writes=[rs])
            kb.op("dve", lambda e: e.reciprocal(out=rs[:T, :], in_=rs[:T, :]), reads=[rs], writes=[rs])
            kb.op("dve", lambda e: e.tensor_scalar(out=og[:T, h * 512:(h + 1) * 512], in0=po[:T, :], scalar1=mv[:T, 0:1], scalar2=rs[:T, 0:1],
                                                   op0=ALU.subtract, op1=ALU.mult), reads=[po, mv, rs], writes=[og])
            kb.op("pool", lambda e: e.tensor_tensor(out=og[:T, h * 512:(h + 1) * 512], in0=og[:T, h * 512:(h + 1) * 512],
                                                    in1=GS[:T, h * 512:(h + 1) * 512], op=ALU.mult),
                  reads=[og, GS], writes=[og])
        gT = [float(np.exp(LG[h] * T)) for h in range(4)]
        for h in range(4):
            for eo in range(2):
                j = 2 * h + eo
                pst_ = next_ps(g)
                kb.op("pe", lambda e: e.matmul(pst_[:, :], kdT[:T, j, :], VT[:T, h * 512:(h + 1) * 512], start=True, stop=True),
                      reads=[kdT, VT], writes=[pst_])
                kb.op("dve", lambda e: e.scalar_tensor_tensor(out=Sr[:, j, :], in0=Sr[:, j, :], scalar=gT[h], in1=pst_[:, :],
                                                              op0=ALU.mult, op1=ALU.add), reads=[Sr, pst_], writes=[Sr])
                kb.op("act", lambda e: e.copy(out=Srb[:, j, :], in_=Sr[:, j, :]), reads=[Sr], writes=[Srb])
        for half in range(2):
            for j in range(8):
                c = half * 8 + j
                kb.op("pe", lambda e: e.transpose(out=PST[:, j * T:(j + 1) * T], in_=og[:T, c * 128:(c + 1) * 128],
                                                  identity=g.ident_b[:T, :T]),
                      reads=[og, g.ident_b], writes=[PST], inc=(j == 7))
            kb.op("act", lambda e: e.copy(out=ogT[:, half * 8:(half + 1) * 8, :T],
                                          in_=PST[:, 0:8 * T].rearrange("p (k t) -> p k t", k=8)),
                  reads=[PST], writes=[ogT])
        if ti + 1 < ntl:
            Tn = g.tiles[ti + 1][1]
            norm_stats(g, HTs[(ti + 1) % 2], Tn, Gb, hn, sss[(ti + 1) % 2], rstds[(ti + 1) % 2], junk)
        pps = []
        for nb in range(2):
            pp = next_ps(g)
            pps.append(pp)
            for c in range(16):
                kb.op("pe", lambda e: e.matmul(pp[:T, :], ogT[:, c, :T], Wout[:, c, nb * 512:(nb + 1) * 512],
                                               start=(c == 0), stop=(c == 15)), reads=[ogT, Wout], writes=[pp], inc=(c == 15))
        if ti + 1 < ntl:
            norm_transpose(g, hn, hnT, g.tiles[ti + 1][1])
        for nb in range(2):
            kb.op("dve", lambda e: e.tensor_tensor(out=HO[:T, nb * 512:(nb + 1) * 512],
                                                   in0=HT[:T, nb * 512:(nb + 1) * 512], in1=pps[nb][:T, :], op=ALU.add),
                  reads=[HT, pps[nb]], writes=[HO])
        store_h(g, dst, ti, HO, final, Gfin, (sss[ti % 2], rstds[ti % 2], junk))
        if ti + 2 < ntl:
            load_h(g, src, ti + 2, HTs[ti % 2])


def make_in_map(inputs, b, NT):
    m = {"x": np.ascontiguousarray(inputs["x"][b, :128 * NT])}
    for k, shp in W_SPECS.items():
        src_k = "o_w_in" if k == "o_w_in_p" else k
        m[k] = np.ascontiguousarray(np.asarray(inputs[src_k], np.float32).reshape(shp))
    m.update(host_consts())
    perm = np.arange(6144)
    for sec in range(2):
        for h in range(4):
            base = sec * 1024 + h * 256
            perm[base:base + 256] = np.concatenate([base + np.arange(0, 256, 2), base + np.arange(1, 256, 2)])
    m["o_w_in_p"] = np.ascontiguousarray(m["o_w_in_p"][:, perm])
    return m


def op_cost(o):
    if o[0] == "dma":
        return 60.0
    return _ESC.get(o[1], 1.0) * float(COST_TAB.get(o[6] - host_consts.__code__.co_firstlineno, COST_DEFAULT.get(o[1], 300.0)))


_ESC = {"pe": float(_os.environ.get("PE_SC", "1.0")), "dve": float(_os.environ.get("DVE_SC", "1.0")), "act": float(_os.environ.get("ACT_SC", "1.0"))}
COST_DEFAULT = {"pe": 120.0, "dve": 450.0, "act": 450.0, "pool": 900.0}
COST_TAB = {248: 2442.0, 303: 56.0, 309: 56.0, 318: 474.2, 325: 259.0, 329: 352.0, 333: 351.0, 337: 630.0, 339: 689.0, 273: 427.0, 349: 100.2, 286: 689.0, 30: 45.0, 58: 227.0, 145: 959.0, 143: 1471.0, 380: 428.0, 383: 428.0, 397: 227.0, 400: 227.0, 403: 153.0, 405: 63.0, 408: 132.0, 165: 264.0, 411: 139.0, 413: 140.0, 415: 142.0, 167: 181.0, 436: 1025.5, 438: 485.0, 440: 488.0, 457: 399.2, 173: 957.2, 472: 485.0, 153: 1283.0, 155: 164.0, 176: 1284.2, 181: 107.0, 184: 1012.0, 524: 56.0, 541: 585.2, 545: 585.2, 530: 216.0, 549: 1283.0, 552: 501.8, 555: 588.0, 559: 66.0, 591: 98.0, 567: 585.0, 593: 1283.0, 596: 205.0, 594: 171.0, 597: 1283.0, 598: 171.0, 602: 343.0, 603: 159.0, 605: 410.0, 610: 214.0, 612: 692.0, 615: 296.5, 617: 480.0, 618: 162.2, 622: 599.2, 620: 212.8, 628: 123.5, 630: 692.0, 635: 134.2, 637: 112.0, 639: 597.0, 640: 427.0, 645: 135.5, 647: 111.5, 642: 3353.0, 649: 691.0, 651: 629.2, 655: 214.0, 657: 1283.0, 658: 3353.2, 659: 693.0, 661: 693.0, 663: 691.2, 668: 163.0, 674: 267.0, 678: 350.0, 682: 617.5, 683: 427.2, 708: 1935.0, 710: 2026.0, 712: 2025.2, 714: 91.2, 718: 1283.0, 743: 692.0, 719: 295.0, 720: 309.2, 724: 95.2, 731: 39.0, 727: 261.8, 734: 258.0, 738: 122.8, 740: 475.0, 744: 529.0, 747: 214.0, 749: 1283.0, 750: 427.0, 752: 3353.0, 753: 693.0, 757: 601.8, 759: 693.0, 764: 316.2, 770: 1283.0, 766: 692.0, 768: 82.0, 771: 519.0, 772: 520.0, 774: 293.0, 778: 693.0, 780: 601.0, 781: 603.2, 787: 507.2, 783: 693.0, 785: 600.0, 790: 603.0, 791: 693.0, 795: 214.2, 802: 107.0, 797: 692.0, 817: 661.2, 805: 107.0, 819: 663.0, 821: 663.0, 807: 586.0, 808: 488.0, 810: 135.0, 833: 107.0, 837: 107.0, 812: 585.0, 841: 425.0, 843: 331.0, 845: 334.0, 847: 331.0, 856: 123.2, 858: 691.2, 861: 1131.5, 897: 112.0, 901: 112.0, 903: 585.0, 904: 692.0, 910: 112.0, 912: 689.0, 944: 60.2, 946: 59.2, 948: 585.0, 951: 112.0, 957: 220.0, 953: 585.2, 962: 192.8, 964: 49.2, 972: 47.0, 974: 47.0, 966: 585.0, 977: 280.0, 980: 404.2, 981: 306.2, 982: 401.2, 1007: 629.2, 1011: 214.0, 1014: 214.0, 1016: 585.0, 1018: 693.0, 1020: 689.0, 1023: 427.0, 1025: 1283.0, 1027: 3353.0, 1028: 692.0, 1031: 692.0, 1033: 692.0, 1036: 692.0, 1038: 692.0, 1041: 692.0, 1047: 427.0, 1049: 689.0, 191: 946.2, 194: 1285.0, 204: 107.0, 207: 1012.0, 1189: 3509.0, 1169: 1050.0, 1172: 310.0, 1177: 200.0, 1181: 93.0, 1174: 293.0, 1183: 153.0, 1188: 3472.0, 1217: 279.0, 1219: 329.0, 1223: 197.0, 1228: 226.2, 1230: 228.2, 1231: 292.2, 1267: 56.0, 1233: 293.0, 1235: 227.0, 1237: 309.0, 1226: 227.0, 1297: 216.0, 1276: 692.0, 1278: 692.0, 1280: 693.0, 1282: 600.2, 1284: 598.0, 1286: 693.0, 1290: 1640.0, 1299: 585.0, 1303: 216.0, 1305: 597.0, 1316: 56.0, 1328: 56.0, 1331: 629.8, 1320: 455.0, 1337: 389.2, 1340: 376.8, 1342: 627.0, 1343: 182.0, 1344: 203.0, 1346: 164.0, 1347: 810.0, 1358: 582.0, 1349: 1153.0, 1360: 689.0, 1362: 618.0, 1367: 107.0, 1370: 1012.0, 1382: 357.0, 1387: 689.0, 116: 955.8, 119: 1285.5}


NT_FULL = 32


def kernel(**inputs):
    nc = build(NT_FULL, phases=(1, 2, 3, 4), debug=False, final=True)
    in_maps = [make_in_map(inputs, b, NT_FULL) for b in range(8)]
    res = run_bass_kernel_spmd(nc, in_maps, core_ids=list(range(8)))
    return np.stack([np.asarray(r["out"], np.float32) for r in res.results], axis=0)
```
